# Optimizing a Trainium2 kernel written in Bass

```python
import math
import jax, jax.numpy as jnp
from jax import lax
import numpy as np

D_MODEL = 1024
BATCH = 2
SEQ = 8192
DEPTH = 2

HEAD_DIM = 64
N_Q_HEADS = 8
N_KV_HEADS = 2
GQA_GROUP = N_Q_HEADS // N_KV_HEADS
ATTN_WIDTH = N_Q_HEADS * HEAD_DIM
KV_WIDTH = N_KV_HEADS * HEAD_DIM
QKV_COLS = ATTN_WIDTH + 2 * KV_WIDTH
CONV_WIDTH = D_MODEL - ATTN_WIDTH
IN_COLS = QKV_COLS + 3 * CONV_WIDTH
D_FF = 2752
BLOCK = 128
WINDOW = 128
N_BUCKETS = 32
MAX_DISTANCE = 128
GRID_W = 64
ROPE_THETA = 10000.0
SHORT_K = 3
HYENA_EMB = 33
HYENA_FILTER_WIDTH = 64
HYENA_MIN_DECAY = math.log(1e-2) / 0.3
HYENA_MAX_DECAY = math.log(1e-2) / 1.5
EPS = 1e-6
NEG_INF = -1e30
N_EVEN = (DEPTH + 1) // 2
N_ODD = DEPTH // 2

kernel_name = 'hybrid_swa_shortconv_axialattn_hyena_macaron'


def rmsnorm(x, g):
    xf = x.astype(jnp.float32)
    y = xf * lax.rsqrt(jnp.mean(xf * xf, axis=-1, keepdims=True) + EPS)
    return (y * g.astype(jnp.float32)).astype(x.dtype)


def adaln_in(x, g, m, i):
    return rmsnorm(x, g) * (1 + m[:, i, 1][:, None]) + m[:, i, 0][:, None]


def swiglu(h, w1, w3, w2):
    return (jax.nn.silu(h @ w1) * (h @ w3)) @ w2


def short_conv(u, w, b):
    up = jnp.pad(u, ((0, 0), (1, 1), (0, 0)))
    return up[:, :-2] * w[0] + up[:, 1:-1] * w[1] + up[:, 2:] * w[2] + b


def t5_bucket(rel):
    half = N_BUCKETS // 2
    max_exact = half // 2
    n = jnp.abs(rel)
    large = max_exact + (jnp.log(jnp.maximum(n, 1).astype(jnp.float32) / max_exact)
                         / math.log(MAX_DISTANCE / max_exact) * (half - max_exact)).astype(jnp.int32)
    large = jnp.minimum(large, half - 1)
    return jnp.where(rel > 0, half, 0) + jnp.where(n < max_exact, n, large)


def split_qkv(h, qk_g):
    bsz, S, _ = h.shape
    q = h[..., :ATTN_WIDTH].reshape(bsz, S, N_Q_HEADS, HEAD_DIM)
    k = h[..., ATTN_WIDTH:ATTN_WIDTH + KV_WIDTH].reshape(bsz, S, N_KV_HEADS, HEAD_DIM)
    v = h[..., ATTN_WIDTH + KV_WIDTH:QKV_COLS].reshape(bsz, S, N_KV_HEADS, HEAD_DIM)
    return rmsnorm(q, qk_g[0]), rmsnorm(k, qk_g[1]), v


def windowed_attention(q, k, v, sink, rel_table):
    bsz, S = q.shape[0], q.shape[1]
    nb = S // BLOCK
    qb = q.reshape(bsz, nb, BLOCK, N_KV_HEADS, GQA_GROUP, HEAD_DIM)
    pad = ((0, 0), (BLOCK, BLOCK), (0, 0), (0, 0))
    kp = jnp.pad(k, pad).reshape(bsz, nb + 2, BLOCK, N_KV_HEADS, HEAD_DIM)
    vp = jnp.pad(v, pad).reshape(bsz, nb + 2, BLOCK, N_KV_HEADS, HEAD_DIM)
    kb = jnp.concatenate([kp[:, :-2], kp[:, 1:-1], kp[:, 2:]], axis=2)
    vb = jnp.concatenate([vp[:, :-2], vp[:, 1:-1], vp[:, 2:]], axis=2)
    s = jnp.einsum('bnqhgd,bnkhd->bnhgqk', qb, kb).astype(jnp.float32) * HEAD_DIM ** -0.5
    koff = jnp.arange(3 * BLOCK) - BLOCK
    rel = koff[None, :] - jnp.arange(BLOCK)[:, None]
    bias = rel_table.astype(jnp.float32)[t5_bucket(rel)]
    bias = jnp.transpose(bias, (2, 0, 1)).reshape(N_KV_HEADS, GQA_GROUP, BLOCK, 3 * BLOCK)
    kpos = jnp.arange(nb)[:, None] * BLOCK + koff[None, :]
    valid = (jnp.abs(rel) <= WINDOW)[None] & ((kpos >= 0) & (kpos < S))[:, None, :]
    s = jnp.where(valid[None, :, None, None], s + bias, NEG_INF)
    sk = sink.astype(jnp.float32).reshape(N_KV_HEADS, GQA_GROUP)[:, :, None, None]
    mx = jnp.maximum(jnp.max(s, axis=-1, keepdims=True), sk)
    p = jnp.exp(s - mx)
    den = jnp.sum(p, axis=-1, keepdims=True) + jnp.exp(sk - mx)
    o = jnp.einsum('bnhgqk,bnkhd->bnqhgd', (p / den).astype(v.dtype), vb)
    return o.reshape(bsz, S, ATTN_WIDTH)


def axial_rope(x):
    S = x.shape[1]
    rows = S // GRID_W
    row, col = jnp.meshgrid(jnp.arange(rows), jnp.arange(GRID_W), indexing='ij')
    row = row.reshape(-1).astype(jnp.float32)
    col = col.reshape(-1).astype(jnp.float32)
    half = HEAD_DIM // 2
    inv = ROPE_THETA ** (-jnp.arange(0, half, 2, dtype=jnp.float32) / half)
    ang = jnp.concatenate([row[:, None] * inv, col[:, None] * inv], axis=-1)
    cos = jnp.cos(ang)[None, :, None, :]
    sin = jnp.sin(ang)[None, :, None, :]
    xf = x.astype(jnp.float32)
    x1, x2 = xf[..., 0::2], xf[..., 1::2]
    out = jnp.stack([x1 * cos - x2 * sin, x1 * sin + x2 * cos], axis=-1).reshape(x.shape)
    return out.astype(x.dtype)


def blocked_dense_attention(q, k, v):
    bsz, S = q.shape[0], q.shape[1]
    nb = S // BLOCK
    qb = jnp.moveaxis(q.reshape(bsz, nb, BLOCK, N_KV_HEADS, GQA_GROUP, HEAD_DIM), 1, 0)
    scale = HEAD_DIM ** -0.5

    def attend(qblk):
        s = jnp.einsum('bqhgd,bkhd->bhgqk', qblk, k).astype(jnp.float32) * scale
        p = jax.nn.softmax(s, axis=-1).astype(v.dtype)
        return jnp.einsum('bhgqk,bkhd->bqhgd', p, v)

    o = lax.map(attend, qb)
    return jnp.moveaxis(o, 0, 1).reshape(bsz, S, ATTN_WIDTH)


def hyena_filter(L, w1, b1, w2, b2, w3, b3, w4, freq):
    f32 = jnp.float32
    t = jnp.linspace(0.0, 1.0, L, dtype=f32)[:, None]
    bands = (HYENA_EMB - 1) // 2
    w = (2.0 * math.pi / L) * jnp.arange(L, dtype=f32)[:, None]
    fr = jnp.linspace(1e-4, bands - 1, bands, dtype=f32)[None, :]
    z = jnp.concatenate([t, jnp.cos(fr * w), -jnp.sin(fr * w)], axis=-1)
    fq = freq.astype(f32)
    h = jnp.sin(fq * (z @ w1.astype(f32) + b1.astype(f32)))
    h = jnp.sin(fq * (h @ w2.astype(f32) + b2.astype(f32)))
    h = jnp.sin(fq * (h @ w3.astype(f32) + b3.astype(f32)))
    h = (h @ w4.astype(f32)).reshape(L, 2, CONV_WIDTH)
    deltas = jnp.abs(jnp.linspace(HYENA_MIN_DECAY, HYENA_MAX_DECAY, CONV_WIDTH, dtype=f32))
    h = h * jnp.exp(-t[:, :, None] * deltas)
    kfull = jnp.concatenate([h[:, 0], jnp.zeros((1, CONV_WIDTH), f32), h[:0:-1, 1]], axis=0)
    return kfull * lax.rsqrt(jnp.sum(kfull * kfull, axis=0, keepdims=True) + EPS)


def fft_long_conv(u, kfull):
    L = u.shape[1]
    uf = jnp.fft.rfft(u.astype(jnp.float32), n=2 * L, axis=1)
    kf = jnp.fft.rfft(kfull, n=2 * L, axis=0)
    return jnp.fft.irfft(uf * kf[None], n=2 * L, axis=1)[:, :L].astype(u.dtype)


def mixer_ab(h, qk_g, sink, conv_w, conv_b, rel_table):
    q, k, v = split_qkv(h, qk_g)
    att = windowed_attention(q, k, v, sink, rel_table)
    gb, gc, u = jnp.split(h[..., QKV_COLS:], 3, axis=-1)
    conv = gb * short_conv(gc * u, conv_w, conv_b)
    return jnp.concatenate([att, conv], axis=-1)


def mixer_cd(h, qk_g, conv_w, conv_b, f_w1, f_b1, f_w2, f_b2, f_w3, f_b3, f_w4, f_freq, skip):
    q, k, v = split_qkv(h, qk_g)
    att = blocked_dense_attention(axial_rope(q), axial_rope(k), v)
    u = short_conv(h[..., QKV_COLS:], conv_w, conv_b)
    x0, x1, vh = jnp.split(u, 3, axis=-1)
    kfull = hyena_filter(h.shape[1], f_w1, f_b1, f_w2, f_b2, f_w3, f_b3, f_w4, f_freq)
    z = x1 * vh
    y = x0 * (fft_long_conv(z, kfull) + skip * z)
    return jnp.concatenate([att, y], axis=-1)


def setup_inputs(seed: int = 0) -> dict:
    key = jax.random.key(seed)
    ks = iter(jax.random.split(key, 32))

    def nrm(shape, s):
        return s * jax.random.normal(next(ks), shape, jnp.float32)

    D = D_MODEL
    FW = HYENA_FILTER_WIDTH
    return {
        'x': nrm((BATCH, SEQ, D), 1.0),
        'c': nrm((BATCH, D), 1.0),
        'rel_table': nrm((N_BUCKETS, N_Q_HEADS), 0.5),
        'norm_g': 1.0 + nrm((DEPTH, 3, D), 0.02),
        'w_mod': nrm((DEPTH, D, 9 * D), 0.5 * D ** -0.5),
        'b_mod': nrm((DEPTH, 9 * D), 0.02),
        'w_in': nrm((DEPTH, D, IN_COLS), D ** -0.5),
        'w_out': nrm((DEPTH, D, D), D ** -0.5),
        'ffn_w1': nrm((DEPTH, 2, D, D_FF), D ** -0.5),
        'ffn_w3': nrm((DEPTH, 2, D, D_FF), D ** -0.5),
        'ffn_w2': nrm((DEPTH, 2, D_FF, D), D_FF ** -0.5),
        'a_qk_g': 1.0 + nrm((N_EVEN, 2, HEAD_DIM), 0.02),
        'a_sink': nrm((N_EVEN, N_Q_HEADS), 0.5),
        'b_conv_w': nrm((N_EVEN, SHORT_K, CONV_WIDTH), SHORT_K ** -0.5),
        'b_conv_b': nrm((N_EVEN, CONV_WIDTH), 0.02),
        'c_qk_g': 1.0 + nrm((N_ODD, 2, HEAD_DIM), 0.02),
        'd_conv_w': nrm((N_ODD, SHORT_K, 3 * CONV_WIDTH), SHORT_K ** -0.5),
        'd_conv_b': nrm((N_ODD, 3 * CONV_WIDTH), 0.02),
        'd_f_w1': nrm((N_ODD, HYENA_EMB, FW), HYENA_EMB ** -0.5),
        'd_f_b1': nrm((N_ODD, FW), 0.02),
        'd_f_w2': nrm((N_ODD, FW, FW), FW ** -0.5),
        'd_f_b2': nrm((N_ODD, FW), 0.02),
        'd_f_w3': nrm((N_ODD, FW, FW), FW ** -0.5),
        'd_f_b3': nrm((N_ODD, FW), 0.02),
        'd_f_w4': nrm((N_ODD, FW, 2 * CONV_WIDTH), FW ** -0.5),
        'd_f_freq': 1.0 + nrm((N_ODD, FW), 0.02),
        'd_skip': nrm((N_ODD, CONV_WIDTH), 0.1),
    }


def reference(x, c, rel_table, norm_g, w_mod, b_mod, w_in, w_out, ffn_w1, ffn_w3, ffn_w2,
              a_qk_g, a_sink, b_conv_w, b_conv_b, c_qk_g, d_conv_w, d_conv_b,
              d_f_w1, d_f_b1, d_f_w2, d_f_b2, d_f_w3, d_f_b3, d_f_w4, d_f_freq, d_skip):
    bsz = x.shape[0]
    cond = jax.nn.silu(c)
    for l in range(DEPTH):
        m = (cond @ w_mod[l] + b_mod[l]).reshape(bsz, 3, 3, D_MODEL)
        x = x + 0.5 * m[:, 0, 2][:, None] * swiglu(adaln_in(x, norm_g[l, 0], m, 0),
                                                   ffn_w1[l, 0], ffn_w3[l, 0], ffn_w2[l, 0])
        h = adaln_in(x, norm_g[l, 1], m, 1) @ w_in[l]
        j = l // 2
        if l % 2 == 0:
            mix = mixer_ab(h, a_qk_g[j], a_sink[j], b_conv_w[j], b_conv_b[j], rel_table)
        else:
            mix = mixer_cd(h, c_qk_g[j], d_conv_w[j], d_conv_b[j], d_f_w1[j], d_f_b1[j],
                           d_f_w2[j], d_f_b2[j], d_f_w3[j], d_f_b3[j], d_f_w4[j], d_f_freq[j], d_skip[j])
        x = x + m[:, 1, 2][:, None] * (mix @ w_out[l])
        x = x + 0.5 * m[:, 2, 2][:, None] * swiglu(adaln_in(x, norm_g[l, 2], m, 2),
                                                   ffn_w1[l, 1], ffn_w3[l, 1], ffn_w2[l, 1])
    return x
```

```python
import math
import numpy as np
import concourse.bass as bass
import concourse.mybir as mybir
from concourse.bass_utils import run_bass_kernel_spmd

AF = mybir.ActivationFunctionType
ALU = mybir.AluOpType
F32 = mybir.dt.float32
BF16 = mybir.dt.bfloat16
EPOCH = 12000

D = 1024
S = 8192
TOK = 2048
NCORE = 8
DFF = 2752
DFFP = 2816
NF = 22
HD = 64
EPS = 1e-6
INC = 2304


class Buf:
    __slots__ = ("name", "w", "r", "dsem", "dcnt")

    def __init__(self, name=""):
        self.name = name
        self.w = None
        self.r = {}
        self.dsem = None
        self.dcnt = 0


class Sink:
    def __init__(self, name=""):
        self.name = name
        self.toks = {}


class KB:
    def __init__(self, nc):
        self.nc = nc
        self.engs = {"pe": nc.tensor, "act": nc.scalar, "dve": nc.vector,
                     "pool": nc.gpsimd, "sp": nc.sync}
        self.cnt = {e: 0 for e in self.engs}
        self.sems = {}
        self.waited = {e: {} for e in self.engs}
        self.nsem = 0
        self._stack = []

    def _newsem(self, name):
        cm = self.nc.semaphore(name)
        h = cm.__enter__()
        self._stack.append(cm)
        self.nsem += 1
        return h

    def sb(self, name, shape, dt):
        cm = self.nc.sbuf_tensor(name, shape, dt)
        t = cm.__enter__()
        self._stack.append(cm)
        return t

    def ps(self, name, shape, dt=F32):
        cm = self.nc.psum_tensor(name, shape, dt)
        t = cm.__enter__()
        self._stack.append(cm)
        return t

    def close(self):
        while self._stack:
            self._stack.pop().__exit__(None, None, None)

    def _engsem(self, eng):
        key = (eng, self.cnt[eng] // EPOCH)
        if key not in self.sems:
            self.sems[key] = self._newsem(f"s_{eng}_{key[1]}")
        return key

    def _deps(self, reads, writes):
        deps = {}

        def add(k, v):
            if deps.get(k, 0) < v:
                deps[k] = v
        for b in reads:
            if b.w is not None:
                add(*b.w)
        for b in writes:
            if b.w is not None:
                add(*b.w)
            for kk, v in b.r.items():
                add(kk, v)
        return deps

    def _wait(self, eng, deps):
        w = self.waited[eng]
        e = self.engs[eng]
        for kk, v in deps.items():
            if w.get(kk, 0) >= v:
                continue
            e.wait_ge(self.sems[kk], v)
            w[kk] = v

    def _commit(self, tok, reads, writes):
        kk, v = tok
        for b in reads:
            if b.r.get(kk, 0) < v:
                b.r[kk] = v
        for b in writes:
            b.w = tok
            b.r = {}

    def op(self, eng, fn, reads=(), writes=()):
        deps = self._deps(reads, writes)
        if eng == "pe":
            deps = {kk: v for kk, v in deps.items() if kk[0] != "pe"}
        self._wait(eng, deps)
        key = self._engsem(eng)
        ins = fn(self.engs[eng])
        self.cnt[eng] += 1
        val = self.cnt[eng] - key[1] * EPOCH
        ins.then_inc(self.sems[key], 1)
        tok = (key, val)
        self._commit(tok, reads, writes)
        return tok

    def dma(self, q, pairs, reads=(), writes=(), sembuf=None, sink=None, **kw):
        sbf = sembuf or (writes[0] if writes else reads[0])
        if sbf.dsem is None:
            sbf.dsem = {}
            sbf.dcnt = {}
        sw = (q == "pool")
        if sw not in sbf.dsem:
            sbf.dsem[sw] = ("d", id(sbf), sw)
            sbf.dcnt[sw] = 0
            self.sems[sbf.dsem[sw]] = self._newsem(f"d_{self.nsem}")
        deps = self._deps(reads, writes)
        self._wait(q, deps)
        e = self.engs[q]
        for (o, i) in pairs:
            e.dma_start(out=o, in_=i, **kw).then_inc(self.sems[sbf.dsem[sw]], 16)
            sbf.dcnt[sw] += 16
        tok = (sbf.dsem[sw], sbf.dcnt[sw])
        self._commit(tok, reads, writes)
        if sink is not None and sink.toks.get(tok[0], 0) < tok[1]:
            sink.toks[tok[0]] = tok[1]
        return tok

    def wait_all(self, eng, bufs):
        deps = {}
        for b in bufs:
            d = dict(b.toks) if isinstance(b, Sink) else self._deps([b], [b])
            for kk, v in d.items():
                if deps.get(kk, 0) < v:
                    deps[kk] = v
        self._wait(eng, deps)

    def barrier(self, dbufs=()):
        deps = {}
        for e in ("pe", "act", "dve"):
            if self.cnt[e] == 0:
                continue
            ep = (self.cnt[e] - 1) // EPOCH
            deps[(e, ep)] = self.cnt[e] - ep * EPOCH
        for b in dbufs:
            for kk, v in self._deps([b], [b]).items():
                if deps.get(kk, 0) < v:
                    deps[kk] = v
        for e in ("pe", "act", "dve", "pool", "sp"):
            self._wait(e, dict(deps))


class Prog:
    def __init__(self, nc, k):
        self.nc = nc
        self.k = k
        self.WA = [k.sb(f"WA{i}", [128, 8, 256], BF16) for i in range(4)]
        self.WAb = [Buf(f"WA{i}") for i in range(4)]
        self.WBf = k.sb("WBf", [128, 3, 1024], F32)
        self.WB = [self.WBf[:, i, :].bitcast(BF16).rearrange("p (f n) -> p f n", f=2) for i in range(3)]
        self.WBb = [Buf(f"WB{i}") for i in range(3)]
        self.wa_i = 0
        self.PS = [k.ps(f"ps{i}", [128, 512]) for i in range(8)]
        self.PSb = [Buf(f"ps{i}") for i in range(8)]
        self.rr = {}

    def next_wa(self, parity=None):
        i = self.wa_i
        self.wa_i = (self.wa_i + 1) % 4
        return i

    def bank(self, group, banks):
        i = self.rr.get(group, 0)
        self.rr[group] = (i + 1) % len(banks)
        b = banks[i]
        return self.PS[b], self.PSb[b]

    def load_wa(self, src_ap):
        i = self.next_wa()
        self.k.dma("pool", [(self.WA[i][:], src_ap.rearrange("(c p) n -> p c n", p=128))], writes=[self.WAb[i]])
        return self.WA[i], self.WAb[i]

    def load_wb(self, i, src_ap):
        self.k.dma("pool", [(self.WB[i], src_ap.rearrange("(f p) n -> p f n", p=128))], writes=[self.WBb[i]])
        return self.WB[i], self.WBb[i]


def emit_mod(P, cond_bf, cond_b, wmod_l, bmod_l, modv, modb, a_out, normg_l):
    k = P.k
    ps, psb = P.PS[7], P.PSb[7]
    for ch in range(36):
        wa, wab = P.load_wa(wmod_l[:, ch * 256:(ch + 1) * 256])
        for cl in range(2):
            cc = ch * 2 + cl
            for kc in range(8):
                k.op("pe", lambda e, cc=cc, kc=kc, cl=cl, wa=wa: e.matmul(
                    ps[:, cc:cc + 1], lhsT=wa[:, kc, cl * 128:(cl + 1) * 128], rhs=cond_bf[:, kc:kc + 1],
                    start=(kc == 0), stop=(kc == 7)),
                    reads=[wab, cond_b], writes=[psb])
    k.op("dve", lambda e: e.tensor_tensor(out=modv[:], in0=ps[:, 0:72], in1=bmod_l, op=ALU.add),
         reads=[psb], writes=[modb])
    for i in range(3):
        k.op("dve", lambda e, i=i: e.scalar_tensor_tensor(
            out=a_out[:, i, :], in0=modv[:, (i * 3 + 1) * 8:(i * 3 + 2) * 8], scalar=1.0, in1=normg_l[:, i, :],
            op0=ALU.add, op1=ALU.mult), reads=[modb], writes=[modb])
    for i in (0, 2):
        k.op("dve", lambda e, i=i: e.tensor_scalar(
            out=modv[:, (i * 3 + 2) * 8:(i * 3 + 3) * 8], in0=modv[:, (i * 3 + 2) * 8:(i * 3 + 3) * 8],
            scalar1=0.5, scalar2=None, op0=ALU.mult), reads=[modb], writes=[modb])


class Act:
    def __init__(self, t, tiles, name):
        self.t = t
        self.tiles = tiles
        self.b = [Buf(f"{name}{j}") for j in range(len(tiles))]

    def ap(self, c, j):
        c0, w = self.tiles[j]
        return self.t[:, c, c0:c0 + w]


def emit_adaln(P, X, H, a_ap, shift_ap, C):
    k = P.k
    for j, (c0, w) in enumerate(X.tiles):
        ps, psb = P.PS[6], P.PSb[6]
        for c in range(8):
            sq, sqb = C.tmpbf()
            k.op("act", lambda e, c=c, j=j, w=w, sq=sq: e.activation(out=sq[:, 0:w], in_=X.ap(c, j), func=AF.Square),
                 reads=[X.b[j]], writes=[sqb])
            k.op("pe", lambda e, c=c, w=w, sq=sq: e.matmul(ps[:, 0:w], lhsT=C.ones_bf[:], rhs=sq[:, 0:w],
                                                     start=(c == 0), stop=(c == 7)),
                 reads=[sqb, C.cb], writes=[psb])
        k.op("act", lambda e, w=w: e.activation(out=C.RS[:, 0:w], in_=ps[:, 0:w], func=AF.Sqrt,
                                                bias=C.eps[:, 0:1], scale=1.0 / D),
             reads=[psb, C.cb], writes=[C.RSb])
        k.op("dve", lambda e, w=w: e.reciprocal(out=C.RS[:, 0:w], in_=C.RS[:, 0:w]), reads=[C.RSb], writes=[C.RSb])
        for c in range(8):
            tt, ttb = C.tmp()
            k.op("dve", lambda e, c=c, j=j, w=w, tt=tt: e.scalar_tensor_tensor(
                out=tt[:, 0:w], in0=X.ap(c, j), scalar=a_ap[:, c:c + 1], in1=C.RS[:, 0:w],
                op0=ALU.mult, op1=ALU.mult), reads=[X.b[j], C.RSb, C.modb], writes=[ttb])
            k.op("act", lambda e, c=c, j=j, w=w, tt=tt: e.activation(
                out=H.ap(c, j), in_=tt[:, 0:w], func=AF.Identity, bias=shift_ap[:, c:c + 1], scale=1.0),
                reads=[ttb, C.modb], writes=[H.b[j]])


def emit_ffn(P, X, H, U, Ub, w1, w3, w2, gate_ap, C):
    k = P.k
    groups = [(0, 3), (3, 3), (6, 3), (9, 2)]
    ntile = len(X.tiles)
    for (p0, npair) in groups:
        for pl in range(npair):
            p = p0 + pl
            wa1, wa1b = P.load_wa(w1[:, p * 256:(p + 1) * 256])
            wa3, wa3b = P.load_wa(w3[:, p * 256:(p + 1) * 256])
            for fl in range(2):
                fu = pl * 2 + fl
                for j in range(ntile):
                    c0, w = X.tiles[j]
                    ps1, ps1b = P.bank("h1", [0, 1])
                    ps3, ps3b = P.bank("h3", [2, 3])
                    for c in range(8):
                        k.op("pe", lambda e, c=c, j=j, w=w, ps1=ps1, wa1=wa1, fl=fl: e.matmul(
                            ps1[:, 0:w], lhsT=wa1[:, c, fl * 128:(fl + 1) * 128], rhs=H.ap(c, j),
                            start=(c == 0), stop=(c == 7)), reads=[wa1b, H.b[j]], writes=[ps1b])
                    for c in range(8):
                        k.op("pe", lambda e, c=c, j=j, w=w, ps3=ps3, wa3=wa3, fl=fl: e.matmul(
                            ps3[:, 0:w], lhsT=wa3[:, c, fl * 128:(fl + 1) * 128], rhs=H.ap(c, j),
                            start=(c == 0), stop=(c == 7)), reads=[wa3b, H.b[j]], writes=[ps3b])
                    sl, slb = C.tmp()
                    k.op("act", lambda e, w=w, ps1=ps1, sl=sl: e.activation(out=sl[:, 0:w], in_=ps1[:, 0:w], func=AF.Silu),
                         reads=[ps1b], writes=[slb])
                    k.op("dve", lambda e, w=w, c0=c0, ps3=ps3, sl=sl, fu=fu: e.tensor_tensor(
                        out=U[:, fu, c0:c0 + w], in0=ps3[:, 0:w], in1=sl[:, 0:w], op=ALU.mult),
                        reads=[ps3b, slb], writes=[Ub[fu][j]])
        for pl in range(npair):
            P.load_wb(pl, w2[(p0 + pl) * 256:(p0 + pl + 1) * 256, :])
        nfu = npair * 2
        for c in range(8):
            for j in range(ntile):
                c0, w = X.tiles[j]
                pso, psob = P.bank("o", [4, 5])
                for fu in range(nfu):
                    k.op("pe", lambda e, fu=fu, c=c, w=w, c0=c0, pso=pso: e.matmul(
                        pso[:, 0:w], lhsT=P.WB[fu // 2][:, fu % 2, c * 128:(c + 1) * 128], rhs=U[:, fu, c0:c0 + w],
                        start=(fu == 0), stop=(fu == nfu - 1)), reads=[P.WBb[fu // 2], Ub[fu][j]], writes=[psob])
                k.op("dve", lambda e, c=c, j=j, w=w, pso=pso: e.scalar_tensor_tensor(
                    out=X.ap(c, j), in0=pso[:, 0:w], scalar=gate_ap[:, c:c + 1], in1=X.ap(c, j),
                    op0=ALU.mult, op1=ALU.add), reads=[psob, X.b[j], C.modb], writes=[X.b[j]])


class Common:
    pass


def emit_inproj_tile(P, H, j, wa, wab, cl, ps, psb, cols=None):
    k = P.k
    c0, w = H.tiles[j]
    if cols is not None:
        c0, w = c0 + cols[0], cols[1]
    for c in range(8):
        k.op("pe", lambda e, c=c, c0=c0, w=w: e.matmul(
            ps[:, 0:w], lhsT=wa[:, c, cl * 128:(cl + 1) * 128], rhs=H.t[:, c, c0:c0 + w],
            start=(c == 0), stop=(c == 7)), reads=[wab, H.b[j]], writes=[psb])
    return w


def emit_qknorm(P, C, ps, psb, w, g_ap, out_ap, out_b, extra_reads=()):
    k = P.k
    sq, sqb = C.tmpbf()
    k.op("act", lambda e: e.activation(out=sq[:, 0:w], in_=ps[:, 0:w], func=AF.Square), reads=[psb], writes=[sqb])
    p2, p2b = P.PS[6], P.PSb[6]
    k.op("pe", lambda e: e.matmul(p2[:, 0:w], lhsT=C.bd_bf[:], rhs=sq[:, 0:w], start=True, stop=True),
         reads=[sqb, C.cb], writes=[p2b])
    rs, rsb = C.tmp()
    k.op("act", lambda e: e.activation(out=rs[:, 0:w], in_=p2[:, 0:w], func=AF.Sqrt, bias=C.eps[:, 0:1],
                                       scale=1.0 / HD), reads=[p2b, C.cb], writes=[rsb])
    k.op("dve", lambda e: e.reciprocal(out=rs[:, 0:w], in_=rs[:, 0:w]), reads=[rsb], writes=[rsb])
    k.op("dve", lambda e: e.scalar_tensor_tensor(out=out_ap, in0=ps[:, 0:w], scalar=g_ap, in1=rs[:, 0:w],
                                                 op0=ALU.mult, op1=ALU.mult),
         reads=[psb, rsb, C.cb] + list(extra_reads), writes=[out_b])


def emit_outproj(P, C, X, MIX, MIXb, wout_l, gate_ap):
    k = P.k
    for pr in range(4):
        wa, wab = P.load_wa(wout_l[:, pr * 256:(pr + 1) * 256])
        for cl in range(2):
            c = pr * 2 + cl
            for j, (c0, w) in enumerate(X.tiles):
                pso, psob = P.bank("o", [4, 5])
                for fc in range(8):
                    k.op("pe", lambda e, fc=fc, c0=c0, w=w, pso=pso, wa=wa, cl=cl: e.matmul(
                        pso[:, 0:w], lhsT=wa[:, fc, cl * 128:(cl + 1) * 128], rhs=MIX(fc, c0, w),
                        start=(fc == 0), stop=(fc == 7)), reads=[wab, MIXb[j]], writes=[psob])
                k.op("dve", lambda e, c=c, j=j, w=w, pso=pso: e.scalar_tensor_tensor(
                    out=X.ap(c, j), in0=pso[:, 0:w], scalar=gate_ap[:, c:c + 1], in1=X.ap(c, j),
                    op0=ALU.mult, op1=ALU.add), reads=[psob, X.b[j], C.modb], writes=[X.b[j]])


def emit_expb(P, C, relrep, relb, onehot, vmask, scr, EXPB, EXPBb):
    k = P.k
    nc = P.nc
    E8 = C.E8
    E8b = Buf("E8")
    for h in range(8):
        lt, ltb = C.tmp()
        k.op("dve", lambda e, h=h, lt=lt: e.tensor_scalar(out=lt[0:32, 0:128], in0=C.ones_f[0:32, 0:128],
                                                          scalar1=relrep[0:32, h:h + 1], scalar2=None, op0=ALU.mult),
             reads=[relb, C.cb], writes=[ltb])
        ps, psb = P.bank("h1", [0, 1])
        k.op("pe", lambda e, lt=lt, ps=ps: e.matmul(ps[:, 0:512], lhsT=lt[0:32, 0:128], rhs=onehot[0:32, :],
                                                    start=True, stop=True), reads=[ltb, C.cb], writes=[psb])
        ex, exb = C.tmp()
        k.op("act", lambda e, ps=ps, ex=ex: e.activation(out=ex[:, :], in_=ps[:, 0:512], func=AF.Exp),
             reads=[psb], writes=[exb])
        k.op("dve", lambda e, h=h, ex=ex: e.tensor_tensor(out=E8[:, h, :], in0=ex[:, :], in1=vmask[:, :], op=ALU.mult),
             reads=[exb, C.cb], writes=[E8b])
    scrb = Buf("scr")
    k.dma("sp", [(scr.ap(), E8[:])], reads=[E8b], writes=[scrb])
    pairs = []
    for h in range(8):
        for bi in range(3):
            off = h * 512 + 128 * (1 - bi) + 255
            src = bass.AP(tensor=scr, offset=off, ap=[[8 * 512 - 1, 128], [1, 128]])
            pairs.append((EXPB[:, h, bi, :], src))
    k.dma("sp", pairs, reads=[scrb], writes=[EXPBb])
    return scrb


def emit_conv0(P, C, H, MIX, MIXb, win_l, convw, convb, flags, S_t, Sb):
    k = P.k
    ntile_own = 4
    for hh in range(2):
        wgb, wgbb = P.load_wa(win_l[:, 768 + hh * 256: 768 + (hh + 1) * 256])
        wgc, wgcb = P.load_wa(win_l[:, 1280 + hh * 256: 1280 + (hh + 1) * 256])
        wu, wub = P.load_wa(win_l[:, 1792 + hh * 256: 1792 + (hh + 1) * 256])
        for cl in range(2):
            cc = hh * 2 + cl
            for j in range(ntile_own):
                c0, w = H.tiles[j]
                pg, pgb = P.bank("h1", [0, 1])
                pu, pub = P.bank("h3", [2, 3])
                emit_inproj_tile(P, H, j, wgc, wgcb, cl, pg, pgb)
                emit_inproj_tile(P, H, j, wu, wub, cl, pu, pub)
                tg, tgb = C.tmp()
                k.op("act", lambda e, tg=tg, pg=pg, w=w: e.activation(out=tg[:, 0:w], in_=pg[:, 0:w], func=AF.Copy),
                     reads=[pgb], writes=[tgb])
                k.op("dve", lambda e, tg=tg, pu=pu, w=w, c0=c0: e.tensor_tensor(
                    out=S_t[:, 1 + c0:1 + c0 + w], in0=pu[:, 0:w], in1=tg[:, 0:w], op=ALU.mult),
                    reads=[pub, tgb], writes=[Sb])
            pg, pgb = P.bank("h1", [0, 1])
            pu, pub = P.bank("h3", [2, 3])
            emit_inproj_tile(P, H, 4, wgc, wgcb, cl, pg, pgb, cols=(127, 2))
            emit_inproj_tile(P, H, 4, wu, wub, cl, pu, pub, cols=(127, 2))
            tg, tgb = C.tmp()
            k.op("act", lambda e, tg=tg, pg=pg: e.activation(out=tg[:, 0:2], in_=pg[:, 0:2], func=AF.Copy),
                 reads=[pgb], writes=[tgb])
            k.op("dve", lambda e, tg=tg, pu=pu: e.scalar_tensor_tensor(
                out=S_t[:, 0:1], in0=pu[:, 0:1], scalar=flags[:, 0:1], in1=tg[:, 0:1], op0=ALU.mult, op1=ALU.mult),
                reads=[pub, tgb, C.cb], writes=[Sb])
            k.op("dve", lambda e, tg=tg, pu=pu: e.scalar_tensor_tensor(
                out=S_t[:, 2049:2050], in0=pu[:, 1:2], scalar=flags[:, 1:2], in1=tg[:, 1:2], op0=ALU.mult, op1=ALU.mult),
                reads=[pub, tgb, C.cb], writes=[Sb])
            for j in range(ntile_own):
                c0, w = H.tiles[j]
                pb_, pbb = P.bank("o", [4, 5])
                emit_inproj_tile(P, H, j, wgb, wgbb, cl, pb_, pbb)
                t, tb = C.tmp()
                k.op("dve", lambda e, t=t, c0=c0, w=w, cc=cc: e.tensor_scalar(
                    out=t[:, 0:w], in0=S_t[:, 1 + c0:1 + c0 + w], scalar1=convw[:, cc, 1:2], scalar2=convb[:, cc:cc + 1],
                    op0=ALU.mult, op1=ALU.add), reads=[Sb, C.cb], writes=[tb])
                k.op("dve", lambda e, t=t, c0=c0, w=w, cc=cc: e.scalar_tensor_tensor(
                    out=t[:, 0:w], in0=S_t[:, c0:c0 + w], scalar=convw[:, cc, 0:1], in1=t[:, 0:w],
                    op0=ALU.mult, op1=ALU.add), reads=[Sb, C.cb, tb], writes=[tb])
                k.op("dve", lambda e, t=t, c0=c0, w=w, cc=cc: e.scalar_tensor_tensor(
                    out=t[:, 0:w], in0=S_t[:, 2 + c0:2 + c0 + w], scalar=convw[:, cc, 2:3], in1=t[:, 0:w],
                    op0=ALU.mult, op1=ALU.add), reads=[Sb, C.cb, tb], writes=[tb])
                k.op("dve", lambda e, t=t, c0=c0, w=w, cc=cc, pb_=pb_: e.tensor_tensor(
                    out=MIX(4 + cc, c0, w), in0=pb_[:, 0:w], in1=t[:, 0:w], op=ALU.mult),
                    reads=[pbb, tb], writes=[MIXb[j]])


def emit_qkv0(P, C, H, win_l, QT, QTb, KT, KTb, VA, VAb, gq, gk, flags):
    k = P.k
    for pr in range(2):
        wa, wab = P.load_wa(win_l[:, pr * 256:(pr + 1) * 256])
        for cl in range(2):
            hh = pr * 2 + cl
            for j in range(4):
                c0, w = H.tiles[j]
                ps, psb = P.bank("h1", [0, 1, 2, 3])
                emit_inproj_tile(P, H, j, wa, wab, cl, ps, psb)
                emit_qknorm(P, C, ps, psb, w, gq[:, 0:1], QT[:, hh, c0:c0 + w], QTb[j])
    wa, wab = P.load_wa(win_l[:, 512:768])
    for j in range(5):
        c0, w = H.tiles[j]
        ps, psb = P.bank("h1", [0, 1, 2, 3])
        emit_inproj_tile(P, H, j, wa, wab, 0, ps, psb)
        emit_qknorm(P, C, ps, psb, w, gk[:, 0:1], KT[:, c0:c0 + w], KTb[j])
    for tb in range(18):
        j = tb // 4 if tb < 16 else 4
        ps, psb = P.bank("o", [4, 5])
        for c in range(8):
            k.op("pe", lambda e, c=c, tb=tb, ps=ps: e.matmul(
                ps[:, 0:128], lhsT=H.t[:, c, tb * 128:(tb + 1) * 128], rhs=wa[:, c, 128:256],
                start=(c == 0), stop=(c == 7)), reads=[wab, H.b[j]], writes=[psb])
        if tb < 16:
            k.op("act", lambda e, tb=tb, ps=ps: e.activation(out=VA[:, tb, 0, 0:64], in_=ps[:, 0:64], func=AF.Copy),
                 reads=[psb], writes=[VAb])
            k.op("act", lambda e, tb=tb, ps=ps: e.activation(out=VA[:, tb, 1, 64:128], in_=ps[:, 64:128], func=AF.Copy),
                 reads=[psb], writes=[VAb])
        else:
            fl = flags[:, tb - 16:tb - 15]
            k.op("dve", lambda e, tb=tb, ps=ps, fl=fl: e.tensor_scalar(
                out=VA[:, tb, 0, 0:64], in0=ps[:, 0:64], scalar1=fl, scalar2=None, op0=ALU.mult),
                reads=[psb, C.cb], writes=[VAb])
            k.op("dve", lambda e, tb=tb, ps=ps, fl=fl: e.tensor_scalar(
                out=VA[:, tb, 1, 64:128], in0=ps[:, 64:128], scalar1=fl, scalar2=None, op0=ALU.mult),
                reads=[psb, C.cb], writes=[VAb])
            k.op("dve", lambda e, tb=tb, fl=fl: e.tensor_scalar(
                out=VA[:, tb, 0, 64:128], in0=C.ones_f[:, 0:64], scalar1=fl, scalar2=None, op0=ALU.mult),
                reads=[C.cb], writes=[VAb])
            k.op("dve", lambda e, tb=tb, fl=fl: e.tensor_scalar(
                out=VA[:, tb, 1, 0:64], in0=C.ones_f[:, 0:64], scalar1=fl, scalar2=None, op0=ALU.mult),
                reads=[C.cb], writes=[VAb])


def emit_attn0(P, C, QT, QTb, KT, KTb, VA, VAb, EXPB, EXPBb, expsink, MIXLO, MIXb):
    k = P.k
    for n in range(16):
        j = n // 4
        for g in range(2):
            gs = slice(g * 64, (g + 1) * 64)
            ds_ = slice((1 - g) * 64, (2 - g) * 64)
            po, pob = P.bank("o", [4, 5])
            for bi in range(3):
                kb = n - 1 + bi
                kidx = 16 if kb < 0 else (17 if kb > 15 else kb)
                kj = kidx // 4 if kidx < 16 else 4
                ps, psb = P.bank("h1", [0, 1, 2, 3])
                k.op("pe", lambda e, ps=ps, kidx=kidx, n=n, gs=gs: e.matmul(
                    ps[:, 0:512], lhsT=KT[gs, kidx * 128:(kidx + 1) * 128], rhs=QT[gs, :, n * 128:(n + 1) * 128],
                    start=True, stop=True), reads=[KTb[kj], QTb[j]], writes=[psb])
                ex, exb = C.tmp()
                k.op("act", lambda e, ps=ps, ex=ex: e.activation(out=ex[:, :], in_=ps[:, 0:512], func=AF.Exp,
                                                                 scale=HD ** -0.5),
                     reads=[psb], writes=[exb])
                pt, ptb = C.tmpbf()
                k.op("dve", lambda e, ex=ex, pt=pt, g=g, bi=bi: e.tensor_tensor(
                    out=pt[:, :].rearrange("p (h q) -> p h q", h=4), in0=ex[:, :].rearrange("p (h q) -> p h q", h=4),
                    in1=EXPB[:, g * 4:(g + 1) * 4, bi, :], op=ALU.mult), reads=[exb, EXPBb], writes=[ptb])
                for hh in range(4):
                    k.op("pe", lambda e, po=po, pt=pt, hh=hh, kidx=kidx, g=g, bi=bi: e.matmul(
                        po[:, hh * 128:(hh + 1) * 128], lhsT=VA[:, kidx, g, :], rhs=pt[:, hh * 128:(hh + 1) * 128],
                        start=(bi == 0 and hh == 0), stop=(bi == 2 and hh == 3), skip_group_check=True),
                        reads=[VAb, ptb], writes=[pob])
            rd, rdb = C.tmp()
            for hh in range(4):
                k.op("dve", lambda e, rd=rd, po=po, hh=hh, g=g, ds_=ds_: e.tensor_scalar(
                    out=rd[ds_, hh * 128:(hh + 1) * 128], in0=po[ds_, hh * 128:(hh + 1) * 128],
                    scalar1=expsink[ds_, g * 4 + hh:g * 4 + hh + 1], scalar2=None, op0=ALU.add),
                    reads=[pob, C.cb], writes=[rdb])
            k.op("dve", lambda e, rd=rd, ds_=ds_: e.reciprocal(out=rd[ds_, :], in_=rd[ds_, :]), reads=[rdb], writes=[rdb])
            k.op("dve", lambda e, rd=rd, po=po, gs=gs, ds_=ds_, n=n: e.tensor_tensor(
                out=MIXLO[gs, 0:4, n * 128:(n + 1) * 128], in0=po[gs, :].rearrange("p (h q) -> p h q", h=4),
                in1=rd[ds_, :].rearrange("p (h q) -> p h q", h=4), op=ALU.mult), reads=[pob, rdb], writes=[MIXb[j]])


class RotTmp:
    def __init__(self, k, name, n, dt):
        self.t = [k.sb(f"{name}{i}", [128, 512], dt) for i in range(n)]
        self.b = [Buf(f"{name}{i}") for i in range(n)]
        self.i = 0

    def __call__(self):
        i = self.i
        self.i = (i + 1) % len(self.t)
        return self.t[i], self.b[i]


BUCKET_MODE = "trunc"


def t5_bucket_np(rel):
    n = np.abs(rel)
    v = (np.log(np.maximum(n, 1).astype(np.float32) / np.float32(8)) / np.float32(math.log(16.0)) * np.float32(8))
    large = 8 + (np.rint(v).astype(np.int32) if BUCKET_MODE == "round" else v.astype(np.int32))
    large = np.minimum(large, 15)
    return np.where(rel > 0, 16, 0) + np.where(n < 8, n, large)


def build_LA(do_l1=True, stop_after=None):
    nc = bass.Bass("TRN2", target_bir_lowering=False)
    k = KB(nc)
    dt_in = lambda name, shape: nc.dram_tensor(name, list(shape), F32, kind="ExternalInput").ap()
    xT_d = dt_in("xT", [128, 8, 2048])
    xH_d = dt_in("xH", [128, 8, 256])
    cf_d = dt_in("cf", [128, 1400])
    oh_d = dt_in("oh", [32, 512])
    vm_d = dt_in("vm", [128, 512])
    wmod_d = dt_in("wmod", [2, 1024, 9216])
    w1_d = dt_in("w1", [2, 2, 1024, DFFP])
    w3_d = dt_in("w3", [2, 2, 1024, DFFP])
    w2_d = dt_in("w2", [2, 2, DFFP, 1024])
    win_d = dt_in("win", [2, 1024, INC])
    wout_d = dt_in("wout", [2, 1024, 1024])
    xo_d = nc.dram_tensor("xo", [128, 8, 2048], F32, kind="ExternalOutput").ap()
    scr = nc.dram_tensor("scr", [128, 8 * 512], F32, kind="Internal")

    P = Prog(nc, k)
    C = Common()
    C.tmp = RotTmp(k, "tf", 5, F32)
    C.tmpbf = RotTmp(k, "tb", 3, BF16)
    C.RS = k.sb("RS", [128, 512], F32)
    C.RSb = Buf("RS")
    C.cb = Buf("consts")
    C.modb = Buf("mod")
    CF = k.sb("CF", [128, 1400], F32)
    OH = k.sb("OH", [32, 512], F32)
    VM = k.sb("VM", [128, 512], F32)
    C.ones_bf = k.sb("ones_bf", [128, 128], BF16)
    C.bd_bf = k.sb("bd_bf", [128, 128], BF16)
    C.ones_f = k.sb("ones_f", [128, 128], F32)
    C.eps = k.sb("eps", [128, 1], F32)
    condbf = k.sb("condbf", [128, 8], BF16)
    MODV = k.sb("modv", [128, 2, 72], F32)
    AV = k.sb("av", [128, 2, 3, 8], F32)
    EXS = k.sb("exs", [128, 8], F32)
    XT = k.sb("XT", [128, 8, 2048], F32)
    Ht = k.sb("H", [128, 8, 2304], BF16)
    UQ = k.sb("UQ", [128, 15360], BF16)
    MX = k.sb("MX", [128, 4096], F32)

    o_cT, o_fl, o_ng, o_bm, o_cw, o_cb, o_gq, o_gk, o_sink = 0, 8, 10, 58, 202, 214, 218, 220, 222
    cT = CF[:, o_cT:o_cT + 8]
    flags = CF[:, o_fl:o_fl + 2]
    normg = CF[:, o_ng:o_ng + 48].rearrange("p (l i c) -> p l i c", l=2, i=3)
    bmod = CF[:, o_bm:o_bm + 144].rearrange("p (l m) -> p l m", l=2)
    convw = CF[:, o_cw:o_cw + 12].rearrange("p (c t) -> p c t", c=4)
    convb = CF[:, o_cb:o_cb + 4]
    gq = CF[:, o_gq:o_gq + 2]
    gk = CF[:, o_gk:o_gk + 2]
    sink = CF[:, o_sink:o_sink + 8]
    relrep = CF[:, 232:240]

    XH = MX[:, 0:2048].rearrange("p (c t) -> p c t", c=8)
    MIXHI = MX[:, :].bitcast(BF16).rearrange("p (c t) -> p c t", c=4)
    U = UQ[:, 0:6 * 2304].rearrange("p (f t) -> p f t", f=6)
    QT = UQ[:, 0:8192].rearrange("p (h t) -> p h t", h=4)
    KT = UQ[:, 8192:8192 + 2304]
    VA = UQ[:, 10496:10496 + 4608].rearrange("p (b g d) -> p b g d", b=18, g=2)
    S_t = UQ[:, 10496:10496 + 4100].bitcast(F32)
    C.E8 = UQ[:, 0:8192].bitcast(F32).rearrange("p (h m) -> p h m", h=8)
    EXPB = P.WBf[:, :, :].rearrange("p a (b q) -> p (a b) q", q=128).rearrange("p (h b) q -> p h b q", h=8)

    tiles5 = [(0, 512), (512, 512), (1024, 512), (1536, 512), (2048, 256)]
    tiles4 = tiles5[:4]

    class XAct:
        def __init__(self, tiles):
            self.tiles = tiles
            self.b = XB[:len(tiles)]

        def ap(self, c, j):
            if j < 4:
                return XT[:, c, j * 512:(j + 1) * 512]
            return XH[:, c, :]
    XB = [Buf(f"x{j}") for j in range(5)]
    X5, X4 = XAct(tiles5), XAct(tiles4)
    H5 = Act(Ht, tiles5, "h")
    H4 = Act(Ht, tiles4, "h")
    H4.b = H5.b[:4]
    Ub = [[Buf(f"u{f}_{j}") for j in range(5)] for f in range(6)]

    k.dma("sp", [(CF[:], cf_d)], writes=[C.cb])
    k.dma("sp", [(OH[:], oh_d), (VM[:], vm_d)], writes=[C.cb], sembuf=C.cb)
    for j in range(4):
        k.dma("sp", [(XT[:, :, j * 512:(j + 1) * 512], xT_d[:, :, j * 512:(j + 1) * 512])], writes=[XB[j]])
    k.dma("sp", [(XH, xH_d)], writes=[XB[4]])
    cst = Buf("cst")
    k.op("dve", lambda e: e.memset(C.ones_bf[:], 1.0), writes=[cst])
    k.op("dve", lambda e: e.memset(C.ones_f[:], 1.0), writes=[cst])
    k.op("dve", lambda e: e.memset(C.eps[:], EPS), writes=[cst])
    k.op("dve", lambda e: e.memset(C.bd_bf[:], 0.0), writes=[cst])
    k.op("dve", lambda e: e.memset(C.bd_bf[0:64, 0:64], 1.0), writes=[cst])
    k.op("dve", lambda e: e.memset(C.bd_bf[64:128, 64:128], 1.0), writes=[cst])
    condb = Buf("cond")
    k.op("act", lambda e: e.activation(out=condbf[:], in_=cT, func=AF.Silu), reads=[C.cb], writes=[condb])
    k.op("act", lambda e: e.activation(out=EXS[:], in_=sink, func=AF.Exp), reads=[C.cb], writes=[cst])
    k.op("dve", lambda e: e.tensor_copy(out=C.eps[:], in_=C.eps[:]), reads=[cst, C.cb], writes=[C.cb])

    def layer_mod(l):
        emit_mod(P, condbf, condb, wmod_d[l], bmod[:, l, :], MODV[:, l, :], C.modb, AV[:, l], normg[:, l])

    def mcol(l, i, kind):
        return MODV[:, l, (i * 3 + kind) * 8:(i * 3 + kind + 1) * 8]

    def mix_ap(fc, c0, w):
        if fc < 4:
            return Ht[:, fc, c0:c0 + w]
        return MIXHI[:, fc - 4, c0:c0 + w]

    def finish():
        ob = Sink("out")
        for j in range(4):
            k.dma("sp", [(xo_d[:, :, j * 512:(j + 1) * 512], XT[:, :, j * 512:(j + 1) * 512])], reads=[XB[j]],
                  sembuf=XB[j], sink=ob)
        k.wait_all("sp", [ob])
        k.close()
        return nc

    layer_mod(0)
    emit_adaln(P, X5, H5, AV[:, 0, 0], mcol(0, 0, 0), C)
    emit_ffn(P, X5, H5, U, Ub, w1_d[0, 0], w3_d[0, 0], w2_d[0, 0], mcol(0, 0, 2), C)
    if stop_after == "ffn0":
        return finish()
    k.barrier()
    ohb = C.cb
    scrb = emit_expb(P, C, relrep, C.cb, OH, VM, scr, EXPB, P.WBb[0])
    k.barrier(dbufs=[scrb])
    if stop_after == "expb":
        dbg = nc.dram_tensor("dbg", [128, 3072], F32, kind="ExternalOutput").ap()
        ob2 = Buf("dbg")
        k.dma("sp", [(dbg, P.WBf[:].rearrange("p a b -> p (a b)"))], reads=[P.WBb[0]], writes=[ob2], sembuf=ob2)
        k.wait_all("sp", [ob2])
        return finish()
    emit_adaln(P, X5, H5, AV[:, 0, 1], mcol(0, 1, 0), C)
    k.barrier()
    MIXb = [Buf(f"mix{j}") for j in range(4)]
    Sb = Buf("S")
    emit_conv0(P, C, H5, mix_ap, MIXb, win_d[0], convw, convb, flags, S_t, Sb)
    k.barrier()
    QTb = [Buf(f"qt{j}") for j in range(4)]
    KTb = [Buf(f"kt{j}") for j in range(5)]
    VAb = Buf("va")
    k.op("dve", lambda e: e.memset(VA[:, 0:16, 0, 64:128], 1.0), writes=[VAb])
    k.op("dve", lambda e: e.memset(VA[:, 0:16, 1, 0:64], 1.0), writes=[VAb])
    emit_qkv0(P, C, H5, win_d[0], QT, QTb, KT, KTb, VA, VAb, gq, gk, flags)
    k.barrier()
    emit_attn0(P, C, QT, QTb, KT, KTb, VA, VAb, EXPB, P.WBb[0], EXS, Ht, MIXb)
    for i in (1, 2):
        P.WBb[i].r = dict(P.WBb[0].r)
        P.WBb[i].w = P.WBb[0].w
    if stop_after == "mix":
        dbg = nc.dram_tensor("dbg", [128, 8, 2048], BF16, kind="ExternalOutput").ap()
        ob2 = Buf("dbg")
        k.dma("sp", [(dbg[:, 0:4, :], Ht[:, 0:4, 0:2048]), (dbg[:, 4:8, :], MIXHI)], reads=MIXb, writes=[ob2], sembuf=ob2)
        k.wait_all("sp", [ob2])
        return finish()
    emit_outproj(P, C, X4, mix_ap, MIXb, wout_d[0], mcol(0, 1, 2))
    if stop_after == "mixer":
        return finish()
    k.barrier()
    emit_adaln(P, X4, H4, AV[:, 0, 2], mcol(0, 2, 0), C)
    emit_ffn(P, X4, H4, U, Ub, w1_d[0, 1], w3_d[0, 1], w2_d[0, 1], mcol(0, 2, 2), C)
    if not do_l1:
        return finish()

    rope_d = dt_in("rope", [128, 2, 2048])
    qo_d = nc.dram_tensor("qo", [128, 4, 2048], BF16, kind="ExternalOutput").ap()
    ko_d = nc.dram_tensor("ko", [128, 2048], BF16, kind="ExternalOutput").ap()
    vo_d = nc.dram_tensor("vo", [16, 128, 128], BF16, kind="ExternalOutput").ap()
    hco_d = nc.dram_tensor("hco", [12, 128, 2048], F32, kind="ExternalOutput").ap()
    layer_mod(1)
    emit_adaln(P, X4, H4, AV[:, 1, 0], mcol(1, 0, 0), C)
    emit_ffn(P, X4, H4, U, Ub, w1_d[1, 0], w3_d[1, 0], w2_d[1, 0], mcol(1, 0, 2), C)
    k.barrier()
    ROPE = UQ[:, 0:8192].bitcast(F32).rearrange("p (a t) -> p a t", a=2)
    ropeb = Buf("rope")
    k.dma("sp", [(ROPE, rope_d)], writes=[ropeb])
    emit_adaln(P, X4, H4, AV[:, 1, 1], mcol(1, 1, 0), C)
    outb = Sink("outs")
    modo_d = nc.dram_tensor("modo", [128, 96], F32, kind="ExternalOutput").ap()
    k.dma("sp", [(modo_d[:, 0:72], MODV[:, 1, :]), (modo_d[:, 72:96], AV[:, 1].rearrange("p i c -> p (i c)"))],
          reads=[C.modb], sembuf=C.modb, sink=outb)
    emit_inproj1(P, C, H4, win_d[1], CF[:, 240:241], CF[:, 241:242], ROPE[:, 0, :], ROPE[:, 1, :], ropeb,
                 qo_d, ko_d, vo_d, hco_d, outb)
    k.wait_all("sp", [outb])
    return finish()


def host_consts():
    idx = np.arange(512)
    rel = 255 - idx
    valid = (np.abs(rel) <= 128) & (idx < 511)
    bucket = t5_bucket_np(rel.astype(np.int64))
    oh = np.zeros((32, 512), np.float32)
    oh[bucket[valid], idx[valid]] = 1.0
    vm = np.tile(valid.astype(np.float32)[None, :], (128, 1))
    return oh, vm


def fm(v):
    v = np.asarray(v, np.float32)
    sh = v.shape
    v = v.reshape(sh[:-1] + (sh[-1] // 128, 128))
    return np.moveaxis(v, -1, 0)


def prep_LA(inp):
    x = np.asarray(inp["x"], np.float32)
    qperm = np.concatenate([np.r_[hh * 64:(hh + 1) * 64, (4 + hh) * 64:(5 + hh) * 64] for hh in range(4)])
    win = np.ascontiguousarray(np.asarray(inp["w_in"], np.float32))
    win = np.concatenate([win[:, :, qperm], win[:, :, 512:]], axis=2)
    eo = np.r_[0:64:2, 1:64:2]
    eo_q = np.concatenate([h * 64 + eo for h in range(8)])
    eo_k = np.concatenate([512 + h * 64 + eo for h in range(2)])
    win[1] = np.concatenate([win[1][:, eo_q], win[1][:, eo_k], win[1][:, 640:]], axis=1)
    wout = np.asarray(inp["w_out"], np.float32)
    wout = np.ascontiguousarray(np.concatenate([wout[:, qperm, :], wout[:, 512:, :]], axis=1))
    pad = DFFP - DFF
    w1 = np.pad(np.asarray(inp["ffn_w1"], np.float32), ((0, 0), (0, 0), (0, 0), (0, pad)))
    w3 = np.pad(np.asarray(inp["ffn_w3"], np.float32), ((0, 0), (0, 0), (0, 0), (0, pad)))
    w2 = np.pad(np.asarray(inp["ffn_w2"], np.float32), ((0, 0), (0, 0), (0, pad), (0, 0)))
    wmod = np.ascontiguousarray(np.asarray(inp["w_mod"], np.float32))
    oh, vm = host_consts()
    shared = dict(oh=oh, vm=vm, wmod=wmod, w1=w1, w3=w3, w2=w2, win=np.ascontiguousarray(win), wout=wout)
    maps = []
    for core in range(NCORE):
        b, qd = core // 4, core % 4
        t0 = qd * TOK
        xs = x[b, t0:t0 + TOK]
        xT = np.ascontiguousarray(xs.T.reshape(8, 128, TOK).transpose(1, 0, 2))
        halo = np.zeros((256, D), np.float32)
        fl = np.zeros((2,), np.float32)
        if qd > 0:
            halo[0:128] = x[b, t0 - 128:t0]
            fl[0] = 1.0
        if qd < 3:
            halo[128:256] = x[b, t0 + TOK:t0 + TOK + 128]
            fl[1] = 1.0
        xH = np.ascontiguousarray(halo.T.reshape(8, 128, 256).transpose(1, 0, 2))
        cf = np.zeros((128, 1400), np.float32)
        cf[:, 0:8] = fm(inp["c"][b])
        cf[:, 8:10] = fl[None, :]
        cf[:, 10:58] = fm(inp["norm_g"]).reshape(128, 48)
        cf[:, 58:202] = fm(inp["b_mod"]).reshape(128, 144)
        cw = np.asarray(inp["b_conv_w"], np.float32)[0]
        cf[:, 202:214] = fm(cw).transpose(0, 2, 1).reshape(128, 12)
        cf[:, 214:218] = fm(np.asarray(inp["b_conv_b"], np.float32)[0])
        aq = np.asarray(inp["a_qk_g"], np.float32)[0]
        cf[:, 218] = np.tile(aq[0], 2)
        cf[:, 220] = np.tile(aq[1], 2)
        cf[:, 222:230] = np.asarray(inp["a_sink"], np.float32)[0][None, :]
        cf[0:32, 232:240] = np.asarray(inp["rel_table"], np.float32)
        cg = np.asarray(inp["c_qk_g"], np.float32)[0]
        cf[:, 240] = np.tile(cg[0][eo], 2)
        cf[:, 241] = np.tile(cg[1][eo], 2)
        pos = np.arange(t0, t0 + TOK)
        row = (pos // 64).astype(np.float32)
        col = (pos % 64).astype(np.float32)
        inv = (np.float32(10000.0) ** (-np.arange(0, 32, 2, dtype=np.float32) / np.float32(32))).astype(np.float32)
        ang = np.concatenate([row[:, None] * inv, col[:, None] * inv], axis=-1).astype(np.float32)
        cs = np.cos(ang).astype(np.float32).T
        sn = np.sin(ang).astype(np.float32).T
        rope = np.zeros((128, 2, TOK), np.float32)
        for qd4 in range(4):
            rope[qd4 * 32:(qd4 + 1) * 32, 0] = cs
            rope[qd4 * 32:(qd4 + 1) * 32, 1] = sn if qd4 % 2 == 0 else -sn
        m_rope = rope
        m = dict(shared)
        m.update(xT=xT, xH=xH, cf=cf, rope=m_rope)
        maps.append(m)
    return maps


def emit_rope(P, C, qn, qnb, CS, SNs, ropeb, c0, w, out_ap, out_b):
    k = P.k
    t1, t1b = C.tmp()
    k.op("dve", lambda e: e.tensor_tensor(out=t1[:, 0:w], in0=qn[:, 0:w], in1=CS[:, c0:c0 + w], op=ALU.mult),
         reads=[qnb, ropeb], writes=[t1b])
    t2, t2b = C.tmp()
    for qd in range(4):
        src = qd ^ 1
        k.op("dve", lambda e, qd=qd, src=src: e.tensor_tensor(
            out=t2[qd * 32:(qd + 1) * 32, 0:w], in0=qn[src * 32:(src + 1) * 32, 0:w],
            in1=SNs[src * 32:(src + 1) * 32, c0:c0 + w], op=ALU.mult), reads=[qnb, ropeb], writes=[t2b])
    k.op("dve", lambda e: e.tensor_tensor(out=out_ap, in0=t1[:, 0:w], in1=t2[:, 0:w], op=ALU.add),
         reads=[t1b, t2b], writes=[out_b])


def emit_inproj1(P, C, H, win_l, gq, gk, CS, SNs, ropeb, qo_d, ko_d, vo_d, hco_d, outb):
    k = P.k
    for pr in range(3):
        wa, wab = P.load_wa(win_l[:, pr * 256:(pr + 1) * 256])
        for cl in range(2):
            if pr == 2 and cl == 1:
                break
            for j in range(4):
                c0, w = H.tiles[j]
                ps, psb = P.bank("h1", [0, 1, 2, 3])
                emit_inproj_tile(P, H, j, wa, wab, cl, ps, psb)
                qn, qnb = C.tmp()
                emit_qknorm(P, C, ps, psb, w, (gq if pr < 2 else gk)[:, 0:1], qn[:, 0:w], qnb)
                st, stb = C.tmpbf()
                emit_rope(P, C, qn, qnb, CS, SNs, ropeb, c0, w, st[:, 0:w], stb)
                dst = qo_d[:, pr * 2 + cl, c0:c0 + w] if pr < 2 else ko_d[:, c0:c0 + w]
                k.dma("sp", [(dst, st[:, 0:w])], reads=[stb], sembuf=stb, sink=outb)
        if pr == 2:
            for tb in range(16):
                j = tb // 4
                ps, psb = P.bank("o", [4, 5])
                for c in range(8):
                    k.op("pe", lambda e, c=c, tb=tb, ps=ps: e.matmul(
                        ps[:, 0:128], lhsT=H.t[:, c, tb * 128:(tb + 1) * 128], rhs=wa[:, c, 128:256],
                        start=(c == 0), stop=(c == 7)), reads=[wab, H.b[j]], writes=[psb])
                st, stb = C.tmpbf()
                k.op("act", lambda e, ps=ps, st=st: e.activation(out=st[:, 0:128], in_=ps[:, 0:128], func=AF.Copy),
                     reads=[psb], writes=[stb])
                k.dma("sp", [(vo_d[tb], st[:, 0:128])], reads=[stb], sembuf=stb, sink=outb)
    for pr in range(6):
        wa, wab = P.load_wa(win_l[:, 768 + pr * 256:768 + (pr + 1) * 256])
        for cl in range(2):
            ch = pr * 2 + cl
            for j in range(4):
                c0, w = H.tiles[j]
                ps, psb = P.bank("h1", [0, 1, 2, 3])
                emit_inproj_tile(P, H, j, wa, wab, cl, ps, psb)
                st, stb = C.tmp()
                k.op("act", lambda e, ps=ps, st=st, w=w: e.activation(out=st[:, 0:w], in_=ps[:, 0:w], func=AF.Copy),
                     reads=[psb], writes=[stb])
                k.dma("sp", [(hco_d[ch, :, c0:c0 + w], st[:, 0:w])], reads=[stb], sembuf=stb, sink=outb)


NFFT = 16384


def build_LB():
    nc = bass.Bass("TRN2", target_bir_lowering=False)
    k = KB(nc)
    dt_in = lambda name, shape: nc.dram_tensor(name, list(shape), F32, kind="ExternalInput").ap()
    hc_d = dt_in("hc3", [3, 128, S])
    cf_d = dt_in("cf2", [128, 32])
    mlp_d = dt_in("mlpw", [64, 456])
    zemb_d = dt_in("zemb", [33, NFFT])
    win_d = dt_in("win", [128, 128, 128])
    dft_d = dt_in("dftc", [128, 512])
    tw_d = dt_in("tw", [128, 2, 2, 128])
    y_d = nc.dram_tensor("yT", [128, S], F32, kind="ExternalOutput").ap()
    zs = nc.dram_tensor("zs", [128, S], F32, kind="Internal")
    cs = nc.dram_tensor("cs", [128, S], F32, kind="Internal")

    PS = [k.ps(f"ps{i}", [128, 512]) for i in range(8)]
    PSb = [Buf(f"ps{i}") for i in range(8)]
    rr = {}

    def bank(group, banks):
        i = rr.get(group, 0)
        rr[group] = (i + 1) % len(banks)
        return PS[banks[i]], PSb[banks[i]]

    tmp = RotTmp(k, "tf", 8, F32)
    tmpbf = RotTmp(k, "tb", 4, BF16)
    cb = Buf("consts")
    CF = k.sb("CF", [128, 32], F32)
    MLP = k.sb("MLP", [64, 456], F32)
    DFT = k.sb("DFT", [128, 512], BF16)
    TW = k.sb("TW", [128, 2, 2, 128], F32)
    X0 = k.sb("X0", [128, S], F32)
    Z = k.sb("Z", [128, S], F32)
    IN = k.sb("IN", [128, S + 2], F32)
    KC = k.sb("KC", [128, 128, 128], BF16)
    ZC = k.sb("ZC", [64, 128, 128], BF16)
    ACC = k.sb("ACC", [128, 128], F32)
    SM = k.sb("SM", [128, 16], F32)
    ones_f = k.sb("ones_f", [128, 1], F32)
    X0b, Zb, INb, KCb, ZCb, ACCb, SMb = [Buf(n) for n in "X0 Z IN KC ZC ACC SM".split()]

    k.dma("sp", [(CF[:], cf_d), (MLP[:], mlp_d), (TW[:], tw_d)], writes=[cb])
    dftb = Buf("dft")
    k.dma("pool", [(DFT[:], dft_d)], writes=[dftb])
    Fre, Fim, nFim = DFT[:, 0:128], DFT[:, 128:256], DFT[:, 384:512]
    Fcat, FcatI2, FcatI1 = DFT[:, 0:256], DFT[:, 128:384], DFT[:, 256:512]
    k.op("dve", lambda e: e.memset(ones_f[:], 1.0), writes=[SMb])
    k.op("dve", lambda e: e.memset(SM[:, 0:1], math.pi / 2), writes=[SMb])
    k.op("dve", lambda e: e.memset(SM[:, 1:2], EPS), writes=[SMb])
    k.op("dve", lambda e: e.memset(IN[:, 0:1], 0.0), writes=[INb])
    k.op("dve", lambda e: e.memset(IN[:, S + 1:S + 2], 0.0), writes=[INb])

    CH = 2048
    for part in range(3):
        k.dma("sp", [(IN[:, 1:S + 1], hc_d[part])], writes=[INb])
        for cc in range(S // CH):
            c0 = cc * CH
            wc = CF[:, part * 4:part * 4 + 3]
            bc = CF[:, part * 4 + 3:part * 4 + 4]
            for s0 in range(0, CH, 512):
                a0 = c0 + s0
                if part == 0:
                    o, ob_ = X0[:, a0:a0 + 512], X0b
                elif part == 1:
                    o, ob_ = Z[:, a0:a0 + 512], Zb
                else:
                    tt, ttb = tmp()
                    o, ob_ = tt[:, 0:512], ttb
                k.op("dve", lambda e, o=o, a0=a0, wc=wc, bc=bc: e.tensor_scalar(
                    out=o, in0=IN[:, a0 + 1:a0 + 513], scalar1=wc[:, 1:2], scalar2=bc, op0=ALU.mult, op1=ALU.add),
                    reads=[INb, cb], writes=[ob_])
                k.op("dve", lambda e, o=o, a0=a0, wc=wc: e.scalar_tensor_tensor(
                    out=o, in0=IN[:, a0:a0 + 512], scalar=wc[:, 0:1], in1=o, op0=ALU.mult, op1=ALU.add),
                    reads=[INb, cb, ob_], writes=[ob_])
                k.op("dve", lambda e, o=o, a0=a0, wc=wc: e.scalar_tensor_tensor(
                    out=o, in0=IN[:, a0 + 2:a0 + 514], scalar=wc[:, 2:3], in1=o, op0=ALU.mult, op1=ALU.add),
                    reads=[INb, cb, ob_], writes=[ob_])
                if part == 2:
                    k.op("pool", lambda e, o=o, a0=a0: e.tensor_tensor(
                        out=Z[:, a0:a0 + 512], in0=Z[:, a0:a0 + 512], in1=o, op=ALU.mult), reads=[ob_, Zb], writes=[Zb])
    zsb = Buf("zs")
    k.dma("sp", [(zs.ap(), Z[:])], reads=[Zb], writes=[zsb])
    src = bass.AP(tensor=zs, offset=0, ap=[[128, 64], [S, 128], [1, 128]])
    k.dma("pool", [(ZC[:], src)], reads=[zsb], writes=[ZCb])

    W1, W2, W3 = MLP[0:33, 0:64], MLP[:, 64:128], MLP[:, 128:192]
    W4 = MLP[:, 200:456].rearrange("p (d c) -> p d c", d=2)
    k.op("dve", lambda e: e.tensor_scalar(out=SM[0:64, 2:3], in0=MLP[:, 195:196], scalar1=0.25, scalar2=None, op0=ALU.mult),
         reads=[cb], writes=[SMb])
    for i in range(3):
        k.op("dve", lambda e, i=i: e.tensor_tensor(out=SM[0:64, 3 + i:4 + i], in0=MLP[:, 192 + i:193 + i], in1=SM[0:64, 2:3],
                                                   op=ALU.mult), reads=[cb, SMb], writes=[SMb])
    k.op("dve", lambda e: e.memset(ACC[:], 0.0), writes=[ACCb])

    def sin4(ps, psb, li):
        s1, s1b = tmp()
        k.op("act", lambda e: e.activation(out=s1[0:64, :], in_=ps[0:64, :], func=AF.Sin, bias=SM[0:64, 3 + li:4 + li],
                                           scale=SM[0:64, 2:3]), reads=[psb, SMb], writes=[s1b])
        a1, a1b = tmp()
        k.op("act", lambda e: e.activation(out=a1[0:64, :], in_=ps[0:64, :], func=AF.Abs, bias=SM[0:64, 3 + li:4 + li],
                                           scale=SM[0:64, 2:3]), reads=[psb, SMb], writes=[a1b])
        k.op("act", lambda e: e.activation(out=a1[0:64, :], in_=a1[0:64, :], func=AF.Sin, bias=SM[0:64, 0:1], scale=-1.0),
             reads=[a1b, SMb], writes=[a1b])
        k.op("dve", lambda e: e.tensor_tensor(out=a1[0:64, :], in0=a1[0:64, :], in1=s1[0:64, :], op=ALU.mult),
             reads=[a1b, s1b], writes=[a1b])
        k.op("dve", lambda e: e.tensor_tensor(out=s1[0:64, :], in0=s1[0:64, :], in1=s1[0:64, :], op=ALU.mult),
             reads=[s1b], writes=[s1b])
        k.op("dve", lambda e: e.tensor_scalar(out=s1[0:64, :], in0=s1[0:64, :], scalar1=-2.0, scalar2=1.0,
                                              op0=ALU.mult, op1=ALU.add), reads=[s1b], writes=[s1b])
        k.op("dve", lambda e: e.scalar_tensor_tensor(out=a1[0:64, :], in0=a1[0:64, :], scalar=4.0, in1=s1[0:64, :],
                                                     op0=ALU.mult, op1=ALU.mult), reads=[a1b, s1b], writes=[a1b])
        return a1, a1b

    for c in range(32):
        ze, zeb = tmp()
        k.dma("sp", [(ze[0:33, :], zemb_d[:, c * 512:(c + 1) * 512])], writes=[zeb])
        wt, wtb = tmp()
        k.dma("sp", [(wt[:, :].rearrange("p (n c) -> p n c", n=4), win_d[:, c * 4:(c + 1) * 4, :])], writes=[wtb])
        ps, psb = bank("m", [6, 7])
        k.op("pe", lambda e, ps=ps, ze=ze: e.matmul(ps[0:64, :], lhsT=W1, rhs=ze[0:33, :], start=True, stop=True),
             reads=[zeb, cb], writes=[psb])
        h, hb = sin4(ps, psb, 0)
        for li, W in ((1, W2), (2, W3)):
            ps, psb = bank("m", [6, 7])
            k.op("pe", lambda e, ps=ps, h=h, W=W: e.matmul(ps[0:64, :], lhsT=W, rhs=h[0:64, :], start=True, stop=True),
                 reads=[hb, cb], writes=[psb])
            h, hb = sin4(ps, psb, li)
        ps4, ps4b = bank("m", [6, 7])
        for n2l in range(4):
            for d in range(2):
                k.op("pe", lambda e, ps4=ps4, h=h, n2l=n2l, d=d: e.matmul(
                    ps4[d * 64:(d + 1) * 64, n2l * 128:(n2l + 1) * 128],
                    lhsT=h[0:64, n2l * 128 + d * 64:n2l * 128 + (d + 1) * 64], rhs=W4[:, d, :],
                    start=True, stop=True, skip_group_check=True), reads=[hb, cb], writes=[ps4b])
        kc, kcb = tmp()
        k.op("dve", lambda e, kc=kc, ps4=ps4, wt=wt: e.tensor_tensor(out=kc[:, :], in0=ps4[:, :], in1=wt[:, :], op=ALU.mult),
             reads=[ps4b, wtb], writes=[kcb])
        k.op("act", lambda e, kc=kc, c=c: e.activation(
            out=KC[:, :, c * 4:(c + 1) * 4], in_=kc[:, :].rearrange("p (n c) -> p c n", n=4), func=AF.Copy),
            reads=[kcb], writes=[KCb])
        sq, sqb = tmp()
        k.op("pool", lambda e, kc=kc, sq=sq: e.tensor_tensor(out=sq[:, :], in0=kc[:, :], in1=kc[:, :], op=ALU.mult),
             reads=[kcb], writes=[sqb])
        rd, rdb = tmp()
        k.op("dve", lambda e, sq=sq, rd=rd: e.tensor_reduce(
            out=rd[:, 0:128], in_=sq[:, :].rearrange("p (n c) -> p c n", n=4), axis=mybir.AxisListType.X, op=ALU.add),
            reads=[sqb], writes=[rdb])
        k.op("pool", lambda e, rd=rd: e.tensor_tensor(out=ACC[:], in0=ACC[:], in1=rd[:, 0:128], op=ALU.add),
             reads=[rdb, ACCb], writes=[ACCb])
    ps, psb = bank("m", [6, 7])
    k.op("pe", lambda e, ps=ps: e.matmul(ps[:, 0:1], lhsT=ACC[:], rhs=ones_f[:, 0:1], start=True, stop=True),
         reads=[ACCb, SMb], writes=[psb])
    k.op("act", lambda e, ps=ps: e.activation(out=SM[:, 8:9], in_=ps[:, 0:1], func=AF.Sqrt, bias=SM[:, 1:2], scale=1.0),
         reads=[psb, SMb], writes=[SMb])
    k.op("dve", lambda e: e.reciprocal(out=SM[:, 8:9], in_=SM[:, 8:9]), reads=[SMb], writes=[SMb])

    Bre = k.sb("Bre", [128, 4, 128], BF16)
    Bim = k.sb("Bim", [128, 4, 128], BF16)
    Hre = k.sb("Hre", [128, 512], F32)
    Him = k.sb("Him", [128, 512], F32)
    Yre = k.sb("Yre", [128, 4, 128], BF16)
    Yim = k.sb("Yim", [128, 4, 128], BF16)
    Qre = k.sb("Qre", [128, 4, 128], BF16)
    Qim = k.sb("Qim", [128, 4, 128], BF16)
    Bb, Hb, Yb, Qb = Buf("B"), Buf("H"), Buf("Y"), Buf("Q")
    TWre2, TWim2 = TW[:, 0], TW[:, 1]

    def cmul_from_psum(A, Ab, sign, outre, outim, outb, pr):
        A4 = A[:, :].rearrange("p (c r k) -> p c r k", c=2, r=2)
        Are, Aim = A4[:, :, 0, :], A4[:, :, 1, :]
        t1, t1b = tmp()
        t2, t2b = tmp()
        v = lambda t: t[:, 0:256].rearrange("p (c k) -> p c k", c=2)
        k.op("dve", lambda e: e.tensor_tensor(out=v(t1), in0=Are, in1=TWre2, op=ALU.mult), reads=[Ab, cb], writes=[t1b])
        k.op("dve", lambda e: e.tensor_tensor(out=v(t2), in0=Aim, in1=TWim2, op=ALU.mult), reads=[Ab, cb], writes=[t2b])
        k.op("pool", lambda e: e.tensor_tensor(out=outre[:, pr * 2:pr * 2 + 2, :], in0=v(t1), in1=v(t2),
                                               op=(ALU.subtract if sign > 0 else ALU.add)),
             reads=[t1b, t2b], writes=[outb])
        t3, t3b = tmp()
        t4, t4b = tmp()
        k.op("dve", lambda e: e.tensor_tensor(out=v(t3), in0=Aim, in1=TWre2, op=ALU.mult), reads=[Ab, cb], writes=[t3b])
        k.op("dve", lambda e: e.tensor_tensor(out=v(t4), in0=Are, in1=TWim2, op=ALU.mult), reads=[Ab, cb], writes=[t4b])
        k.op("pool", lambda e: e.tensor_tensor(out=outim[:, pr * 2:pr * 2 + 2, :], in0=v(t3), in1=v(t4),
                                               op=(ALU.add if sign > 0 else ALU.subtract)),
             reads=[t3b, t4b], writes=[outb])

    def fft_fwd(src, srcb, krows, ch0, xre, xreb, xim, ximb):
        for pr in range(2):
            A, Ab = bank("A", [0, 1])
            for cl in range(2):
                ch = ch0 + pr * 2 + cl
                k.op("pe", lambda e, A=A, cl=cl, ch=ch: e.matmul(
                    A[:, cl * 256:(cl + 1) * 256], lhsT=src[0:krows, ch, :], rhs=Fcat[0:krows, :],
                    start=True, stop=True, skip_group_check=True), reads=[srcb, dftb], writes=[Ab])
            cmul_from_psum(A, Ab, +1, Bre, Bim, Bb, pr)
        bre = Bre[:, :, :].rearrange("p c k -> p (c k)")
        bim = Bim[:, :, :].rearrange("p c k -> p (c k)")
        k.op("pe", lambda e: e.matmul(xre[:, :], lhsT=Fre, rhs=bre, start=True, stop=False), reads=[Bb, dftb], writes=[xreb])
        k.op("pe", lambda e: e.matmul(xre[:, :], lhsT=nFim, rhs=bim, start=False, stop=True), reads=[Bb, dftb], writes=[xreb])
        k.op("pe", lambda e: e.matmul(xim[:, :], lhsT=Fim, rhs=bre, start=True, stop=False), reads=[Bb, dftb], writes=[ximb])
        k.op("pe", lambda e: e.matmul(xim[:, :], lhsT=Fre, rhs=bim, start=False, stop=True), reads=[Bb, dftb], writes=[ximb])

    csb = Sink("cs")
    for g in range(32):
        ch0 = g * 4
        fft_fwd(KC, KCb, 128, ch0, PS[2], PSb[2], PS[3], PSb[3])
        k.op("act", lambda e: e.activation(out=Hre[:], in_=PS[2][:, :], func=AF.Copy), reads=[PSb[2]], writes=[Hb])
        k.op("act", lambda e: e.activation(out=Him[:], in_=PS[3][:, :], func=AF.Copy), reads=[PSb[3]], writes=[Hb])
        fft_fwd(ZC, ZCb, 64, ch0, PS[4], PSb[4], PS[5], PSb[5])
        t1, t1b = tmp()
        t2, t2b = tmp()
        k.op("dve", lambda e, t1=t1: e.tensor_tensor(out=t1[:, :], in0=PS[4][:, :], in1=Hre[:], op=ALU.mult),
             reads=[PSb[4], Hb], writes=[t1b])
        k.op("dve", lambda e, t2=t2: e.tensor_tensor(out=t2[:, :], in0=PS[5][:, :], in1=Him[:], op=ALU.mult),
             reads=[PSb[5], Hb], writes=[t2b])
        k.op("pool", lambda e, t1=t1, t2=t2: e.tensor_tensor(out=Yre[:, :, :].rearrange("p c k -> p (c k)"), in0=t1[:, :],
                                                             in1=t2[:, :], op=ALU.subtract), reads=[t1b, t2b], writes=[Yb])
        t3, t3b = tmp()
        t4, t4b = tmp()
        k.op("dve", lambda e, t3=t3: e.tensor_tensor(out=t3[:, :], in0=PS[4][:, :], in1=Him[:], op=ALU.mult),
             reads=[PSb[4], Hb], writes=[t3b])
        k.op("dve", lambda e, t4=t4: e.tensor_tensor(out=t4[:, :], in0=PS[5][:, :], in1=Hre[:], op=ALU.mult),
             reads=[PSb[5], Hb], writes=[t4b])
        k.op("pool", lambda e, t3=t3, t4=t4: e.tensor_tensor(out=Yim[:, :, :].rearrange("p c k -> p (c k)"), in0=t3[:, :],
                                                             in1=t4[:, :], op=ALU.add), reads=[t3b, t4b], writes=[Yb])
        for pr in range(2):
            Pk, Pkb = bank("A", [0, 1])
            for cl in range(2):
                c4 = pr * 2 + cl
                k.op("pe", lambda e, Pk=Pk, cl=cl, c4=c4: e.matmul(
                    Pk[:, cl * 256:(cl + 1) * 256], lhsT=Yre[:, c4, :], rhs=FcatI1, start=True, stop=False,
                    skip_group_check=True), reads=[Yb, dftb], writes=[Pkb])
                k.op("pe", lambda e, Pk=Pk, cl=cl, c4=c4: e.matmul(
                    Pk[:, cl * 256:(cl + 1) * 256], lhsT=Yim[:, c4, :], rhs=FcatI2, start=False, stop=True,
                    skip_group_check=True), reads=[Yb, dftb], writes=[Pkb])
            cmul_from_psum(Pk, Pkb, -1, Qre, Qim, Qb, pr)
        yo, yob = PS[6], PSb[6]
        k.op("pe", lambda e: e.matmul(yo[0:64, :], lhsT=DFT[:, 0:64], rhs=Qre[:, :, :].rearrange("p c k -> p (c k)"),
                                      start=True, stop=False), reads=[Qb, dftb], writes=[yob])
        k.op("pe", lambda e: e.matmul(yo[0:64, :], lhsT=DFT[:, 128:192], rhs=Qim[:, :, :].rearrange("p c k -> p (c k)"),
                                      start=False, stop=True), reads=[Qb, dftb], writes=[yob])
        ys, ysb = tmp()
        k.op("act", lambda e, ys=ys: e.activation(out=ys[0:64, :], in_=yo[0:64, :], func=AF.Copy, scale=1.0 / NFFT),
             reads=[yob], writes=[ysb])
        dst = bass.AP(tensor=cs, offset=ch0 * S, ap=[[128, 64], [S, 4], [1, 128]])
        k.dma("sp", [(dst, ys[0:64, :].rearrange("p (c k) -> p c k", c=4))], reads=[ysb], sembuf=ysb, sink=csb)

    k.wait_all("sp", [csb])
    k.dma("sp", [(IN[:, 0:S], cs.ap())], writes=[INb])
    ob = Sink("out")
    for s0 in range(0, S, 512):
        t, tb = tmp()
        k.op("dve", lambda e, t=t, s0=s0: e.tensor_scalar(out=t[:, :], in0=Z[:, s0:s0 + 512], scalar1=CF[:, 12:13],
                                                          scalar2=None, op0=ALU.mult), reads=[Zb, cb], writes=[tb])
        k.op("dve", lambda e, t=t, s0=s0: e.scalar_tensor_tensor(out=t[:, :], in0=IN[:, s0:s0 + 512], scalar=SM[:, 8:9],
                                                                 in1=t[:, :], op0=ALU.mult, op1=ALU.add),
             reads=[INb, SMb, tb], writes=[tb])
        k.op("pool", lambda e, t=t, s0=s0: e.tensor_tensor(out=t[:, :], in0=t[:, :], in1=X0[:, s0:s0 + 512], op=ALU.mult),
             reads=[tb, X0b], writes=[tb])
        k.dma("sp", [(y_d[:, s0:s0 + 512], t[:, :])], reads=[tb], sembuf=tb, sink=ob)
    k.wait_all("sp", [ob])
    k.close()
    return nc


def host_consts_LB(cq):
    f32 = np.float32
    L = S
    m = np.arange(NFFT)
    lag = np.where(m < L, m, NFFT - m)
    lag = np.where(m == L, 0, lag)
    t_all = np.linspace(0.0, 1.0, L, dtype=f32)
    t = t_all[lag]
    w = (f32(2.0 * math.pi / L) * lag.astype(f32)).astype(f32)
    fr = np.linspace(1e-4, 15, 16, dtype=f32)
    zf = np.concatenate([t[:, None], np.cos(fr[None, :] * w[:, None]), -np.sin(fr[None, :] * w[:, None])], axis=-1).astype(f32)
    zemb = np.ascontiguousarray(zf.reshape(128, 128, 33).transpose(2, 1, 0).reshape(33, NFFT))
    dmin, dmax = math.log(1e-2) / 0.3, math.log(1e-2) / 1.5
    deltas = np.abs(np.linspace(dmin, dmax, 512, dtype=f32))[cq * 128:(cq + 1) * 128]
    win = np.exp(-t[:, None] * deltas[None, :]).astype(f32)
    win[L] = 0.0
    win = np.ascontiguousarray(win.reshape(128, 128, 128))
    n = np.arange(128)
    ang = 2.0 * np.pi * np.outer(n, n) / 128.0
    fre, fim = np.cos(ang), -np.sin(ang)
    dftc = np.concatenate([fre, fim, fre, -fim], axis=1).astype(f32)
    ang2 = 2.0 * np.pi * np.outer(n, n) / NFFT
    tw = np.stack([np.cos(ang2), -np.sin(ang2)], 0).astype(f32)
    tw = np.ascontiguousarray(np.broadcast_to(tw[:, None], (2, 2, 128, 128)).transpose(2, 0, 1, 3))
    return zemb, win, dftc, tw


def prep_LB(inp, hco_all):
    maps = []
    cache = {}
    for core in range(NCORE):
        b, cq = core // 4, core % 4
        if cq not in cache:
            cache[cq] = host_consts_LB(cq)
        zemb, win, dftc, tw = cache[cq]
        hc3 = np.zeros((3, 128, S), np.float32)
        for part in range(3):
            for src in range(4):
                hc3[part, :, src * TOK:(src + 1) * TOK] = hco_all[b * 4 + src][part * 4 + cq]
        cf2 = np.zeros((128, 32), np.float32)
        cw = np.asarray(inp["d_conv_w"], np.float32)[0]
        cbias = np.asarray(inp["d_conv_b"], np.float32)[0]
        for part in range(3):
            sl = slice(part * 512 + cq * 128, part * 512 + (cq + 1) * 128)
            cf2[:, part * 4:part * 4 + 3] = cw[:, sl].T
            cf2[:, part * 4 + 3] = cbias[sl]
        cf2[:, 12] = np.asarray(inp["d_skip"], np.float32)[0][cq * 128:(cq + 1) * 128]
        mlp = np.zeros((64, 456), np.float32)
        mlp[0:33, 0:64] = np.asarray(inp["d_f_w1"], np.float32)[0]
        mlp[:, 64:128] = np.asarray(inp["d_f_w2"], np.float32)[0]
        mlp[:, 128:192] = np.asarray(inp["d_f_w3"], np.float32)[0]
        mlp[:, 192] = np.asarray(inp["d_f_b1"], np.float32)[0]
        mlp[:, 193] = np.asarray(inp["d_f_b2"], np.float32)[0]
        mlp[:, 194] = np.asarray(inp["d_f_b3"], np.float32)[0]
        mlp[:, 195] = np.asarray(inp["d_f_freq"], np.float32)[0]
        w4 = np.asarray(inp["d_f_w4"], np.float32)[0]
        mlp[:, 200:328] = w4[:, cq * 128:(cq + 1) * 128]
        mlp[:, 328:456] = w4[:, 512 + cq * 128:512 + (cq + 1) * 128]
        maps.append(dict(hc3=hc3, cf2=cf2, mlpw=mlp, zemb=zemb, win=win, dftc=dftc, tw=tw))
    return maps


def emit_attn1(P, C, QT, QTb, KT, KTb, VA, VAb, MIX, MIXb):
    k = P.k
    for jq in range(4):
        q0 = jq * 512
        for g in range(2):
            gs = slice(g * 64, (g + 1) * 64)
            ds_ = slice((1 - g) * 64, (2 - g) * 64)
            for hh in range(4):
                po, pob = P.bank("o", [4, 5])
                for kb in range(64):
                    ps, psb = P.bank("h1", [0, 1, 2, 3])
                    k.op("pe", lambda e, ps=ps, kb=kb, gs=gs, hh=hh, q0=q0: e.matmul(
                        ps[:, :], lhsT=KT[gs, kb * 128:(kb + 1) * 128], rhs=QT[gs, hh, q0:q0 + 512],
                        start=True, stop=True), reads=[KTb, QTb], writes=[psb])
                    pt, ptb = C.tmpbf()
                    k.op("act", lambda e, ps=ps, pt=pt: e.activation(out=pt[:, :], in_=ps[:, :], func=AF.Exp,
                                                                     scale=HD ** -0.5), reads=[psb], writes=[ptb])
                    k.op("pe", lambda e, po=po, pt=pt, kb=kb, g=g: e.matmul(
                        po[:, :], lhsT=VA[:, kb, g, :], rhs=pt[:, :], start=(kb == 0), stop=(kb == 63)),
                        reads=[VAb, ptb], writes=[pob])
                rd, rdb = C.tmp()
                k.op("dve", lambda e, rd=rd, po=po, ds_=ds_: e.reciprocal(out=rd[ds_, :], in_=po[ds_, :]),
                     reads=[pob], writes=[rdb])
                k.op("dve", lambda e, rd=rd, po=po, gs=gs, ds_=ds_, hh=hh, q0=q0: e.tensor_tensor(
                    out=MIX[gs, hh, q0:q0 + 512], in0=po[gs, :], in1=rd[ds_, :], op=ALU.mult),
                    reads=[pob, rdb], writes=[MIXb[jq]])


def build_LC():
    nc = bass.Bass("TRN2", target_bir_lowering=False)
    k = KB(nc)
    dt_in = lambda name, shape, dt=F32: nc.dram_tensor(name, list(shape), dt, kind="ExternalInput").ap()
    xT_d = dt_in("xT", [128, 8, 2048])
    mod_d = dt_in("modi", [128, 96])
    q_d = dt_in("q", [128, 4, 2048], BF16)
    k_d = dt_in("kk", [128, S], BF16)
    v_d = dt_in("v", [64, 128, 128], BF16)
    y_d = dt_in("yh", [128, 4, 2048])
    w1_d = dt_in("w1", [1024, DFFP])
    w3_d = dt_in("w3", [1024, DFFP])
    w2_d = dt_in("w2", [DFFP, 1024])
    wout_d = dt_in("wout", [1024, 1024])
    xo_d = nc.dram_tensor("xo", [128, 8, 2048], F32, kind="ExternalOutput").ap()

    P = Prog(nc, k)
    C = Common()
    C.tmp = RotTmp(k, "tf", 5, F32)
    C.tmpbf = RotTmp(k, "tb", 4, BF16)
    C.RS = k.sb("RS", [128, 512], F32)
    C.RSb = Buf("RS")
    C.cb = Buf("consts")
    C.modb = Buf("mod")
    C.ones_bf = k.sb("ones_bf", [128, 128], BF16)
    C.eps = k.sb("eps", [128, 1], F32)
    MOD = k.sb("mod", [128, 96], F32)
    XT = k.sb("XT", [128, 8, 2048], F32)
    AR = k.sb("AR", [128, 32768], BF16)
    MIXt = k.sb("MIX", [128, 8, 2048], BF16)
    VA = AR[:, 0:16384].rearrange("p (b g d) -> p b g d", b=64, g=2)
    KT = AR[:, 16384:24576]
    QT = AR[:, 24576:32768].rearrange("p (h t) -> p h t", h=4)
    Ht = AR[:, 0:16384].rearrange("p (c t) -> p c t", c=8)
    U = AR[:, 16384:16384 + 6 * 2048].rearrange("p (f t) -> p f t", f=6)

    tiles4 = [(0, 512), (512, 512), (1024, 512), (1536, 512)]
    XB = [Buf(f"x{j}") for j in range(4)]

    class XAct:
        tiles = tiles4
        b = XB

        def ap(self, c, j):
            return XT[:, c, j * 512:(j + 1) * 512]
    X4 = XAct()
    H4 = Act(Ht, tiles4, "h")
    Ub = [[Buf(f"u{f}_{j}") for j in range(4)] for f in range(6)]
    MIXb = [Buf(f"mix{j}") for j in range(4)]
    QTb, KTb, VAb = Buf("qt"), Buf("kt"), Buf("va")

    k.dma("sp", [(MOD[:], mod_d)], writes=[C.modb])
    k.dma("sp", [(QT, q_d)], writes=[QTb])
    k.dma("sp", [(KT, k_d)], writes=[KTb])
    k.op("dve", lambda e: e.memset(C.ones_bf[:], 1.0), writes=[C.cb])
    k.op("dve", lambda e: e.memset(C.eps[:], EPS), writes=[C.cb])
    k.op("dve", lambda e: e.memset(VA[:, :, 0, 64:128], 1.0), writes=[VAb])
    k.op("dve", lambda e: e.memset(VA[:, :, 1, 0:64], 1.0), writes=[VAb])
    vsrc = v_d.rearrange("b p d -> p b d")
    k.dma("sp", [(VA[:, :, 0, 0:64], vsrc[:, :, 0:64]), (VA[:, :, 1, 64:128], vsrc[:, :, 64:128])], writes=[VAb])
    for j in range(4):
        k.dma("sp", [(XT[:, :, j * 512:(j + 1) * 512], xT_d[:, :, j * 512:(j + 1) * 512])], writes=[XB[j]])
    for j in range(4):
        k.dma("pool", [(MIXt[:, 4:8, j * 512:(j + 1) * 512], y_d[:, :, j * 512:(j + 1) * 512])], writes=[MIXb[j]])

    def mcol(i, kind):
        return MOD[:, (i * 3 + kind) * 8:(i * 3 + kind + 1) * 8]

    emit_attn1(P, C, QT, QTb, KT, KTb, VA, VAb, MIXt, MIXb)
    emit_outproj(P, C, X4, lambda fc, c0, w: MIXt[:, fc, c0:c0 + w], MIXb, wout_d, mcol(1, 2))
    k.barrier()
    emit_adaln(P, X4, H4, MOD[:, 72 + 16:72 + 24], mcol(2, 0), C)
    emit_ffn(P, X4, H4, U, Ub, w1_d, w3_d, w2_d, mcol(2, 2), C)
    ob = Sink("out")
    for j in range(4):
        k.dma("sp", [(xo_d[:, :, j * 512:(j + 1) * 512], XT[:, :, j * 512:(j + 1) * 512])], reads=[XB[j]],
              sembuf=XB[j], sink=ob)
    k.wait_all("sp", [ob])
    k.close()
    return nc


def prep_LC(inp, la_res, lb_res, shared):
    maps = []
    for core in range(NCORE):
        b, qd = core // 4, core % 4
        kk = np.concatenate([la_res[b * 4 + s]["ko"] for s in range(4)], axis=1)
        v = np.concatenate([la_res[b * 4 + s]["vo"] for s in range(4)], axis=0)
        yh = np.stack([lb_res[b * 4 + cq]["yT"][:, qd * TOK:(qd + 1) * TOK] for cq in range(4)], axis=1)
        maps.append(dict(xT=la_res[core]["xo"], modi=la_res[core]["modo"], q=la_res[core]["qo"],
                         kk=np.ascontiguousarray(kk), v=np.ascontiguousarray(v), yh=np.ascontiguousarray(yh),
                         w1=shared["w1"][1, 1], w3=shared["w3"][1, 1], w2=shared["w2"][1, 1], wout=shared["wout"][1]))
    return maps


_CACHE = {}


FUSED = True


def kernel(**inputs):
    inp = {kk: np.asarray(v) for kk, v in inputs.items()}
    cores = list(range(NCORE))
    if FUSED:
        if "fused" not in _CACHE:
            _CACHE["fused"] = build_fused()
        maps = prep_fused(inp)
        res = run_bass_kernel_spmd(_CACHE["fused"], maps, core_ids=cores).results
        out = np.zeros((2, S, D), np.float32)
        for core in cores:
            b, qd = core // 4, core % 4
            xo = np.asarray(res[core]["xo"], np.float32)
            out[b, qd * TOK:(qd + 1) * TOK] = xo.transpose(2, 1, 0).reshape(TOK, D)
        return out
    if "la" not in _CACHE:
        _CACHE["la"] = build_LA(True)
        _CACHE["lb"] = build_LB()
        _CACHE["lc"] = build_LC()
    la_maps = prep_LA(inp)
    la = run_bass_kernel_spmd(_CACHE["la"], la_maps, core_ids=cores).results
    lb_maps = prep_LB(inp, [la[c]["hco"] for c in cores])
    lb = run_bass_kernel_spmd(_CACHE["lb"], lb_maps, core_ids=cores).results
    lc_maps = prep_LC(inp, la, lb, la_maps[0])
    lc = run_bass_kernel_spmd(_CACHE["lc"], lc_maps, core_ids=cores).results
    out = np.zeros((2, S, D), np.float32)
    for core in cores:
        b, qd = core // 4, core % 4
        xo = np.asarray(lc[core]["xo"], np.float32)
        out[b, qd * TOK:(qd + 1) * TOK] = xo.transpose(2, 1, 0).reshape(TOK, D)
    return out


U32 = mybir.dt.uint32


def kb_gather(k, dst_ap, src_dram_ap, idx_ap, reads=(), writes=()):
    sbf = writes[0]
    if sbf.dsem is None:
        sbf.dsem = {}
        sbf.dcnt = {}
    if True not in sbf.dsem:
        sbf.dsem[True] = ("d", id(sbf), True)
        sbf.dcnt[True] = 0
        k.sems[sbf.dsem[True]] = k._newsem(f"d_{k.nsem}")
    k._wait("pool", k._deps(reads, writes))
    k.nc.gpsimd.indirect_dma_start(out=dst_ap, out_offset=None, in_=src_dram_ap,
                                   in_offset=bass.IndirectOffsetOnAxis(ap=idx_ap, axis=0)
                                   ).then_inc(k.sems[sbf.dsem[True]], 16)
    sbf.dcnt[True] += 16
    tok = (sbf.dsem[True], sbf.dcnt[True])
    k._commit(tok, reads, writes)
    return tok


class CC:
    n = 0

    def __init__(self, k, groups, fake):
        self.k, self.groups, self.fake = k, groups, fake
        self.pool_sems = [] if fake else [k._newsem(f"cc{i}") for i in range(16)]
        self.alltoks = {}

    def allgather(self, src_t, dst_t, reads, dstb, sinks=(), sl=None):
        k = self.k
        sap = src_t.ap() if sl is None else src_t.ap()[sl]
        dap = dst_t.ap() if sl is None else dst_t.ap()[sl]
        rows = sap.shape[0]
        k.wait_all("sp" if self.fake else "pool", list(sinks))
        if self.fake:
            pairs = [(dap[r * rows:(r + 1) * rows, :], sap) for r in range(4)]
            k.dma("sp", pairs, reads=reads, writes=[dstb])
            return
        CC.n += 1
        key = ("cc", CC.n)
        k.sems[key] = self.pool_sems.pop(0)
        k._wait("pool", k._deps(reads, [dstb]))
        k.nc.gpsimd.collective_compute("AllGather", ALU.bypass, replica_groups=self.groups,
                                       ins=[sap.opt()], outs=[dap.opt()]).then_inc(k.sems[key])
        k._commit((key, 1), reads, [dstb])
        self.alltoks.setdefault(id(dstb), Sink("cc")).toks[key] = 1

    def sink(self, dstb):
        return self.alltoks.get(id(dstb), Sink("none"))


def emit_filter(P, C, F, kc_s, kcsb):
    k = P.k
    tmp, MLP, SM = C.tmp, F.MLP, F.SM
    cb = C.cb
    W1, W2, W3 = MLP[0:33, 0:64], MLP[:, 64:128], MLP[:, 128:192]
    W4 = MLP[:, 200:456].rearrange("p (d c) -> p d c", d=2)
    SMb, ACC, ACCb = F.SMb, F.ACC, F.ACCb
    k.op("dve", lambda e: e.memset(SM[:, 0:1], math.pi / 2), writes=[SMb])
    k.op("dve", lambda e: e.memset(SM[:, 1:2], EPS), writes=[SMb])
    k.op("dve", lambda e: e.tensor_scalar(out=SM[0:64, 2:3], in0=MLP[:, 195:196], scalar1=0.25, scalar2=None, op0=ALU.mult),
         reads=[cb], writes=[SMb])
    for i in range(3):
        k.op("dve", lambda e, i=i: e.tensor_tensor(out=SM[0:64, 3 + i:4 + i], in0=MLP[:, 192 + i:193 + i], in1=SM[0:64, 2:3],
                                                   op=ALU.mult), reads=[cb, SMb], writes=[SMb])
    k.op("dve", lambda e: e.memset(ACC[:], 0.0), writes=[ACCb])

    def sin4(ps, psb, li):
        s1, s1b = tmp()
        k.op("act", lambda e: e.activation(out=s1[0:64, :], in_=ps[0:64, :], func=AF.Sin, bias=SM[0:64, 3 + li:4 + li],
                                           scale=SM[0:64, 2:3]), reads=[psb, SMb], writes=[s1b])
        a1, a1b = tmp()
        k.op("act", lambda e: e.activation(out=a1[0:64, :], in_=ps[0:64, :], func=AF.Abs, bias=SM[0:64, 3 + li:4 + li],
                                           scale=SM[0:64, 2:3]), reads=[psb, SMb], writes=[a1b])
        k.op("act", lambda e: e.activation(out=a1[0:64, :], in_=a1[0:64, :], func=AF.Sin, bias=SM[0:64, 0:1], scale=-1.0),
             reads=[a1b, SMb], writes=[a1b])
        k.op("dve", lambda e: e.tensor_tensor(out=a1[0:64, :], in0=a1[0:64, :], in1=s1[0:64, :], op=ALU.mult),
             reads=[a1b, s1b], writes=[a1b])
        k.op("dve", lambda e: e.tensor_tensor(out=s1[0:64, :], in0=s1[0:64, :], in1=s1[0:64, :], op=ALU.mult),
             reads=[s1b], writes=[s1b])
        k.op("dve", lambda e: e.tensor_scalar(out=s1[0:64, :], in0=s1[0:64, :], scalar1=-2.0, scalar2=1.0,
                                              op0=ALU.mult, op1=ALU.add), reads=[s1b], writes=[s1b])
        k.op("dve", lambda e: e.scalar_tensor_tensor(out=a1[0:64, :], in0=a1[0:64, :], scalar=4.0, in1=s1[0:64, :],
                                                     op0=ALU.mult, op1=ALU.mult), reads=[a1b, s1b], writes=[a1b])
        return a1, a1b

    for c in range(32):
        ze, zeb = tmp()
        k.dma("sp", [(ze[0:33, :], F.zemb_d[:, c * 512:(c + 1) * 512])], writes=[zeb])
        wt, wtb = tmp()
        k.dma("sp", [(wt[:, :].rearrange("p (n c) -> p n c", n=4), F.win_d[:, c * 4:(c + 1) * 4, :])], writes=[wtb])
        ps, psb = P.bank("m", [6, 7])
        k.op("pe", lambda e, ps=ps, ze=ze: e.matmul(ps[0:64, :], lhsT=W1, rhs=ze[0:33, :], start=True, stop=True),
             reads=[zeb, cb], writes=[psb])
        h, hb = sin4(ps, psb, 0)
        for li, W in ((1, W2), (2, W3)):
            ps, psb = P.bank("m", [6, 7])
            k.op("pe", lambda e, ps=ps, h=h, W=W: e.matmul(ps[0:64, :], lhsT=W, rhs=h[0:64, :], start=True, stop=True),
                 reads=[hb, cb], writes=[psb])
            h, hb = sin4(ps, psb, li)
        ps4, ps4b = P.bank("m", [6, 7])
        for n2l in range(4):
            for d in range(2):
                k.op("pe", lambda e, ps4=ps4, h=h, n2l=n2l, d=d: e.matmul(
                    ps4[d * 64:(d + 1) * 64, n2l * 128:(n2l + 1) * 128],
                    lhsT=h[0:64, n2l * 128 + d * 64:n2l * 128 + (d + 1) * 64], rhs=W4[:, d, :],
                    start=True, stop=True, skip_group_check=True), reads=[hb, cb], writes=[ps4b])
        kc, kcb = tmp()
        k.op("dve", lambda e, kc=kc, ps4=ps4, wt=wt: e.tensor_tensor(out=kc[:, :], in0=ps4[:, :], in1=wt[:, :], op=ALU.mult),
             reads=[ps4b, wtb], writes=[kcb])
        st, stb = C.tmpbf()
        k.op("act", lambda e, kc=kc, st=st: e.activation(out=st[:, :], in_=kc[:, :], func=AF.Copy), reads=[kcb], writes=[stb])
        k.dma("sp", [(kc_s.ap()[:, c * 4:(c + 1) * 4, :], st[:, :].rearrange("p (n c) -> p n c", n=4))], reads=[stb],
              sembuf=stb, sink=kcsb)
        sq, sqb = tmp()
        k.op("dve", lambda e, kc=kc, sq=sq: e.tensor_tensor(out=sq[:, :], in0=kc[:, :], in1=kc[:, :], op=ALU.mult),
             reads=[kcb], writes=[sqb])
        rd, rdb = tmp()
        k.op("dve", lambda e, sq=sq, rd=rd: e.tensor_reduce(
            out=rd[:, 0:128], in_=sq[:, :].rearrange("p (n c) -> p c n", n=4), axis=mybir.AxisListType.X, op=ALU.add),
            reads=[sqb], writes=[rdb])
        k.op("dve", lambda e, rd=rd: e.tensor_tensor(out=ACC[:], in0=ACC[:], in1=rd[:, 0:128], op=ALU.add),
             reads=[rdb, ACCb], writes=[ACCb])
    ps, psb = P.bank("m", [6, 7])
    k.op("pe", lambda e, ps=ps: e.matmul(ps[:, 0:128], lhsT=C.ones_f[:, :], rhs=ACC[:], start=True, stop=True),
         reads=[ACCb, cb], writes=[psb])
    k.op("act", lambda e, ps=ps: e.activation(out=F.RSrow[:], in_=ps[:, 0:128], func=AF.Sqrt, bias=SM[:, 1:2], scale=1.0),
         reads=[psb, SMb], writes=[F.RSb])
    k.op("dve", lambda e: e.reciprocal(out=F.RSrow[:], in_=F.RSrow[:]), reads=[F.RSb], writes=[F.RSb])


def emit_fftconv(P, C, F, KC, KCb, zs2, zs2b, c_src, csink, V):
    k = P.k
    tmp = C.tmp
    DFT, TW, dftb, cb = F.DFT, F.TW, F.dftb, C.cb
    Fre, Fim, nFim = DFT[:, 0:128], DFT[:, 128:256], DFT[:, 384:512]
    Fcat, FcatI2, FcatI1 = DFT[:, 0:256], DFT[:, 128:384], DFT[:, 256:512]
    TWre2, TWim2 = TW[:, 0], TW[:, 1]
    Bre, Bim, Yre, Yim, Qre, Qim, Hre, Him = V.Bre, V.Bim, V.Yre, V.Yim, V.Qre, V.Qim, V.Hre, V.Him
    Bb, Hb, Yb, Qb = Buf("B"), Buf("H"), Buf("Y"), Buf("Q")
    PS, PSb = P.PS, P.PSb

    def cmul_from_psum(A, Ab, sign, outre, outim, outb, pr):
        A4 = A[:, :].rearrange("p (c r k) -> p c r k", c=2, r=2)
        Are, Aim = A4[:, :, 0, :], A4[:, :, 1, :]
        v = lambda t: t[:, 0:256].rearrange("p (c k) -> p c k", c=2)
        t1, t1b = tmp()
        t2, t2b = tmp()
        k.op("dve", lambda e: e.tensor_tensor(out=v(t1), in0=Are, in1=TWre2, op=ALU.mult), reads=[Ab, cb], writes=[t1b])
        k.op("dve", lambda e: e.tensor_tensor(out=v(t2), in0=Aim, in1=TWim2, op=ALU.mult), reads=[Ab, cb], writes=[t2b])
        k.op("pool", lambda e: e.tensor_tensor(out=outre[:, pr * 2:pr * 2 + 2, :], in0=v(t1), in1=v(t2),
                                               op=(ALU.subtract if sign > 0 else ALU.add)),
             reads=[t1b, t2b], writes=[outb])
        t3, t3b = tmp()
        t4, t4b = tmp()
        k.op("dve", lambda e: e.tensor_tensor(out=v(t3), in0=Aim, in1=TWre2, op=ALU.mult), reads=[Ab, cb], writes=[t3b])
        k.op("dve", lambda e: e.tensor_tensor(out=v(t4), in0=Are, in1=TWim2, op=ALU.mult), reads=[Ab, cb], writes=[t4b])
        k.op("pool", lambda e: e.tensor_tensor(out=outim[:, pr * 2:pr * 2 + 2, :], in0=v(t3), in1=v(t4),
                                               op=(ALU.add if sign > 0 else ALU.subtract)),
             reads=[t3b, t4b], writes=[outb])

    def fft_fwd(lhs_of, srcb, krows, xre, xreb, xim, ximb):
        for pr in range(2):
            A, Ab = P.bank("A", [0, 1])
            for cl in range(2):
                c4 = pr * 2 + cl
                k.op("pe", lambda e, A=A, cl=cl, c4=c4: e.matmul(
                    A[:, cl * 256:(cl + 1) * 256], lhsT=lhs_of(c4), rhs=Fcat[0:krows, :],
                    start=True, stop=True, skip_group_check=True), reads=[srcb, dftb], writes=[Ab])
            cmul_from_psum(A, Ab, +1, Bre, Bim, Bb, pr)
        bre = Bre.rearrange("p c k -> p (c k)")
        bim = Bim.rearrange("p c k -> p (c k)")
        k.op("pe", lambda e: e.matmul(xre[:, :], lhsT=Fre, rhs=bre, start=True, stop=False), reads=[Bb, dftb], writes=[xreb])
        k.op("pe", lambda e: e.matmul(xre[:, :], lhsT=nFim, rhs=bim, start=False, stop=True), reads=[Bb, dftb], writes=[xreb])
        k.op("pe", lambda e: e.matmul(xim[:, :], lhsT=Fim, rhs=bre, start=True, stop=False), reads=[Bb, dftb], writes=[ximb])
        k.op("pe", lambda e: e.matmul(xim[:, :], lhsT=Fre, rhs=bim, start=False, stop=True), reads=[Bb, dftb], writes=[ximb])

    for g in range(32):
        ch0 = g * 4
        ZC, ZCb = V.ZC[g % 2], V.ZCb[g % 2]
        src = bass.AP(tensor=zs2, offset=ch0 * S, ap=[[128, 64], [S, 4], [1, 128]])
        k.dma("sp", [(ZC[0:64], src)], reads=[zs2b], writes=[ZCb])
        fft_fwd(lambda c4, ch0=ch0: KC[:, :, ch0 + c4], KCb, 128, PS[2], PSb[2], PS[3], PSb[3])
        rsb_ap = F.RSrow[:, ch0:ch0 + 4].unsqueeze(2).broadcast_to([128, 4, 128])
        k.op("dve", lambda e, rsb_ap=rsb_ap: e.tensor_tensor(out=Hre.rearrange("p (c k) -> p c k", c=4),
                                                              in0=PS[2][:, :].rearrange("p (c k) -> p c k", c=4),
                                                              in1=rsb_ap, op=ALU.mult), reads=[PSb[2], F.RSb], writes=[Hb])
        k.op("dve", lambda e, rsb_ap=rsb_ap: e.tensor_tensor(out=Him.rearrange("p (c k) -> p c k", c=4),
                                                              in0=PS[3][:, :].rearrange("p (c k) -> p c k", c=4),
                                                              in1=rsb_ap, op=ALU.mult), reads=[PSb[3], F.RSb], writes=[Hb])
        fft_fwd(lambda c4, ZC=ZC: ZC[0:64, c4, :], ZCb, 64, PS[4], PSb[4], PS[5], PSb[5])
        t1, t1b = tmp()
        t2, t2b = tmp()
        k.op("dve", lambda e, t1=t1: e.tensor_tensor(out=t1[:, :], in0=PS[4][:, :], in1=Hre, op=ALU.mult),
             reads=[PSb[4], Hb], writes=[t1b])
        k.op("dve", lambda e, t2=t2: e.tensor_tensor(out=t2[:, :], in0=PS[5][:, :], in1=Him, op=ALU.mult),
             reads=[PSb[5], Hb], writes=[t2b])
        k.op("pool", lambda e, t1=t1, t2=t2: e.tensor_tensor(out=Yre.rearrange("p c k -> p (c k)"), in0=t1[:, :],
                                                             in1=t2[:, :], op=ALU.subtract), reads=[t1b, t2b], writes=[Yb])
        t3, t3b = tmp()
        t4, t4b = tmp()
        k.op("dve", lambda e, t3=t3: e.tensor_tensor(out=t3[:, :], in0=PS[4][:, :], in1=Him, op=ALU.mult),
             reads=[PSb[4], Hb], writes=[t3b])
        k.op("dve", lambda e, t4=t4: e.tensor_tensor(out=t4[:, :], in0=PS[5][:, :], in1=Hre, op=ALU.mult),
             reads=[PSb[5], Hb], writes=[t4b])
        k.op("pool", lambda e, t3=t3, t4=t4: e.tensor_tensor(out=Yim.rearrange("p c k -> p (c k)"), in0=t3[:, :],
                                                             in1=t4[:, :], op=ALU.add), reads=[t3b, t4b], writes=[Yb])
        for pr in range(2):
            Pk, Pkb = P.bank("A", [0, 1])
            for cl in range(2):
                c4 = pr * 2 + cl
                k.op("pe", lambda e, Pk=Pk, cl=cl, c4=c4: e.matmul(
                    Pk[:, cl * 256:(cl + 1) * 256], lhsT=Yre[:, c4, :], rhs=FcatI1, start=True, stop=False,
                    skip_group_check=True), reads=[Yb, dftb], writes=[Pkb])
                k.op("pe", lambda e, Pk=Pk, cl=cl, c4=c4: e.matmul(
                    Pk[:, cl * 256:(cl + 1) * 256], lhsT=Yim[:, c4, :], rhs=FcatI2, start=False, stop=True,
                    skip_group_check=True), reads=[Yb, dftb], writes=[Pkb])
            cmul_from_psum(Pk, Pkb, -1, Qre, Qim, Qb, pr)
        yo, yob = PS[6], PSb[6]
        k.op("pe", lambda e: e.matmul(yo[0:64, :], lhsT=DFT[:, 0:64], rhs=Qre.rearrange("p c k -> p (c k)"),
                                      start=True, stop=False), reads=[Qb, dftb], writes=[yob])
        k.op("pe", lambda e: e.matmul(yo[0:64, :], lhsT=DFT[:, 128:192], rhs=Qim.rearrange("p c k -> p (c k)"),
                                      start=False, stop=True), reads=[Qb, dftb], writes=[yob])
        ys, ysb = tmp()
        k.op("act", lambda e, ys=ys: e.activation(out=ys[0:64, :], in_=yo[0:64, :], func=AF.Copy, scale=1.0 / NFFT),
             reads=[yob], writes=[ysb])
        dst = bass.AP(tensor=c_src, offset=ch0 * S, ap=[[128, 64], [S, 4], [1, 128]])
        k.dma("sp", [(dst, ys[0:64, :].rearrange("p (c k) -> p c k", c=4))], reads=[ysb], sembuf=ysb, sink=csink)


def emit_attn1f(P, C, q_s, KT, KTb, VB, VBb, QS, QSb, MIX, MIXb):
    k = P.k
    for jq in range(4):
        q0 = jq * 512
        QT, QTb = QS[jq % 2], QSb[jq % 2]
        k.dma("sp", [(QT, q_s.ap()[:, :, q0:q0 + 512])], writes=[QTb])
        for g in range(2):
            gs = slice(g * 64, (g + 1) * 64)
            ds_ = slice((1 - g) * 64, (2 - g) * 64)
            for hh in range(4):
                po, pob = P.bank("o", [4, 5])
                for kb in range(64):
                    ps, psb = P.bank("h1", [0, 1, 2, 3])
                    k.op("pe", lambda e, ps=ps, kb=kb, gs=gs, hh=hh, QT=QT: e.matmul(
                        ps[:, :], lhsT=KT[gs, kb * 128:(kb + 1) * 128], rhs=QT[gs, hh, :],
                        start=True, stop=True), reads=[KTb, QTb], writes=[psb])
                    pt, ptb = C.tmpbf()
                    k.op("act", lambda e, ps=ps, pt=pt: e.activation(out=pt[:, :], in_=ps[:, :], func=AF.Exp,
                                                                     scale=HD ** -0.5), reads=[psb], writes=[ptb])
                    k.op("pe", lambda e, po=po, pt=pt, kb=kb, g=g: e.matmul(
                        po[:, :], lhsT=VB[:, kb, g * 64:g * 64 + 128], rhs=pt[:, :], start=(kb == 0), stop=(kb == 63)),
                        reads=[VBb, ptb], writes=[pob])
                rd, rdb = C.tmp()
                k.op("dve", lambda e, rd=rd, po=po, ds_=ds_: e.reciprocal(out=rd[ds_, :], in_=po[ds_, :]),
                     reads=[pob], writes=[rdb])
                k.op("dve", lambda e, rd=rd, po=po, gs=gs, ds_=ds_, hh=hh, q0=q0: e.tensor_tensor(
                    out=MIX[gs, hh, q0:q0 + 512], in0=po[gs, :], in1=rd[ds_, :], op=ALU.mult),
                    reads=[pob, rdb], writes=[MIXb[jq]])


def build_fused(fake_ag=False, ncore=8, stop_after=None):
    nc = bass.Bass("TRN2", target_bir_lowering=False)
    k = KB(nc)
    groups = [[0, 1, 2, 3], [4, 5, 6, 7]] if ncore == 8 else [[0, 1, 2, 3]]
    cc = CC(k, groups, fake_ag)
    dt_in = lambda name, shape, dt=F32: nc.dram_tensor(name, list(shape), dt, kind="ExternalInput").ap()
    dram = lambda name, shape, dt=F32: nc.dram_tensor(name, list(shape), dt, kind="Internal")
    xT_d = dt_in("xT", [128, 8, 2048])
    xH_d = dt_in("xH", [128, 8, 256])
    cf_d = dt_in("cf", [128, 400])
    oh_d = dt_in("oh", [32, 512])
    vm_d = dt_in("vm", [128, 512])
    rope_d = dt_in("rope", [128, 2, 2048])
    idx_d = dt_in("idx", [128, 24], U32)
    mlp_d = dt_in("mlpw", [64, 456])
    zemb_d = dt_in("zemb", [33, NFFT])
    win_d = dt_in("win", [128, 128, 128])
    dft_d = dt_in("dftc", [128, 512])
    tw_d = dt_in("tw", [128, 2, 2, 128])
    wmod_d = dt_in("wmod", [2, 1024, 9216])
    w1_d = dt_in("w1", [2, 2, 1024, DFFP])
    w3_d = dt_in("w3", [2, 2, 1024, DFFP])
    w2_d = dt_in("w2", [2, 2, DFFP, 1024])
    win_w = dt_in("win_w", [2, 1024, INC])
    wout_d = dt_in("wout", [2, 1024, 1024])
    xo_d = nc.dram_tensor("xo", [128, 8, 2048], F32, kind="ExternalOutput").ap()
    scr = dram("scr", [128, 8 * 512])
    q_s = dram("q_s", [128, 4, 2048], BF16)
    k_src, k_g = dram("k_src", [128, 2048], BF16), dram("k_g", [512, 2048], BF16)
    v_src, v_g = dram("v_src", [2048, 128], BF16), dram("v_g", [8192, 128], BF16)
    hb_src, hb_g = dram("hb_src", [128, 24]), dram("hb_g", [512, 24])
    z_src, z_g = dram("z_src", [4, 128, 2048], BF16), dram("z_g", [4, 512, 2048], BF16)
    zf_s = dram("zf_s", [128, 4, 2048])
    x0_s = dram("x0_s", [128, 4, 2048])
    zs2 = dram("zs2", [128, S], BF16)
    kc_s = dram("kc_s", [128, 128, 128], BF16)
    c_src, c_g = dram("c_src", [8, 16, S]), dram("c_g", [8, 64, S])

    P = Prog(nc, k)
    C = Common()
    C.tmp = RotTmp(k, "tf", 8, F32)
    C.tmpbf = RotTmp(k, "tb", 4, BF16)
    C.RS = k.sb("RS", [128, 512], F32)
    C.RSb = Buf("RS")
    C.cb = Buf("consts")
    C.modb = Buf("mod")
    CF = k.sb("CF", [128, 400], F32)
    IDX = k.sb("IDX", [128, 24], U32)
    C.ones_bf = k.sb("ones_bf", [128, 128], BF16)
    C.bd_bf = k.sb("bd_bf", [128, 128], BF16)
    C.ones_f = k.sb("ones_f", [128, 128], F32)
    C.eps = k.sb("eps", [128, 1], F32)
    condbf = k.sb("condbf", [128, 8], BF16)
    MODV = k.sb("modv", [128, 2, 72], F32)
    AV = k.sb("av", [128, 2, 3, 8], F32)
    EXS = k.sb("exs", [128, 8], F32)
    F = Common()
    F.MLP = k.sb("MLP", [64, 456], F32)
    F.DFT = k.sb("DFT", [128, 512], BF16)
    F.TW = k.sb("TW", [128, 2, 2, 128], F32)
    F.ACC = k.sb("ACC", [128, 128], F32)
    F.SM = k.sb("SM", [128, 16], F32)
    F.RSrow = k.sb("RSrow", [128, 128], F32)
    F.SMb, F.ACCb, F.RSb, F.dftb = Buf("SM"), Buf("ACC"), Buf("RSr"), Buf("dft")
    F.zemb_d, F.win_d = zemb_d, win_d
    HBS = k.sb("HBS", [128, 12, 2], F32)
    HG = k.sb("HG", [128, 4, 12, 2], F32)
    HLR = k.sb("HLR", [128, 2, 12], F32)
    XT = k.sb("XT", [128, 8, 2048], F32)
    AR = k.sb("AR", [128, 41984], BF16)

    cT = CF[:, 0:8]
    flags = CF[:, 8:10]
    normg = CF[:, 10:58].rearrange("p (l i c) -> p l i c", l=2, i=3)
    bmod = CF[:, 58:202].rearrange("p (l m) -> p l m", l=2)
    convw = CF[:, 202:214].rearrange("p (c t) -> p c t", c=4)
    convb = CF[:, 214:218]
    gq, gk = CF[:, 218:220], CF[:, 220:222]
    sink = CF[:, 222:230]
    relrep = CF[:, 232:240]
    gq1, gk1 = CF[:, 240:241], CF[:, 241:242]
    dcw = CF[:, 244:280].rearrange("p (c t) -> p c t", c=12)
    dcb = CF[:, 280:292]
    skipv = CF[:, 292:296]
    selL, selR = CF[:, 296:300], CF[:, 300:304]

    Ht = AR[:, 0:18432].rearrange("p (c t) -> p c t", c=8)
    UQ = AR[:, 18432:33792]
    MXf = AR[:, 33792:41984].bitcast(F32)
    XH = MXf[:, 0:2048].rearrange("p (c t) -> p c t", c=8)
    MIXHI0 = AR[:, 33792:41984].rearrange("p (c t) -> p c t", c=4)
    U = UQ[:, 0:6 * 2304].rearrange("p (f t) -> p f t", f=6)
    QT0 = UQ[:, 0:8192].rearrange("p (h t) -> p h t", h=4)
    KT0 = UQ[:, 8192:8192 + 2304]
    VA0 = UQ[:, 10496:10496 + 4608].rearrange("p (b g d) -> p b g d", b=18, g=2)
    S_t = UQ[:, 10496:10496 + 4100].bitcast(F32)
    C.E8 = UQ[:, 0:8192].bitcast(F32).rearrange("p (h m) -> p h m", h=8)
    EXPB = P.WBf[:, :, :].rearrange("p a (b q) -> p (a b) q", q=128).rearrange("p (h b) q -> p h b q", h=8)
    OH = AR[0:32, 0:1024].bitcast(F32)
    VM = AR[:, 1024:2048].bitcast(F32)

    tiles5 = [(0, 512), (512, 512), (1024, 512), (1536, 512), (2048, 256)]
    tiles4 = tiles5[:4]
    XB = [Buf(f"x{j}") for j in range(5)]

    class XAct:
        def __init__(self, tiles):
            self.tiles = tiles
            self.b = XB[:len(tiles)]

        def ap(self, c, j):
            if j < 4:
                return XT[:, c, j * 512:(j + 1) * 512]
            return XH[:, c, :]
    X5, X4 = XAct(tiles5), XAct(tiles4)
    H5 = Act(Ht, tiles5, "h")
    H4 = Act(Ht, tiles4, "h")
    H4.b = H5.b[:4]
    Ub = [[Buf(f"u{f}_{j}") for j in range(5)] for f in range(6)]

    k.dma("sp", [(CF[:], cf_d), (F.MLP[:], mlp_d), (F.TW[:], tw_d), (IDX[:], idx_d)], writes=[C.cb])
    k.dma("pool", [(F.DFT[:], dft_d)], writes=[F.dftb])
    for j in range(4):
        k.dma("sp", [(XT[:, :, j * 512:(j + 1) * 512], xT_d[:, :, j * 512:(j + 1) * 512])], writes=[XB[j]])
    k.dma("sp", [(XH, xH_d)], writes=[XB[4]])
    cst = Buf("cst")
    k.op("dve", lambda e: e.memset(C.ones_bf[:], 1.0), writes=[cst])
    k.op("dve", lambda e: e.memset(C.ones_f[:], 1.0), writes=[cst])
    k.op("dve", lambda e: e.memset(C.eps[:], EPS), writes=[cst])
    k.op("dve", lambda e: e.memset(C.bd_bf[:], 0.0), writes=[cst])
    k.op("dve", lambda e: e.memset(C.bd_bf[0:64, 0:64], 1.0), writes=[cst])
    k.op("dve", lambda e: e.memset(C.bd_bf[64:128, 64:128], 1.0), writes=[cst])
    condb = Buf("cond")
    k.op("act", lambda e: e.activation(out=condbf[:], in_=cT, func=AF.Silu), reads=[C.cb], writes=[condb])
    k.op("act", lambda e: e.activation(out=EXS[:], in_=sink, func=AF.Exp), reads=[C.cb], writes=[cst])
    k.op("dve", lambda e: e.tensor_copy(out=C.eps[:], in_=C.eps[:]), reads=[cst, C.cb], writes=[C.cb])

    def layer_mod(l):
        emit_mod(P, condbf, condb, wmod_d[l], bmod[:, l, :], MODV[:, l, :], C.modb, AV[:, l], normg[:, l])

    def mcol(l, i, kind):
        return MODV[:, l, (i * 3 + kind) * 8:(i * 3 + kind + 1) * 8]

    def mix_ap0(fc, c0, w):
        if fc < 4:
            return Ht[:, fc, c0:c0 + w]
        return MIXHI0[:, fc - 4, c0:c0 + w]

    def finish(extra=()):
        ob = Sink("out")
        k.wait_all("sp", list(extra))
        for j in range(4):
            k.dma("sp", [(xo_d[:, :, j * 512:(j + 1) * 512], XT[:, :, j * 512:(j + 1) * 512])], reads=[XB[j]],
                  sembuf=XB[j], sink=ob)
        k.wait_all("sp", [ob])
        k.close()
        return nc

    kcsb = Sink("kcs")
    if stop_after != "nofilter":
        emit_filter(P, C, F, kc_s, kcsb)
    if stop_after == "filter":
        return finish([kcsb, F.RSb])

    layer_mod(0)
    emit_adaln(P, X5, H5, AV[:, 0, 0], mcol(0, 0, 0), C)
    emit_ffn(P, X5, H5, U, Ub, w1_d[0, 0], w3_d[0, 0], w2_d[0, 0], mcol(0, 0, 2), C)
    k.barrier()
    ohb = Buf("oh")
    k.dma("sp", [(OH, oh_d), (VM, vm_d)], writes=[ohb])
    k.op("dve", lambda e: e.tensor_copy(out=C.eps[:], in_=C.eps[:]), reads=[ohb, C.cb], writes=[C.cb])
    scrb = emit_expb(P, C, relrep, C.cb, OH, VM, scr, EXPB, P.WBb[0])
    k.barrier(dbufs=[scrb])
    emit_adaln(P, X5, H5, AV[:, 0, 1], mcol(0, 1, 0), C)
    k.barrier()
    MIXb = [Buf(f"mix{j}") for j in range(4)]
    Sb = Buf("S")
    emit_conv0(P, C, H5, mix_ap0, MIXb, win_w[0], convw, convb, flags, S_t, Sb)
    k.barrier()
    QTb = [Buf(f"qt{j}") for j in range(4)]
    KTb = [Buf(f"kt{j}") for j in range(5)]
    VAb = Buf("va")
    k.op("dve", lambda e: e.memset(VA0[:, 0:16, 0, 64:128], 1.0), writes=[VAb])
    k.op("dve", lambda e: e.memset(VA0[:, 0:16, 1, 0:64], 1.0), writes=[VAb])
    emit_qkv0(P, C, H5, win_w[0], QT0, QTb, KT0, KTb, VA0, VAb, gq, gk, flags)
    k.barrier()
    emit_attn0(P, C, QT0, QTb, KT0, KTb, VA0, VAb, EXPB, P.WBb[0], EXS, Ht, MIXb)
    for i in (1, 2):
        P.WBb[i].r = dict(P.WBb[0].r)
        P.WBb[i].w = P.WBb[0].w
    emit_outproj(P, C, X4, mix_ap0, MIXb, wout_d[0], mcol(0, 1, 2))
    k.barrier()
    emit_adaln(P, X4, H4, AV[:, 0, 2], mcol(0, 2, 0), C)
    emit_ffn(P, X4, H4, U, Ub, w1_d[0, 1], w3_d[0, 1], w2_d[0, 1], mcol(0, 2, 2), C)

    if stop_after == "l0":
        return finish([kcsb])

    layer_mod(1)
    emit_adaln(P, X4, H4, AV[:, 1, 0], mcol(1, 0, 0), C)
    emit_ffn(P, X4, H4, U, Ub, w1_d[1, 0], w3_d[1, 0], w2_d[1, 0], mcol(1, 0, 2), C)
    k.barrier()
    ROPE = UQ[:, 0:8192].bitcast(F32).rearrange("p (a t) -> p a t", a=2)
    HB = AR[:, 26624:26624 + 12300].bitcast(F32).rearrange("p (a t) -> p a t", a=3)
    ropeb, HBb = Buf("rope"), Buf("HB")
    k.dma("sp", [(ROPE, rope_d)], writes=[ropeb])
    emit_adaln(P, X4, H4, AV[:, 1, 1], mcol(1, 1, 0), C)
    W1L = win_w[1]
    hbsb = Buf("hbs")
    for pr in range(6):
        wa, wab = P.load_wa(W1L[:, 768 + pr * 256:768 + (pr + 1) * 256])
        for cl in range(2):
            ch = pr * 2 + cl
            ps, psb = P.bank("h1", [0, 1, 2, 3])
            for ci, col in enumerate((0, 2047)):
                for c in range(8):
                    k.op("pe", lambda e, c=c, ps=ps, wa=wa, cl=cl, ci=ci, col=col: e.matmul(
                        ps[:, ci:ci + 1], lhsT=wa[:, c, cl * 128:(cl + 1) * 128], rhs=Ht[:, c, col:col + 1],
                        start=(c == 0), stop=(c == 7)), reads=[wab] + H4.b, writes=[psb])
            k.op("act", lambda e, ps=ps, ch=ch: e.activation(out=HBS[:, ch, :], in_=ps[:, 0:2], func=AF.Copy),
                 reads=[psb], writes=[hbsb])
    hbsrcb, hbgb, hgb = Buf("hbsrc"), Buf("hbg"), Buf("hg")
    k.dma("sp", [(hb_src.ap(), HBS[:].rearrange("p c t -> p (c t)"))], reads=[hbsb], writes=[hbsrcb])
    cc.allgather(hb_src, hb_g, [hbsrcb], hbgb)
    k.dma("sp", [(HG[:].rearrange("p r c t -> p r (c t)"), hb_g.ap().rearrange("(r p) t -> p r t", p=128))],
          reads=[hbgb], writes=[hgb])
    outs = Sink("l1outs")
    for pr in range(3):
        wa, wab = P.load_wa(W1L[:, pr * 256:(pr + 1) * 256])
        for cl in range(2):
            if pr == 2 and cl == 1:
                break
            for j in range(4):
                c0, w = H4.tiles[j]
                ps, psb = P.bank("h1", [0, 1, 2, 3])
                emit_inproj_tile(P, H4, j, wa, wab, cl, ps, psb)
                qn, qnb = C.tmp()
                emit_qknorm(P, C, ps, psb, w, (gq1 if pr < 2 else gk1), qn[:, 0:w], qnb)
                st, stb = C.tmpbf()
                emit_rope(P, C, qn, qnb, ROPE[:, 0, :], ROPE[:, 1, :], ropeb, c0, w, st[:, 0:w], stb)
                dst = q_s.ap()[:, pr * 2 + cl, c0:c0 + w] if pr < 2 else k_src.ap()[:, c0:c0 + w]
                k.dma("sp", [(dst, st[:, 0:w])], reads=[stb], sembuf=stb, sink=outs)
        if pr == 2:
            for tb in range(16):
                j = tb // 4
                ps, psb = P.bank("o", [4, 5])
                for c in range(8):
                    k.op("pe", lambda e, c=c, tb=tb, ps=ps, wa=wa: e.matmul(
                        ps[:, 0:128], lhsT=Ht[:, c, tb * 128:(tb + 1) * 128], rhs=wa[:, c, 128:256],
                        start=(c == 0), stop=(c == 7)), reads=[wab, H4.b[j]], writes=[psb])
                st, stb = C.tmpbf()
                k.op("act", lambda e, ps=ps, st=st: e.activation(out=st[:, 0:128], in_=ps[:, 0:128], func=AF.Copy),
                     reads=[psb], writes=[stb])
                k.dma("sp", [(v_src.ap()[tb * 128:(tb + 1) * 128, :], st[:, 0:128])], reads=[stb], sembuf=stb, sink=outs)
    kgb, vgb = Buf("kg"), Buf("vg")
    if stop_after == "qkv":
        return finish([outs, kcsb])
    cc.allgather(k_src, k_g, [], kgb, sinks=[outs])
    cc.allgather(v_src, v_g, [], vgb, sinks=[outs])
    for side, sel, colx in ((0, selL, 1), (1, selR, 0)):
        k.op("dve", lambda e, side=side, sel=sel, colx=colx: e.tensor_scalar(
            out=HLR[:, side, :], in0=HG[:, 0, :, colx], scalar1=sel[:, 0:1], scalar2=None, op0=ALU.mult),
            reads=[hgb, C.cb], writes=[hgb])
        for r in range(1, 4):
            k.op("dve", lambda e, side=side, sel=sel, colx=colx, r=r: e.scalar_tensor_tensor(
                out=HLR[:, side, :], in0=HG[:, r, :, colx], scalar=sel[:, r:r + 1], in1=HLR[:, side, :],
                op0=ALU.mult, op1=ALU.add), reads=[hgb, C.cb], writes=[hgb])

    zouts = Sink("zouts")
    for hh in range(2):
        was = [P.load_wa(W1L[:, 768 + part * 512 + hh * 256: 768 + part * 512 + (hh + 1) * 256]) for part in range(3)]
        for cl in range(2):
            cch = hh * 2 + cl
            for part in range(3):
                wa, wab = was[part]
                for j in range(4):
                    c0, w = H4.tiles[j]
                    ps, psb = P.bank("h1", [0, 1, 2, 3])
                    emit_inproj_tile(P, H4, j, wa, wab, cl, ps, psb)
                    k.op("act", lambda e, ps=ps, part=part, c0=c0, w=w: e.activation(
                        out=HB[:, part, 1 + c0:1 + c0 + w], in_=ps[:, 0:w], func=AF.Copy), reads=[psb], writes=[HBb])
                ch12 = part * 4 + cch
                k.op("act", lambda e, part=part, ch12=ch12: e.activation(out=HB[:, part, 0:1], in_=HLR[:, 0, ch12:ch12 + 1],
                                                                         func=AF.Copy), reads=[hgb], writes=[HBb])
                k.op("act", lambda e, part=part, ch12=ch12: e.activation(out=HB[:, part, 2049:2050],
                                                                         in_=HLR[:, 1, ch12:ch12 + 1], func=AF.Copy),
                     reads=[hgb], writes=[HBb])
            for j in range(4):
                c0, w = H4.tiles[j]
                cv = []
                for part in range(3):
                    ch12 = part * 4 + cch
                    t, tb_ = C.tmp()
                    k.op("dve", lambda e, t=t, part=part, c0=c0, ch12=ch12: e.tensor_scalar(
                        out=t[:, :], in0=HB[:, part, 1 + c0:1 + c0 + 512], scalar1=dcw[:, ch12, 1:2],
                        scalar2=dcb[:, ch12:ch12 + 1], op0=ALU.mult, op1=ALU.add), reads=[HBb, C.cb], writes=[tb_])
                    k.op("dve", lambda e, t=t, part=part, c0=c0, ch12=ch12: e.scalar_tensor_tensor(
                        out=t[:, :], in0=HB[:, part, c0:c0 + 512], scalar=dcw[:, ch12, 0:1], in1=t[:, :],
                        op0=ALU.mult, op1=ALU.add), reads=[HBb, C.cb, tb_], writes=[tb_])
                    k.op("dve", lambda e, t=t, part=part, c0=c0, ch12=ch12: e.scalar_tensor_tensor(
                        out=t[:, :], in0=HB[:, part, 2 + c0:2 + c0 + 512], scalar=dcw[:, ch12, 2:3], in1=t[:, :],
                        op0=ALU.mult, op1=ALU.add), reads=[HBb, C.cb, tb_], writes=[tb_])
                    cv.append((t, tb_))
                (x0t, x0b), (x1t, x1b), (vt, vb_) = cv
                k.dma("sp", [(x0_s.ap()[:, cch, c0:c0 + 512], x0t[:, :])], reads=[x0b], sembuf=x0b, sink=zouts)
                k.op("dve", lambda e, x1t=x1t, vt=vt: e.tensor_tensor(out=x1t[:, :], in0=x1t[:, :], in1=vt[:, :], op=ALU.mult),
                     reads=[x1b, vb_], writes=[x1b])
                k.dma("sp", [(zf_s.ap()[:, cch, c0:c0 + 512], x1t[:, :])], reads=[x1b], sembuf=x1b, sink=zouts)
                zb16, zb16b = C.tmpbf()
                k.op("act", lambda e, x1t=x1t, zb16=zb16: e.activation(out=zb16[:, :], in_=x1t[:, :], func=AF.Copy),
                     reads=[x1b], writes=[zb16b])
                k.dma("sp", [(z_src.ap()[cch, :, c0:c0 + 512], zb16[:, :])], reads=[zb16b],
                      sembuf=zb16b, sink=zouts)
    zgb = Buf("zg")
    for cch in range(4):
        cc.allgather(z_src, z_g, [], zgb, sinks=[zouts], sl=cch)

    if stop_after == "l1pre":
        return finish([kgb, vgb, zgb, kcsb])

    k.barrier()
    VB = AR[:, 0:12288].rearrange("p (b d) -> p b d", b=64)
    KT = AR[:, 12288:20480]
    QS = [AR[:, 20480 + i * 2048:20480 + (i + 1) * 2048].rearrange("p (h t) -> p h t", h=4) for i in range(2)]
    QSb = [Buf("qs0"), Buf("qs1")]
    MIX = AR[:, 24576:40960].rearrange("p (c t) -> p c t", c=8)
    MIXb = [Buf(f"mixb{j}") for j in range(4)]
    KTb, VBb = Buf("KT"), Buf("VB")
    k.op("dve", lambda e: e.memset(VB[:, :, 64:128], 1.0), writes=[VBb])
    k.dma("sp", [(KT.rearrange("p (r t) -> p r t", r=4), k_g.ap().rearrange("(r p) t -> p r t", p=128))],
          reads=[kgb], writes=[KTb])
    vsrc = v_g.ap().rearrange("(b p) d -> p b d", p=128)
    k.dma("sp", [(VB[:, :, 0:64], vsrc[:, :, 0:64]), (VB[:, :, 128:192], vsrc[:, :, 64:128])], reads=[vgb], writes=[VBb])
    emit_attn1f(P, C, q_s, KT, KTb, VB, VBb, QS, QSb, MIX, MIXb)

    if stop_after == "attn":
        return finish([zgb, kcsb] + MIXb)

    k.barrier()
    V = Common()
    KC = AR[:, 0:16384].rearrange("p (n c) -> p n c", n=128)
    Zg = AR[:, 0:8192]
    V.ZC = [AR[:, 16384 + i * 512:16384 + (i + 1) * 512].rearrange("p (c k) -> p c k", c=4) for i in range(2)]
    V.ZCb = [Buf("zc0"), Buf("zc1")]
    six = [AR[:, 17408 + i * 512:17408 + (i + 1) * 512].rearrange("p (c k) -> p c k", c=4) for i in range(6)]
    V.Bre, V.Bim, V.Yre, V.Yim, V.Qre, V.Qim = six
    V.Hre = AR[:, 20480:21504].bitcast(F32)
    V.Him = AR[:, 21504:22528].bitcast(F32)
    Zgb, zs2b, KCb = Buf("Zg"), Buf("zs2"), Buf("KC")
    zg_rows = z_g.ap().rearrange("c r t -> (c r) t")
    k.wait_all("pool", [cc.sink(zgb)])
    for r in range(4):
        kb_gather(k, Zg[:, r * 2048:(r + 1) * 2048], zg_rows, IDX[:, r:r + 1], reads=[zgb, C.cb], writes=[Zgb])
    k.dma("sp", [(zs2.ap(), Zg)], reads=[Zgb], writes=[zs2b])
    k.wait_all("sp", [kcsb])
    k.dma("sp", [(KC, kc_s.ap())], reads=[zs2b], writes=[KCb])
    csink = Sink("csrc")
    emit_fftconv(P, C, F, KC, KCb, zs2, zs2b, c_src, csink, V)
    cgb = Buf("cg")
    for i in range(8):
        cc.allgather(c_src, c_g, [], cgb, sinks=[csink], sl=i)

    if stop_after == "fft":
        return finish([cgb] + MIXb)

    cg_rows = c_g.ap().rearrange("i r (a t) -> (i r a) t", t=512)
    k.wait_all("pool", [cc.sink(cgb)])
    for cq in range(4):
        for tq in range(4):
            c0 = tq * 512
            ct, ctb = C.tmp()
            kb_gather(k, ct[:, :], cg_rows, IDX[:, 4 + cq * 4 + tq:5 + cq * 4 + tq], reads=[cgb, C.cb], writes=[ctb])
            zt, ztb = C.tmp()
            k.dma("sp", [(zt[:, :], zf_s.ap()[:, cq, c0:c0 + 512])], writes=[ztb])
            xt_, xtb = C.tmp()
            k.dma("sp", [(xt_[:, :], x0_s.ap()[:, cq, c0:c0 + 512])], writes=[xtb])
            k.op("dve", lambda e, zt=zt, ct=ct, cq=cq: e.scalar_tensor_tensor(
                out=zt[:, :], in0=zt[:, :], scalar=skipv[:, cq:cq + 1], in1=ct[:, :], op0=ALU.mult, op1=ALU.add),
                reads=[ztb, ctb, C.cb], writes=[ztb])
            k.op("dve", lambda e, zt=zt, xt_=xt_, cq=cq, c0=c0: e.tensor_tensor(
                out=MIX[:, 4 + cq, c0:c0 + 512], in0=zt[:, :], in1=xt_[:, :], op=ALU.mult),
                reads=[ztb, xtb], writes=[MIXb[tq]])

    emit_outproj(P, C, X4, lambda fc, c0, w: MIX[:, fc, c0:c0 + w], MIXb, wout_d[1], mcol(1, 1, 2))
    k.barrier()
    Ht2 = AR[:, 0:16384].rearrange("p (c t) -> p c t", c=8)
    U2 = AR[:, 16384:16384 + 12288].rearrange("p (f t) -> p f t", f=6)
    H42 = Act(Ht2, tiles4, "h2")
    Ub2 = [[Buf(f"v{f}_{j}") for j in range(4)] for f in range(6)]
    emit_adaln(P, X4, H42, AV[:, 1, 2], mcol(1, 2, 0), C)
    emit_ffn(P, X4, H42, U2, Ub2, w1_d[1, 1], w3_d[1, 1], w2_d[1, 1], mcol(1, 2, 2), C)
    ob = Sink("out")
    for j in range(4):
        k.dma("sp", [(xo_d[:, :, j * 512:(j + 1) * 512], XT[:, :, j * 512:(j + 1) * 512])], reads=[XB[j]],
              sembuf=XB[j], sink=ob)
    k.wait_all("sp", [ob])
    k.close()
    return nc


def prep_fused(inp, ncore=NCORE):
    base = prep_LA(inp)
    maps = []
    cache = {}
    dcw = np.asarray(inp["d_conv_w"], np.float32)[0]
    dcb = np.asarray(inp["d_conv_b"], np.float32)[0]
    skip = np.asarray(inp["d_skip"], np.float32)[0]
    w4 = np.asarray(inp["d_f_w4"], np.float32)[0]
    p = np.arange(128)
    for core in range(ncore):
        b, qd = core // 4, core % 4
        cq = qd
        if cq not in cache:
            cache[cq] = host_consts_LB(cq)
        zemb, win, dftc, tw = cache[cq]
        m = dict(base[core])
        cf = np.zeros((128, 400), np.float32)
        cf[:, 0:244] = m.pop("cf")[:, 0:244]
        cf[:, 244:280] = fm(dcw).transpose(0, 2, 1).reshape(128, 36)
        cf[:, 280:292] = fm(dcb)
        cf[:, 292:296] = fm(skip)
        if qd > 0:
            cf[:, 296 + qd - 1] = 1.0
        if qd < 3:
            cf[:, 300 + qd + 1] = 1.0
        idx = np.zeros((128, 24), np.uint32)
        for r in range(4):
            idx[:, r] = cq * 512 + r * 128 + p
        for c2 in range(4):
            for tq in range(4):
                idx[:, 4 + c2 * 4 + tq] = ((p // 16) * 64 + c2 * 16 + (p % 16)) * 16 + qd * 4 + tq
        mlp = np.zeros((64, 456), np.float32)
        mlp[0:33, 0:64] = np.asarray(inp["d_f_w1"], np.float32)[0]
        mlp[:, 64:128] = np.asarray(inp["d_f_w2"], np.float32)[0]
        mlp[:, 128:192] = np.asarray(inp["d_f_w3"], np.float32)[0]
        mlp[:, 192] = np.asarray(inp["d_f_b1"], np.float32)[0]
        mlp[:, 193] = np.asarray(inp["d_f_b2"], np.float32)[0]
        mlp[:, 194] = np.asarray(inp["d_f_b3"], np.float32)[0]
        mlp[:, 195] = np.asarray(inp["d_f_freq"], np.float32)[0]
        mlp[:, 200:328] = w4[:, cq * 128:(cq + 1) * 128]
        mlp[:, 328:456] = w4[:, 512 + cq * 128:512 + (cq + 1) * 128]
        m["win_w"] = m.pop("win")
        m.update(cf=cf, idx=idx, mlpw=mlp, zemb=zemb, win=win, dftc=dftc, tw=tw)
        maps.append(m)
    return maps
```

```python
import math
import numpy as np
import concourse.bass as bass
import concourse.mybir as mybir
from concourse.bass_utils import run_bass_kernel_spmd

AF = mybir.ActivationFunctionType
ALU = mybir.AluOpType
F32 = mybir.dt.float32
BF16 = mybir.dt.bfloat16
EPOCH = 12000

D = 1024
S = 8192
TOK = 2048
NCORE = 8
DFF = 2752
DFFP = 2816
NF = 22
HD = 64
EPS = 1e-6
INC = 2304


class Buf:
    __slots__ = ("name", "w", "r", "dsem", "dcnt")

    def __init__(self, name=""):
        self.name = name
        self.w = None
        self.r = {}
        self.dsem = None
        self.dcnt = 0


class Sink:
    def __init__(self, name=""):
        self.name = name
        self.toks = {}


class KB:
    def __init__(self, nc):
        self.nc = nc
        self.engs = {"pe": nc.tensor, "act": nc.scalar, "dve": nc.vector,
                     "pool": nc.gpsimd, "sp": nc.sync}
        self.cnt = {e: 0 for e in self.engs}
        self.sems = {}
        self.waited = {e: {} for e in self.engs}
        self.nsem = 0
        self._stack = []

    def _newsem(self, name):
        cm = self.nc.semaphore(name)
        h = cm.__enter__()
        self._stack.append(cm)
        self.nsem += 1
        return h

    def sb(self, name, shape, dt):
        cm = self.nc.sbuf_tensor(name, shape, dt)
        t = cm.__enter__()
        self._stack.append(cm)
        return t

    def ps(self, name, shape, dt=F32):
        cm = self.nc.psum_tensor(name, shape, dt)
        t = cm.__enter__()
        self._stack.append(cm)
        return t

    def close(self):
        while self._stack:
            self._stack.pop().__exit__(None, None, None)

    def _engsem(self, eng):
        key = (eng, self.cnt[eng] // EPOCH)
        if key not in self.sems:
            self.sems[key] = self._newsem(f"s_{eng}_{key[1]}")
        return key

    def _deps(self, reads, writes):
        deps = {}

        def add(k, v):
            if deps.get(k, 0) < v:
                deps[k] = v
        for b in reads:
            if b.w is not None:
                add(*b.w)
        for b in writes:
            if b.w is not None:
                add(*b.w)
            for kk, v in b.r.items():
                add(kk, v)
        return deps

    def _wait(self, eng, deps):
        w = self.waited[eng]
        e = self.engs[eng]
        for kk, v in deps.items():
            if w.get(kk, 0) >= v:
                continue
            e.wait_ge(self.sems[kk], v)
            w[kk] = v

    def _commit(self, tok, reads, writes):
        kk, v = tok
        for b in reads:
            if b.r.get(kk, 0) < v:
                b.r[kk] = v
        for b in writes:
            b.w = tok
            b.r = {}

    def op(self, eng, fn, reads=(), writes=()):
        deps = self._deps(reads, writes)
        if eng == "pe":
            deps = {kk: v for kk, v in deps.items() if kk[0] != "pe"}
        self._wait(eng, deps)
        key = self._engsem(eng)
        ins = fn(self.engs[eng])
        self.cnt[eng] += 1
        val = self.cnt[eng] - key[1] * EPOCH
        ins.then_inc(self.sems[key], 1)
        tok = (key, val)
        self._commit(tok, reads, writes)
        return tok

    def dma(self, q, pairs, reads=(), writes=(), sembuf=None, sink=None, **kw):
        sbf = sembuf or (writes[0] if writes else reads[0])
        if sbf.dsem is None:
            sbf.dsem = {}
            sbf.dcnt = {}
        sw = (q == "pool")
        if sw not in sbf.dsem:
            sbf.dsem[sw] = ("d", id(sbf), sw)
            sbf.dcnt[sw] = 0
            self.sems[sbf.dsem[sw]] = self._newsem(f"d_{self.nsem}")
        deps = self._deps(reads, writes)
        self._wait(q, deps)
        e = self.engs[q]
        for (o, i) in pairs:
            e.dma_start(out=o, in_=i, **kw).then_inc(self.sems[sbf.dsem[sw]], 16)
            sbf.dcnt[sw] += 16
        tok = (sbf.dsem[sw], sbf.dcnt[sw])
        self._commit(tok, reads, writes)
        if sink is not None and sink.toks.get(tok[0], 0) < tok[1]:
            sink.toks[tok[0]] = tok[1]
        return tok

    def wait_all(self, eng, bufs):
        deps = {}
        for b in bufs:
            d = dict(b.toks) if isinstance(b, Sink) else self._deps([b], [b])
            for kk, v in d.items():
                if deps.get(kk, 0) < v:
                    deps[kk] = v
        self._wait(eng, deps)

    def barrier(self, dbufs=()):
        deps = {}
        for e in ("pe", "act", "dve"):
            if self.cnt[e] == 0:
                continue
            ep = (self.cnt[e] - 1) // EPOCH
            deps[(e, ep)] = self.cnt[e] - ep * EPOCH
        for b in dbufs:
            for kk, v in self._deps([b], [b]).items():
                if deps.get(kk, 0) < v:
                    deps[kk] = v
        for e in ("pe", "act", "dve", "pool", "sp"):
            self._wait(e, dict(deps))


class Prog:
    def __init__(self, nc, k):
        self.nc = nc
        self.k = k
        self.WA = [k.sb(f"WA{i}", [128, 8, 256], BF16) for i in range(4)]
        self.WAb = [Buf(f"WA{i}") for i in range(4)]
        self.WBf = k.sb("WBf", [128, 3, 1024], F32)
        self.WB = [self.WBf[:, i, :].bitcast(BF16).rearrange("p (f n) -> p f n", f=2) for i in range(3)]
        self.WBb = [Buf(f"WB{i}") for i in range(3)]
        self.wa_i = 0
        self.PS = [k.ps(f"ps{i}", [128, 512]) for i in range(8)]
        self.PSb = [Buf(f"ps{i}") for i in range(8)]
        self.rr = {}

    def next_wa(self, parity=None):
        i = self.wa_i
        self.wa_i = (self.wa_i + 1) % 4
        return i

    def bank(self, group, banks):
        i = self.rr.get(group, 0)
        self.rr[group] = (i + 1) % len(banks)
        b = banks[i]
        return self.PS[b], self.PSb[b]

    def load_wa(self, src_ap):
        i = self.next_wa()
        self.k.dma("pool", [(self.WA[i][:], src_ap.rearrange("(c p) n -> p c n", p=128))], writes=[self.WAb[i]])
        return self.WA[i], self.WAb[i]

    def load_wb(self, i, src_ap):
        self.k.dma("pool", [(self.WB[i], src_ap.rearrange("(f p) n -> p f n", p=128))], writes=[self.WBb[i]])
        return self.WB[i], self.WBb[i]


def emit_mod(P, cond_bf, cond_b, wmod_l, bmod_l, modv, modb, a_out, normg_l):
    k = P.k
    ps, psb = P.PS[7], P.PSb[7]
    for ch in range(36):
        wa, wab = P.load_wa(wmod_l[:, ch * 256:(ch + 1) * 256])
        for cl in range(2):
            cc = ch * 2 + cl
            for kc in range(8):
                k.op("pe", lambda e, cc=cc, kc=kc, cl=cl, wa=wa: e.matmul(
                    ps[:, cc:cc + 1], lhsT=wa[:, kc, cl * 128:(cl + 1) * 128], rhs=cond_bf[:, kc:kc + 1],
                    start=(kc == 0), stop=(kc == 7)),
                    reads=[wab, cond_b], writes=[psb])
    k.op("dve", lambda e: e.tensor_tensor(out=modv[:], in0=ps[:, 0:72], in1=bmod_l, op=ALU.add),
         reads=[psb], writes=[modb])
    for i in range(3):
        k.op("dve", lambda e, i=i: e.scalar_tensor_tensor(
            out=a_out[:, i, :], in0=modv[:, (i * 3 + 1) * 8:(i * 3 + 2) * 8], scalar=1.0, in1=normg_l[:, i, :],
            op0=ALU.add, op1=ALU.mult), reads=[modb], writes=[modb])
    for i in (0, 2):
        k.op("dve", lambda e, i=i: e.tensor_scalar(
            out=modv[:, (i * 3 + 2) * 8:(i * 3 + 3) * 8], in0=modv[:, (i * 3 + 2) * 8:(i * 3 + 3) * 8],
            scalar1=0.5, scalar2=None, op0=ALU.mult), reads=[modb], writes=[modb])


class Act:
    def __init__(self, t, tiles, name):
        self.t = t
        self.tiles = tiles
        self.b = [Buf(f"{name}{j}") for j in range(len(tiles))]

    def ap(self, c, j):
        c0, w = self.tiles[j]
        return self.t[:, c, c0:c0 + w]


def emit_adaln(P, X, H, a_ap, shift_ap, C):
    k = P.k
    for j, (c0, w) in enumerate(X.tiles):
        ps, psb = P.PS[6], P.PSb[6]
        for c in range(8):
            sq, sqb = C.tmpbf()
            k.op("act", lambda e, c=c, j=j, w=w, sq=sq: e.activation(out=sq[:, 0:w], in_=X.ap(c, j), func=AF.Square),
                 reads=[X.b[j]], writes=[sqb])
            k.op("pe", lambda e, c=c, w=w, sq=sq: e.matmul(ps[:, 0:w], lhsT=C.ones_bf[:], rhs=sq[:, 0:w],
                                                     start=(c == 0), stop=(c == 7)),
                 reads=[sqb, C.cb], writes=[psb])
        k.op("act", lambda e, w=w: e.activation(out=C.RS[:, 0:w], in_=ps[:, 0:w], func=AF.Sqrt,
                                                bias=C.eps[:, 0:1], scale=1.0 / D),
             reads=[psb, C.cb], writes=[C.RSb])
        k.op("dve", lambda e, w=w: e.reciprocal(out=C.RS[:, 0:w], in_=C.RS[:, 0:w]), reads=[C.RSb], writes=[C.RSb])
        for c in range(8):
            tt, ttb = C.tmp()
            k.op("dve", lambda e, c=c, j=j, w=w, tt=tt: e.scalar_tensor_tensor(
                out=tt[:, 0:w], in0=X.ap(c, j), scalar=a_ap[:, c:c + 1], in1=C.RS[:, 0:w],
                op0=ALU.mult, op1=ALU.mult), reads=[X.b[j], C.RSb, C.modb], writes=[ttb])
            k.op("act", lambda e, c=c, j=j, w=w, tt=tt: e.activation(
                out=H.ap(c, j), in_=tt[:, 0:w], func=AF.Identity, bias=shift_ap[:, c:c + 1], scale=1.0),
                reads=[ttb, C.modb], writes=[H.b[j]])


def emit_ffn(P, X, H, U, Ub, w1, w3, w2, gate_ap, C):
    k = P.k
    groups = [(0, 3), (3, 3), (6, 3), (9, 2)]
    ntile = len(X.tiles)
    for (p0, npair) in groups:
        for pl in range(npair):
            p = p0 + pl
            wa1, wa1b = P.load_wa(w1[:, p * 256:(p + 1) * 256])
            wa3, wa3b = P.load_wa(w3[:, p * 256:(p + 1) * 256])
            for fl in range(2):
                fu = pl * 2 + fl
                for j in range(ntile):
                    c0, w = X.tiles[j]
                    ps1, ps1b = P.bank("h1", [0, 1])
                    ps3, ps3b = P.bank("h3", [2, 3])
                    for c in range(8):
                        k.op("pe", lambda e, c=c, j=j, w=w, ps1=ps1, wa1=wa1, fl=fl: e.matmul(
                            ps1[:, 0:w], lhsT=wa1[:, c, fl * 128:(fl + 1) * 128], rhs=H.ap(c, j),
                            start=(c == 0), stop=(c == 7)), reads=[wa1b, H.b[j]], writes=[ps1b])
                    for c in range(8):
                        k.op("pe", lambda e, c=c, j=j, w=w, ps3=ps3, wa3=wa3, fl=fl: e.matmul(
                            ps3[:, 0:w], lhsT=wa3[:, c, fl * 128:(fl + 1) * 128], rhs=H.ap(c, j),
                            start=(c == 0), stop=(c == 7)), reads=[wa3b, H.b[j]], writes=[ps3b])
                    sl, slb = C.tmp()
                    k.op("act", lambda e, w=w, ps1=ps1, sl=sl: e.activation(out=sl[:, 0:w], in_=ps1[:, 0:w], func=AF.Silu),
                         reads=[ps1b], writes=[slb])
                    k.op("dve", lambda e, w=w, c0=c0, ps3=ps3, sl=sl, fu=fu: e.tensor_tensor(
                        out=U[:, fu, c0:c0 + w], in0=ps3[:, 0:w], in1=sl[:, 0:w], op=ALU.mult),
                        reads=[ps3b, slb], writes=[Ub[fu][j]])
        for pl in range(npair):
            P.load_wb(pl, w2[(p0 + pl) * 256:(p0 + pl + 1) * 256, :])
        nfu = npair * 2
        for c in range(8):
            for j in range(ntile):
                c0, w = X.tiles[j]
                pso, psob = P.bank("o", [4, 5])
                for fu in range(nfu):
                    k.op("pe", lambda e, fu=fu, c=c, w=w, c0=c0, pso=pso: e.matmul(
                        pso[:, 0:w], lhsT=P.WB[fu // 2][:, fu % 2, c * 128:(c + 1) * 128], rhs=U[:, fu, c0:c0 + w],
                        start=(fu == 0), stop=(fu == nfu - 1)), reads=[P.WBb[fu // 2], Ub[fu][j]], writes=[psob])
                k.op("dve", lambda e, c=c, j=j, w=w, pso=pso: e.scalar_tensor_tensor(
                    out=X.ap(c, j), in0=pso[:, 0:w], scalar=gate_ap[:, c:c + 1], in1=X.ap(c, j),
                    op0=ALU.mult, op1=ALU.add), reads=[psob, X.b[j], C.modb], writes=[X.b[j]])


class Common:
    pass


def emit_inproj_tile(P, H, j, wa, wab, cl, ps, psb, cols=None):
    k = P.k
    c0, w = H.tiles[j]
    if cols is not None:
        c0, w = c0 + cols[0], cols[1]
    for c in range(8):
        k.op("pe", lambda e, c=c, c0=c0, w=w: e.matmul(
            ps[:, 0:w], lhsT=wa[:, c, cl * 128:(cl + 1) * 128], rhs=H.t[:, c, c0:c0 + w],
            start=(c == 0), stop=(c == 7)), reads=[wab, H.b[j]], writes=[psb])
    return w


def emit_qknorm(P, C, ps, psb, w, g_ap, out_ap, out_b, extra_reads=()):
    k = P.k
    sq, sqb = C.tmpbf()
    k.op("act", lambda e: e.activation(out=sq[:, 0:w], in_=ps[:, 0:w], func=AF.Square), reads=[psb], writes=[sqb])
    p2, p2b = P.PS[6], P.PSb[6]
    k.op("pe", lambda e: e.matmul(p2[:, 0:w], lhsT=C.bd_bf[:], rhs=sq[:, 0:w], start=True, stop=True),
         reads=[sqb, C.cb], writes=[p2b])
    rs, rsb = C.tmp()
    k.op("act", lambda e: e.activation(out=rs[:, 0:w], in_=p2[:, 0:w], func=AF.Sqrt, bias=C.eps[:, 0:1],
                                       scale=1.0 / HD), reads=[p2b, C.cb], writes=[rsb])
    k.op("dve", lambda e: e.reciprocal(out=rs[:, 0:w], in_=rs[:, 0:w]), reads=[rsb], writes=[rsb])
    k.op("dve", lambda e: e.scalar_tensor_tensor(out=out_ap, in0=ps[:, 0:w], scalar=g_ap, in1=rs[:, 0:w],
                                                 op0=ALU.mult, op1=ALU.mult),
         reads=[psb, rsb, C.cb] + list(extra_reads), writes=[out_b])


def emit_outproj(P, C, X, MIX, MIXb, wout_l, gate_ap):
    k = P.k
    for pr in range(4):
        wa, wab = P.load_wa(wout_l[:, pr * 256:(pr + 1) * 256])
        for cl in range(2):
            c = pr * 2 + cl
            for j, (c0, w) in enumerate(X.tiles):
                pso, psob = P.bank("o", [4, 5])
                for fc in range(8):
                    k.op("pe", lambda e, fc=fc, c0=c0, w=w, pso=pso, wa=wa, cl=cl: e.matmul(
                        pso[:, 0:w], lhsT=wa[:, fc, cl * 128:(cl + 1) * 128], rhs=MIX(fc, c0, w),
                        start=(fc == 0), stop=(fc == 7)), reads=[wab, MIXb[j]], writes=[psob])
                k.op("dve", lambda e, c=c, j=j, w=w, pso=pso: e.scalar_tensor_tensor(
                    out=X.ap(c, j), in0=pso[:, 0:w], scalar=gate_ap[:, c:c + 1], in1=X.ap(c, j),
                    op0=ALU.mult, op1=ALU.add), reads=[psob, X.b[j], C.modb], writes=[X.b[j]])


def emit_expb(P, C, relrep, relb, onehot, vmask, scr, EXPB, EXPBb):
    k = P.k
    nc = P.nc
    E8 = C.E8
    E8b = Buf("E8")
    for h in range(8):
        lt, ltb = C.tmp()
        k.op("dve", lambda e, h=h, lt=lt: e.tensor_scalar(out=lt[0:32, 0:128], in0=C.ones_f[0:32, 0:128],
                                                          scalar1=relrep[0:32, h:h + 1], scalar2=None, op0=ALU.mult),
             reads=[relb, C.cb], writes=[ltb])
        ps, psb = P.bank("h1", [0, 1])
        k.op("pe", lambda e, lt=lt, ps=ps: e.matmul(ps[:, 0:512], lhsT=lt[0:32, 0:128], rhs=onehot[0:32, :],
                                                    start=True, stop=True), reads=[ltb, C.cb], writes=[psb])
        ex, exb = C.tmp()
        k.op("act", lambda e, ps=ps, ex=ex: e.activation(out=ex[:, :], in_=ps[:, 0:512], func=AF.Exp),
             reads=[psb], writes=[exb])
        k.op("dve", lambda e, h=h, ex=ex: e.tensor_tensor(out=E8[:, h, :], in0=ex[:, :], in1=vmask[:, :], op=ALU.mult),
             reads=[exb, C.cb], writes=[E8b])
    scrb = Buf("scr")
    k.dma("sp", [(scr.ap(), E8[:])], reads=[E8b], writes=[scrb])
    pairs = []
    for h in range(8):
        for bi in range(3):
            off = h * 512 + 128 * (1 - bi) + 255
            src = bass.AP(tensor=scr, offset=off, ap=[[8 * 512 - 1, 128], [1, 128]])
            pairs.append((EXPB[:, h, bi, :], src))
    k.dma("sp", pairs, reads=[scrb], writes=[EXPBb])
    return scrb


def emit_conv0(P, C, H, MIX, MIXb, win_l, convw, convb, flags, S_t, Sb):
    k = P.k
    ntile_own = 4
    for hh in range(2):
        wgb, wgbb = P.load_wa(win_l[:, 768 + hh * 256: 768 + (hh + 1) * 256])
        wgc, wgcb = P.load_wa(win_l[:, 1280 + hh * 256: 1280 + (hh + 1) * 256])
        wu, wub = P.load_wa(win_l[:, 1792 + hh * 256: 1792 + (hh + 1) * 256])
        for cl in range(2):
            cc = hh * 2 + cl
            for j in range(ntile_own):
                c0, w = H.tiles[j]
                pg, pgb = P.bank("h1", [0, 1])
                pu, pub = P.bank("h3", [2, 3])
                emit_inproj_tile(P, H, j, wgc, wgcb, cl, pg, pgb)
                emit_inproj_tile(P, H, j, wu, wub, cl, pu, pub)
                tg, tgb = C.tmp()
                k.op("act", lambda e, tg=tg, pg=pg, w=w: e.activation(out=tg[:, 0:w], in_=pg[:, 0:w], func=AF.Copy),
                     reads=[pgb], writes=[tgb])
                k.op("dve", lambda e, tg=tg, pu=pu, w=w, c0=c0: e.tensor_tensor(
                    out=S_t[:, 1 + c0:1 + c0 + w], in0=pu[:, 0:w], in1=tg[:, 0:w], op=ALU.mult),
                    reads=[pub, tgb], writes=[Sb])
            pg, pgb = P.bank("h1", [0, 1])
            pu, pub = P.bank("h3", [2, 3])
            emit_inproj_tile(P, H, 4, wgc, wgcb, cl, pg, pgb, cols=(127, 2))
            emit_inproj_tile(P, H, 4, wu, wub, cl, pu, pub, cols=(127, 2))
            tg, tgb = C.tmp()
            k.op("act", lambda e, tg=tg, pg=pg: e.activation(out=tg[:, 0:2], in_=pg[:, 0:2], func=AF.Copy),
                 reads=[pgb], writes=[tgb])
            k.op("dve", lambda e, tg=tg, pu=pu: e.scalar_tensor_tensor(
                out=S_t[:, 0:1], in0=pu[:, 0:1], scalar=flags[:, 0:1], in1=tg[:, 0:1], op0=ALU.mult, op1=ALU.mult),
                reads=[pub, tgb, C.cb], writes=[Sb])
            k.op("dve", lambda e, tg=tg, pu=pu: e.scalar_tensor_tensor(
                out=S_t[:, 2049:2050], in0=pu[:, 1:2], scalar=flags[:, 1:2], in1=tg[:, 1:2], op0=ALU.mult, op1=ALU.mult),
                reads=[pub, tgb, C.cb], writes=[Sb])
            for j in range(ntile_own):
                c0, w = H.tiles[j]
                pb_, pbb = P.bank("o", [4, 5])
                emit_inproj_tile(P, H, j, wgb, wgbb, cl, pb_, pbb)
                t, tb = C.tmp()
                k.op("dve", lambda e, t=t, c0=c0, w=w, cc=cc: e.tensor_scalar(
                    out=t[:, 0:w], in0=S_t[:, 1 + c0:1 + c0 + w], scalar1=convw[:, cc, 1:2], scalar2=convb[:, cc:cc + 1],
                    op0=ALU.mult, op1=ALU.add), reads=[Sb, C.cb], writes=[tb])
                k.op("dve", lambda e, t=t, c0=c0, w=w, cc=cc: e.scalar_tensor_tensor(
                    out=t[:, 0:w], in0=S_t[:, c0:c0 + w], scalar=convw[:, cc, 0:1], in1=t[:, 0:w],
                    op0=ALU.mult, op1=ALU.add), reads=[Sb, C.cb, tb], writes=[tb])
                k.op("dve", lambda e, t=t, c0=c0, w=w, cc=cc: e.scalar_tensor_tensor(
                    out=t[:, 0:w], in0=S_t[:, 2 + c0:2 + c0 + w], scalar=convw[:, cc, 2:3], in1=t[:, 0:w],
                    op0=ALU.mult, op1=ALU.add), reads=[Sb, C.cb, tb], writes=[tb])
                k.op("dve", lambda e, t=t, c0=c0, w=w, cc=cc, pb_=pb_: e.tensor_tensor(
                    out=MIX(4 + cc, c0, w), in0=pb_[:, 0:w], in1=t[:, 0:w], op=ALU.mult),
                    reads=[pbb, tb], writes=[MIXb[j]])


def emit_qkv0(P, C, H, win_l, QT, QTb, KT, KTb, VA, VAb, gq, gk, flags):
    k = P.k
    for pr in range(2):
        wa, wab = P.load_wa(win_l[:, pr * 256:(pr + 1) * 256])
        for cl in range(2):
            hh = pr * 2 + cl
            for j in range(4):
                c0, w = H.tiles[j]
                ps, psb = P.bank("h1", [0, 1, 2, 3])
                emit_inproj_tile(P, H, j, wa, wab, cl, ps, psb)
                emit_qknorm(P, C, ps, psb, w, gq[:, 0:1], QT[:, hh, c0:c0 + w], QTb[j])
    wa, wab = P.load_wa(win_l[:, 512:768])
    for j in range(5):
        c0, w = H.tiles[j]
        ps, psb = P.bank("h1", [0, 1, 2, 3])
        emit_inproj_tile(P, H, j, wa, wab, 0, ps, psb)
        emit_qknorm(P, C, ps, psb, w, gk[:, 0:1], KT[:, c0:c0 + w], KTb[j])
    for tb in range(18):
        j = tb // 4 if tb < 16 else 4
        ps, psb = P.bank("o", [4, 5])
        for c in range(8):
            k.op("pe", lambda e, c=c, tb=tb, ps=ps: e.matmul(
                ps[:, 0:128], lhsT=H.t[:, c, tb * 128:(tb + 1) * 128], rhs=wa[:, c, 128:256],
                start=(c == 0), stop=(c == 7)), reads=[wab, H.b[j]], writes=[psb])
        if tb < 16:
            k.op("act", lambda e, tb=tb, ps=ps: e.activation(out=VA[:, tb, 0, 0:64], in_=ps[:, 0:64], func=AF.Copy),
                 reads=[psb], writes=[VAb])
            k.op("act", lambda e, tb=tb, ps=ps: e.activation(out=VA[:, tb, 1, 64:128], in_=ps[:, 64:128], func=AF.Copy),
                 reads=[psb], writes=[VAb])
        else:
            fl = flags[:, tb - 16:tb - 15]
            k.op("dve", lambda e, tb=tb, ps=ps, fl=fl: e.tensor_scalar(
                out=VA[:, tb, 0, 0:64], in0=ps[:, 0:64], scalar1=fl, scalar2=None, op0=ALU.mult),
                reads=[psb, C.cb], writes=[VAb])
            k.op("dve", lambda e, tb=tb, ps=ps, fl=fl: e.tensor_scalar(
                out=VA[:, tb, 1, 64:128], in0=ps[:, 64:128], scalar1=fl, scalar2=None, op0=ALU.mult),
                reads=[psb, C.cb], writes=[VAb])
            k.op("dve", lambda e, tb=tb, fl=fl: e.tensor_scalar(
                out=VA[:, tb, 0, 64:128], in0=C.ones_f[:, 0:64], scalar1=fl, scalar2=None, op0=ALU.mult),
                reads=[C.cb], writes=[VAb])
            k.op("dve", lambda e, tb=tb, fl=fl: e.tensor_scalar(
                out=VA[:, tb, 1, 0:64], in0=C.ones_f[:, 0:64], scalar1=fl, scalar2=None, op0=ALU.mult),
                reads=[C.cb], writes=[VAb])


def emit_attn0(P, C, QT, QTb, KT, KTb, VA, VAb, EXPB, EXPBb, expsink, MIXLO, MIXb):
    k = P.k
    for n in range(16):
        j = n // 4
        for g in range(2):
            gs = slice(g * 64, (g + 1) * 64)
            ds_ = slice((1 - g) * 64, (2 - g) * 64)
            po, pob = P.bank("o", [4, 5])
            for bi in range(3):
                kb = n - 1 + bi
                kidx = 16 if kb < 0 else (17 if kb > 15 else kb)
                kj = kidx // 4 if kidx < 16 else 4
                ps, psb = P.bank("h1", [0, 1, 2, 3])
                k.op("pe", lambda e, ps=ps, kidx=kidx, n=n, gs=gs: e.matmul(
                    ps[:, 0:512], lhsT=KT[gs, kidx * 128:(kidx + 1) * 128], rhs=QT[gs, :, n * 128:(n + 1) * 128],
                    start=True, stop=True), reads=[KTb[kj], QTb[j]], writes=[psb])
                ex, exb = C.tmp()
                k.op("act", lambda e, ps=ps, ex=ex: e.activation(out=ex[:, :], in_=ps[:, 0:512], func=AF.Exp,
                                                                 scale=HD ** -0.5),
                     reads=[psb], writes=[exb])
                pt, ptb = C.tmpbf()
                k.op("dve", lambda e, ex=ex, pt=pt, g=g, bi=bi: e.tensor_tensor(
                    out=pt[:, :].rearrange("p (h q) -> p h q", h=4), in0=ex[:, :].rearrange("p (h q) -> p h q", h=4),
                    in1=EXPB[:, g * 4:(g + 1) * 4, bi, :], op=ALU.mult), reads=[exb, EXPBb], writes=[ptb])
                for hh in range(4):
                    k.op("pe", lambda e, po=po, pt=pt, hh=hh, kidx=kidx, g=g, bi=bi: e.matmul(
                        po[:, hh * 128:(hh + 1) * 128], lhsT=VA[:, kidx, g, :], rhs=pt[:, hh * 128:(hh + 1) * 128],
                        start=(bi == 0 and hh == 0), stop=(bi == 2 and hh == 3), skip_group_check=True),
                        reads=[VAb, ptb], writes=[pob])
            rd, rdb = C.tmp()
            for hh in range(4):
                k.op("dve", lambda e, rd=rd, po=po, hh=hh, g=g, ds_=ds_: e.tensor_scalar(
                    out=rd[ds_, hh * 128:(hh + 1) * 128], in0=po[ds_, hh * 128:(hh + 1) * 128],
                    scalar1=expsink[ds_, g * 4 + hh:g * 4 + hh + 1], scalar2=None, op0=ALU.add),
                    reads=[pob, C.cb], writes=[rdb])
            k.op("dve", lambda e, rd=rd, ds_=ds_: e.reciprocal(out=rd[ds_, :], in_=rd[ds_, :]), reads=[rdb], writes=[rdb])
            k.op("dve", lambda e, rd=rd, po=po, gs=gs, ds_=ds_, n=n: e.tensor_tensor(
                out=MIXLO[gs, 0:4, n * 128:(n + 1) * 128], in0=po[gs, :].rearrange("p (h q) -> p h q", h=4),
                in1=rd[ds_, :].rearrange("p (h q) -> p h q", h=4), op=ALU.mult), reads=[pob, rdb], writes=[MIXb[j]])


class RotTmp:
    def __init__(self, k, name, n, dt):
        self.t = [k.sb(f"{name}{i}", [128, 512], dt) for i in range(n)]
        self.b = [Buf(f"{name}{i}") for i in range(n)]
        self.i = 0

    def __call__(self):
        i = self.i
        self.i = (i + 1) % len(self.t)
        return self.t[i], self.b[i]


BUCKET_MODE = "trunc"


def t5_bucket_np(rel):
    n = np.abs(rel)
    v = (np.log(np.maximum(n, 1).astype(np.float32) / np.float32(8)) / np.float32(math.log(16.0)) * np.float32(8))
    large = 8 + (np.rint(v).astype(np.int32) if BUCKET_MODE == "round" else v.astype(np.int32))
    large = np.minimum(large, 15)
    return np.where(rel > 0, 16, 0) + np.where(n < 8, n, large)


def build_LA(do_l1=True, stop_after=None):
    nc = bass.Bass("TRN2", target_bir_lowering=False)
    k = KB(nc)
    dt_in = lambda name, shape: nc.dram_tensor(name, list(shape), F32, kind="ExternalInput").ap()
    xT_d = dt_in("xT", [128, 8, 2048])
    xH_d = dt_in("xH", [128, 8, 256])
    cf_d = dt_in("cf", [128, 1400])
    oh_d = dt_in("oh", [32, 512])
    vm_d = dt_in("vm", [128, 512])
    wmod_d = dt_in("wmod", [2, 1024, 9216])
    w1_d = dt_in("w1", [2, 2, 1024, DFFP])
    w3_d = dt_in("w3", [2, 2, 1024, DFFP])
    w2_d = dt_in("w2", [2, 2, DFFP, 1024])
    win_d = dt_in("win", [2, 1024, INC])
    wout_d = dt_in("wout", [2, 1024, 1024])
    xo_d = nc.dram_tensor("xo", [128, 8, 2048], F32, kind="ExternalOutput").ap()
    scr = nc.dram_tensor("scr", [128, 8 * 512], F32, kind="Internal")

    P = Prog(nc, k)
    C = Common()
    C.tmp = RotTmp(k, "tf", 5, F32)
    C.tmpbf = RotTmp(k, "tb", 3, BF16)
    C.RS = k.sb("RS", [128, 512], F32)
    C.RSb = Buf("RS")
    C.cb = Buf("consts")
    C.modb = Buf("mod")
    CF = k.sb("CF", [128, 1400], F32)
    OH = k.sb("OH", [32, 512], F32)
    VM = k.sb("VM", [128, 512], F32)
    C.ones_bf = k.sb("ones_bf", [128, 128], BF16)
    C.bd_bf = k.sb("bd_bf", [128, 128], BF16)
    C.ones_f = k.sb("ones_f", [128, 128], F32)
    C.eps = k.sb("eps", [128, 1], F32)
    condbf = k.sb("condbf", [128, 8], BF16)
    MODV = k.sb("modv", [128, 2, 72], F32)
    AV = k.sb("av", [128, 2, 3, 8], F32)
    EXS = k.sb("exs", [128, 8], F32)
    XT = k.sb("XT", [128, 8, 2048], F32)
    Ht = k.sb("H", [128, 8, 2304], BF16)
    UQ = k.sb("UQ", [128, 15360], BF16)
    MX = k.sb("MX", [128, 4096], F32)

    o_cT, o_fl, o_ng, o_bm, o_cw, o_cb, o_gq, o_gk, o_sink = 0, 8, 10, 58, 202, 214, 218, 220, 222
    cT = CF[:, o_cT:o_cT + 8]
    flags = CF[:, o_fl:o_fl + 2]
    normg = CF[:, o_ng:o_ng + 48].rearrange("p (l i c) -> p l i c", l=2, i=3)
    bmod = CF[:, o_bm:o_bm + 144].rearrange("p (l m) -> p l m", l=2)
    convw = CF[:, o_cw:o_cw + 12].rearrange("p (c t) -> p c t", c=4)
    convb = CF[:, o_cb:o_cb + 4]
    gq = CF[:, o_gq:o_gq + 2]
    gk = CF[:, o_gk:o_gk + 2]
    sink = CF[:, o_sink:o_sink + 8]
    relrep = CF[:, 232:240]

    XH = MX[:, 0:2048].rearrange("p (c t) -> p c t", c=8)
    MIXHI = MX[:, :].bitcast(BF16).rearrange("p (c t) -> p c t", c=4)
    U = UQ[:, 0:6 * 2304].rearrange("p (f t) -> p f t", f=6)
    QT = UQ[:, 0:8192].rearrange("p (h t) -> p h t", h=4)
    KT = UQ[:, 8192:8192 + 2304]
    VA = UQ[:, 10496:10496 + 4608].rearrange("p (b g d) -> p b g d", b=18, g=2)
    S_t = UQ[:, 10496:10496 + 4100].bitcast(F32)
    C.E8 = UQ[:, 0:8192].bitcast(F32).rearrange("p (h m) -> p h m", h=8)
    EXPB = P.WBf[:, :, :].rearrange("p a (b q) -> p (a b) q", q=128).rearrange("p (h b) q -> p h b q", h=8)

    tiles5 = [(0, 512), (512, 512), (1024, 512), (1536, 512), (2048, 256)]
    tiles4 = tiles5[:4]

    class XAct:
        def __init__(self, tiles):
            self.tiles = tiles
            self.b = XB[:len(tiles)]

        def ap(self, c, j):
            if j < 4:
                return XT[:, c, j * 512:(j + 1) * 512]
            return XH[:, c, :]
    XB = [Buf(f"x{j}") for j in range(5)]
    X5, X4 = XAct(tiles5), XAct(tiles4)
    H5 = Act(Ht, tiles5, "h")
    H4 = Act(Ht, tiles4, "h")
    H4.b = H5.b[:4]
    Ub = [[Buf(f"u{f}_{j}") for j in range(5)] for f in range(6)]

    k.dma("sp", [(CF[:], cf_d)], writes=[C.cb])
    k.dma("sp", [(OH[:], oh_d), (VM[:], vm_d)], writes=[C.cb], sembuf=C.cb)
    for j in range(4):
        k.dma("sp", [(XT[:, :, j * 512:(j + 1) * 512], xT_d[:, :, j * 512:(j + 1) * 512])], writes=[XB[j]])
    k.dma("sp", [(XH, xH_d)], writes=[XB[4]])
    cst = Buf("cst")
    k.op("dve", lambda e: e.memset(C.ones_bf[:], 1.0), writes=[cst])
    k.op("dve", lambda e: e.memset(C.ones_f[:], 1.0), writes=[cst])
    k.op("dve", lambda e: e.memset(C.eps[:], EPS), writes=[cst])
    k.op("dve", lambda e: e.memset(C.bd_bf[:], 0.0), writes=[cst])
    k.op("dve", lambda e: e.memset(C.bd_bf[0:64, 0:64], 1.0), writes=[cst])
    k.op("dve", lambda e: e.memset(C.bd_bf[64:128, 64:128], 1.0), writes=[cst])
    condb = Buf("cond")
    k.op("act", lambda e: e.activation(out=condbf[:], in_=cT, func=AF.Silu), reads=[C.cb], writes=[condb])
    k.op("act", lambda e: e.activation(out=EXS[:], in_=sink, func=AF.Exp), reads=[C.cb], writes=[cst])
    k.op("dve", lambda e: e.tensor_copy(out=C.eps[:], in_=C.eps[:]), reads=[cst, C.cb], writes=[C.cb])

    def layer_mod(l):
        emit_mod(P, condbf, condb, wmod_d[l], bmod[:, l, :], MODV[:, l, :], C.modb, AV[:, l], normg[:, l])

    def mcol(l, i, kind):
        return MODV[:, l, (i * 3 + kind) * 8:(i * 3 + kind + 1) * 8]

    def mix_ap(fc, c0, w):
        if fc < 4:
            return Ht[:, fc, c0:c0 + w]
        return MIXHI[:, fc - 4, c0:c0 + w]

    def finish():
        ob = Sink("out")
        for j in range(4):
            k.dma("sp", [(xo_d[:, :, j * 512:(j + 1) * 512], XT[:, :, j * 512:(j + 1) * 512])], reads=[XB[j]],
                  sembuf=XB[j], sink=ob)
        k.wait_all("sp", [ob])
        k.close()
        return nc

    layer_mod(0)
    emit_adaln(P, X5, H5, AV[:, 0, 0], mcol(0, 0, 0), C)
    emit_ffn(P, X5, H5, U, Ub, w1_d[0, 0], w3_d[0, 0], w2_d[0, 0], mcol(0, 0, 2), C)
    if stop_after == "ffn0":
        return finish()
    k.barrier()
    ohb = C.cb
    scrb = emit_expb(P, C, relrep, C.cb, OH, VM, scr, EXPB, P.WBb[0])
    k.barrier(dbufs=[scrb])
    if stop_after == "expb":
        dbg = nc.dram_tensor("dbg", [128, 3072], F32, kind="ExternalOutput").ap()
        ob2 = Buf("dbg")
        k.dma("sp", [(dbg, P.WBf[:].rearrange("p a b -> p (a b)"))], reads=[P.WBb[0]], writes=[ob2], sembuf=ob2)
        k.wait_all("sp", [ob2])
        return finish()
    emit_adaln(P, X5, H5, AV[:, 0, 1], mcol(0, 1, 0), C)
    k.barrier()
    MIXb = [Buf(f"mix{j}") for j in range(4)]
    Sb = Buf("S")
    emit_conv0(P, C, H5, mix_ap, MIXb, win_d[0], convw, convb, flags, S_t, Sb)
    k.barrier()
    QTb = [Buf(f"qt{j}") for j in range(4)]
    KTb = [Buf(f"kt{j}") for j in range(5)]
    VAb = Buf("va")
    k.op("dve", lambda e: e.memset(VA[:, 0:16, 0, 64:128], 1.0), writes=[VAb])
    k.op("dve", lambda e: e.memset(VA[:, 0:16, 1, 0:64], 1.0), writes=[VAb])
    emit_qkv0(P, C, H5, win_d[0], QT, QTb, KT, KTb, VA, VAb, gq, gk, flags)
    k.barrier()
    emit_attn0(P, C, QT, QTb, KT, KTb, VA, VAb, EXPB, P.WBb[0], EXS, Ht, MIXb)
    for i in (1, 2):
        P.WBb[i].r = dict(P.WBb[0].r)
        P.WBb[i].w = P.WBb[0].w
    if stop_after == "mix":
        dbg = nc.dram_tensor("dbg", [128, 8, 2048], BF16, kind="ExternalOutput").ap()
        ob2 = Buf("dbg")
        k.dma("sp", [(dbg[:, 0:4, :], Ht[:, 0:4, 0:2048]), (dbg[:, 4:8, :], MIXHI)], reads=MIXb, writes=[ob2], sembuf=ob2)
        k.wait_all("sp", [ob2])
        return finish()
    emit_outproj(P, C, X4, mix_ap, MIXb, wout_d[0], mcol(0, 1, 2))
    if stop_after == "mixer":
        return finish()
    k.barrier()
    emit_adaln(P, X4, H4, AV[:, 0, 2], mcol(0, 2, 0), C)
    emit_ffn(P, X4, H4, U, Ub, w1_d[0, 1], w3_d[0, 1], w2_d[0, 1], mcol(0, 2, 2), C)
    if not do_l1:
        return finish()

    rope_d = dt_in("rope", [128, 2, 2048])
    qo_d = nc.dram_tensor("qo", [128, 4, 2048], BF16, kind="ExternalOutput").ap()
    ko_d = nc.dram_tensor("ko", [128, 2048], BF16, kind="ExternalOutput").ap()
    vo_d = nc.dram_tensor("vo", [16, 128, 128], BF16, kind="ExternalOutput").ap()
    hco_d = nc.dram_tensor("hco", [12, 128, 2048], F32, kind="ExternalOutput").ap()
    layer_mod(1)
    emit_adaln(P, X4, H4, AV[:, 1, 0], mcol(1, 0, 0), C)
    emit_ffn(P, X4, H4, U, Ub, w1_d[1, 0], w3_d[1, 0], w2_d[1, 0], mcol(1, 0, 2), C)
    k.barrier()
    ROPE = UQ[:, 0:8192].bitcast(F32).rearrange("p (a t) -> p a t", a=2)
    ropeb = Buf("rope")
    k.dma("sp", [(ROPE, rope_d)], writes=[ropeb])
    emit_adaln(P, X4, H4, AV[:, 1, 1], mcol(1, 1, 0), C)
    outb = Sink("outs")
    modo_d = nc.dram_tensor("modo", [128, 96], F32, kind="ExternalOutput").ap()
    k.dma("sp", [(modo_d[:, 0:72], MODV[:, 1, :]), (modo_d[:, 72:96], AV[:, 1].rearrange("p i c -> p (i c)"))],
          reads=[C.modb], sembuf=C.modb, sink=outb)
    emit_inproj1(P, C, H4, win_d[1], CF[:, 240:241], CF[:, 241:242], ROPE[:, 0, :], ROPE[:, 1, :], ropeb,
                 qo_d, ko_d, vo_d, hco_d, outb)
    k.wait_all("sp", [outb])
    return finish()


def host_consts():
    idx = np.arange(512)
    rel = 255 - idx
    valid = (np.abs(rel) <= 128) & (idx < 511)
    bucket = t5_bucket_np(rel.astype(np.int64))
    oh = np.zeros((32, 512), np.float32)
    oh[bucket[valid], idx[valid]] = 1.0
    vm = np.tile(valid.astype(np.float32)[None, :], (128, 1))
    return oh, vm


def fm(v):
    v = np.asarray(v, np.float32)
    sh = v.shape
    v = v.reshape(sh[:-1] + (sh[-1] // 128, 128))
    return np.moveaxis(v, -1, 0)


def prep_LA(inp):
    x = np.asarray(inp["x"], np.float32)
    qperm = np.concatenate([np.r_[hh * 64:(hh + 1) * 64, (4 + hh) * 64:(5 + hh) * 64] for hh in range(4)])
    win = np.ascontiguousarray(np.asarray(inp["w_in"], np.float32))
    win = np.concatenate([win[:, :, qperm], win[:, :, 512:]], axis=2)
    eo = np.r_[0:64:2, 1:64:2]
    eo_q = np.concatenate([h * 64 + eo for h in range(8)])
    eo_k = np.concatenate([512 + h * 64 + eo for h in range(2)])
    win[1] = np.concatenate([win[1][:, eo_q], win[1][:, eo_k], win[1][:, 640:]], axis=1)
    wout = np.asarray(inp["w_out"], np.float32)
    wout = np.ascontiguousarray(np.concatenate([wout[:, qperm, :], wout[:, 512:, :]], axis=1))
    pad = DFFP - DFF
    w1 = np.pad(np.asarray(inp["ffn_w1"], np.float32), ((0, 0), (0, 0), (0, 0), (0, pad)))
    w3 = np.pad(np.asarray(inp["ffn_w3"], np.float32), ((0, 0), (0, 0), (0, 0), (0, pad)))
    w2 = np.pad(np.asarray(inp["ffn_w2"], np.float32), ((0, 0), (0, 0), (0, pad), (0, 0)))
    wmod = np.ascontiguousarray(np.asarray(inp["w_mod"], np.float32))
    oh, vm = host_consts()
    shared = dict(oh=oh, vm=vm, wmod=wmod, w1=w1, w3=w3, w2=w2, win=np.ascontiguousarray(win), wout=wout)
    maps = []
    for core in range(NCORE):
        b, qd = core // 4, core % 4
        t0 = qd * TOK
        xs = x[b, t0:t0 + TOK]
        xT = np.ascontiguousarray(xs.T.reshape(8, 128, TOK).transpose(1, 0, 2))
        halo = np.zeros((256, D), np.float32)
        fl = np.zeros((2,), np.float32)
        if qd > 0:
            halo[0:128] = x[b, t0 - 128:t0]
            fl[0] = 1.0
        if qd < 3:
            halo[128:256] = x[b, t0 + TOK:t0 + TOK + 128]
            fl[1] = 1.0
        xH = np.ascontiguousarray(halo.T.reshape(8, 128, 256).transpose(1, 0, 2))
        cf = np.zeros((128, 1400), np.float32)
        cf[:, 0:8] = fm(inp["c"][b])
        cf[:, 8:10] = fl[None, :]
        cf[:, 10:58] = fm(inp["norm_g"]).reshape(128, 48)
        cf[:, 58:202] = fm(inp["b_mod"]).reshape(128, 144)
        cw = np.asarray(inp["b_conv_w"], np.float32)[0]
        cf[:, 202:214] = fm(cw).transpose(0, 2, 1).reshape(128, 12)
        cf[:, 214:218] = fm(np.asarray(inp["b_conv_b"], np.float32)[0])
        aq = np.asarray(inp["a_qk_g"], np.float32)[0]
        cf[:, 218] = np.tile(aq[0], 2)
        cf[:, 220] = np.tile(aq[1], 2)
        cf[:, 222:230] = np.asarray(inp["a_sink"], np.float32)[0][None, :]
        cf[0:32, 232:240] = np.asarray(inp["rel_table"], np.float32)
        cg = np.asarray(inp["c_qk_g"], np.float32)[0]
        cf[:, 240] = np.tile(cg[0][eo], 2)
        cf[:, 241] = np.tile(cg[1][eo], 2)
        pos = np.arange(t0, t0 + TOK)
        row = (pos // 64).astype(np.float32)
        col = (pos % 64).astype(np.float32)
        inv = (np.float32(10000.0) ** (-np.arange(0, 32, 2, dtype=np.float32) / np.float32(32))).astype(np.float32)
        ang = np.concatenate([row[:, None] * inv, col[:, None] * inv], axis=-1).astype(np.float32)
        cs = np.cos(ang).astype(np.float32).T
        sn = np.sin(ang).astype(np.float32).T
        rope = np.zeros((128, 2, TOK), np.float32)
        for qd4 in range(4):
            rope[qd4 * 32:(qd4 + 1) * 32, 0] = cs
            rope[qd4 * 32:(qd4 + 1) * 32, 1] = sn if qd4 % 2 == 0 else -sn
        m_rope = rope
        m = dict(shared)
        m.update(xT=xT, xH=xH, cf=cf, rope=m_rope)
        maps.append(m)
    return maps


def emit_rope(P, C, qn, qnb, CS, SNs, ropeb, c0, w, out_ap, out_b):
    k = P.k
    t1, t1b = C.tmp()
    k.op("dve", lambda e: e.tensor_tensor(out=t1[:, 0:w], in0=qn[:, 0:w], in1=CS[:, c0:c0 + w], op=ALU.mult),
         reads=[qnb, ropeb], writes=[t1b])
    t2, t2b = C.tmp()
    for qd in range(4):
        src = qd ^ 1
        k.op("dve", lambda e, qd=qd, src=src: e.tensor_tensor(
            out=t2[qd * 32:(qd + 1) * 32, 0:w], in0=qn[src * 32:(src + 1) * 32, 0:w],
            in1=SNs[src * 32:(src + 1) * 32, c0:c0 + w], op=ALU.mult), reads=[qnb, ropeb], writes=[t2b])
    k.op("dve", lambda e: e.tensor_tensor(out=out_ap, in0=t1[:, 0:w], in1=t2[:, 0:w], op=ALU.add),
         reads=[t1b, t2b], writes=[out_b])


def emit_inproj1(P, C, H, win_l, gq, gk, CS, SNs, ropeb, qo_d, ko_d, vo_d, hco_d, outb):
    k = P.k
    for pr in range(3):
        wa, wab = P.load_wa(win_l[:, pr * 256:(pr + 1) * 256])
        for cl in range(2):
            if pr == 2 and cl == 1:
                break
            for j in range(4):
                c0, w = H.tiles[j]
                ps, psb = P.bank("h1", [0, 1, 2, 3])
                emit_inproj_tile(P, H, j, wa, wab, cl, ps, psb)
                qn, qnb = C.tmp()
                emit_qknorm(P, C, ps, psb, w, (gq if pr < 2 else gk)[:, 0:1], qn[:, 0:w], qnb)
                st, stb = C.tmpbf()
                emit_rope(P, C, qn, qnb, CS, SNs, ropeb, c0, w, st[:, 0:w], stb)
                dst = qo_d[:, pr * 2 + cl, c0:c0 + w] if pr < 2 else ko_d[:, c0:c0 + w]
                k.dma("sp", [(dst, st[:, 0:w])], reads=[stb], sembuf=stb, sink=outb)
        if pr == 2:
            for tb in range(16):
                j = tb // 4
                ps, psb = P.bank("o", [4, 5])
                for c in range(8):
                    k.op("pe", lambda e, c=c, tb=tb, ps=ps: e.matmul(
                        ps[:, 0:128], lhsT=H.t[:, c, tb * 128:(tb + 1) * 128], rhs=wa[:, c, 128:256],
                        start=(c == 0), stop=(c == 7)), reads=[wab, H.b[j]], writes=[psb])
                st, stb = C.tmpbf()
                k.op("act", lambda e, ps=ps, st=st: e.activation(out=st[:, 0:128], in_=ps[:, 0:128], func=AF.Copy),
                     reads=[psb], writes=[stb])
                k.dma("sp", [(vo_d[tb], st[:, 0:128])], reads=[stb], sembuf=stb, sink=outb)
    for pr in range(6):
        wa, wab = P.load_wa(win_l[:, 768 + pr * 256:768 + (pr + 1) * 256])
        for cl in range(2):
            ch = pr * 2 + cl
            for j in range(4):
                c0, w = H.tiles[j]
                ps, psb = P.bank("h1", [0, 1, 2, 3])
                emit_inproj_tile(P, H, j, wa, wab, cl, ps, psb)
                st, stb = C.tmp()
                k.op("act", lambda e, ps=ps, st=st, w=w: e.activation(out=st[:, 0:w], in_=ps[:, 0:w], func=AF.Copy),
                     reads=[psb], writes=[stb])
                k.dma("sp", [(hco_d[ch, :, c0:c0 + w], st[:, 0:w])], reads=[stb], sembuf=stb, sink=outb)


NFFT = 16384


def build_LB():
    nc = bass.Bass("TRN2", target_bir_lowering=False)
    k = KB(nc)
    dt_in = lambda name, shape: nc.dram_tensor(name, list(shape), F32, kind="ExternalInput").ap()
    hc_d = dt_in("hc3", [3, 128, S])
    cf_d = dt_in("cf2", [128, 32])
    mlp_d = dt_in("mlpw", [64, 456])
    zemb_d = dt_in("zemb", [33, NFFT])
    win_d = dt_in("win", [128, 128, 128])
    dft_d = dt_in("dftc", [128, 512])
    tw_d = dt_in("tw", [128, 2, 2, 128])
    y_d = nc.dram_tensor("yT", [128, S], F32, kind="ExternalOutput").ap()
    zs = nc.dram_tensor("zs", [128, S], F32, kind="Internal")
    cs = nc.dram_tensor("cs", [128, S], F32, kind="Internal")

    PS = [k.ps(f"ps{i}", [128, 512]) for i in range(8)]
    PSb = [Buf(f"ps{i}") for i in range(8)]
    rr = {}

    def bank(group, banks):
        i = rr.get(group, 0)
        rr[group] = (i + 1) % len(banks)
        return PS[banks[i]], PSb[banks[i]]

    tmp = RotTmp(k, "tf", 8, F32)
    tmpbf = RotTmp(k, "tb", 4, BF16)
    cb = Buf("consts")
    CF = k.sb("CF", [128, 32], F32)
    MLP = k.sb("MLP", [64, 456], F32)
    DFT = k.sb("DFT", [128, 512], BF16)
    TW = k.sb("TW", [128, 2, 2, 128], F32)
    X0 = k.sb("X0", [128, S], F32)
    Z = k.sb("Z", [128, S], F32)
    IN = k.sb("IN", [128, S + 2], F32)
    KC = k.sb("KC", [128, 128, 128], BF16)
    ZC = k.sb("ZC", [64, 128, 128], BF16)
    ACC = k.sb("ACC", [128, 128], F32)
    SM = k.sb("SM", [128, 16], F32)
    ones_f = k.sb("ones_f", [128, 1], F32)
    X0b, Zb, INb, KCb, ZCb, ACCb, SMb = [Buf(n) for n in "X0 Z IN KC ZC ACC SM".split()]

    k.dma("sp", [(CF[:], cf_d), (MLP[:], mlp_d), (TW[:], tw_d)], writes=[cb])
    dftb = Buf("dft")
    k.dma("pool", [(DFT[:], dft_d)], writes=[dftb])
    Fre, Fim, nFim = DFT[:, 0:128], DFT[:, 128:256], DFT[:, 384:512]
    Fcat, FcatI2, FcatI1 = DFT[:, 0:256], DFT[:, 128:384], DFT[:, 256:512]
    k.op("dve", lambda e: e.memset(ones_f[:], 1.0), writes=[SMb])
    k.op("dve", lambda e: e.memset(SM[:, 0:1], math.pi / 2), writes=[SMb])
    k.op("dve", lambda e: e.memset(SM[:, 1:2], EPS), writes=[SMb])
    k.op("dve", lambda e: e.memset(IN[:, 0:1], 0.0), writes=[INb])
    k.op("dve", lambda e: e.memset(IN[:, S + 1:S + 2], 0.0), writes=[INb])

    CH = 2048
    for part in range(3):
        k.dma("sp", [(IN[:, 1:S + 1], hc_d[part])], writes=[INb])
        for cc in range(S // CH):
            c0 = cc * CH
            wc = CF[:, part * 4:part * 4 + 3]
            bc = CF[:, part * 4 + 3:part * 4 + 4]
            for s0 in range(0, CH, 512):
                a0 = c0 + s0
                if part == 0:
                    o, ob_ = X0[:, a0:a0 + 512], X0b
                elif part == 1:
                    o, ob_ = Z[:, a0:a0 + 512], Zb
                else:
                    tt, ttb = tmp()
                    o, ob_ = tt[:, 0:512], ttb
                k.op("dve", lambda e, o=o, a0=a0, wc=wc, bc=bc: e.tensor_scalar(
                    out=o, in0=IN[:, a0 + 1:a0 + 513], scalar1=wc[:, 1:2], scalar2=bc, op0=ALU.mult, op1=ALU.add),
                    reads=[INb, cb], writes=[ob_])
                k.op("dve", lambda e, o=o, a0=a0, wc=wc: e.scalar_tensor_tensor(
                    out=o, in0=IN[:, a0:a0 + 512], scalar=wc[:, 0:1], in1=o, op0=ALU.mult, op1=ALU.add),
                    reads=[INb, cb, ob_], writes=[ob_])
                k.op("dve", lambda e, o=o, a0=a0, wc=wc: e.scalar_tensor_tensor(
                    out=o, in0=IN[:, a0 + 2:a0 + 514], scalar=wc[:, 2:3], in1=o, op0=ALU.mult, op1=ALU.add),
                    reads=[INb, cb, ob_], writes=[ob_])
                if part == 2:
                    k.op("pool", lambda e, o=o, a0=a0: e.tensor_tensor(
                        out=Z[:, a0:a0 + 512], in0=Z[:, a0:a0 + 512], in1=o, op=ALU.mult), reads=[ob_, Zb], writes=[Zb])
    zsb = Buf("zs")
    k.dma("sp", [(zs.ap(), Z[:])], reads=[Zb], writes=[zsb])
    src = bass.AP(tensor=zs, offset=0, ap=[[128, 64], [S, 128], [1, 128]])
    k.dma("pool", [(ZC[:], src)], reads=[zsb], writes=[ZCb])

    W1, W2, W3 = MLP[0:33, 0:64], MLP[:, 64:128], MLP[:, 128:192]
    W4 = MLP[:, 200:456].rearrange("p (d c) -> p d c", d=2)
    k.op("dve", lambda e: e.tensor_scalar(out=SM[0:64, 2:3], in0=MLP[:, 195:196], scalar1=0.25, scalar2=None, op0=ALU.mult),
         reads=[cb], writes=[SMb])
    for i in range(3):
        k.op("dve", lambda e, i=i: e.tensor_tensor(out=SM[0:64, 3 + i:4 + i], in0=MLP[:, 192 + i:193 + i], in1=SM[0:64, 2:3],
                                                   op=ALU.mult), reads=[cb, SMb], writes=[SMb])
    k.op("dve", lambda e: e.memset(ACC[:], 0.0), writes=[ACCb])

    def sin4(ps, psb, li):
        s1, s1b = tmp()
        k.op("act", lambda e: e.activation(out=s1[0:64, :], in_=ps[0:64, :], func=AF.Sin, bias=SM[0:64, 3 + li:4 + li],
                                           scale=SM[0:64, 2:3]), reads=[psb, SMb], writes=[s1b])
        a1, a1b = tmp()
        k.op("act", lambda e: e.activation(out=a1[0:64, :], in_=ps[0:64, :], func=AF.Abs, bias=SM[0:64, 3 + li:4 + li],
                                           scale=SM[0:64, 2:3]), reads=[psb, SMb], writes=[a1b])
        k.op("act", lambda e: e.activation(out=a1[0:64, :], in_=a1[0:64, :], func=AF.Sin, bias=SM[0:64, 0:1], scale=-1.0),
             reads=[a1b, SMb], writes=[a1b])
        k.op("dve", lambda e: e.tensor_tensor(out=a1[0:64, :], in0=a1[0:64, :], in1=s1[0:64, :], op=ALU.mult),
             reads=[a1b, s1b], writes=[a1b])
        k.op("dve", lambda e: e.tensor_tensor(out=s1[0:64, :], in0=s1[0:64, :], in1=s1[0:64, :], op=ALU.mult),
             reads=[s1b], writes=[s1b])
        k.op("dve", lambda e: e.tensor_scalar(out=s1[0:64, :], in0=s1[0:64, :], scalar1=-2.0, scalar2=1.0,
                                              op0=ALU.mult, op1=ALU.add), reads=[s1b], writes=[s1b])
        k.op("dve", lambda e: e.scalar_tensor_tensor(out=a1[0:64, :], in0=a1[0:64, :], scalar=4.0, in1=s1[0:64, :],
                                                     op0=ALU.mult, op1=ALU.mult), reads=[a1b, s1b], writes=[a1b])
        return a1, a1b

    for c in range(32):
        ze, zeb = tmp()
        k.dma("sp", [(ze[0:33, :], zemb_d[:, c * 512:(c + 1) * 512])], writes=[zeb])
        wt, wtb = tmp()
        k.dma("sp", [(wt[:, :].rearrange("p (n c) -> p n c", n=4), win_d[:, c * 4:(c + 1) * 4, :])], writes=[wtb])
        ps, psb = bank("m", [6, 7])
        k.op("pe", lambda e, ps=ps, ze=ze: e.matmul(ps[0:64, :], lhsT=W1, rhs=ze[0:33, :], start=True, stop=True),
             reads=[zeb, cb], writes=[psb])
        h, hb = sin4(ps, psb, 0)
        for li, W in ((1, W2), (2, W3)):
            ps, psb = bank("m", [6, 7])
            k.op("pe", lambda e, ps=ps, h=h, W=W: e.matmul(ps[0:64, :], lhsT=W, rhs=h[0:64, :], start=True, stop=True),
                 reads=[hb, cb], writes=[psb])
            h, hb = sin4(ps, psb, li)
        ps4, ps4b = bank("m", [6, 7])
        for n2l in range(4):
            for d in range(2):
                k.op("pe", lambda e, ps4=ps4, h=h, n2l=n2l, d=d: e.matmul(
                    ps4[d * 64:(d + 1) * 64, n2l * 128:(n2l + 1) * 128],
                    lhsT=h[0:64, n2l * 128 + d * 64:n2l * 128 + (d + 1) * 64], rhs=W4[:, d, :],
                    start=True, stop=True, skip_group_check=True), reads=[hb, cb], writes=[ps4b])
        kc, kcb = tmp()
        k.op("dve", lambda e, kc=kc, ps4=ps4, wt=wt: e.tensor_tensor(out=kc[:, :], in0=ps4[:, :], in1=wt[:, :], op=ALU.mult),
             reads=[ps4b, wtb], writes=[kcb])
        k.op("act", lambda e, kc=kc, c=c: e.activation(
            out=KC[:, :, c * 4:(c + 1) * 4], in_=kc[:, :].rearrange("p (n c) -> p c n", n=4), func=AF.Copy),
            reads=[kcb], writes=[KCb])
        sq, sqb = tmp()
        k.op("pool", lambda e, kc=kc, sq=sq: e.tensor_tensor(out=sq[:, :], in0=kc[:, :], in1=kc[:, :], op=ALU.mult),
             reads=[kcb], writes=[sqb])
        rd, rdb = tmp()
        k.op("dve", lambda e, sq=sq, rd=rd: e.tensor_reduce(
            out=rd[:, 0:128], in_=sq[:, :].rearrange("p (n c) -> p c n", n=4), axis=mybir.AxisListType.X, op=ALU.add),
            reads=[sqb], writes=[rdb])
        k.op("pool", lambda e, rd=rd: e.tensor_tensor(out=ACC[:], in0=ACC[:], in1=rd[:, 0:128], op=ALU.add),
             reads=[rdb, ACCb], writes=[ACCb])
    ps, psb = bank("m", [6, 7])
    k.op("pe", lambda e, ps=ps: e.matmul(ps[:, 0:1], lhsT=ACC[:], rhs=ones_f[:, 0:1], start=True, stop=True),
         reads=[ACCb, SMb], writes=[psb])
    k.op("act", lambda e, ps=ps: e.activation(out=SM[:, 8:9], in_=ps[:, 0:1], func=AF.Sqrt, bias=SM[:, 1:2], scale=1.0),
         reads=[psb, SMb], writes=[SMb])
    k.op("dve", lambda e: e.reciprocal(out=SM[:, 8:9], in_=SM[:, 8:9]), reads=[SMb], writes=[SMb])

    Bre = k.sb("Bre", [128, 4, 128], BF16)
    Bim = k.sb("Bim", [128, 4, 128], BF16)
    Hre = k.sb("Hre", [128, 512], F32)
    Him = k.sb("Him", [128, 512], F32)
    Yre = k.sb("Yre", [128, 4, 128], BF16)
    Yim = k.sb("Yim", [128, 4, 128], BF16)
    Qre = k.sb("Qre", [128, 4, 128], BF16)
    Qim = k.sb("Qim", [128, 4, 128], BF16)
    Bb, Hb, Yb, Qb = Buf("B"), Buf("H"), Buf("Y"), Buf("Q")
    TWre2, TWim2 = TW[:, 0], TW[:, 1]

    def cmul_from_psum(A, Ab, sign, outre, outim, outb, pr):
        A4 = A[:, :].rearrange("p (c r k) -> p c r k", c=2, r=2)
        Are, Aim = A4[:, :, 0, :], A4[:, :, 1, :]
        t1, t1b = tmp()
        t2, t2b = tmp()
        v = lambda t: t[:, 0:256].rearrange("p (c k) -> p c k", c=2)
        k.op("dve", lambda e: e.tensor_tensor(out=v(t1), in0=Are, in1=TWre2, op=ALU.mult), reads=[Ab, cb], writes=[t1b])
        k.op("dve", lambda e: e.tensor_tensor(out=v(t2), in0=Aim, in1=TWim2, op=ALU.mult), reads=[Ab, cb], writes=[t2b])
        k.op("pool", lambda e: e.tensor_tensor(out=outre[:, pr * 2:pr * 2 + 2, :], in0=v(t1), in1=v(t2),
                                               op=(ALU.subtract if sign > 0 else ALU.add)),
             reads=[t1b, t2b], writes=[outb])
        t3, t3b = tmp()
        t4, t4b = tmp()
        k.op("dve", lambda e: e.tensor_tensor(out=v(t3), in0=Aim, in1=TWre2, op=ALU.mult), reads=[Ab, cb], writes=[t3b])
        k.op("dve", lambda e: e.tensor_tensor(out=v(t4), in0=Are, in1=TWim2, op=ALU.mult), reads=[Ab, cb], writes=[t4b])
        k.op("pool", lambda e: e.tensor_tensor(out=outim[:, pr * 2:pr * 2 + 2, :], in0=v(t3), in1=v(t4),
                                               op=(ALU.add if sign > 0 else ALU.subtract)),
             reads=[t3b, t4b], writes=[outb])

    def fft_fwd(src, srcb, krows, ch0, xre, xreb, xim, ximb):
        for pr in range(2):
            A, Ab = bank("A", [0, 1])
            for cl in range(2):
                ch = ch0 + pr * 2 + cl
                k.op("pe", lambda e, A=A, cl=cl, ch=ch: e.matmul(
                    A[:, cl * 256:(cl + 1) * 256], lhsT=src[0:krows, ch, :], rhs=Fcat[0:krows, :],
                    start=True, stop=True, skip_group_check=True), reads=[srcb, dftb], writes=[Ab])
            cmul_from_psum(A, Ab, +1, Bre, Bim, Bb, pr)
        bre = Bre[:, :, :].rearrange("p c k -> p (c k)")
        bim = Bim[:, :, :].rearrange("p c k -> p (c k)")
        k.op("pe", lambda e: e.matmul(xre[:, :], lhsT=Fre, rhs=bre, start=True, stop=False), reads=[Bb, dftb], writes=[xreb])
        k.op("pe", lambda e: e.matmul(xre[:, :], lhsT=nFim, rhs=bim, start=False, stop=True), reads=[Bb, dftb], writes=[xreb])
        k.op("pe", lambda e: e.matmul(xim[:, :], lhsT=Fim, rhs=bre, start=True, stop=False), reads=[Bb, dftb], writes=[ximb])
        k.op("pe", lambda e: e.matmul(xim[:, :], lhsT=Fre, rhs=bim, start=False, stop=True), reads=[Bb, dftb], writes=[ximb])

    csb = Sink("cs")
    for g in range(32):
        ch0 = g * 4
        fft_fwd(KC, KCb, 128, ch0, PS[2], PSb[2], PS[3], PSb[3])
        k.op("act", lambda e: e.activation(out=Hre[:], in_=PS[2][:, :], func=AF.Copy), reads=[PSb[2]], writes=[Hb])
        k.op("act", lambda e: e.activation(out=Him[:], in_=PS[3][:, :], func=AF.Copy), reads=[PSb[3]], writes=[Hb])
        fft_fwd(ZC, ZCb, 64, ch0, PS[4], PSb[4], PS[5], PSb[5])
        t1, t1b = tmp()
        t2, t2b = tmp()
        k.op("dve", lambda e, t1=t1: e.tensor_tensor(out=t1[:, :], in0=PS[4][:, :], in1=Hre[:], op=ALU.mult),
             reads=[PSb[4], Hb], writes=[t1b])
        k.op("dve", lambda e, t2=t2: e.tensor_tensor(out=t2[:, :], in0=PS[5][:, :], in1=Him[:], op=ALU.mult),
             reads=[PSb[5], Hb], writes=[t2b])
        k.op("pool", lambda e, t1=t1, t2=t2: e.tensor_tensor(out=Yre[:, :, :].rearrange("p c k -> p (c k)"), in0=t1[:, :],
                                                             in1=t2[:, :], op=ALU.subtract), reads=[t1b, t2b], writes=[Yb])
        t3, t3b = tmp()
        t4, t4b = tmp()
        k.op("dve", lambda e, t3=t3: e.tensor_tensor(out=t3[:, :], in0=PS[4][:, :], in1=Him[:], op=ALU.mult),
             reads=[PSb[4], Hb], writes=[t3b])
        k.op("dve", lambda e, t4=t4: e.tensor_tensor(out=t4[:, :], in0=PS[5][:, :], in1=Hre[:], op=ALU.mult),
             reads=[PSb[5], Hb], writes=[t4b])
        k.op("pool", lambda e, t3=t3, t4=t4: e.tensor_tensor(out=Yim[:, :, :].rearrange("p c k -> p (c k)"), in0=t3[:, :],
                                                             in1=t4[:, :], op=ALU.add), reads=[t3b, t4b], writes=[Yb])
        for pr in range(2):
            Pk, Pkb = bank("A", [0, 1])
            for cl in range(2):
                c4 = pr * 2 + cl
                k.op("pe", lambda e, Pk=Pk, cl=cl, c4=c4: e.matmul(
                    Pk[:, cl * 256:(cl + 1) * 256], lhsT=Yre[:, c4, :], rhs=FcatI1, start=True, stop=False,
                    skip_group_check=True), reads=[Yb, dftb], writes=[Pkb])
                k.op("pe", lambda e, Pk=Pk, cl=cl, c4=c4: e.matmul(
                    Pk[:, cl * 256:(cl + 1) * 256], lhsT=Yim[:, c4, :], rhs=FcatI2, start=False, stop=True,
                    skip_group_check=True), reads=[Yb, dftb], writes=[Pkb])
            cmul_from_psum(Pk, Pkb, -1, Qre, Qim, Qb, pr)
        yo, yob = PS[6], PSb[6]
        k.op("pe", lambda e: e.matmul(yo[0:64, :], lhsT=DFT[:, 0:64], rhs=Qre[:, :, :].rearrange("p c k -> p (c k)"),
                                      start=True, stop=False), reads=[Qb, dftb], writes=[yob])
        k.op("pe", lambda e: e.matmul(yo[0:64, :], lhsT=DFT[:, 128:192], rhs=Qim[:, :, :].rearrange("p c k -> p (c k)"),
                                      start=False, stop=True), reads=[Qb, dftb], writes=[yob])
        ys, ysb = tmp()
        k.op("act", lambda e, ys=ys: e.activation(out=ys[0:64, :], in_=yo[0:64, :], func=AF.Copy, scale=1.0 / NFFT),
             reads=[yob], writes=[ysb])
        dst = bass.AP(tensor=cs, offset=ch0 * S, ap=[[128, 64], [S, 4], [1, 128]])
        k.dma("sp", [(dst, ys[0:64, :].rearrange("p (c k) -> p c k", c=4))], reads=[ysb], sembuf=ysb, sink=csb)

    k.wait_all("sp", [csb])
    k.dma("sp", [(IN[:, 0:S], cs.ap())], writes=[INb])
    ob = Sink("out")
    for s0 in range(0, S, 512):
        t, tb = tmp()
        k.op("dve", lambda e, t=t, s0=s0: e.tensor_scalar(out=t[:, :], in0=Z[:, s0:s0 + 512], scalar1=CF[:, 12:13],
                                                          scalar2=None, op0=ALU.mult), reads=[Zb, cb], writes=[tb])
        k.op("dve", lambda e, t=t, s0=s0: e.scalar_tensor_tensor(out=t[:, :], in0=IN[:, s0:s0 + 512], scalar=SM[:, 8:9],
                                                                 in1=t[:, :], op0=ALU.mult, op1=ALU.add),
             reads=[INb, SMb, tb], writes=[tb])
        k.op("pool", lambda e, t=t, s0=s0: e.tensor_tensor(out=t[:, :], in0=t[:, :], in1=X0[:, s0:s0 + 512], op=ALU.mult),
             reads=[tb, X0b], writes=[tb])
        k.dma("sp", [(y_d[:, s0:s0 + 512], t[:, :])], reads=[tb], sembuf=tb, sink=ob)
    k.wait_all("sp", [ob])
    k.close()
    return nc


def host_consts_LB(cq):
    f32 = np.float32
    L = S
    m = np.arange(NFFT)
    lag = np.where(m < L, m, NFFT - m)
    lag = np.where(m == L, 0, lag)
    t_all = np.linspace(0.0, 1.0, L, dtype=f32)
    t = t_all[lag]
    w = (f32(2.0 * math.pi / L) * lag.astype(f32)).astype(f32)
    fr = np.linspace(1e-4, 15, 16, dtype=f32)
    zf = np.concatenate([t[:, None], np.cos(fr[None, :] * w[:, None]), -np.sin(fr[None, :] * w[:, None])], axis=-1).astype(f32)
    zemb = np.ascontiguousarray(zf.reshape(128, 128, 33).transpose(2, 1, 0).reshape(33, NFFT))
    dmin, dmax = math.log(1e-2) / 0.3, math.log(1e-2) / 1.5
    deltas = np.abs(np.linspace(dmin, dmax, 512, dtype=f32))[cq * 128:(cq + 1) * 128]
    win = np.exp(-t[:, None] * deltas[None, :]).astype(f32)
    win[L] = 0.0
    win = np.ascontiguousarray(win.reshape(128, 128, 128))
    n = np.arange(128)
    ang = 2.0 * np.pi * np.outer(n, n) / 128.0
    fre, fim = np.cos(ang), -np.sin(ang)
    dftc = np.concatenate([fre, fim, fre, -fim], axis=1).astype(f32)
    ang2 = 2.0 * np.pi * np.outer(n, n) / NFFT
    tw = np.stack([np.cos(ang2), -np.sin(ang2)], 0).astype(f32)
    tw = np.ascontiguousarray(np.broadcast_to(tw[:, None], (2, 2, 128, 128)).transpose(2, 0, 1, 3))
    return zemb, win, dftc, tw


def prep_LB(inp, hco_all):
    maps = []
    cache = {}
    for core in range(NCORE):
        b, cq = core // 4, core % 4
        if cq not in cache:
            cache[cq] = host_consts_LB(cq)
        zemb, win, dftc, tw = cache[cq]
        hc3 = np.zeros((3, 128, S), np.float32)
        for part in range(3):
            for src in range(4):
                hc3[part, :, src * TOK:(src + 1) * TOK] = hco_all[b * 4 + src][part * 4 + cq]
        cf2 = np.zeros((128, 32), np.float32)
        cw = np.asarray(inp["d_conv_w"], np.float32)[0]
        cbias = np.asarray(inp["d_conv_b"], np.float32)[0]
        for part in range(3):
            sl = slice(part * 512 + cq * 128, part * 512 + (cq + 1) * 128)
            cf2[:, part * 4:part * 4 + 3] = cw[:, sl].T
            cf2[:, part * 4 + 3] = cbias[sl]
        cf2[:, 12] = np.asarray(inp["d_skip"], np.float32)[0][cq * 128:(cq + 1) * 128]
        mlp = np.zeros((64, 456), np.float32)
        mlp[0:33, 0:64] = np.asarray(inp["d_f_w1"], np.float32)[0]
        mlp[:, 64:128] = np.asarray(inp["d_f_w2"], np.float32)[0]
        mlp[:, 128:192] = np.asarray(inp["d_f_w3"], np.float32)[0]
        mlp[:, 192] = np.asarray(inp["d_f_b1"], np.float32)[0]
        mlp[:, 193] = np.asarray(inp["d_f_b2"], np.float32)[0]
        mlp[:, 194] = np.asarray(inp["d_f_b3"], np.float32)[0]
        mlp[:, 195] = np.asarray(inp["d_f_freq"], np.float32)[0]
        w4 = np.asarray(inp["d_f_w4"], np.float32)[0]
        mlp[:, 200:328] = w4[:, cq * 128:(cq + 1) * 128]
        mlp[:, 328:456] = w4[:, 512 + cq * 128:512 + (cq + 1) * 128]
        maps.append(dict(hc3=hc3, cf2=cf2, mlpw=mlp, zemb=zemb, win=win, dftc=dftc, tw=tw))
    return maps


def emit_attn1(P, C, QT, QTb, KT, KTb, VA, VAb, MIX, MIXb):
    k = P.k
    for jq in range(4):
        q0 = jq * 512
        for g in range(2):
            gs = slice(g * 64, (g + 1) * 64)
            ds_ = slice((1 - g) * 64, (2 - g) * 64)
            for hh in range(4):
                po, pob = P.bank("o", [4, 5])
                for kb in range(64):
                    ps, psb = P.bank("h1", [0, 1, 2, 3])
                    k.op("pe", lambda e, ps=ps, kb=kb, gs=gs, hh=hh, q0=q0: e.matmul(
                        ps[:, :], lhsT=KT[gs, kb * 128:(kb + 1) * 128], rhs=QT[gs, hh, q0:q0 + 512],
                        start=True, stop=True), reads=[KTb, QTb], writes=[psb])
                    pt, ptb = C.tmpbf()
                    k.op("act", lambda e, ps=ps, pt=pt: e.activation(out=pt[:, :], in_=ps[:, :], func=AF.Exp,
                                                                     scale=HD ** -0.5), reads=[psb], writes=[ptb])
                    k.op("pe", lambda e, po=po, pt=pt, kb=kb, g=g: e.matmul(
                        po[:, :], lhsT=VA[:, kb, g, :], rhs=pt[:, :], start=(kb == 0), stop=(kb == 63)),
                        reads=[VAb, ptb], writes=[pob])
                rd, rdb = C.tmp()
                k.op("dve", lambda e, rd=rd, po=po, ds_=ds_: e.reciprocal(out=rd[ds_, :], in_=po[ds_, :]),
                     reads=[pob], writes=[rdb])
                k.op("dve", lambda e, rd=rd, po=po, gs=gs, ds_=ds_, hh=hh, q0=q0: e.tensor_tensor(
                    out=MIX[gs, hh, q0:q0 + 512], in0=po[gs, :], in1=rd[ds_, :], op=ALU.mult),
                    reads=[pob, rdb], writes=[MIXb[jq]])


def build_LC():
    nc = bass.Bass("TRN2", target_bir_lowering=False)
    k = KB(nc)
    dt_in = lambda name, shape, dt=F32: nc.dram_tensor(name, list(shape), dt, kind="ExternalInput").ap()
    xT_d = dt_in("xT", [128, 8, 2048])
    mod_d = dt_in("modi", [128, 96])
    q_d = dt_in("q", [128, 4, 2048], BF16)
    k_d = dt_in("kk", [128, S], BF16)
    v_d = dt_in("v", [64, 128, 128], BF16)
    y_d = dt_in("yh", [128, 4, 2048])
    w1_d = dt_in("w1", [1024, DFFP])
    w3_d = dt_in("w3", [1024, DFFP])
    w2_d = dt_in("w2", [DFFP, 1024])
    wout_d = dt_in("wout", [1024, 1024])
    xo_d = nc.dram_tensor("xo", [128, 8, 2048], F32, kind="ExternalOutput").ap()

    P = Prog(nc, k)
    C = Common()
    C.tmp = RotTmp(k, "tf", 5, F32)
    C.tmpbf = RotTmp(k, "tb", 4, BF16)
    C.RS = k.sb("RS", [128, 512], F32)
    C.RSb = Buf("RS")
    C.cb = Buf("consts")
    C.modb = Buf("mod")
    C.ones_bf = k.sb("ones_bf", [128, 128], BF16)
    C.eps = k.sb("eps", [128, 1], F32)
    MOD = k.sb("mod", [128, 96], F32)
    XT = k.sb("XT", [128, 8, 2048], F32)
    AR = k.sb("AR", [128, 32768], BF16)
    MIXt = k.sb("MIX", [128, 8, 2048], BF16)
    VA = AR[:, 0:16384].rearrange("p (b g d) -> p b g d", b=64, g=2)
    KT = AR[:, 16384:24576]
    QT = AR[:, 24576:32768].rearrange("p (h t) -> p h t", h=4)
    Ht = AR[:, 0:16384].rearrange("p (c t) -> p c t", c=8)
    U = AR[:, 16384:16384 + 6 * 2048].rearrange("p (f t) -> p f t", f=6)

    tiles4 = [(0, 512), (512, 512), (1024, 512), (1536, 512)]
    XB = [Buf(f"x{j}") for j in range(4)]

    class XAct:
        tiles = tiles4
        b = XB

        def ap(self, c, j):
            return XT[:, c, j * 512:(j + 1) * 512]
    X4 = XAct()
    H4 = Act(Ht, tiles4, "h")
    Ub = [[Buf(f"u{f}_{j}") for j in range(4)] for f in range(6)]
    MIXb = [Buf(f"mix{j}") for j in range(4)]
    QTb, KTb, VAb = Buf("qt"), Buf("kt"), Buf("va")

    k.dma("sp", [(MOD[:], mod_d)], writes=[C.modb])
    k.dma("sp", [(QT, q_d)], writes=[QTb])
    k.dma("sp", [(KT, k_d)], writes=[KTb])
    k.op("dve", lambda e: e.memset(C.ones_bf[:], 1.0), writes=[C.cb])
    k.op("dve", lambda e: e.memset(C.eps[:], EPS), writes=[C.cb])
    k.op("dve", lambda e: e.memset(VA[:, :, 0, 64:128], 1.0), writes=[VAb])
    k.op("dve", lambda e: e.memset(VA[:, :, 1, 0:64], 1.0), writes=[VAb])
    vsrc = v_d.rearrange("b p d -> p b d")
    k.dma("sp", [(VA[:, :, 0, 0:64], vsrc[:, :, 0:64]), (VA[:, :, 1, 64:128], vsrc[:, :, 64:128])], writes=[VAb])
    for j in range(4):
        k.dma("sp", [(XT[:, :, j * 512:(j + 1) * 512], xT_d[:, :, j * 512:(j + 1) * 512])], writes=[XB[j]])
    for j in range(4):
        k.dma("pool", [(MIXt[:, 4:8, j * 512:(j + 1) * 512], y_d[:, :, j * 512:(j + 1) * 512])], writes=[MIXb[j]])

    def mcol(i, kind):
        return MOD[:, (i * 3 + kind) * 8:(i * 3 + kind + 1) * 8]

    emit_attn1(P, C, QT, QTb, KT, KTb, VA, VAb, MIXt, MIXb)
    emit_outproj(P, C, X4, lambda fc, c0, w: MIXt[:, fc, c0:c0 + w], MIXb, wout_d, mcol(1, 2))
    k.barrier()
    emit_adaln(P, X4, H4, MOD[:, 72 + 16:72 + 24], mcol(2, 0), C)
    emit_ffn(P, X4, H4, U, Ub, w1_d, w3_d, w2_d, mcol(2, 2), C)
    ob = Sink("out")
    for j in range(4):
        k.dma("sp", [(xo_d[:, :, j * 512:(j + 1) * 512], XT[:, :, j * 512:(j + 1) * 512])], reads=[XB[j]],
              sembuf=XB[j], sink=ob)
    k.wait_all("sp", [ob])
    k.close()
    return nc


def prep_LC(inp, la_res, lb_res, shared):
    maps = []
    for core in range(NCORE):
        b, qd = core // 4, core % 4
        kk = np.concatenate([la_res[b * 4 + s]["ko"] for s in range(4)], axis=1)
        v = np.concatenate([la_res[b * 4 + s]["vo"] for s in range(4)], axis=0)
        yh = np.stack([lb_res[b * 4 + cq]["yT"][:, qd * TOK:(qd + 1) * TOK] for cq in range(4)], axis=1)
        maps.append(dict(xT=la_res[core]["xo"], modi=la_res[core]["modo"], q=la_res[core]["qo"],
                         kk=np.ascontiguousarray(kk), v=np.ascontiguousarray(v), yh=np.ascontiguousarray(yh),
                         w1=shared["w1"][1, 1], w3=shared["w3"][1, 1], w2=shared["w2"][1, 1], wout=shared["wout"][1]))
    return maps


_CACHE = {}


FUSED = True


def kernel(**inputs):
    inp = {kk: np.asarray(v) for kk, v in inputs.items()}
    cores = list(range(NCORE))
    if FUSED:
        if "fused" not in _CACHE:
            _CACHE["fused"] = build_fused()
        maps = prep_fused(inp)
        res = run_bass_kernel_spmd(_CACHE["fused"], maps, core_ids=cores).results
        out = np.zeros((2, S, D), np.float32)
        for core in cores:
            b, qd = core // 4, core % 4
            xo = np.asarray(res[core]["xo"], np.float32)
            out[b, qd * TOK:(qd + 1) * TOK] = xo.transpose(2, 1, 0).reshape(TOK, D)
        return out
    if "la" not in _CACHE:
        _CACHE["la"] = build_LA(True)
        _CACHE["lb"] = build_LB()
        _CACHE["lc"] = build_LC()
    la_maps = prep_LA(inp)
    la = run_bass_kernel_spmd(_CACHE["la"], la_maps, core_ids=cores).results
    lb_maps = prep_LB(inp, [la[c]["hco"] for c in cores])
    lb = run_bass_kernel_spmd(_CACHE["lb"], lb_maps, core_ids=cores).results
    lc_maps = prep_LC(inp, la, lb, la_maps[0])
    lc = run_bass_kernel_spmd(_CACHE["lc"], lc_maps, core_ids=cores).results
    out = np.zeros((2, S, D), np.float32)
    for core in cores:
        b, qd = core // 4, core % 4
        xo = np.asarray(lc[core]["xo"], np.float32)
        out[b, qd * TOK:(qd + 1) * TOK] = xo.transpose(2, 1, 0).reshape(TOK, D)
    return out


U32 = mybir.dt.uint32


def kb_gather(k, dst_ap, src_dram_ap, idx_ap, reads=(), writes=()):
    sbf = writes[0]
    if sbf.dsem is None:
        sbf.dsem = {}
        sbf.dcnt = {}
    if True not in sbf.dsem:
        sbf.dsem[True] = ("d", id(sbf), True)
        sbf.dcnt[True] = 0
        k.sems[sbf.dsem[True]] = k._newsem(f"d_{k.nsem}")
    k._wait("pool", k._deps(reads, writes))
    k.nc.gpsimd.indirect_dma_start(out=dst_ap, out_offset=None, in_=src_dram_ap,
                                   in_offset=bass.IndirectOffsetOnAxis(ap=idx_ap, axis=0)
                                   ).then_inc(k.sems[sbf.dsem[True]], 16)
    sbf.dcnt[True] += 16
    tok = (sbf.dsem[True], sbf.dcnt[True])
    k._commit(tok, reads, writes)
    return tok


class CC:
    n = 0

    def __init__(self, k, groups, fake):
        self.k, self.groups, self.fake = k, groups, fake
        self.pool_sems = [] if fake else [k._newsem(f"cc{i}") for i in range(16)]
        self.alltoks = {}

    def allgather(self, src_t, dst_t, reads, dstb, sinks=(), sl=None):
        k = self.k
        sap = src_t.ap() if sl is None else src_t.ap()[sl]
        dap = dst_t.ap() if sl is None else dst_t.ap()[sl]
        rows = sap.shape[0]
        k.wait_all("sp" if self.fake else "pool", list(sinks))
        if self.fake:
            pairs = [(dap[r * rows:(r + 1) * rows, :], sap) for r in range(4)]
            k.dma("sp", pairs, reads=reads, writes=[dstb])
            return
        CC.n += 1
        key = ("cc", CC.n)
        k.sems[key] = self.pool_sems.pop(0)
        k._wait("pool", k._deps(reads, [dstb]))
        k.nc.gpsimd.collective_compute("AllGather", ALU.bypass, replica_groups=self.groups,
                                       ins=[sap.opt()], outs=[dap.opt()]).then_inc(k.sems[key])
        k._commit((key, 1), reads, [dstb])
        self.alltoks.setdefault(id(dstb), Sink("cc")).toks[key] = 1

    def sink(self, dstb):
        return self.alltoks.get(id(dstb), Sink("none"))


def emit_filter(P, C, F, kc_s, kcsb):
    k = P.k
    tmp, MLP, SM = C.tmp, F.MLP, F.SM
    cb = C.cb
    W1, W2, W3 = MLP[0:33, 0:64], MLP[:, 64:128], MLP[:, 128:192]
    W4 = MLP[:, 200:456].rearrange("p (d c) -> p d c", d=2)
    SMb, ACC, ACCb = F.SMb, F.ACC, F.ACCb
    k.op("dve", lambda e: e.memset(SM[:, 0:1], math.pi / 2), writes=[SMb])
    k.op("dve", lambda e: e.memset(SM[:, 1:2], EPS), writes=[SMb])
    k.op("dve", lambda e: e.tensor_scalar(out=SM[0:64, 2:3], in0=MLP[:, 195:196], scalar1=0.25, scalar2=None, op0=ALU.mult),
         reads=[cb], writes=[SMb])
    for i in range(3):
        k.op("dve", lambda e, i=i: e.tensor_tensor(out=SM[0:64, 3 + i:4 + i], in0=MLP[:, 192 + i:193 + i], in1=SM[0:64, 2:3],
                                                   op=ALU.mult), reads=[cb, SMb], writes=[SMb])
    k.op("dve", lambda e: e.memset(ACC[:], 0.0), writes=[ACCb])

    def sin4(ps, psb, li):
        s1, s1b = tmp()
        k.op("act", lambda e: e.activation(out=s1[0:64, :], in_=ps[0:64, :], func=AF.Sin, bias=SM[0:64, 3 + li:4 + li],
                                           scale=SM[0:64, 2:3]), reads=[psb, SMb], writes=[s1b])
        a1, a1b = tmp()
        k.op("act", lambda e: e.activation(out=a1[0:64, :], in_=ps[0:64, :], func=AF.Abs, bias=SM[0:64, 3 + li:4 + li],
                                           scale=SM[0:64, 2:3]), reads=[psb, SMb], writes=[a1b])
        k.op("act", lambda e: e.activation(out=a1[0:64, :], in_=a1[0:64, :], func=AF.Sin, bias=SM[0:64, 0:1], scale=-1.0),
             reads=[a1b, SMb], writes=[a1b])
        k.op("dve", lambda e: e.tensor_tensor(out=a1[0:64, :], in0=a1[0:64, :], in1=s1[0:64, :], op=ALU.mult),
             reads=[a1b, s1b], writes=[a1b])
        k.op("dve", lambda e: e.tensor_tensor(out=s1[0:64, :], in0=s1[0:64, :], in1=s1[0:64, :], op=ALU.mult),
             reads=[s1b], writes=[s1b])
        k.op("dve", lambda e: e.tensor_scalar(out=s1[0:64, :], in0=s1[0:64, :], scalar1=-2.0, scalar2=1.0,
                                              op0=ALU.mult, op1=ALU.add), reads=[s1b], writes=[s1b])
        k.op("dve", lambda e: e.scalar_tensor_tensor(out=a1[0:64, :], in0=a1[0:64, :], scalar=4.0, in1=s1[0:64, :],
                                                     op0=ALU.mult, op1=ALU.mult), reads=[a1b, s1b], writes=[a1b])
        return a1, a1b

    for c in range(32):
        ze, zeb = tmp()
        k.dma("sp", [(ze[0:33, :], F.zemb_d[:, c * 512:(c + 1) * 512])], writes=[zeb])
        wt, wtb = tmp()
        k.dma("sp", [(wt[:, :].rearrange("p (n c) -> p n c", n=4), F.win_d[:, c * 4:(c + 1) * 4, :])], writes=[wtb])
        ps, psb = P.bank("m", [6, 7])
        k.op("pe", lambda e, ps=ps, ze=ze: e.matmul(ps[0:64, :], lhsT=W1, rhs=ze[0:33, :], start=True, stop=True),
             reads=[zeb, cb], writes=[psb])
        h, hb = sin4(ps, psb, 0)
        for li, W in ((1, W2), (2, W3)):
            ps, psb = P.bank("m", [6, 7])
            k.op("pe", lambda e, ps=ps, h=h, W=W: e.matmul(ps[0:64, :], lhsT=W, rhs=h[0:64, :], start=True, stop=True),
                 reads=[hb, cb], writes=[psb])
            h, hb = sin4(ps, psb, li)
        ps4, ps4b = P.bank("m", [6, 7])
        for n2l in range(4):
            for d in range(2):
                k.op("pe", lambda e, ps4=ps4, h=h, n2l=n2l, d=d: e.matmul(
                    ps4[d * 64:(d + 1) * 64, n2l * 128:(n2l + 1) * 128],
                    lhsT=h[0:64, n2l * 128 + d * 64:n2l * 128 + (d + 1) * 64], rhs=W4[:, d, :],
                    start=True, stop=True, skip_group_check=True), reads=[hb, cb], writes=[ps4b])
        kc, kcb = tmp()
        k.op("dve", lambda e, kc=kc, ps4=ps4, wt=wt: e.tensor_tensor(out=kc[:, :], in0=ps4[:, :], in1=wt[:, :], op=ALU.mult),
             reads=[ps4b, wtb], writes=[kcb])
        st, stb = C.tmpbf()
        k.op("act", lambda e, kc=kc, st=st: e.activation(out=st[:, :], in_=kc[:, :], func=AF.Copy), reads=[kcb], writes=[stb])
        k.dma("sp", [(kc_s.ap()[:, c * 4:(c + 1) * 4, :], st[:, :].rearrange("p (n c) -> p n c", n=4))], reads=[stb],
              sembuf=stb, sink=kcsb)
        sq, sqb = tmp()
        k.op("dve", lambda e, kc=kc, sq=sq: e.tensor_tensor(out=sq[:, :], in0=kc[:, :], in1=kc[:, :], op=ALU.mult),
             reads=[kcb], writes=[sqb])
        rd, rdb = tmp()
        k.op("dve", lambda e, sq=sq, rd=rd: e.tensor_reduce(
            out=rd[:, 0:128], in_=sq[:, :].rearrange("p (n c) -> p c n", n=4), axis=mybir.AxisListType.X, op=ALU.add),
            reads=[sqb], writes=[rdb])
        k.op("dve", lambda e, rd=rd: e.tensor_tensor(out=ACC[:], in0=ACC[:], in1=rd[:, 0:128], op=ALU.add),
             reads=[rdb, ACCb], writes=[ACCb])
    ps, psb = P.bank("m", [6, 7])
    k.op("pe", lambda e, ps=ps: e.matmul(ps[:, 0:128], lhsT=C.ones_f[:, :], rhs=ACC[:], start=True, stop=True),
         reads=[ACCb, cb], writes=[psb])
    k.op("act", lambda e, ps=ps: e.activation(out=F.RSrow[:], in_=ps[:, 0:128], func=AF.Sqrt, bias=SM[:, 1:2], scale=1.0),
         reads=[psb, SMb], writes=[F.RSb])
    k.op("dve", lambda e: e.reciprocal(out=F.RSrow[:], in_=F.RSrow[:]), reads=[F.RSb], writes=[F.RSb])


def emit_fftconv(P, C, F, KC, KCb, zs2, zs2b, c_src, csink, V):
    k = P.k
    tmp = C.tmp
    DFT, TW, dftb, cb = F.DFT, F.TW, F.dftb, C.cb
    Fre, Fim, nFim = DFT[:, 0:128], DFT[:, 128:256], DFT[:, 384:512]
    Fcat, FcatI2, FcatI1 = DFT[:, 0:256], DFT[:, 128:384], DFT[:, 256:512]
    TWre2, TWim2 = TW[:, 0], TW[:, 1]
    Bre, Bim, Yre, Yim, Qre, Qim, Hre, Him = V.Bre, V.Bim, V.Yre, V.Yim, V.Qre, V.Qim, V.Hre, V.Him
    Bb, Hb, Yb, Qb = Buf("B"), Buf("H"), Buf("Y"), Buf("Q")
    PS, PSb = P.PS, P.PSb

    def cmul_from_psum(A, Ab, sign, outre, outim, outb, pr):
        A4 = A[:, :].rearrange("p (c r k) -> p c r k", c=2, r=2)
        Are, Aim = A4[:, :, 0, :], A4[:, :, 1, :]
        v = lambda t: t[:, 0:256].rearrange("p (c k) -> p c k", c=2)
        t1, t1b = tmp()
        t2, t2b = tmp()
        k.op("dve", lambda e: e.tensor_tensor(out=v(t1), in0=Are, in1=TWre2, op=ALU.mult), reads=[Ab, cb], writes=[t1b])
        k.op("dve", lambda e: e.tensor_tensor(out=v(t2), in0=Aim, in1=TWim2, op=ALU.mult), reads=[Ab, cb], writes=[t2b])
        k.op("pool", lambda e: e.tensor_tensor(out=outre[:, pr * 2:pr * 2 + 2, :], in0=v(t1), in1=v(t2),
                                               op=(ALU.subtract if sign > 0 else ALU.add)),
             reads=[t1b, t2b], writes=[outb])
        t3, t3b = tmp()
        t4, t4b = tmp()
        k.op("dve", lambda e: e.tensor_tensor(out=v(t3), in0=Aim, in1=TWre2, op=ALU.mult), reads=[Ab, cb], writes=[t3b])
        k.op("dve", lambda e: e.tensor_tensor(out=v(t4), in0=Are, in1=TWim2, op=ALU.mult), reads=[Ab, cb], writes=[t4b])
        k.op("pool", lambda e: e.tensor_tensor(out=outim[:, pr * 2:pr * 2 + 2, :], in0=v(t3), in1=v(t4),
                                               op=(ALU.add if sign > 0 else ALU.subtract)),
             reads=[t3b, t4b], writes=[outb])

    def fft_fwd(lhs_of, srcb, krows, xre, xreb, xim, ximb):
        for pr in range(2):
            A, Ab = P.bank("A", [0, 1])
            for cl in range(2):
                c4 = pr * 2 + cl
                k.op("pe", lambda e, A=A, cl=cl, c4=c4: e.matmul(
                    A[:, cl * 256:(cl + 1) * 256], lhsT=lhs_of(c4), rhs=Fcat[0:krows, :],
                    start=True, stop=True, skip_group_check=True), reads=[srcb, dftb], writes=[Ab])
            cmul_from_psum(A, Ab, +1, Bre, Bim, Bb, pr)
        bre = Bre.rearrange("p c k -> p (c k)")
        bim = Bim.rearrange("p c k -> p (c k)")
        k.op("pe", lambda e: e.matmul(xre[:, :], lhsT=Fre, rhs=bre, start=True, stop=False), reads=[Bb, dftb], writes=[xreb])
        k.op("pe", lambda e: e.matmul(xre[:, :], lhsT=nFim, rhs=bim, start=False, stop=True), reads=[Bb, dftb], writes=[xreb])
        k.op("pe", lambda e: e.matmul(xim[:, :], lhsT=Fim, rhs=bre, start=True, stop=False), reads=[Bb, dftb], writes=[ximb])
        k.op("pe", lambda e: e.matmul(xim[:, :], lhsT=Fre, rhs=bim, start=False, stop=True), reads=[Bb, dftb], writes=[ximb])

    for g in range(32):
        ch0 = g * 4
        ZC, ZCb = V.ZC[g % 2], V.ZCb[g % 2]
        src = bass.AP(tensor=zs2, offset=ch0 * S, ap=[[128, 64], [S, 4], [1, 128]])
        k.dma("sp", [(ZC[0:64], src)], reads=[zs2b], writes=[ZCb])
        fft_fwd(lambda c4, ch0=ch0: KC[:, :, ch0 + c4], KCb, 128, PS[2], PSb[2], PS[3], PSb[3])
        rsb_ap = F.RSrow[:, ch0:ch0 + 4].unsqueeze(2).broadcast_to([128, 4, 128])
        k.op("dve", lambda e, rsb_ap=rsb_ap: e.tensor_tensor(out=Hre.rearrange("p (c k) -> p c k", c=4),
                                                              in0=PS[2][:, :].rearrange("p (c k) -> p c k", c=4),
                                                              in1=rsb_ap, op=ALU.mult), reads=[PSb[2], F.RSb], writes=[Hb])
        k.op("dve", lambda e, rsb_ap=rsb_ap: e.tensor_tensor(out=Him.rearrange("p (c k) -> p c k", c=4),
                                                              in0=PS[3][:, :].rearrange("p (c k) -> p c k", c=4),
                                                              in1=rsb_ap, op=ALU.mult), reads=[PSb[3], F.RSb], writes=[Hb])
        fft_fwd(lambda c4, ZC=ZC: ZC[0:64, c4, :], ZCb, 64, PS[4], PSb[4], PS[5], PSb[5])
        t1, t1b = tmp()
        t2, t2b = tmp()
        k.op("dve", lambda e, t1=t1: e.tensor_tensor(out=t1[:, :], in0=PS[4][:, :], in1=Hre, op=ALU.mult),
             reads=[PSb[4], Hb], writes=[t1b])
        k.op("dve", lambda e, t2=t2: e.tensor_tensor(out=t2[:, :], in0=PS[5][:, :], in1=Him, op=ALU.mult),
             reads=[PSb[5], Hb], writes=[t2b])
        k.op("pool", lambda e, t1=t1, t2=t2: e.tensor_tensor(out=Yre.rearrange("p c k -> p (c k)"), in0=t1[:, :],
                                                             in1=t2[:, :], op=ALU.subtract), reads=[t1b, t2b], writes=[Yb])
        t3, t3b = tmp()
        t4, t4b = tmp()
        k.op("dve", lambda e, t3=t3: e.tensor_tensor(out=t3[:, :], in0=PS[4][:, :], in1=Him, op=ALU.mult),
             reads=[PSb[4], Hb], writes=[t3b])
        k.op("dve", lambda e, t4=t4: e.tensor_tensor(out=t4[:, :], in0=PS[5][:, :], in1=Hre, op=ALU.mult),
             reads=[PSb[5], Hb], writes=[t4b])
        k.op("pool", lambda e, t3=t3, t4=t4: e.tensor_tensor(out=Yim.rearrange("p c k -> p (c k)"), in0=t3[:, :],
                                                             in1=t4[:, :], op=ALU.add), reads=[t3b, t4b], writes=[Yb])
        for pr in range(2):
            Pk, Pkb = P.bank("A", [0, 1])
            for cl in range(2):
                c4 = pr * 2 + cl
                k.op("pe", lambda e, Pk=Pk, cl=cl, c4=c4: e.matmul(
                    Pk[:, cl * 256:(cl + 1) * 256], lhsT=Yre[:, c4, :], rhs=FcatI1, start=True, stop=False,
                    skip_group_check=True), reads=[Yb, dftb], writes=[Pkb])
                k.op("pe", lambda e, Pk=Pk, cl=cl, c4=c4: e.matmul(
                    Pk[:, cl * 256:(cl + 1) * 256], lhsT=Yim[:, c4, :], rhs=FcatI2, start=False, stop=True,
                    skip_group_check=True), reads=[Yb, dftb], writes=[Pkb])
            cmul_from_psum(Pk, Pkb, -1, Qre, Qim, Qb, pr)
        yo, yob = PS[6], PSb[6]
        k.op("pe", lambda e: e.matmul(yo[0:64, :], lhsT=DFT[:, 0:64], rhs=Qre.rearrange("p c k -> p (c k)"),
                                      start=True, stop=False), reads=[Qb, dftb], writes=[yob])
        k.op("pe", lambda e: e.matmul(yo[0:64, :], lhsT=DFT[:, 128:192], rhs=Qim.rearrange("p c k -> p (c k)"),
                                      start=False, stop=True), reads=[Qb, dftb], writes=[yob])
        ys, ysb = tmp()
        k.op("act", lambda e, ys=ys: e.activation(out=ys[0:64, :], in_=yo[0:64, :], func=AF.Copy, scale=1.0 / NFFT),
             reads=[yob], writes=[ysb])
        dst = bass.AP(tensor=c_src, offset=ch0 * S, ap=[[128, 64], [S, 4], [1, 128]])
        k.dma("sp", [(dst, ys[0:64, :].rearrange("p (c k) -> p c k", c=4))], reads=[ysb], sembuf=ysb, sink=csink)


def emit_attn1f(P, C, q_s, KT, KTb, VB, VBb, QS, QSb, MIX, MIXb, SK=2):
    k = P.k
    its = [(jq, g, hh, kb) for jq in range(4) for g in range(2) for hh in range(4) for kb in range(64)]
    st = {}
    cur = {}

    def stage_a(i):
        jq, g, hh, kb = its[i]
        q0 = jq * 512
        QT, QTb = QS[jq % 2], QSb[jq % 2]
        if (g, hh, kb) == (0, 0, 0):
            k.dma("sp", [(QT, q_s.ap()[:, :, q0:q0 + 512])], writes=[QTb])
        gs = slice(g * 64, (g + 1) * 64)
        ps, psb = P.bank("h1", [0, 1, 2, 3])
        k.op("pe", lambda e: e.matmul(ps[:, :], lhsT=KT[gs, kb * 128:(kb + 1) * 128], rhs=QT[gs, hh, :],
                                      start=True, stop=True), reads=[KTb, QTb], writes=[psb])
        pt, ptb = C.tmpbf()
        k.op("act", lambda e: e.activation(out=pt[:, :], in_=ps[:, :], func=AF.Exp, scale=HD ** -0.5),
             reads=[psb], writes=[ptb])
        st[i] = (pt, ptb)

    def stage_b(i):
        jq, g, hh, kb = its[i]
        q0 = jq * 512
        gs = slice(g * 64, (g + 1) * 64)
        ds_ = slice((1 - g) * 64, (2 - g) * 64)
        if kb == 0:
            cur["po"] = P.bank("o", [4, 5])
        po, pob = cur["po"]
        pt, ptb = st.pop(i)
        k.op("pe", lambda e: e.matmul(po[:, :], lhsT=VB[:, kb, g * 64:g * 64 + 128], rhs=pt[:, :],
                                      start=(kb == 0), stop=(kb == 63)), reads=[VBb, ptb], writes=[pob])
        if kb == 63:
            rd, rdb = C.tmp()
            k.op("dve", lambda e: e.reciprocal(out=rd[ds_, :], in_=po[ds_, :]), reads=[pob], writes=[rdb])
            k.op("dve", lambda e: e.tensor_tensor(out=MIX[gs, hh, q0:q0 + 512], in0=po[gs, :], in1=rd[ds_, :],
                                                  op=ALU.mult), reads=[pob, rdb], writes=[MIXb[jq]])

    n = len(its)
    for t in range(n + SK):
        if t < n:
            stage_a(t)
        if t - SK >= 0:
            stage_b(t - SK)


def build_fused(fake_ag=False, ncore=8, stop_after=None):
    nc = bass.Bass("TRN2", target_bir_lowering=False)
    k = KB(nc)
    groups = [[0, 1, 2, 3], [4, 5, 6, 7]] if ncore == 8 else [[0, 1, 2, 3]]
    cc = CC(k, groups, fake_ag)
    dt_in = lambda name, shape, dt=F32: nc.dram_tensor(name, list(shape), dt, kind="ExternalInput").ap()
    dram = lambda name, shape, dt=F32: nc.dram_tensor(name, list(shape), dt, kind="Internal")
    xT_d = dt_in("xT", [128, 8, 2048])
    xH_d = dt_in("xH", [128, 8, 256])
    cf_d = dt_in("cf", [128, 400])
    oh_d = dt_in("oh", [32, 512])
    vm_d = dt_in("vm", [128, 512])
    rope_d = dt_in("rope", [128, 2, 2048])
    idx_d = dt_in("idx", [128, 24], U32)
    mlp_d = dt_in("mlpw", [64, 456])
    zemb_d = dt_in("zemb", [33, NFFT])
    win_d = dt_in("win", [128, 128, 128])
    dft_d = dt_in("dftc", [128, 512])
    tw_d = dt_in("tw", [128, 2, 2, 128])
    wmod_d = dt_in("wmod", [2, 1024, 9216])
    w1_d = dt_in("w1", [2, 2, 1024, DFFP])
    w3_d = dt_in("w3", [2, 2, 1024, DFFP])
    w2_d = dt_in("w2", [2, 2, DFFP, 1024])
    win_w = dt_in("win_w", [2, 1024, INC])
    wout_d = dt_in("wout", [2, 1024, 1024])
    xo_d = nc.dram_tensor("xo", [128, 8, 2048], F32, kind="ExternalOutput").ap()
    scr = dram("scr", [128, 8 * 512])
    q_s = dram("q_s", [128, 4, 2048], BF16)
    k_src, k_g = dram("k_src", [128, 2048], BF16), dram("k_g", [512, 2048], BF16)
    v_src, v_g = dram("v_src", [2048, 128], BF16), dram("v_g", [8192, 128], BF16)
    hb_src, hb_g = dram("hb_src", [128, 24]), dram("hb_g", [512, 24])
    z_src, z_g = dram("z_src", [4, 128, 2048], BF16), dram("z_g", [4, 512, 2048], BF16)
    zf_s = dram("zf_s", [128, 4, 2048])
    x0_s = dram("x0_s", [128, 4, 2048])
    zs2 = dram("zs2", [128, S], BF16)
    kc_s = dram("kc_s", [128, 128, 128], BF16)
    c_src, c_g = dram("c_src", [8, 16, S]), dram("c_g", [8, 64, S])

    P = Prog(nc, k)
    C = Common()
    C.tmp = RotTmp(k, "tf", 8, F32)
    C.tmpbf = RotTmp(k, "tb", 4, BF16)
    C.RS = k.sb("RS", [128, 512], F32)
    C.RSb = Buf("RS")
    C.cb = Buf("consts")
    C.modb = Buf("mod")
    CF = k.sb("CF", [128, 400], F32)
    IDX = k.sb("IDX", [128, 24], U32)
    C.ones_bf = k.sb("ones_bf", [128, 128], BF16)
    C.bd_bf = k.sb("bd_bf", [128, 128], BF16)
    C.ones_f = k.sb("ones_f", [128, 128], F32)
    C.eps = k.sb("eps", [128, 1], F32)
    condbf = k.sb("condbf", [128, 8], BF16)
    MODV = k.sb("modv", [128, 2, 72], F32)
    AV = k.sb("av", [128, 2, 3, 8], F32)
    EXS = k.sb("exs", [128, 8], F32)
    F = Common()
    F.MLP = k.sb("MLP", [64, 456], F32)
    F.DFT = k.sb("DFT", [128, 512], BF16)
    F.TW = k.sb("TW", [128, 2, 2, 128], F32)
    F.ACC = k.sb("ACC", [128, 128], F32)
    F.SM = k.sb("SM", [128, 16], F32)
    F.RSrow = k.sb("RSrow", [128, 128], F32)
    F.SMb, F.ACCb, F.RSb, F.dftb = Buf("SM"), Buf("ACC"), Buf("RSr"), Buf("dft")
    F.zemb_d, F.win_d = zemb_d, win_d
    HBS = k.sb("HBS", [128, 12, 2], F32)
    HG = k.sb("HG", [128, 4, 12, 2], F32)
    HLR = k.sb("HLR", [128, 2, 12], F32)
    XT = k.sb("XT", [128, 8, 2048], F32)
    AR = k.sb("AR", [128, 41984], BF16)

    cT = CF[:, 0:8]
    flags = CF[:, 8:10]
    normg = CF[:, 10:58].rearrange("p (l i c) -> p l i c", l=2, i=3)
    bmod = CF[:, 58:202].rearrange("p (l m) -> p l m", l=2)
    convw = CF[:, 202:214].rearrange("p (c t) -> p c t", c=4)
    convb = CF[:, 214:218]
    gq, gk = CF[:, 218:220], CF[:, 220:222]
    sink = CF[:, 222:230]
    relrep = CF[:, 232:240]
    gq1, gk1 = CF[:, 240:241], CF[:, 241:242]
    dcw = CF[:, 244:280].rearrange("p (c t) -> p c t", c=12)
    dcb = CF[:, 280:292]
    skipv = CF[:, 292:296]
    selL, selR = CF[:, 296:300], CF[:, 300:304]

    Ht = AR[:, 0:18432].rearrange("p (c t) -> p c t", c=8)
    UQ = AR[:, 18432:33792]
    MXf = AR[:, 33792:41984].bitcast(F32)
    XH = MXf[:, 0:2048].rearrange("p (c t) -> p c t", c=8)
    MIXHI0 = AR[:, 33792:41984].rearrange("p (c t) -> p c t", c=4)
    U = UQ[:, 0:6 * 2304].rearrange("p (f t) -> p f t", f=6)
    QT0 = UQ[:, 0:8192].rearrange("p (h t) -> p h t", h=4)
    KT0 = UQ[:, 8192:8192 + 2304]
    VA0 = UQ[:, 10496:10496 + 4608].rearrange("p (b g d) -> p b g d", b=18, g=2)
    S_t = UQ[:, 10496:10496 + 4100].bitcast(F32)
    C.E8 = UQ[:, 0:8192].bitcast(F32).rearrange("p (h m) -> p h m", h=8)
    EXPB = P.WBf[:, :, :].rearrange("p a (b q) -> p (a b) q", q=128).rearrange("p (h b) q -> p h b q", h=8)
    OH = AR[0:32, 0:1024].bitcast(F32)
    VM = AR[:, 1024:2048].bitcast(F32)

    tiles5 = [(0, 512), (512, 512), (1024, 512), (1536, 512), (2048, 256)]
    tiles4 = tiles5[:4]
    XB = [Buf(f"x{j}") for j in range(5)]

    class XAct:
        def __init__(self, tiles):
            self.tiles = tiles
            self.b = XB[:len(tiles)]

        def ap(self, c, j):
            if j < 4:
                return XT[:, c, j * 512:(j + 1) * 512]
            return XH[:, c, :]
    X5, X4 = XAct(tiles5), XAct(tiles4)
    H5 = Act(Ht, tiles5, "h")
    H4 = Act(Ht, tiles4, "h")
    H4.b = H5.b[:4]
    Ub = [[Buf(f"u{f}_{j}") for j in range(5)] for f in range(6)]

    k.dma("sp", [(CF[:], cf_d), (F.MLP[:], mlp_d), (F.TW[:], tw_d), (IDX[:], idx_d)], writes=[C.cb])
    k.dma("pool", [(F.DFT[:], dft_d)], writes=[F.dftb])
    for j in range(4):
        k.dma("sp", [(XT[:, :, j * 512:(j + 1) * 512], xT_d[:, :, j * 512:(j + 1) * 512])], writes=[XB[j]])
    k.dma("sp", [(XH, xH_d)], writes=[XB[4]])
    cst = Buf("cst")
    k.op("dve", lambda e: e.memset(C.ones_bf[:], 1.0), writes=[cst])
    k.op("dve", lambda e: e.memset(C.ones_f[:], 1.0), writes=[cst])
    k.op("dve", lambda e: e.memset(C.eps[:], EPS), writes=[cst])
    k.op("dve", lambda e: e.memset(C.bd_bf[:], 0.0), writes=[cst])
    k.op("dve", lambda e: e.memset(C.bd_bf[0:64, 0:64], 1.0), writes=[cst])
    k.op("dve", lambda e: e.memset(C.bd_bf[64:128, 64:128], 1.0), writes=[cst])
    condb = Buf("cond")
    k.op("act", lambda e: e.activation(out=condbf[:], in_=cT, func=AF.Silu), reads=[C.cb], writes=[condb])
    k.op("act", lambda e: e.activation(out=EXS[:], in_=sink, func=AF.Exp), reads=[C.cb], writes=[cst])
    k.op("dve", lambda e: e.tensor_copy(out=C.eps[:], in_=C.eps[:]), reads=[cst, C.cb], writes=[C.cb])

    def layer_mod(l):
        emit_mod(P, condbf, condb, wmod_d[l], bmod[:, l, :], MODV[:, l, :], C.modb, AV[:, l], normg[:, l])

    def mcol(l, i, kind):
        return MODV[:, l, (i * 3 + kind) * 8:(i * 3 + kind + 1) * 8]

    def mix_ap0(fc, c0, w):
        if fc < 4:
            return Ht[:, fc, c0:c0 + w]
        return MIXHI0[:, fc - 4, c0:c0 + w]

    def finish(extra=()):
        ob = Sink("out")
        k.wait_all("sp", list(extra))
        for j in range(4):
            k.dma("sp", [(xo_d[:, :, j * 512:(j + 1) * 512], XT[:, :, j * 512:(j + 1) * 512])], reads=[XB[j]],
                  sembuf=XB[j], sink=ob)
        k.wait_all("sp", [ob])
        k.close()
        return nc

    kcsb = Sink("kcs")
    if stop_after != "nofilter":
        emit_filter(P, C, F, kc_s, kcsb)
    if stop_after == "filter":
        return finish([kcsb, F.RSb])

    layer_mod(0)
    emit_adaln(P, X5, H5, AV[:, 0, 0], mcol(0, 0, 0), C)
    emit_ffn(P, X5, H5, U, Ub, w1_d[0, 0], w3_d[0, 0], w2_d[0, 0], mcol(0, 0, 2), C)
    k.barrier()
    ohb = Buf("oh")
    k.dma("sp", [(OH, oh_d), (VM, vm_d)], writes=[ohb])
    k.op("dve", lambda e: e.tensor_copy(out=C.eps[:], in_=C.eps[:]), reads=[ohb, C.cb], writes=[C.cb])
    scrb = emit_expb(P, C, relrep, C.cb, OH, VM, scr, EXPB, P.WBb[0])
    k.barrier(dbufs=[scrb])
    emit_adaln(P, X5, H5, AV[:, 0, 1], mcol(0, 1, 0), C)
    k.barrier()
    MIXb = [Buf(f"mix{j}") for j in range(4)]
    Sb = Buf("S")
    emit_conv0(P, C, H5, mix_ap0, MIXb, win_w[0], convw, convb, flags, S_t, Sb)
    k.barrier()
    QTb = [Buf(f"qt{j}") for j in range(4)]
    KTb = [Buf(f"kt{j}") for j in range(5)]
    VAb = Buf("va")
    k.op("dve", lambda e: e.memset(VA0[:, 0:16, 0, 64:128], 1.0), writes=[VAb])
    k.op("dve", lambda e: e.memset(VA0[:, 0:16, 1, 0:64], 1.0), writes=[VAb])
    emit_qkv0(P, C, H5, win_w[0], QT0, QTb, KT0, KTb, VA0, VAb, gq, gk, flags)
    k.barrier()
    emit_attn0(P, C, QT0, QTb, KT0, KTb, VA0, VAb, EXPB, P.WBb[0], EXS, Ht, MIXb)
    for i in (1, 2):
        P.WBb[i].r = dict(P.WBb[0].r)
        P.WBb[i].w = P.WBb[0].w
    emit_outproj(P, C, X4, mix_ap0, MIXb, wout_d[0], mcol(0, 1, 2))
    k.barrier()
    emit_adaln(P, X4, H4, AV[:, 0, 2], mcol(0, 2, 0), C)
    emit_ffn(P, X4, H4, U, Ub, w1_d[0, 1], w3_d[0, 1], w2_d[0, 1], mcol(0, 2, 2), C)

    if stop_after == "l0":
        return finish([kcsb])

    layer_mod(1)
    emit_adaln(P, X4, H4, AV[:, 1, 0], mcol(1, 0, 0), C)
    emit_ffn(P, X4, H4, U, Ub, w1_d[1, 0], w3_d[1, 0], w2_d[1, 0], mcol(1, 0, 2), C)
    k.barrier()
    ROPE = UQ[:, 0:8192].bitcast(F32).rearrange("p (a t) -> p a t", a=2)
    HB = AR[:, 26624:26624 + 12300].bitcast(F32).rearrange("p (a t) -> p a t", a=3)
    ropeb, HBb = Buf("rope"), Buf("HB")
    k.dma("sp", [(ROPE, rope_d)], writes=[ropeb])
    emit_adaln(P, X4, H4, AV[:, 1, 1], mcol(1, 1, 0), C)
    W1L = win_w[1]
    hbsb = Buf("hbs")
    for pr in range(6):
        wa, wab = P.load_wa(W1L[:, 768 + pr * 256:768 + (pr + 1) * 256])
        for cl in range(2):
            ch = pr * 2 + cl
            ps, psb = P.bank("h1", [0, 1, 2, 3])
            for ci, col in enumerate((0, 2047)):
                for c in range(8):
                    k.op("pe", lambda e, c=c, ps=ps, wa=wa, cl=cl, ci=ci, col=col: e.matmul(
                        ps[:, ci:ci + 1], lhsT=wa[:, c, cl * 128:(cl + 1) * 128], rhs=Ht[:, c, col:col + 1],
                        start=(c == 0), stop=(c == 7)), reads=[wab] + H4.b, writes=[psb])
            k.op("act", lambda e, ps=ps, ch=ch: e.activation(out=HBS[:, ch, :], in_=ps[:, 0:2], func=AF.Copy),
                 reads=[psb], writes=[hbsb])
    hbsrcb, hbgb, hgb = Buf("hbsrc"), Buf("hbg"), Buf("hg")
    k.dma("sp", [(hb_src.ap(), HBS[:].rearrange("p c t -> p (c t)"))], reads=[hbsb], writes=[hbsrcb])
    cc.allgather(hb_src, hb_g, [hbsrcb], hbgb)
    k.dma("sp", [(HG[:].rearrange("p r c t -> p r (c t)"), hb_g.ap().rearrange("(r p) t -> p r t", p=128))],
          reads=[hbgb], writes=[hgb])
    outs = Sink("l1outs")
    for pr in range(3):
        wa, wab = P.load_wa(W1L[:, pr * 256:(pr + 1) * 256])
        for cl in range(2):
            if pr == 2 and cl == 1:
                break
            for j in range(4):
                c0, w = H4.tiles[j]
                ps, psb = P.bank("h1", [0, 1, 2, 3])
                emit_inproj_tile(P, H4, j, wa, wab, cl, ps, psb)
                qn, qnb = C.tmp()
                emit_qknorm(P, C, ps, psb, w, (gq1 if pr < 2 else gk1), qn[:, 0:w], qnb)
                st, stb = C.tmpbf()
                emit_rope(P, C, qn, qnb, ROPE[:, 0, :], ROPE[:, 1, :], ropeb, c0, w, st[:, 0:w], stb)
                dst = q_s.ap()[:, pr * 2 + cl, c0:c0 + w] if pr < 2 else k_src.ap()[:, c0:c0 + w]
                k.dma("sp", [(dst, st[:, 0:w])], reads=[stb], sembuf=stb, sink=outs)
        if pr == 2:
            for tb in range(16):
                j = tb // 4
                ps, psb = P.bank("o", [4, 5])
                for c in range(8):
                    k.op("pe", lambda e, c=c, tb=tb, ps=ps, wa=wa: e.matmul(
                        ps[:, 0:128], lhsT=Ht[:, c, tb * 128:(tb + 1) * 128], rhs=wa[:, c, 128:256],
                        start=(c == 0), stop=(c == 7)), reads=[wab, H4.b[j]], writes=[psb])
                st, stb = C.tmpbf()
                k.op("act", lambda e, ps=ps, st=st: e.activation(out=st[:, 0:128], in_=ps[:, 0:128], func=AF.Copy),
                     reads=[psb], writes=[stb])
                k.dma("sp", [(v_src.ap()[tb * 128:(tb + 1) * 128, :], st[:, 0:128])], reads=[stb], sembuf=stb, sink=outs)
    kgb, vgb = Buf("kg"), Buf("vg")
    if stop_after == "qkv":
        return finish([outs, kcsb])
    cc.allgather(k_src, k_g, [], kgb, sinks=[outs])
    cc.allgather(v_src, v_g, [], vgb, sinks=[outs])
    for side, sel, colx in ((0, selL, 1), (1, selR, 0)):
        k.op("dve", lambda e, side=side, sel=sel, colx=colx: e.tensor_scalar(
            out=HLR[:, side, :], in0=HG[:, 0, :, colx], scalar1=sel[:, 0:1], scalar2=None, op0=ALU.mult),
            reads=[hgb, C.cb], writes=[hgb])
        for r in range(1, 4):
            k.op("dve", lambda e, side=side, sel=sel, colx=colx, r=r: e.scalar_tensor_tensor(
                out=HLR[:, side, :], in0=HG[:, r, :, colx], scalar=sel[:, r:r + 1], in1=HLR[:, side, :],
                op0=ALU.mult, op1=ALU.add), reads=[hgb, C.cb], writes=[hgb])

    zouts = Sink("zouts")
    for hh in range(2):
        was = [P.load_wa(W1L[:, 768 + part * 512 + hh * 256: 768 + part * 512 + (hh + 1) * 256]) for part in range(3)]
        for cl in range(2):
            cch = hh * 2 + cl
            for part in range(3):
                wa, wab = was[part]
                for j in range(4):
                    c0, w = H4.tiles[j]
                    ps, psb = P.bank("h1", [0, 1, 2, 3])
                    emit_inproj_tile(P, H4, j, wa, wab, cl, ps, psb)
                    k.op("act", lambda e, ps=ps, part=part, c0=c0, w=w: e.activation(
                        out=HB[:, part, 1 + c0:1 + c0 + w], in_=ps[:, 0:w], func=AF.Copy), reads=[psb], writes=[HBb])
                ch12 = part * 4 + cch
                k.op("act", lambda e, part=part, ch12=ch12: e.activation(out=HB[:, part, 0:1], in_=HLR[:, 0, ch12:ch12 + 1],
                                                                         func=AF.Copy), reads=[hgb], writes=[HBb])
                k.op("act", lambda e, part=part, ch12=ch12: e.activation(out=HB[:, part, 2049:2050],
                                                                         in_=HLR[:, 1, ch12:ch12 + 1], func=AF.Copy),
                     reads=[hgb], writes=[HBb])
            for j in range(4):
                c0, w = H4.tiles[j]
                cv = []
                for part in range(3):
                    ch12 = part * 4 + cch
                    t, tb_ = C.tmp()
                    k.op("dve", lambda e, t=t, part=part, c0=c0, ch12=ch12: e.tensor_scalar(
                        out=t[:, :], in0=HB[:, part, 1 + c0:1 + c0 + 512], scalar1=dcw[:, ch12, 1:2],
                        scalar2=dcb[:, ch12:ch12 + 1], op0=ALU.mult, op1=ALU.add), reads=[HBb, C.cb], writes=[tb_])
                    k.op("dve", lambda e, t=t, part=part, c0=c0, ch12=ch12: e.scalar_tensor_tensor(
                        out=t[:, :], in0=HB[:, part, c0:c0 + 512], scalar=dcw[:, ch12, 0:1], in1=t[:, :],
                        op0=ALU.mult, op1=ALU.add), reads=[HBb, C.cb, tb_], writes=[tb_])
                    k.op("dve", lambda e, t=t, part=part, c0=c0, ch12=ch12: e.scalar_tensor_tensor(
                        out=t[:, :], in0=HB[:, part, 2 + c0:2 + c0 + 512], scalar=dcw[:, ch12, 2:3], in1=t[:, :],
                        op0=ALU.mult, op1=ALU.add), reads=[HBb, C.cb, tb_], writes=[tb_])
                    cv.append((t, tb_))
                (x0t, x0b), (x1t, x1b), (vt, vb_) = cv
                k.dma("sp", [(x0_s.ap()[:, cch, c0:c0 + 512], x0t[:, :])], reads=[x0b], sembuf=x0b, sink=zouts)
                k.op("dve", lambda e, x1t=x1t, vt=vt: e.tensor_tensor(out=x1t[:, :], in0=x1t[:, :], in1=vt[:, :], op=ALU.mult),
                     reads=[x1b, vb_], writes=[x1b])
                k.dma("sp", [(zf_s.ap()[:, cch, c0:c0 + 512], x1t[:, :])], reads=[x1b], sembuf=x1b, sink=zouts)
                zb16, zb16b = C.tmpbf()
                k.op("act", lambda e, x1t=x1t, zb16=zb16: e.activation(out=zb16[:, :], in_=x1t[:, :], func=AF.Copy),
                     reads=[x1b], writes=[zb16b])
                k.dma("sp", [(z_src.ap()[cch, :, c0:c0 + 512], zb16[:, :])], reads=[zb16b],
                      sembuf=zb16b, sink=zouts)
    zgb = Buf("zg")
    for cch in range(4):
        cc.allgather(z_src, z_g, [], zgb, sinks=[zouts], sl=cch)

    if stop_after == "l1pre":
        return finish([kgb, vgb, zgb, kcsb])

    k.barrier()
    VB = AR[:, 0:12288].rearrange("p (b d) -> p b d", b=64)
    KT = AR[:, 12288:20480]
    QS = [AR[:, 20480 + i * 2048:20480 + (i + 1) * 2048].rearrange("p (h t) -> p h t", h=4) for i in range(2)]
    QSb = [Buf("qs0"), Buf("qs1")]
    MIX = AR[:, 24576:40960].rearrange("p (c t) -> p c t", c=8)
    MIXb = [Buf(f"mixb{j}") for j in range(4)]
    KTb, VBb = Buf("KT"), Buf("VB")
    k.op("dve", lambda e: e.memset(VB[:, :, 64:128], 1.0), writes=[VBb])
    k.dma("sp", [(KT.rearrange("p (r t) -> p r t", r=4), k_g.ap().rearrange("(r p) t -> p r t", p=128))],
          reads=[kgb], writes=[KTb])
    vsrc = v_g.ap().rearrange("(b p) d -> p b d", p=128)
    k.dma("sp", [(VB[:, :, 0:64], vsrc[:, :, 0:64]), (VB[:, :, 128:192], vsrc[:, :, 64:128])], reads=[vgb], writes=[VBb])
    emit_attn1f(P, C, q_s, KT, KTb, VB, VBb, QS, QSb, MIX, MIXb)

    if stop_after == "attn":
        return finish([zgb, kcsb] + MIXb)

    k.barrier()
    V = Common()
    KC = AR[:, 0:16384].rearrange("p (n c) -> p n c", n=128)
    Zg = AR[:, 0:8192]
    V.ZC = [AR[:, 16384 + i * 512:16384 + (i + 1) * 512].rearrange("p (c k) -> p c k", c=4) for i in range(2)]
    V.ZCb = [Buf("zc0"), Buf("zc1")]
    six = [AR[:, 17408 + i * 512:17408 + (i + 1) * 512].rearrange("p (c k) -> p c k", c=4) for i in range(6)]
    V.Bre, V.Bim, V.Yre, V.Yim, V.Qre, V.Qim = six
    V.Hre = AR[:, 20480:21504].bitcast(F32)
    V.Him = AR[:, 21504:22528].bitcast(F32)
    Zgb, zs2b, KCb = Buf("Zg"), Buf("zs2"), Buf("KC")
    zg_rows = z_g.ap().rearrange("c r t -> (c r) t")
    k.wait_all("pool", [cc.sink(zgb)])
    for r in range(4):
        kb_gather(k, Zg[:, r * 2048:(r + 1) * 2048], zg_rows, IDX[:, r:r + 1], reads=[zgb, C.cb], writes=[Zgb])
    k.dma("sp", [(zs2.ap(), Zg)], reads=[Zgb], writes=[zs2b])
    k.wait_all("sp", [kcsb])
    k.dma("sp", [(KC, kc_s.ap())], reads=[zs2b], writes=[KCb])
    csink = Sink("csrc")
    emit_fftconv(P, C, F, KC, KCb, zs2, zs2b, c_src, csink, V)
    cgb = Buf("cg")
    for i in range(8):
        cc.allgather(c_src, c_g, [], cgb, sinks=[csink], sl=i)

    if stop_after == "fft":
        return finish([cgb] + MIXb)

    cg_rows = c_g.ap().rearrange("i r (a t) -> (i r a) t", t=512)
    k.wait_all("pool", [cc.sink(cgb)])
    for cq in range(4):
        for tq in range(4):
            c0 = tq * 512
            ct, ctb = C.tmp()
            kb_gather(k, ct[:, :], cg_rows, IDX[:, 4 + cq * 4 + tq:5 + cq * 4 + tq], reads=[cgb, C.cb], writes=[ctb])
            zt, ztb = C.tmp()
            k.dma("sp", [(zt[:, :], zf_s.ap()[:, cq, c0:c0 + 512])], writes=[ztb])
            xt_, xtb = C.tmp()
            k.dma("sp", [(xt_[:, :], x0_s.ap()[:, cq, c0:c0 + 512])], writes=[xtb])
            k.op("dve", lambda e, zt=zt, ct=ct, cq=cq: e.scalar_tensor_tensor(
                out=zt[:, :], in0=zt[:, :], scalar=skipv[:, cq:cq + 1], in1=ct[:, :], op0=ALU.mult, op1=ALU.add),
                reads=[ztb, ctb, C.cb], writes=[ztb])
            k.op("dve", lambda e, zt=zt, xt_=xt_, cq=cq, c0=c0: e.tensor_tensor(
                out=MIX[:, 4 + cq, c0:c0 + 512], in0=zt[:, :], in1=xt_[:, :], op=ALU.mult),
                reads=[ztb, xtb], writes=[MIXb[tq]])

    emit_outproj(P, C, X4, lambda fc, c0, w: MIX[:, fc, c0:c0 + w], MIXb, wout_d[1], mcol(1, 1, 2))
    k.barrier()
    Ht2 = AR[:, 0:16384].rearrange("p (c t) -> p c t", c=8)
    U2 = AR[:, 16384:16384 + 12288].rearrange("p (f t) -> p f t", f=6)
    H42 = Act(Ht2, tiles4, "h2")
    Ub2 = [[Buf(f"v{f}_{j}") for j in range(4)] for f in range(6)]
    emit_adaln(P, X4, H42, AV[:, 1, 2], mcol(1, 2, 0), C)
    emit_ffn(P, X4, H42, U2, Ub2, w1_d[1, 1], w3_d[1, 1], w2_d[1, 1], mcol(1, 2, 2), C)
    ob = Sink("out")
    for j in range(4):
        k.dma("sp", [(xo_d[:, :, j * 512:(j + 1) * 512], XT[:, :, j * 512:(j + 1) * 512])], reads=[XB[j]],
              sembuf=XB[j], sink=ob)
    k.wait_all("sp", [ob])
    k.close()
    return nc


def prep_fused(inp, ncore=NCORE):
    base = prep_LA(inp)
    maps = []
    cache = {}
    dcw = np.asarray(inp["d_conv_w"], np.float32)[0]
    dcb = np.asarray(inp["d_conv_b"], np.float32)[0]
    skip = np.asarray(inp["d_skip"], np.float32)[0]
    w4 = np.asarray(inp["d_f_w4"], np.float32)[0]
    p = np.arange(128)
    for core in range(ncore):
        b, qd = core // 4, core % 4
        cq = qd
        if cq not in cache:
            cache[cq] = host_consts_LB(cq)
        zemb, win, dftc, tw = cache[cq]
        m = dict(base[core])
        cf = np.zeros((128, 400), np.float32)
        cf[:, 0:244] = m.pop("cf")[:, 0:244]
        cf[:, 244:280] = fm(dcw).transpose(0, 2, 1).reshape(128, 36)
        cf[:, 280:292] = fm(dcb)
        cf[:, 292:296] = fm(skip)
        if qd > 0:
            cf[:, 296 + qd - 1] = 1.0
        if qd < 3:
            cf[:, 300 + qd + 1] = 1.0
        idx = np.zeros((128, 24), np.uint32)
        for r in range(4):
            idx[:, r] = cq * 512 + r * 128 + p
        for c2 in range(4):
            for tq in range(4):
                idx[:, 4 + c2 * 4 + tq] = ((p // 16) * 64 + c2 * 16 + (p % 16)) * 16 + qd * 4 + tq
        mlp = np.zeros((64, 456), np.float32)
        mlp[0:33, 0:64] = np.asarray(inp["d_f_w1"], np.float32)[0]
        mlp[:, 64:128] = np.asarray(inp["d_f_w2"], np.float32)[0]
        mlp[:, 128:192] = np.asarray(inp["d_f_w3"], np.float32)[0]
        mlp[:, 192] = np.asarray(inp["d_f_b1"], np.float32)[0]
        mlp[:, 193] = np.asarray(inp["d_f_b2"], np.float32)[0]
        mlp[:, 194] = np.asarray(inp["d_f_b3"], np.float32)[0]
        mlp[:, 195] = np.asarray(inp["d_f_freq"], np.float32)[0]
        mlp[:, 200:328] = w4[:, cq * 128:(cq + 1) * 128]
        mlp[:, 328:456] = w4[:, 512 + cq * 128:512 + (cq + 1) * 128]
        m["win_w"] = m.pop("win")
        m.update(cf=cf, idx=idx, mlpw=mlp, zemb=zemb, win=win, dftc=dftc, tw=tw)
        maps.append(m)
    return maps
```

```python
import math
import numpy as np
import concourse.bass as bass
import concourse.mybir as mybir
from concourse.bass_utils import run_bass_kernel_spmd

AF = mybir.ActivationFunctionType
ALU = mybir.AluOpType
F32 = mybir.dt.float32
BF16 = mybir.dt.bfloat16
EPOCH = 12000

D = 1024
S = 8192
TOK = 2048
NCORE = 8
DFF = 2752
DFFP = 2816
NF = 22
HD = 64
EPS = 1e-6
INC = 2304


class Buf:
    __slots__ = ("name", "w", "r", "dsem", "dcnt")

    def __init__(self, name=""):
        self.name = name
        self.w = None
        self.r = {}
        self.dsem = None
        self.dcnt = 0


class Sink:
    def __init__(self, name=""):
        self.name = name
        self.toks = {}


class KB:
    def __init__(self, nc):
        self.nc = nc
        self.engs = {"pe": nc.tensor, "act": nc.scalar, "dve": nc.vector,
                     "pool": nc.gpsimd, "sp": nc.sync}
        self.cnt = {e: 0 for e in self.engs}
        self.sems = {}
        self.waited = {e: {} for e in self.engs}
        self.nsem = 0
        self._stack = []

    def _newsem(self, name):
        cm = self.nc.semaphore(name)
        h = cm.__enter__()
        self._stack.append(cm)
        self.nsem += 1
        return h

    def sb(self, name, shape, dt):
        cm = self.nc.sbuf_tensor(name, shape, dt)
        t = cm.__enter__()
        self._stack.append(cm)
        return t

    def ps(self, name, shape, dt=F32):
        cm = self.nc.psum_tensor(name, shape, dt)
        t = cm.__enter__()
        self._stack.append(cm)
        return t

    def close(self):
        while self._stack:
            self._stack.pop().__exit__(None, None, None)

    def _engsem(self, eng):
        key = (eng, self.cnt[eng] // EPOCH)
        if key not in self.sems:
            self.sems[key] = self._newsem(f"s_{eng}_{key[1]}")
        return key

    def _deps(self, reads, writes):
        deps = {}

        def add(k, v):
            if deps.get(k, 0) < v:
                deps[k] = v
        for b in reads:
            if b.w is not None:
                add(*b.w)
        for b in writes:
            if b.w is not None:
                add(*b.w)
            for kk, v in b.r.items():
                add(kk, v)
        return deps

    def _wait(self, eng, deps):
        w = self.waited[eng]
        e = self.engs[eng]
        for kk, v in deps.items():
            if w.get(kk, 0) >= v:
                continue
            e.wait_ge(self.sems[kk], v)
            w[kk] = v

    def _commit(self, tok, reads, writes):
        kk, v = tok
        for b in reads:
            if b.r.get(kk, 0) < v:
                b.r[kk] = v
        for b in writes:
            b.w = tok
            b.r = {}

    def op(self, eng, fn, reads=(), writes=()):
        deps = self._deps(reads, writes)
        if eng == "pe":
            deps = {kk: v for kk, v in deps.items() if kk[0] != "pe"}
        self._wait(eng, deps)
        key = self._engsem(eng)
        ins = fn(self.engs[eng])
        self.cnt[eng] += 1
        val = self.cnt[eng] - key[1] * EPOCH
        ins.then_inc(self.sems[key], 1)
        tok = (key, val)
        self._commit(tok, reads, writes)
        return tok

    def dma(self, q, pairs, reads=(), writes=(), sembuf=None, sink=None, **kw):
        sbf = sembuf or (writes[0] if writes else reads[0])
        if sbf.dsem is None:
            sbf.dsem = {}
            sbf.dcnt = {}
        sw = (q == "pool")
        if sw not in sbf.dsem:
            sbf.dsem[sw] = ("d", id(sbf), sw)
            sbf.dcnt[sw] = 0
            self.sems[sbf.dsem[sw]] = self._newsem(f"d_{self.nsem}")
        deps = self._deps(reads, writes)
        self._wait(q, deps)
        e = self.engs[q]
        for (o, i) in pairs:
            e.dma_start(out=o, in_=i, **kw).then_inc(self.sems[sbf.dsem[sw]], 16)
            sbf.dcnt[sw] += 16
        tok = (sbf.dsem[sw], sbf.dcnt[sw])
        self._commit(tok, reads, writes)
        if sink is not None and sink.toks.get(tok[0], 0) < tok[1]:
            sink.toks[tok[0]] = tok[1]
        return tok

    def wait_all(self, eng, bufs):
        deps = {}
        for b in bufs:
            d = dict(b.toks) if isinstance(b, Sink) else self._deps([b], [b])
            for kk, v in d.items():
                if deps.get(kk, 0) < v:
                    deps[kk] = v
        self._wait(eng, deps)

    def barrier(self, dbufs=()):
        deps = {}
        for e in ("pe", "act", "dve"):
            if self.cnt[e] == 0:
                continue
            ep = (self.cnt[e] - 1) // EPOCH
            deps[(e, ep)] = self.cnt[e] - ep * EPOCH
        for b in dbufs:
            for kk, v in self._deps([b], [b]).items():
                if deps.get(kk, 0) < v:
                    deps[kk] = v
        for e in ("pe", "act", "dve", "pool", "sp"):
            self._wait(e, dict(deps))


class Prog:
    def __init__(self, nc, k):
        self.nc = nc
        self.k = k
        self.WA = [k.sb(f"WA{i}", [128, 8, 256], BF16) for i in range(4)]
        self.WAb = [Buf(f"WA{i}") for i in range(4)]
        self.WBf = k.sb("WBf", [128, 3, 1024], F32)
        self.WB = [self.WBf[:, i, :].bitcast(BF16).rearrange("p (f n) -> p f n", f=2) for i in range(3)]
        self.WBb = [Buf(f"WB{i}") for i in range(3)]
        self.wa_i = 0
        self.PS = [k.ps(f"ps{i}", [128, 512]) for i in range(8)]
        self.PSb = [Buf(f"ps{i}") for i in range(8)]
        self.rr = {}

    def next_wa(self, parity=None):
        i = self.wa_i
        self.wa_i = (self.wa_i + 1) % 4
        return i

    def bank(self, group, banks):
        i = self.rr.get(group, 0)
        self.rr[group] = (i + 1) % len(banks)
        b = banks[i]
        return self.PS[b], self.PSb[b]

    def load_wa(self, src_ap):
        i = self.next_wa()
        self.k.dma("pool", [(self.WA[i][:], src_ap.rearrange("(c p) n -> p c n", p=128))], writes=[self.WAb[i]])
        return self.WA[i], self.WAb[i]

    def load_wb(self, i, src_ap):
        self.k.dma("pool", [(self.WB[i], src_ap.rearrange("(f p) n -> p f n", p=128))], writes=[self.WBb[i]])
        return self.WB[i], self.WBb[i]


def emit_mod(P, cond_bf, cond_b, wmod_l, bmod_l, modv, modb, a_out, normg_l):
    k = P.k
    ps, psb = P.PS[7], P.PSb[7]
    for ch in range(36):
        wa, wab = P.load_wa(wmod_l[:, ch * 256:(ch + 1) * 256])
        for cl in range(2):
            cc = ch * 2 + cl
            for kc in range(8):
                k.op("pe", lambda e, cc=cc, kc=kc, cl=cl, wa=wa: e.matmul(
                    ps[:, cc:cc + 1], lhsT=wa[:, kc, cl * 128:(cl + 1) * 128], rhs=cond_bf[:, kc:kc + 1],
                    start=(kc == 0), stop=(kc == 7)),
                    reads=[wab, cond_b], writes=[psb])
    k.op("dve", lambda e: e.tensor_tensor(out=modv[:], in0=ps[:, 0:72], in1=bmod_l, op=ALU.add),
         reads=[psb], writes=[modb])
    for i in range(3):
        k.op("dve", lambda e, i=i: e.scalar_tensor_tensor(
            out=a_out[:, i, :], in0=modv[:, (i * 3 + 1) * 8:(i * 3 + 2) * 8], scalar=1.0, in1=normg_l[:, i, :],
            op0=ALU.add, op1=ALU.mult), reads=[modb], writes=[modb])
    for i in (0, 2):
        k.op("dve", lambda e, i=i: e.tensor_scalar(
            out=modv[:, (i * 3 + 2) * 8:(i * 3 + 3) * 8], in0=modv[:, (i * 3 + 2) * 8:(i * 3 + 3) * 8],
            scalar1=0.5, scalar2=None, op0=ALU.mult), reads=[modb], writes=[modb])


class Act:
    def __init__(self, t, tiles, name):
        self.t = t
        self.tiles = tiles
        self.b = [Buf(f"{name}{j}") for j in range(len(tiles))]

    def ap(self, c, j):
        c0, w = self.tiles[j]
        return self.t[:, c, c0:c0 + w]


def emit_adaln(P, X, H, a_ap, shift_ap, C):
    k = P.k
    for j, (c0, w) in enumerate(X.tiles):
        ps, psb = P.PS[6], P.PSb[6]
        for c in range(8):
            sq, sqb = C.tmpbf()
            k.op("act", lambda e, c=c, j=j, w=w, sq=sq: e.activation(out=sq[:, 0:w], in_=X.ap(c, j), func=AF.Square),
                 reads=[X.b[j]], writes=[sqb])
            k.op("pe", lambda e, c=c, w=w, sq=sq: e.matmul(ps[:, 0:w], lhsT=C.ones_bf[:], rhs=sq[:, 0:w],
                                                     start=(c == 0), stop=(c == 7)),
                 reads=[sqb, C.cb], writes=[psb])
        k.op("act", lambda e, w=w: e.activation(out=C.RS[:, 0:w], in_=ps[:, 0:w], func=AF.Sqrt,
                                                bias=C.eps[:, 0:1], scale=1.0 / D),
             reads=[psb, C.cb], writes=[C.RSb])
        k.op("dve", lambda e, w=w: e.reciprocal(out=C.RS[:, 0:w], in_=C.RS[:, 0:w]), reads=[C.RSb], writes=[C.RSb])
        for c in range(8):
            tt, ttb = C.tmp()
            k.op("dve", lambda e, c=c, j=j, w=w, tt=tt: e.scalar_tensor_tensor(
                out=tt[:, 0:w], in0=X.ap(c, j), scalar=a_ap[:, c:c + 1], in1=C.RS[:, 0:w],
                op0=ALU.mult, op1=ALU.mult), reads=[X.b[j], C.RSb, C.modb], writes=[ttb])
            k.op("act", lambda e, c=c, j=j, w=w, tt=tt: e.activation(
                out=H.ap(c, j), in_=tt[:, 0:w], func=AF.Identity, bias=shift_ap[:, c:c + 1], scale=1.0),
                reads=[ttb, C.modb], writes=[H.b[j]])


def emit_ffn(P, X, H, U, Ub, w1, w3, w2, gate_ap, C, hook=None):
    k = P.k
    groups = [(0, 3), (3, 3), (6, 3), (9, 2)]
    ntile = len(X.tiles)
    for (p0, npair) in groups:
        for pl in range(npair):
            p = p0 + pl
            wa1, wa1b = P.load_wa(w1[:, p * 256:(p + 1) * 256])
            wa3, wa3b = P.load_wa(w3[:, p * 256:(p + 1) * 256])
            for fl in range(2):
                fu = pl * 2 + fl
                for j in range(ntile):
                    c0, w = X.tiles[j]
                    ps1, ps1b = P.bank("h1", [0, 1])
                    ps3, ps3b = P.bank("h3", [2, 3])
                    for c in range(8):
                        k.op("pe", lambda e, c=c, j=j, w=w, ps1=ps1, wa1=wa1, fl=fl: e.matmul(
                            ps1[:, 0:w], lhsT=wa1[:, c, fl * 128:(fl + 1) * 128], rhs=H.ap(c, j),
                            start=(c == 0), stop=(c == 7)), reads=[wa1b, H.b[j]], writes=[ps1b])
                    for c in range(8):
                        k.op("pe", lambda e, c=c, j=j, w=w, ps3=ps3, wa3=wa3, fl=fl: e.matmul(
                            ps3[:, 0:w], lhsT=wa3[:, c, fl * 128:(fl + 1) * 128], rhs=H.ap(c, j),
                            start=(c == 0), stop=(c == 7)), reads=[wa3b, H.b[j]], writes=[ps3b])
                    sl, slb = C.tmp()
                    k.op("act", lambda e, w=w, ps1=ps1, sl=sl: e.activation(out=sl[:, 0:w], in_=ps1[:, 0:w], func=AF.Silu),
                         reads=[ps1b], writes=[slb])
                    k.op("dve", lambda e, w=w, c0=c0, ps3=ps3, sl=sl, fu=fu: e.tensor_tensor(
                        out=U[:, fu, c0:c0 + w], in0=ps3[:, 0:w], in1=sl[:, 0:w], op=ALU.mult),
                        reads=[ps3b, slb], writes=[Ub[fu][j]])
                    if hook is not None:
                        hook()
        for pl in range(npair):
            P.load_wb(pl, w2[(p0 + pl) * 256:(p0 + pl + 1) * 256, :])
        nfu = npair * 2
        for c in range(8):
            for j in range(ntile):
                c0, w = X.tiles[j]
                pso, psob = P.bank("o", [4, 5])
                for fu in range(nfu):
                    k.op("pe", lambda e, fu=fu, c=c, w=w, c0=c0, pso=pso: e.matmul(
                        pso[:, 0:w], lhsT=P.WB[fu // 2][:, fu % 2, c * 128:(c + 1) * 128], rhs=U[:, fu, c0:c0 + w],
                        start=(fu == 0), stop=(fu == nfu - 1)), reads=[P.WBb[fu // 2], Ub[fu][j]], writes=[psob])
                k.op("dve", lambda e, c=c, j=j, w=w, pso=pso: e.scalar_tensor_tensor(
                    out=X.ap(c, j), in0=pso[:, 0:w], scalar=gate_ap[:, c:c + 1], in1=X.ap(c, j),
                    op0=ALU.mult, op1=ALU.add), reads=[psob, X.b[j], C.modb], writes=[X.b[j]])


class Common:
    pass


def emit_inproj_tile(P, H, j, wa, wab, cl, ps, psb, cols=None):
    k = P.k
    c0, w = H.tiles[j]
    if cols is not None:
        c0, w = c0 + cols[0], cols[1]
    for c in range(8):
        k.op("pe", lambda e, c=c, c0=c0, w=w: e.matmul(
            ps[:, 0:w], lhsT=wa[:, c, cl * 128:(cl + 1) * 128], rhs=H.t[:, c, c0:c0 + w],
            start=(c == 0), stop=(c == 7)), reads=[wab, H.b[j]], writes=[psb])
    return w


def emit_qknorm(P, C, ps, psb, w, g_ap, out_ap, out_b, extra_reads=()):
    k = P.k
    sq, sqb = C.tmpbf()
    k.op("act", lambda e: e.activation(out=sq[:, 0:w], in_=ps[:, 0:w], func=AF.Square), reads=[psb], writes=[sqb])
    p2, p2b = P.PS[6], P.PSb[6]
    k.op("pe", lambda e: e.matmul(p2[:, 0:w], lhsT=C.bd_bf[:], rhs=sq[:, 0:w], start=True, stop=True),
         reads=[sqb, C.cb], writes=[p2b])
    rs, rsb = C.tmp()
    k.op("act", lambda e: e.activation(out=rs[:, 0:w], in_=p2[:, 0:w], func=AF.Sqrt, bias=C.eps[:, 0:1],
                                       scale=1.0 / HD), reads=[p2b, C.cb], writes=[rsb])
    k.op("dve", lambda e: e.reciprocal(out=rs[:, 0:w], in_=rs[:, 0:w]), reads=[rsb], writes=[rsb])
    k.op("dve", lambda e: e.scalar_tensor_tensor(out=out_ap, in0=ps[:, 0:w], scalar=g_ap, in1=rs[:, 0:w],
                                                 op0=ALU.mult, op1=ALU.mult),
         reads=[psb, rsb, C.cb] + list(extra_reads), writes=[out_b])


def emit_outproj(P, C, X, MIX, MIXb, wout_l, gate_ap):
    k = P.k
    for pr in range(4):
        wa, wab = P.load_wa(wout_l[:, pr * 256:(pr + 1) * 256])
        for cl in range(2):
            c = pr * 2 + cl
            for j, (c0, w) in enumerate(X.tiles):
                pso, psob = P.bank("o", [4, 5])
                for fc in range(8):
                    k.op("pe", lambda e, fc=fc, c0=c0, w=w, pso=pso, wa=wa, cl=cl: e.matmul(
                        pso[:, 0:w], lhsT=wa[:, fc, cl * 128:(cl + 1) * 128], rhs=MIX(fc, c0, w),
                        start=(fc == 0), stop=(fc == 7)), reads=[wab, MIXb[j]], writes=[psob])
                k.op("dve", lambda e, c=c, j=j, w=w, pso=pso: e.scalar_tensor_tensor(
                    out=X.ap(c, j), in0=pso[:, 0:w], scalar=gate_ap[:, c:c + 1], in1=X.ap(c, j),
                    op0=ALU.mult, op1=ALU.add), reads=[psob, X.b[j], C.modb], writes=[X.b[j]])


def emit_expb(P, C, relrep, relb, onehot, vmask, scr, EXPB, EXPBb):
    k = P.k
    nc = P.nc
    E8 = C.E8
    E8b = Buf("E8")
    for h in range(8):
        lt, ltb = C.tmp()
        k.op("dve", lambda e, h=h, lt=lt: e.tensor_scalar(out=lt[0:32, 0:128], in0=C.ones_f[0:32, 0:128],
                                                          scalar1=relrep[0:32, h:h + 1], scalar2=None, op0=ALU.mult),
             reads=[relb, C.cb], writes=[ltb])
        ps, psb = P.bank("h1", [0, 1])
        k.op("pe", lambda e, lt=lt, ps=ps: e.matmul(ps[:, 0:512], lhsT=lt[0:32, 0:128], rhs=onehot[0:32, :],
                                                    start=True, stop=True), reads=[ltb, C.cb], writes=[psb])
        ex, exb = C.tmp()
        k.op("act", lambda e, ps=ps, ex=ex: e.activation(out=ex[:, :], in_=ps[:, 0:512], func=AF.Exp),
             reads=[psb], writes=[exb])
        k.op("dve", lambda e, h=h, ex=ex: e.tensor_tensor(out=E8[:, h, :], in0=ex[:, :], in1=vmask[:, :], op=ALU.mult),
             reads=[exb, C.cb], writes=[E8b])
    scrb = Buf("scr")
    k.dma("sp", [(scr.ap(), E8[:])], reads=[E8b], writes=[scrb])
    pairs = []
    for h in range(8):
        for bi in range(3):
            off = h * 512 + 128 * (1 - bi) + 255
            src = bass.AP(tensor=scr, offset=off, ap=[[8 * 512 - 1, 128], [1, 128]])
            pairs.append((EXPB[:, h, bi, :], src))
    k.dma("sp", pairs, reads=[scrb], writes=[EXPBb])
    return scrb


def emit_conv0(P, C, H, MIX, MIXb, win_l, convw, convb, flags, S_t, Sb):
    k = P.k
    ntile_own = 4
    for hh in range(2):
        wgb, wgbb = P.load_wa(win_l[:, 768 + hh * 256: 768 + (hh + 1) * 256])
        wgc, wgcb = P.load_wa(win_l[:, 1280 + hh * 256: 1280 + (hh + 1) * 256])
        wu, wub = P.load_wa(win_l[:, 1792 + hh * 256: 1792 + (hh + 1) * 256])
        for cl in range(2):
            cc = hh * 2 + cl
            for j in range(ntile_own):
                c0, w = H.tiles[j]
                pg, pgb = P.bank("h1", [0, 1])
                pu, pub = P.bank("h3", [2, 3])
                emit_inproj_tile(P, H, j, wgc, wgcb, cl, pg, pgb)
                emit_inproj_tile(P, H, j, wu, wub, cl, pu, pub)
                tg, tgb = C.tmp()
                k.op("act", lambda e, tg=tg, pg=pg, w=w: e.activation(out=tg[:, 0:w], in_=pg[:, 0:w], func=AF.Copy),
                     reads=[pgb], writes=[tgb])
                k.op("dve", lambda e, tg=tg, pu=pu, w=w, c0=c0: e.tensor_tensor(
                    out=S_t[:, 1 + c0:1 + c0 + w], in0=pu[:, 0:w], in1=tg[:, 0:w], op=ALU.mult),
                    reads=[pub, tgb], writes=[Sb])
            pg, pgb = P.bank("h1", [0, 1])
            pu, pub = P.bank("h3", [2, 3])
            emit_inproj_tile(P, H, 4, wgc, wgcb, cl, pg, pgb, cols=(127, 2))
            emit_inproj_tile(P, H, 4, wu, wub, cl, pu, pub, cols=(127, 2))
            tg, tgb = C.tmp()
            k.op("act", lambda e, tg=tg, pg=pg: e.activation(out=tg[:, 0:2], in_=pg[:, 0:2], func=AF.Copy),
                 reads=[pgb], writes=[tgb])
            k.op("dve", lambda e, tg=tg, pu=pu: e.scalar_tensor_tensor(
                out=S_t[:, 0:1], in0=pu[:, 0:1], scalar=flags[:, 0:1], in1=tg[:, 0:1], op0=ALU.mult, op1=ALU.mult),
                reads=[pub, tgb, C.cb], writes=[Sb])
            k.op("dve", lambda e, tg=tg, pu=pu: e.scalar_tensor_tensor(
                out=S_t[:, 2049:2050], in0=pu[:, 1:2], scalar=flags[:, 1:2], in1=tg[:, 1:2], op0=ALU.mult, op1=ALU.mult),
                reads=[pub, tgb, C.cb], writes=[Sb])
            for j in range(ntile_own):
                c0, w = H.tiles[j]
                pb_, pbb = P.bank("o", [4, 5])
                emit_inproj_tile(P, H, j, wgb, wgbb, cl, pb_, pbb)
                t, tb = C.tmp()
                k.op("dve", lambda e, t=t, c0=c0, w=w, cc=cc: e.tensor_scalar(
                    out=t[:, 0:w], in0=S_t[:, 1 + c0:1 + c0 + w], scalar1=convw[:, cc, 1:2], scalar2=convb[:, cc:cc + 1],
                    op0=ALU.mult, op1=ALU.add), reads=[Sb, C.cb], writes=[tb])
                k.op("dve", lambda e, t=t, c0=c0, w=w, cc=cc: e.scalar_tensor_tensor(
                    out=t[:, 0:w], in0=S_t[:, c0:c0 + w], scalar=convw[:, cc, 0:1], in1=t[:, 0:w],
                    op0=ALU.mult, op1=ALU.add), reads=[Sb, C.cb, tb], writes=[tb])
                k.op("dve", lambda e, t=t, c0=c0, w=w, cc=cc: e.scalar_tensor_tensor(
                    out=t[:, 0:w], in0=S_t[:, 2 + c0:2 + c0 + w], scalar=convw[:, cc, 2:3], in1=t[:, 0:w],
                    op0=ALU.mult, op1=ALU.add), reads=[Sb, C.cb, tb], writes=[tb])
                k.op("dve", lambda e, t=t, c0=c0, w=w, cc=cc, pb_=pb_: e.tensor_tensor(
                    out=MIX(4 + cc, c0, w), in0=pb_[:, 0:w], in1=t[:, 0:w], op=ALU.mult),
                    reads=[pbb, tb], writes=[MIXb[j]])


def emit_qkv0(P, C, H, win_l, QT, QTb, KT, KTb, VA, VAb, gq, gk, flags):
    k = P.k
    for pr in range(2):
        wa, wab = P.load_wa(win_l[:, pr * 256:(pr + 1) * 256])
        for cl in range(2):
            hh = pr * 2 + cl
            for j in range(4):
                c0, w = H.tiles[j]
                ps, psb = P.bank("h1", [0, 1, 2, 3])
                emit_inproj_tile(P, H, j, wa, wab, cl, ps, psb)
                emit_qknorm(P, C, ps, psb, w, gq[:, 0:1], QT[:, hh, c0:c0 + w], QTb[j])
    wa, wab = P.load_wa(win_l[:, 512:768])
    for j in range(5):
        c0, w = H.tiles[j]
        ps, psb = P.bank("h1", [0, 1, 2, 3])
        emit_inproj_tile(P, H, j, wa, wab, 0, ps, psb)
        emit_qknorm(P, C, ps, psb, w, gk[:, 0:1], KT[:, c0:c0 + w], KTb[j])
    for tb in range(18):
        j = tb // 4 if tb < 16 else 4
        ps, psb = P.bank("o", [4, 5])
        for c in range(8):
            k.op("pe", lambda e, c=c, tb=tb, ps=ps: e.matmul(
                ps[:, 0:128], lhsT=H.t[:, c, tb * 128:(tb + 1) * 128], rhs=wa[:, c, 128:256],
                start=(c == 0), stop=(c == 7)), reads=[wab, H.b[j]], writes=[psb])
        if tb < 16:
            k.op("act", lambda e, tb=tb, ps=ps: e.activation(out=VA[:, tb, 0, 0:64], in_=ps[:, 0:64], func=AF.Copy),
                 reads=[psb], writes=[VAb])
            k.op("act", lambda e, tb=tb, ps=ps: e.activation(out=VA[:, tb, 1, 64:128], in_=ps[:, 64:128], func=AF.Copy),
                 reads=[psb], writes=[VAb])
        else:
            fl = flags[:, tb - 16:tb - 15]
            k.op("dve", lambda e, tb=tb, ps=ps, fl=fl: e.tensor_scalar(
                out=VA[:, tb, 0, 0:64], in0=ps[:, 0:64], scalar1=fl, scalar2=None, op0=ALU.mult),
                reads=[psb, C.cb], writes=[VAb])
            k.op("dve", lambda e, tb=tb, ps=ps, fl=fl: e.tensor_scalar(
                out=VA[:, tb, 1, 64:128], in0=ps[:, 64:128], scalar1=fl, scalar2=None, op0=ALU.mult),
                reads=[psb, C.cb], writes=[VAb])
            k.op("dve", lambda e, tb=tb, fl=fl: e.tensor_scalar(
                out=VA[:, tb, 0, 64:128], in0=C.ones_f[:, 0:64], scalar1=fl, scalar2=None, op0=ALU.mult),
                reads=[C.cb], writes=[VAb])
            k.op("dve", lambda e, tb=tb, fl=fl: e.tensor_scalar(
                out=VA[:, tb, 1, 0:64], in0=C.ones_f[:, 0:64], scalar1=fl, scalar2=None, op0=ALU.mult),
                reads=[C.cb], writes=[VAb])


def emit_attn0(P, C, QT, QTb, KT, KTb, VA, VAb, EXPB, EXPBb, expsink, MIXLO, MIXb):
    k = P.k
    for n in range(16):
        j = n // 4
        for g in range(2):
            gs = slice(g * 64, (g + 1) * 64)
            ds_ = slice((1 - g) * 64, (2 - g) * 64)
            po, pob = P.bank("o", [4, 5])
            for bi in range(3):
                kb = n - 1 + bi
                kidx = 16 if kb < 0 else (17 if kb > 15 else kb)
                kj = kidx // 4 if kidx < 16 else 4
                ps, psb = P.bank("h1", [0, 1, 2, 3])
                k.op("pe", lambda e, ps=ps, kidx=kidx, n=n, gs=gs: e.matmul(
                    ps[:, 0:512], lhsT=KT[gs, kidx * 128:(kidx + 1) * 128], rhs=QT[gs, :, n * 128:(n + 1) * 128],
                    start=True, stop=True), reads=[KTb[kj], QTb[j]], writes=[psb])
                ex, exb = C.tmp()
                k.op("act", lambda e, ps=ps, ex=ex: e.activation(out=ex[:, :], in_=ps[:, 0:512], func=AF.Exp,
                                                                 scale=HD ** -0.5),
                     reads=[psb], writes=[exb])
                pt, ptb = C.tmpbf()
                k.op("dve", lambda e, ex=ex, pt=pt, g=g, bi=bi: e.tensor_tensor(
                    out=pt[:, :].rearrange("p (h q) -> p h q", h=4), in0=ex[:, :].rearrange("p (h q) -> p h q", h=4),
                    in1=EXPB[:, g * 4:(g + 1) * 4, bi, :], op=ALU.mult), reads=[exb, EXPBb], writes=[ptb])
                for hh in range(4):
                    k.op("pe", lambda e, po=po, pt=pt, hh=hh, kidx=kidx, g=g, bi=bi: e.matmul(
                        po[:, hh * 128:(hh + 1) * 128], lhsT=VA[:, kidx, g, :], rhs=pt[:, hh * 128:(hh + 1) * 128],
                        start=(bi == 0 and hh == 0), stop=(bi == 2 and hh == 3), skip_group_check=True),
                        reads=[VAb, ptb], writes=[pob])
            rd, rdb = C.tmp()
            for hh in range(4):
                k.op("dve", lambda e, rd=rd, po=po, hh=hh, g=g, ds_=ds_: e.tensor_scalar(
                    out=rd[ds_, hh * 128:(hh + 1) * 128], in0=po[ds_, hh * 128:(hh + 1) * 128],
                    scalar1=expsink[ds_, g * 4 + hh:g * 4 + hh + 1], scalar2=None, op0=ALU.add),
                    reads=[pob, C.cb], writes=[rdb])
            k.op("dve", lambda e, rd=rd, ds_=ds_: e.reciprocal(out=rd[ds_, :], in_=rd[ds_, :]), reads=[rdb], writes=[rdb])
            k.op("dve", lambda e, rd=rd, po=po, gs=gs, ds_=ds_, n=n: e.tensor_tensor(
                out=MIXLO[gs, 0:4, n * 128:(n + 1) * 128], in0=po[gs, :].rearrange("p (h q) -> p h q", h=4),
                in1=rd[ds_, :].rearrange("p (h q) -> p h q", h=4), op=ALU.mult), reads=[pob, rdb], writes=[MIXb[j]])


class RotTmp:
    def __init__(self, k, name, n, dt):
        self.t = [k.sb(f"{name}{i}", [128, 512], dt) for i in range(n)]
        self.b = [Buf(f"{name}{i}") for i in range(n)]
        self.i = 0

    def __call__(self):
        i = self.i
        self.i = (i + 1) % len(self.t)
        return self.t[i], self.b[i]


BUCKET_MODE = "trunc"


def t5_bucket_np(rel):
    n = np.abs(rel)
    v = (np.log(np.maximum(n, 1).astype(np.float32) / np.float32(8)) / np.float32(math.log(16.0)) * np.float32(8))
    large = 8 + (np.rint(v).astype(np.int32) if BUCKET_MODE == "round" else v.astype(np.int32))
    large = np.minimum(large, 15)
    return np.where(rel > 0, 16, 0) + np.where(n < 8, n, large)


def build_LA(do_l1=True, stop_after=None):
    nc = bass.Bass("TRN2", target_bir_lowering=False)
    k = KB(nc)
    dt_in = lambda name, shape: nc.dram_tensor(name, list(shape), F32, kind="ExternalInput").ap()
    xT_d = dt_in("xT", [128, 8, 2048])
    xH_d = dt_in("xH", [128, 8, 256])
    cf_d = dt_in("cf", [128, 1400])
    oh_d = dt_in("oh", [32, 512])
    vm_d = dt_in("vm", [128, 512])
    wmod_d = dt_in("wmod", [2, 1024, 9216])
    w1_d = dt_in("w1", [2, 2, 1024, DFFP])
    w3_d = dt_in("w3", [2, 2, 1024, DFFP])
    w2_d = dt_in("w2", [2, 2, DFFP, 1024])
    win_d = dt_in("win", [2, 1024, INC])
    wout_d = dt_in("wout", [2, 1024, 1024])
    xo_d = nc.dram_tensor("xo", [128, 8, 2048], F32, kind="ExternalOutput").ap()
    scr = nc.dram_tensor("scr", [128, 8 * 512], F32, kind="Internal")

    P = Prog(nc, k)
    C = Common()
    C.tmp = RotTmp(k, "tf", 5, F32)
    C.tmpbf = RotTmp(k, "tb", 3, BF16)
    C.RS = k.sb("RS", [128, 512], F32)
    C.RSb = Buf("RS")
    C.cb = Buf("consts")
    C.modb = Buf("mod")
    CF = k.sb("CF", [128, 1400], F32)
    OH = k.sb("OH", [32, 512], F32)
    VM = k.sb("VM", [128, 512], F32)
    C.ones_bf = k.sb("ones_bf", [128, 128], BF16)
    C.bd_bf = k.sb("bd_bf", [128, 128], BF16)
    C.ones_f = k.sb("ones_f", [128, 128], F32)
    C.eps = k.sb("eps", [128, 1], F32)
    condbf = k.sb("condbf", [128, 8], BF16)
    MODV = k.sb("modv", [128, 2, 72], F32)
    AV = k.sb("av", [128, 2, 3, 8], F32)
    EXS = k.sb("exs", [128, 8], F32)
    XT = k.sb("XT", [128, 8, 2048], F32)
    Ht = k.sb("H", [128, 8, 2304], BF16)
    UQ = k.sb("UQ", [128, 15360], BF16)
    MX = k.sb("MX", [128, 4096], F32)

    o_cT, o_fl, o_ng, o_bm, o_cw, o_cb, o_gq, o_gk, o_sink = 0, 8, 10, 58, 202, 214, 218, 220, 222
    cT = CF[:, o_cT:o_cT + 8]
    flags = CF[:, o_fl:o_fl + 2]
    normg = CF[:, o_ng:o_ng + 48].rearrange("p (l i c) -> p l i c", l=2, i=3)
    bmod = CF[:, o_bm:o_bm + 144].rearrange("p (l m) -> p l m", l=2)
    convw = CF[:, o_cw:o_cw + 12].rearrange("p (c t) -> p c t", c=4)
    convb = CF[:, o_cb:o_cb + 4]
    gq = CF[:, o_gq:o_gq + 2]
    gk = CF[:, o_gk:o_gk + 2]
    sink = CF[:, o_sink:o_sink + 8]
    relrep = CF[:, 232:240]

    XH = MX[:, 0:2048].rearrange("p (c t) -> p c t", c=8)
    MIXHI = MX[:, :].bitcast(BF16).rearrange("p (c t) -> p c t", c=4)
    U = UQ[:, 0:6 * 2304].rearrange("p (f t) -> p f t", f=6)
    QT = UQ[:, 0:8192].rearrange("p (h t) -> p h t", h=4)
    KT = UQ[:, 8192:8192 + 2304]
    VA = UQ[:, 10496:10496 + 4608].rearrange("p (b g d) -> p b g d", b=18, g=2)
    S_t = UQ[:, 10496:10496 + 4100].bitcast(F32)
    C.E8 = UQ[:, 0:8192].bitcast(F32).rearrange("p (h m) -> p h m", h=8)
    EXPB = P.WBf[:, :, :].rearrange("p a (b q) -> p (a b) q", q=128).rearrange("p (h b) q -> p h b q", h=8)

    tiles5 = [(0, 512), (512, 512), (1024, 512), (1536, 512), (2048, 256)]
    tiles4 = tiles5[:4]

    class XAct:
        def __init__(self, tiles):
            self.tiles = tiles
            self.b = XB[:len(tiles)]

        def ap(self, c, j):
            if j < 4:
                return XT[:, c, j * 512:(j + 1) * 512]
            return XH[:, c, :]
    XB = [Buf(f"x{j}") for j in range(5)]
    X5, X4 = XAct(tiles5), XAct(tiles4)
    H5 = Act(Ht, tiles5, "h")
    H4 = Act(Ht, tiles4, "h")
    H4.b = H5.b[:4]
    Ub = [[Buf(f"u{f}_{j}") for j in range(5)] for f in range(6)]

    k.dma("sp", [(CF[:], cf_d)], writes=[C.cb])
    k.dma("sp", [(OH[:], oh_d), (VM[:], vm_d)], writes=[C.cb], sembuf=C.cb)
    for j in range(4):
        k.dma("sp", [(XT[:, :, j * 512:(j + 1) * 512], xT_d[:, :, j * 512:(j + 1) * 512])], writes=[XB[j]])
    k.dma("sp", [(XH, xH_d)], writes=[XB[4]])
    cst = Buf("cst")
    k.op("dve", lambda e: e.memset(C.ones_bf[:], 1.0), writes=[cst])
    k.op("dve", lambda e: e.memset(C.ones_f[:], 1.0), writes=[cst])
    k.op("dve", lambda e: e.memset(C.eps[:], EPS), writes=[cst])
    k.op("dve", lambda e: e.memset(C.bd_bf[:], 0.0), writes=[cst])
    k.op("dve", lambda e: e.memset(C.bd_bf[0:64, 0:64], 1.0), writes=[cst])
    k.op("dve", lambda e: e.memset(C.bd_bf[64:128, 64:128], 1.0), writes=[cst])
    condb = Buf("cond")
    k.op("act", lambda e: e.activation(out=condbf[:], in_=cT, func=AF.Silu), reads=[C.cb], writes=[condb])
    k.op("act", lambda e: e.activation(out=EXS[:], in_=sink, func=AF.Exp), reads=[C.cb], writes=[cst])
    k.op("dve", lambda e: e.tensor_copy(out=C.eps[:], in_=C.eps[:]), reads=[cst, C.cb], writes=[C.cb])

    def layer_mod(l):
        emit_mod(P, condbf, condb, wmod_d[l], bmod[:, l, :], MODV[:, l, :], C.modb, AV[:, l], normg[:, l])

    def mcol(l, i, kind):
        return MODV[:, l, (i * 3 + kind) * 8:(i * 3 + kind + 1) * 8]

    def mix_ap(fc, c0, w):
        if fc < 4:
            return Ht[:, fc, c0:c0 + w]
        return MIXHI[:, fc - 4, c0:c0 + w]

    def finish():
        ob = Sink("out")
        for j in range(4):
            k.dma("sp", [(xo_d[:, :, j * 512:(j + 1) * 512], XT[:, :, j * 512:(j + 1) * 512])], reads=[XB[j]],
                  sembuf=XB[j], sink=ob)
        k.wait_all("sp", [ob])
        k.close()
        return nc

    layer_mod(0)
    emit_adaln(P, X5, H5, AV[:, 0, 0], mcol(0, 0, 0), C)
    emit_ffn(P, X5, H5, U, Ub, w1_d[0, 0], w3_d[0, 0], w2_d[0, 0], mcol(0, 0, 2), C)
    if stop_after == "ffn0":
        return finish()
    k.barrier()
    ohb = C.cb
    scrb = emit_expb(P, C, relrep, C.cb, OH, VM, scr, EXPB, P.WBb[0])
    k.barrier(dbufs=[scrb])
    if stop_after == "expb":
        dbg = nc.dram_tensor("dbg", [128, 3072], F32, kind="ExternalOutput").ap()
        ob2 = Buf("dbg")
        k.dma("sp", [(dbg, P.WBf[:].rearrange("p a b -> p (a b)"))], reads=[P.WBb[0]], writes=[ob2], sembuf=ob2)
        k.wait_all("sp", [ob2])
        return finish()
    emit_adaln(P, X5, H5, AV[:, 0, 1], mcol(0, 1, 0), C)
    k.barrier()
    MIXb = [Buf(f"mix{j}") for j in range(4)]
    Sb = Buf("S")
    emit_conv0(P, C, H5, mix_ap, MIXb, win_d[0], convw, convb, flags, S_t, Sb)
    k.barrier()
    QTb = [Buf(f"qt{j}") for j in range(4)]
    KTb = [Buf(f"kt{j}") for j in range(5)]
    VAb = Buf("va")
    k.op("dve", lambda e: e.memset(VA[:, 0:16, 0, 64:128], 1.0), writes=[VAb])
    k.op("dve", lambda e: e.memset(VA[:, 0:16, 1, 0:64], 1.0), writes=[VAb])
    emit_qkv0(P, C, H5, win_d[0], QT, QTb, KT, KTb, VA, VAb, gq, gk, flags)
    k.barrier()
    emit_attn0(P, C, QT, QTb, KT, KTb, VA, VAb, EXPB, P.WBb[0], EXS, Ht, MIXb)
    for i in (1, 2):
        P.WBb[i].r = dict(P.WBb[0].r)
        P.WBb[i].w = P.WBb[0].w
    if stop_after == "mix":
        dbg = nc.dram_tensor("dbg", [128, 8, 2048], BF16, kind="ExternalOutput").ap()
        ob2 = Buf("dbg")
        k.dma("sp", [(dbg[:, 0:4, :], Ht[:, 0:4, 0:2048]), (dbg[:, 4:8, :], MIXHI)], reads=MIXb, writes=[ob2], sembuf=ob2)
        k.wait_all("sp", [ob2])
        return finish()
    emit_outproj(P, C, X4, mix_ap, MIXb, wout_d[0], mcol(0, 1, 2))
    if stop_after == "mixer":
        return finish()
    k.barrier()
    emit_adaln(P, X4, H4, AV[:, 0, 2], mcol(0, 2, 0), C)
    emit_ffn(P, X4, H4, U, Ub, w1_d[0, 1], w3_d[0, 1], w2_d[0, 1], mcol(0, 2, 2), C)
    if not do_l1:
        return finish()

    rope_d = dt_in("rope", [128, 2, 2048])
    qo_d = nc.dram_tensor("qo", [128, 4, 2048], BF16, kind="ExternalOutput").ap()
    ko_d = nc.dram_tensor("ko", [128, 2048], BF16, kind="ExternalOutput").ap()
    vo_d = nc.dram_tensor("vo", [16, 128, 128], BF16, kind="ExternalOutput").ap()
    hco_d = nc.dram_tensor("hco", [12, 128, 2048], F32, kind="ExternalOutput").ap()
    layer_mod(1)
    emit_adaln(P, X4, H4, AV[:, 1, 0], mcol(1, 0, 0), C)
    emit_ffn(P, X4, H4, U, Ub, w1_d[1, 0], w3_d[1, 0], w2_d[1, 0], mcol(1, 0, 2), C)
    k.barrier()
    ROPE = UQ[:, 0:8192].bitcast(F32).rearrange("p (a t) -> p a t", a=2)
    ropeb = Buf("rope")
    k.dma("sp", [(ROPE, rope_d)], writes=[ropeb])
    emit_adaln(P, X4, H4, AV[:, 1, 1], mcol(1, 1, 0), C)
    outb = Sink("outs")
    modo_d = nc.dram_tensor("modo", [128, 96], F32, kind="ExternalOutput").ap()
    k.dma("sp", [(modo_d[:, 0:72], MODV[:, 1, :]), (modo_d[:, 72:96], AV[:, 1].rearrange("p i c -> p (i c)"))],
          reads=[C.modb], sembuf=C.modb, sink=outb)
    emit_inproj1(P, C, H4, win_d[1], CF[:, 240:241], CF[:, 241:242], ROPE[:, 0, :], ROPE[:, 1, :], ropeb,
                 qo_d, ko_d, vo_d, hco_d, outb)
    k.wait_all("sp", [outb])
    return finish()


def host_consts():
    idx = np.arange(512)
    rel = 255 - idx
    valid = (np.abs(rel) <= 128) & (idx < 511)
    bucket = t5_bucket_np(rel.astype(np.int64))
    oh = np.zeros((32, 512), np.float32)
    oh[bucket[valid], idx[valid]] = 1.0
    vm = np.tile(valid.astype(np.float32)[None, :], (128, 1))
    return oh, vm


def fm(v):
    v = np.asarray(v, np.float32)
    sh = v.shape
    v = v.reshape(sh[:-1] + (sh[-1] // 128, 128))
    return np.moveaxis(v, -1, 0)


def prep_LA(inp):
    x = np.asarray(inp["x"], np.float32)
    qperm = np.concatenate([np.r_[hh * 64:(hh + 1) * 64, (4 + hh) * 64:(5 + hh) * 64] for hh in range(4)])
    win = np.ascontiguousarray(np.asarray(inp["w_in"], np.float32))
    win = np.concatenate([win[:, :, qperm], win[:, :, 512:]], axis=2)
    eo = np.r_[0:64:2, 1:64:2]
    eo_q = np.concatenate([h * 64 + eo for h in range(8)])
    eo_k = np.concatenate([512 + h * 64 + eo for h in range(2)])
    win[1] = np.concatenate([win[1][:, eo_q], win[1][:, eo_k], win[1][:, 640:]], axis=1)
    wout = np.asarray(inp["w_out"], np.float32)
    wout = np.ascontiguousarray(np.concatenate([wout[:, qperm, :], wout[:, 512:, :]], axis=1))
    pad = DFFP - DFF
    w1 = np.pad(np.asarray(inp["ffn_w1"], np.float32), ((0, 0), (0, 0), (0, 0), (0, pad)))
    w3 = np.pad(np.asarray(inp["ffn_w3"], np.float32), ((0, 0), (0, 0), (0, 0), (0, pad)))
    w2 = np.pad(np.asarray(inp["ffn_w2"], np.float32), ((0, 0), (0, 0), (0, pad), (0, 0)))
    wmod = np.ascontiguousarray(np.asarray(inp["w_mod"], np.float32))
    oh, vm = host_consts()
    shared = dict(oh=oh, vm=vm, wmod=wmod, w1=w1, w3=w3, w2=w2, win=np.ascontiguousarray(win), wout=wout)
    maps = []
    for core in range(NCORE):
        b, qd = core // 4, core % 4
        t0 = qd * TOK
        xs = x[b, t0:t0 + TOK]
        xT = np.ascontiguousarray(xs.T.reshape(8, 128, TOK).transpose(1, 0, 2))
        halo = np.zeros((256, D), np.float32)
        fl = np.zeros((2,), np.float32)
        if qd > 0:
            halo[0:128] = x[b, t0 - 128:t0]
            fl[0] = 1.0
        if qd < 3:
            halo[128:256] = x[b, t0 + TOK:t0 + TOK + 128]
            fl[1] = 1.0
        xH = np.ascontiguousarray(halo.T.reshape(8, 128, 256).transpose(1, 0, 2))
        cf = np.zeros((128, 1400), np.float32)
        cf[:, 0:8] = fm(inp["c"][b])
        cf[:, 8:10] = fl[None, :]
        cf[:, 10:58] = fm(inp["norm_g"]).reshape(128, 48)
        cf[:, 58:202] = fm(inp["b_mod"]).reshape(128, 144)
        cw = np.asarray(inp["b_conv_w"], np.float32)[0]
        cf[:, 202:214] = fm(cw).transpose(0, 2, 1).reshape(128, 12)
        cf[:, 214:218] = fm(np.asarray(inp["b_conv_b"], np.float32)[0])
        aq = np.asarray(inp["a_qk_g"], np.float32)[0]
        cf[:, 218] = np.tile(aq[0], 2)
        cf[:, 220] = np.tile(aq[1], 2)
        cf[:, 222:230] = np.asarray(inp["a_sink"], np.float32)[0][None, :]
        cf[0:32, 232:240] = np.asarray(inp["rel_table"], np.float32)
        cg = np.asarray(inp["c_qk_g"], np.float32)[0]
        cf[:, 240] = np.tile(cg[0][eo], 2)
        cf[:, 241] = np.tile(cg[1][eo], 2)
        pos = np.arange(t0, t0 + TOK)
        row = (pos // 64).astype(np.float32)
        col = (pos % 64).astype(np.float32)
        inv = (np.float32(10000.0) ** (-np.arange(0, 32, 2, dtype=np.float32) / np.float32(32))).astype(np.float32)
        ang = np.concatenate([row[:, None] * inv, col[:, None] * inv], axis=-1).astype(np.float32)
        cs = np.cos(ang).astype(np.float32).T
        sn = np.sin(ang).astype(np.float32).T
        rope = np.zeros((128, 2, TOK), np.float32)
        for qd4 in range(4):
            rope[qd4 * 32:(qd4 + 1) * 32, 0] = cs
            rope[qd4 * 32:(qd4 + 1) * 32, 1] = sn if qd4 % 2 == 0 else -sn
        m_rope = rope
        m = dict(shared)
        m.update(xT=xT, xH=xH, cf=cf, rope=m_rope)
        maps.append(m)
    return maps


def emit_rope(P, C, qn, qnb, CS, SNs, ropeb, c0, w, out_ap, out_b):
    k = P.k
    t1, t1b = C.tmp()
    k.op("dve", lambda e: e.tensor_tensor(out=t1[:, 0:w], in0=qn[:, 0:w], in1=CS[:, c0:c0 + w], op=ALU.mult),
         reads=[qnb, ropeb], writes=[t1b])
    t2, t2b = C.tmp()
    for qd in range(4):
        src = qd ^ 1
        k.op("dve", lambda e, qd=qd, src=src: e.tensor_tensor(
            out=t2[qd * 32:(qd + 1) * 32, 0:w], in0=qn[src * 32:(src + 1) * 32, 0:w],
            in1=SNs[src * 32:(src + 1) * 32, c0:c0 + w], op=ALU.mult), reads=[qnb, ropeb], writes=[t2b])
    k.op("dve", lambda e: e.tensor_tensor(out=out_ap, in0=t1[:, 0:w], in1=t2[:, 0:w], op=ALU.add),
         reads=[t1b, t2b], writes=[out_b])


def emit_inproj1(P, C, H, win_l, gq, gk, CS, SNs, ropeb, qo_d, ko_d, vo_d, hco_d, outb):
    k = P.k
    for pr in range(3):
        wa, wab = P.load_wa(win_l[:, pr * 256:(pr + 1) * 256])
        for cl in range(2):
            if pr == 2 and cl == 1:
                break
            for j in range(4):
                c0, w = H.tiles[j]
                ps, psb = P.bank("h1", [0, 1, 2, 3])
                emit_inproj_tile(P, H, j, wa, wab, cl, ps, psb)
                qn, qnb = C.tmp()
                emit_qknorm(P, C, ps, psb, w, (gq if pr < 2 else gk)[:, 0:1], qn[:, 0:w], qnb)
                st, stb = C.tmpbf()
                emit_rope(P, C, qn, qnb, CS, SNs, ropeb, c0, w, st[:, 0:w], stb)
                dst = qo_d[:, pr * 2 + cl, c0:c0 + w] if pr < 2 else ko_d[:, c0:c0 + w]
                k.dma("sp", [(dst, st[:, 0:w])], reads=[stb], sembuf=stb, sink=outb)
        if pr == 2:
            for tb in range(16):
                j = tb // 4
                ps, psb = P.bank("o", [4, 5])
                for c in range(8):
                    k.op("pe", lambda e, c=c, tb=tb, ps=ps: e.matmul(
                        ps[:, 0:128], lhsT=H.t[:, c, tb * 128:(tb + 1) * 128], rhs=wa[:, c, 128:256],
                        start=(c == 0), stop=(c == 7)), reads=[wab, H.b[j]], writes=[psb])
                st, stb = C.tmpbf()
                k.op("act", lambda e, ps=ps, st=st: e.activation(out=st[:, 0:128], in_=ps[:, 0:128], func=AF.Copy),
                     reads=[psb], writes=[stb])
                k.dma("sp", [(vo_d[tb], st[:, 0:128])], reads=[stb], sembuf=stb, sink=outb)
    for pr in range(6):
        wa, wab = P.load_wa(win_l[:, 768 + pr * 256:768 + (pr + 1) * 256])
        for cl in range(2):
            ch = pr * 2 + cl
            for j in range(4):
                c0, w = H.tiles[j]
                ps, psb = P.bank("h1", [0, 1, 2, 3])
                emit_inproj_tile(P, H, j, wa, wab, cl, ps, psb)
                st, stb = C.tmp()
                k.op("act", lambda e, ps=ps, st=st, w=w: e.activation(out=st[:, 0:w], in_=ps[:, 0:w], func=AF.Copy),
                     reads=[psb], writes=[stb])
                k.dma("sp", [(hco_d[ch, :, c0:c0 + w], st[:, 0:w])], reads=[stb], sembuf=stb, sink=outb)


NFFT = 16384


def build_LB():
    nc = bass.Bass("TRN2", target_bir_lowering=False)
    k = KB(nc)
    dt_in = lambda name, shape: nc.dram_tensor(name, list(shape), F32, kind="ExternalInput").ap()
    hc_d = dt_in("hc3", [3, 128, S])
    cf_d = dt_in("cf2", [128, 32])
    mlp_d = dt_in("mlpw", [64, 456])
    zemb_d = dt_in("zemb", [33, NFFT])
    win_d = dt_in("win", [128, 128, 128])
    dft_d = dt_in("dftc", [128, 512])
    tw_d = dt_in("tw", [128, 2, 2, 128])
    y_d = nc.dram_tensor("yT", [128, S], F32, kind="ExternalOutput").ap()
    zs = nc.dram_tensor("zs", [128, S], F32, kind="Internal")
    cs = nc.dram_tensor("cs", [128, S], F32, kind="Internal")

    PS = [k.ps(f"ps{i}", [128, 512]) for i in range(8)]
    PSb = [Buf(f"ps{i}") for i in range(8)]
    rr = {}

    def bank(group, banks):
        i = rr.get(group, 0)
        rr[group] = (i + 1) % len(banks)
        return PS[banks[i]], PSb[banks[i]]

    tmp = RotTmp(k, "tf", 8, F32)
    tmpbf = RotTmp(k, "tb", 4, BF16)
    cb = Buf("consts")
    CF = k.sb("CF", [128, 32], F32)
    MLP = k.sb("MLP", [64, 456], F32)
    DFT = k.sb("DFT", [128, 512], BF16)
    TW = k.sb("TW", [128, 2, 2, 128], F32)
    X0 = k.sb("X0", [128, S], F32)
    Z = k.sb("Z", [128, S], F32)
    IN = k.sb("IN", [128, S + 2], F32)
    KC = k.sb("KC", [128, 128, 128], BF16)
    ZC = k.sb("ZC", [64, 128, 128], BF16)
    ACC = k.sb("ACC", [128, 128], F32)
    SM = k.sb("SM", [128, 16], F32)
    ones_f = k.sb("ones_f", [128, 1], F32)
    X0b, Zb, INb, KCb, ZCb, ACCb, SMb = [Buf(n) for n in "X0 Z IN KC ZC ACC SM".split()]

    k.dma("sp", [(CF[:], cf_d), (MLP[:], mlp_d), (TW[:], tw_d)], writes=[cb])
    dftb = Buf("dft")
    k.dma("pool", [(DFT[:], dft_d)], writes=[dftb])
    Fre, Fim, nFim = DFT[:, 0:128], DFT[:, 128:256], DFT[:, 384:512]
    Fcat, FcatI2, FcatI1 = DFT[:, 0:256], DFT[:, 128:384], DFT[:, 256:512]
    k.op("dve", lambda e: e.memset(ones_f[:], 1.0), writes=[SMb])
    k.op("dve", lambda e: e.memset(SM[:, 0:1], math.pi / 2), writes=[SMb])
    k.op("dve", lambda e: e.memset(SM[:, 1:2], EPS), writes=[SMb])
    k.op("dve", lambda e: e.memset(IN[:, 0:1], 0.0), writes=[INb])
    k.op("dve", lambda e: e.memset(IN[:, S + 1:S + 2], 0.0), writes=[INb])

    CH = 2048
    for part in range(3):
        k.dma("sp", [(IN[:, 1:S + 1], hc_d[part])], writes=[INb])
        for cc in range(S // CH):
            c0 = cc * CH
            wc = CF[:, part * 4:part * 4 + 3]
            bc = CF[:, part * 4 + 3:part * 4 + 4]
            for s0 in range(0, CH, 512):
                a0 = c0 + s0
                if part == 0:
                    o, ob_ = X0[:, a0:a0 + 512], X0b
                elif part == 1:
                    o, ob_ = Z[:, a0:a0 + 512], Zb
                else:
                    tt, ttb = tmp()
                    o, ob_ = tt[:, 0:512], ttb
                k.op("dve", lambda e, o=o, a0=a0, wc=wc, bc=bc: e.tensor_scalar(
                    out=o, in0=IN[:, a0 + 1:a0 + 513], scalar1=wc[:, 1:2], scalar2=bc, op0=ALU.mult, op1=ALU.add),
                    reads=[INb, cb], writes=[ob_])
                k.op("dve", lambda e, o=o, a0=a0, wc=wc: e.scalar_tensor_tensor(
                    out=o, in0=IN[:, a0:a0 + 512], scalar=wc[:, 0:1], in1=o, op0=ALU.mult, op1=ALU.add),
                    reads=[INb, cb, ob_], writes=[ob_])
                k.op("dve", lambda e, o=o, a0=a0, wc=wc: e.scalar_tensor_tensor(
                    out=o, in0=IN[:, a0 + 2:a0 + 514], scalar=wc[:, 2:3], in1=o, op0=ALU.mult, op1=ALU.add),
                    reads=[INb, cb, ob_], writes=[ob_])
                if part == 2:
                    k.op("pool", lambda e, o=o, a0=a0: e.tensor_tensor(
                        out=Z[:, a0:a0 + 512], in0=Z[:, a0:a0 + 512], in1=o, op=ALU.mult), reads=[ob_, Zb], writes=[Zb])
    zsb = Buf("zs")
    k.dma("sp", [(zs.ap(), Z[:])], reads=[Zb], writes=[zsb])
    src = bass.AP(tensor=zs, offset=0, ap=[[128, 64], [S, 128], [1, 128]])
    k.dma("pool", [(ZC[:], src)], reads=[zsb], writes=[ZCb])

    W1, W2, W3 = MLP[0:33, 0:64], MLP[:, 64:128], MLP[:, 128:192]
    W4 = MLP[:, 200:456].rearrange("p (d c) -> p d c", d=2)
    k.op("dve", lambda e: e.tensor_scalar(out=SM[0:64, 2:3], in0=MLP[:, 195:196], scalar1=0.25, scalar2=None, op0=ALU.mult),
         reads=[cb], writes=[SMb])
    for i in range(3):
        k.op("dve", lambda e, i=i: e.tensor_tensor(out=SM[0:64, 3 + i:4 + i], in0=MLP[:, 192 + i:193 + i], in1=SM[0:64, 2:3],
                                                   op=ALU.mult), reads=[cb, SMb], writes=[SMb])
    k.op("dve", lambda e: e.memset(ACC[:], 0.0), writes=[ACCb])

    def sin4(ps, psb, li):
        s1, s1b = tmp()
        k.op("act", lambda e: e.activation(out=s1[0:64, :], in_=ps[0:64, :], func=AF.Sin, bias=SM[0:64, 3 + li:4 + li],
                                           scale=SM[0:64, 2:3]), reads=[psb, SMb], writes=[s1b])
        a1, a1b = tmp()
        k.op("act", lambda e: e.activation(out=a1[0:64, :], in_=ps[0:64, :], func=AF.Abs, bias=SM[0:64, 3 + li:4 + li],
                                           scale=SM[0:64, 2:3]), reads=[psb, SMb], writes=[a1b])
        k.op("act", lambda e: e.activation(out=a1[0:64, :], in_=a1[0:64, :], func=AF.Sin, bias=SM[0:64, 0:1], scale=-1.0),
             reads=[a1b, SMb], writes=[a1b])
        k.op("dve", lambda e: e.tensor_tensor(out=a1[0:64, :], in0=a1[0:64, :], in1=s1[0:64, :], op=ALU.mult),
             reads=[a1b, s1b], writes=[a1b])
        k.op("dve", lambda e: e.tensor_tensor(out=s1[0:64, :], in0=s1[0:64, :], in1=s1[0:64, :], op=ALU.mult),
             reads=[s1b], writes=[s1b])
        k.op("dve", lambda e: e.tensor_scalar(out=s1[0:64, :], in0=s1[0:64, :], scalar1=-2.0, scalar2=1.0,
                                              op0=ALU.mult, op1=ALU.add), reads=[s1b], writes=[s1b])
        k.op("dve", lambda e: e.scalar_tensor_tensor(out=a1[0:64, :], in0=a1[0:64, :], scalar=4.0, in1=s1[0:64, :],
                                                     op0=ALU.mult, op1=ALU.mult), reads=[a1b, s1b], writes=[a1b])
        return a1, a1b

    for c in range(32):
        ze, zeb = tmp()
        k.dma("sp", [(ze[0:33, :], zemb_d[:, c * 512:(c + 1) * 512])], writes=[zeb])
        wt, wtb = tmp()
        k.dma("sp", [(wt[:, :].rearrange("p (n c) -> p n c", n=4), win_d[:, c * 4:(c + 1) * 4, :])], writes=[wtb])
        ps, psb = bank("m", [6, 7])
        k.op("pe", lambda e, ps=ps, ze=ze: e.matmul(ps[0:64, :], lhsT=W1, rhs=ze[0:33, :], start=True, stop=True),
             reads=[zeb, cb], writes=[psb])
        h, hb = sin4(ps, psb, 0)
        for li, W in ((1, W2), (2, W3)):
            ps, psb = bank("m", [6, 7])
            k.op("pe", lambda e, ps=ps, h=h, W=W: e.matmul(ps[0:64, :], lhsT=W, rhs=h[0:64, :], start=True, stop=True),
                 reads=[hb, cb], writes=[psb])
            h, hb = sin4(ps, psb, li)
        ps4, ps4b = bank("m", [6, 7])
        for n2l in range(4):
            for d in range(2):
                k.op("pe", lambda e, ps4=ps4, h=h, n2l=n2l, d=d: e.matmul(
                    ps4[d * 64:(d + 1) * 64, n2l * 128:(n2l + 1) * 128],
                    lhsT=h[0:64, n2l * 128 + d * 64:n2l * 128 + (d + 1) * 64], rhs=W4[:, d, :],
                    start=True, stop=True, skip_group_check=True), reads=[hb, cb], writes=[ps4b])
        kc, kcb = tmp()
        k.op("dve", lambda e, kc=kc, ps4=ps4, wt=wt: e.tensor_tensor(out=kc[:, :], in0=ps4[:, :], in1=wt[:, :], op=ALU.mult),
             reads=[ps4b, wtb], writes=[kcb])
        k.op("act", lambda e, kc=kc, c=c: e.activation(
            out=KC[:, :, c * 4:(c + 1) * 4], in_=kc[:, :].rearrange("p (n c) -> p c n", n=4), func=AF.Copy),
            reads=[kcb], writes=[KCb])
        sq, sqb = tmp()
        k.op("pool", lambda e, kc=kc, sq=sq: e.tensor_tensor(out=sq[:, :], in0=kc[:, :], in1=kc[:, :], op=ALU.mult),
             reads=[kcb], writes=[sqb])
        rd, rdb = tmp()
        k.op("dve", lambda e, sq=sq, rd=rd: e.tensor_reduce(
            out=rd[:, 0:128], in_=sq[:, :].rearrange("p (n c) -> p c n", n=4), axis=mybir.AxisListType.X, op=ALU.add),
            reads=[sqb], writes=[rdb])
        k.op("pool", lambda e, rd=rd: e.tensor_tensor(out=ACC[:], in0=ACC[:], in1=rd[:, 0:128], op=ALU.add),
             reads=[rdb, ACCb], writes=[ACCb])
    ps, psb = bank("m", [6, 7])
    k.op("pe", lambda e, ps=ps: e.matmul(ps[:, 0:1], lhsT=ACC[:], rhs=ones_f[:, 0:1], start=True, stop=True),
         reads=[ACCb, SMb], writes=[psb])
    k.op("act", lambda e, ps=ps: e.activation(out=SM[:, 8:9], in_=ps[:, 0:1], func=AF.Sqrt, bias=SM[:, 1:2], scale=1.0),
         reads=[psb, SMb], writes=[SMb])
    k.op("dve", lambda e: e.reciprocal(out=SM[:, 8:9], in_=SM[:, 8:9]), reads=[SMb], writes=[SMb])

    Bre = k.sb("Bre", [128, 4, 128], BF16)
    Bim = k.sb("Bim", [128, 4, 128], BF16)
    Hre = k.sb("Hre", [128, 512], F32)
    Him = k.sb("Him", [128, 512], F32)
    Yre = k.sb("Yre", [128, 4, 128], BF16)
    Yim = k.sb("Yim", [128, 4, 128], BF16)
    Qre = k.sb("Qre", [128, 4, 128], BF16)
    Qim = k.sb("Qim", [128, 4, 128], BF16)
    Bb, Hb, Yb, Qb = Buf("B"), Buf("H"), Buf("Y"), Buf("Q")
    TWre2, TWim2 = TW[:, 0], TW[:, 1]

    def cmul_from_psum(A, Ab, sign, outre, outim, outb, pr):
        A4 = A[:, :].rearrange("p (c r k) -> p c r k", c=2, r=2)
        Are, Aim = A4[:, :, 0, :], A4[:, :, 1, :]
        t1, t1b = tmp()
        t2, t2b = tmp()
        v = lambda t: t[:, 0:256].rearrange("p (c k) -> p c k", c=2)
        k.op("dve", lambda e: e.tensor_tensor(out=v(t1), in0=Are, in1=TWre2, op=ALU.mult), reads=[Ab, cb], writes=[t1b])
        k.op("dve", lambda e: e.tensor_tensor(out=v(t2), in0=Aim, in1=TWim2, op=ALU.mult), reads=[Ab, cb], writes=[t2b])
        k.op("pool", lambda e: e.tensor_tensor(out=outre[:, pr * 2:pr * 2 + 2, :], in0=v(t1), in1=v(t2),
                                               op=(ALU.subtract if sign > 0 else ALU.add)),
             reads=[t1b, t2b], writes=[outb])
        t3, t3b = tmp()
        t4, t4b = tmp()
        k.op("dve", lambda e: e.tensor_tensor(out=v(t3), in0=Aim, in1=TWre2, op=ALU.mult), reads=[Ab, cb], writes=[t3b])
        k.op("dve", lambda e: e.tensor_tensor(out=v(t4), in0=Are, in1=TWim2, op=ALU.mult), reads=[Ab, cb], writes=[t4b])
        k.op("pool", lambda e: e.tensor_tensor(out=outim[:, pr * 2:pr * 2 + 2, :], in0=v(t3), in1=v(t4),
                                               op=(ALU.add if sign > 0 else ALU.subtract)),
             reads=[t3b, t4b], writes=[outb])

    def fft_fwd(src, srcb, krows, ch0, xre, xreb, xim, ximb):
        for pr in range(2):
            A, Ab = bank("A", [0, 1])
            for cl in range(2):
                ch = ch0 + pr * 2 + cl
                k.op("pe", lambda e, A=A, cl=cl, ch=ch: e.matmul(
                    A[:, cl * 256:(cl + 1) * 256], lhsT=src[0:krows, ch, :], rhs=Fcat[0:krows, :],
                    start=True, stop=True, skip_group_check=True), reads=[srcb, dftb], writes=[Ab])
            cmul_from_psum(A, Ab, +1, Bre, Bim, Bb, pr)
        bre = Bre[:, :, :].rearrange("p c k -> p (c k)")
        bim = Bim[:, :, :].rearrange("p c k -> p (c k)")
        k.op("pe", lambda e: e.matmul(xre[:, :], lhsT=Fre, rhs=bre, start=True, stop=False), reads=[Bb, dftb], writes=[xreb])
        k.op("pe", lambda e: e.matmul(xre[:, :], lhsT=nFim, rhs=bim, start=False, stop=True), reads=[Bb, dftb], writes=[xreb])
        k.op("pe", lambda e: e.matmul(xim[:, :], lhsT=Fim, rhs=bre, start=True, stop=False), reads=[Bb, dftb], writes=[ximb])
        k.op("pe", lambda e: e.matmul(xim[:, :], lhsT=Fre, rhs=bim, start=False, stop=True), reads=[Bb, dftb], writes=[ximb])

    csb = Sink("cs")
    for g in range(32):
        ch0 = g * 4
        fft_fwd(KC, KCb, 128, ch0, PS[2], PSb[2], PS[3], PSb[3])
        k.op("act", lambda e: e.activation(out=Hre[:], in_=PS[2][:, :], func=AF.Copy), reads=[PSb[2]], writes=[Hb])
        k.op("act", lambda e: e.activation(out=Him[:], in_=PS[3][:, :], func=AF.Copy), reads=[PSb[3]], writes=[Hb])
        fft_fwd(ZC, ZCb, 64, ch0, PS[4], PSb[4], PS[5], PSb[5])
        t1, t1b = tmp()
        t2, t2b = tmp()
        k.op("dve", lambda e, t1=t1: e.tensor_tensor(out=t1[:, :], in0=PS[4][:, :], in1=Hre[:], op=ALU.mult),
             reads=[PSb[4], Hb], writes=[t1b])
        k.op("dve", lambda e, t2=t2: e.tensor_tensor(out=t2[:, :], in0=PS[5][:, :], in1=Him[:], op=ALU.mult),
             reads=[PSb[5], Hb], writes=[t2b])
        k.op("pool", lambda e, t1=t1, t2=t2: e.tensor_tensor(out=Yre[:, :, :].rearrange("p c k -> p (c k)"), in0=t1[:, :],
                                                             in1=t2[:, :], op=ALU.subtract), reads=[t1b, t2b], writes=[Yb])
        t3, t3b = tmp()
        t4, t4b = tmp()
        k.op("dve", lambda e, t3=t3: e.tensor_tensor(out=t3[:, :], in0=PS[4][:, :], in1=Him[:], op=ALU.mult),
             reads=[PSb[4], Hb], writes=[t3b])
        k.op("dve", lambda e, t4=t4: e.tensor_tensor(out=t4[:, :], in0=PS[5][:, :], in1=Hre[:], op=ALU.mult),
             reads=[PSb[5], Hb], writes=[t4b])
        k.op("pool", lambda e, t3=t3, t4=t4: e.tensor_tensor(out=Yim[:, :, :].rearrange("p c k -> p (c k)"), in0=t3[:, :],
                                                             in1=t4[:, :], op=ALU.add), reads=[t3b, t4b], writes=[Yb])
        for pr in range(2):
            Pk, Pkb = bank("A", [0, 1])
            for cl in range(2):
                c4 = pr * 2 + cl
                k.op("pe", lambda e, Pk=Pk, cl=cl, c4=c4: e.matmul(
                    Pk[:, cl * 256:(cl + 1) * 256], lhsT=Yre[:, c4, :], rhs=FcatI1, start=True, stop=False,
                    skip_group_check=True), reads=[Yb, dftb], writes=[Pkb])
                k.op("pe", lambda e, Pk=Pk, cl=cl, c4=c4: e.matmul(
                    Pk[:, cl * 256:(cl + 1) * 256], lhsT=Yim[:, c4, :], rhs=FcatI2, start=False, stop=True,
                    skip_group_check=True), reads=[Yb, dftb], writes=[Pkb])
            cmul_from_psum(Pk, Pkb, -1, Qre, Qim, Qb, pr)
        yo, yob = PS[6], PSb[6]
        k.op("pe", lambda e: e.matmul(yo[0:64, :], lhsT=DFT[:, 0:64], rhs=Qre[:, :, :].rearrange("p c k -> p (c k)"),
                                      start=True, stop=False), reads=[Qb, dftb], writes=[yob])
        k.op("pe", lambda e: e.matmul(yo[0:64, :], lhsT=DFT[:, 128:192], rhs=Qim[:, :, :].rearrange("p c k -> p (c k)"),
                                      start=False, stop=True), reads=[Qb, dftb], writes=[yob])
        ys, ysb = tmp()
        k.op("act", lambda e, ys=ys: e.activation(out=ys[0:64, :], in_=yo[0:64, :], func=AF.Copy, scale=1.0 / NFFT),
             reads=[yob], writes=[ysb])
        dst = bass.AP(tensor=cs, offset=ch0 * S, ap=[[128, 64], [S, 4], [1, 128]])
        k.dma("sp", [(dst, ys[0:64, :].rearrange("p (c k) -> p c k", c=4))], reads=[ysb], sembuf=ysb, sink=csb)

    k.wait_all("sp", [csb])
    k.dma("sp", [(IN[:, 0:S], cs.ap())], writes=[INb])
    ob = Sink("out")
    for s0 in range(0, S, 512):
        t, tb = tmp()
        k.op("dve", lambda e, t=t, s0=s0: e.tensor_scalar(out=t[:, :], in0=Z[:, s0:s0 + 512], scalar1=CF[:, 12:13],
                                                          scalar2=None, op0=ALU.mult), reads=[Zb, cb], writes=[tb])
        k.op("dve", lambda e, t=t, s0=s0: e.scalar_tensor_tensor(out=t[:, :], in0=IN[:, s0:s0 + 512], scalar=SM[:, 8:9],
                                                                 in1=t[:, :], op0=ALU.mult, op1=ALU.add),
             reads=[INb, SMb, tb], writes=[tb])
        k.op("pool", lambda e, t=t, s0=s0: e.tensor_tensor(out=t[:, :], in0=t[:, :], in1=X0[:, s0:s0 + 512], op=ALU.mult),
             reads=[tb, X0b], writes=[tb])
        k.dma("sp", [(y_d[:, s0:s0 + 512], t[:, :])], reads=[tb], sembuf=tb, sink=ob)
    k.wait_all("sp", [ob])
    k.close()
    return nc


def host_consts_LB(cq):
    f32 = np.float32
    L = S
    m = np.arange(NFFT)
    lag = np.where(m < L, m, NFFT - m)
    lag = np.where(m == L, 0, lag)
    t_all = np.linspace(0.0, 1.0, L, dtype=f32)
    t = t_all[lag]
    w = (f32(2.0 * math.pi / L) * lag.astype(f32)).astype(f32)
    fr = np.linspace(1e-4, 15, 16, dtype=f32)
    zf = np.concatenate([t[:, None], np.cos(fr[None, :] * w[:, None]), -np.sin(fr[None, :] * w[:, None])], axis=-1).astype(f32)
    zemb = np.ascontiguousarray(zf.reshape(128, 128, 33).transpose(2, 1, 0).reshape(33, NFFT))
    dmin, dmax = math.log(1e-2) / 0.3, math.log(1e-2) / 1.5
    deltas = np.abs(np.linspace(dmin, dmax, 512, dtype=f32))[cq * 128:(cq + 1) * 128]
    win = np.exp(-t[:, None] * deltas[None, :]).astype(f32)
    win[L] = 0.0
    win = np.ascontiguousarray(win.reshape(128, 128, 128))
    n = np.arange(128)
    ang = 2.0 * np.pi * np.outer(n, n) / 128.0
    fre, fim = np.cos(ang), -np.sin(ang)
    dftc = np.concatenate([fre, fim, fre, -fim], axis=1).astype(f32)
    ang2 = 2.0 * np.pi * np.outer(n, n) / NFFT
    tw = np.stack([np.cos(ang2), -np.sin(ang2)], 0).astype(f32)
    tw = np.ascontiguousarray(np.broadcast_to(tw[:, None], (2, 2, 128, 128)).transpose(2, 0, 1, 3))
    return zemb, win, dftc, tw


def prep_LB(inp, hco_all):
    maps = []
    cache = {}
    for core in range(NCORE):
        b, cq = core // 4, core % 4
        if cq not in cache:
            cache[cq] = host_consts_LB(cq)
        zemb, win, dftc, tw = cache[cq]
        hc3 = np.zeros((3, 128, S), np.float32)
        for part in range(3):
            for src in range(4):
                hc3[part, :, src * TOK:(src + 1) * TOK] = hco_all[b * 4 + src][part * 4 + cq]
        cf2 = np.zeros((128, 32), np.float32)
        cw = np.asarray(inp["d_conv_w"], np.float32)[0]
        cbias = np.asarray(inp["d_conv_b"], np.float32)[0]
        for part in range(3):
            sl = slice(part * 512 + cq * 128, part * 512 + (cq + 1) * 128)
            cf2[:, part * 4:part * 4 + 3] = cw[:, sl].T
            cf2[:, part * 4 + 3] = cbias[sl]
        cf2[:, 12] = np.asarray(inp["d_skip"], np.float32)[0][cq * 128:(cq + 1) * 128]
        mlp = np.zeros((64, 456), np.float32)
        mlp[0:33, 0:64] = np.asarray(inp["d_f_w1"], np.float32)[0]
        mlp[:, 64:128] = np.asarray(inp["d_f_w2"], np.float32)[0]
        mlp[:, 128:192] = np.asarray(inp["d_f_w3"], np.float32)[0]
        mlp[:, 192] = np.asarray(inp["d_f_b1"], np.float32)[0]
        mlp[:, 193] = np.asarray(inp["d_f_b2"], np.float32)[0]
        mlp[:, 194] = np.asarray(inp["d_f_b3"], np.float32)[0]
        mlp[:, 195] = np.asarray(inp["d_f_freq"], np.float32)[0]
        w4 = np.asarray(inp["d_f_w4"], np.float32)[0]
        mlp[:, 200:328] = w4[:, cq * 128:(cq + 1) * 128]
        mlp[:, 328:456] = w4[:, 512 + cq * 128:512 + (cq + 1) * 128]
        maps.append(dict(hc3=hc3, cf2=cf2, mlpw=mlp, zemb=zemb, win=win, dftc=dftc, tw=tw))
    return maps


def emit_attn1(P, C, QT, QTb, KT, KTb, VA, VAb, MIX, MIXb):
    k = P.k
    for jq in range(4):
        q0 = jq * 512
        for g in range(2):
            gs = slice(g * 64, (g + 1) * 64)
            ds_ = slice((1 - g) * 64, (2 - g) * 64)
            for hh in range(4):
                po, pob = P.bank("o", [4, 5])
                for kb in range(64):
                    ps, psb = P.bank("h1", [0, 1, 2, 3])
                    k.op("pe", lambda e, ps=ps, kb=kb, gs=gs, hh=hh, q0=q0: e.matmul(
                        ps[:, :], lhsT=KT[gs, kb * 128:(kb + 1) * 128], rhs=QT[gs, hh, q0:q0 + 512],
                        start=True, stop=True), reads=[KTb, QTb], writes=[psb])
                    pt, ptb = C.tmpbf()
                    k.op("act", lambda e, ps=ps, pt=pt: e.activation(out=pt[:, :], in_=ps[:, :], func=AF.Exp,
                                                                     scale=HD ** -0.5), reads=[psb], writes=[ptb])
                    k.op("pe", lambda e, po=po, pt=pt, kb=kb, g=g: e.matmul(
                        po[:, :], lhsT=VA[:, kb, g, :], rhs=pt[:, :], start=(kb == 0), stop=(kb == 63)),
                        reads=[VAb, ptb], writes=[pob])
                rd, rdb = C.tmp()
                k.op("dve", lambda e, rd=rd, po=po, ds_=ds_: e.reciprocal(out=rd[ds_, :], in_=po[ds_, :]),
                     reads=[pob], writes=[rdb])
                k.op("dve", lambda e, rd=rd, po=po, gs=gs, ds_=ds_, hh=hh, q0=q0: e.tensor_tensor(
                    out=MIX[gs, hh, q0:q0 + 512], in0=po[gs, :], in1=rd[ds_, :], op=ALU.mult),
                    reads=[pob, rdb], writes=[MIXb[jq]])


def build_LC():
    nc = bass.Bass("TRN2", target_bir_lowering=False)
    k = KB(nc)
    dt_in = lambda name, shape, dt=F32: nc.dram_tensor(name, list(shape), dt, kind="ExternalInput").ap()
    xT_d = dt_in("xT", [128, 8, 2048])
    mod_d = dt_in("modi", [128, 96])
    q_d = dt_in("q", [128, 4, 2048], BF16)
    k_d = dt_in("kk", [128, S], BF16)
    v_d = dt_in("v", [64, 128, 128], BF16)
    y_d = dt_in("yh", [128, 4, 2048])
    w1_d = dt_in("w1", [1024, DFFP])
    w3_d = dt_in("w3", [1024, DFFP])
    w2_d = dt_in("w2", [DFFP, 1024])
    wout_d = dt_in("wout", [1024, 1024])
    xo_d = nc.dram_tensor("xo", [128, 8, 2048], F32, kind="ExternalOutput").ap()

    P = Prog(nc, k)
    C = Common()
    C.tmp = RotTmp(k, "tf", 5, F32)
    C.tmpbf = RotTmp(k, "tb", 4, BF16)
    C.RS = k.sb("RS", [128, 512], F32)
    C.RSb = Buf("RS")
    C.cb = Buf("consts")
    C.modb = Buf("mod")
    C.ones_bf = k.sb("ones_bf", [128, 128], BF16)
    C.eps = k.sb("eps", [128, 1], F32)
    MOD = k.sb("mod", [128, 96], F32)
    XT = k.sb("XT", [128, 8, 2048], F32)
    AR = k.sb("AR", [128, 32768], BF16)
    MIXt = k.sb("MIX", [128, 8, 2048], BF16)
    VA = AR[:, 0:16384].rearrange("p (b g d) -> p b g d", b=64, g=2)
    KT = AR[:, 16384:24576]
    QT = AR[:, 24576:32768].rearrange("p (h t) -> p h t", h=4)
    Ht = AR[:, 0:16384].rearrange("p (c t) -> p c t", c=8)
    U = AR[:, 16384:16384 + 6 * 2048].rearrange("p (f t) -> p f t", f=6)

    tiles4 = [(0, 512), (512, 512), (1024, 512), (1536, 512)]
    XB = [Buf(f"x{j}") for j in range(4)]

    class XAct:
        tiles = tiles4
        b = XB

        def ap(self, c, j):
            return XT[:, c, j * 512:(j + 1) * 512]
    X4 = XAct()
    H4 = Act(Ht, tiles4, "h")
    Ub = [[Buf(f"u{f}_{j}") for j in range(4)] for f in range(6)]
    MIXb = [Buf(f"mix{j}") for j in range(4)]
    QTb, KTb, VAb = Buf("qt"), Buf("kt"), Buf("va")

    k.dma("sp", [(MOD[:], mod_d)], writes=[C.modb])
    k.dma("sp", [(QT, q_d)], writes=[QTb])
    k.dma("sp", [(KT, k_d)], writes=[KTb])
    k.op("dve", lambda e: e.memset(C.ones_bf[:], 1.0), writes=[C.cb])
    k.op("dve", lambda e: e.memset(C.eps[:], EPS), writes=[C.cb])
    k.op("dve", lambda e: e.memset(VA[:, :, 0, 64:128], 1.0), writes=[VAb])
    k.op("dve", lambda e: e.memset(VA[:, :, 1, 0:64], 1.0), writes=[VAb])
    vsrc = v_d.rearrange("b p d -> p b d")
    k.dma("sp", [(VA[:, :, 0, 0:64], vsrc[:, :, 0:64]), (VA[:, :, 1, 64:128], vsrc[:, :, 64:128])], writes=[VAb])
    for j in range(4):
        k.dma("sp", [(XT[:, :, j * 512:(j + 1) * 512], xT_d[:, :, j * 512:(j + 1) * 512])], writes=[XB[j]])
    for j in range(4):
        k.dma("pool", [(MIXt[:, 4:8, j * 512:(j + 1) * 512], y_d[:, :, j * 512:(j + 1) * 512])], writes=[MIXb[j]])

    def mcol(i, kind):
        return MOD[:, (i * 3 + kind) * 8:(i * 3 + kind + 1) * 8]

    emit_attn1(P, C, QT, QTb, KT, KTb, VA, VAb, MIXt, MIXb)
    emit_outproj(P, C, X4, lambda fc, c0, w: MIXt[:, fc, c0:c0 + w], MIXb, wout_d, mcol(1, 2))
    k.barrier()
    emit_adaln(P, X4, H4, MOD[:, 72 + 16:72 + 24], mcol(2, 0), C)
    emit_ffn(P, X4, H4, U, Ub, w1_d, w3_d, w2_d, mcol(2, 2), C)
    ob = Sink("out")
    for j in range(4):
        k.dma("sp", [(xo_d[:, :, j * 512:(j + 1) * 512], XT[:, :, j * 512:(j + 1) * 512])], reads=[XB[j]],
              sembuf=XB[j], sink=ob)
    k.wait_all("sp", [ob])
    k.close()
    return nc


def prep_LC(inp, la_res, lb_res, shared):
    maps = []
    for core in range(NCORE):
        b, qd = core // 4, core % 4
        kk = np.concatenate([la_res[b * 4 + s]["ko"] for s in range(4)], axis=1)
        v = np.concatenate([la_res[b * 4 + s]["vo"] for s in range(4)], axis=0)
        yh = np.stack([lb_res[b * 4 + cq]["yT"][:, qd * TOK:(qd + 1) * TOK] for cq in range(4)], axis=1)
        maps.append(dict(xT=la_res[core]["xo"], modi=la_res[core]["modo"], q=la_res[core]["qo"],
                         kk=np.ascontiguousarray(kk), v=np.ascontiguousarray(v), yh=np.ascontiguousarray(yh),
                         w1=shared["w1"][1, 1], w3=shared["w3"][1, 1], w2=shared["w2"][1, 1], wout=shared["wout"][1]))
    return maps


_CACHE = {}


FUSED = True


def kernel(**inputs):
    inp = {kk: np.asarray(v) for kk, v in inputs.items()}
    cores = list(range(NCORE))
    if FUSED:
        if "fused" not in _CACHE:
            _CACHE["fused"] = build_fused()
        maps = prep_fused(inp)
        res = run_bass_kernel_spmd(_CACHE["fused"], maps, core_ids=cores).results
        out = np.zeros((2, S, D), np.float32)
        for core in cores:
            b, qd = core // 4, core % 4
            xo = np.asarray(res[core]["xo"], np.float32)
            out[b, qd * TOK:(qd + 1) * TOK] = xo.transpose(2, 1, 0).reshape(TOK, D)
        return out
    if "la" not in _CACHE:
        _CACHE["la"] = build_LA(True)
        _CACHE["lb"] = build_LB()
        _CACHE["lc"] = build_LC()
    la_maps = prep_LA(inp)
    la = run_bass_kernel_spmd(_CACHE["la"], la_maps, core_ids=cores).results
    lb_maps = prep_LB(inp, [la[c]["hco"] for c in cores])
    lb = run_bass_kernel_spmd(_CACHE["lb"], lb_maps, core_ids=cores).results
    lc_maps = prep_LC(inp, la, lb, la_maps[0])
    lc = run_bass_kernel_spmd(_CACHE["lc"], lc_maps, core_ids=cores).results
    out = np.zeros((2, S, D), np.float32)
    for core in cores:
        b, qd = core // 4, core % 4
        xo = np.asarray(lc[core]["xo"], np.float32)
        out[b, qd * TOK:(qd + 1) * TOK] = xo.transpose(2, 1, 0).reshape(TOK, D)
    return out


U32 = mybir.dt.uint32


def kb_gather(k, dst_ap, src_dram_ap, idx_ap, reads=(), writes=()):
    sbf = writes[0]
    if sbf.dsem is None:
        sbf.dsem = {}
        sbf.dcnt = {}
    if True not in sbf.dsem:
        sbf.dsem[True] = ("d", id(sbf), True)
        sbf.dcnt[True] = 0
        k.sems[sbf.dsem[True]] = k._newsem(f"d_{k.nsem}")
    k._wait("pool", k._deps(reads, writes))
    k.nc.gpsimd.indirect_dma_start(out=dst_ap, out_offset=None, in_=src_dram_ap,
                                   in_offset=bass.IndirectOffsetOnAxis(ap=idx_ap, axis=0)
                                   ).then_inc(k.sems[sbf.dsem[True]], 16)
    sbf.dcnt[True] += 16
    tok = (sbf.dsem[True], sbf.dcnt[True])
    k._commit(tok, reads, writes)
    return tok


class CC:
    n = 0

    def __init__(self, k, groups, fake):
        self.k, self.groups, self.fake = k, groups, fake
        self.pool_sems = [] if fake else [k._newsem(f"cc{i}") for i in range(16)]
        self.alltoks = {}

    def allgather(self, src_t, dst_t, reads, dstb, sinks=(), sl=None):
        k = self.k
        sap = src_t.ap() if sl is None else src_t.ap()[sl]
        dap = dst_t.ap() if sl is None else dst_t.ap()[sl]
        rows = sap.shape[0]
        k.wait_all("sp" if self.fake else "pool", list(sinks))
        if self.fake:
            pairs = [(dap[r * rows:(r + 1) * rows, :], sap) for r in range(4)]
            k.dma("sp", pairs, reads=reads, writes=[dstb])
            return
        CC.n += 1
        key = ("cc", CC.n)
        k.sems[key] = self.pool_sems.pop(0)
        k._wait("pool", k._deps(reads, [dstb]))
        k.nc.gpsimd.collective_compute("AllGather", ALU.bypass, replica_groups=self.groups,
                                       ins=[sap.opt()], outs=[dap.opt()]).then_inc(k.sems[key])
        k._commit((key, 1), reads, [dstb])
        self.alltoks.setdefault(id(dstb), Sink("cc")).toks[key] = 1

    def sink(self, dstb):
        return self.alltoks.get(id(dstb), Sink("none"))


def emit_filter(P, C, F, kc_s, kcsb):
    k = P.k
    tmp, MLP, SM = C.tmp, F.MLP, F.SM
    cb = C.cb
    W1, W2, W3 = MLP[0:33, 0:64], MLP[:, 64:128], MLP[:, 128:192]
    W4 = MLP[:, 200:456].rearrange("p (d c) -> p d c", d=2)
    SMb, ACC, ACCb = F.SMb, F.ACC, F.ACCb
    k.op("dve", lambda e: e.memset(SM[:, 0:1], math.pi / 2), writes=[SMb])
    k.op("dve", lambda e: e.memset(SM[:, 1:2], EPS), writes=[SMb])
    k.op("dve", lambda e: e.tensor_scalar(out=SM[0:64, 2:3], in0=MLP[:, 195:196], scalar1=0.25, scalar2=None, op0=ALU.mult),
         reads=[cb], writes=[SMb])
    for i in range(3):
        k.op("dve", lambda e, i=i: e.tensor_tensor(out=SM[0:64, 3 + i:4 + i], in0=MLP[:, 192 + i:193 + i], in1=SM[0:64, 2:3],
                                                   op=ALU.mult), reads=[cb, SMb], writes=[SMb])
    k.op("dve", lambda e: e.memset(ACC[:], 0.0), writes=[ACCb])

    def sin4(ps, psb, li):
        s1, s1b = tmp()
        k.op("act", lambda e: e.activation(out=s1[0:64, :], in_=ps[0:64, :], func=AF.Sin, bias=SM[0:64, 3 + li:4 + li],
                                           scale=SM[0:64, 2:3]), reads=[psb, SMb], writes=[s1b])
        a1, a1b = tmp()
        k.op("act", lambda e: e.activation(out=a1[0:64, :], in_=ps[0:64, :], func=AF.Abs, bias=SM[0:64, 3 + li:4 + li],
                                           scale=SM[0:64, 2:3]), reads=[psb, SMb], writes=[a1b])
        k.op("act", lambda e: e.activation(out=a1[0:64, :], in_=a1[0:64, :], func=AF.Sin, bias=SM[0:64, 0:1], scale=-1.0),
             reads=[a1b, SMb], writes=[a1b])
        k.op("dve", lambda e: e.tensor_tensor(out=a1[0:64, :], in0=a1[0:64, :], in1=s1[0:64, :], op=ALU.mult),
             reads=[a1b, s1b], writes=[a1b])
        k.op("dve", lambda e: e.tensor_tensor(out=s1[0:64, :], in0=s1[0:64, :], in1=s1[0:64, :], op=ALU.mult),
             reads=[s1b], writes=[s1b])
        k.op("dve", lambda e: e.tensor_scalar(out=s1[0:64, :], in0=s1[0:64, :], scalar1=-2.0, scalar2=1.0,
                                              op0=ALU.mult, op1=ALU.add), reads=[s1b], writes=[s1b])
        k.op("dve", lambda e: e.scalar_tensor_tensor(out=a1[0:64, :], in0=a1[0:64, :], scalar=4.0, in1=s1[0:64, :],
                                                     op0=ALU.mult, op1=ALU.mult), reads=[a1b, s1b], writes=[a1b])
        return a1, a1b

    for c in range(32):
        yield c
        ze, zeb = tmp()
        k.dma("sp", [(ze[0:33, :], F.zemb_d[:, c * 512:(c + 1) * 512])], writes=[zeb])
        wt, wtb = tmp()
        k.dma("sp", [(wt[:, :].rearrange("p (n c) -> p n c", n=4), F.win_d[:, c * 4:(c + 1) * 4, :])], writes=[wtb])
        ps, psb = P.bank("m", [6, 7])
        k.op("pe", lambda e, ps=ps, ze=ze: e.matmul(ps[0:64, :], lhsT=W1, rhs=ze[0:33, :], start=True, stop=True),
             reads=[zeb, cb], writes=[psb])
        h, hb = sin4(ps, psb, 0)
        for li, W in ((1, W2), (2, W3)):
            ps, psb = P.bank("m", [6, 7])
            k.op("pe", lambda e, ps=ps, h=h, W=W: e.matmul(ps[0:64, :], lhsT=W, rhs=h[0:64, :], start=True, stop=True),
                 reads=[hb, cb], writes=[psb])
            h, hb = sin4(ps, psb, li)
        ps4, ps4b = P.bank("m", [6, 7])
        for n2l in range(4):
            for d in range(2):
                k.op("pe", lambda e, ps4=ps4, h=h, n2l=n2l, d=d: e.matmul(
                    ps4[d * 64:(d + 1) * 64, n2l * 128:(n2l + 1) * 128],
                    lhsT=h[0:64, n2l * 128 + d * 64:n2l * 128 + (d + 1) * 64], rhs=W4[:, d, :],
                    start=True, stop=True, skip_group_check=True), reads=[hb, cb], writes=[ps4b])
        kc, kcb = tmp()
        k.op("dve", lambda e, kc=kc, ps4=ps4, wt=wt: e.tensor_tensor(out=kc[:, :], in0=ps4[:, :], in1=wt[:, :], op=ALU.mult),
             reads=[ps4b, wtb], writes=[kcb])
        st, stb = C.tmpbf()
        k.op("act", lambda e, kc=kc, st=st: e.activation(out=st[:, :], in_=kc[:, :], func=AF.Copy), reads=[kcb], writes=[stb])
        k.dma("sp", [(kc_s.ap()[:, c * 4:(c + 1) * 4, :], st[:, :].rearrange("p (n c) -> p n c", n=4))], reads=[stb],
              sembuf=stb, sink=kcsb)
        sq, sqb = tmp()
        k.op("dve", lambda e, kc=kc, sq=sq: e.tensor_tensor(out=sq[:, :], in0=kc[:, :], in1=kc[:, :], op=ALU.mult),
             reads=[kcb], writes=[sqb])
        rd, rdb = tmp()
        k.op("dve", lambda e, sq=sq, rd=rd: e.tensor_reduce(
            out=rd[:, 0:128], in_=sq[:, :].rearrange("p (n c) -> p c n", n=4), axis=mybir.AxisListType.X, op=ALU.add),
            reads=[sqb], writes=[rdb])
        k.op("dve", lambda e, rd=rd: e.tensor_tensor(out=ACC[:], in0=ACC[:], in1=rd[:, 0:128], op=ALU.add),
             reads=[rdb, ACCb], writes=[ACCb])
    ps, psb = P.bank("m", [6, 7])
    k.op("pe", lambda e, ps=ps: e.matmul(ps[:, 0:128], lhsT=C.ones_f[:, :], rhs=ACC[:], start=True, stop=True),
         reads=[ACCb, cb], writes=[psb])
    k.op("act", lambda e, ps=ps: e.activation(out=F.RSrow[:], in_=ps[:, 0:128], func=AF.Sqrt, bias=SM[:, 1:2], scale=1.0),
         reads=[psb, SMb], writes=[F.RSb])
    k.op("dve", lambda e: e.reciprocal(out=F.RSrow[:], in_=F.RSrow[:]), reads=[F.RSb], writes=[F.RSb])


def emit_fftconv(P, C, F, KC, KCb, zs2, zs2b, c_src, csink, V):
    k = P.k
    tmp = C.tmp
    DFT, TW, dftb, cb = F.DFT, F.TW, F.dftb, C.cb
    Fre, Fim, nFim = DFT[:, 0:128], DFT[:, 128:256], DFT[:, 384:512]
    Fcat, FcatI2, FcatI1 = DFT[:, 0:256], DFT[:, 128:384], DFT[:, 256:512]
    TWre2, TWim2 = TW[:, 0], TW[:, 1]
    Bre, Bim, Yre, Yim, Qre, Qim, Hre, Him = V.Bre, V.Bim, V.Yre, V.Yim, V.Qre, V.Qim, V.Hre, V.Him
    Bb, Hb, Yb, Qb = Buf("B"), Buf("H"), Buf("Y"), Buf("Q")
    PS, PSb = P.PS, P.PSb

    def cmul_from_psum(A, Ab, sign, outre, outim, outb, pr):
        A4 = A[:, :].rearrange("p (c r k) -> p c r k", c=2, r=2)
        Are, Aim = A4[:, :, 0, :], A4[:, :, 1, :]
        v = lambda t: t[:, 0:256].rearrange("p (c k) -> p c k", c=2)
        t1, t1b = tmp()
        t2, t2b = tmp()
        k.op("dve", lambda e: e.tensor_tensor(out=v(t1), in0=Are, in1=TWre2, op=ALU.mult), reads=[Ab, cb], writes=[t1b])
        k.op("dve", lambda e: e.tensor_tensor(out=v(t2), in0=Aim, in1=TWim2, op=ALU.mult), reads=[Ab, cb], writes=[t2b])
        k.op("pool", lambda e: e.tensor_tensor(out=outre[:, pr * 2:pr * 2 + 2, :], in0=v(t1), in1=v(t2),
                                               op=(ALU.subtract if sign > 0 else ALU.add)),
             reads=[t1b, t2b], writes=[outb])
        t3, t3b = tmp()
        t4, t4b = tmp()
        k.op("dve", lambda e: e.tensor_tensor(out=v(t3), in0=Aim, in1=TWre2, op=ALU.mult), reads=[Ab, cb], writes=[t3b])
        k.op("dve", lambda e: e.tensor_tensor(out=v(t4), in0=Are, in1=TWim2, op=ALU.mult), reads=[Ab, cb], writes=[t4b])
        k.op("pool", lambda e: e.tensor_tensor(out=outim[:, pr * 2:pr * 2 + 2, :], in0=v(t3), in1=v(t4),
                                               op=(ALU.add if sign > 0 else ALU.subtract)),
             reads=[t3b, t4b], writes=[outb])

    def fft_fwd(lhs_of, srcb, krows, xre, xreb, xim, ximb):
        for pr in range(2):
            A, Ab = P.bank("A", [0, 1])
            for cl in range(2):
                c4 = pr * 2 + cl
                k.op("pe", lambda e, A=A, cl=cl, c4=c4: e.matmul(
                    A[:, cl * 256:(cl + 1) * 256], lhsT=lhs_of(c4), rhs=Fcat[0:krows, :],
                    start=True, stop=True, skip_group_check=True), reads=[srcb, dftb], writes=[Ab])
            cmul_from_psum(A, Ab, +1, Bre, Bim, Bb, pr)
        bre = Bre.rearrange("p c k -> p (c k)")
        bim = Bim.rearrange("p c k -> p (c k)")
        k.op("pe", lambda e: e.matmul(xre[:, :], lhsT=Fre, rhs=bre, start=True, stop=False), reads=[Bb, dftb], writes=[xreb])
        k.op("pe", lambda e: e.matmul(xre[:, :], lhsT=nFim, rhs=bim, start=False, stop=True), reads=[Bb, dftb], writes=[xreb])
        k.op("pe", lambda e: e.matmul(xim[:, :], lhsT=Fim, rhs=bre, start=True, stop=False), reads=[Bb, dftb], writes=[ximb])
        k.op("pe", lambda e: e.matmul(xim[:, :], lhsT=Fre, rhs=bim, start=False, stop=True), reads=[Bb, dftb], writes=[ximb])

    for g in range(32):
        ch0 = g * 4
        ZC, ZCb = V.ZC[g % 2], V.ZCb[g % 2]
        src = bass.AP(tensor=zs2, offset=ch0 * S, ap=[[128, 64], [S, 4], [1, 128]])
        k.dma("sp", [(ZC[0:64], src)], reads=[zs2b], writes=[ZCb])
        fft_fwd(lambda c4, ch0=ch0: KC[:, :, ch0 + c4], KCb, 128, PS[2], PSb[2], PS[3], PSb[3])
        rsb_ap = F.RSrow[:, ch0:ch0 + 4].unsqueeze(2).broadcast_to([128, 4, 128])
        k.op("dve", lambda e, rsb_ap=rsb_ap: e.tensor_tensor(out=Hre.rearrange("p (c k) -> p c k", c=4),
                                                              in0=PS[2][:, :].rearrange("p (c k) -> p c k", c=4),
                                                              in1=rsb_ap, op=ALU.mult), reads=[PSb[2], F.RSb], writes=[Hb])
        k.op("dve", lambda e, rsb_ap=rsb_ap: e.tensor_tensor(out=Him.rearrange("p (c k) -> p c k", c=4),
                                                              in0=PS[3][:, :].rearrange("p (c k) -> p c k", c=4),
                                                              in1=rsb_ap, op=ALU.mult), reads=[PSb[3], F.RSb], writes=[Hb])
        fft_fwd(lambda c4, ZC=ZC: ZC[0:64, c4, :], ZCb, 64, PS[4], PSb[4], PS[5], PSb[5])
        t1, t1b = tmp()
        t2, t2b = tmp()
        k.op("dve", lambda e, t1=t1: e.tensor_tensor(out=t1[:, :], in0=PS[4][:, :], in1=Hre, op=ALU.mult),
             reads=[PSb[4], Hb], writes=[t1b])
        k.op("dve", lambda e, t2=t2: e.tensor_tensor(out=t2[:, :], in0=PS[5][:, :], in1=Him, op=ALU.mult),
             reads=[PSb[5], Hb], writes=[t2b])
        k.op("pool", lambda e, t1=t1, t2=t2: e.tensor_tensor(out=Yre.rearrange("p c k -> p (c k)"), in0=t1[:, :],
                                                             in1=t2[:, :], op=ALU.subtract), reads=[t1b, t2b], writes=[Yb])
        t3, t3b = tmp()
        t4, t4b = tmp()
        k.op("dve", lambda e, t3=t3: e.tensor_tensor(out=t3[:, :], in0=PS[4][:, :], in1=Him, op=ALU.mult),
             reads=[PSb[4], Hb], writes=[t3b])
        k.op("dve", lambda e, t4=t4: e.tensor_tensor(out=t4[:, :], in0=PS[5][:, :], in1=Hre, op=ALU.mult),
             reads=[PSb[5], Hb], writes=[t4b])
        k.op("pool", lambda e, t3=t3, t4=t4: e.tensor_tensor(out=Yim.rearrange("p c k -> p (c k)"), in0=t3[:, :],
                                                             in1=t4[:, :], op=ALU.add), reads=[t3b, t4b], writes=[Yb])
        for pr in range(2):
            Pk, Pkb = P.bank("A", [0, 1])
            for cl in range(2):
                c4 = pr * 2 + cl
                k.op("pe", lambda e, Pk=Pk, cl=cl, c4=c4: e.matmul(
                    Pk[:, cl * 256:(cl + 1) * 256], lhsT=Yre[:, c4, :], rhs=FcatI1, start=True, stop=False,
                    skip_group_check=True), reads=[Yb, dftb], writes=[Pkb])
                k.op("pe", lambda e, Pk=Pk, cl=cl, c4=c4: e.matmul(
                    Pk[:, cl * 256:(cl + 1) * 256], lhsT=Yim[:, c4, :], rhs=FcatI2, start=False, stop=True,
                    skip_group_check=True), reads=[Yb, dftb], writes=[Pkb])
            cmul_from_psum(Pk, Pkb, -1, Qre, Qim, Qb, pr)
        yo, yob = PS[6], PSb[6]
        k.op("pe", lambda e: e.matmul(yo[0:64, :], lhsT=DFT[:, 0:64], rhs=Qre.rearrange("p c k -> p (c k)"),
                                      start=True, stop=False), reads=[Qb, dftb], writes=[yob])
        k.op("pe", lambda e: e.matmul(yo[0:64, :], lhsT=DFT[:, 128:192], rhs=Qim.rearrange("p c k -> p (c k)"),
                                      start=False, stop=True), reads=[Qb, dftb], writes=[yob])
        ys, ysb = tmp()
        k.op("act", lambda e, ys=ys: e.activation(out=ys[0:64, :], in_=yo[0:64, :], func=AF.Copy, scale=1.0 / NFFT),
             reads=[yob], writes=[ysb])
        dst = bass.AP(tensor=c_src, offset=ch0 * S, ap=[[128, 64], [S, 4], [1, 128]])
        k.dma("sp", [(dst, ys[0:64, :].rearrange("p (c k) -> p c k", c=4))], reads=[ysb], sembuf=ysb, sink=csink)


def emit_attn1f(P, C, q_s, KT, KTb, VB, VBb, QS, QSb, MIX, MIXb, SK=3):
    k = P.k
    its = [(jq, g, hh, kb) for jq in range(4) for g in range(2) for hh in range(4) for kb in range(64)]
    st = {}
    cur = {}

    def stage_a(i):
        jq, g, hh, kb = its[i]
        q0 = jq * 512
        QT, QTb = QS[jq % 2], QSb[jq % 2]
        if (g, hh, kb) == (0, 0, 0):
            k.dma("sp", [(QT, q_s.ap()[:, :, q0:q0 + 512])], writes=[QTb])
        gs = slice(g * 64, (g + 1) * 64)
        ps, psb = P.bank("att", [0, 1, 2, 3, 6, 7])
        k.op("pe", lambda e: e.matmul(ps[:, :], lhsT=KT[gs, kb * 128:(kb + 1) * 128], rhs=QT[gs, hh, :],
                                      start=True, stop=True), reads=[KTb, QTb], writes=[psb])
        pt, ptb = C.tmpbf()
        k.op("act", lambda e: e.activation(out=pt[:, :], in_=ps[:, :], func=AF.Exp, scale=HD ** -0.5),
             reads=[psb], writes=[ptb])
        st[i] = (pt, ptb)

    def stage_b(i):
        jq, g, hh, kb = its[i]
        q0 = jq * 512
        gs = slice(g * 64, (g + 1) * 64)
        ds_ = slice((1 - g) * 64, (2 - g) * 64)
        if kb == 0:
            cur["po"] = P.bank("o", [4, 5])
        po, pob = cur["po"]
        pt, ptb = st.pop(i)
        k.op("pe", lambda e: e.matmul(po[:, :], lhsT=VB[:, kb, g * 64:g * 64 + 128], rhs=pt[:, :],
                                      start=(kb == 0), stop=(kb == 63)), reads=[VBb, ptb], writes=[pob])
        if kb == 63:
            rd, rdb = C.tmp()
            k.op("dve", lambda e: e.reciprocal(out=rd[ds_, :], in_=po[ds_, :]), reads=[pob], writes=[rdb])
            k.op("dve", lambda e: e.tensor_tensor(out=MIX[gs, hh, q0:q0 + 512], in0=po[gs, :], in1=rd[ds_, :],
                                                  op=ALU.mult), reads=[pob, rdb], writes=[MIXb[jq]])

    n = len(its)
    for t in range(n + SK):
        if t < n:
            stage_a(t)
        if t - SK >= 0:
            stage_b(t - SK)


def build_fused(fake_ag=False, ncore=8, stop_after=None):
    nc = bass.Bass("TRN2", target_bir_lowering=False)
    k = KB(nc)
    groups = [[0, 1, 2, 3], [4, 5, 6, 7]] if ncore == 8 else [[0, 1, 2, 3]]
    cc = CC(k, groups, fake_ag)
    dt_in = lambda name, shape, dt=F32: nc.dram_tensor(name, list(shape), dt, kind="ExternalInput").ap()
    dram = lambda name, shape, dt=F32: nc.dram_tensor(name, list(shape), dt, kind="Internal")
    xT_d = dt_in("xT", [128, 8, 2048])
    xH_d = dt_in("xH", [128, 8, 256])
    cf_d = dt_in("cf", [128, 400])
    oh_d = dt_in("oh", [32, 512])
    vm_d = dt_in("vm", [128, 512])
    rope_d = dt_in("rope", [128, 2, 2048])
    idx_d = dt_in("idx", [128, 24], U32)
    mlp_d = dt_in("mlpw", [64, 456])
    zemb_d = dt_in("zemb", [33, NFFT])
    win_d = dt_in("win", [128, 128, 128])
    dft_d = dt_in("dftc", [128, 512])
    tw_d = dt_in("tw", [128, 2, 2, 128])
    wmod_d = dt_in("wmod", [2, 1024, 9216])
    w1_d = dt_in("w1", [2, 2, 1024, DFFP])
    w3_d = dt_in("w3", [2, 2, 1024, DFFP])
    w2_d = dt_in("w2", [2, 2, DFFP, 1024])
    win_w = dt_in("win_w", [2, 1024, INC])
    wout_d = dt_in("wout", [2, 1024, 1024])
    xo_d = nc.dram_tensor("xo", [128, 8, 2048], F32, kind="ExternalOutput").ap()
    scr = dram("scr", [128, 8 * 512])
    q_s = dram("q_s", [128, 4, 2048], BF16)
    k_src, k_g = dram("k_src", [128, 2048], BF16), dram("k_g", [512, 2048], BF16)
    v_src, v_g = dram("v_src", [2048, 128], BF16), dram("v_g", [8192, 128], BF16)
    hb_src, hb_g = dram("hb_src", [128, 24]), dram("hb_g", [512, 24])
    z_src, z_g = dram("z_src", [4, 128, 2048], BF16), dram("z_g", [4, 512, 2048], BF16)
    zf_s = dram("zf_s", [128, 4, 2048])
    x0_s = dram("x0_s", [128, 4, 2048])
    zs2 = dram("zs2", [128, S], BF16)
    kc_s = dram("kc_s", [128, 128, 128], BF16)
    c_src, c_g = dram("c_src", [8, 16, S]), dram("c_g", [8, 64, S])

    P = Prog(nc, k)
    C = Common()
    C.tmp = RotTmp(k, "tf", 8, F32)
    C.tmpbf = RotTmp(k, "tb", 5, BF16)
    C.RS = k.sb("RS", [128, 512], F32)
    C.RSb = Buf("RS")
    C.cb = Buf("consts")
    C.modb = Buf("mod")
    CF = k.sb("CF", [128, 400], F32)
    IDX = k.sb("IDX", [128, 24], U32)
    C.ones_bf = k.sb("ones_bf", [128, 128], BF16)
    C.bd_bf = k.sb("bd_bf", [128, 128], BF16)
    C.ones_f = k.sb("ones_f", [128, 128], F32)
    C.eps = k.sb("eps", [128, 1], F32)
    condbf = k.sb("condbf", [128, 8], BF16)
    MODV = k.sb("modv", [128, 2, 72], F32)
    AV = k.sb("av", [128, 2, 3, 8], F32)
    EXS = k.sb("exs", [128, 8], F32)
    F = Common()
    F.MLP = k.sb("MLP", [64, 456], F32)
    F.DFT = k.sb("DFT", [128, 512], BF16)
    F.TW = k.sb("TW", [128, 2, 2, 128], F32)
    F.ACC = k.sb("ACC", [128, 128], F32)
    F.SM = k.sb("SM", [128, 16], F32)
    F.RSrow = k.sb("RSrow", [128, 128], F32)
    F.SMb, F.ACCb, F.RSb, F.dftb = Buf("SM"), Buf("ACC"), Buf("RSr"), Buf("dft")
    F.zemb_d, F.win_d = zemb_d, win_d
    HBS = k.sb("HBS", [128, 12, 2], F32)
    HG = k.sb("HG", [128, 4, 12, 2], F32)
    HLR = k.sb("HLR", [128, 2, 12], F32)
    XT = k.sb("XT", [128, 8, 2048], F32)
    AR = k.sb("AR", [128, 41984], BF16)

    cT = CF[:, 0:8]
    flags = CF[:, 8:10]
    normg = CF[:, 10:58].rearrange("p (l i c) -> p l i c", l=2, i=3)
    bmod = CF[:, 58:202].rearrange("p (l m) -> p l m", l=2)
    convw = CF[:, 202:214].rearrange("p (c t) -> p c t", c=4)
    convb = CF[:, 214:218]
    gq, gk = CF[:, 218:220], CF[:, 220:222]
    sink = CF[:, 222:230]
    relrep = CF[:, 232:240]
    gq1, gk1 = CF[:, 240:241], CF[:, 241:242]
    dcw = CF[:, 244:280].rearrange("p (c t) -> p c t", c=12)
    dcb = CF[:, 280:292]
    skipv = CF[:, 292:296]
    selL, selR = CF[:, 296:300], CF[:, 300:304]

    Ht = AR[:, 0:18432].rearrange("p (c t) -> p c t", c=8)
    UQ = AR[:, 18432:33792]
    MXf = AR[:, 33792:41984].bitcast(F32)
    XH = MXf[:, 0:2048].rearrange("p (c t) -> p c t", c=8)
    MIXHI0 = AR[:, 33792:41984].rearrange("p (c t) -> p c t", c=4)
    U = UQ[:, 0:6 * 2304].rearrange("p (f t) -> p f t", f=6)
    QT0 = UQ[:, 0:8192].rearrange("p (h t) -> p h t", h=4)
    KT0 = UQ[:, 8192:8192 + 2304]
    VA0 = UQ[:, 10496:10496 + 4608].rearrange("p (b g d) -> p b g d", b=18, g=2)
    S_t = UQ[:, 10496:10496 + 4100].bitcast(F32)
    C.E8 = UQ[:, 0:8192].bitcast(F32).rearrange("p (h m) -> p h m", h=8)
    EXPB = P.WBf[:, :, :].rearrange("p a (b q) -> p (a b) q", q=128).rearrange("p (h b) q -> p h b q", h=8)
    OH = AR[0:32, 0:1024].bitcast(F32)
    VM = AR[:, 1024:2048].bitcast(F32)

    tiles5 = [(0, 512), (512, 512), (1024, 512), (1536, 512), (2048, 256)]
    tiles4 = tiles5[:4]
    XB = [Buf(f"x{j}") for j in range(5)]

    class XAct:
        def __init__(self, tiles):
            self.tiles = tiles
            self.b = XB[:len(tiles)]

        def ap(self, c, j):
            if j < 4:
                return XT[:, c, j * 512:(j + 1) * 512]
            return XH[:, c, :]
    X5, X4 = XAct(tiles5), XAct(tiles4)
    H5 = Act(Ht, tiles5, "h")
    H4 = Act(Ht, tiles4, "h")
    H4.b = H5.b[:4]
    Ub = [[Buf(f"u{f}_{j}") for j in range(5)] for f in range(6)]

    k.dma("sp", [(CF[:], cf_d), (F.MLP[:], mlp_d), (F.TW[:], tw_d), (IDX[:], idx_d)], writes=[C.cb])
    k.dma("pool", [(F.DFT[:], dft_d)], writes=[F.dftb])
    for j in range(4):
        k.dma("sp", [(XT[:, :, j * 512:(j + 1) * 512], xT_d[:, :, j * 512:(j + 1) * 512])], writes=[XB[j]])
    k.dma("sp", [(XH, xH_d)], writes=[XB[4]])
    cst = Buf("cst")
    k.op("dve", lambda e: e.memset(C.ones_bf[:], 1.0), writes=[cst])
    k.op("dve", lambda e: e.memset(C.ones_f[:], 1.0), writes=[cst])
    k.op("dve", lambda e: e.memset(C.eps[:], EPS), writes=[cst])
    k.op("dve", lambda e: e.memset(C.bd_bf[:], 0.0), writes=[cst])
    k.op("dve", lambda e: e.memset(C.bd_bf[0:64, 0:64], 1.0), writes=[cst])
    k.op("dve", lambda e: e.memset(C.bd_bf[64:128, 64:128], 1.0), writes=[cst])
    condb = Buf("cond")
    k.op("act", lambda e: e.activation(out=condbf[:], in_=cT, func=AF.Silu), reads=[C.cb], writes=[condb])
    k.op("act", lambda e: e.activation(out=EXS[:], in_=sink, func=AF.Exp), reads=[C.cb], writes=[cst])
    k.op("dve", lambda e: e.tensor_copy(out=C.eps[:], in_=C.eps[:]), reads=[cst, C.cb], writes=[C.cb])

    def layer_mod(l):
        emit_mod(P, condbf, condb, wmod_d[l], bmod[:, l, :], MODV[:, l, :], C.modb, AV[:, l], normg[:, l])

    def mcol(l, i, kind):
        return MODV[:, l, (i * 3 + kind) * 8:(i * 3 + kind + 1) * 8]

    def mix_ap0(fc, c0, w):
        if fc < 4:
            return Ht[:, fc, c0:c0 + w]
        return MIXHI0[:, fc - 4, c0:c0 + w]

    def finish(extra=()):
        ob = Sink("out")
        k.wait_all("sp", list(extra))
        for j in range(4):
            k.dma("sp", [(xo_d[:, :, j * 512:(j + 1) * 512], XT[:, :, j * 512:(j + 1) * 512])], reads=[XB[j]],
                  sembuf=XB[j], sink=ob)
        k.wait_all("sp", [ob])
        k.close()
        return nc

    kcsb = Sink("kcs")
    fgen = emit_filter(P, C, F, kc_s, kcsb)
    next(fgen)
    hcnt = [0]

    def fhook():
        hcnt[0] += 1
        if hcnt[0] % 3 == 0:
            next(fgen, None)

    layer_mod(0)
    emit_adaln(P, X5, H5, AV[:, 0, 0], mcol(0, 0, 0), C)
    emit_ffn(P, X5, H5, U, Ub, w1_d[0, 0], w3_d[0, 0], w2_d[0, 0], mcol(0, 0, 2), C, hook=fhook)
    for _ in fgen:
        pass
    k.barrier()
    ohb = Buf("oh")
    k.dma("sp", [(OH, oh_d), (VM, vm_d)], writes=[ohb])
    k.op("dve", lambda e: e.tensor_copy(out=C.eps[:], in_=C.eps[:]), reads=[ohb, C.cb], writes=[C.cb])
    scrb = emit_expb(P, C, relrep, C.cb, OH, VM, scr, EXPB, P.WBb[0])
    k.barrier(dbufs=[scrb])
    emit_adaln(P, X5, H5, AV[:, 0, 1], mcol(0, 1, 0), C)
    k.barrier()
    MIXb = [Buf(f"mix{j}") for j in range(4)]
    Sb = Buf("S")
    emit_conv0(P, C, H5, mix_ap0, MIXb, win_w[0], convw, convb, flags, S_t, Sb)
    k.barrier()
    QTb = [Buf(f"qt{j}") for j in range(4)]
    KTb = [Buf(f"kt{j}") for j in range(5)]
    VAb = Buf("va")
    k.op("dve", lambda e: e.memset(VA0[:, 0:16, 0, 64:128], 1.0), writes=[VAb])
    k.op("dve", lambda e: e.memset(VA0[:, 0:16, 1, 0:64], 1.0), writes=[VAb])
    emit_qkv0(P, C, H5, win_w[0], QT0, QTb, KT0, KTb, VA0, VAb, gq, gk, flags)
    k.barrier()
    emit_attn0(P, C, QT0, QTb, KT0, KTb, VA0, VAb, EXPB, P.WBb[0], EXS, Ht, MIXb)
    for i in (1, 2):
        P.WBb[i].r = dict(P.WBb[0].r)
        P.WBb[i].w = P.WBb[0].w
    emit_outproj(P, C, X4, mix_ap0, MIXb, wout_d[0], mcol(0, 1, 2))
    k.barrier()
    emit_adaln(P, X4, H4, AV[:, 0, 2], mcol(0, 2, 0), C)
    emit_ffn(P, X4, H4, U, Ub, w1_d[0, 1], w3_d[0, 1], w2_d[0, 1], mcol(0, 2, 2), C)

    if stop_after == "l0":
        return finish([kcsb])

    layer_mod(1)
    emit_adaln(P, X4, H4, AV[:, 1, 0], mcol(1, 0, 0), C)
    emit_ffn(P, X4, H4, U, Ub, w1_d[1, 0], w3_d[1, 0], w2_d[1, 0], mcol(1, 0, 2), C)
    k.barrier()
    ROPE = UQ[:, 0:8192].bitcast(F32).rearrange("p (a t) -> p a t", a=2)
    HB = AR[:, 26624:26624 + 12300].bitcast(F32).rearrange("p (a t) -> p a t", a=3)
    ropeb, HBb = Buf("rope"), Buf("HB")
    k.dma("sp", [(ROPE, rope_d)], writes=[ropeb])
    emit_adaln(P, X4, H4, AV[:, 1, 1], mcol(1, 1, 0), C)
    W1L = win_w[1]
    hbsb = Buf("hbs")
    for pr in range(6):
        wa, wab = P.load_wa(W1L[:, 768 + pr * 256:768 + (pr + 1) * 256])
        for cl in range(2):
            ch = pr * 2 + cl
            ps, psb = P.bank("h1", [0, 1, 2, 3])
            for ci, col in enumerate((0, 2047)):
                for c in range(8):
                    k.op("pe", lambda e, c=c, ps=ps, wa=wa, cl=cl, ci=ci, col=col: e.matmul(
                        ps[:, ci:ci + 1], lhsT=wa[:, c, cl * 128:(cl + 1) * 128], rhs=Ht[:, c, col:col + 1],
                        start=(c == 0), stop=(c == 7)), reads=[wab] + H4.b, writes=[psb])
            k.op("act", lambda e, ps=ps, ch=ch: e.activation(out=HBS[:, ch, :], in_=ps[:, 0:2], func=AF.Copy),
                 reads=[psb], writes=[hbsb])
    hbsrcb, hbgb, hgb = Buf("hbsrc"), Buf("hbg"), Buf("hg")
    k.dma("sp", [(hb_src.ap(), HBS[:].rearrange("p c t -> p (c t)"))], reads=[hbsb], writes=[hbsrcb])
    cc.allgather(hb_src, hb_g, [hbsrcb], hbgb)
    k.dma("sp", [(HG[:].rearrange("p r c t -> p r (c t)"), hb_g.ap().rearrange("(r p) t -> p r t", p=128))],
          reads=[hbgb], writes=[hgb])
    outs = Sink("l1outs")
    for pr in range(3):
        wa, wab = P.load_wa(W1L[:, pr * 256:(pr + 1) * 256])
        for cl in range(2):
            if pr == 2 and cl == 1:
                break
            for j in range(4):
                c0, w = H4.tiles[j]
                ps, psb = P.bank("h1", [0, 1, 2, 3])
                emit_inproj_tile(P, H4, j, wa, wab, cl, ps, psb)
                qn, qnb = C.tmp()
                emit_qknorm(P, C, ps, psb, w, (gq1 if pr < 2 else gk1), qn[:, 0:w], qnb)
                st, stb = C.tmpbf()
                emit_rope(P, C, qn, qnb, ROPE[:, 0, :], ROPE[:, 1, :], ropeb, c0, w, st[:, 0:w], stb)
                dst = q_s.ap()[:, pr * 2 + cl, c0:c0 + w] if pr < 2 else k_src.ap()[:, c0:c0 + w]
                k.dma("sp", [(dst, st[:, 0:w])], reads=[stb], sembuf=stb, sink=outs)
        if pr == 2:
            for tb in range(16):
                j = tb // 4
                ps, psb = P.bank("o", [4, 5])
                for c in range(8):
                    k.op("pe", lambda e, c=c, tb=tb, ps=ps, wa=wa: e.matmul(
                        ps[:, 0:128], lhsT=Ht[:, c, tb * 128:(tb + 1) * 128], rhs=wa[:, c, 128:256],
                        start=(c == 0), stop=(c == 7)), reads=[wab, H4.b[j]], writes=[psb])
                st, stb = C.tmpbf()
                k.op("act", lambda e, ps=ps, st=st: e.activation(out=st[:, 0:128], in_=ps[:, 0:128], func=AF.Copy),
                     reads=[psb], writes=[stb])
                k.dma("sp", [(v_src.ap()[tb * 128:(tb + 1) * 128, :], st[:, 0:128])], reads=[stb], sembuf=stb, sink=outs)
    kgb, vgb = Buf("kg"), Buf("vg")
    if stop_after == "qkv":
        return finish([outs, kcsb])
    cc.allgather(k_src, k_g, [], kgb, sinks=[outs])
    cc.allgather(v_src, v_g, [], vgb, sinks=[outs])
    for side, sel, colx in ((0, selL, 1), (1, selR, 0)):
        k.op("dve", lambda e, side=side, sel=sel, colx=colx: e.tensor_scalar(
            out=HLR[:, side, :], in0=HG[:, 0, :, colx], scalar1=sel[:, 0:1], scalar2=None, op0=ALU.mult),
            reads=[hgb, C.cb], writes=[hgb])
        for r in range(1, 4):
            k.op("dve", lambda e, side=side, sel=sel, colx=colx, r=r: e.scalar_tensor_tensor(
                out=HLR[:, side, :], in0=HG[:, r, :, colx], scalar=sel[:, r:r + 1], in1=HLR[:, side, :],
                op0=ALU.mult, op1=ALU.add), reads=[hgb, C.cb], writes=[hgb])

    zouts = Sink("zouts")
    for hh in range(2):
        was = [P.load_wa(W1L[:, 768 + part * 512 + hh * 256: 768 + part * 512 + (hh + 1) * 256]) for part in range(3)]
        for cl in range(2):
            cch = hh * 2 + cl
            for part in range(3):
                wa, wab = was[part]
                for j in range(4):
                    c0, w = H4.tiles[j]
                    ps, psb = P.bank("h1", [0, 1, 2, 3])
                    emit_inproj_tile(P, H4, j, wa, wab, cl, ps, psb)
                    k.op("act", lambda e, ps=ps, part=part, c0=c0, w=w: e.activation(
                        out=HB[:, part, 1 + c0:1 + c0 + w], in_=ps[:, 0:w], func=AF.Copy), reads=[psb], writes=[HBb])
                ch12 = part * 4 + cch
                k.op("act", lambda e, part=part, ch12=ch12: e.activation(out=HB[:, part, 0:1], in_=HLR[:, 0, ch12:ch12 + 1],
                                                                         func=AF.Copy), reads=[hgb], writes=[HBb])
                k.op("act", lambda e, part=part, ch12=ch12: e.activation(out=HB[:, part, 2049:2050],
                                                                         in_=HLR[:, 1, ch12:ch12 + 1], func=AF.Copy),
                     reads=[hgb], writes=[HBb])
            for j in range(4):
                c0, w = H4.tiles[j]
                cv = []
                for part in range(3):
                    ch12 = part * 4 + cch
                    t, tb_ = C.tmp()
                    k.op("dve", lambda e, t=t, part=part, c0=c0, ch12=ch12: e.tensor_scalar(
                        out=t[:, :], in0=HB[:, part, 1 + c0:1 + c0 + 512], scalar1=dcw[:, ch12, 1:2],
                        scalar2=dcb[:, ch12:ch12 + 1], op0=ALU.mult, op1=ALU.add), reads=[HBb, C.cb], writes=[tb_])
                    k.op("dve", lambda e, t=t, part=part, c0=c0, ch12=ch12: e.scalar_tensor_tensor(
                        out=t[:, :], in0=HB[:, part, c0:c0 + 512], scalar=dcw[:, ch12, 0:1], in1=t[:, :],
                        op0=ALU.mult, op1=ALU.add), reads=[HBb, C.cb, tb_], writes=[tb_])
                    k.op("dve", lambda e, t=t, part=part, c0=c0, ch12=ch12: e.scalar_tensor_tensor(
                        out=t[:, :], in0=HB[:, part, 2 + c0:2 + c0 + 512], scalar=dcw[:, ch12, 2:3], in1=t[:, :],
                        op0=ALU.mult, op1=ALU.add), reads=[HBb, C.cb, tb_], writes=[tb_])
                    cv.append((t, tb_))
                (x0t, x0b), (x1t, x1b), (vt, vb_) = cv
                k.dma("sp", [(x0_s.ap()[:, cch, c0:c0 + 512], x0t[:, :])], reads=[x0b], sembuf=x0b, sink=zouts)
                k.op("dve", lambda e, x1t=x1t, vt=vt: e.tensor_tensor(out=x1t[:, :], in0=x1t[:, :], in1=vt[:, :], op=ALU.mult),
                     reads=[x1b, vb_], writes=[x1b])
                k.dma("sp", [(zf_s.ap()[:, cch, c0:c0 + 512], x1t[:, :])], reads=[x1b], sembuf=x1b, sink=zouts)
                zb16, zb16b = C.tmpbf()
                k.op("act", lambda e, x1t=x1t, zb16=zb16: e.activation(out=zb16[:, :], in_=x1t[:, :], func=AF.Copy),
                     reads=[x1b], writes=[zb16b])
                k.dma("sp", [(z_src.ap()[cch, :, c0:c0 + 512], zb16[:, :])], reads=[zb16b],
                      sembuf=zb16b, sink=zouts)
    zgb = Buf("zg")
    for cch in range(4):
        cc.allgather(z_src, z_g, [], zgb, sinks=[zouts], sl=cch)

    if stop_after == "l1pre":
        return finish([kgb, vgb, zgb, kcsb])

    k.barrier()
    VB = AR[:, 0:12288].rearrange("p (b d) -> p b d", b=64)
    KT = AR[:, 12288:20480]
    QS = [AR[:, 20480 + i * 2048:20480 + (i + 1) * 2048].rearrange("p (h t) -> p h t", h=4) for i in range(2)]
    QSb = [Buf("qs0"), Buf("qs1")]
    MIX = AR[:, 24576:40960].rearrange("p (c t) -> p c t", c=8)
    MIXb = [Buf(f"mixb{j}") for j in range(4)]
    KTb, VBb = Buf("KT"), Buf("VB")
    k.op("dve", lambda e: e.memset(VB[:, :, 64:128], 1.0), writes=[VBb])
    k.dma("sp", [(KT.rearrange("p (r t) -> p r t", r=4), k_g.ap().rearrange("(r p) t -> p r t", p=128))],
          reads=[kgb], writes=[KTb])
    vsrc = v_g.ap().rearrange("(b p) d -> p b d", p=128)
    k.dma("sp", [(VB[:, :, 0:64], vsrc[:, :, 0:64]), (VB[:, :, 128:192], vsrc[:, :, 64:128])], reads=[vgb], writes=[VBb])
    emit_attn1f(P, C, q_s, KT, KTb, VB, VBb, QS, QSb, MIX, MIXb)

    if stop_after == "attn":
        return finish([zgb, kcsb] + MIXb)

    k.barrier()
    V = Common()
    KC = AR[:, 0:16384].rearrange("p (n c) -> p n c", n=128)
    Zg = AR[:, 0:8192]
    V.ZC = [AR[:, 16384 + i * 512:16384 + (i + 1) * 512].rearrange("p (c k) -> p c k", c=4) for i in range(2)]
    V.ZCb = [Buf("zc0"), Buf("zc1")]
    six = [AR[:, 17408 + i * 512:17408 + (i + 1) * 512].rearrange("p (c k) -> p c k", c=4) for i in range(6)]
    V.Bre, V.Bim, V.Yre, V.Yim, V.Qre, V.Qim = six
    V.Hre = AR[:, 20480:21504].bitcast(F32)
    V.Him = AR[:, 21504:22528].bitcast(F32)
    Zgb, zs2b, KCb = Buf("Zg"), Buf("zs2"), Buf("KC")
    zg_rows = z_g.ap().rearrange("c r t -> (c r) t")
    k.wait_all("pool", [cc.sink(zgb)])
    for r in range(4):
        kb_gather(k, Zg[:, r * 2048:(r + 1) * 2048], zg_rows, IDX[:, r:r + 1], reads=[zgb, C.cb], writes=[Zgb])
    k.dma("sp", [(zs2.ap(), Zg)], reads=[Zgb], writes=[zs2b])
    k.wait_all("sp", [kcsb])
    k.dma("sp", [(KC, kc_s.ap())], reads=[zs2b], writes=[KCb])
    csink = Sink("csrc")
    emit_fftconv(P, C, F, KC, KCb, zs2, zs2b, c_src, csink, V)
    cgb = Buf("cg")
    for i in range(8):
        cc.allgather(c_src, c_g, [], cgb, sinks=[csink], sl=i)

    if stop_after == "fft":
        return finish([cgb] + MIXb)

    cg_rows = c_g.ap().rearrange("i r (a t) -> (i r a) t", t=512)
    k.wait_all("pool", [cc.sink(cgb)])
    for cq in range(4):
        for tq in range(4):
            c0 = tq * 512
            ct, ctb = C.tmp()
            kb_gather(k, ct[:, :], cg_rows, IDX[:, 4 + cq * 4 + tq:5 + cq * 4 + tq], reads=[cgb, C.cb], writes=[ctb])
            zt, ztb = C.tmp()
            k.dma("sp", [(zt[:, :], zf_s.ap()[:, cq, c0:c0 + 512])], writes=[ztb])
            xt_, xtb = C.tmp()
            k.dma("sp", [(xt_[:, :], x0_s.ap()[:, cq, c0:c0 + 512])], writes=[xtb])
            k.op("dve", lambda e, zt=zt, ct=ct, cq=cq: e.scalar_tensor_tensor(
                out=zt[:, :], in0=zt[:, :], scalar=skipv[:, cq:cq + 1], in1=ct[:, :], op0=ALU.mult, op1=ALU.add),
                reads=[ztb, ctb, C.cb], writes=[ztb])
            k.op("dve", lambda e, zt=zt, xt_=xt_, cq=cq, c0=c0: e.tensor_tensor(
                out=MIX[:, 4 + cq, c0:c0 + 512], in0=zt[:, :], in1=xt_[:, :], op=ALU.mult),
                reads=[ztb, xtb], writes=[MIXb[tq]])

    emit_outproj(P, C, X4, lambda fc, c0, w: MIX[:, fc, c0:c0 + w], MIXb, wout_d[1], mcol(1, 1, 2))
    k.barrier()
    Ht2 = AR[:, 0:16384].rearrange("p (c t) -> p c t", c=8)
    U2 = AR[:, 16384:16384 + 12288].rearrange("p (f t) -> p f t", f=6)
    H42 = Act(Ht2, tiles4, "h2")
    Ub2 = [[Buf(f"v{f}_{j}") for j in range(4)] for f in range(6)]
    emit_adaln(P, X4, H42, AV[:, 1, 2], mcol(1, 2, 0), C)
    emit_ffn(P, X4, H42, U2, Ub2, w1_d[1, 1], w3_d[1, 1], w2_d[1, 1], mcol(1, 2, 2), C)
    ob = Sink("out")
    for j in range(4):
        k.dma("sp", [(xo_d[:, :, j * 512:(j + 1) * 512], XT[:, :, j * 512:(j + 1) * 512])], reads=[XB[j]],
              sembuf=XB[j], sink=ob)
    k.wait_all("sp", [ob])
    k.close()
    return nc


def prep_fused(inp, ncore=NCORE):
    base = prep_LA(inp)
    maps = []
    cache = {}
    dcw = np.asarray(inp["d_conv_w"], np.float32)[0]
    dcb = np.asarray(inp["d_conv_b"], np.float32)[0]
    skip = np.asarray(inp["d_skip"], np.float32)[0]
    w4 = np.asarray(inp["d_f_w4"], np.float32)[0]
    p = np.arange(128)
    for core in range(ncore):
        b, qd = core // 4, core % 4
        cq = qd
        if cq not in cache:
            cache[cq] = host_consts_LB(cq)
        zemb, win, dftc, tw = cache[cq]
        m = dict(base[core])
        cf = np.zeros((128, 400), np.float32)
        cf[:, 0:244] = m.pop("cf")[:, 0:244]
        cf[:, 244:280] = fm(dcw).transpose(0, 2, 1).reshape(128, 36)
        cf[:, 280:292] = fm(dcb)
        cf[:, 292:296] = fm(skip)
        if qd > 0:
            cf[:, 296 + qd - 1] = 1.0
        if qd < 3:
            cf[:, 300 + qd + 1] = 1.0
        idx = np.zeros((128, 24), np.uint32)
        for r in range(4):
            idx[:, r] = cq * 512 + r * 128 + p
        for c2 in range(4):
            for tq in range(4):
                idx[:, 4 + c2 * 4 + tq] = ((p // 16) * 64 + c2 * 16 + (p % 16)) * 16 + qd * 4 + tq
        mlp = np.zeros((64, 456), np.float32)
        mlp[0:33, 0:64] = np.asarray(inp["d_f_w1"], np.float32)[0]
        mlp[:, 64:128] = np.asarray(inp["d_f_w2"], np.float32)[0]
        mlp[:, 128:192] = np.asarray(inp["d_f_w3"], np.float32)[0]
        mlp[:, 192] = np.asarray(inp["d_f_b1"], np.float32)[0]
        mlp[:, 193] = np.asarray(inp["d_f_b2"], np.float32)[0]
        mlp[:, 194] = np.asarray(inp["d_f_b3"], np.float32)[0]
        mlp[:, 195] = np.asarray(inp["d_f_freq"], np.float32)[0]
        mlp[:, 200:328] = w4[:, cq * 128:(cq + 1) * 128]
        mlp[:, 328:456] = w4[:, 512 + cq * 128:512 + (cq + 1) * 128]
        m["win_w"] = m.pop("win")
        m.update(cf=cf, idx=idx, mlpw=mlp, zemb=zemb, win=win, dftc=dftc, tw=tw)
        maps.append(m)
    return maps
```

```python
import math
import numpy as np
import concourse.bass as bass
import concourse.mybir as mybir
from concourse.bass_utils import run_bass_kernel_spmd

AF = mybir.ActivationFunctionType
ALU = mybir.AluOpType
F32 = mybir.dt.float32
BF16 = mybir.dt.bfloat16
EPOCH = 12000

D = 1024
S = 8192
TOK = 2048
NCORE = 8
DFF = 2752
DFFP = 2816
NF = 22
HD = 64
EPS = 1e-6
INC = 2304


class Buf:
    __slots__ = ("name", "w", "r", "dsem", "dcnt")

    def __init__(self, name=""):
        self.name = name
        self.w = None
        self.r = {}
        self.dsem = None
        self.dcnt = 0


class Sink:
    def __init__(self, name=""):
        self.name = name
        self.toks = {}


class KB:
    def __init__(self, nc):
        self.nc = nc
        self.engs = {"pe": nc.tensor, "act": nc.scalar, "dve": nc.vector,
                     "pool": nc.gpsimd, "sp": nc.sync}
        self.cnt = {e: 0 for e in self.engs}
        self.sems = {}
        self.waited = {e: {} for e in self.engs}
        self.nsem = 0
        self._stack = []

    def _newsem(self, name):
        cm = self.nc.semaphore(name)
        h = cm.__enter__()
        self._stack.append(cm)
        self.nsem += 1
        return h

    def sb(self, name, shape, dt):
        cm = self.nc.sbuf_tensor(name, shape, dt)
        t = cm.__enter__()
        self._stack.append(cm)
        return t

    def ps(self, name, shape, dt=F32):
        cm = self.nc.psum_tensor(name, shape, dt)
        t = cm.__enter__()
        self._stack.append(cm)
        return t

    def close(self):
        while self._stack:
            self._stack.pop().__exit__(None, None, None)

    def _engsem(self, eng):
        key = (eng, self.cnt[eng] // EPOCH)
        if key not in self.sems:
            self.sems[key] = self._newsem(f"s_{eng}_{key[1]}")
        return key

    def _deps(self, reads, writes):
        deps = {}

        def add(k, v):
            if deps.get(k, 0) < v:
                deps[k] = v
        for b in reads:
            if b.w is not None:
                add(*b.w)
        for b in writes:
            if b.w is not None:
                add(*b.w)
            for kk, v in b.r.items():
                add(kk, v)
        return deps

    def _wait(self, eng, deps):
        w = self.waited[eng]
        e = self.engs[eng]
        for kk, v in deps.items():
            if w.get(kk, 0) >= v:
                continue
            e.wait_ge(self.sems[kk], v)
            w[kk] = v

    def _commit(self, tok, reads, writes):
        kk, v = tok
        for b in reads:
            if b.r.get(kk, 0) < v:
                b.r[kk] = v
        for b in writes:
            b.w = tok
            b.r = {}

    def op(self, eng, fn, reads=(), writes=()):
        deps = self._deps(reads, writes)
        if eng == "pe":
            deps = {kk: v for kk, v in deps.items() if kk[0] != "pe"}
        self._wait(eng, deps)
        key = self._engsem(eng)
        ins = fn(self.engs[eng])
        self.cnt[eng] += 1
        val = self.cnt[eng] - key[1] * EPOCH
        ins.then_inc(self.sems[key], 1)
        tok = (key, val)
        self._commit(tok, reads, writes)
        return tok

    def dma(self, q, pairs, reads=(), writes=(), sembuf=None, sink=None, **kw):
        sbf = sembuf or (writes[0] if writes else reads[0])
        if sbf.dsem is None:
            sbf.dsem = {}
            sbf.dcnt = {}
        sw = (q == "pool")
        if sw not in sbf.dsem:
            sbf.dsem[sw] = ("d", id(sbf), sw)
            sbf.dcnt[sw] = 0
            self.sems[sbf.dsem[sw]] = self._newsem(f"d_{self.nsem}")
        deps = self._deps(reads, writes)
        self._wait(q, deps)
        e = self.engs[q]
        for (o, i) in pairs:
            e.dma_start(out=o, in_=i, **kw).then_inc(self.sems[sbf.dsem[sw]], 16)
            sbf.dcnt[sw] += 16
        tok = (sbf.dsem[sw], sbf.dcnt[sw])
        self._commit(tok, reads, writes)
        if sink is not None and sink.toks.get(tok[0], 0) < tok[1]:
            sink.toks[tok[0]] = tok[1]
        return tok

    def wait_all(self, eng, bufs):
        deps = {}
        for b in bufs:
            d = dict(b.toks) if isinstance(b, Sink) else self._deps([b], [b])
            for kk, v in d.items():
                if deps.get(kk, 0) < v:
                    deps[kk] = v
        self._wait(eng, deps)

    def barrier(self, dbufs=()):
        deps = {}
        for e in ("pe", "act", "dve"):
            if self.cnt[e] == 0:
                continue
            ep = (self.cnt[e] - 1) // EPOCH
            deps[(e, ep)] = self.cnt[e] - ep * EPOCH
        for b in dbufs:
            for kk, v in self._deps([b], [b]).items():
                if deps.get(kk, 0) < v:
                    deps[kk] = v
        for e in ("pe", "act", "dve", "pool", "sp"):
            self._wait(e, dict(deps))


class Prog:
    def __init__(self, nc, k):
        self.nc = nc
        self.k = k
        self.WA = [k.sb(f"WA{i}", [128, 8, 256], BF16) for i in range(4)]
        self.WAb = [Buf(f"WA{i}") for i in range(4)]
        self.WBf = k.sb("WBf", [128, 3, 1024], F32)
        self.WB = [self.WBf[:, i, :].bitcast(BF16).rearrange("p (f n) -> p f n", f=2) for i in range(3)]
        self.WBb = [Buf(f"WB{i}") for i in range(3)]
        self.wa_i = 0
        self.PS = [k.ps(f"ps{i}", [128, 512]) for i in range(8)]
        self.PSb = [Buf(f"ps{i}") for i in range(8)]
        self.rr = {}

    def next_wa(self, parity=None):
        i = self.wa_i
        self.wa_i = (self.wa_i + 1) % 4
        return i

    def bank(self, group, banks):
        i = self.rr.get(group, 0)
        self.rr[group] = (i + 1) % len(banks)
        b = banks[i]
        return self.PS[b], self.PSb[b]

    def load_wa(self, src_ap):
        i = self.next_wa()
        self.k.dma("pool", [(self.WA[i][:], src_ap.rearrange("(c p) n -> p c n", p=128))], writes=[self.WAb[i]])
        return self.WA[i], self.WAb[i]

    def load_wb(self, i, src_ap):
        self.k.dma("pool", [(self.WB[i], src_ap.rearrange("(f p) n -> p f n", p=128))], writes=[self.WBb[i]])
        return self.WB[i], self.WBb[i]


def emit_mod(P, cond_bf, cond_b, wmod_l, bmod_l, modv, modb, a_out, normg_l):
    k = P.k
    ps, psb = P.PS[7], P.PSb[7]
    for ch in range(36):
        wa, wab = P.load_wa(wmod_l[:, ch * 256:(ch + 1) * 256])
        for cl in range(2):
            cc = ch * 2 + cl
            for kc in range(8):
                k.op("pe", lambda e, cc=cc, kc=kc, cl=cl, wa=wa: e.matmul(
                    ps[:, cc:cc + 1], lhsT=wa[:, kc, cl * 128:(cl + 1) * 128], rhs=cond_bf[:, kc:kc + 1],
                    start=(kc == 0), stop=(kc == 7)),
                    reads=[wab, cond_b], writes=[psb])
    k.op("dve", lambda e: e.tensor_tensor(out=modv[:], in0=ps[:, 0:72], in1=bmod_l, op=ALU.add),
         reads=[psb], writes=[modb])
    for i in range(3):
        k.op("dve", lambda e, i=i: e.scalar_tensor_tensor(
            out=a_out[:, i, :], in0=modv[:, (i * 3 + 1) * 8:(i * 3 + 2) * 8], scalar=1.0, in1=normg_l[:, i, :],
            op0=ALU.add, op1=ALU.mult), reads=[modb], writes=[modb])
    for i in (0, 2):
        k.op("dve", lambda e, i=i: e.tensor_scalar(
            out=modv[:, (i * 3 + 2) * 8:(i * 3 + 3) * 8], in0=modv[:, (i * 3 + 2) * 8:(i * 3 + 3) * 8],
            scalar1=0.5, scalar2=None, op0=ALU.mult), reads=[modb], writes=[modb])


class Act:
    def __init__(self, t, tiles, name):
        self.t = t
        self.tiles = tiles
        self.b = [Buf(f"{name}{j}") for j in range(len(tiles))]

    def ap(self, c, j):
        c0, w = self.tiles[j]
        return self.t[:, c, c0:c0 + w]


def emit_adaln(P, X, H, a_ap, shift_ap, C):
    k = P.k
    for j, (c0, w) in enumerate(X.tiles):
        ps, psb = P.PS[6], P.PSb[6]
        for c in range(8):
            sq, sqb = C.tmpbf()
            k.op("act", lambda e, c=c, j=j, w=w, sq=sq: e.activation(out=sq[:, 0:w], in_=X.ap(c, j), func=AF.Square),
                 reads=[X.b[j]], writes=[sqb])
            k.op("pe", lambda e, c=c, w=w, sq=sq: e.matmul(ps[:, 0:w], lhsT=C.ones_bf[:], rhs=sq[:, 0:w],
                                                     start=(c == 0), stop=(c == 7)),
                 reads=[sqb, C.cb], writes=[psb])
        k.op("act", lambda e, w=w: e.activation(out=C.RS[:, 0:w], in_=ps[:, 0:w], func=AF.Sqrt,
                                                bias=C.eps[:, 0:1], scale=1.0 / D),
             reads=[psb, C.cb], writes=[C.RSb])
        k.op("dve", lambda e, w=w: e.reciprocal(out=C.RS[:, 0:w], in_=C.RS[:, 0:w]), reads=[C.RSb], writes=[C.RSb])
        for c in range(8):
            tt, ttb = C.tmp()
            k.op("dve", lambda e, c=c, j=j, w=w, tt=tt: e.scalar_tensor_tensor(
                out=tt[:, 0:w], in0=X.ap(c, j), scalar=a_ap[:, c:c + 1], in1=C.RS[:, 0:w],
                op0=ALU.mult, op1=ALU.mult), reads=[X.b[j], C.RSb, C.modb], writes=[ttb])
            k.op("act", lambda e, c=c, j=j, w=w, tt=tt: e.activation(
                out=H.ap(c, j), in_=tt[:, 0:w], func=AF.Identity, bias=shift_ap[:, c:c + 1], scale=1.0),
                reads=[ttb, C.modb], writes=[H.b[j]])


def emit_ffn(P, X, H, U, Ub, w1, w3, w2, gate_ap, C, hook=None):
    k = P.k
    groups = [(0, 3), (3, 3), (6, 3), (9, 2)]
    ntile = len(X.tiles)
    for (p0, npair) in groups:
        for pl in range(npair):
            p = p0 + pl
            wa1, wa1b = P.load_wa(w1[:, p * 256:(p + 1) * 256])
            wa3, wa3b = P.load_wa(w3[:, p * 256:(p + 1) * 256])
            for fl in range(2):
                fu = pl * 2 + fl
                for j in range(ntile):
                    c0, w = X.tiles[j]
                    ps1, ps1b = P.bank("h1", [0, 1])
                    ps3, ps3b = P.bank("h3", [2, 3])
                    for c in range(8):
                        k.op("pe", lambda e, c=c, j=j, w=w, ps1=ps1, wa1=wa1, fl=fl: e.matmul(
                            ps1[:, 0:w], lhsT=wa1[:, c, fl * 128:(fl + 1) * 128], rhs=H.ap(c, j),
                            start=(c == 0), stop=(c == 7)), reads=[wa1b, H.b[j]], writes=[ps1b])
                    for c in range(8):
                        k.op("pe", lambda e, c=c, j=j, w=w, ps3=ps3, wa3=wa3, fl=fl: e.matmul(
                            ps3[:, 0:w], lhsT=wa3[:, c, fl * 128:(fl + 1) * 128], rhs=H.ap(c, j),
                            start=(c == 0), stop=(c == 7)), reads=[wa3b, H.b[j]], writes=[ps3b])
                    sl, slb = C.tmp()
                    k.op("act", lambda e, w=w, ps1=ps1, sl=sl: e.activation(out=sl[:, 0:w], in_=ps1[:, 0:w], func=AF.Silu),
                         reads=[ps1b], writes=[slb])
                    k.op("dve", lambda e, w=w, c0=c0, ps3=ps3, sl=sl, fu=fu: e.tensor_tensor(
                        out=U[:, fu, c0:c0 + w], in0=ps3[:, 0:w], in1=sl[:, 0:w], op=ALU.mult),
                        reads=[ps3b, slb], writes=[Ub[fu][j]])
                    if hook is not None:
                        hook()
        for pl in range(npair):
            P.load_wb(pl, w2[(p0 + pl) * 256:(p0 + pl + 1) * 256, :])
        nfu = npair * 2
        for c in range(8):
            for j in range(ntile):
                c0, w = X.tiles[j]
                pso, psob = P.bank("o", [4, 5])
                for fu in range(nfu):
                    k.op("pe", lambda e, fu=fu, c=c, w=w, c0=c0, pso=pso: e.matmul(
                        pso[:, 0:w], lhsT=P.WB[fu // 2][:, fu % 2, c * 128:(c + 1) * 128], rhs=U[:, fu, c0:c0 + w],
                        start=(fu == 0), stop=(fu == nfu - 1)), reads=[P.WBb[fu // 2], Ub[fu][j]], writes=[psob])
                k.op("dve", lambda e, c=c, j=j, w=w, pso=pso: e.scalar_tensor_tensor(
                    out=X.ap(c, j), in0=pso[:, 0:w], scalar=gate_ap[:, c:c + 1], in1=X.ap(c, j),
                    op0=ALU.mult, op1=ALU.add), reads=[psob, X.b[j], C.modb], writes=[X.b[j]])


class Common:
    pass


def emit_inproj_tile(P, H, j, wa, wab, cl, ps, psb, cols=None):
    k = P.k
    c0, w = H.tiles[j]
    if cols is not None:
        c0, w = c0 + cols[0], cols[1]
    for c in range(8):
        k.op("pe", lambda e, c=c, c0=c0, w=w: e.matmul(
            ps[:, 0:w], lhsT=wa[:, c, cl * 128:(cl + 1) * 128], rhs=H.t[:, c, c0:c0 + w],
            start=(c == 0), stop=(c == 7)), reads=[wab, H.b[j]], writes=[psb])
    return w


def emit_qknorm(P, C, ps, psb, w, g_ap, out_ap, out_b, extra_reads=()):
    k = P.k
    sq, sqb = C.tmpbf()
    k.op("act", lambda e: e.activation(out=sq[:, 0:w], in_=ps[:, 0:w], func=AF.Square), reads=[psb], writes=[sqb])
    p2, p2b = P.PS[6], P.PSb[6]
    k.op("pe", lambda e: e.matmul(p2[:, 0:w], lhsT=C.bd_bf[:], rhs=sq[:, 0:w], start=True, stop=True),
         reads=[sqb, C.cb], writes=[p2b])
    rs, rsb = C.tmp()
    k.op("act", lambda e: e.activation(out=rs[:, 0:w], in_=p2[:, 0:w], func=AF.Sqrt, bias=C.eps[:, 0:1],
                                       scale=1.0 / HD), reads=[p2b, C.cb], writes=[rsb])
    k.op("dve", lambda e: e.reciprocal(out=rs[:, 0:w], in_=rs[:, 0:w]), reads=[rsb], writes=[rsb])
    k.op("dve", lambda e: e.scalar_tensor_tensor(out=out_ap, in0=ps[:, 0:w], scalar=g_ap, in1=rs[:, 0:w],
                                                 op0=ALU.mult, op1=ALU.mult),
         reads=[psb, rsb, C.cb] + list(extra_reads), writes=[out_b])


def emit_outproj(P, C, X, MIX, MIXb, wout_l, gate_ap):
    k = P.k
    for pr in range(4):
        wa, wab = P.load_wa(wout_l[:, pr * 256:(pr + 1) * 256])
        for cl in range(2):
            c = pr * 2 + cl
            for j, (c0, w) in enumerate(X.tiles):
                pso, psob = P.bank("o", [4, 5])
                for fc in range(8):
                    k.op("pe", lambda e, fc=fc, c0=c0, w=w, pso=pso, wa=wa, cl=cl: e.matmul(
                        pso[:, 0:w], lhsT=wa[:, fc, cl * 128:(cl + 1) * 128], rhs=MIX(fc, c0, w),
                        start=(fc == 0), stop=(fc == 7)), reads=[wab, MIXb[j]], writes=[psob])
                k.op("dve", lambda e, c=c, j=j, w=w, pso=pso: e.scalar_tensor_tensor(
                    out=X.ap(c, j), in0=pso[:, 0:w], scalar=gate_ap[:, c:c + 1], in1=X.ap(c, j),
                    op0=ALU.mult, op1=ALU.add), reads=[psob, X.b[j], C.modb], writes=[X.b[j]])


def emit_expb(P, C, relrep, relb, onehot, vmask, scr, EXPB, EXPBb):
    k = P.k
    nc = P.nc
    E8 = C.E8
    E8b = Buf("E8")
    for h in range(8):
        lt, ltb = C.tmp()
        k.op("dve", lambda e, h=h, lt=lt: e.tensor_scalar(out=lt[0:32, 0:128], in0=C.ones_f[0:32, 0:128],
                                                          scalar1=relrep[0:32, h:h + 1], scalar2=None, op0=ALU.mult),
             reads=[relb, C.cb], writes=[ltb])
        ps, psb = P.bank("h1", [0, 1])
        k.op("pe", lambda e, lt=lt, ps=ps: e.matmul(ps[:, 0:512], lhsT=lt[0:32, 0:128], rhs=onehot[0:32, :],
                                                    start=True, stop=True), reads=[ltb, C.cb], writes=[psb])
        ex, exb = C.tmp()
        k.op("act", lambda e, ps=ps, ex=ex: e.activation(out=ex[:, :], in_=ps[:, 0:512], func=AF.Exp),
             reads=[psb], writes=[exb])
        k.op("dve", lambda e, h=h, ex=ex: e.tensor_tensor(out=E8[:, h, :], in0=ex[:, :], in1=vmask[:, :], op=ALU.mult),
             reads=[exb, C.cb], writes=[E8b])
    scrb = Buf("scr")
    k.dma("sp", [(scr.ap(), E8[:])], reads=[E8b], writes=[scrb])
    pairs = []
    for h in range(8):
        for bi in range(3):
            off = h * 512 + 128 * (1 - bi) + 255
            src = bass.AP(tensor=scr, offset=off, ap=[[8 * 512 - 1, 128], [1, 128]])
            pairs.append((EXPB[:, h, bi, :], src))
    k.dma("sp", pairs, reads=[scrb], writes=[EXPBb])
    return scrb


def emit_conv0(P, C, H, MIX, MIXb, win_l, convw, convb, flags, S_t, Sb):
    k = P.k
    ntile_own = 4
    for hh in range(2):
        wgb, wgbb = P.load_wa(win_l[:, 768 + hh * 256: 768 + (hh + 1) * 256])
        wgc, wgcb = P.load_wa(win_l[:, 1280 + hh * 256: 1280 + (hh + 1) * 256])
        wu, wub = P.load_wa(win_l[:, 1792 + hh * 256: 1792 + (hh + 1) * 256])
        for cl in range(2):
            cc = hh * 2 + cl
            for j in range(ntile_own):
                c0, w = H.tiles[j]
                pg, pgb = P.bank("h1", [0, 1])
                pu, pub = P.bank("h3", [2, 3])
                emit_inproj_tile(P, H, j, wgc, wgcb, cl, pg, pgb)
                emit_inproj_tile(P, H, j, wu, wub, cl, pu, pub)
                tg, tgb = C.tmp()
                k.op("act", lambda e, tg=tg, pg=pg, w=w: e.activation(out=tg[:, 0:w], in_=pg[:, 0:w], func=AF.Copy),
                     reads=[pgb], writes=[tgb])
                k.op("dve", lambda e, tg=tg, pu=pu, w=w, c0=c0: e.tensor_tensor(
                    out=S_t[:, 1 + c0:1 + c0 + w], in0=pu[:, 0:w], in1=tg[:, 0:w], op=ALU.mult),
                    reads=[pub, tgb], writes=[Sb])
            pg, pgb = P.bank("h1", [0, 1])
            pu, pub = P.bank("h3", [2, 3])
            emit_inproj_tile(P, H, 4, wgc, wgcb, cl, pg, pgb, cols=(127, 2))
            emit_inproj_tile(P, H, 4, wu, wub, cl, pu, pub, cols=(127, 2))
            tg, tgb = C.tmp()
            k.op("act", lambda e, tg=tg, pg=pg: e.activation(out=tg[:, 0:2], in_=pg[:, 0:2], func=AF.Copy),
                 reads=[pgb], writes=[tgb])
            k.op("dve", lambda e, tg=tg, pu=pu: e.scalar_tensor_tensor(
                out=S_t[:, 0:1], in0=pu[:, 0:1], scalar=flags[:, 0:1], in1=tg[:, 0:1], op0=ALU.mult, op1=ALU.mult),
                reads=[pub, tgb, C.cb], writes=[Sb])
            k.op("dve", lambda e, tg=tg, pu=pu: e.scalar_tensor_tensor(
                out=S_t[:, 2049:2050], in0=pu[:, 1:2], scalar=flags[:, 1:2], in1=tg[:, 1:2], op0=ALU.mult, op1=ALU.mult),
                reads=[pub, tgb, C.cb], writes=[Sb])
            for j in range(ntile_own):
                c0, w = H.tiles[j]
                pb_, pbb = P.bank("o", [4, 5])
                emit_inproj_tile(P, H, j, wgb, wgbb, cl, pb_, pbb)
                t, tb = C.tmp()
                k.op("dve", lambda e, t=t, c0=c0, w=w, cc=cc: e.tensor_scalar(
                    out=t[:, 0:w], in0=S_t[:, 1 + c0:1 + c0 + w], scalar1=convw[:, cc, 1:2], scalar2=convb[:, cc:cc + 1],
                    op0=ALU.mult, op1=ALU.add), reads=[Sb, C.cb], writes=[tb])
                k.op("dve", lambda e, t=t, c0=c0, w=w, cc=cc: e.scalar_tensor_tensor(
                    out=t[:, 0:w], in0=S_t[:, c0:c0 + w], scalar=convw[:, cc, 0:1], in1=t[:, 0:w],
                    op0=ALU.mult, op1=ALU.add), reads=[Sb, C.cb, tb], writes=[tb])
                k.op("dve", lambda e, t=t, c0=c0, w=w, cc=cc: e.scalar_tensor_tensor(
                    out=t[:, 0:w], in0=S_t[:, 2 + c0:2 + c0 + w], scalar=convw[:, cc, 2:3], in1=t[:, 0:w],
                    op0=ALU.mult, op1=ALU.add), reads=[Sb, C.cb, tb], writes=[tb])
                k.op("dve", lambda e, t=t, c0=c0, w=w, cc=cc, pb_=pb_: e.tensor_tensor(
                    out=MIX(4 + cc, c0, w), in0=pb_[:, 0:w], in1=t[:, 0:w], op=ALU.mult),
                    reads=[pbb, tb], writes=[MIXb[j]])


def emit_qkv0(P, C, H, win_l, QT, QTb, KT, KTb, VA, VAb, gq, gk, flags):
    k = P.k
    for pr in range(2):
        wa, wab = P.load_wa(win_l[:, pr * 256:(pr + 1) * 256])
        for cl in range(2):
            hh = pr * 2 + cl
            for j in range(4):
                c0, w = H.tiles[j]
                ps, psb = P.bank("h1", [0, 1, 2, 3])
                emit_inproj_tile(P, H, j, wa, wab, cl, ps, psb)
                emit_qknorm(P, C, ps, psb, w, gq[:, 0:1], QT[:, hh, c0:c0 + w], QTb[j])
    wa, wab = P.load_wa(win_l[:, 512:768])
    for j in range(5):
        c0, w = H.tiles[j]
        ps, psb = P.bank("h1", [0, 1, 2, 3])
        emit_inproj_tile(P, H, j, wa, wab, 0, ps, psb)
        emit_qknorm(P, C, ps, psb, w, gk[:, 0:1], KT[:, c0:c0 + w], KTb[j])
    for tb in range(18):
        j = tb // 4 if tb < 16 else 4
        ps, psb = P.bank("o", [4, 5])
        for c in range(8):
            k.op("pe", lambda e, c=c, tb=tb, ps=ps: e.matmul(
                ps[:, 0:128], lhsT=H.t[:, c, tb * 128:(tb + 1) * 128], rhs=wa[:, c, 128:256],
                start=(c == 0), stop=(c == 7)), reads=[wab, H.b[j]], writes=[psb])
        if tb < 16:
            k.op("act", lambda e, tb=tb, ps=ps: e.activation(out=VA[:, tb, 0, 0:64], in_=ps[:, 0:64], func=AF.Copy),
                 reads=[psb], writes=[VAb])
            k.op("act", lambda e, tb=tb, ps=ps: e.activation(out=VA[:, tb, 1, 64:128], in_=ps[:, 64:128], func=AF.Copy),
                 reads=[psb], writes=[VAb])
        else:
            fl = flags[:, tb - 16:tb - 15]
            k.op("dve", lambda e, tb=tb, ps=ps, fl=fl: e.tensor_scalar(
                out=VA[:, tb, 0, 0:64], in0=ps[:, 0:64], scalar1=fl, scalar2=None, op0=ALU.mult),
                reads=[psb, C.cb], writes=[VAb])
            k.op("dve", lambda e, tb=tb, ps=ps, fl=fl: e.tensor_scalar(
                out=VA[:, tb, 1, 64:128], in0=ps[:, 64:128], scalar1=fl, scalar2=None, op0=ALU.mult),
                reads=[psb, C.cb], writes=[VAb])
            k.op("dve", lambda e, tb=tb, fl=fl: e.tensor_scalar(
                out=VA[:, tb, 0, 64:128], in0=C.ones_f[:, 0:64], scalar1=fl, scalar2=None, op0=ALU.mult),
                reads=[C.cb], writes=[VAb])
            k.op("dve", lambda e, tb=tb, fl=fl: e.tensor_scalar(
                out=VA[:, tb, 1, 0:64], in0=C.ones_f[:, 0:64], scalar1=fl, scalar2=None, op0=ALU.mult),
                reads=[C.cb], writes=[VAb])


def emit_attn0(P, C, QT, QTb, KT, KTb, VA, VAb, EXPB, EXPBb, expsink, MIXLO, MIXb):
    k = P.k
    for n in range(16):
        j = n // 4
        for g in range(2):
            gs = slice(g * 64, (g + 1) * 64)
            ds_ = slice((1 - g) * 64, (2 - g) * 64)
            po, pob = P.bank("o", [4, 5])
            for bi in range(3):
                kb = n - 1 + bi
                kidx = 16 if kb < 0 else (17 if kb > 15 else kb)
                kj = kidx // 4 if kidx < 16 else 4
                ps, psb = P.bank("h1", [0, 1, 2, 3])
                k.op("pe", lambda e, ps=ps, kidx=kidx, n=n, gs=gs: e.matmul(
                    ps[:, 0:512], lhsT=KT[gs, kidx * 128:(kidx + 1) * 128], rhs=QT[gs, :, n * 128:(n + 1) * 128],
                    start=True, stop=True), reads=[KTb[kj], QTb[j]], writes=[psb])
                ex, exb = C.tmp()
                k.op("act", lambda e, ps=ps, ex=ex: e.activation(out=ex[:, :], in_=ps[:, 0:512], func=AF.Exp,
                                                                 scale=HD ** -0.5),
                     reads=[psb], writes=[exb])
                pt, ptb = C.tmpbf()
                k.op("dve", lambda e, ex=ex, pt=pt, g=g, bi=bi: e.tensor_tensor(
                    out=pt[:, :].rearrange("p (h q) -> p h q", h=4), in0=ex[:, :].rearrange("p (h q) -> p h q", h=4),
                    in1=EXPB[:, g * 4:(g + 1) * 4, bi, :], op=ALU.mult), reads=[exb, EXPBb], writes=[ptb])
                for hh in range(4):
                    k.op("pe", lambda e, po=po, pt=pt, hh=hh, kidx=kidx, g=g, bi=bi: e.matmul(
                        po[:, hh * 128:(hh + 1) * 128], lhsT=VA[:, kidx, g, :], rhs=pt[:, hh * 128:(hh + 1) * 128],
                        start=(bi == 0 and hh == 0), stop=(bi == 2 and hh == 3), skip_group_check=True),
                        reads=[VAb, ptb], writes=[pob])
            rd, rdb = C.tmp()
            for hh in range(4):
                k.op("dve", lambda e, rd=rd, po=po, hh=hh, g=g, ds_=ds_: e.tensor_scalar(
                    out=rd[ds_, hh * 128:(hh + 1) * 128], in0=po[ds_, hh * 128:(hh + 1) * 128],
                    scalar1=expsink[ds_, g * 4 + hh:g * 4 + hh + 1], scalar2=None, op0=ALU.add),
                    reads=[pob, C.cb], writes=[rdb])
            k.op("dve", lambda e, rd=rd, ds_=ds_: e.reciprocal(out=rd[ds_, :], in_=rd[ds_, :]), reads=[rdb], writes=[rdb])
            k.op("dve", lambda e, rd=rd, po=po, gs=gs, ds_=ds_, n=n: e.tensor_tensor(
                out=MIXLO[gs, 0:4, n * 128:(n + 1) * 128], in0=po[gs, :].rearrange("p (h q) -> p h q", h=4),
                in1=rd[ds_, :].rearrange("p (h q) -> p h q", h=4), op=ALU.mult), reads=[pob, rdb], writes=[MIXb[j]])


class RotTmp:
    def __init__(self, k, name, n, dt):
        self.t = [k.sb(f"{name}{i}", [128, 512], dt) for i in range(n)]
        self.b = [Buf(f"{name}{i}") for i in range(n)]
        self.i = 0

    def __call__(self):
        i = self.i
        self.i = (i + 1) % len(self.t)
        return self.t[i], self.b[i]


BUCKET_MODE = "trunc"


def t5_bucket_np(rel):
    n = np.abs(rel)
    v = (np.log(np.maximum(n, 1).astype(np.float32) / np.float32(8)) / np.float32(math.log(16.0)) * np.float32(8))
    large = 8 + (np.rint(v).astype(np.int32) if BUCKET_MODE == "round" else v.astype(np.int32))
    large = np.minimum(large, 15)
    return np.where(rel > 0, 16, 0) + np.where(n < 8, n, large)


def build_LA(do_l1=True, stop_after=None):
    nc = bass.Bass("TRN2", target_bir_lowering=False)
    k = KB(nc)
    dt_in = lambda name, shape: nc.dram_tensor(name, list(shape), F32, kind="ExternalInput").ap()
    xT_d = dt_in("xT", [128, 8, 2048])
    xH_d = dt_in("xH", [128, 8, 256])
    cf_d = dt_in("cf", [128, 1400])
    oh_d = dt_in("oh", [32, 512])
    vm_d = dt_in("vm", [128, 512])
    wmod_d = dt_in("wmod", [2, 1024, 9216])
    w1_d = dt_in("w1", [2, 2, 1024, DFFP])
    w3_d = dt_in("w3", [2, 2, 1024, DFFP])
    w2_d = dt_in("w2", [2, 2, DFFP, 1024])
    win_d = dt_in("win", [2, 1024, INC])
    wout_d = dt_in("wout", [2, 1024, 1024])
    xo_d = nc.dram_tensor("xo", [128, 8, 2048], F32, kind="ExternalOutput").ap()
    scr = nc.dram_tensor("scr", [128, 8 * 512], F32, kind="Internal")

    P = Prog(nc, k)
    C = Common()
    C.tmp = RotTmp(k, "tf", 5, F32)
    C.tmpbf = RotTmp(k, "tb", 3, BF16)
    C.RS = k.sb("RS", [128, 512], F32)
    C.RSb = Buf("RS")
    C.cb = Buf("consts")
    C.modb = Buf("mod")
    CF = k.sb("CF", [128, 1400], F32)
    OH = k.sb("OH", [32, 512], F32)
    VM = k.sb("VM", [128, 512], F32)
    C.ones_bf = k.sb("ones_bf", [128, 128], BF16)
    C.bd_bf = k.sb("bd_bf", [128, 128], BF16)
    C.ones_f = k.sb("ones_f", [128, 128], F32)
    C.eps = k.sb("eps", [128, 1], F32)
    condbf = k.sb("condbf", [128, 8], BF16)
    MODV = k.sb("modv", [128, 2, 72], F32)
    AV = k.sb("av", [128, 2, 3, 8], F32)
    EXS = k.sb("exs", [128, 8], F32)
    XT = k.sb("XT", [128, 8, 2048], F32)
    Ht = k.sb("H", [128, 8, 2304], BF16)
    UQ = k.sb("UQ", [128, 15360], BF16)
    MX = k.sb("MX", [128, 4096], F32)

    o_cT, o_fl, o_ng, o_bm, o_cw, o_cb, o_gq, o_gk, o_sink = 0, 8, 10, 58, 202, 214, 218, 220, 222
    cT = CF[:, o_cT:o_cT + 8]
    flags = CF[:, o_fl:o_fl + 2]
    normg = CF[:, o_ng:o_ng + 48].rearrange("p (l i c) -> p l i c", l=2, i=3)
    bmod = CF[:, o_bm:o_bm + 144].rearrange("p (l m) -> p l m", l=2)
    convw = CF[:, o_cw:o_cw + 12].rearrange("p (c t) -> p c t", c=4)
    convb = CF[:, o_cb:o_cb + 4]
    gq = CF[:, o_gq:o_gq + 2]
    gk = CF[:, o_gk:o_gk + 2]
    sink = CF[:, o_sink:o_sink + 8]
    relrep = CF[:, 232:240]

    XH = MX[:, 0:2048].rearrange("p (c t) -> p c t", c=8)
    MIXHI = MX[:, :].bitcast(BF16).rearrange("p (c t) -> p c t", c=4)
    U = UQ[:, 0:6 * 2304].rearrange("p (f t) -> p f t", f=6)
    QT = UQ[:, 0:8192].rearrange("p (h t) -> p h t", h=4)
    KT = UQ[:, 8192:8192 + 2304]
    VA = UQ[:, 10496:10496 + 4608].rearrange("p (b g d) -> p b g d", b=18, g=2)
    S_t = UQ[:, 10496:10496 + 4100].bitcast(F32)
    C.E8 = UQ[:, 0:8192].bitcast(F32).rearrange("p (h m) -> p h m", h=8)
    EXPB = P.WBf[:, :, :].rearrange("p a (b q) -> p (a b) q", q=128).rearrange("p (h b) q -> p h b q", h=8)

    tiles5 = [(0, 512), (512, 512), (1024, 512), (1536, 512), (2048, 256)]
    tiles4 = tiles5[:4]

    class XAct:
        def __init__(self, tiles):
            self.tiles = tiles
            self.b = XB[:len(tiles)]

        def ap(self, c, j):
            if j < 4:
                return XT[:, c, j * 512:(j + 1) * 512]
            return XH[:, c, :]
    XB = [Buf(f"x{j}") for j in range(5)]
    X5, X4 = XAct(tiles5), XAct(tiles4)
    H5 = Act(Ht, tiles5, "h")
    H4 = Act(Ht, tiles4, "h")
    H4.b = H5.b[:4]
    Ub = [[Buf(f"u{f}_{j}") for j in range(5)] for f in range(6)]

    k.dma("sp", [(CF[:], cf_d)], writes=[C.cb])
    k.dma("sp", [(OH[:], oh_d), (VM[:], vm_d)], writes=[C.cb], sembuf=C.cb)
    for j in range(4):
        k.dma("sp", [(XT[:, :, j * 512:(j + 1) * 512], xT_d[:, :, j * 512:(j + 1) * 512])], writes=[XB[j]])
    k.dma("sp", [(XH, xH_d)], writes=[XB[4]])
    cst = Buf("cst")
    k.op("dve", lambda e: e.memset(C.ones_bf[:], 1.0), writes=[cst])
    k.op("dve", lambda e: e.memset(C.ones_f[:], 1.0), writes=[cst])
    k.op("dve", lambda e: e.memset(C.eps[:], EPS), writes=[cst])
    k.op("dve", lambda e: e.memset(C.bd_bf[:], 0.0), writes=[cst])
    k.op("dve", lambda e: e.memset(C.bd_bf[0:64, 0:64], 1.0), writes=[cst])
    k.op("dve", lambda e: e.memset(C.bd_bf[64:128, 64:128], 1.0), writes=[cst])
    condb = Buf("cond")
    k.op("act", lambda e: e.activation(out=condbf[:], in_=cT, func=AF.Silu), reads=[C.cb], writes=[condb])
    k.op("act", lambda e: e.activation(out=EXS[:], in_=sink, func=AF.Exp), reads=[C.cb], writes=[cst])
    k.op("dve", lambda e: e.tensor_copy(out=C.eps[:], in_=C.eps[:]), reads=[cst, C.cb], writes=[C.cb])

    def layer_mod(l):
        emit_mod(P, condbf, condb, wmod_d[l], bmod[:, l, :], MODV[:, l, :], C.modb, AV[:, l], normg[:, l])

    def mcol(l, i, kind):
        return MODV[:, l, (i * 3 + kind) * 8:(i * 3 + kind + 1) * 8]

    def mix_ap(fc, c0, w):
        if fc < 4:
            return Ht[:, fc, c0:c0 + w]
        return MIXHI[:, fc - 4, c0:c0 + w]

    def finish():
        ob = Sink("out")
        for j in range(4):
            k.dma("sp", [(xo_d[:, :, j * 512:(j + 1) * 512], XT[:, :, j * 512:(j + 1) * 512])], reads=[XB[j]],
                  sembuf=XB[j], sink=ob)
        k.wait_all("sp", [ob])
        k.close()
        return nc

    layer_mod(0)
    emit_adaln(P, X5, H5, AV[:, 0, 0], mcol(0, 0, 0), C)
    emit_ffn(P, X5, H5, U, Ub, w1_d[0, 0], w3_d[0, 0], w2_d[0, 0], mcol(0, 0, 2), C)
    if stop_after == "ffn0":
        return finish()
    k.barrier()
    ohb = C.cb
    scrb = emit_expb(P, C, relrep, C.cb, OH, VM, scr, EXPB, P.WBb[0])
    k.barrier(dbufs=[scrb])
    if stop_after == "expb":
        dbg = nc.dram_tensor("dbg", [128, 3072], F32, kind="ExternalOutput").ap()
        ob2 = Buf("dbg")
        k.dma("sp", [(dbg, P.WBf[:].rearrange("p a b -> p (a b)"))], reads=[P.WBb[0]], writes=[ob2], sembuf=ob2)
        k.wait_all("sp", [ob2])
        return finish()
    emit_adaln(P, X5, H5, AV[:, 0, 1], mcol(0, 1, 0), C)
    k.barrier()
    MIXb = [Buf(f"mix{j}") for j in range(4)]
    Sb = Buf("S")
    emit_conv0(P, C, H5, mix_ap, MIXb, win_d[0], convw, convb, flags, S_t, Sb)
    k.barrier()
    QTb = [Buf(f"qt{j}") for j in range(4)]
    KTb = [Buf(f"kt{j}") for j in range(5)]
    VAb = Buf("va")
    k.op("dve", lambda e: e.memset(VA[:, 0:16, 0, 64:128], 1.0), writes=[VAb])
    k.op("dve", lambda e: e.memset(VA[:, 0:16, 1, 0:64], 1.0), writes=[VAb])
    emit_qkv0(P, C, H5, win_d[0], QT, QTb, KT, KTb, VA, VAb, gq, gk, flags)
    k.barrier()
    emit_attn0(P, C, QT, QTb, KT, KTb, VA, VAb, EXPB, P.WBb[0], EXS, Ht, MIXb)
    for i in (1, 2):
        P.WBb[i].r = dict(P.WBb[0].r)
        P.WBb[i].w = P.WBb[0].w
    if stop_after == "mix":
        dbg = nc.dram_tensor("dbg", [128, 8, 2048], BF16, kind="ExternalOutput").ap()
        ob2 = Buf("dbg")
        k.dma("sp", [(dbg[:, 0:4, :], Ht[:, 0:4, 0:2048]), (dbg[:, 4:8, :], MIXHI)], reads=MIXb, writes=[ob2], sembuf=ob2)
        k.wait_all("sp", [ob2])
        return finish()
    emit_outproj(P, C, X4, mix_ap, MIXb, wout_d[0], mcol(0, 1, 2))
    if stop_after == "mixer":
        return finish()
    k.barrier()
    emit_adaln(P, X4, H4, AV[:, 0, 2], mcol(0, 2, 0), C)
    emit_ffn(P, X4, H4, U, Ub, w1_d[0, 1], w3_d[0, 1], w2_d[0, 1], mcol(0, 2, 2), C)
    if not do_l1:
        return finish()

    rope_d = dt_in("rope", [128, 2, 2048])
    qo_d = nc.dram_tensor("qo", [128, 4, 2048], BF16, kind="ExternalOutput").ap()
    ko_d = nc.dram_tensor("ko", [128, 2048], BF16, kind="ExternalOutput").ap()
    vo_d = nc.dram_tensor("vo", [16, 128, 128], BF16, kind="ExternalOutput").ap()
    hco_d = nc.dram_tensor("hco", [12, 128, 2048], F32, kind="ExternalOutput").ap()
    layer_mod(1)
    emit_adaln(P, X4, H4, AV[:, 1, 0], mcol(1, 0, 0), C)
    emit_ffn(P, X4, H4, U, Ub, w1_d[1, 0], w3_d[1, 0], w2_d[1, 0], mcol(1, 0, 2), C)
    k.barrier()
    ROPE = UQ[:, 0:8192].bitcast(F32).rearrange("p (a t) -> p a t", a=2)
    ropeb = Buf("rope")
    k.dma("sp", [(ROPE, rope_d)], writes=[ropeb])
    emit_adaln(P, X4, H4, AV[:, 1, 1], mcol(1, 1, 0), C)
    outb = Sink("outs")
    modo_d = nc.dram_tensor("modo", [128, 96], F32, kind="ExternalOutput").ap()
    k.dma("sp", [(modo_d[:, 0:72], MODV[:, 1, :]), (modo_d[:, 72:96], AV[:, 1].rearrange("p i c -> p (i c)"))],
          reads=[C.modb], sembuf=C.modb, sink=outb)
    emit_inproj1(P, C, H4, win_d[1], CF[:, 240:241], CF[:, 241:242], ROPE[:, 0, :], ROPE[:, 1, :], ropeb,
                 qo_d, ko_d, vo_d, hco_d, outb)
    k.wait_all("sp", [outb])
    return finish()


def host_consts():
    idx = np.arange(512)
    rel = 255 - idx
    valid = (np.abs(rel) <= 128) & (idx < 511)
    bucket = t5_bucket_np(rel.astype(np.int64))
    oh = np.zeros((32, 512), np.float32)
    oh[bucket[valid], idx[valid]] = 1.0
    vm = np.tile(valid.astype(np.float32)[None, :], (128, 1))
    return oh, vm


def fm(v):
    v = np.asarray(v, np.float32)
    sh = v.shape
    v = v.reshape(sh[:-1] + (sh[-1] // 128, 128))
    return np.moveaxis(v, -1, 0)


def prep_LA(inp):
    x = np.asarray(inp["x"], np.float32)
    qperm = np.concatenate([np.r_[hh * 64:(hh + 1) * 64, (4 + hh) * 64:(5 + hh) * 64] for hh in range(4)])
    win = np.ascontiguousarray(np.asarray(inp["w_in"], np.float32))
    win = np.concatenate([win[:, :, qperm], win[:, :, 512:]], axis=2)
    eo = np.r_[0:64:2, 1:64:2]
    eo_q = np.concatenate([h * 64 + eo for h in range(8)])
    eo_k = np.concatenate([512 + h * 64 + eo for h in range(2)])
    win[1] = np.concatenate([win[1][:, eo_q], win[1][:, eo_k], win[1][:, 640:]], axis=1)
    wout = np.asarray(inp["w_out"], np.float32)
    wout = np.ascontiguousarray(np.concatenate([wout[:, qperm, :], wout[:, 512:, :]], axis=1))
    pad = DFFP - DFF
    w1 = np.pad(np.asarray(inp["ffn_w1"], np.float32), ((0, 0), (0, 0), (0, 0), (0, pad)))
    w3 = np.pad(np.asarray(inp["ffn_w3"], np.float32), ((0, 0), (0, 0), (0, 0), (0, pad)))
    w2 = np.pad(np.asarray(inp["ffn_w2"], np.float32), ((0, 0), (0, 0), (0, pad), (0, 0)))
    wmod = np.ascontiguousarray(np.asarray(inp["w_mod"], np.float32))
    oh, vm = host_consts()
    shared = dict(oh=oh, vm=vm, wmod=wmod, w1=w1, w3=w3, w2=w2, win=np.ascontiguousarray(win), wout=wout)
    maps = []
    for core in range(NCORE):
        b, qd = core // 4, core % 4
        t0 = qd * TOK
        xs = x[b, t0:t0 + TOK]
        xT = np.ascontiguousarray(xs.T.reshape(8, 128, TOK).transpose(1, 0, 2))
        halo = np.zeros((256, D), np.float32)
        fl = np.zeros((2,), np.float32)
        if qd > 0:
            halo[0:128] = x[b, t0 - 128:t0]
            fl[0] = 1.0
        if qd < 3:
            halo[128:256] = x[b, t0 + TOK:t0 + TOK + 128]
            fl[1] = 1.0
        xH = np.ascontiguousarray(halo.T.reshape(8, 128, 256).transpose(1, 0, 2))
        cf = np.zeros((128, 1400), np.float32)
        cf[:, 0:8] = fm(inp["c"][b])
        cf[:, 8:10] = fl[None, :]
        cf[:, 10:58] = fm(inp["norm_g"]).reshape(128, 48)
        cf[:, 58:202] = fm(inp["b_mod"]).reshape(128, 144)
        cw = np.asarray(inp["b_conv_w"], np.float32)[0]
        cf[:, 202:214] = fm(cw).transpose(0, 2, 1).reshape(128, 12)
        cf[:, 214:218] = fm(np.asarray(inp["b_conv_b"], np.float32)[0])
        aq = np.asarray(inp["a_qk_g"], np.float32)[0]
        cf[:, 218] = np.tile(aq[0], 2)
        cf[:, 220] = np.tile(aq[1], 2)
        cf[:, 222:230] = np.asarray(inp["a_sink"], np.float32)[0][None, :]
        cf[0:32, 232:240] = np.asarray(inp["rel_table"], np.float32)
        cg = np.asarray(inp["c_qk_g"], np.float32)[0]
        cf[:, 240] = np.tile(cg[0][eo], 2)
        cf[:, 241] = np.tile(cg[1][eo], 2)
        pos = np.arange(t0, t0 + TOK)
        row = (pos // 64).astype(np.float32)
        col = (pos % 64).astype(np.float32)
        inv = (np.float32(10000.0) ** (-np.arange(0, 32, 2, dtype=np.float32) / np.float32(32))).astype(np.float32)
        ang = np.concatenate([row[:, None] * inv, col[:, None] * inv], axis=-1).astype(np.float32)
        cs = np.cos(ang).astype(np.float32).T
        sn = np.sin(ang).astype(np.float32).T
        rope = np.zeros((128, 2, TOK), np.float32)
        for qd4 in range(4):
            rope[qd4 * 32:(qd4 + 1) * 32, 0] = cs
            rope[qd4 * 32:(qd4 + 1) * 32, 1] = sn if qd4 % 2 == 0 else -sn
        m_rope = rope
        m = dict(shared)
        m.update(xT=xT, xH=xH, cf=cf, rope=m_rope)
        maps.append(m)
    return maps


def emit_rope(P, C, qn, qnb, CS, SNs, ropeb, c0, w, out_ap, out_b):
    k = P.k
    t1, t1b = C.tmp()
    k.op("dve", lambda e: e.tensor_tensor(out=t1[:, 0:w], in0=qn[:, 0:w], in1=CS[:, c0:c0 + w], op=ALU.mult),
         reads=[qnb, ropeb], writes=[t1b])
    t2, t2b = C.tmp()
    for qd in range(4):
        src = qd ^ 1
        k.op("dve", lambda e, qd=qd, src=src: e.tensor_tensor(
            out=t2[qd * 32:(qd + 1) * 32, 0:w], in0=qn[src * 32:(src + 1) * 32, 0:w],
            in1=SNs[src * 32:(src + 1) * 32, c0:c0 + w], op=ALU.mult), reads=[qnb, ropeb], writes=[t2b])
    k.op("dve", lambda e: e.tensor_tensor(out=out_ap, in0=t1[:, 0:w], in1=t2[:, 0:w], op=ALU.add),
         reads=[t1b, t2b], writes=[out_b])


def emit_inproj1(P, C, H, win_l, gq, gk, CS, SNs, ropeb, qo_d, ko_d, vo_d, hco_d, outb):
    k = P.k
    for pr in range(3):
        wa, wab = P.load_wa(win_l[:, pr * 256:(pr + 1) * 256])
        for cl in range(2):
            if pr == 2 and cl == 1:
                break
            for j in range(4):
                c0, w = H.tiles[j]
                ps, psb = P.bank("h1", [0, 1, 2, 3])
                emit_inproj_tile(P, H, j, wa, wab, cl, ps, psb)
                qn, qnb = C.tmp()
                emit_qknorm(P, C, ps, psb, w, (gq if pr < 2 else gk)[:, 0:1], qn[:, 0:w], qnb)
                st, stb = C.tmpbf()
                emit_rope(P, C, qn, qnb, CS, SNs, ropeb, c0, w, st[:, 0:w], stb)
                dst = qo_d[:, pr * 2 + cl, c0:c0 + w] if pr < 2 else ko_d[:, c0:c0 + w]
                k.dma("sp", [(dst, st[:, 0:w])], reads=[stb], sembuf=stb, sink=outb)
        if pr == 2:
            for tb in range(16):
                j = tb // 4
                ps, psb = P.bank("o", [4, 5])
                for c in range(8):
                    k.op("pe", lambda e, c=c, tb=tb, ps=ps: e.matmul(
                        ps[:, 0:128], lhsT=H.t[:, c, tb * 128:(tb + 1) * 128], rhs=wa[:, c, 128:256],
                        start=(c == 0), stop=(c == 7)), reads=[wab, H.b[j]], writes=[psb])
                st, stb = C.tmpbf()
                k.op("act", lambda e, ps=ps, st=st: e.activation(out=st[:, 0:128], in_=ps[:, 0:128], func=AF.Copy),
                     reads=[psb], writes=[stb])
                k.dma("sp", [(vo_d[tb], st[:, 0:128])], reads=[stb], sembuf=stb, sink=outb)
    for pr in range(6):
        wa, wab = P.load_wa(win_l[:, 768 + pr * 256:768 + (pr + 1) * 256])
        for cl in range(2):
            ch = pr * 2 + cl
            for j in range(4):
                c0, w = H.tiles[j]
                ps, psb = P.bank("h1", [0, 1, 2, 3])
                emit_inproj_tile(P, H, j, wa, wab, cl, ps, psb)
                st, stb = C.tmp()
                k.op("act", lambda e, ps=ps, st=st, w=w: e.activation(out=st[:, 0:w], in_=ps[:, 0:w], func=AF.Copy),
                     reads=[psb], writes=[stb])
                k.dma("sp", [(hco_d[ch, :, c0:c0 + w], st[:, 0:w])], reads=[stb], sembuf=stb, sink=outb)


NFFT = 16384


def build_LB():
    nc = bass.Bass("TRN2", target_bir_lowering=False)
    k = KB(nc)
    dt_in = lambda name, shape: nc.dram_tensor(name, list(shape), F32, kind="ExternalInput").ap()
    hc_d = dt_in("hc3", [3, 128, S])
    cf_d = dt_in("cf2", [128, 32])
    mlp_d = dt_in("mlpw", [64, 456])
    zemb_d = dt_in("zemb", [33, NFFT])
    win_d = dt_in("win", [128, 128, 128])
    dft_d = dt_in("dftc", [128, 512])
    tw_d = dt_in("tw", [128, 2, 2, 128])
    y_d = nc.dram_tensor("yT", [128, S], F32, kind="ExternalOutput").ap()
    zs = nc.dram_tensor("zs", [128, S], F32, kind="Internal")
    cs = nc.dram_tensor("cs", [128, S], F32, kind="Internal")

    PS = [k.ps(f"ps{i}", [128, 512]) for i in range(8)]
    PSb = [Buf(f"ps{i}") for i in range(8)]
    rr = {}

    def bank(group, banks):
        i = rr.get(group, 0)
        rr[group] = (i + 1) % len(banks)
        return PS[banks[i]], PSb[banks[i]]

    tmp = RotTmp(k, "tf", 8, F32)
    tmpbf = RotTmp(k, "tb", 4, BF16)
    cb = Buf("consts")
    CF = k.sb("CF", [128, 32], F32)
    MLP = k.sb("MLP", [64, 456], F32)
    DFT = k.sb("DFT", [128, 512], BF16)
    TW = k.sb("TW", [128, 2, 2, 128], F32)
    X0 = k.sb("X0", [128, S], F32)
    Z = k.sb("Z", [128, S], F32)
    IN = k.sb("IN", [128, S + 2], F32)
    KC = k.sb("KC", [128, 128, 128], BF16)
    ZC = k.sb("ZC", [64, 128, 128], BF16)
    ACC = k.sb("ACC", [128, 128], F32)
    SM = k.sb("SM", [128, 16], F32)
    ones_f = k.sb("ones_f", [128, 1], F32)
    X0b, Zb, INb, KCb, ZCb, ACCb, SMb = [Buf(n) for n in "X0 Z IN KC ZC ACC SM".split()]

    k.dma("sp", [(CF[:], cf_d), (MLP[:], mlp_d), (TW[:], tw_d)], writes=[cb])
    dftb = Buf("dft")
    k.dma("pool", [(DFT[:], dft_d)], writes=[dftb])
    Fre, Fim, nFim = DFT[:, 0:128], DFT[:, 128:256], DFT[:, 384:512]
    Fcat, FcatI2, FcatI1 = DFT[:, 0:256], DFT[:, 128:384], DFT[:, 256:512]
    k.op("dve", lambda e: e.memset(ones_f[:], 1.0), writes=[SMb])
    k.op("dve", lambda e: e.memset(SM[:, 0:1], math.pi / 2), writes=[SMb])
    k.op("dve", lambda e: e.memset(SM[:, 1:2], EPS), writes=[SMb])
    k.op("dve", lambda e: e.memset(IN[:, 0:1], 0.0), writes=[INb])
    k.op("dve", lambda e: e.memset(IN[:, S + 1:S + 2], 0.0), writes=[INb])

    CH = 2048
    for part in range(3):
        k.dma("sp", [(IN[:, 1:S + 1], hc_d[part])], writes=[INb])
        for cc in range(S // CH):
            c0 = cc * CH
            wc = CF[:, part * 4:part * 4 + 3]
            bc = CF[:, part * 4 + 3:part * 4 + 4]
            for s0 in range(0, CH, 512):
                a0 = c0 + s0
                if part == 0:
                    o, ob_ = X0[:, a0:a0 + 512], X0b
                elif part == 1:
                    o, ob_ = Z[:, a0:a0 + 512], Zb
                else:
                    tt, ttb = tmp()
                    o, ob_ = tt[:, 0:512], ttb
                k.op("dve", lambda e, o=o, a0=a0, wc=wc, bc=bc: e.tensor_scalar(
                    out=o, in0=IN[:, a0 + 1:a0 + 513], scalar1=wc[:, 1:2], scalar2=bc, op0=ALU.mult, op1=ALU.add),
                    reads=[INb, cb], writes=[ob_])
                k.op("dve", lambda e, o=o, a0=a0, wc=wc: e.scalar_tensor_tensor(
                    out=o, in0=IN[:, a0:a0 + 512], scalar=wc[:, 0:1], in1=o, op0=ALU.mult, op1=ALU.add),
                    reads=[INb, cb, ob_], writes=[ob_])
                k.op("dve", lambda e, o=o, a0=a0, wc=wc: e.scalar_tensor_tensor(
                    out=o, in0=IN[:, a0 + 2:a0 + 514], scalar=wc[:, 2:3], in1=o, op0=ALU.mult, op1=ALU.add),
                    reads=[INb, cb, ob_], writes=[ob_])
                if part == 2:
                    k.op("pool", lambda e, o=o, a0=a0: e.tensor_tensor(
                        out=Z[:, a0:a0 + 512], in0=Z[:, a0:a0 + 512], in1=o, op=ALU.mult), reads=[ob_, Zb], writes=[Zb])
    zsb = Buf("zs")
    k.dma("sp", [(zs.ap(), Z[:])], reads=[Zb], writes=[zsb])
    src = bass.AP(tensor=zs, offset=0, ap=[[128, 64], [S, 128], [1, 128]])
    k.dma("pool", [(ZC[:], src)], reads=[zsb], writes=[ZCb])

    W1, W2, W3 = MLP[0:33, 0:64], MLP[:, 64:128], MLP[:, 128:192]
    W4 = MLP[:, 200:456].rearrange("p (d c) -> p d c", d=2)
    k.op("dve", lambda e: e.tensor_scalar(out=SM[0:64, 2:3], in0=MLP[:, 195:196], scalar1=0.25, scalar2=None, op0=ALU.mult),
         reads=[cb], writes=[SMb])
    for i in range(3):
        k.op("dve", lambda e, i=i: e.tensor_tensor(out=SM[0:64, 3 + i:4 + i], in0=MLP[:, 192 + i:193 + i], in1=SM[0:64, 2:3],
                                                   op=ALU.mult), reads=[cb, SMb], writes=[SMb])
    k.op("dve", lambda e: e.memset(ACC[:], 0.0), writes=[ACCb])

    def sin4(ps, psb, li):
        s1, s1b = tmp()
        k.op("act", lambda e: e.activation(out=s1[0:64, :], in_=ps[0:64, :], func=AF.Sin, bias=SM[0:64, 3 + li:4 + li],
                                           scale=SM[0:64, 2:3]), reads=[psb, SMb], writes=[s1b])
        a1, a1b = tmp()
        k.op("act", lambda e: e.activation(out=a1[0:64, :], in_=ps[0:64, :], func=AF.Abs, bias=SM[0:64, 3 + li:4 + li],
                                           scale=SM[0:64, 2:3]), reads=[psb, SMb], writes=[a1b])
        k.op("act", lambda e: e.activation(out=a1[0:64, :], in_=a1[0:64, :], func=AF.Sin, bias=SM[0:64, 0:1], scale=-1.0),
             reads=[a1b, SMb], writes=[a1b])
        k.op("dve", lambda e: e.tensor_tensor(out=a1[0:64, :], in0=a1[0:64, :], in1=s1[0:64, :], op=ALU.mult),
             reads=[a1b, s1b], writes=[a1b])
        k.op("dve", lambda e: e.tensor_tensor(out=s1[0:64, :], in0=s1[0:64, :], in1=s1[0:64, :], op=ALU.mult),
             reads=[s1b], writes=[s1b])
        k.op("dve", lambda e: e.tensor_scalar(out=s1[0:64, :], in0=s1[0:64, :], scalar1=-2.0, scalar2=1.0,
                                              op0=ALU.mult, op1=ALU.add), reads=[s1b], writes=[s1b])
        k.op("dve", lambda e: e.scalar_tensor_tensor(out=a1[0:64, :], in0=a1[0:64, :], scalar=4.0, in1=s1[0:64, :],
                                                     op0=ALU.mult, op1=ALU.mult), reads=[a1b, s1b], writes=[a1b])
        return a1, a1b

    for c in range(32):
        ze, zeb = tmp()
        k.dma("sp", [(ze[0:33, :], zemb_d[:, c * 512:(c + 1) * 512])], writes=[zeb])
        wt, wtb = tmp()
        k.dma("sp", [(wt[:, :].rearrange("p (n c) -> p n c", n=4), win_d[:, c * 4:(c + 1) * 4, :])], writes=[wtb])
        ps, psb = bank("m", [6, 7])
        k.op("pe", lambda e, ps=ps, ze=ze: e.matmul(ps[0:64, :], lhsT=W1, rhs=ze[0:33, :], start=True, stop=True),
             reads=[zeb, cb], writes=[psb])
        h, hb = sin4(ps, psb, 0)
        for li, W in ((1, W2), (2, W3)):
            ps, psb = bank("m", [6, 7])
            k.op("pe", lambda e, ps=ps, h=h, W=W: e.matmul(ps[0:64, :], lhsT=W, rhs=h[0:64, :], start=True, stop=True),
                 reads=[hb, cb], writes=[psb])
            h, hb = sin4(ps, psb, li)
        ps4, ps4b = bank("m", [6, 7])
        for n2l in range(4):
            for d in range(2):
                k.op("pe", lambda e, ps4=ps4, h=h, n2l=n2l, d=d: e.matmul(
                    ps4[d * 64:(d + 1) * 64, n2l * 128:(n2l + 1) * 128],
                    lhsT=h[0:64, n2l * 128 + d * 64:n2l * 128 + (d + 1) * 64], rhs=W4[:, d, :],
                    start=True, stop=True, skip_group_check=True), reads=[hb, cb], writes=[ps4b])
        kc, kcb = tmp()
        k.op("dve", lambda e, kc=kc, ps4=ps4, wt=wt: e.tensor_tensor(out=kc[:, :], in0=ps4[:, :], in1=wt[:, :], op=ALU.mult),
             reads=[ps4b, wtb], writes=[kcb])
        k.op("act", lambda e, kc=kc, c=c: e.activation(
            out=KC[:, :, c * 4:(c + 1) * 4], in_=kc[:, :].rearrange("p (n c) -> p c n", n=4), func=AF.Copy),
            reads=[kcb], writes=[KCb])
        sq, sqb = tmp()
        k.op("pool", lambda e, kc=kc, sq=sq: e.tensor_tensor(out=sq[:, :], in0=kc[:, :], in1=kc[:, :], op=ALU.mult),
             reads=[kcb], writes=[sqb])
        rd, rdb = tmp()
        k.op("dve", lambda e, sq=sq, rd=rd: e.tensor_reduce(
            out=rd[:, 0:128], in_=sq[:, :].rearrange("p (n c) -> p c n", n=4), axis=mybir.AxisListType.X, op=ALU.add),
            reads=[sqb], writes=[rdb])
        k.op("pool", lambda e, rd=rd: e.tensor_tensor(out=ACC[:], in0=ACC[:], in1=rd[:, 0:128], op=ALU.add),
             reads=[rdb, ACCb], writes=[ACCb])
    ps, psb = bank("m", [6, 7])
    k.op("pe", lambda e, ps=ps: e.matmul(ps[:, 0:1], lhsT=ACC[:], rhs=ones_f[:, 0:1], start=True, stop=True),
         reads=[ACCb, SMb], writes=[psb])
    k.op("act", lambda e, ps=ps: e.activation(out=SM[:, 8:9], in_=ps[:, 0:1], func=AF.Sqrt, bias=SM[:, 1:2], scale=1.0),
         reads=[psb, SMb], writes=[SMb])
    k.op("dve", lambda e: e.reciprocal(out=SM[:, 8:9], in_=SM[:, 8:9]), reads=[SMb], writes=[SMb])

    Bre = k.sb("Bre", [128, 4, 128], BF16)
    Bim = k.sb("Bim", [128, 4, 128], BF16)
    Hre = k.sb("Hre", [128, 512], F32)
    Him = k.sb("Him", [128, 512], F32)
    Yre = k.sb("Yre", [128, 4, 128], BF16)
    Yim = k.sb("Yim", [128, 4, 128], BF16)
    Qre = k.sb("Qre", [128, 4, 128], BF16)
    Qim = k.sb("Qim", [128, 4, 128], BF16)
    Bb, Hb, Yb, Qb = Buf("B"), Buf("H"), Buf("Y"), Buf("Q")
    TWre2, TWim2 = TW[:, 0], TW[:, 1]

    def cmul_from_psum(A, Ab, sign, outre, outim, outb, pr):
        A4 = A[:, :].rearrange("p (c r k) -> p c r k", c=2, r=2)
        Are, Aim = A4[:, :, 0, :], A4[:, :, 1, :]
        t1, t1b = tmp()
        t2, t2b = tmp()
        v = lambda t: t[:, 0:256].rearrange("p (c k) -> p c k", c=2)
        k.op("dve", lambda e: e.tensor_tensor(out=v(t1), in0=Are, in1=TWre2, op=ALU.mult), reads=[Ab, cb], writes=[t1b])
        k.op("dve", lambda e: e.tensor_tensor(out=v(t2), in0=Aim, in1=TWim2, op=ALU.mult), reads=[Ab, cb], writes=[t2b])
        k.op("pool", lambda e: e.tensor_tensor(out=outre[:, pr * 2:pr * 2 + 2, :], in0=v(t1), in1=v(t2),
                                               op=(ALU.subtract if sign > 0 else ALU.add)),
             reads=[t1b, t2b], writes=[outb])
        t3, t3b = tmp()
        t4, t4b = tmp()
        k.op("dve", lambda e: e.tensor_tensor(out=v(t3), in0=Aim, in1=TWre2, op=ALU.mult), reads=[Ab, cb], writes=[t3b])
        k.op("dve", lambda e: e.tensor_tensor(out=v(t4), in0=Are, in1=TWim2, op=ALU.mult), reads=[Ab, cb], writes=[t4b])
        k.op("pool", lambda e: e.tensor_tensor(out=outim[:, pr * 2:pr * 2 + 2, :], in0=v(t3), in1=v(t4),
                                               op=(ALU.add if sign > 0 else ALU.subtract)),
             reads=[t3b, t4b], writes=[outb])

    def fft_fwd(src, srcb, krows, ch0, xre, xreb, xim, ximb):
        for pr in range(2):
            A, Ab = bank("A", [0, 1])
            for cl in range(2):
                ch = ch0 + pr * 2 + cl
                k.op("pe", lambda e, A=A, cl=cl, ch=ch: e.matmul(
                    A[:, cl * 256:(cl + 1) * 256], lhsT=src[0:krows, ch, :], rhs=Fcat[0:krows, :],
                    start=True, stop=True, skip_group_check=True), reads=[srcb, dftb], writes=[Ab])
            cmul_from_psum(A, Ab, +1, Bre, Bim, Bb, pr)
        bre = Bre[:, :, :].rearrange("p c k -> p (c k)")
        bim = Bim[:, :, :].rearrange("p c k -> p (c k)")
        k.op("pe", lambda e: e.matmul(xre[:, :], lhsT=Fre, rhs=bre, start=True, stop=False), reads=[Bb, dftb], writes=[xreb])
        k.op("pe", lambda e: e.matmul(xre[:, :], lhsT=nFim, rhs=bim, start=False, stop=True), reads=[Bb, dftb], writes=[xreb])
        k.op("pe", lambda e: e.matmul(xim[:, :], lhsT=Fim, rhs=bre, start=True, stop=False), reads=[Bb, dftb], writes=[ximb])
        k.op("pe", lambda e: e.matmul(xim[:, :], lhsT=Fre, rhs=bim, start=False, stop=True), reads=[Bb, dftb], writes=[ximb])

    csb = Sink("cs")
    for g in range(32):
        ch0 = g * 4
        fft_fwd(KC, KCb, 128, ch0, PS[2], PSb[2], PS[3], PSb[3])
        k.op("act", lambda e: e.activation(out=Hre[:], in_=PS[2][:, :], func=AF.Copy), reads=[PSb[2]], writes=[Hb])
        k.op("act", lambda e: e.activation(out=Him[:], in_=PS[3][:, :], func=AF.Copy), reads=[PSb[3]], writes=[Hb])
        fft_fwd(ZC, ZCb, 64, ch0, PS[4], PSb[4], PS[5], PSb[5])
        t1, t1b = tmp()
        t2, t2b = tmp()
        k.op("dve", lambda e, t1=t1: e.tensor_tensor(out=t1[:, :], in0=PS[4][:, :], in1=Hre[:], op=ALU.mult),
             reads=[PSb[4], Hb], writes=[t1b])
        k.op("dve", lambda e, t2=t2: e.tensor_tensor(out=t2[:, :], in0=PS[5][:, :], in1=Him[:], op=ALU.mult),
             reads=[PSb[5], Hb], writes=[t2b])
        k.op("pool", lambda e, t1=t1, t2=t2: e.tensor_tensor(out=Yre[:, :, :].rearrange("p c k -> p (c k)"), in0=t1[:, :],
                                                             in1=t2[:, :], op=ALU.subtract), reads=[t1b, t2b], writes=[Yb])
        t3, t3b = tmp()
        t4, t4b = tmp()
        k.op("dve", lambda e, t3=t3: e.tensor_tensor(out=t3[:, :], in0=PS[4][:, :], in1=Him[:], op=ALU.mult),
             reads=[PSb[4], Hb], writes=[t3b])
        k.op("dve", lambda e, t4=t4: e.tensor_tensor(out=t4[:, :], in0=PS[5][:, :], in1=Hre[:], op=ALU.mult),
             reads=[PSb[5], Hb], writes=[t4b])
        k.op("pool", lambda e, t3=t3, t4=t4: e.tensor_tensor(out=Yim[:, :, :].rearrange("p c k -> p (c k)"), in0=t3[:, :],
                                                             in1=t4[:, :], op=ALU.add), reads=[t3b, t4b], writes=[Yb])
        for pr in range(2):
            Pk, Pkb = bank("A", [0, 1])
            for cl in range(2):
                c4 = pr * 2 + cl
                k.op("pe", lambda e, Pk=Pk, cl=cl, c4=c4: e.matmul(
                    Pk[:, cl * 256:(cl + 1) * 256], lhsT=Yre[:, c4, :], rhs=FcatI1, start=True, stop=False,
                    skip_group_check=True), reads=[Yb, dftb], writes=[Pkb])
                k.op("pe", lambda e, Pk=Pk, cl=cl, c4=c4: e.matmul(
                    Pk[:, cl * 256:(cl + 1) * 256], lhsT=Yim[:, c4, :], rhs=FcatI2, start=False, stop=True,
                    skip_group_check=True), reads=[Yb, dftb], writes=[Pkb])
            cmul_from_psum(Pk, Pkb, -1, Qre, Qim, Qb, pr)
        yo, yob = PS[6], PSb[6]
        k.op("pe", lambda e: e.matmul(yo[0:64, :], lhsT=DFT[:, 0:64], rhs=Qre[:, :, :].rearrange("p c k -> p (c k)"),
                                      start=True, stop=False), reads=[Qb, dftb], writes=[yob])
        k.op("pe", lambda e: e.matmul(yo[0:64, :], lhsT=DFT[:, 128:192], rhs=Qim[:, :, :].rearrange("p c k -> p (c k)"),
                                      start=False, stop=True), reads=[Qb, dftb], writes=[yob])
        ys, ysb = tmp()
        k.op("act", lambda e, ys=ys: e.activation(out=ys[0:64, :], in_=yo[0:64, :], func=AF.Copy, scale=1.0 / NFFT),
             reads=[yob], writes=[ysb])
        dst = bass.AP(tensor=cs, offset=ch0 * S, ap=[[128, 64], [S, 4], [1, 128]])
        k.dma("sp", [(dst, ys[0:64, :].rearrange("p (c k) -> p c k", c=4))], reads=[ysb], sembuf=ysb, sink=csb)

    k.wait_all("sp", [csb])
    k.dma("sp", [(IN[:, 0:S], cs.ap())], writes=[INb])
    ob = Sink("out")
    for s0 in range(0, S, 512):
        t, tb = tmp()
        k.op("dve", lambda e, t=t, s0=s0: e.tensor_scalar(out=t[:, :], in0=Z[:, s0:s0 + 512], scalar1=CF[:, 12:13],
                                                          scalar2=None, op0=ALU.mult), reads=[Zb, cb], writes=[tb])
        k.op("dve", lambda e, t=t, s0=s0: e.scalar_tensor_tensor(out=t[:, :], in0=IN[:, s0:s0 + 512], scalar=SM[:, 8:9],
                                                                 in1=t[:, :], op0=ALU.mult, op1=ALU.add),
             reads=[INb, SMb, tb], writes=[tb])
        k.op("pool", lambda e, t=t, s0=s0: e.tensor_tensor(out=t[:, :], in0=t[:, :], in1=X0[:, s0:s0 + 512], op=ALU.mult),
             reads=[tb, X0b], writes=[tb])
        k.dma("sp", [(y_d[:, s0:s0 + 512], t[:, :])], reads=[tb], sembuf=tb, sink=ob)
    k.wait_all("sp", [ob])
    k.close()
    return nc


def host_consts_LB(cq):
    f32 = np.float32
    L = S
    m = np.arange(NFFT)
    lag = np.where(m < L, m, NFFT - m)
    lag = np.where(m == L, 0, lag)
    t_all = np.linspace(0.0, 1.0, L, dtype=f32)
    t = t_all[lag]
    w = (f32(2.0 * math.pi / L) * lag.astype(f32)).astype(f32)
    fr = np.linspace(1e-4, 15, 16, dtype=f32)
    zf = np.concatenate([t[:, None], np.cos(fr[None, :] * w[:, None]), -np.sin(fr[None, :] * w[:, None])], axis=-1).astype(f32)
    zemb = np.ascontiguousarray(zf.reshape(128, 128, 33).transpose(2, 1, 0).reshape(33, NFFT))
    dmin, dmax = math.log(1e-2) / 0.3, math.log(1e-2) / 1.5
    deltas = np.abs(np.linspace(dmin, dmax, 512, dtype=f32))[cq * 128:(cq + 1) * 128]
    win = np.exp(-t[:, None] * deltas[None, :]).astype(f32)
    win[L] = 0.0
    win = np.ascontiguousarray(win.reshape(128, 128, 128))
    n = np.arange(128)
    ang = 2.0 * np.pi * np.outer(n, n) / 128.0
    fre, fim = np.cos(ang), -np.sin(ang)
    dftc = np.concatenate([fre, fim, fre, -fim], axis=1).astype(f32)
    ang2 = 2.0 * np.pi * np.outer(n, n) / NFFT
    tw = np.stack([np.cos(ang2), -np.sin(ang2)], 0).astype(f32)
    tw = np.ascontiguousarray(np.broadcast_to(tw[:, None], (2, 2, 128, 128)).transpose(2, 0, 1, 3))
    return zemb, win, dftc, tw


def prep_LB(inp, hco_all):
    maps = []
    cache = {}
    for core in range(NCORE):
        b, cq = core // 4, core % 4
        if cq not in cache:
            cache[cq] = host_consts_LB(cq)
        zemb, win, dftc, tw = cache[cq]
        hc3 = np.zeros((3, 128, S), np.float32)
        for part in range(3):
            for src in range(4):
                hc3[part, :, src * TOK:(src + 1) * TOK] = hco_all[b * 4 + src][part * 4 + cq]
        cf2 = np.zeros((128, 32), np.float32)
        cw = np.asarray(inp["d_conv_w"], np.float32)[0]
        cbias = np.asarray(inp["d_conv_b"], np.float32)[0]
        for part in range(3):
            sl = slice(part * 512 + cq * 128, part * 512 + (cq + 1) * 128)
            cf2[:, part * 4:part * 4 + 3] = cw[:, sl].T
            cf2[:, part * 4 + 3] = cbias[sl]
        cf2[:, 12] = np.asarray(inp["d_skip"], np.float32)[0][cq * 128:(cq + 1) * 128]
        mlp = np.zeros((64, 456), np.float32)
        mlp[0:33, 0:64] = np.asarray(inp["d_f_w1"], np.float32)[0]
        mlp[:, 64:128] = np.asarray(inp["d_f_w2"], np.float32)[0]
        mlp[:, 128:192] = np.asarray(inp["d_f_w3"], np.float32)[0]
        mlp[:, 192] = np.asarray(inp["d_f_b1"], np.float32)[0]
        mlp[:, 193] = np.asarray(inp["d_f_b2"], np.float32)[0]
        mlp[:, 194] = np.asarray(inp["d_f_b3"], np.float32)[0]
        mlp[:, 195] = np.asarray(inp["d_f_freq"], np.float32)[0]
        w4 = np.asarray(inp["d_f_w4"], np.float32)[0]
        mlp[:, 200:328] = w4[:, cq * 128:(cq + 1) * 128]
        mlp[:, 328:456] = w4[:, 512 + cq * 128:512 + (cq + 1) * 128]
        maps.append(dict(hc3=hc3, cf2=cf2, mlpw=mlp, zemb=zemb, win=win, dftc=dftc, tw=tw))
    return maps


def emit_attn1(P, C, QT, QTb, KT, KTb, VA, VAb, MIX, MIXb):
    k = P.k
    for jq in range(4):
        q0 = jq * 512
        for g in range(2):
            gs = slice(g * 64, (g + 1) * 64)
            ds_ = slice((1 - g) * 64, (2 - g) * 64)
            for hh in range(4):
                po, pob = P.bank("o", [4, 5])
                for kb in range(64):
                    ps, psb = P.bank("h1", [0, 1, 2, 3])
                    k.op("pe", lambda e, ps=ps, kb=kb, gs=gs, hh=hh, q0=q0: e.matmul(
                        ps[:, :], lhsT=KT[gs, kb * 128:(kb + 1) * 128], rhs=QT[gs, hh, q0:q0 + 512],
                        start=True, stop=True), reads=[KTb, QTb], writes=[psb])
                    pt, ptb = C.tmpbf()
                    k.op("act", lambda e, ps=ps, pt=pt: e.activation(out=pt[:, :], in_=ps[:, :], func=AF.Exp,
                                                                     scale=HD ** -0.5), reads=[psb], writes=[ptb])
                    k.op("pe", lambda e, po=po, pt=pt, kb=kb, g=g: e.matmul(
                        po[:, :], lhsT=VA[:, kb, g, :], rhs=pt[:, :], start=(kb == 0), stop=(kb == 63)),
                        reads=[VAb, ptb], writes=[pob])
                rd, rdb = C.tmp()
                k.op("dve", lambda e, rd=rd, po=po, ds_=ds_: e.reciprocal(out=rd[ds_, :], in_=po[ds_, :]),
                     reads=[pob], writes=[rdb])
                k.op("dve", lambda e, rd=rd, po=po, gs=gs, ds_=ds_, hh=hh, q0=q0: e.tensor_tensor(
                    out=MIX[gs, hh, q0:q0 + 512], in0=po[gs, :], in1=rd[ds_, :], op=ALU.mult),
                    reads=[pob, rdb], writes=[MIXb[jq]])


def build_LC():
    nc = bass.Bass("TRN2", target_bir_lowering=False)
    k = KB(nc)
    dt_in = lambda name, shape, dt=F32: nc.dram_tensor(name, list(shape), dt, kind="ExternalInput").ap()
    xT_d = dt_in("xT", [128, 8, 2048])
    mod_d = dt_in("modi", [128, 96])
    q_d = dt_in("q", [128, 4, 2048], BF16)
    k_d = dt_in("kk", [128, S], BF16)
    v_d = dt_in("v", [64, 128, 128], BF16)
    y_d = dt_in("yh", [128, 4, 2048])
    w1_d = dt_in("w1", [1024, DFFP])
    w3_d = dt_in("w3", [1024, DFFP])
    w2_d = dt_in("w2", [DFFP, 1024])
    wout_d = dt_in("wout", [1024, 1024])
    xo_d = nc.dram_tensor("xo", [128, 8, 2048], F32, kind="ExternalOutput").ap()

    P = Prog(nc, k)
    C = Common()
    C.tmp = RotTmp(k, "tf", 5, F32)
    C.tmpbf = RotTmp(k, "tb", 4, BF16)
    C.RS = k.sb("RS", [128, 512], F32)
    C.RSb = Buf("RS")
    C.cb = Buf("consts")
    C.modb = Buf("mod")
    C.ones_bf = k.sb("ones_bf", [128, 128], BF16)
    C.eps = k.sb("eps", [128, 1], F32)
    MOD = k.sb("mod", [128, 96], F32)
    XT = k.sb("XT", [128, 8, 2048], F32)
    AR = k.sb("AR", [128, 32768], BF16)
    MIXt = k.sb("MIX", [128, 8, 2048], BF16)
    VA = AR[:, 0:16384].rearrange("p (b g d) -> p b g d", b=64, g=2)
    KT = AR[:, 16384:24576]
    QT = AR[:, 24576:32768].rearrange("p (h t) -> p h t", h=4)
    Ht = AR[:, 0:16384].rearrange("p (c t) -> p c t", c=8)
    U = AR[:, 16384:16384 + 6 * 2048].rearrange("p (f t) -> p f t", f=6)

    tiles4 = [(0, 512), (512, 512), (1024, 512), (1536, 512)]
    XB = [Buf(f"x{j}") for j in range(4)]

    class XAct:
        tiles = tiles4
        b = XB

        def ap(self, c, j):
            return XT[:, c, j * 512:(j + 1) * 512]
    X4 = XAct()
    H4 = Act(Ht, tiles4, "h")
    Ub = [[Buf(f"u{f}_{j}") for j in range(4)] for f in range(6)]
    MIXb = [Buf(f"mix{j}") for j in range(4)]
    QTb, KTb, VAb = Buf("qt"), Buf("kt"), Buf("va")

    k.dma("sp", [(MOD[:], mod_d)], writes=[C.modb])
    k.dma("sp", [(QT, q_d)], writes=[QTb])
    k.dma("sp", [(KT, k_d)], writes=[KTb])
    k.op("dve", lambda e: e.memset(C.ones_bf[:], 1.0), writes=[C.cb])
    k.op("dve", lambda e: e.memset(C.eps[:], EPS), writes=[C.cb])
    k.op("dve", lambda e: e.memset(VA[:, :, 0, 64:128], 1.0), writes=[VAb])
    k.op("dve", lambda e: e.memset(VA[:, :, 1, 0:64], 1.0), writes=[VAb])
    vsrc = v_d.rearrange("b p d -> p b d")
    k.dma("sp", [(VA[:, :, 0, 0:64], vsrc[:, :, 0:64]), (VA[:, :, 1, 64:128], vsrc[:, :, 64:128])], writes=[VAb])
    for j in range(4):
        k.dma("sp", [(XT[:, :, j * 512:(j + 1) * 512], xT_d[:, :, j * 512:(j + 1) * 512])], writes=[XB[j]])
    for j in range(4):
        k.dma("pool", [(MIXt[:, 4:8, j * 512:(j + 1) * 512], y_d[:, :, j * 512:(j + 1) * 512])], writes=[MIXb[j]])

    def mcol(i, kind):
        return MOD[:, (i * 3 + kind) * 8:(i * 3 + kind + 1) * 8]

    emit_attn1(P, C, QT, QTb, KT, KTb, VA, VAb, MIXt, MIXb)
    emit_outproj(P, C, X4, lambda fc, c0, w: MIXt[:, fc, c0:c0 + w], MIXb, wout_d, mcol(1, 2))
    k.barrier()
    emit_adaln(P, X4, H4, MOD[:, 72 + 16:72 + 24], mcol(2, 0), C)
    emit_ffn(P, X4, H4, U, Ub, w1_d, w3_d, w2_d, mcol(2, 2), C)
    ob = Sink("out")
    for j in range(4):
        k.dma("sp", [(xo_d[:, :, j * 512:(j + 1) * 512], XT[:, :, j * 512:(j + 1) * 512])], reads=[XB[j]],
              sembuf=XB[j], sink=ob)
    k.wait_all("sp", [ob])
    k.close()
    return nc


def prep_LC(inp, la_res, lb_res, shared):
    maps = []
    for core in range(NCORE):
        b, qd = core // 4, core % 4
        kk = np.concatenate([la_res[b * 4 + s]["ko"] for s in range(4)], axis=1)
        v = np.concatenate([la_res[b * 4 + s]["vo"] for s in range(4)], axis=0)
        yh = np.stack([lb_res[b * 4 + cq]["yT"][:, qd * TOK:(qd + 1) * TOK] for cq in range(4)], axis=1)
        maps.append(dict(xT=la_res[core]["xo"], modi=la_res[core]["modo"], q=la_res[core]["qo"],
                         kk=np.ascontiguousarray(kk), v=np.ascontiguousarray(v), yh=np.ascontiguousarray(yh),
                         w1=shared["w1"][1, 1], w3=shared["w3"][1, 1], w2=shared["w2"][1, 1], wout=shared["wout"][1]))
    return maps


_CACHE = {}


FUSED = True


def kernel(**inputs):
    inp = {kk: np.asarray(v) for kk, v in inputs.items()}
    cores = list(range(NCORE))
    if FUSED:
        if "fused" not in _CACHE:
            _CACHE["fused"] = build_fused()
        maps = prep_fused(inp)
        res = run_bass_kernel_spmd(_CACHE["fused"], maps, core_ids=cores).results
        out = np.zeros((2, S, D), np.float32)
        for core in cores:
            b, qd = core // 4, core % 4
            xo = np.asarray(res[core]["xo"], np.float32)
            out[b, qd * TOK:(qd + 1) * TOK] = xo.transpose(2, 1, 0).reshape(TOK, D)
        return out
    if "la" not in _CACHE:
        _CACHE["la"] = build_LA(True)
        _CACHE["lb"] = build_LB()
        _CACHE["lc"] = build_LC()
    la_maps = prep_LA(inp)
    la = run_bass_kernel_spmd(_CACHE["la"], la_maps, core_ids=cores).results
    lb_maps = prep_LB(inp, [la[c]["hco"] for c in cores])
    lb = run_bass_kernel_spmd(_CACHE["lb"], lb_maps, core_ids=cores).results
    lc_maps = prep_LC(inp, la, lb, la_maps[0])
    lc = run_bass_kernel_spmd(_CACHE["lc"], lc_maps, core_ids=cores).results
    out = np.zeros((2, S, D), np.float32)
    for core in cores:
        b, qd = core // 4, core % 4
        xo = np.asarray(lc[core]["xo"], np.float32)
        out[b, qd * TOK:(qd + 1) * TOK] = xo.transpose(2, 1, 0).reshape(TOK, D)
    return out


U32 = mybir.dt.uint32


def kb_gather(k, dst_ap, src_dram_ap, idx_ap, reads=(), writes=()):
    sbf = writes[0]
    if sbf.dsem is None:
        sbf.dsem = {}
        sbf.dcnt = {}
    if True not in sbf.dsem:
        sbf.dsem[True] = ("d", id(sbf), True)
        sbf.dcnt[True] = 0
        k.sems[sbf.dsem[True]] = k._newsem(f"d_{k.nsem}")
    k._wait("pool", k._deps(reads, writes))
    k.nc.gpsimd.indirect_dma_start(out=dst_ap, out_offset=None, in_=src_dram_ap,
                                   in_offset=bass.IndirectOffsetOnAxis(ap=idx_ap, axis=0)
                                   ).then_inc(k.sems[sbf.dsem[True]], 16)
    sbf.dcnt[True] += 16
    tok = (sbf.dsem[True], sbf.dcnt[True])
    k._commit(tok, reads, writes)
    return tok


class CC:
    n = 0

    def __init__(self, k, groups, fake):
        self.k, self.groups, self.fake = k, groups, fake
        self.pool_sems = [] if fake else [k._newsem(f"cc{i}") for i in range(16)]
        self.alltoks = {}

    def allgather(self, src_t, dst_t, reads, dstb, sinks=(), sl=None):
        k = self.k
        sap = src_t.ap() if sl is None else src_t.ap()[sl]
        dap = dst_t.ap() if sl is None else dst_t.ap()[sl]
        rows = sap.shape[0]
        k.wait_all("sp" if self.fake else "pool", list(sinks))
        if self.fake:
            pairs = [(dap[r * rows:(r + 1) * rows, :], sap) for r in range(4)]
            k.dma("sp", pairs, reads=reads, writes=[dstb])
            return
        CC.n += 1
        key = ("cc", CC.n)
        k.sems[key] = self.pool_sems.pop(0)
        k._wait("pool", k._deps(reads, [dstb]))
        k.nc.gpsimd.collective_compute("AllGather", ALU.bypass, replica_groups=self.groups,
                                       ins=[sap.opt()], outs=[dap.opt()]).then_inc(k.sems[key])
        k._commit((key, 1), reads, [dstb])
        self.alltoks.setdefault(id(dstb), Sink("cc")).toks[key] = 1

    def sink(self, dstb):
        return self.alltoks.get(id(dstb), Sink("none"))


def emit_filter(P, C, F, kc_s, kcsb):
    k = P.k
    tmp, MLP, SM = C.tmp, F.MLP, F.SM
    cb = C.cb
    W1, W2, W3 = MLP[0:33, 0:64], MLP[:, 64:128], MLP[:, 128:192]
    W4 = MLP[:, 200:456].rearrange("p (d c) -> p d c", d=2)
    SMb, ACC, ACCb = F.SMb, F.ACC, F.ACCb
    k.op("dve", lambda e: e.memset(SM[:, 0:1], math.pi / 2), writes=[SMb])
    k.op("dve", lambda e: e.memset(SM[:, 1:2], EPS), writes=[SMb])
    k.op("dve", lambda e: e.tensor_scalar(out=SM[0:64, 2:3], in0=MLP[:, 195:196], scalar1=0.25, scalar2=None, op0=ALU.mult),
         reads=[cb], writes=[SMb])
    for i in range(3):
        k.op("dve", lambda e, i=i: e.tensor_tensor(out=SM[0:64, 3 + i:4 + i], in0=MLP[:, 192 + i:193 + i], in1=SM[0:64, 2:3],
                                                   op=ALU.mult), reads=[cb, SMb], writes=[SMb])
    k.op("dve", lambda e: e.memset(ACC[:], 0.0), writes=[ACCb])

    def sin4(ps, psb, li, out, outb):
        s1, s1b = tmp()
        k.op("act", lambda e: e.activation(out=s1[0:64, :], in_=ps[0:64, :], func=AF.Sin, bias=SM[0:64, 3 + li:4 + li],
                                           scale=SM[0:64, 2:3]), reads=[psb, SMb], writes=[s1b])
        a1, a1b = tmp()
        k.op("act", lambda e: e.activation(out=a1[0:64, :], in_=ps[0:64, :], func=AF.Abs, bias=SM[0:64, 3 + li:4 + li],
                                           scale=SM[0:64, 2:3]), reads=[psb, SMb], writes=[a1b])
        k.op("act", lambda e: e.activation(out=a1[0:64, :], in_=a1[0:64, :], func=AF.Sin, bias=SM[0:64, 0:1], scale=-1.0),
             reads=[a1b, SMb], writes=[a1b])
        k.op("dve", lambda e: e.tensor_tensor(out=a1[0:64, :], in0=a1[0:64, :], in1=s1[0:64, :], op=ALU.mult),
             reads=[a1b, s1b], writes=[a1b])
        k.op("dve", lambda e: e.tensor_tensor(out=s1[0:64, :], in0=s1[0:64, :], in1=s1[0:64, :], op=ALU.mult),
             reads=[s1b], writes=[s1b])
        k.op("dve", lambda e: e.tensor_scalar(out=s1[0:64, :], in0=s1[0:64, :], scalar1=-2.0, scalar2=1.0,
                                              op0=ALU.mult, op1=ALU.add), reads=[s1b], writes=[s1b])
        k.op("dve", lambda e: e.scalar_tensor_tensor(out=out[0:64, :], in0=a1[0:64, :], scalar=4.0, in1=s1[0:64, :],
                                                     op0=ALU.mult, op1=ALU.mult), reads=[a1b, s1b], writes=[outb])
        return out, outb

    for c in range(32):
        yield c
        ze, zeb = tmp()
        k.dma("sp", [(ze[0:33, :], F.zemb_d[:, c * 512:(c + 1) * 512])], writes=[zeb])
        ps, psb = P.bank("m", [6, 7])
        k.op("pe", lambda e, ps=ps, ze=ze: e.matmul(ps[0:64, :], lhsT=W1, rhs=ze[0:33, :], start=True, stop=True),
             reads=[zeb, cb], writes=[psb])
        h, hb = sin4(ps, psb, 0, F.HF[0], F.HFb[0])
        yield c
        for li, W in ((1, W2), (2, W3)):
            ps, psb = P.bank("m", [6, 7])
            k.op("pe", lambda e, ps=ps, h=h, W=W: e.matmul(ps[0:64, :], lhsT=W, rhs=h[0:64, :], start=True, stop=True),
                 reads=[hb, cb], writes=[psb])
            h, hb = sin4(ps, psb, li, F.HF[li % 2], F.HFb[li % 2])
            yield c
        wt, wtb = tmp()
        k.dma("sp", [(wt[:, :].rearrange("p (n c) -> p n c", n=4), F.win_d[:, c * 4:(c + 1) * 4, :])], writes=[wtb])
        ps4, ps4b = P.bank("m", [6, 7])
        for n2l in range(4):
            for d in range(2):
                k.op("pe", lambda e, ps4=ps4, h=h, n2l=n2l, d=d: e.matmul(
                    ps4[d * 64:(d + 1) * 64, n2l * 128:(n2l + 1) * 128],
                    lhsT=h[0:64, n2l * 128 + d * 64:n2l * 128 + (d + 1) * 64], rhs=W4[:, d, :],
                    start=True, stop=True, skip_group_check=True), reads=[hb, cb], writes=[ps4b])
        kc, kcb = tmp()
        k.op("dve", lambda e, kc=kc, ps4=ps4, wt=wt: e.tensor_tensor(out=kc[:, :], in0=ps4[:, :], in1=wt[:, :], op=ALU.mult),
             reads=[ps4b, wtb], writes=[kcb])
        st, stb = C.tmpbf()
        k.op("act", lambda e, kc=kc, st=st: e.activation(out=st[:, :], in_=kc[:, :], func=AF.Copy), reads=[kcb], writes=[stb])
        k.dma("sp", [(kc_s.ap()[:, c * 4:(c + 1) * 4, :], st[:, :].rearrange("p (n c) -> p n c", n=4))], reads=[stb],
              sembuf=stb, sink=kcsb)
        sq, sqb = tmp()
        k.op("dve", lambda e, kc=kc, sq=sq: e.tensor_tensor(out=sq[:, :], in0=kc[:, :], in1=kc[:, :], op=ALU.mult),
             reads=[kcb], writes=[sqb])
        rd, rdb = tmp()
        k.op("dve", lambda e, sq=sq, rd=rd: e.tensor_reduce(
            out=rd[:, 0:128], in_=sq[:, :].rearrange("p (n c) -> p c n", n=4), axis=mybir.AxisListType.X, op=ALU.add),
            reads=[sqb], writes=[rdb])
        k.op("dve", lambda e, rd=rd: e.tensor_tensor(out=ACC[:], in0=ACC[:], in1=rd[:, 0:128], op=ALU.add),
             reads=[rdb, ACCb], writes=[ACCb])
    ps, psb = P.bank("m", [6, 7])
    k.op("pe", lambda e, ps=ps: e.matmul(ps[:, 0:128], lhsT=C.ones_f[:, :], rhs=ACC[:], start=True, stop=True),
         reads=[ACCb, cb], writes=[psb])
    k.op("act", lambda e, ps=ps: e.activation(out=F.RSrow[:], in_=ps[:, 0:128], func=AF.Sqrt, bias=SM[:, 1:2], scale=1.0),
         reads=[psb, SMb], writes=[F.RSb])
    k.op("dve", lambda e: e.reciprocal(out=F.RSrow[:], in_=F.RSrow[:]), reads=[F.RSb], writes=[F.RSb])


def emit_fftconv(P, C, F, KC, KCb, zs2, zs2b, c_src, csink, V):
    k = P.k
    tmp = C.tmp
    DFT, TW, dftb, cb = F.DFT, F.TW, F.dftb, C.cb
    Fre, Fim, nFim = DFT[:, 0:128], DFT[:, 128:256], DFT[:, 384:512]
    Fcat, FcatI2, FcatI1 = DFT[:, 0:256], DFT[:, 128:384], DFT[:, 256:512]
    TWre2, TWim2 = TW[:, 0], TW[:, 1]
    Bre, Bim, Yre, Yim, Qre, Qim, Hre, Him = V.Bre, V.Bim, V.Yre, V.Yim, V.Qre, V.Qim, V.Hre, V.Him
    Bb, Hb, Yb, Qb = Buf("B"), Buf("H"), Buf("Y"), Buf("Q")
    PS, PSb = P.PS, P.PSb

    def cmul_from_psum(A, Ab, sign, outre, outim, outb, pr):
        A4 = A[:, :].rearrange("p (c r k) -> p c r k", c=2, r=2)
        Are, Aim = A4[:, :, 0, :], A4[:, :, 1, :]
        v = lambda t: t[:, 0:256].rearrange("p (c k) -> p c k", c=2)
        t1, t1b = tmp()
        t2, t2b = tmp()
        k.op("dve", lambda e: e.tensor_tensor(out=v(t1), in0=Are, in1=TWre2, op=ALU.mult), reads=[Ab, cb], writes=[t1b])
        k.op("dve", lambda e: e.tensor_tensor(out=v(t2), in0=Aim, in1=TWim2, op=ALU.mult), reads=[Ab, cb], writes=[t2b])
        k.op("pool", lambda e: e.tensor_tensor(out=outre[:, pr * 2:pr * 2 + 2, :], in0=v(t1), in1=v(t2),
                                               op=(ALU.subtract if sign > 0 else ALU.add)),
             reads=[t1b, t2b], writes=[outb])
        t3, t3b = tmp()
        t4, t4b = tmp()
        k.op("dve", lambda e: e.tensor_tensor(out=v(t3), in0=Aim, in1=TWre2, op=ALU.mult), reads=[Ab, cb], writes=[t3b])
        k.op("dve", lambda e: e.tensor_tensor(out=v(t4), in0=Are, in1=TWim2, op=ALU.mult), reads=[Ab, cb], writes=[t4b])
        k.op("pool", lambda e: e.tensor_tensor(out=outim[:, pr * 2:pr * 2 + 2, :], in0=v(t3), in1=v(t4),
                                               op=(ALU.add if sign > 0 else ALU.subtract)),
             reads=[t3b, t4b], writes=[outb])

    def fft_fwd(lhs_of, srcb, krows, xre, xreb, xim, ximb):
        for pr in range(2):
            A, Ab = P.bank("A", [0, 1])
            for cl in range(2):
                c4 = pr * 2 + cl
                k.op("pe", lambda e, A=A, cl=cl, c4=c4: e.matmul(
                    A[:, cl * 256:(cl + 1) * 256], lhsT=lhs_of(c4), rhs=Fcat[0:krows, :],
                    start=True, stop=True, skip_group_check=True), reads=[srcb, dftb], writes=[Ab])
            cmul_from_psum(A, Ab, +1, Bre, Bim, Bb, pr)
        bre = Bre.rearrange("p c k -> p (c k)")
        bim = Bim.rearrange("p c k -> p (c k)")
        k.op("pe", lambda e: e.matmul(xre[:, :], lhsT=Fre, rhs=bre, start=True, stop=False), reads=[Bb, dftb], writes=[xreb])
        k.op("pe", lambda e: e.matmul(xre[:, :], lhsT=nFim, rhs=bim, start=False, stop=True), reads=[Bb, dftb], writes=[xreb])
        k.op("pe", lambda e: e.matmul(xim[:, :], lhsT=Fim, rhs=bre, start=True, stop=False), reads=[Bb, dftb], writes=[ximb])
        k.op("pe", lambda e: e.matmul(xim[:, :], lhsT=Fre, rhs=bim, start=False, stop=True), reads=[Bb, dftb], writes=[ximb])

    for g in range(32):
        ch0 = g * 4
        ZC, ZCb = V.ZC[g % 2], V.ZCb[g % 2]
        src = bass.AP(tensor=zs2, offset=ch0 * S, ap=[[128, 64], [S, 4], [1, 128]])
        k.dma("sp", [(ZC[0:64], src)], reads=[zs2b], writes=[ZCb])
        fft_fwd(lambda c4, ch0=ch0: KC[:, :, ch0 + c4], KCb, 128, PS[2], PSb[2], PS[3], PSb[3])
        rsb_ap = F.RSrow[:, ch0:ch0 + 4].unsqueeze(2).broadcast_to([128, 4, 128])
        k.op("dve", lambda e, rsb_ap=rsb_ap: e.tensor_tensor(out=Hre.rearrange("p (c k) -> p c k", c=4),
                                                              in0=PS[2][:, :].rearrange("p (c k) -> p c k", c=4),
                                                              in1=rsb_ap, op=ALU.mult), reads=[PSb[2], F.RSb], writes=[Hb])
        k.op("dve", lambda e, rsb_ap=rsb_ap: e.tensor_tensor(out=Him.rearrange("p (c k) -> p c k", c=4),
                                                              in0=PS[3][:, :].rearrange("p (c k) -> p c k", c=4),
                                                              in1=rsb_ap, op=ALU.mult), reads=[PSb[3], F.RSb], writes=[Hb])
        fft_fwd(lambda c4, ZC=ZC: ZC[0:64, c4, :], ZCb, 64, PS[4], PSb[4], PS[5], PSb[5])
        t1, t1b = tmp()
        t2, t2b = tmp()
        k.op("dve", lambda e, t1=t1: e.tensor_tensor(out=t1[:, :], in0=PS[4][:, :], in1=Hre, op=ALU.mult),
             reads=[PSb[4], Hb], writes=[t1b])
        k.op("dve", lambda e, t2=t2: e.tensor_tensor(out=t2[:, :], in0=PS[5][:, :], in1=Him, op=ALU.mult),
             reads=[PSb[5], Hb], writes=[t2b])
        k.op("pool", lambda e, t1=t1, t2=t2: e.tensor_tensor(out=Yre.rearrange("p c k -> p (c k)"), in0=t1[:, :],
                                                             in1=t2[:, :], op=ALU.subtract), reads=[t1b, t2b], writes=[Yb])
        t3, t3b = tmp()
        t4, t4b = tmp()
        k.op("dve", lambda e, t3=t3: e.tensor_tensor(out=t3[:, :], in0=PS[4][:, :], in1=Him, op=ALU.mult),
             reads=[PSb[4], Hb], writes=[t3b])
        k.op("dve", lambda e, t4=t4: e.tensor_tensor(out=t4[:, :], in0=PS[5][:, :], in1=Hre, op=ALU.mult),
             reads=[PSb[5], Hb], writes=[t4b])
        k.op("pool", lambda e, t3=t3, t4=t4: e.tensor_tensor(out=Yim.rearrange("p c k -> p (c k)"), in0=t3[:, :],
                                                             in1=t4[:, :], op=ALU.add), reads=[t3b, t4b], writes=[Yb])
        for pr in range(2):
            Pk, Pkb = P.bank("A", [0, 1])
            for cl in range(2):
                c4 = pr * 2 + cl
                k.op("pe", lambda e, Pk=Pk, cl=cl, c4=c4: e.matmul(
                    Pk[:, cl * 256:(cl + 1) * 256], lhsT=Yre[:, c4, :], rhs=FcatI1, start=True, stop=False,
                    skip_group_check=True), reads=[Yb, dftb], writes=[Pkb])
                k.op("pe", lambda e, Pk=Pk, cl=cl, c4=c4: e.matmul(
                    Pk[:, cl * 256:(cl + 1) * 256], lhsT=Yim[:, c4, :], rhs=FcatI2, start=False, stop=True,
                    skip_group_check=True), reads=[Yb, dftb], writes=[Pkb])
            cmul_from_psum(Pk, Pkb, -1, Qre, Qim, Qb, pr)
        yo, yob = PS[6], PSb[6]
        k.op("pe", lambda e: e.matmul(yo[0:64, :], lhsT=DFT[:, 0:64], rhs=Qre.rearrange("p c k -> p (c k)"),
                                      start=True, stop=False), reads=[Qb, dftb], writes=[yob])
        k.op("pe", lambda e: e.matmul(yo[0:64, :], lhsT=DFT[:, 128:192], rhs=Qim.rearrange("p c k -> p (c k)"),
                                      start=False, stop=True), reads=[Qb, dftb], writes=[yob])
        ys, ysb = tmp()
        k.op("act", lambda e, ys=ys: e.activation(out=ys[0:64, :], in_=yo[0:64, :], func=AF.Copy, scale=1.0 / NFFT),
             reads=[yob], writes=[ysb])
        dst = bass.AP(tensor=c_src, offset=ch0 * S, ap=[[128, 64], [S, 4], [1, 128]])
        k.dma("sp", [(dst, ys[0:64, :].rearrange("p (c k) -> p c k", c=4))], reads=[ysb], sembuf=ysb, sink=csink)


def emit_attn1f(P, C, q_s, KT, KTb, VB, VBb, QS, QSb, MIX, MIXb, SK=3):
    k = P.k
    its = [(jq, g, hh, kb) for jq in range(4) for g in range(2) for hh in range(4) for kb in range(64)]
    st = {}
    cur = {}

    def stage_a(i):
        jq, g, hh, kb = its[i]
        q0 = jq * 512
        QT, QTb = QS[jq % 2], QSb[jq % 2]
        if (g, hh, kb) == (0, 0, 0):
            k.dma("sp", [(QT, q_s.ap()[:, :, q0:q0 + 512])], writes=[QTb])
        gs = slice(g * 64, (g + 1) * 64)
        ps, psb = P.bank("att", [0, 1, 2, 3, 6, 7])
        k.op("pe", lambda e: e.matmul(ps[:, :], lhsT=KT[gs, kb * 128:(kb + 1) * 128], rhs=QT[gs, hh, :],
                                      start=True, stop=True), reads=[KTb, QTb], writes=[psb])
        pt, ptb = C.tmpbf()
        k.op("act", lambda e: e.activation(out=pt[:, :], in_=ps[:, :], func=AF.Exp, scale=HD ** -0.5),
             reads=[psb], writes=[ptb])
        st[i] = (pt, ptb)

    def stage_b(i):
        jq, g, hh, kb = its[i]
        q0 = jq * 512
        gs = slice(g * 64, (g + 1) * 64)
        ds_ = slice((1 - g) * 64, (2 - g) * 64)
        if kb == 0:
            cur["po"] = P.bank("o", [4, 5])
        po, pob = cur["po"]
        pt, ptb = st.pop(i)
        k.op("pe", lambda e: e.matmul(po[:, :], lhsT=VB[:, kb, g * 64:g * 64 + 128], rhs=pt[:, :],
                                      start=(kb == 0), stop=(kb == 63)), reads=[VBb, ptb], writes=[pob])
        if kb == 63:
            rd, rdb = C.tmp()
            k.op("dve", lambda e: e.reciprocal(out=rd[ds_, :], in_=po[ds_, :]), reads=[pob], writes=[rdb])
            k.op("dve", lambda e: e.tensor_tensor(out=MIX[gs, hh, q0:q0 + 512], in0=po[gs, :], in1=rd[ds_, :],
                                                  op=ALU.mult), reads=[pob, rdb], writes=[MIXb[jq]])

    n = len(its)
    for t in range(n + SK):
        if t < n:
            stage_a(t)
        if t - SK >= 0:
            stage_b(t - SK)


def build_fused(fake_ag=False, ncore=8, stop_after=None):
    nc = bass.Bass("TRN2", target_bir_lowering=False)
    k = KB(nc)
    groups = [[0, 1, 2, 3], [4, 5, 6, 7]] if ncore == 8 else [[0, 1, 2, 3]]
    cc = CC(k, groups, fake_ag)
    dt_in = lambda name, shape, dt=F32: nc.dram_tensor(name, list(shape), dt, kind="ExternalInput").ap()
    dram = lambda name, shape, dt=F32: nc.dram_tensor(name, list(shape), dt, kind="Internal")
    xT_d = dt_in("xT", [128, 8, 2048])
    xH_d = dt_in("xH", [128, 8, 256])
    cf_d = dt_in("cf", [128, 400])
    oh_d = dt_in("oh", [32, 512])
    vm_d = dt_in("vm", [128, 512])
    rope_d = dt_in("rope", [128, 2, 2048])
    idx_d = dt_in("idx", [128, 24], U32)
    mlp_d = dt_in("mlpw", [64, 456])
    zemb_d = dt_in("zemb", [33, NFFT])
    win_d = dt_in("win", [128, 128, 128])
    dft_d = dt_in("dftc", [128, 512])
    tw_d = dt_in("tw", [128, 2, 2, 128])
    wmod_d = dt_in("wmod", [2, 1024, 9216])
    w1_d = dt_in("w1", [2, 2, 1024, DFFP])
    w3_d = dt_in("w3", [2, 2, 1024, DFFP])
    w2_d = dt_in("w2", [2, 2, DFFP, 1024])
    win_w = dt_in("win_w", [2, 1024, INC])
    wout_d = dt_in("wout", [2, 1024, 1024])
    xo_d = nc.dram_tensor("xo", [128, 8, 2048], F32, kind="ExternalOutput").ap()
    scr = dram("scr", [128, 8 * 512])
    q_s = dram("q_s", [128, 4, 2048], BF16)
    k_src, k_g = dram("k_src", [128, 2048], BF16), dram("k_g", [512, 2048], BF16)
    v_src, v_g = dram("v_src", [2048, 128], BF16), dram("v_g", [8192, 128], BF16)
    hb_src, hb_g = dram("hb_src", [128, 24]), dram("hb_g", [512, 24])
    z_src, z_g = dram("z_src", [4, 128, 2048], BF16), dram("z_g", [4, 512, 2048], BF16)
    zf_s = dram("zf_s", [128, 4, 2048])
    x0_s = dram("x0_s", [128, 4, 2048])
    zs2 = dram("zs2", [128, S], BF16)
    kc_s = dram("kc_s", [128, 128, 128], BF16)
    c_src, c_g = dram("c_src", [8, 16, S]), dram("c_g", [8, 64, S])

    P = Prog(nc, k)
    C = Common()
    C.tmp = RotTmp(k, "tf", 6, F32)
    C.tmpbf = RotTmp(k, "tb", 5, BF16)
    C.RS = k.sb("RS", [128, 512], F32)
    C.RSb = Buf("RS")
    C.cb = Buf("consts")
    C.modb = Buf("mod")
    CF = k.sb("CF", [128, 400], F32)
    IDX = k.sb("IDX", [128, 24], U32)
    C.ones_bf = k.sb("ones_bf", [128, 128], BF16)
    C.bd_bf = k.sb("bd_bf", [128, 128], BF16)
    C.ones_f = k.sb("ones_f", [128, 128], F32)
    C.eps = k.sb("eps", [128, 1], F32)
    condbf = k.sb("condbf", [128, 8], BF16)
    MODV = k.sb("modv", [128, 2, 72], F32)
    AV = k.sb("av", [128, 2, 3, 8], F32)
    EXS = k.sb("exs", [128, 8], F32)
    F = Common()
    F.MLP = k.sb("MLP", [64, 456], F32)
    F.DFT = k.sb("DFT", [128, 512], BF16)
    F.TW = k.sb("TW", [128, 2, 2, 128], F32)
    F.ACC = k.sb("ACC", [128, 128], F32)
    F.SM = k.sb("SM", [128, 16], F32)
    F.RSrow = k.sb("RSrow", [128, 128], F32)
    F.SMb, F.ACCb, F.RSb, F.dftb = Buf("SM"), Buf("ACC"), Buf("RSr"), Buf("dft")
    F.zemb_d, F.win_d = zemb_d, win_d
    F.HF = [k.sb(f"HF{i}", [64, 512], F32) for i in range(2)]
    F.HFb = [Buf("HF0"), Buf("HF1")]
    HBS = k.sb("HBS", [128, 12, 2], F32)
    HG = k.sb("HG", [128, 4, 12, 2], F32)
    HLR = k.sb("HLR", [128, 2, 12], F32)
    XT = k.sb("XT", [128, 8, 2048], F32)
    AR = k.sb("AR", [128, 41984], BF16)

    cT = CF[:, 0:8]
    flags = CF[:, 8:10]
    normg = CF[:, 10:58].rearrange("p (l i c) -> p l i c", l=2, i=3)
    bmod = CF[:, 58:202].rearrange("p (l m) -> p l m", l=2)
    convw = CF[:, 202:214].rearrange("p (c t) -> p c t", c=4)
    convb = CF[:, 214:218]
    gq, gk = CF[:, 218:220], CF[:, 220:222]
    sink = CF[:, 222:230]
    relrep = CF[:, 232:240]
    gq1, gk1 = CF[:, 240:241], CF[:, 241:242]
    dcw = CF[:, 244:280].rearrange("p (c t) -> p c t", c=12)
    dcb = CF[:, 280:292]
    skipv = CF[:, 292:296]
    selL, selR = CF[:, 296:300], CF[:, 300:304]

    Ht = AR[:, 0:18432].rearrange("p (c t) -> p c t", c=8)
    UQ = AR[:, 18432:33792]
    MXf = AR[:, 33792:41984].bitcast(F32)
    XH = MXf[:, 0:2048].rearrange("p (c t) -> p c t", c=8)
    MIXHI0 = AR[:, 33792:41984].rearrange("p (c t) -> p c t", c=4)
    U = UQ[:, 0:6 * 2304].rearrange("p (f t) -> p f t", f=6)
    QT0 = UQ[:, 0:8192].rearrange("p (h t) -> p h t", h=4)
    KT0 = UQ[:, 8192:8192 + 2304]
    VA0 = UQ[:, 10496:10496 + 4608].rearrange("p (b g d) -> p b g d", b=18, g=2)
    S_t = UQ[:, 10496:10496 + 4100].bitcast(F32)
    C.E8 = UQ[:, 0:8192].bitcast(F32).rearrange("p (h m) -> p h m", h=8)
    EXPB = P.WBf[:, :, :].rearrange("p a (b q) -> p (a b) q", q=128).rearrange("p (h b) q -> p h b q", h=8)
    OH = AR[0:32, 0:1024].bitcast(F32)
    VM = AR[:, 1024:2048].bitcast(F32)

    tiles5 = [(0, 512), (512, 512), (1024, 512), (1536, 512), (2048, 256)]
    tiles4 = tiles5[:4]
    XB = [Buf(f"x{j}") for j in range(5)]

    class XAct:
        def __init__(self, tiles):
            self.tiles = tiles
            self.b = XB[:len(tiles)]

        def ap(self, c, j):
            if j < 4:
                return XT[:, c, j * 512:(j + 1) * 512]
            return XH[:, c, :]
    X5, X4 = XAct(tiles5), XAct(tiles4)
    H5 = Act(Ht, tiles5, "h")
    H4 = Act(Ht, tiles4, "h")
    H4.b = H5.b[:4]
    Ub = [[Buf(f"u{f}_{j}") for j in range(5)] for f in range(6)]

    k.dma("sp", [(CF[:], cf_d), (F.MLP[:], mlp_d), (F.TW[:], tw_d), (IDX[:], idx_d)], writes=[C.cb])
    k.dma("pool", [(F.DFT[:], dft_d)], writes=[F.dftb])
    for j in range(4):
        k.dma("sp", [(XT[:, :, j * 512:(j + 1) * 512], xT_d[:, :, j * 512:(j + 1) * 512])], writes=[XB[j]])
    k.dma("sp", [(XH, xH_d)], writes=[XB[4]])
    cst = Buf("cst")
    k.op("dve", lambda e: e.memset(C.ones_bf[:], 1.0), writes=[cst])
    k.op("dve", lambda e: e.memset(C.ones_f[:], 1.0), writes=[cst])
    k.op("dve", lambda e: e.memset(C.eps[:], EPS), writes=[cst])
    k.op("dve", lambda e: e.memset(C.bd_bf[:], 0.0), writes=[cst])
    k.op("dve", lambda e: e.memset(C.bd_bf[0:64, 0:64], 1.0), writes=[cst])
    k.op("dve", lambda e: e.memset(C.bd_bf[64:128, 64:128], 1.0), writes=[cst])
    condb = Buf("cond")
    k.op("act", lambda e: e.activation(out=condbf[:], in_=cT, func=AF.Silu), reads=[C.cb], writes=[condb])
    k.op("act", lambda e: e.activation(out=EXS[:], in_=sink, func=AF.Exp), reads=[C.cb], writes=[cst])
    k.op("dve", lambda e: e.tensor_copy(out=C.eps[:], in_=C.eps[:]), reads=[cst, C.cb], writes=[C.cb])

    def layer_mod(l):
        emit_mod(P, condbf, condb, wmod_d[l], bmod[:, l, :], MODV[:, l, :], C.modb, AV[:, l], normg[:, l])

    def mcol(l, i, kind):
        return MODV[:, l, (i * 3 + kind) * 8:(i * 3 + kind + 1) * 8]

    def mix_ap0(fc, c0, w):
        if fc < 4:
            return Ht[:, fc, c0:c0 + w]
        return MIXHI0[:, fc - 4, c0:c0 + w]

    def finish(extra=()):
        ob = Sink("out")
        k.wait_all("sp", list(extra))
        for j in range(4):
            k.dma("sp", [(xo_d[:, :, j * 512:(j + 1) * 512], XT[:, :, j * 512:(j + 1) * 512])], reads=[XB[j]],
                  sembuf=XB[j], sink=ob)
        k.wait_all("sp", [ob])
        k.close()
        return nc

    kcsb = Sink("kcs")
    fgen = emit_filter(P, C, F, kc_s, kcsb)
    next(fgen)
    hcnt = [0]

    def fhook():
        hcnt[0] += 1
        next(fgen, None)

    layer_mod(0)
    emit_adaln(P, X5, H5, AV[:, 0, 0], mcol(0, 0, 0), C)
    emit_ffn(P, X5, H5, U, Ub, w1_d[0, 0], w3_d[0, 0], w2_d[0, 0], mcol(0, 0, 2), C, hook=fhook)
    for _ in fgen:
        pass
    k.barrier()
    ohb = Buf("oh")
    k.dma("sp", [(OH, oh_d), (VM, vm_d)], writes=[ohb])
    k.op("dve", lambda e: e.tensor_copy(out=C.eps[:], in_=C.eps[:]), reads=[ohb, C.cb], writes=[C.cb])
    scrb = emit_expb(P, C, relrep, C.cb, OH, VM, scr, EXPB, P.WBb[0])
    k.barrier(dbufs=[scrb])
    emit_adaln(P, X5, H5, AV[:, 0, 1], mcol(0, 1, 0), C)
    k.barrier()
    MIXb = [Buf(f"mix{j}") for j in range(4)]
    Sb = Buf("S")
    emit_conv0(P, C, H5, mix_ap0, MIXb, win_w[0], convw, convb, flags, S_t, Sb)
    k.barrier()
    QTb = [Buf(f"qt{j}") for j in range(4)]
    KTb = [Buf(f"kt{j}") for j in range(5)]
    VAb = Buf("va")
    k.op("dve", lambda e: e.memset(VA0[:, 0:16, 0, 64:128], 1.0), writes=[VAb])
    k.op("dve", lambda e: e.memset(VA0[:, 0:16, 1, 0:64], 1.0), writes=[VAb])
    emit_qkv0(P, C, H5, win_w[0], QT0, QTb, KT0, KTb, VA0, VAb, gq, gk, flags)
    k.barrier()
    emit_attn0(P, C, QT0, QTb, KT0, KTb, VA0, VAb, EXPB, P.WBb[0], EXS, Ht, MIXb)
    for i in (1, 2):
        P.WBb[i].r = dict(P.WBb[0].r)
        P.WBb[i].w = P.WBb[0].w
    emit_outproj(P, C, X4, mix_ap0, MIXb, wout_d[0], mcol(0, 1, 2))
    k.barrier()
    emit_adaln(P, X4, H4, AV[:, 0, 2], mcol(0, 2, 0), C)
    emit_ffn(P, X4, H4, U, Ub, w1_d[0, 1], w3_d[0, 1], w2_d[0, 1], mcol(0, 2, 2), C)

    if stop_after == "l0":
        return finish([kcsb])

    layer_mod(1)
    emit_adaln(P, X4, H4, AV[:, 1, 0], mcol(1, 0, 0), C)
    emit_ffn(P, X4, H4, U, Ub, w1_d[1, 0], w3_d[1, 0], w2_d[1, 0], mcol(1, 0, 2), C)
    k.barrier()
    ROPE = UQ[:, 0:8192].bitcast(F32).rearrange("p (a t) -> p a t", a=2)
    HB = AR[:, 26624:26624 + 12300].bitcast(F32).rearrange("p (a t) -> p a t", a=3)
    ropeb, HBb = Buf("rope"), Buf("HB")
    k.dma("sp", [(ROPE, rope_d)], writes=[ropeb])
    emit_adaln(P, X4, H4, AV[:, 1, 1], mcol(1, 1, 0), C)
    W1L = win_w[1]
    hbsb = Buf("hbs")
    for pr in range(6):
        wa, wab = P.load_wa(W1L[:, 768 + pr * 256:768 + (pr + 1) * 256])
        for cl in range(2):
            ch = pr * 2 + cl
            ps, psb = P.bank("h1", [0, 1, 2, 3])
            for ci, col in enumerate((0, 2047)):
                for c in range(8):
                    k.op("pe", lambda e, c=c, ps=ps, wa=wa, cl=cl, ci=ci, col=col: e.matmul(
                        ps[:, ci:ci + 1], lhsT=wa[:, c, cl * 128:(cl + 1) * 128], rhs=Ht[:, c, col:col + 1],
                        start=(c == 0), stop=(c == 7)), reads=[wab] + H4.b, writes=[psb])
            k.op("act", lambda e, ps=ps, ch=ch: e.activation(out=HBS[:, ch, :], in_=ps[:, 0:2], func=AF.Copy),
                 reads=[psb], writes=[hbsb])
    hbsrcb, hbgb, hgb = Buf("hbsrc"), Buf("hbg"), Buf("hg")
    k.dma("sp", [(hb_src.ap(), HBS[:].rearrange("p c t -> p (c t)"))], reads=[hbsb], writes=[hbsrcb])
    cc.allgather(hb_src, hb_g, [hbsrcb], hbgb)
    k.dma("sp", [(HG[:].rearrange("p r c t -> p r (c t)"), hb_g.ap().rearrange("(r p) t -> p r t", p=128))],
          reads=[hbgb], writes=[hgb])
    outs = Sink("l1outs")
    for pr in range(3):
        wa, wab = P.load_wa(W1L[:, pr * 256:(pr + 1) * 256])
        for cl in range(2):
            if pr == 2 and cl == 1:
                break
            for j in range(4):
                c0, w = H4.tiles[j]
                ps, psb = P.bank("h1", [0, 1, 2, 3])
                emit_inproj_tile(P, H4, j, wa, wab, cl, ps, psb)
                qn, qnb = C.tmp()
                emit_qknorm(P, C, ps, psb, w, (gq1 if pr < 2 else gk1), qn[:, 0:w], qnb)
                st, stb = C.tmpbf()
                emit_rope(P, C, qn, qnb, ROPE[:, 0, :], ROPE[:, 1, :], ropeb, c0, w, st[:, 0:w], stb)
                dst = q_s.ap()[:, pr * 2 + cl, c0:c0 + w] if pr < 2 else k_src.ap()[:, c0:c0 + w]
                k.dma("sp", [(dst, st[:, 0:w])], reads=[stb], sembuf=stb, sink=outs)
        if pr == 2:
            for tb in range(16):
                j = tb // 4
                ps, psb = P.bank("o", [4, 5])
                for c in range(8):
                    k.op("pe", lambda e, c=c, tb=tb, ps=ps, wa=wa: e.matmul(
                        ps[:, 0:128], lhsT=Ht[:, c, tb * 128:(tb + 1) * 128], rhs=wa[:, c, 128:256],
                        start=(c == 0), stop=(c == 7)), reads=[wab, H4.b[j]], writes=[psb])
                st, stb = C.tmpbf()
                k.op("act", lambda e, ps=ps, st=st: e.activation(out=st[:, 0:128], in_=ps[:, 0:128], func=AF.Copy),
                     reads=[psb], writes=[stb])
                k.dma("sp", [(v_src.ap()[tb * 128:(tb + 1) * 128, :], st[:, 0:128])], reads=[stb], sembuf=stb, sink=outs)
    kgb, vgb = Buf("kg"), Buf("vg")
    if stop_after == "qkv":
        return finish([outs, kcsb])
    cc.allgather(k_src, k_g, [], kgb, sinks=[outs])
    cc.allgather(v_src, v_g, [], vgb, sinks=[outs])
    for side, sel, colx in ((0, selL, 1), (1, selR, 0)):
        k.op("dve", lambda e, side=side, sel=sel, colx=colx: e.tensor_scalar(
            out=HLR[:, side, :], in0=HG[:, 0, :, colx], scalar1=sel[:, 0:1], scalar2=None, op0=ALU.mult),
            reads=[hgb, C.cb], writes=[hgb])
        for r in range(1, 4):
            k.op("dve", lambda e, side=side, sel=sel, colx=colx, r=r: e.scalar_tensor_tensor(
                out=HLR[:, side, :], in0=HG[:, r, :, colx], scalar=sel[:, r:r + 1], in1=HLR[:, side, :],
                op0=ALU.mult, op1=ALU.add), reads=[hgb, C.cb], writes=[hgb])

    zouts = Sink("zouts")
    for hh in range(2):
        was = [P.load_wa(W1L[:, 768 + part * 512 + hh * 256: 768 + part * 512 + (hh + 1) * 256]) for part in range(3)]
        for cl in range(2):
            cch = hh * 2 + cl
            for part in range(3):
                wa, wab = was[part]
                for j in range(4):
                    c0, w = H4.tiles[j]
                    ps, psb = P.bank("h1", [0, 1, 2, 3])
                    emit_inproj_tile(P, H4, j, wa, wab, cl, ps, psb)
                    k.op("act", lambda e, ps=ps, part=part, c0=c0, w=w: e.activation(
                        out=HB[:, part, 1 + c0:1 + c0 + w], in_=ps[:, 0:w], func=AF.Copy), reads=[psb], writes=[HBb])
                ch12 = part * 4 + cch
                k.op("act", lambda e, part=part, ch12=ch12: e.activation(out=HB[:, part, 0:1], in_=HLR[:, 0, ch12:ch12 + 1],
                                                                         func=AF.Copy), reads=[hgb], writes=[HBb])
                k.op("act", lambda e, part=part, ch12=ch12: e.activation(out=HB[:, part, 2049:2050],
                                                                         in_=HLR[:, 1, ch12:ch12 + 1], func=AF.Copy),
                     reads=[hgb], writes=[HBb])
            for j in range(4):
                c0, w = H4.tiles[j]
                cv = []
                for part in range(3):
                    ch12 = part * 4 + cch
                    t, tb_ = C.tmp()
                    k.op("dve", lambda e, t=t, part=part, c0=c0, ch12=ch12: e.tensor_scalar(
                        out=t[:, :], in0=HB[:, part, 1 + c0:1 + c0 + 512], scalar1=dcw[:, ch12, 1:2],
                        scalar2=dcb[:, ch12:ch12 + 1], op0=ALU.mult, op1=ALU.add), reads=[HBb, C.cb], writes=[tb_])
                    k.op("dve", lambda e, t=t, part=part, c0=c0, ch12=ch12: e.scalar_tensor_tensor(
                        out=t[:, :], in0=HB[:, part, c0:c0 + 512], scalar=dcw[:, ch12, 0:1], in1=t[:, :],
                        op0=ALU.mult, op1=ALU.add), reads=[HBb, C.cb, tb_], writes=[tb_])
                    k.op("dve", lambda e, t=t, part=part, c0=c0, ch12=ch12: e.scalar_tensor_tensor(
                        out=t[:, :], in0=HB[:, part, 2 + c0:2 + c0 + 512], scalar=dcw[:, ch12, 2:3], in1=t[:, :],
                        op0=ALU.mult, op1=ALU.add), reads=[HBb, C.cb, tb_], writes=[tb_])
                    cv.append((t, tb_))
                (x0t, x0b), (x1t, x1b), (vt, vb_) = cv
                k.dma("sp", [(x0_s.ap()[:, cch, c0:c0 + 512], x0t[:, :])], reads=[x0b], sembuf=x0b, sink=zouts)
                k.op("dve", lambda e, x1t=x1t, vt=vt: e.tensor_tensor(out=x1t[:, :], in0=x1t[:, :], in1=vt[:, :], op=ALU.mult),
                     reads=[x1b, vb_], writes=[x1b])
                k.dma("sp", [(zf_s.ap()[:, cch, c0:c0 + 512], x1t[:, :])], reads=[x1b], sembuf=x1b, sink=zouts)
                zb16, zb16b = C.tmpbf()
                k.op("act", lambda e, x1t=x1t, zb16=zb16: e.activation(out=zb16[:, :], in_=x1t[:, :], func=AF.Copy),
                     reads=[x1b], writes=[zb16b])
                k.dma("sp", [(z_src.ap()[cch, :, c0:c0 + 512], zb16[:, :])], reads=[zb16b],
                      sembuf=zb16b, sink=zouts)
    zgb = Buf("zg")
    for cch in range(4):
        cc.allgather(z_src, z_g, [], zgb, sinks=[zouts], sl=cch)

    if stop_after == "l1pre":
        return finish([kgb, vgb, zgb, kcsb])

    k.barrier()
    VB = AR[:, 0:12288].rearrange("p (b d) -> p b d", b=64)
    KT = AR[:, 12288:20480]
    QS = [AR[:, 20480 + i * 2048:20480 + (i + 1) * 2048].rearrange("p (h t) -> p h t", h=4) for i in range(2)]
    QSb = [Buf("qs0"), Buf("qs1")]
    MIX = AR[:, 24576:40960].rearrange("p (c t) -> p c t", c=8)
    MIXb = [Buf(f"mixb{j}") for j in range(4)]
    KTb, VBb = Buf("KT"), Buf("VB")
    k.op("dve", lambda e: e.memset(VB[:, :, 64:128], 1.0), writes=[VBb])
    k.dma("sp", [(KT.rearrange("p (r t) -> p r t", r=4), k_g.ap().rearrange("(r p) t -> p r t", p=128))],
          reads=[kgb], writes=[KTb])
    vsrc = v_g.ap().rearrange("(b p) d -> p b d", p=128)
    k.dma("sp", [(VB[:, :, 0:64], vsrc[:, :, 0:64]), (VB[:, :, 128:192], vsrc[:, :, 64:128])], reads=[vgb], writes=[VBb])
    emit_attn1f(P, C, q_s, KT, KTb, VB, VBb, QS, QSb, MIX, MIXb)

    if stop_after == "attn":
        return finish([zgb, kcsb] + MIXb)

    k.barrier()
    V = Common()
    KC = AR[:, 0:16384].rearrange("p (n c) -> p n c", n=128)
    Zg = AR[:, 0:8192]
    V.ZC = [AR[:, 16384 + i * 512:16384 + (i + 1) * 512].rearrange("p (c k) -> p c k", c=4) for i in range(2)]
    V.ZCb = [Buf("zc0"), Buf("zc1")]
    six = [AR[:, 17408 + i * 512:17408 + (i + 1) * 512].rearrange("p (c k) -> p c k", c=4) for i in range(6)]
    V.Bre, V.Bim, V.Yre, V.Yim, V.Qre, V.Qim = six
    V.Hre = AR[:, 20480:21504].bitcast(F32)
    V.Him = AR[:, 21504:22528].bitcast(F32)
    Zgb, zs2b, KCb = Buf("Zg"), Buf("zs2"), Buf("KC")
    zg_rows = z_g.ap().rearrange("c r t -> (c r) t")
    k.wait_all("pool", [cc.sink(zgb)])
    for r in range(4):
        kb_gather(k, Zg[:, r * 2048:(r + 1) * 2048], zg_rows, IDX[:, r:r + 1], reads=[zgb, C.cb], writes=[Zgb])
    k.dma("sp", [(zs2.ap(), Zg)], reads=[Zgb], writes=[zs2b])
    k.wait_all("sp", [kcsb])
    k.dma("sp", [(KC, kc_s.ap())], reads=[zs2b], writes=[KCb])
    csink = Sink("csrc")
    emit_fftconv(P, C, F, KC, KCb, zs2, zs2b, c_src, csink, V)
    cgb = Buf("cg")
    for i in range(8):
        cc.allgather(c_src, c_g, [], cgb, sinks=[csink], sl=i)

    if stop_after == "fft":
        return finish([cgb] + MIXb)

    cg_rows = c_g.ap().rearrange("i r (a t) -> (i r a) t", t=512)
    k.wait_all("pool", [cc.sink(cgb)])
    for cq in range(4):
        for tq in range(4):
            c0 = tq * 512
            ct, ctb = C.tmp()
            kb_gather(k, ct[:, :], cg_rows, IDX[:, 4 + cq * 4 + tq:5 + cq * 4 + tq], reads=[cgb, C.cb], writes=[ctb])
            zt, ztb = C.tmp()
            k.dma("sp", [(zt[:, :], zf_s.ap()[:, cq, c0:c0 + 512])], writes=[ztb])
            xt_, xtb = C.tmp()
            k.dma("sp", [(xt_[:, :], x0_s.ap()[:, cq, c0:c0 + 512])], writes=[xtb])
            k.op("dve", lambda e, zt=zt, ct=ct, cq=cq: e.scalar_tensor_tensor(
                out=zt[:, :], in0=zt[:, :], scalar=skipv[:, cq:cq + 1], in1=ct[:, :], op0=ALU.mult, op1=ALU.add),
                reads=[ztb, ctb, C.cb], writes=[ztb])
            k.op("dve", lambda e, zt=zt, xt_=xt_, cq=cq, c0=c0: e.tensor_tensor(
                out=MIX[:, 4 + cq, c0:c0 + 512], in0=zt[:, :], in1=xt_[:, :], op=ALU.mult),
                reads=[ztb, xtb], writes=[MIXb[tq]])

    emit_outproj(P, C, X4, lambda fc, c0, w: MIX[:, fc, c0:c0 + w], MIXb, wout_d[1], mcol(1, 1, 2))
    k.barrier()
    Ht2 = AR[:, 0:16384].rearrange("p (c t) -> p c t", c=8)
    U2 = AR[:, 16384:16384 + 12288].rearrange("p (f t) -> p f t", f=6)
    H42 = Act(Ht2, tiles4, "h2")
    Ub2 = [[Buf(f"v{f}_{j}") for j in range(4)] for f in range(6)]
    emit_adaln(P, X4, H42, AV[:, 1, 2], mcol(1, 2, 0), C)
    emit_ffn(P, X4, H42, U2, Ub2, w1_d[1, 1], w3_d[1, 1], w2_d[1, 1], mcol(1, 2, 2), C)
    ob = Sink("out")
    for j in range(4):
        k.dma("sp", [(xo_d[:, :, j * 512:(j + 1) * 512], XT[:, :, j * 512:(j + 1) * 512])], reads=[XB[j]],
              sembuf=XB[j], sink=ob)
    k.wait_all("sp", [ob])
    k.close()
    return nc


def prep_fused(inp, ncore=NCORE):
    base = prep_LA(inp)
    maps = []
    cache = {}
    dcw = np.asarray(inp["d_conv_w"], np.float32)[0]
    dcb = np.asarray(inp["d_conv_b"], np.float32)[0]
    skip = np.asarray(inp["d_skip"], np.float32)[0]
    w4 = np.asarray(inp["d_f_w4"], np.float32)[0]
    p = np.arange(128)
    for core in range(ncore):
        b, qd = core // 4, core % 4
        cq = qd
        if cq not in cache:
            cache[cq] = host_consts_LB(cq)
        zemb, win, dftc, tw = cache[cq]
        m = dict(base[core])
        cf = np.zeros((128, 400), np.float32)
        cf[:, 0:244] = m.pop("cf")[:, 0:244]
        cf[:, 244:280] = fm(dcw).transpose(0, 2, 1).reshape(128, 36)
        cf[:, 280:292] = fm(dcb)
        cf[:, 292:296] = fm(skip)
        if qd > 0:
            cf[:, 296 + qd - 1] = 1.0
        if qd < 3:
            cf[:, 300 + qd + 1] = 1.0
        idx = np.zeros((128, 24), np.uint32)
        for r in range(4):
            idx[:, r] = cq * 512 + r * 128 + p
        for c2 in range(4):
            for tq in range(4):
                idx[:, 4 + c2 * 4 + tq] = ((p // 16) * 64 + c2 * 16 + (p % 16)) * 16 + qd * 4 + tq
        mlp = np.zeros((64, 456), np.float32)
        mlp[0:33, 0:64] = np.asarray(inp["d_f_w1"], np.float32)[0]
        mlp[:, 64:128] = np.asarray(inp["d_f_w2"], np.float32)[0]
        mlp[:, 128:192] = np.asarray(inp["d_f_w3"], np.float32)[0]
        mlp[:, 192] = np.asarray(inp["d_f_b1"], np.float32)[0]
        mlp[:, 193] = np.asarray(inp["d_f_b2"], np.float32)[0]
        mlp[:, 194] = np.asarray(inp["d_f_b3"], np.float32)[0]
        mlp[:, 195] = np.asarray(inp["d_f_freq"], np.float32)[0]
        mlp[:, 200:328] = w4[:, cq * 128:(cq + 1) * 128]
        mlp[:, 328:456] = w4[:, 512 + cq * 128:512 + (cq + 1) * 128]
        m["win_w"] = m.pop("win")
        m.update(cf=cf, idx=idx, mlpw=mlp, zemb=zemb, win=win, dftc=dftc, tw=tw)
        maps.append(m)
    return maps
```

```python
import math
import numpy as np
import concourse.bass as bass
import concourse.mybir as mybir
from concourse.bass_utils import run_bass_kernel_spmd

AF = mybir.ActivationFunctionType
ALU = mybir.AluOpType
F32 = mybir.dt.float32
BF16 = mybir.dt.bfloat16
EPOCH = 12000

D = 1024
S = 8192
TOK = 2048
NCORE = 8
DFF = 2752
DFFP = 2816
NF = 22
HD = 64
EPS = 1e-6
INC = 2304


class Buf:
    __slots__ = ("name", "w", "r", "dsem", "dcnt")

    def __init__(self, name=""):
        self.name = name
        self.w = None
        self.r = {}
        self.dsem = None
        self.dcnt = 0


class Sink:
    def __init__(self, name=""):
        self.name = name
        self.toks = {}


class KB:
    def __init__(self, nc):
        self.nc = nc
        self.engs = {"pe": nc.tensor, "act": nc.scalar, "dve": nc.vector,
                     "pool": nc.gpsimd, "sp": nc.sync}
        self.cnt = {e: 0 for e in self.engs}
        self.sems = {}
        self.waited = {e: {} for e in self.engs}
        self.nsem = 0
        self._stack = []

    def _newsem(self, name):
        cm = self.nc.semaphore(name)
        h = cm.__enter__()
        self._stack.append(cm)
        self.nsem += 1
        return h

    def sb(self, name, shape, dt):
        cm = self.nc.sbuf_tensor(name, shape, dt)
        t = cm.__enter__()
        self._stack.append(cm)
        return t

    def ps(self, name, shape, dt=F32):
        cm = self.nc.psum_tensor(name, shape, dt)
        t = cm.__enter__()
        self._stack.append(cm)
        return t

    def close(self):
        while self._stack:
            self._stack.pop().__exit__(None, None, None)

    def _engsem(self, eng):
        key = (eng, self.cnt[eng] // EPOCH)
        if key not in self.sems:
            self.sems[key] = self._newsem(f"s_{eng}_{key[1]}")
        return key

    def _deps(self, reads, writes):
        deps = {}

        def add(k, v):
            if deps.get(k, 0) < v:
                deps[k] = v
        for b in reads:
            if b.w is not None:
                add(*b.w)
        for b in writes:
            if b.w is not None:
                add(*b.w)
            for kk, v in b.r.items():
                add(kk, v)
        return deps

    def _wait(self, eng, deps):
        w = self.waited[eng]
        e = self.engs[eng]
        for kk, v in deps.items():
            if w.get(kk, 0) >= v:
                continue
            e.wait_ge(self.sems[kk], v)
            w[kk] = v

    def _commit(self, tok, reads, writes):
        kk, v = tok
        for b in reads:
            if b.r.get(kk, 0) < v:
                b.r[kk] = v
        for b in writes:
            b.w = tok
            b.r = {}

    def op(self, eng, fn, reads=(), writes=()):
        deps = self._deps(reads, writes)
        if eng == "pe":
            deps = {kk: v for kk, v in deps.items() if kk[0] != "pe"}
        self._wait(eng, deps)
        key = self._engsem(eng)
        ins = fn(self.engs[eng])
        self.cnt[eng] += 1
        val = self.cnt[eng] - key[1] * EPOCH
        ins.then_inc(self.sems[key], 1)
        tok = (key, val)
        self._commit(tok, reads, writes)
        return tok

    def dma(self, q, pairs, reads=(), writes=(), sembuf=None, sink=None, **kw):
        sbf = sembuf or (writes[0] if writes else reads[0])
        if sbf.dsem is None:
            sbf.dsem = {}
            sbf.dcnt = {}
        sw = (q == "pool")
        if sw not in sbf.dsem:
            sbf.dsem[sw] = ("d", id(sbf), sw)
            sbf.dcnt[sw] = 0
            self.sems[sbf.dsem[sw]] = self._newsem(f"d_{self.nsem}")
        deps = self._deps(reads, writes)
        self._wait(q, deps)
        e = self.engs[q]
        for (o, i) in pairs:
            e.dma_start(out=o, in_=i, **kw).then_inc(self.sems[sbf.dsem[sw]], 16)
            sbf.dcnt[sw] += 16
        tok = (sbf.dsem[sw], sbf.dcnt[sw])
        self._commit(tok, reads, writes)
        if sink is not None and sink.toks.get(tok[0], 0) < tok[1]:
            sink.toks[tok[0]] = tok[1]
        return tok

    def wait_all(self, eng, bufs):
        deps = {}
        for b in bufs:
            d = dict(b.toks) if isinstance(b, Sink) else self._deps([b], [b])
            for kk, v in d.items():
                if deps.get(kk, 0) < v:
                    deps[kk] = v
        self._wait(eng, deps)

    def barrier(self, dbufs=()):
        deps = {}
        for e in ("pe", "act", "dve"):
            if self.cnt[e] == 0:
                continue
            ep = (self.cnt[e] - 1) // EPOCH
            deps[(e, ep)] = self.cnt[e] - ep * EPOCH
        for b in dbufs:
            for kk, v in self._deps([b], [b]).items():
                if deps.get(kk, 0) < v:
                    deps[kk] = v
        for e in ("pe", "act", "dve", "pool", "sp"):
            self._wait(e, dict(deps))


class Prog:
    def __init__(self, nc, k):
        self.nc = nc
        self.k = k
        self.WA = [k.sb(f"WA{i}", [128, 8, 256], BF16) for i in range(4)]
        self.WAb = [Buf(f"WA{i}") for i in range(4)]
        self.WBf = k.sb("WBf", [128, 3, 1024], F32)
        self.WB = [self.WBf[:, i, :].bitcast(BF16).rearrange("p (f n) -> p f n", f=2) for i in range(3)]
        self.WBb = [Buf(f"WB{i}") for i in range(3)]
        self.wa_i = 0
        self.PS = [k.ps(f"ps{i}", [128, 512]) for i in range(8)]
        self.PSb = [Buf(f"ps{i}") for i in range(8)]
        self.rr = {}

    def next_wa(self, parity=None):
        i = self.wa_i
        self.wa_i = (self.wa_i + 1) % 4
        return i

    def bank(self, group, banks):
        i = self.rr.get(group, 0)
        self.rr[group] = (i + 1) % len(banks)
        b = banks[i]
        return self.PS[b], self.PSb[b]

    def load_wa(self, src_ap):
        i = self.next_wa()
        self.k.dma("pool", [(self.WA[i][:], src_ap.rearrange("(c p) n -> p c n", p=128))], writes=[self.WAb[i]])
        return self.WA[i], self.WAb[i]

    def load_wb(self, i, src_ap):
        self.k.dma("pool", [(self.WB[i], src_ap.rearrange("(f p) n -> p f n", p=128))], writes=[self.WBb[i]])
        return self.WB[i], self.WBb[i]


def emit_mod(P, cond_bf, cond_b, wmod_l, bmod_l, modv, modb, a_out, normg_l):
    k = P.k
    ps, psb = P.PS[7], P.PSb[7]
    for ch in range(36):
        wa, wab = P.load_wa(wmod_l[:, ch * 256:(ch + 1) * 256])
        for cl in range(2):
            cc = ch * 2 + cl
            for kc in range(8):
                k.op("pe", lambda e, cc=cc, kc=kc, cl=cl, wa=wa: e.matmul(
                    ps[:, cc:cc + 1], lhsT=wa[:, kc, cl * 128:(cl + 1) * 128], rhs=cond_bf[:, kc:kc + 1],
                    start=(kc == 0), stop=(kc == 7)),
                    reads=[wab, cond_b], writes=[psb])
    k.op("dve", lambda e: e.tensor_tensor(out=modv[:], in0=ps[:, 0:72], in1=bmod_l, op=ALU.add),
         reads=[psb], writes=[modb])
    for i in range(3):
        k.op("dve", lambda e, i=i: e.scalar_tensor_tensor(
            out=a_out[:, i, :], in0=modv[:, (i * 3 + 1) * 8:(i * 3 + 2) * 8], scalar=1.0, in1=normg_l[:, i, :],
            op0=ALU.add, op1=ALU.mult), reads=[modb], writes=[modb])
    for i in (0, 2):
        k.op("dve", lambda e, i=i: e.tensor_scalar(
            out=modv[:, (i * 3 + 2) * 8:(i * 3 + 3) * 8], in0=modv[:, (i * 3 + 2) * 8:(i * 3 + 3) * 8],
            scalar1=0.5, scalar2=None, op0=ALU.mult), reads=[modb], writes=[modb])


class Act:
    def __init__(self, t, tiles, name):
        self.t = t
        self.tiles = tiles
        self.b = [Buf(f"{name}{j}") for j in range(len(tiles))]

    def ap(self, c, j):
        c0, w = self.tiles[j]
        return self.t[:, c, c0:c0 + w]


def emit_adaln(P, X, H, a_ap, shift_ap, C):
    k = P.k
    for j, (c0, w) in enumerate(X.tiles):
        ps, psb = P.PS[6], P.PSb[6]
        for c in range(8):
            sq, sqb = C.tmpbf()
            k.op("act", lambda e, c=c, j=j, w=w, sq=sq: e.activation(out=sq[:, 0:w], in_=X.ap(c, j), func=AF.Square),
                 reads=[X.b[j]], writes=[sqb])
            k.op("pe", lambda e, c=c, w=w, sq=sq: e.matmul(ps[:, 0:w], lhsT=C.ones_bf[:], rhs=sq[:, 0:w],
                                                     start=(c == 0), stop=(c == 7)),
                 reads=[sqb, C.cb], writes=[psb])
        k.op("act", lambda e, w=w: e.activation(out=C.RS[:, 0:w], in_=ps[:, 0:w], func=AF.Sqrt,
                                                bias=C.eps[:, 0:1], scale=1.0 / D),
             reads=[psb, C.cb], writes=[C.RSb])
        k.op("dve", lambda e, w=w: e.reciprocal(out=C.RS[:, 0:w], in_=C.RS[:, 0:w]), reads=[C.RSb], writes=[C.RSb])
        for c in range(8):
            tt, ttb = C.tmp()
            k.op("dve", lambda e, c=c, j=j, w=w, tt=tt: e.scalar_tensor_tensor(
                out=tt[:, 0:w], in0=X.ap(c, j), scalar=a_ap[:, c:c + 1], in1=C.RS[:, 0:w],
                op0=ALU.mult, op1=ALU.mult), reads=[X.b[j], C.RSb, C.modb], writes=[ttb])
            k.op("act", lambda e, c=c, j=j, w=w, tt=tt: e.activation(
                out=H.ap(c, j), in_=tt[:, 0:w], func=AF.Identity, bias=shift_ap[:, c:c + 1], scale=1.0),
                reads=[ttb, C.modb], writes=[H.b[j]])


def emit_ffn(P, X, H, U, Ub, w1, w3, w2, gate_ap, C, hook=None):
    k = P.k
    groups = [(0, 3), (3, 3), (6, 3), (9, 2)]
    ntile = len(X.tiles)
    for (p0, npair) in groups:
        for pl in range(npair):
            p = p0 + pl
            wa1, wa1b = P.load_wa(w1[:, p * 256:(p + 1) * 256])
            wa3, wa3b = P.load_wa(w3[:, p * 256:(p + 1) * 256])
            for fl in range(2):
                fu = pl * 2 + fl
                for j in range(ntile):
                    c0, w = X.tiles[j]
                    ps1, ps1b = P.bank("h1", [0, 1])
                    ps3, ps3b = P.bank("h3", [2, 3])
                    for c in range(8):
                        k.op("pe", lambda e, c=c, j=j, w=w, ps1=ps1, wa1=wa1, fl=fl: e.matmul(
                            ps1[:, 0:w], lhsT=wa1[:, c, fl * 128:(fl + 1) * 128], rhs=H.ap(c, j),
                            start=(c == 0), stop=(c == 7)), reads=[wa1b, H.b[j]], writes=[ps1b])
                    for c in range(8):
                        k.op("pe", lambda e, c=c, j=j, w=w, ps3=ps3, wa3=wa3, fl=fl: e.matmul(
                            ps3[:, 0:w], lhsT=wa3[:, c, fl * 128:(fl + 1) * 128], rhs=H.ap(c, j),
                            start=(c == 0), stop=(c == 7)), reads=[wa3b, H.b[j]], writes=[ps3b])
                    sl, slb = C.tmp()
                    k.op("act", lambda e, w=w, ps1=ps1, sl=sl: e.activation(out=sl[:, 0:w], in_=ps1[:, 0:w], func=AF.Silu),
                         reads=[ps1b], writes=[slb])
                    k.op("dve", lambda e, w=w, c0=c0, ps3=ps3, sl=sl, fu=fu: e.tensor_tensor(
                        out=U[:, fu, c0:c0 + w], in0=ps3[:, 0:w], in1=sl[:, 0:w], op=ALU.mult),
                        reads=[ps3b, slb], writes=[Ub[fu][j]])
                    if hook is not None:
                        hook()
        for pl in range(npair):
            P.load_wb(pl, w2[(p0 + pl) * 256:(p0 + pl + 1) * 256, :])
        nfu = npair * 2
        for c in range(8):
            for j in range(ntile):
                c0, w = X.tiles[j]
                pso, psob = P.bank("o", [4, 5])
                for fu in range(nfu):
                    k.op("pe", lambda e, fu=fu, c=c, w=w, c0=c0, pso=pso: e.matmul(
                        pso[:, 0:w], lhsT=P.WB[fu // 2][:, fu % 2, c * 128:(c + 1) * 128], rhs=U[:, fu, c0:c0 + w],
                        start=(fu == 0), stop=(fu == nfu - 1)), reads=[P.WBb[fu // 2], Ub[fu][j]], writes=[psob])
                k.op("dve", lambda e, c=c, j=j, w=w, pso=pso: e.scalar_tensor_tensor(
                    out=X.ap(c, j), in0=pso[:, 0:w], scalar=gate_ap[:, c:c + 1], in1=X.ap(c, j),
                    op0=ALU.mult, op1=ALU.add), reads=[psob, X.b[j], C.modb], writes=[X.b[j]])


class Common:
    pass


def emit_inproj_tile(P, H, j, wa, wab, cl, ps, psb, cols=None):
    k = P.k
    c0, w = H.tiles[j]
    if cols is not None:
        c0, w = c0 + cols[0], cols[1]
    for c in range(8):
        k.op("pe", lambda e, c=c, c0=c0, w=w: e.matmul(
            ps[:, 0:w], lhsT=wa[:, c, cl * 128:(cl + 1) * 128], rhs=H.t[:, c, c0:c0 + w],
            start=(c == 0), stop=(c == 7)), reads=[wab, H.b[j]], writes=[psb])
    return w


def emit_qknorm(P, C, ps, psb, w, g_ap, out_ap, out_b, extra_reads=()):
    k = P.k
    sq, sqb = C.tmpbf()
    k.op("act", lambda e: e.activation(out=sq[:, 0:w], in_=ps[:, 0:w], func=AF.Square), reads=[psb], writes=[sqb])
    p2, p2b = P.PS[6], P.PSb[6]
    k.op("pe", lambda e: e.matmul(p2[:, 0:w], lhsT=C.bd_bf[:], rhs=sq[:, 0:w], start=True, stop=True),
         reads=[sqb, C.cb], writes=[p2b])
    rs, rsb = C.tmp()
    k.op("act", lambda e: e.activation(out=rs[:, 0:w], in_=p2[:, 0:w], func=AF.Sqrt, bias=C.eps[:, 0:1],
                                       scale=1.0 / HD), reads=[p2b, C.cb], writes=[rsb])
    k.op("dve", lambda e: e.reciprocal(out=rs[:, 0:w], in_=rs[:, 0:w]), reads=[rsb], writes=[rsb])
    k.op("dve", lambda e: e.scalar_tensor_tensor(out=out_ap, in0=ps[:, 0:w], scalar=g_ap, in1=rs[:, 0:w],
                                                 op0=ALU.mult, op1=ALU.mult),
         reads=[psb, rsb, C.cb] + list(extra_reads), writes=[out_b])


def emit_outproj(P, C, X, MIX, MIXb, wout_l, gate_ap):
    k = P.k
    for pr in range(4):
        wa, wab = P.load_wa(wout_l[:, pr * 256:(pr + 1) * 256])
        for cl in range(2):
            c = pr * 2 + cl
            for j, (c0, w) in enumerate(X.tiles):
                pso, psob = P.bank("o", [4, 5])
                for fc in range(8):
                    k.op("pe", lambda e, fc=fc, c0=c0, w=w, pso=pso, wa=wa, cl=cl: e.matmul(
                        pso[:, 0:w], lhsT=wa[:, fc, cl * 128:(cl + 1) * 128], rhs=MIX(fc, c0, w),
                        start=(fc == 0), stop=(fc == 7)), reads=[wab, MIXb[j]], writes=[psob])
                k.op("dve", lambda e, c=c, j=j, w=w, pso=pso: e.scalar_tensor_tensor(
                    out=X.ap(c, j), in0=pso[:, 0:w], scalar=gate_ap[:, c:c + 1], in1=X.ap(c, j),
                    op0=ALU.mult, op1=ALU.add), reads=[psob, X.b[j], C.modb], writes=[X.b[j]])


def emit_expb(P, C, relrep, relb, onehot, vmask, scr, EXPB, EXPBb):
    k = P.k
    nc = P.nc
    E8 = C.E8
    E8b = Buf("E8")
    for h in range(8):
        lt, ltb = C.tmp()
        k.op("dve", lambda e, h=h, lt=lt: e.tensor_scalar(out=lt[0:32, 0:128], in0=C.ones_f[0:32, 0:128],
                                                          scalar1=relrep[0:32, h:h + 1], scalar2=None, op0=ALU.mult),
             reads=[relb, C.cb], writes=[ltb])
        ps, psb = P.bank("h1", [0, 1])
        k.op("pe", lambda e, lt=lt, ps=ps: e.matmul(ps[:, 0:512], lhsT=lt[0:32, 0:128], rhs=onehot[0:32, :],
                                                    start=True, stop=True), reads=[ltb, C.cb], writes=[psb])
        ex, exb = C.tmp()
        k.op("act", lambda e, ps=ps, ex=ex: e.activation(out=ex[:, :], in_=ps[:, 0:512], func=AF.Exp),
             reads=[psb], writes=[exb])
        k.op("dve", lambda e, h=h, ex=ex: e.tensor_tensor(out=E8[:, h, :], in0=ex[:, :], in1=vmask[:, :], op=ALU.mult),
             reads=[exb, C.cb], writes=[E8b])
    scrb = Buf("scr")
    k.dma("sp", [(scr.ap(), E8[:])], reads=[E8b], writes=[scrb])
    pairs = []
    for h in range(8):
        for bi in range(3):
            off = h * 512 + 128 * (1 - bi) + 255
            src = bass.AP(tensor=scr, offset=off, ap=[[8 * 512 - 1, 128], [1, 128]])
            pairs.append((EXPB[:, h, bi, :], src))
    k.dma("sp", pairs, reads=[scrb], writes=[EXPBb])
    return scrb


def emit_conv0(P, C, H, MIX, MIXb, win_l, convw, convb, flags, S_t, Sb):
    k = P.k
    ntile_own = 4
    for hh in range(2):
        wgb, wgbb = P.load_wa(win_l[:, 768 + hh * 256: 768 + (hh + 1) * 256])
        wgc, wgcb = P.load_wa(win_l[:, 1280 + hh * 256: 1280 + (hh + 1) * 256])
        wu, wub = P.load_wa(win_l[:, 1792 + hh * 256: 1792 + (hh + 1) * 256])
        for cl in range(2):
            cc = hh * 2 + cl
            for j in range(ntile_own):
                c0, w = H.tiles[j]
                pg, pgb = P.bank("h1", [0, 1])
                pu, pub = P.bank("h3", [2, 3])
                emit_inproj_tile(P, H, j, wgc, wgcb, cl, pg, pgb)
                emit_inproj_tile(P, H, j, wu, wub, cl, pu, pub)
                tg, tgb = C.tmp()
                k.op("act", lambda e, tg=tg, pg=pg, w=w: e.activation(out=tg[:, 0:w], in_=pg[:, 0:w], func=AF.Copy),
                     reads=[pgb], writes=[tgb])
                k.op("dve", lambda e, tg=tg, pu=pu, w=w, c0=c0: e.tensor_tensor(
                    out=S_t[:, 1 + c0:1 + c0 + w], in0=pu[:, 0:w], in1=tg[:, 0:w], op=ALU.mult),
                    reads=[pub, tgb], writes=[Sb])
            pg, pgb = P.bank("h1", [0, 1])
            pu, pub = P.bank("h3", [2, 3])
            emit_inproj_tile(P, H, 4, wgc, wgcb, cl, pg, pgb, cols=(127, 2))
            emit_inproj_tile(P, H, 4, wu, wub, cl, pu, pub, cols=(127, 2))
            tg, tgb = C.tmp()
            k.op("act", lambda e, tg=tg, pg=pg: e.activation(out=tg[:, 0:2], in_=pg[:, 0:2], func=AF.Copy),
                 reads=[pgb], writes=[tgb])
            k.op("dve", lambda e, tg=tg, pu=pu: e.scalar_tensor_tensor(
                out=S_t[:, 0:1], in0=pu[:, 0:1], scalar=flags[:, 0:1], in1=tg[:, 0:1], op0=ALU.mult, op1=ALU.mult),
                reads=[pub, tgb, C.cb], writes=[Sb])
            k.op("dve", lambda e, tg=tg, pu=pu: e.scalar_tensor_tensor(
                out=S_t[:, 2049:2050], in0=pu[:, 1:2], scalar=flags[:, 1:2], in1=tg[:, 1:2], op0=ALU.mult, op1=ALU.mult),
                reads=[pub, tgb, C.cb], writes=[Sb])
            for j in range(ntile_own):
                c0, w = H.tiles[j]
                pb_, pbb = P.bank("o", [4, 5])
                emit_inproj_tile(P, H, j, wgb, wgbb, cl, pb_, pbb)
                t, tb = C.tmp()
                k.op("dve", lambda e, t=t, c0=c0, w=w, cc=cc: e.tensor_scalar(
                    out=t[:, 0:w], in0=S_t[:, 1 + c0:1 + c0 + w], scalar1=convw[:, cc, 1:2], scalar2=convb[:, cc:cc + 1],
                    op0=ALU.mult, op1=ALU.add), reads=[Sb, C.cb], writes=[tb])
                k.op("dve", lambda e, t=t, c0=c0, w=w, cc=cc: e.scalar_tensor_tensor(
                    out=t[:, 0:w], in0=S_t[:, c0:c0 + w], scalar=convw[:, cc, 0:1], in1=t[:, 0:w],
                    op0=ALU.mult, op1=ALU.add), reads=[Sb, C.cb, tb], writes=[tb])
                k.op("dve", lambda e, t=t, c0=c0, w=w, cc=cc: e.scalar_tensor_tensor(
                    out=t[:, 0:w], in0=S_t[:, 2 + c0:2 + c0 + w], scalar=convw[:, cc, 2:3], in1=t[:, 0:w],
                    op0=ALU.mult, op1=ALU.add), reads=[Sb, C.cb, tb], writes=[tb])
                k.op("dve", lambda e, t=t, c0=c0, w=w, cc=cc, pb_=pb_: e.tensor_tensor(
                    out=MIX(4 + cc, c0, w), in0=pb_[:, 0:w], in1=t[:, 0:w], op=ALU.mult),
                    reads=[pbb, tb], writes=[MIXb[j]])


def emit_qkv0(P, C, H, win_l, QT, QTb, KT, KTb, VA, VAb, gq, gk, flags):
    k = P.k
    for pr in range(2):
        wa, wab = P.load_wa(win_l[:, pr * 256:(pr + 1) * 256])
        for cl in range(2):
            hh = pr * 2 + cl
            for j in range(4):
                c0, w = H.tiles[j]
                ps, psb = P.bank("h1", [0, 1, 2, 3])
                emit_inproj_tile(P, H, j, wa, wab, cl, ps, psb)
                emit_qknorm(P, C, ps, psb, w, gq[:, 0:1], QT[:, hh, c0:c0 + w], QTb[j])
    wa, wab = P.load_wa(win_l[:, 512:768])
    for j in range(5):
        c0, w = H.tiles[j]
        ps, psb = P.bank("h1", [0, 1, 2, 3])
        emit_inproj_tile(P, H, j, wa, wab, 0, ps, psb)
        emit_qknorm(P, C, ps, psb, w, gk[:, 0:1], KT[:, c0:c0 + w], KTb[j])
    for tb in range(18):
        j = tb // 4 if tb < 16 else 4
        ps, psb = P.bank("o", [4, 5])
        for c in range(8):
            k.op("pe", lambda e, c=c, tb=tb, ps=ps: e.matmul(
                ps[:, 0:128], lhsT=H.t[:, c, tb * 128:(tb + 1) * 128], rhs=wa[:, c, 128:256],
                start=(c == 0), stop=(c == 7)), reads=[wab, H.b[j]], writes=[psb])
        if tb < 16:
            k.op("act", lambda e, tb=tb, ps=ps: e.activation(out=VA[:, tb, 0, 0:64], in_=ps[:, 0:64], func=AF.Copy),
                 reads=[psb], writes=[VAb])
            k.op("act", lambda e, tb=tb, ps=ps: e.activation(out=VA[:, tb, 1, 64:128], in_=ps[:, 64:128], func=AF.Copy),
                 reads=[psb], writes=[VAb])
        else:
            fl = flags[:, tb - 16:tb - 15]
            k.op("dve", lambda e, tb=tb, ps=ps, fl=fl: e.tensor_scalar(
                out=VA[:, tb, 0, 0:64], in0=ps[:, 0:64], scalar1=fl, scalar2=None, op0=ALU.mult),
                reads=[psb, C.cb], writes=[VAb])
            k.op("dve", lambda e, tb=tb, ps=ps, fl=fl: e.tensor_scalar(
                out=VA[:, tb, 1, 64:128], in0=ps[:, 64:128], scalar1=fl, scalar2=None, op0=ALU.mult),
                reads=[psb, C.cb], writes=[VAb])
            k.op("dve", lambda e, tb=tb, fl=fl: e.tensor_scalar(
                out=VA[:, tb, 0, 64:128], in0=C.ones_f[:, 0:64], scalar1=fl, scalar2=None, op0=ALU.mult),
                reads=[C.cb], writes=[VAb])
            k.op("dve", lambda e, tb=tb, fl=fl: e.tensor_scalar(
                out=VA[:, tb, 1, 0:64], in0=C.ones_f[:, 0:64], scalar1=fl, scalar2=None, op0=ALU.mult),
                reads=[C.cb], writes=[VAb])


def emit_attn0(P, C, QT, QTb, KT, KTb, VA, VAb, EXPB, EXPBb, expsink, MIXLO, MIXb):
    k = P.k
    for n in range(16):
        j = n // 4
        for g in range(2):
            gs = slice(g * 64, (g + 1) * 64)
            ds_ = slice((1 - g) * 64, (2 - g) * 64)
            po, pob = P.bank("o", [4, 5])
            for bi in range(3):
                kb = n - 1 + bi
                kidx = 16 if kb < 0 else (17 if kb > 15 else kb)
                kj = kidx // 4 if kidx < 16 else 4
                ps, psb = P.bank("h1", [0, 1, 2, 3])
                k.op("pe", lambda e, ps=ps, kidx=kidx, n=n, gs=gs: e.matmul(
                    ps[:, 0:512], lhsT=KT[gs, kidx * 128:(kidx + 1) * 128], rhs=QT[gs, :, n * 128:(n + 1) * 128],
                    start=True, stop=True), reads=[KTb[kj], QTb[j]], writes=[psb])
                ex, exb = C.tmp()
                k.op("act", lambda e, ps=ps, ex=ex: e.activation(out=ex[:, :], in_=ps[:, 0:512], func=AF.Exp,
                                                                 scale=HD ** -0.5),
                     reads=[psb], writes=[exb])
                pt, ptb = C.tmpbf()
                k.op("dve", lambda e, ex=ex, pt=pt, g=g, bi=bi: e.tensor_tensor(
                    out=pt[:, :].rearrange("p (h q) -> p h q", h=4), in0=ex[:, :].rearrange("p (h q) -> p h q", h=4),
                    in1=EXPB[:, g * 4:(g + 1) * 4, bi, :], op=ALU.mult), reads=[exb, EXPBb], writes=[ptb])
                for hh in range(4):
                    k.op("pe", lambda e, po=po, pt=pt, hh=hh, kidx=kidx, g=g, bi=bi: e.matmul(
                        po[:, hh * 128:(hh + 1) * 128], lhsT=VA[:, kidx, g, :], rhs=pt[:, hh * 128:(hh + 1) * 128],
                        start=(bi == 0 and hh == 0), stop=(bi == 2 and hh == 3), skip_group_check=True),
                        reads=[VAb, ptb], writes=[pob])
            rd, rdb = C.tmp()
            for hh in range(4):
                k.op("dve", lambda e, rd=rd, po=po, hh=hh, g=g, ds_=ds_: e.tensor_scalar(
                    out=rd[ds_, hh * 128:(hh + 1) * 128], in0=po[ds_, hh * 128:(hh + 1) * 128],
                    scalar1=expsink[ds_, g * 4 + hh:g * 4 + hh + 1], scalar2=None, op0=ALU.add),
                    reads=[pob, C.cb], writes=[rdb])
            k.op("dve", lambda e, rd=rd, ds_=ds_: e.reciprocal(out=rd[ds_, :], in_=rd[ds_, :]), reads=[rdb], writes=[rdb])
            k.op("dve", lambda e, rd=rd, po=po, gs=gs, ds_=ds_, n=n: e.tensor_tensor(
                out=MIXLO[gs, 0:4, n * 128:(n + 1) * 128], in0=po[gs, :].rearrange("p (h q) -> p h q", h=4),
                in1=rd[ds_, :].rearrange("p (h q) -> p h q", h=4), op=ALU.mult), reads=[pob, rdb], writes=[MIXb[j]])


class RotTmp:
    def __init__(self, k, name, n, dt):
        self.t = [k.sb(f"{name}{i}", [128, 512], dt) for i in range(n)]
        self.b = [Buf(f"{name}{i}") for i in range(n)]
        self.i = 0

    def __call__(self):
        i = self.i
        self.i = (i + 1) % len(self.t)
        return self.t[i], self.b[i]


BUCKET_MODE = "trunc"


def t5_bucket_np(rel):
    n = np.abs(rel)
    v = (np.log(np.maximum(n, 1).astype(np.float32) / np.float32(8)) / np.float32(math.log(16.0)) * np.float32(8))
    large = 8 + (np.rint(v).astype(np.int32) if BUCKET_MODE == "round" else v.astype(np.int32))
    large = np.minimum(large, 15)
    return np.where(rel > 0, 16, 0) + np.where(n < 8, n, large)


def build_LA(do_l1=True, stop_after=None):
    nc = bass.Bass("TRN2", target_bir_lowering=False)
    k = KB(nc)
    dt_in = lambda name, shape: nc.dram_tensor(name, list(shape), F32, kind="ExternalInput").ap()
    xT_d = dt_in("xT", [128, 8, 2048])
    xH_d = dt_in("xH", [128, 8, 256])
    cf_d = dt_in("cf", [128, 1400])
    oh_d = dt_in("oh", [32, 512])
    vm_d = dt_in("vm", [128, 512])
    wmod_d = dt_in("wmod", [2, 1024, 9216])
    w1_d = dt_in("w1", [2, 2, 1024, DFFP])
    w3_d = dt_in("w3", [2, 2, 1024, DFFP])
    w2_d = dt_in("w2", [2, 2, DFFP, 1024])
    win_d = dt_in("win", [2, 1024, INC])
    wout_d = dt_in("wout", [2, 1024, 1024])
    xo_d = nc.dram_tensor("xo", [128, 8, 2048], F32, kind="ExternalOutput").ap()
    scr = nc.dram_tensor("scr", [128, 8 * 512], F32, kind="Internal")

    P = Prog(nc, k)
    C = Common()
    C.tmp = RotTmp(k, "tf", 5, F32)
    C.tmpbf = RotTmp(k, "tb", 3, BF16)
    C.RS = k.sb("RS", [128, 512], F32)
    C.RSb = Buf("RS")
    C.cb = Buf("consts")
    C.modb = Buf("mod")
    CF = k.sb("CF", [128, 1400], F32)
    OH = k.sb("OH", [32, 512], F32)
    VM = k.sb("VM", [128, 512], F32)
    C.ones_bf = k.sb("ones_bf", [128, 128], BF16)
    C.bd_bf = k.sb("bd_bf", [128, 128], BF16)
    C.ones_f = k.sb("ones_f", [128, 128], F32)
    C.eps = k.sb("eps", [128, 1], F32)
    condbf = k.sb("condbf", [128, 8], BF16)
    MODV = k.sb("modv", [128, 2, 72], F32)
    AV = k.sb("av", [128, 2, 3, 8], F32)
    EXS = k.sb("exs", [128, 8], F32)
    XT = k.sb("XT", [128, 8, 2048], F32)
    Ht = k.sb("H", [128, 8, 2304], BF16)
    UQ = k.sb("UQ", [128, 15360], BF16)
    MX = k.sb("MX", [128, 4096], F32)

    o_cT, o_fl, o_ng, o_bm, o_cw, o_cb, o_gq, o_gk, o_sink = 0, 8, 10, 58, 202, 214, 218, 220, 222
    cT = CF[:, o_cT:o_cT + 8]
    flags = CF[:, o_fl:o_fl + 2]
    normg = CF[:, o_ng:o_ng + 48].rearrange("p (l i c) -> p l i c", l=2, i=3)
    bmod = CF[:, o_bm:o_bm + 144].rearrange("p (l m) -> p l m", l=2)
    convw = CF[:, o_cw:o_cw + 12].rearrange("p (c t) -> p c t", c=4)
    convb = CF[:, o_cb:o_cb + 4]
    gq = CF[:, o_gq:o_gq + 2]
    gk = CF[:, o_gk:o_gk + 2]
    sink = CF[:, o_sink:o_sink + 8]
    relrep = CF[:, 232:240]

    XH = MX[:, 0:2048].rearrange("p (c t) -> p c t", c=8)
    MIXHI = MX[:, :].bitcast(BF16).rearrange("p (c t) -> p c t", c=4)
    U = UQ[:, 0:6 * 2304].rearrange("p (f t) -> p f t", f=6)
    QT = UQ[:, 0:8192].rearrange("p (h t) -> p h t", h=4)
    KT = UQ[:, 8192:8192 + 2304]
    VA = UQ[:, 10496:10496 + 4608].rearrange("p (b g d) -> p b g d", b=18, g=2)
    S_t = UQ[:, 10496:10496 + 4100].bitcast(F32)
    C.E8 = UQ[:, 0:8192].bitcast(F32).rearrange("p (h m) -> p h m", h=8)
    EXPB = P.WBf[:, :, :].rearrange("p a (b q) -> p (a b) q", q=128).rearrange("p (h b) q -> p h b q", h=8)

    tiles5 = [(0, 512), (512, 512), (1024, 512), (1536, 512), (2048, 256)]
    tiles4 = tiles5[:4]

    class XAct:
        def __init__(self, tiles):
            self.tiles = tiles
            self.b = XB[:len(tiles)]

        def ap(self, c, j):
            if j < 4:
                return XT[:, c, j * 512:(j + 1) * 512]
            return XH[:, c, :]
    XB = [Buf(f"x{j}") for j in range(5)]
    X5, X4 = XAct(tiles5), XAct(tiles4)
    H5 = Act(Ht, tiles5, "h")
    H4 = Act(Ht, tiles4, "h")
    H4.b = H5.b[:4]
    Ub = [[Buf(f"u{f}_{j}") for j in range(5)] for f in range(6)]

    k.dma("sp", [(CF[:], cf_d)], writes=[C.cb])
    k.dma("sp", [(OH[:], oh_d), (VM[:], vm_d)], writes=[C.cb], sembuf=C.cb)
    for j in range(4):
        k.dma("sp", [(XT[:, :, j * 512:(j + 1) * 512], xT_d[:, :, j * 512:(j + 1) * 512])], writes=[XB[j]])
    k.dma("sp", [(XH, xH_d)], writes=[XB[4]])
    cst = Buf("cst")
    k.op("dve", lambda e: e.memset(C.ones_bf[:], 1.0), writes=[cst])
    k.op("dve", lambda e: e.memset(C.ones_f[:], 1.0), writes=[cst])
    k.op("dve", lambda e: e.memset(C.eps[:], EPS), writes=[cst])
    k.op("dve", lambda e: e.memset(C.bd_bf[:], 0.0), writes=[cst])
    k.op("dve", lambda e: e.memset(C.bd_bf[0:64, 0:64], 1.0), writes=[cst])
    k.op("dve", lambda e: e.memset(C.bd_bf[64:128, 64:128], 1.0), writes=[cst])
    condb = Buf("cond")
    k.op("act", lambda e: e.activation(out=condbf[:], in_=cT, func=AF.Silu), reads=[C.cb], writes=[condb])
    k.op("act", lambda e: e.activation(out=EXS[:], in_=sink, func=AF.Exp), reads=[C.cb], writes=[cst])
    k.op("dve", lambda e: e.tensor_copy(out=C.eps[:], in_=C.eps[:]), reads=[cst, C.cb], writes=[C.cb])

    def layer_mod(l):
        emit_mod(P, condbf, condb, wmod_d[l], bmod[:, l, :], MODV[:, l, :], C.modb, AV[:, l], normg[:, l])

    def mcol(l, i, kind):
        return MODV[:, l, (i * 3 + kind) * 8:(i * 3 + kind + 1) * 8]

    def mix_ap(fc, c0, w):
        if fc < 4:
            return Ht[:, fc, c0:c0 + w]
        return MIXHI[:, fc - 4, c0:c0 + w]

    def finish():
        ob = Sink("out")
        for j in range(4):
            k.dma("sp", [(xo_d[:, :, j * 512:(j + 1) * 512], XT[:, :, j * 512:(j + 1) * 512])], reads=[XB[j]],
                  sembuf=XB[j], sink=ob)
        k.wait_all("sp", [ob])
        k.close()
        return nc

    layer_mod(0)
    emit_adaln(P, X5, H5, AV[:, 0, 0], mcol(0, 0, 0), C)
    emit_ffn(P, X5, H5, U, Ub, w1_d[0, 0], w3_d[0, 0], w2_d[0, 0], mcol(0, 0, 2), C)
    if stop_after == "ffn0":
        return finish()
    k.barrier()
    ohb = C.cb
    scrb = emit_expb(P, C, relrep, C.cb, OH, VM, scr, EXPB, P.WBb[0])
    k.barrier(dbufs=[scrb])
    if stop_after == "expb":
        dbg = nc.dram_tensor("dbg", [128, 3072], F32, kind="ExternalOutput").ap()
        ob2 = Buf("dbg")
        k.dma("sp", [(dbg, P.WBf[:].rearrange("p a b -> p (a b)"))], reads=[P.WBb[0]], writes=[ob2], sembuf=ob2)
        k.wait_all("sp", [ob2])
        return finish()
    emit_adaln(P, X5, H5, AV[:, 0, 1], mcol(0, 1, 0), C)
    k.barrier()
    MIXb = [Buf(f"mix{j}") for j in range(4)]
    Sb = Buf("S")
    emit_conv0(P, C, H5, mix_ap, MIXb, win_d[0], convw, convb, flags, S_t, Sb)
    k.barrier()
    QTb = [Buf(f"qt{j}") for j in range(4)]
    KTb = [Buf(f"kt{j}") for j in range(5)]
    VAb = Buf("va")
    k.op("dve", lambda e: e.memset(VA[:, 0:16, 0, 64:128], 1.0), writes=[VAb])
    k.op("dve", lambda e: e.memset(VA[:, 0:16, 1, 0:64], 1.0), writes=[VAb])
    emit_qkv0(P, C, H5, win_d[0], QT, QTb, KT, KTb, VA, VAb, gq, gk, flags)
    k.barrier()
    emit_attn0(P, C, QT, QTb, KT, KTb, VA, VAb, EXPB, P.WBb[0], EXS, Ht, MIXb)
    for i in (1, 2):
        P.WBb[i].r = dict(P.WBb[0].r)
        P.WBb[i].w = P.WBb[0].w
    if stop_after == "mix":
        dbg = nc.dram_tensor("dbg", [128, 8, 2048], BF16, kind="ExternalOutput").ap()
        ob2 = Buf("dbg")
        k.dma("sp", [(dbg[:, 0:4, :], Ht[:, 0:4, 0:2048]), (dbg[:, 4:8, :], MIXHI)], reads=MIXb, writes=[ob2], sembuf=ob2)
        k.wait_all("sp", [ob2])
        return finish()
    emit_outproj(P, C, X4, mix_ap, MIXb, wout_d[0], mcol(0, 1, 2))
    if stop_after == "mixer":
        return finish()
    k.barrier()
    emit_adaln(P, X4, H4, AV[:, 0, 2], mcol(0, 2, 0), C)
    emit_ffn(P, X4, H4, U, Ub, w1_d[0, 1], w3_d[0, 1], w2_d[0, 1], mcol(0, 2, 2), C)
    if not do_l1:
        return finish()

    rope_d = dt_in("rope", [128, 2, 2048])
    qo_d = nc.dram_tensor("qo", [128, 4, 2048], BF16, kind="ExternalOutput").ap()
    ko_d = nc.dram_tensor("ko", [128, 2048], BF16, kind="ExternalOutput").ap()
    vo_d = nc.dram_tensor("vo", [16, 128, 128], BF16, kind="ExternalOutput").ap()
    hco_d = nc.dram_tensor("hco", [12, 128, 2048], F32, kind="ExternalOutput").ap()
    layer_mod(1)
    emit_adaln(P, X4, H4, AV[:, 1, 0], mcol(1, 0, 0), C)
    emit_ffn(P, X4, H4, U, Ub, w1_d[1, 0], w3_d[1, 0], w2_d[1, 0], mcol(1, 0, 2), C)
    k.barrier()
    ROPE = UQ[:, 0:8192].bitcast(F32).rearrange("p (a t) -> p a t", a=2)
    ropeb = Buf("rope")
    k.dma("sp", [(ROPE, rope_d)], writes=[ropeb])
    emit_adaln(P, X4, H4, AV[:, 1, 1], mcol(1, 1, 0), C)
    outb = Sink("outs")
    modo_d = nc.dram_tensor("modo", [128, 96], F32, kind="ExternalOutput").ap()
    k.dma("sp", [(modo_d[:, 0:72], MODV[:, 1, :]), (modo_d[:, 72:96], AV[:, 1].rearrange("p i c -> p (i c)"))],
          reads=[C.modb], sembuf=C.modb, sink=outb)
    emit_inproj1(P, C, H4, win_d[1], CF[:, 240:241], CF[:, 241:242], ROPE[:, 0, :], ROPE[:, 1, :], ropeb,
                 qo_d, ko_d, vo_d, hco_d, outb)
    k.wait_all("sp", [outb])
    return finish()


def host_consts():
    idx = np.arange(512)
    rel = 255 - idx
    valid = (np.abs(rel) <= 128) & (idx < 511)
    bucket = t5_bucket_np(rel.astype(np.int64))
    oh = np.zeros((32, 512), np.float32)
    oh[bucket[valid], idx[valid]] = 1.0
    vm = np.tile(valid.astype(np.float32)[None, :], (128, 1))
    return oh, vm


def fm(v):
    v = np.asarray(v, np.float32)
    sh = v.shape
    v = v.reshape(sh[:-1] + (sh[-1] // 128, 128))
    return np.moveaxis(v, -1, 0)


def prep_LA(inp):
    x = np.asarray(inp["x"], np.float32)
    qperm = np.concatenate([np.r_[hh * 64:(hh + 1) * 64, (4 + hh) * 64:(5 + hh) * 64] for hh in range(4)])
    win = np.ascontiguousarray(np.asarray(inp["w_in"], np.float32))
    win = np.concatenate([win[:, :, qperm], win[:, :, 512:]], axis=2)
    eo = np.r_[0:64:2, 1:64:2]
    eo_q = np.concatenate([h * 64 + eo for h in range(8)])
    eo_k = np.concatenate([512 + h * 64 + eo for h in range(2)])
    win[1] = np.concatenate([win[1][:, eo_q], win[1][:, eo_k], win[1][:, 640:]], axis=1)
    wout = np.asarray(inp["w_out"], np.float32)
    wout = np.ascontiguousarray(np.concatenate([wout[:, qperm, :], wout[:, 512:, :]], axis=1))
    pad = DFFP - DFF
    w1 = np.pad(np.asarray(inp["ffn_w1"], np.float32), ((0, 0), (0, 0), (0, 0), (0, pad)))
    w3 = np.pad(np.asarray(inp["ffn_w3"], np.float32), ((0, 0), (0, 0), (0, 0), (0, pad)))
    w2 = np.pad(np.asarray(inp["ffn_w2"], np.float32), ((0, 0), (0, 0), (0, pad), (0, 0)))
    wmod = np.ascontiguousarray(np.asarray(inp["w_mod"], np.float32))
    oh, vm = host_consts()
    shared = dict(oh=oh, vm=vm, wmod=wmod, w1=w1, w3=w3, w2=w2, win=np.ascontiguousarray(win), wout=wout)
    maps = []
    for core in range(NCORE):
        b, qd = core // 4, core % 4
        t0 = qd * TOK
        xs = x[b, t0:t0 + TOK]
        xT = np.ascontiguousarray(xs.T.reshape(8, 128, TOK).transpose(1, 0, 2))
        halo = np.zeros((256, D), np.float32)
        fl = np.zeros((2,), np.float32)
        if qd > 0:
            halo[0:128] = x[b, t0 - 128:t0]
            fl[0] = 1.0
        if qd < 3:
            halo[128:256] = x[b, t0 + TOK:t0 + TOK + 128]
            fl[1] = 1.0
        xH = np.ascontiguousarray(halo.T.reshape(8, 128, 256).transpose(1, 0, 2))
        cf = np.zeros((128, 1400), np.float32)
        cf[:, 0:8] = fm(inp["c"][b])
        cf[:, 8:10] = fl[None, :]
        cf[:, 10:58] = fm(inp["norm_g"]).reshape(128, 48)
        cf[:, 58:202] = fm(inp["b_mod"]).reshape(128, 144)
        cw = np.asarray(inp["b_conv_w"], np.float32)[0]
        cf[:, 202:214] = fm(cw).transpose(0, 2, 1).reshape(128, 12)
        cf[:, 214:218] = fm(np.asarray(inp["b_conv_b"], np.float32)[0])
        aq = np.asarray(inp["a_qk_g"], np.float32)[0]
        cf[:, 218] = np.tile(aq[0], 2)
        cf[:, 220] = np.tile(aq[1], 2)
        cf[:, 222:230] = np.asarray(inp["a_sink"], np.float32)[0][None, :]
        cf[0:32, 232:240] = np.asarray(inp["rel_table"], np.float32)
        cg = np.asarray(inp["c_qk_g"], np.float32)[0]
        cf[:, 240] = np.tile(cg[0][eo], 2)
        cf[:, 241] = np.tile(cg[1][eo], 2)
        pos = np.arange(t0, t0 + TOK)
        row = (pos // 64).astype(np.float32)
        col = (pos % 64).astype(np.float32)
        inv = (np.float32(10000.0) ** (-np.arange(0, 32, 2, dtype=np.float32) / np.float32(32))).astype(np.float32)
        ang = np.concatenate([row[:, None] * inv, col[:, None] * inv], axis=-1).astype(np.float32)
        cs = np.cos(ang).astype(np.float32).T
        sn = np.sin(ang).astype(np.float32).T
        rope = np.zeros((128, 2, TOK), np.float32)
        for qd4 in range(4):
            rope[qd4 * 32:(qd4 + 1) * 32, 0] = cs
            rope[qd4 * 32:(qd4 + 1) * 32, 1] = sn if qd4 % 2 == 0 else -sn
        m_rope = rope
        m = dict(shared)
        m.update(xT=xT, xH=xH, cf=cf, rope=m_rope)
        maps.append(m)
    return maps


def emit_rope(P, C, qn, qnb, CS, SNs, ropeb, c0, w, out_ap, out_b):
    k = P.k
    t1, t1b = C.tmp()
    k.op("dve", lambda e: e.tensor_tensor(out=t1[:, 0:w], in0=qn[:, 0:w], in1=CS[:, c0:c0 + w], op=ALU.mult),
         reads=[qnb, ropeb], writes=[t1b])
    t2, t2b = C.tmp()
    for qd in range(4):
        src = qd ^ 1
        k.op("dve", lambda e, qd=qd, src=src: e.tensor_tensor(
            out=t2[qd * 32:(qd + 1) * 32, 0:w], in0=qn[src * 32:(src + 1) * 32, 0:w],
            in1=SNs[src * 32:(src + 1) * 32, c0:c0 + w], op=ALU.mult), reads=[qnb, ropeb], writes=[t2b])
    k.op("dve", lambda e: e.tensor_tensor(out=out_ap, in0=t1[:, 0:w], in1=t2[:, 0:w], op=ALU.add),
         reads=[t1b, t2b], writes=[out_b])


def emit_inproj1(P, C, H, win_l, gq, gk, CS, SNs, ropeb, qo_d, ko_d, vo_d, hco_d, outb):
    k = P.k
    for pr in range(3):
        wa, wab = P.load_wa(win_l[:, pr * 256:(pr + 1) * 256])
        for cl in range(2):
            if pr == 2 and cl == 1:
                break
            for j in range(4):
                c0, w = H.tiles[j]
                ps, psb = P.bank("h1", [0, 1, 2, 3])
                emit_inproj_tile(P, H, j, wa, wab, cl, ps, psb)
                qn, qnb = C.tmp()
                emit_qknorm(P, C, ps, psb, w, (gq if pr < 2 else gk)[:, 0:1], qn[:, 0:w], qnb)
                st, stb = C.tmpbf()
                emit_rope(P, C, qn, qnb, CS, SNs, ropeb, c0, w, st[:, 0:w], stb)
                dst = qo_d[:, pr * 2 + cl, c0:c0 + w] if pr < 2 else ko_d[:, c0:c0 + w]
                k.dma("sp", [(dst, st[:, 0:w])], reads=[stb], sembuf=stb, sink=outb)
        if pr == 2:
            for tb in range(16):
                j = tb // 4
                ps, psb = P.bank("o", [4, 5])
                for c in range(8):
                    k.op("pe", lambda e, c=c, tb=tb, ps=ps: e.matmul(
                        ps[:, 0:128], lhsT=H.t[:, c, tb * 128:(tb + 1) * 128], rhs=wa[:, c, 128:256],
                        start=(c == 0), stop=(c == 7)), reads=[wab, H.b[j]], writes=[psb])
                st, stb = C.tmpbf()
                k.op("act", lambda e, ps=ps, st=st: e.activation(out=st[:, 0:128], in_=ps[:, 0:128], func=AF.Copy),
                     reads=[psb], writes=[stb])
                k.dma("sp", [(vo_d[tb], st[:, 0:128])], reads=[stb], sembuf=stb, sink=outb)
    for pr in range(6):
        wa, wab = P.load_wa(win_l[:, 768 + pr * 256:768 + (pr + 1) * 256])
        for cl in range(2):
            ch = pr * 2 + cl
            for j in range(4):
                c0, w = H.tiles[j]
                ps, psb = P.bank("h1", [0, 1, 2, 3])
                emit_inproj_tile(P, H, j, wa, wab, cl, ps, psb)
                st, stb = C.tmp()
                k.op("act", lambda e, ps=ps, st=st, w=w: e.activation(out=st[:, 0:w], in_=ps[:, 0:w], func=AF.Copy),
                     reads=[psb], writes=[stb])
                k.dma("sp", [(hco_d[ch, :, c0:c0 + w], st[:, 0:w])], reads=[stb], sembuf=stb, sink=outb)


NFFT = 16384


def build_LB():
    nc = bass.Bass("TRN2", target_bir_lowering=False)
    k = KB(nc)
    dt_in = lambda name, shape: nc.dram_tensor(name, list(shape), F32, kind="ExternalInput").ap()
    hc_d = dt_in("hc3", [3, 128, S])
    cf_d = dt_in("cf2", [128, 32])
    mlp_d = dt_in("mlpw", [64, 456])
    zemb_d = dt_in("zemb", [33, NFFT])
    win_d = dt_in("win", [128, 128, 128])
    dft_d = dt_in("dftc", [128, 512])
    tw_d = dt_in("tw", [128, 2, 2, 128])
    y_d = nc.dram_tensor("yT", [128, S], F32, kind="ExternalOutput").ap()
    zs = nc.dram_tensor("zs", [128, S], F32, kind="Internal")
    cs = nc.dram_tensor("cs", [128, S], F32, kind="Internal")

    PS = [k.ps(f"ps{i}", [128, 512]) for i in range(8)]
    PSb = [Buf(f"ps{i}") for i in range(8)]
    rr = {}

    def bank(group, banks):
        i = rr.get(group, 0)
        rr[group] = (i + 1) % len(banks)
        return PS[banks[i]], PSb[banks[i]]

    tmp = RotTmp(k, "tf", 8, F32)
    tmpbf = RotTmp(k, "tb", 4, BF16)
    cb = Buf("consts")
    CF = k.sb("CF", [128, 32], F32)
    MLP = k.sb("MLP", [64, 456], F32)
    DFT = k.sb("DFT", [128, 512], BF16)
    TW = k.sb("TW", [128, 2, 2, 128], F32)
    X0 = k.sb("X0", [128, S], F32)
    Z = k.sb("Z", [128, S], F32)
    IN = k.sb("IN", [128, S + 2], F32)
    KC = k.sb("KC", [128, 128, 128], BF16)
    ZC = k.sb("ZC", [64, 128, 128], BF16)
    ACC = k.sb("ACC", [128, 128], F32)
    SM = k.sb("SM", [128, 16], F32)
    ones_f = k.sb("ones_f", [128, 1], F32)
    X0b, Zb, INb, KCb, ZCb, ACCb, SMb = [Buf(n) for n in "X0 Z IN KC ZC ACC SM".split()]

    k.dma("sp", [(CF[:], cf_d), (MLP[:], mlp_d), (TW[:], tw_d)], writes=[cb])
    dftb = Buf("dft")
    k.dma("pool", [(DFT[:], dft_d)], writes=[dftb])
    Fre, Fim, nFim = DFT[:, 0:128], DFT[:, 128:256], DFT[:, 384:512]
    Fcat, FcatI2, FcatI1 = DFT[:, 0:256], DFT[:, 128:384], DFT[:, 256:512]
    k.op("dve", lambda e: e.memset(ones_f[:], 1.0), writes=[SMb])
    k.op("dve", lambda e: e.memset(SM[:, 0:1], math.pi / 2), writes=[SMb])
    k.op("dve", lambda e: e.memset(SM[:, 1:2], EPS), writes=[SMb])
    k.op("dve", lambda e: e.memset(IN[:, 0:1], 0.0), writes=[INb])
    k.op("dve", lambda e: e.memset(IN[:, S + 1:S + 2], 0.0), writes=[INb])

    CH = 2048
    for part in range(3):
        k.dma("sp", [(IN[:, 1:S + 1], hc_d[part])], writes=[INb])
        for cc in range(S // CH):
            c0 = cc * CH
            wc = CF[:, part * 4:part * 4 + 3]
            bc = CF[:, part * 4 + 3:part * 4 + 4]
            for s0 in range(0, CH, 512):
                a0 = c0 + s0
                if part == 0:
                    o, ob_ = X0[:, a0:a0 + 512], X0b
                elif part == 1:
                    o, ob_ = Z[:, a0:a0 + 512], Zb
                else:
                    tt, ttb = tmp()
                    o, ob_ = tt[:, 0:512], ttb
                k.op("dve", lambda e, o=o, a0=a0, wc=wc, bc=bc: e.tensor_scalar(
                    out=o, in0=IN[:, a0 + 1:a0 + 513], scalar1=wc[:, 1:2], scalar2=bc, op0=ALU.mult, op1=ALU.add),
                    reads=[INb, cb], writes=[ob_])
                k.op("dve", lambda e, o=o, a0=a0, wc=wc: e.scalar_tensor_tensor(
                    out=o, in0=IN[:, a0:a0 + 512], scalar=wc[:, 0:1], in1=o, op0=ALU.mult, op1=ALU.add),
                    reads=[INb, cb, ob_], writes=[ob_])
                k.op("dve", lambda e, o=o, a0=a0, wc=wc: e.scalar_tensor_tensor(
                    out=o, in0=IN[:, a0 + 2:a0 + 514], scalar=wc[:, 2:3], in1=o, op0=ALU.mult, op1=ALU.add),
                    reads=[INb, cb, ob_], writes=[ob_])
                if part == 2:
                    k.op("pool", lambda e, o=o, a0=a0: e.tensor_tensor(
                        out=Z[:, a0:a0 + 512], in0=Z[:, a0:a0 + 512], in1=o, op=ALU.mult), reads=[ob_, Zb], writes=[Zb])
    zsb = Buf("zs")
    k.dma("sp", [(zs.ap(), Z[:])], reads=[Zb], writes=[zsb])
    src = bass.AP(tensor=zs, offset=0, ap=[[128, 64], [S, 128], [1, 128]])
    k.dma("pool", [(ZC[:], src)], reads=[zsb], writes=[ZCb])

    W1, W2, W3 = MLP[0:33, 0:64], MLP[:, 64:128], MLP[:, 128:192]
    W4 = MLP[:, 200:456].rearrange("p (d c) -> p d c", d=2)
    k.op("dve", lambda e: e.tensor_scalar(out=SM[0:64, 2:3], in0=MLP[:, 195:196], scalar1=0.25, scalar2=None, op0=ALU.mult),
         reads=[cb], writes=[SMb])
    for i in range(3):
        k.op("dve", lambda e, i=i: e.tensor_tensor(out=SM[0:64, 3 + i:4 + i], in0=MLP[:, 192 + i:193 + i], in1=SM[0:64, 2:3],
                                                   op=ALU.mult), reads=[cb, SMb], writes=[SMb])
    k.op("dve", lambda e: e.memset(ACC[:], 0.0), writes=[ACCb])

    def sin4(ps, psb, li):
        s1, s1b = tmp()
        k.op("act", lambda e: e.activation(out=s1[0:64, :], in_=ps[0:64, :], func=AF.Sin, bias=SM[0:64, 3 + li:4 + li],
                                           scale=SM[0:64, 2:3]), reads=[psb, SMb], writes=[s1b])
        a1, a1b = tmp()
        k.op("act", lambda e: e.activation(out=a1[0:64, :], in_=ps[0:64, :], func=AF.Abs, bias=SM[0:64, 3 + li:4 + li],
                                           scale=SM[0:64, 2:3]), reads=[psb, SMb], writes=[a1b])
        k.op("act", lambda e: e.activation(out=a1[0:64, :], in_=a1[0:64, :], func=AF.Sin, bias=SM[0:64, 0:1], scale=-1.0),
             reads=[a1b, SMb], writes=[a1b])
        k.op("dve", lambda e: e.tensor_tensor(out=a1[0:64, :], in0=a1[0:64, :], in1=s1[0:64, :], op=ALU.mult),
             reads=[a1b, s1b], writes=[a1b])
        k.op("dve", lambda e: e.tensor_tensor(out=s1[0:64, :], in0=s1[0:64, :], in1=s1[0:64, :], op=ALU.mult),
             reads=[s1b], writes=[s1b])
        k.op("dve", lambda e: e.tensor_scalar(out=s1[0:64, :], in0=s1[0:64, :], scalar1=-2.0, scalar2=1.0,
                                              op0=ALU.mult, op1=ALU.add), reads=[s1b], writes=[s1b])
        k.op("dve", lambda e: e.scalar_tensor_tensor(out=a1[0:64, :], in0=a1[0:64, :], scalar=4.0, in1=s1[0:64, :],
                                                     op0=ALU.mult, op1=ALU.mult), reads=[a1b, s1b], writes=[a1b])
        return a1, a1b

    for c in range(32):
        ze, zeb = tmp()
        k.dma("sp", [(ze[0:33, :], zemb_d[:, c * 512:(c + 1) * 512])], writes=[zeb])
        wt, wtb = tmp()
        k.dma("sp", [(wt[:, :].rearrange("p (n c) -> p n c", n=4), win_d[:, c * 4:(c + 1) * 4, :])], writes=[wtb])
        ps, psb = bank("m", [6, 7])
        k.op("pe", lambda e, ps=ps, ze=ze: e.matmul(ps[0:64, :], lhsT=W1, rhs=ze[0:33, :], start=True, stop=True),
             reads=[zeb, cb], writes=[psb])
        h, hb = sin4(ps, psb, 0)
        for li, W in ((1, W2), (2, W3)):
            ps, psb = bank("m", [6, 7])
            k.op("pe", lambda e, ps=ps, h=h, W=W: e.matmul(ps[0:64, :], lhsT=W, rhs=h[0:64, :], start=True, stop=True),
                 reads=[hb, cb], writes=[psb])
            h, hb = sin4(ps, psb, li)
        ps4, ps4b = bank("m", [6, 7])
        for n2l in range(4):
            for d in range(2):
                k.op("pe", lambda e, ps4=ps4, h=h, n2l=n2l, d=d: e.matmul(
                    ps4[d * 64:(d + 1) * 64, n2l * 128:(n2l + 1) * 128],
                    lhsT=h[0:64, n2l * 128 + d * 64:n2l * 128 + (d + 1) * 64], rhs=W4[:, d, :],
                    start=True, stop=True, skip_group_check=True), reads=[hb, cb], writes=[ps4b])
        kc, kcb = tmp()
        k.op("dve", lambda e, kc=kc, ps4=ps4, wt=wt: e.tensor_tensor(out=kc[:, :], in0=ps4[:, :], in1=wt[:, :], op=ALU.mult),
             reads=[ps4b, wtb], writes=[kcb])
        k.op("act", lambda e, kc=kc, c=c: e.activation(
            out=KC[:, :, c * 4:(c + 1) * 4], in_=kc[:, :].rearrange("p (n c) -> p c n", n=4), func=AF.Copy),
            reads=[kcb], writes=[KCb])
        sq, sqb = tmp()
        k.op("pool", lambda e, kc=kc, sq=sq: e.tensor_tensor(out=sq[:, :], in0=kc[:, :], in1=kc[:, :], op=ALU.mult),
             reads=[kcb], writes=[sqb])
        rd, rdb = tmp()
        k.op("dve", lambda e, sq=sq, rd=rd: e.tensor_reduce(
            out=rd[:, 0:128], in_=sq[:, :].rearrange("p (n c) -> p c n", n=4), axis=mybir.AxisListType.X, op=ALU.add),
            reads=[sqb], writes=[rdb])
        k.op("pool", lambda e, rd=rd: e.tensor_tensor(out=ACC[:], in0=ACC[:], in1=rd[:, 0:128], op=ALU.add),
             reads=[rdb, ACCb], writes=[ACCb])
    ps, psb = bank("m", [6, 7])
    k.op("pe", lambda e, ps=ps: e.matmul(ps[:, 0:1], lhsT=ACC[:], rhs=ones_f[:, 0:1], start=True, stop=True),
         reads=[ACCb, SMb], writes=[psb])
    k.op("act", lambda e, ps=ps: e.activation(out=SM[:, 8:9], in_=ps[:, 0:1], func=AF.Sqrt, bias=SM[:, 1:2], scale=1.0),
         reads=[psb, SMb], writes=[SMb])
    k.op("dve", lambda e: e.reciprocal(out=SM[:, 8:9], in_=SM[:, 8:9]), reads=[SMb], writes=[SMb])

    Bre = k.sb("Bre", [128, 4, 128], BF16)
    Bim = k.sb("Bim", [128, 4, 128], BF16)
    Hre = k.sb("Hre", [128, 512], F32)
    Him = k.sb("Him", [128, 512], F32)
    Yre = k.sb("Yre", [128, 4, 128], BF16)
    Yim = k.sb("Yim", [128, 4, 128], BF16)
    Qre = k.sb("Qre", [128, 4, 128], BF16)
    Qim = k.sb("Qim", [128, 4, 128], BF16)
    Bb, Hb, Yb, Qb = Buf("B"), Buf("H"), Buf("Y"), Buf("Q")
    TWre2, TWim2 = TW[:, 0], TW[:, 1]

    def cmul_from_psum(A, Ab, sign, outre, outim, outb, pr):
        A4 = A[:, :].rearrange("p (c r k) -> p c r k", c=2, r=2)
        Are, Aim = A4[:, :, 0, :], A4[:, :, 1, :]
        t1, t1b = tmp()
        t2, t2b = tmp()
        v = lambda t: t[:, 0:256].rearrange("p (c k) -> p c k", c=2)
        k.op("dve", lambda e: e.tensor_tensor(out=v(t1), in0=Are, in1=TWre2, op=ALU.mult), reads=[Ab, cb], writes=[t1b])
        k.op("dve", lambda e: e.tensor_tensor(out=v(t2), in0=Aim, in1=TWim2, op=ALU.mult), reads=[Ab, cb], writes=[t2b])
        k.op("pool", lambda e: e.tensor_tensor(out=outre[:, pr * 2:pr * 2 + 2, :], in0=v(t1), in1=v(t2),
                                               op=(ALU.subtract if sign > 0 else ALU.add)),
             reads=[t1b, t2b], writes=[outb])
        t3, t3b = tmp()
        t4, t4b = tmp()
        k.op("dve", lambda e: e.tensor_tensor(out=v(t3), in0=Aim, in1=TWre2, op=ALU.mult), reads=[Ab, cb], writes=[t3b])
        k.op("dve", lambda e: e.tensor_tensor(out=v(t4), in0=Are, in1=TWim2, op=ALU.mult), reads=[Ab, cb], writes=[t4b])
        k.op("pool", lambda e: e.tensor_tensor(out=outim[:, pr * 2:pr * 2 + 2, :], in0=v(t3), in1=v(t4),
                                               op=(ALU.add if sign > 0 else ALU.subtract)),
             reads=[t3b, t4b], writes=[outb])

    def fft_fwd(src, srcb, krows, ch0, xre, xreb, xim, ximb):
        for pr in range(2):
            A, Ab = bank("A", [0, 1])
            for cl in range(2):
                ch = ch0 + pr * 2 + cl
                k.op("pe", lambda e, A=A, cl=cl, ch=ch: e.matmul(
                    A[:, cl * 256:(cl + 1) * 256], lhsT=src[0:krows, ch, :], rhs=Fcat[0:krows, :],
                    start=True, stop=True, skip_group_check=True), reads=[srcb, dftb], writes=[Ab])
            cmul_from_psum(A, Ab, +1, Bre, Bim, Bb, pr)
        bre = Bre[:, :, :].rearrange("p c k -> p (c k)")
        bim = Bim[:, :, :].rearrange("p c k -> p (c k)")
        k.op("pe", lambda e: e.matmul(xre[:, :], lhsT=Fre, rhs=bre, start=True, stop=False), reads=[Bb, dftb], writes=[xreb])
        k.op("pe", lambda e: e.matmul(xre[:, :], lhsT=nFim, rhs=bim, start=False, stop=True), reads=[Bb, dftb], writes=[xreb])
        k.op("pe", lambda e: e.matmul(xim[:, :], lhsT=Fim, rhs=bre, start=True, stop=False), reads=[Bb, dftb], writes=[ximb])
        k.op("pe", lambda e: e.matmul(xim[:, :], lhsT=Fre, rhs=bim, start=False, stop=True), reads=[Bb, dftb], writes=[ximb])

    csb = Sink("cs")
    for g in range(32):
        ch0 = g * 4
        fft_fwd(KC, KCb, 128, ch0, PS[2], PSb[2], PS[3], PSb[3])
        k.op("act", lambda e: e.activation(out=Hre[:], in_=PS[2][:, :], func=AF.Copy), reads=[PSb[2]], writes=[Hb])
        k.op("act", lambda e: e.activation(out=Him[:], in_=PS[3][:, :], func=AF.Copy), reads=[PSb[3]], writes=[Hb])
        fft_fwd(ZC, ZCb, 64, ch0, PS[4], PSb[4], PS[5], PSb[5])
        t1, t1b = tmp()
        t2, t2b = tmp()
        k.op("dve", lambda e, t1=t1: e.tensor_tensor(out=t1[:, :], in0=PS[4][:, :], in1=Hre[:], op=ALU.mult),
             reads=[PSb[4], Hb], writes=[t1b])
        k.op("dve", lambda e, t2=t2: e.tensor_tensor(out=t2[:, :], in0=PS[5][:, :], in1=Him[:], op=ALU.mult),
             reads=[PSb[5], Hb], writes=[t2b])
        k.op("pool", lambda e, t1=t1, t2=t2: e.tensor_tensor(out=Yre[:, :, :].rearrange("p c k -> p (c k)"), in0=t1[:, :],
                                                             in1=t2[:, :], op=ALU.subtract), reads=[t1b, t2b], writes=[Yb])
        t3, t3b = tmp()
        t4, t4b = tmp()
        k.op("dve", lambda e, t3=t3: e.tensor_tensor(out=t3[:, :], in0=PS[4][:, :], in1=Him[:], op=ALU.mult),
             reads=[PSb[4], Hb], writes=[t3b])
        k.op("dve", lambda e, t4=t4: e.tensor_tensor(out=t4[:, :], in0=PS[5][:, :], in1=Hre[:], op=ALU.mult),
             reads=[PSb[5], Hb], writes=[t4b])
        k.op("pool", lambda e, t3=t3, t4=t4: e.tensor_tensor(out=Yim[:, :, :].rearrange("p c k -> p (c k)"), in0=t3[:, :],
                                                             in1=t4[:, :], op=ALU.add), reads=[t3b, t4b], writes=[Yb])
        for pr in range(2):
            Pk, Pkb = bank("A", [0, 1])
            for cl in range(2):
                c4 = pr * 2 + cl
                k.op("pe", lambda e, Pk=Pk, cl=cl, c4=c4: e.matmul(
                    Pk[:, cl * 256:(cl + 1) * 256], lhsT=Yre[:, c4, :], rhs=FcatI1, start=True, stop=False,
                    skip_group_check=True), reads=[Yb, dftb], writes=[Pkb])
                k.op("pe", lambda e, Pk=Pk, cl=cl, c4=c4: e.matmul(
                    Pk[:, cl * 256:(cl + 1) * 256], lhsT=Yim[:, c4, :], rhs=FcatI2, start=False, stop=True,
                    skip_group_check=True), reads=[Yb, dftb], writes=[Pkb])
            cmul_from_psum(Pk, Pkb, -1, Qre, Qim, Qb, pr)
        yo, yob = PS[6], PSb[6]
        k.op("pe", lambda e: e.matmul(yo[0:64, :], lhsT=DFT[:, 0:64], rhs=Qre[:, :, :].rearrange("p c k -> p (c k)"),
                                      start=True, stop=False), reads=[Qb, dftb], writes=[yob])
        k.op("pe", lambda e: e.matmul(yo[0:64, :], lhsT=DFT[:, 128:192], rhs=Qim[:, :, :].rearrange("p c k -> p (c k)"),
                                      start=False, stop=True), reads=[Qb, dftb], writes=[yob])
        ys, ysb = tmp()
        k.op("act", lambda e, ys=ys: e.activation(out=ys[0:64, :], in_=yo[0:64, :], func=AF.Copy, scale=1.0 / NFFT),
             reads=[yob], writes=[ysb])
        dst = bass.AP(tensor=cs, offset=ch0 * S, ap=[[128, 64], [S, 4], [1, 128]])
        k.dma("sp", [(dst, ys[0:64, :].rearrange("p (c k) -> p c k", c=4))], reads=[ysb], sembuf=ysb, sink=csb)

    k.wait_all("sp", [csb])
    k.dma("sp", [(IN[:, 0:S], cs.ap())], writes=[INb])
    ob = Sink("out")
    for s0 in range(0, S, 512):
        t, tb = tmp()
        k.op("dve", lambda e, t=t, s0=s0: e.tensor_scalar(out=t[:, :], in0=Z[:, s0:s0 + 512], scalar1=CF[:, 12:13],
                                                          scalar2=None, op0=ALU.mult), reads=[Zb, cb], writes=[tb])
        k.op("dve", lambda e, t=t, s0=s0: e.scalar_tensor_tensor(out=t[:, :], in0=IN[:, s0:s0 + 512], scalar=SM[:, 8:9],
                                                                 in1=t[:, :], op0=ALU.mult, op1=ALU.add),
             reads=[INb, SMb, tb], writes=[tb])
        k.op("pool", lambda e, t=t, s0=s0: e.tensor_tensor(out=t[:, :], in0=t[:, :], in1=X0[:, s0:s0 + 512], op=ALU.mult),
             reads=[tb, X0b], writes=[tb])
        k.dma("sp", [(y_d[:, s0:s0 + 512], t[:, :])], reads=[tb], sembuf=tb, sink=ob)
    k.wait_all("sp", [ob])
    k.close()
    return nc


def host_consts_LB(cq):
    f32 = np.float32
    L = S
    m = np.arange(NFFT)
    lag = np.where(m < L, m, NFFT - m)
    lag = np.where(m == L, 0, lag)
    t_all = np.linspace(0.0, 1.0, L, dtype=f32)
    t = t_all[lag]
    w = (f32(2.0 * math.pi / L) * lag.astype(f32)).astype(f32)
    fr = np.linspace(1e-4, 15, 16, dtype=f32)
    zf = np.concatenate([t[:, None], np.cos(fr[None, :] * w[:, None]), -np.sin(fr[None, :] * w[:, None])], axis=-1).astype(f32)
    zemb = np.ascontiguousarray(zf.reshape(128, 128, 33).transpose(2, 1, 0).reshape(33, NFFT))
    dmin, dmax = math.log(1e-2) / 0.3, math.log(1e-2) / 1.5
    deltas = np.abs(np.linspace(dmin, dmax, 512, dtype=f32))[cq * 128:(cq + 1) * 128]
    win = np.exp(-t[:, None] * deltas[None, :]).astype(f32)
    win[L] = 0.0
    win = np.ascontiguousarray(win.reshape(128, 128, 128))
    n = np.arange(128)
    ang = 2.0 * np.pi * np.outer(n, n) / 128.0
    fre, fim = np.cos(ang), -np.sin(ang)
    dftc = np.concatenate([fre, fim, fre, -fim], axis=1).astype(f32)
    ang2 = 2.0 * np.pi * np.outer(n, n) / NFFT
    tw = np.stack([np.cos(ang2), -np.sin(ang2)], 0).astype(f32)
    tw = np.ascontiguousarray(np.broadcast_to(tw[:, None], (2, 2, 128, 128)).transpose(2, 0, 1, 3))
    return zemb, win, dftc, tw


def prep_LB(inp, hco_all):
    maps = []
    cache = {}
    for core in range(NCORE):
        b, cq = core // 4, core % 4
        if cq not in cache:
            cache[cq] = host_consts_LB(cq)
        zemb, win, dftc, tw = cache[cq]
        hc3 = np.zeros((3, 128, S), np.float32)
        for part in range(3):
            for src in range(4):
                hc3[part, :, src * TOK:(src + 1) * TOK] = hco_all[b * 4 + src][part * 4 + cq]
        cf2 = np.zeros((128, 32), np.float32)
        cw = np.asarray(inp["d_conv_w"], np.float32)[0]
        cbias = np.asarray(inp["d_conv_b"], np.float32)[0]
        for part in range(3):
            sl = slice(part * 512 + cq * 128, part * 512 + (cq + 1) * 128)
            cf2[:, part * 4:part * 4 + 3] = cw[:, sl].T
            cf2[:, part * 4 + 3] = cbias[sl]
        cf2[:, 12] = np.asarray(inp["d_skip"], np.float32)[0][cq * 128:(cq + 1) * 128]
        mlp = np.zeros((64, 456), np.float32)
        mlp[0:33, 0:64] = np.asarray(inp["d_f_w1"], np.float32)[0]
        mlp[:, 64:128] = np.asarray(inp["d_f_w2"], np.float32)[0]
        mlp[:, 128:192] = np.asarray(inp["d_f_w3"], np.float32)[0]
        mlp[:, 192] = np.asarray(inp["d_f_b1"], np.float32)[0]
        mlp[:, 193] = np.asarray(inp["d_f_b2"], np.float32)[0]
        mlp[:, 194] = np.asarray(inp["d_f_b3"], np.float32)[0]
        mlp[:, 195] = np.asarray(inp["d_f_freq"], np.float32)[0]
        w4 = np.asarray(inp["d_f_w4"], np.float32)[0]
        mlp[:, 200:328] = w4[:, cq * 128:(cq + 1) * 128]
        mlp[:, 328:456] = w4[:, 512 + cq * 128:512 + (cq + 1) * 128]
        maps.append(dict(hc3=hc3, cf2=cf2, mlpw=mlp, zemb=zemb, win=win, dftc=dftc, tw=tw))
    return maps


def emit_attn1(P, C, QT, QTb, KT, KTb, VA, VAb, MIX, MIXb):
    k = P.k
    for jq in range(4):
        q0 = jq * 512
        for g in range(2):
            gs = slice(g * 64, (g + 1) * 64)
            ds_ = slice((1 - g) * 64, (2 - g) * 64)
            for hh in range(4):
                po, pob = P.bank("o", [4, 5])
                for kb in range(64):
                    ps, psb = P.bank("h1", [0, 1, 2, 3])
                    k.op("pe", lambda e, ps=ps, kb=kb, gs=gs, hh=hh, q0=q0: e.matmul(
                        ps[:, :], lhsT=KT[gs, kb * 128:(kb + 1) * 128], rhs=QT[gs, hh, q0:q0 + 512],
                        start=True, stop=True), reads=[KTb, QTb], writes=[psb])
                    pt, ptb = C.tmpbf()
                    k.op("act", lambda e, ps=ps, pt=pt: e.activation(out=pt[:, :], in_=ps[:, :], func=AF.Exp,
                                                                     scale=HD ** -0.5), reads=[psb], writes=[ptb])
                    k.op("pe", lambda e, po=po, pt=pt, kb=kb, g=g: e.matmul(
                        po[:, :], lhsT=VA[:, kb, g, :], rhs=pt[:, :], start=(kb == 0), stop=(kb == 63)),
                        reads=[VAb, ptb], writes=[pob])
                rd, rdb = C.tmp()
                k.op("dve", lambda e, rd=rd, po=po, ds_=ds_: e.reciprocal(out=rd[ds_, :], in_=po[ds_, :]),
                     reads=[pob], writes=[rdb])
                k.op("dve", lambda e, rd=rd, po=po, gs=gs, ds_=ds_, hh=hh, q0=q0: e.tensor_tensor(
                    out=MIX[gs, hh, q0:q0 + 512], in0=po[gs, :], in1=rd[ds_, :], op=ALU.mult),
                    reads=[pob, rdb], writes=[MIXb[jq]])


def build_LC():
    nc = bass.Bass("TRN2", target_bir_lowering=False)
    k = KB(nc)
    dt_in = lambda name, shape, dt=F32: nc.dram_tensor(name, list(shape), dt, kind="ExternalInput").ap()
    xT_d = dt_in("xT", [128, 8, 2048])
    mod_d = dt_in("modi", [128, 96])
    q_d = dt_in("q", [128, 4, 2048], BF16)
    k_d = dt_in("kk", [128, S], BF16)
    v_d = dt_in("v", [64, 128, 128], BF16)
    y_d = dt_in("yh", [128, 4, 2048])
    w1_d = dt_in("w1", [1024, DFFP])
    w3_d = dt_in("w3", [1024, DFFP])
    w2_d = dt_in("w2", [DFFP, 1024])
    wout_d = dt_in("wout", [1024, 1024])
    xo_d = nc.dram_tensor("xo", [128, 8, 2048], F32, kind="ExternalOutput").ap()

    P = Prog(nc, k)
    C = Common()
    C.tmp = RotTmp(k, "tf", 5, F32)
    C.tmpbf = RotTmp(k, "tb", 4, BF16)
    C.RS = k.sb("RS", [128, 512], F32)
    C.RSb = Buf("RS")
    C.cb = Buf("consts")
    C.modb = Buf("mod")
    C.ones_bf = k.sb("ones_bf", [128, 128], BF16)
    C.eps = k.sb("eps", [128, 1], F32)
    MOD = k.sb("mod", [128, 96], F32)
    XT = k.sb("XT", [128, 8, 2048], F32)
    AR = k.sb("AR", [128, 32768], BF16)
    MIXt = k.sb("MIX", [128, 8, 2048], BF16)
    VA = AR[:, 0:16384].rearrange("p (b g d) -> p b g d", b=64, g=2)
    KT = AR[:, 16384:24576]
    QT = AR[:, 24576:32768].rearrange("p (h t) -> p h t", h=4)
    Ht = AR[:, 0:16384].rearrange("p (c t) -> p c t", c=8)
    U = AR[:, 16384:16384 + 6 * 2048].rearrange("p (f t) -> p f t", f=6)

    tiles4 = [(0, 512), (512, 512), (1024, 512), (1536, 512)]
    XB = [Buf(f"x{j}") for j in range(4)]

    class XAct:
        tiles = tiles4
        b = XB

        def ap(self, c, j):
            return XT[:, c, j * 512:(j + 1) * 512]
    X4 = XAct()
    H4 = Act(Ht, tiles4, "h")
    Ub = [[Buf(f"u{f}_{j}") for j in range(4)] for f in range(6)]
    MIXb = [Buf(f"mix{j}") for j in range(4)]
    QTb, KTb, VAb = Buf("qt"), Buf("kt"), Buf("va")

    k.dma("sp", [(MOD[:], mod_d)], writes=[C.modb])
    k.dma("sp", [(QT, q_d)], writes=[QTb])
    k.dma("sp", [(KT, k_d)], writes=[KTb])
    k.op("dve", lambda e: e.memset(C.ones_bf[:], 1.0), writes=[C.cb])
    k.op("dve", lambda e: e.memset(C.eps[:], EPS), writes=[C.cb])
    k.op("dve", lambda e: e.memset(VA[:, :, 0, 64:128], 1.0), writes=[VAb])
    k.op("dve", lambda e: e.memset(VA[:, :, 1, 0:64], 1.0), writes=[VAb])
    vsrc = v_d.rearrange("b p d -> p b d")
    k.dma("sp", [(VA[:, :, 0, 0:64], vsrc[:, :, 0:64]), (VA[:, :, 1, 64:128], vsrc[:, :, 64:128])], writes=[VAb])
    for j in range(4):
        k.dma("sp", [(XT[:, :, j * 512:(j + 1) * 512], xT_d[:, :, j * 512:(j + 1) * 512])], writes=[XB[j]])
    for j in range(4):
        k.dma("pool", [(MIXt[:, 4:8, j * 512:(j + 1) * 512], y_d[:, :, j * 512:(j + 1) * 512])], writes=[MIXb[j]])

    def mcol(i, kind):
        return MOD[:, (i * 3 + kind) * 8:(i * 3 + kind + 1) * 8]

    emit_attn1(P, C, QT, QTb, KT, KTb, VA, VAb, MIXt, MIXb)
    emit_outproj(P, C, X4, lambda fc, c0, w: MIXt[:, fc, c0:c0 + w], MIXb, wout_d, mcol(1, 2))
    k.barrier()
    emit_adaln(P, X4, H4, MOD[:, 72 + 16:72 + 24], mcol(2, 0), C)
    emit_ffn(P, X4, H4, U, Ub, w1_d, w3_d, w2_d, mcol(2, 2), C)
    ob = Sink("out")
    for j in range(4):
        k.dma("sp", [(xo_d[:, :, j * 512:(j + 1) * 512], XT[:, :, j * 512:(j + 1) * 512])], reads=[XB[j]],
              sembuf=XB[j], sink=ob)
    k.wait_all("sp", [ob])
    k.close()
    return nc


def prep_LC(inp, la_res, lb_res, shared):
    maps = []
    for core in range(NCORE):
        b, qd = core // 4, core % 4
        kk = np.concatenate([la_res[b * 4 + s]["ko"] for s in range(4)], axis=1)
        v = np.concatenate([la_res[b * 4 + s]["vo"] for s in range(4)], axis=0)
        yh = np.stack([lb_res[b * 4 + cq]["yT"][:, qd * TOK:(qd + 1) * TOK] for cq in range(4)], axis=1)
        maps.append(dict(xT=la_res[core]["xo"], modi=la_res[core]["modo"], q=la_res[core]["qo"],
                         kk=np.ascontiguousarray(kk), v=np.ascontiguousarray(v), yh=np.ascontiguousarray(yh),
                         w1=shared["w1"][1, 1], w3=shared["w3"][1, 1], w2=shared["w2"][1, 1], wout=shared["wout"][1]))
    return maps


_CACHE = {}


FUSED = True


def kernel(**inputs):
    inp = {kk: np.asarray(v) for kk, v in inputs.items()}
    cores = list(range(NCORE))
    if FUSED:
        if "fused" not in _CACHE:
            _CACHE["fused"] = build_fused()
        maps = prep_fused(inp)
        res = run_bass_kernel_spmd(_CACHE["fused"], maps, core_ids=cores).results
        out = np.zeros((2, S, D), np.float32)
        for core in cores:
            b, qd = core // 4, core % 4
            xo = np.asarray(res[core]["xo"], np.float32)
            out[b, qd * TOK:(qd + 1) * TOK] = xo.transpose(2, 1, 0).reshape(TOK, D)
        return out
    if "la" not in _CACHE:
        _CACHE["la"] = build_LA(True)
        _CACHE["lb"] = build_LB()
        _CACHE["lc"] = build_LC()
    la_maps = prep_LA(inp)
    la = run_bass_kernel_spmd(_CACHE["la"], la_maps, core_ids=cores).results
    lb_maps = prep_LB(inp, [la[c]["hco"] for c in cores])
    lb = run_bass_kernel_spmd(_CACHE["lb"], lb_maps, core_ids=cores).results
    lc_maps = prep_LC(inp, la, lb, la_maps[0])
    lc = run_bass_kernel_spmd(_CACHE["lc"], lc_maps, core_ids=cores).results
    out = np.zeros((2, S, D), np.float32)
    for core in cores:
        b, qd = core // 4, core % 4
        xo = np.asarray(lc[core]["xo"], np.float32)
        out[b, qd * TOK:(qd + 1) * TOK] = xo.transpose(2, 1, 0).reshape(TOK, D)
    return out


U32 = mybir.dt.uint32


def kb_gather(k, dst_ap, src_dram_ap, idx_ap, reads=(), writes=()):
    sbf = writes[0]
    if sbf.dsem is None:
        sbf.dsem = {}
        sbf.dcnt = {}
    if True not in sbf.dsem:
        sbf.dsem[True] = ("d", id(sbf), True)
        sbf.dcnt[True] = 0
        k.sems[sbf.dsem[True]] = k._newsem(f"d_{k.nsem}")
    k._wait("pool", k._deps(reads, writes))
    k.nc.gpsimd.indirect_dma_start(out=dst_ap, out_offset=None, in_=src_dram_ap,
                                   in_offset=bass.IndirectOffsetOnAxis(ap=idx_ap, axis=0)
                                   ).then_inc(k.sems[sbf.dsem[True]], 16)
    sbf.dcnt[True] += 16
    tok = (sbf.dsem[True], sbf.dcnt[True])
    k._commit(tok, reads, writes)
    return tok


class CC:
    n = 0

    def __init__(self, k, groups, fake):
        self.k, self.groups, self.fake = k, groups, fake
        self.pool_sems = [] if fake else [k._newsem(f"cc{i}") for i in range(16)]
        self.alltoks = {}

    def allgather(self, src_t, dst_t, reads, dstb, sinks=(), sl=None):
        k = self.k
        sap = src_t.ap() if sl is None else src_t.ap()[sl]
        dap = dst_t.ap() if sl is None else dst_t.ap()[sl]
        rows = sap.shape[0]
        k.wait_all("sp" if self.fake else "pool", list(sinks))
        if self.fake:
            pairs = [(dap[r * rows:(r + 1) * rows, :], sap) for r in range(4)]
            k.dma("sp", pairs, reads=reads, writes=[dstb])
            return
        CC.n += 1
        key = ("cc", CC.n)
        k.sems[key] = self.pool_sems.pop(0)
        k._wait("pool", k._deps(reads, [dstb]))
        k.nc.gpsimd.collective_compute("AllGather", ALU.bypass, replica_groups=self.groups,
                                       ins=[sap.opt()], outs=[dap.opt()]).then_inc(k.sems[key])
        k._commit((key, 1), reads, [dstb])
        self.alltoks.setdefault(id(dstb), Sink("cc")).toks[key] = 1

    def sink(self, dstb):
        return self.alltoks.get(id(dstb), Sink("none"))


def emit_filter(P, C, F, kc_s, kcsb):
    k = P.k
    tmp, MLP, SM = C.tmp, F.MLP, F.SM
    cb = C.cb
    W1, W2, W3 = MLP[0:33, 0:64], MLP[:, 64:128], MLP[:, 128:192]
    W4 = MLP[:, 200:456].rearrange("p (d c) -> p d c", d=2)
    SMb, ACC, ACCb = F.SMb, F.ACC, F.ACCb
    k.op("dve", lambda e: e.memset(SM[:, 0:1], math.pi / 2), writes=[SMb])
    k.op("dve", lambda e: e.memset(SM[:, 1:2], EPS), writes=[SMb])
    k.op("dve", lambda e: e.tensor_scalar(out=SM[0:64, 2:3], in0=MLP[:, 195:196], scalar1=0.25, scalar2=None, op0=ALU.mult),
         reads=[cb], writes=[SMb])
    for i in range(3):
        k.op("dve", lambda e, i=i: e.tensor_tensor(out=SM[0:64, 3 + i:4 + i], in0=MLP[:, 192 + i:193 + i], in1=SM[0:64, 2:3],
                                                   op=ALU.mult), reads=[cb, SMb], writes=[SMb])
    k.op("dve", lambda e: e.memset(ACC[:], 0.0), writes=[ACCb])

    def sin4(ps, psb, li, out, outb):
        s1, s1b = tmp()
        k.op("act", lambda e: e.activation(out=s1[0:64, :], in_=ps[0:64, :], func=AF.Sin, bias=SM[0:64, 3 + li:4 + li],
                                           scale=SM[0:64, 2:3]), reads=[psb, SMb], writes=[s1b])
        a1, a1b = tmp()
        k.op("act", lambda e: e.activation(out=a1[0:64, :], in_=ps[0:64, :], func=AF.Abs, bias=SM[0:64, 3 + li:4 + li],
                                           scale=SM[0:64, 2:3]), reads=[psb, SMb], writes=[a1b])
        k.op("act", lambda e: e.activation(out=a1[0:64, :], in_=a1[0:64, :], func=AF.Sin, bias=SM[0:64, 0:1], scale=-1.0),
             reads=[a1b, SMb], writes=[a1b])
        k.op("dve", lambda e: e.tensor_tensor(out=a1[0:64, :], in0=a1[0:64, :], in1=s1[0:64, :], op=ALU.mult),
             reads=[a1b, s1b], writes=[a1b])
        k.op("dve", lambda e: e.tensor_tensor(out=s1[0:64, :], in0=s1[0:64, :], in1=s1[0:64, :], op=ALU.mult),
             reads=[s1b], writes=[s1b])
        k.op("dve", lambda e: e.tensor_scalar(out=s1[0:64, :], in0=s1[0:64, :], scalar1=-2.0, scalar2=1.0,
                                              op0=ALU.mult, op1=ALU.add), reads=[s1b], writes=[s1b])
        k.op("dve", lambda e: e.scalar_tensor_tensor(out=out[0:64, :], in0=a1[0:64, :], scalar=4.0, in1=s1[0:64, :],
                                                     op0=ALU.mult, op1=ALU.mult), reads=[a1b, s1b], writes=[outb])
        return out, outb

    for c in range(32):
        yield c
        ze, zeb = tmp()
        k.dma("sp", [(ze[0:33, :], F.zemb_d[:, c * 512:(c + 1) * 512])], writes=[zeb])
        ps, psb = P.bank("m", [6, 7])
        k.op("pe", lambda e, ps=ps, ze=ze: e.matmul(ps[0:64, :], lhsT=W1, rhs=ze[0:33, :], start=True, stop=True),
             reads=[zeb, cb], writes=[psb])
        h, hb = sin4(ps, psb, 0, F.HF[0], F.HFb[0])
        yield c
        for li, W in ((1, W2), (2, W3)):
            ps, psb = P.bank("m", [6, 7])
            k.op("pe", lambda e, ps=ps, h=h, W=W: e.matmul(ps[0:64, :], lhsT=W, rhs=h[0:64, :], start=True, stop=True),
                 reads=[hb, cb], writes=[psb])
            h, hb = sin4(ps, psb, li, F.HF[li % 2], F.HFb[li % 2])
            yield c
        wt, wtb = tmp()
        k.dma("sp", [(wt[:, :].rearrange("p (n c) -> p n c", n=4), F.win_d[:, c * 4:(c + 1) * 4, :])], writes=[wtb])
        ps4, ps4b = P.bank("m", [6, 7])
        for n2l in range(4):
            for d in range(2):
                k.op("pe", lambda e, ps4=ps4, h=h, n2l=n2l, d=d: e.matmul(
                    ps4[d * 64:(d + 1) * 64, n2l * 128:(n2l + 1) * 128],
                    lhsT=h[0:64, n2l * 128 + d * 64:n2l * 128 + (d + 1) * 64], rhs=W4[:, d, :],
                    start=True, stop=True, skip_group_check=True), reads=[hb, cb], writes=[ps4b])
        kc, kcb = tmp()
        k.op("dve", lambda e, kc=kc, ps4=ps4, wt=wt: e.tensor_tensor(out=kc[:, :], in0=ps4[:, :], in1=wt[:, :], op=ALU.mult),
             reads=[ps4b, wtb], writes=[kcb])
        st, stb = C.tmpbf()
        k.op("act", lambda e, kc=kc, st=st: e.activation(out=st[:, :], in_=kc[:, :], func=AF.Copy), reads=[kcb], writes=[stb])
        k.dma("sp", [(kc_s.ap()[:, c * 4:(c + 1) * 4, :], st[:, :].rearrange("p (n c) -> p n c", n=4))], reads=[stb],
              sembuf=stb, sink=kcsb)
        sq, sqb = tmp()
        k.op("dve", lambda e, kc=kc, sq=sq: e.tensor_tensor(out=sq[:, :], in0=kc[:, :], in1=kc[:, :], op=ALU.mult),
             reads=[kcb], writes=[sqb])
        rd, rdb = tmp()
        k.op("dve", lambda e, sq=sq, rd=rd: e.tensor_reduce(
            out=rd[:, 0:128], in_=sq[:, :].rearrange("p (n c) -> p c n", n=4), axis=mybir.AxisListType.X, op=ALU.add),
            reads=[sqb], writes=[rdb])
        k.op("dve", lambda e, rd=rd: e.tensor_tensor(out=ACC[:], in0=ACC[:], in1=rd[:, 0:128], op=ALU.add),
             reads=[rdb, ACCb], writes=[ACCb])
    ps, psb = P.bank("m", [6, 7])
    k.op("pe", lambda e, ps=ps: e.matmul(ps[:, 0:128], lhsT=C.ones_f[:, :], rhs=ACC[:], start=True, stop=True),
         reads=[ACCb, cb], writes=[psb])
    k.op("act", lambda e, ps=ps: e.activation(out=F.RSrow[:], in_=ps[:, 0:128], func=AF.Sqrt, bias=SM[:, 1:2], scale=1.0),
         reads=[psb, SMb], writes=[F.RSb])
    k.op("dve", lambda e: e.reciprocal(out=F.RSrow[:], in_=F.RSrow[:]), reads=[F.RSb], writes=[F.RSb])


def emit_fftconv(P, C, F, KC, KCb, zs2, zs2b, c_src, csink, V):
    k = P.k
    tmp = C.tmp
    DFT, TW, dftb, cb = F.DFT, F.TW, F.dftb, C.cb
    Fre, Fim, nFim = DFT[:, 0:128], DFT[:, 128:256], DFT[:, 384:512]
    Fcat, FcatI2, FcatI1 = DFT[:, 0:256], DFT[:, 128:384], DFT[:, 256:512]
    TWre2, TWim2 = TW[:, 0], TW[:, 1]
    Bre, Bim, Yre, Yim, Qre, Qim, Hre, Him = V.Bre, V.Bim, V.Yre, V.Yim, V.Qre, V.Qim, V.Hre, V.Him
    Bb, Hb, Yb, Qb = Buf("B"), Buf("H"), Buf("Y"), Buf("Q")
    PS, PSb = P.PS, P.PSb

    def cmul_from_psum(A, Ab, sign, outre, outim, outb, pr):
        A4 = A[:, :].rearrange("p (c r k) -> p c r k", c=2, r=2)
        Are, Aim = A4[:, :, 0, :], A4[:, :, 1, :]
        v = lambda t: t[:, 0:256].rearrange("p (c k) -> p c k", c=2)
        t1, t1b = tmp()
        t2, t2b = tmp()
        k.op("dve", lambda e: e.tensor_tensor(out=v(t1), in0=Are, in1=TWre2, op=ALU.mult), reads=[Ab, cb], writes=[t1b])
        k.op("dve", lambda e: e.tensor_tensor(out=v(t2), in0=Aim, in1=TWim2, op=ALU.mult), reads=[Ab, cb], writes=[t2b])
        k.op("pool", lambda e: e.tensor_tensor(out=outre[:, pr * 2:pr * 2 + 2, :], in0=v(t1), in1=v(t2),
                                               op=(ALU.subtract if sign > 0 else ALU.add)),
             reads=[t1b, t2b], writes=[outb])
        t3, t3b = tmp()
        t4, t4b = tmp()
        k.op("dve", lambda e: e.tensor_tensor(out=v(t3), in0=Aim, in1=TWre2, op=ALU.mult), reads=[Ab, cb], writes=[t3b])
        k.op("dve", lambda e: e.tensor_tensor(out=v(t4), in0=Are, in1=TWim2, op=ALU.mult), reads=[Ab, cb], writes=[t4b])
        k.op("pool", lambda e: e.tensor_tensor(out=outim[:, pr * 2:pr * 2 + 2, :], in0=v(t3), in1=v(t4),
                                               op=(ALU.add if sign > 0 else ALU.subtract)),
             reads=[t3b, t4b], writes=[outb])

    def fft_fwd(lhs_of, srcb, krows, xre, xreb, xim, ximb):
        for pr in range(2):
            A, Ab = P.bank("A", [0, 1])
            for cl in range(2):
                c4 = pr * 2 + cl
                k.op("pe", lambda e, A=A, cl=cl, c4=c4: e.matmul(
                    A[:, cl * 256:(cl + 1) * 256], lhsT=lhs_of(c4), rhs=Fcat[0:krows, :],
                    start=True, stop=True, skip_group_check=True), reads=[srcb, dftb], writes=[Ab])
            cmul_from_psum(A, Ab, +1, Bre, Bim, Bb, pr)
        bre = Bre.rearrange("p c k -> p (c k)")
        bim = Bim.rearrange("p c k -> p (c k)")
        k.op("pe", lambda e: e.matmul(xre[:, :], lhsT=Fre, rhs=bre, start=True, stop=False), reads=[Bb, dftb], writes=[xreb])
        k.op("pe", lambda e: e.matmul(xre[:, :], lhsT=nFim, rhs=bim, start=False, stop=True), reads=[Bb, dftb], writes=[xreb])
        k.op("pe", lambda e: e.matmul(xim[:, :], lhsT=Fim, rhs=bre, start=True, stop=False), reads=[Bb, dftb], writes=[ximb])
        k.op("pe", lambda e: e.matmul(xim[:, :], lhsT=Fre, rhs=bim, start=False, stop=True), reads=[Bb, dftb], writes=[ximb])

    for g in range(32):
        ch0 = g * 4
        ZC, ZCb = V.ZC[g % 2], V.ZCb[g % 2]
        src = bass.AP(tensor=zs2, offset=ch0 * S, ap=[[128, 64], [S, 4], [1, 128]])
        k.dma("sp", [(ZC[0:64], src)], reads=[zs2b], writes=[ZCb])
        fft_fwd(lambda c4, ch0=ch0: KC[:, :, ch0 + c4], KCb, 128, PS[2], PSb[2], PS[3], PSb[3])
        rsb_ap = F.RSrow[:, ch0:ch0 + 4].unsqueeze(2).broadcast_to([128, 4, 128])
        k.op("dve", lambda e, rsb_ap=rsb_ap: e.tensor_tensor(out=Hre.rearrange("p (c k) -> p c k", c=4),
                                                              in0=PS[2][:, :].rearrange("p (c k) -> p c k", c=4),
                                                              in1=rsb_ap, op=ALU.mult), reads=[PSb[2], F.RSb], writes=[Hb])
        k.op("dve", lambda e, rsb_ap=rsb_ap: e.tensor_tensor(out=Him.rearrange("p (c k) -> p c k", c=4),
                                                              in0=PS[3][:, :].rearrange("p (c k) -> p c k", c=4),
                                                              in1=rsb_ap, op=ALU.mult), reads=[PSb[3], F.RSb], writes=[Hb])
        fft_fwd(lambda c4, ZC=ZC: ZC[0:64, c4, :], ZCb, 64, PS[4], PSb[4], PS[5], PSb[5])
        t1, t1b = tmp()
        t2, t2b = tmp()
        k.op("dve", lambda e, t1=t1: e.tensor_tensor(out=t1[:, :], in0=PS[4][:, :], in1=Hre, op=ALU.mult),
             reads=[PSb[4], Hb], writes=[t1b])
        k.op("dve", lambda e, t2=t2: e.tensor_tensor(out=t2[:, :], in0=PS[5][:, :], in1=Him, op=ALU.mult),
             reads=[PSb[5], Hb], writes=[t2b])
        k.op("pool", lambda e, t1=t1, t2=t2: e.tensor_tensor(out=Yre.rearrange("p c k -> p (c k)"), in0=t1[:, :],
                                                             in1=t2[:, :], op=ALU.subtract), reads=[t1b, t2b], writes=[Yb])
        t3, t3b = tmp()
        t4, t4b = tmp()
        k.op("dve", lambda e, t3=t3: e.tensor_tensor(out=t3[:, :], in0=PS[4][:, :], in1=Him, op=ALU.mult),
             reads=[PSb[4], Hb], writes=[t3b])
        k.op("dve", lambda e, t4=t4: e.tensor_tensor(out=t4[:, :], in0=PS[5][:, :], in1=Hre, op=ALU.mult),
             reads=[PSb[5], Hb], writes=[t4b])
        k.op("pool", lambda e, t3=t3, t4=t4: e.tensor_tensor(out=Yim.rearrange("p c k -> p (c k)"), in0=t3[:, :],
                                                             in1=t4[:, :], op=ALU.add), reads=[t3b, t4b], writes=[Yb])
        for pr in range(2):
            Pk, Pkb = P.bank("A", [0, 1])
            for cl in range(2):
                c4 = pr * 2 + cl
                k.op("pe", lambda e, Pk=Pk, cl=cl, c4=c4: e.matmul(
                    Pk[:, cl * 256:(cl + 1) * 256], lhsT=Yre[:, c4, :], rhs=FcatI1, start=True, stop=False,
                    skip_group_check=True), reads=[Yb, dftb], writes=[Pkb])
                k.op("pe", lambda e, Pk=Pk, cl=cl, c4=c4: e.matmul(
                    Pk[:, cl * 256:(cl + 1) * 256], lhsT=Yim[:, c4, :], rhs=FcatI2, start=False, stop=True,
                    skip_group_check=True), reads=[Yb, dftb], writes=[Pkb])
            cmul_from_psum(Pk, Pkb, -1, Qre, Qim, Qb, pr)
        yo, yob = PS[6], PSb[6]
        k.op("pe", lambda e: e.matmul(yo[0:64, :], lhsT=DFT[:, 0:64], rhs=Qre.rearrange("p c k -> p (c k)"),
                                      start=True, stop=False), reads=[Qb, dftb], writes=[yob])
        k.op("pe", lambda e: e.matmul(yo[0:64, :], lhsT=DFT[:, 128:192], rhs=Qim.rearrange("p c k -> p (c k)"),
                                      start=False, stop=True), reads=[Qb, dftb], writes=[yob])
        ys, ysb = tmp()
        k.op("act", lambda e, ys=ys: e.activation(out=ys[0:64, :], in_=yo[0:64, :], func=AF.Copy, scale=1.0 / NFFT),
             reads=[yob], writes=[ysb])
        dst = bass.AP(tensor=c_src, offset=ch0 * S, ap=[[128, 64], [S, 4], [1, 128]])
        k.dma("sp", [(dst, ys[0:64, :].rearrange("p (c k) -> p c k", c=4))], reads=[ysb], sembuf=ysb, sink=csink)


def emit_attn1f(P, C, q_s, KT, KTb, VB, VBb, QS, QSb, MIX, MIXb, SK=3):
    k = P.k
    its = [(jq, g, hh, kb) for jq in range(4) for g in range(2) for hh in range(4) for kb in range(64)]
    st = {}
    cur = {}

    def stage_a(i):
        jq, g, hh, kb = its[i]
        q0 = jq * 512
        QT, QTb = QS[jq % 2], QSb[jq % 2]
        if (g, hh, kb) == (0, 0, 0):
            k.dma("sp", [(QT, q_s.ap()[:, :, q0:q0 + 512])], writes=[QTb])
        gs = slice(g * 64, (g + 1) * 64)
        ps, psb = P.bank("att", [0, 1, 2, 3, 6, 7])
        k.op("pe", lambda e: e.matmul(ps[:, :], lhsT=KT[gs, kb * 128:(kb + 1) * 128], rhs=QT[gs, hh, :],
                                      start=True, stop=True), reads=[KTb, QTb], writes=[psb])
        pt, ptb = C.tmpbf()
        k.op("act", lambda e: e.activation(out=pt[:, :], in_=ps[:, :], func=AF.Exp, scale=HD ** -0.5),
             reads=[psb], writes=[ptb])
        st[i] = (pt, ptb)

    def stage_b(i):
        jq, g, hh, kb = its[i]
        q0 = jq * 512
        gs = slice(g * 64, (g + 1) * 64)
        ds_ = slice((1 - g) * 64, (2 - g) * 64)
        if kb == 0:
            cur["po"] = P.bank("o", [4, 5])
        po, pob = cur["po"]
        pt, ptb = st.pop(i)
        k.op("pe", lambda e: e.matmul(po[:, :], lhsT=VB[:, kb, g * 64:g * 64 + 128], rhs=pt[:, :],
                                      start=(kb == 0), stop=(kb == 63)), reads=[VBb, ptb], writes=[pob])
        if kb == 63:
            rd, rdb = C.tmp()
            k.op("dve", lambda e: e.reciprocal(out=rd[ds_, :], in_=po[ds_, :]), reads=[pob], writes=[rdb])
            k.op("dve", lambda e: e.tensor_tensor(out=MIX[gs, hh, q0:q0 + 512], in0=po[gs, :], in1=rd[ds_, :],
                                                  op=ALU.mult), reads=[pob, rdb], writes=[MIXb[jq]])

    n = len(its)
    for t in range(n + SK):
        if t < n:
            stage_a(t)
        if t - SK >= 0:
            stage_b(t - SK)


def emit_attn1g(P, C, q_s, KT, KTb, VB, VBb, QS, QSb, MIX, MIXb, extra_p):
    k = P.k
    pb = list(zip(C.tmpbf.t, C.tmpbf.b)) + list(extra_p)
    assert len(pb) >= 8
    pi = [0]
    blocks = [(jq, g, kb) for jq in range(4) for g in range(2) for kb in range(64)]
    st = {}

    def stage_a(bi):
        jq, g, kb = blocks[bi]
        q0 = jq * 512
        QT, QTb = QS[jq % 2], QSb[jq % 2]
        if (g, kb) == (0, 0):
            k.dma("sp", [(QT, q_s.ap()[:, :, q0:q0 + 512])], writes=[QTb])
        gs = slice(g * 64, (g + 1) * 64)
        for hh in range(4):
            ps, psb = P.PS[hh], P.PSb[hh]
            k.op("pe", lambda e: e.matmul(ps[:, :], lhsT=KT[gs, kb * 128:(kb + 1) * 128], rhs=QT[gs, hh, :],
                                          start=True, stop=True), reads=[KTb, QTb], writes=[psb])
            pt, ptb = pb[pi[0] % len(pb)]
            pi[0] += 1
            k.op("act", lambda e: e.activation(out=pt[:, :], in_=ps[:, :], func=AF.Exp, scale=HD ** -0.5),
                 reads=[psb], writes=[ptb])
            st[(bi, hh)] = (pt, ptb)

    def stage_b(bi):
        jq, g, kb = blocks[bi]
        q0 = jq * 512
        gs = slice(g * 64, (g + 1) * 64)
        ds_ = slice((1 - g) * 64, (2 - g) * 64)
        for hh in range(4):
            po, pob = P.PS[4 + hh], P.PSb[4 + hh]
            pt, ptb = st.pop((bi, hh))
            k.op("pe", lambda e: e.matmul(po[:, :], lhsT=VB[:, kb, g * 64:g * 64 + 128], rhs=pt[:, :],
                                          start=(kb == 0), stop=(kb == 63)), reads=[VBb, ptb], writes=[pob])
            if kb == 63:
                rd, rdb = C.tmp()
                k.op("dve", lambda e: e.reciprocal(out=rd[ds_, :], in_=po[ds_, :]), reads=[pob], writes=[rdb])
                k.op("dve", lambda e: e.tensor_tensor(out=MIX[gs, hh, q0:q0 + 512], in0=po[gs, :], in1=rd[ds_, :],
                                                      op=ALU.mult), reads=[pob, rdb], writes=[MIXb[jq]])

    n = len(blocks)
    for t in range(n + 1):
        if t < n:
            stage_a(t)
        if t >= 1:
            stage_b(t - 1)


def build_fused(fake_ag=False, ncore=8, stop_after=None):
    nc = bass.Bass("TRN2", target_bir_lowering=False)
    k = KB(nc)
    groups = [[0, 1, 2, 3], [4, 5, 6, 7]] if ncore == 8 else [[0, 1, 2, 3]]
    cc = CC(k, groups, fake_ag)
    dt_in = lambda name, shape, dt=F32: nc.dram_tensor(name, list(shape), dt, kind="ExternalInput").ap()
    dram = lambda name, shape, dt=F32: nc.dram_tensor(name, list(shape), dt, kind="Internal")
    xT_d = dt_in("xT", [128, 8, 2048])
    xH_d = dt_in("xH", [128, 8, 256])
    cf_d = dt_in("cf", [128, 400])
    oh_d = dt_in("oh", [32, 512])
    vm_d = dt_in("vm", [128, 512])
    rope_d = dt_in("rope", [128, 2, 2048])
    idx_d = dt_in("idx", [128, 24], U32)
    mlp_d = dt_in("mlpw", [64, 456])
    zemb_d = dt_in("zemb", [33, NFFT])
    win_d = dt_in("win", [128, 128, 128])
    dft_d = dt_in("dftc", [128, 512])
    tw_d = dt_in("tw", [128, 2, 2, 128])
    wmod_d = dt_in("wmod", [2, 1024, 9216])
    w1_d = dt_in("w1", [2, 2, 1024, DFFP])
    w3_d = dt_in("w3", [2, 2, 1024, DFFP])
    w2_d = dt_in("w2", [2, 2, DFFP, 1024])
    win_w = dt_in("win_w", [2, 1024, INC])
    wout_d = dt_in("wout", [2, 1024, 1024])
    xo_d = nc.dram_tensor("xo", [128, 8, 2048], F32, kind="ExternalOutput").ap()
    scr = dram("scr", [128, 8 * 512])
    q_s = dram("q_s", [128, 4, 2048], BF16)
    k_src, k_g = dram("k_src", [128, 2048], BF16), dram("k_g", [512, 2048], BF16)
    v_src, v_g = dram("v_src", [2048, 128], BF16), dram("v_g", [8192, 128], BF16)
    hb_src, hb_g = dram("hb_src", [128, 24]), dram("hb_g", [512, 24])
    z_src, z_g = dram("z_src", [4, 128, 2048], BF16), dram("z_g", [4, 512, 2048], BF16)
    zf_s = dram("zf_s", [128, 4, 2048])
    x0_s = dram("x0_s", [128, 4, 2048])
    zs2 = dram("zs2", [128, S], BF16)
    kc_s = dram("kc_s", [128, 128, 128], BF16)
    c_src, c_g = dram("c_src", [8, 16, S]), dram("c_g", [8, 64, S])

    P = Prog(nc, k)
    C = Common()
    C.tmp = RotTmp(k, "tf", 6, F32)
    C.tmpbf = RotTmp(k, "tb", 5, BF16)
    C.RS = k.sb("RS", [128, 512], F32)
    C.RSb = Buf("RS")
    C.cb = Buf("consts")
    C.modb = Buf("mod")
    CF = k.sb("CF", [128, 400], F32)
    IDX = k.sb("IDX", [128, 24], U32)
    C.ones_bf = k.sb("ones_bf", [128, 128], BF16)
    C.bd_bf = k.sb("bd_bf", [128, 128], BF16)
    C.ones_f = k.sb("ones_f", [128, 128], F32)
    C.eps = k.sb("eps", [128, 1], F32)
    condbf = k.sb("condbf", [128, 8], BF16)
    MODV = k.sb("modv", [128, 2, 72], F32)
    AV = k.sb("av", [128, 2, 3, 8], F32)
    EXS = k.sb("exs", [128, 8], F32)
    F = Common()
    F.MLP = k.sb("MLP", [64, 456], F32)
    F.DFT = k.sb("DFT", [128, 512], BF16)
    F.TW = k.sb("TW", [128, 2, 2, 128], F32)
    F.ACC = k.sb("ACC", [128, 128], F32)
    F.SM = k.sb("SM", [128, 16], F32)
    F.RSrow = k.sb("RSrow", [128, 128], F32)
    F.SMb, F.ACCb, F.RSb, F.dftb = Buf("SM"), Buf("ACC"), Buf("RSr"), Buf("dft")
    F.zemb_d, F.win_d = zemb_d, win_d
    F.HF = [k.sb(f"HF{i}", [64, 512], F32) for i in range(2)]
    F.HFb = [Buf("HF0"), Buf("HF1")]
    HBS = k.sb("HBS", [128, 12, 2], F32)
    HG = k.sb("HG", [128, 4, 12, 2], F32)
    HLR = k.sb("HLR", [128, 2, 12], F32)
    XT = k.sb("XT", [128, 8, 2048], F32)
    AR = k.sb("AR", [128, 41984], BF16)

    cT = CF[:, 0:8]
    flags = CF[:, 8:10]
    normg = CF[:, 10:58].rearrange("p (l i c) -> p l i c", l=2, i=3)
    bmod = CF[:, 58:202].rearrange("p (l m) -> p l m", l=2)
    convw = CF[:, 202:214].rearrange("p (c t) -> p c t", c=4)
    convb = CF[:, 214:218]
    gq, gk = CF[:, 218:220], CF[:, 220:222]
    sink = CF[:, 222:230]
    relrep = CF[:, 232:240]
    gq1, gk1 = CF[:, 240:241], CF[:, 241:242]
    dcw = CF[:, 244:280].rearrange("p (c t) -> p c t", c=12)
    dcb = CF[:, 280:292]
    skipv = CF[:, 292:296]
    selL, selR = CF[:, 296:300], CF[:, 300:304]

    Ht = AR[:, 0:18432].rearrange("p (c t) -> p c t", c=8)
    UQ = AR[:, 18432:33792]
    MXf = AR[:, 33792:41984].bitcast(F32)
    XH = MXf[:, 0:2048].rearrange("p (c t) -> p c t", c=8)
    MIXHI0 = AR[:, 33792:41984].rearrange("p (c t) -> p c t", c=4)
    U = UQ[:, 0:6 * 2304].rearrange("p (f t) -> p f t", f=6)
    QT0 = UQ[:, 0:8192].rearrange("p (h t) -> p h t", h=4)
    KT0 = UQ[:, 8192:8192 + 2304]
    VA0 = UQ[:, 10496:10496 + 4608].rearrange("p (b g d) -> p b g d", b=18, g=2)
    S_t = UQ[:, 10496:10496 + 4100].bitcast(F32)
    C.E8 = UQ[:, 0:8192].bitcast(F32).rearrange("p (h m) -> p h m", h=8)
    EXPB = P.WBf[:, :, :].rearrange("p a (b q) -> p (a b) q", q=128).rearrange("p (h b) q -> p h b q", h=8)
    OH = AR[0:32, 0:1024].bitcast(F32)
    VM = AR[:, 1024:2048].bitcast(F32)

    tiles5 = [(0, 512), (512, 512), (1024, 512), (1536, 512), (2048, 256)]
    tiles4 = tiles5[:4]
    XB = [Buf(f"x{j}") for j in range(5)]

    class XAct:
        def __init__(self, tiles):
            self.tiles = tiles
            self.b = XB[:len(tiles)]

        def ap(self, c, j):
            if j < 4:
                return XT[:, c, j * 512:(j + 1) * 512]
            return XH[:, c, :]
    X5, X4 = XAct(tiles5), XAct(tiles4)
    H5 = Act(Ht, tiles5, "h")
    H4 = Act(Ht, tiles4, "h")
    H4.b = H5.b[:4]
    Ub = [[Buf(f"u{f}_{j}") for j in range(5)] for f in range(6)]

    k.dma("sp", [(CF[:], cf_d), (F.MLP[:], mlp_d), (F.TW[:], tw_d), (IDX[:], idx_d)], writes=[C.cb])
    k.dma("pool", [(F.DFT[:], dft_d)], writes=[F.dftb])
    for j in range(4):
        k.dma("sp", [(XT[:, :, j * 512:(j + 1) * 512], xT_d[:, :, j * 512:(j + 1) * 512])], writes=[XB[j]])
    k.dma("sp", [(XH, xH_d)], writes=[XB[4]])
    cst = Buf("cst")
    k.op("dve", lambda e: e.memset(C.ones_bf[:], 1.0), writes=[cst])
    k.op("dve", lambda e: e.memset(C.ones_f[:], 1.0), writes=[cst])
    k.op("dve", lambda e: e.memset(C.eps[:], EPS), writes=[cst])
    k.op("dve", lambda e: e.memset(C.bd_bf[:], 0.0), writes=[cst])
    k.op("dve", lambda e: e.memset(C.bd_bf[0:64, 0:64], 1.0), writes=[cst])
    k.op("dve", lambda e: e.memset(C.bd_bf[64:128, 64:128], 1.0), writes=[cst])
    condb = Buf("cond")
    k.op("act", lambda e: e.activation(out=condbf[:], in_=cT, func=AF.Silu), reads=[C.cb], writes=[condb])
    k.op("act", lambda e: e.activation(out=EXS[:], in_=sink, func=AF.Exp), reads=[C.cb], writes=[cst])
    k.op("dve", lambda e: e.tensor_copy(out=C.eps[:], in_=C.eps[:]), reads=[cst, C.cb], writes=[C.cb])

    def layer_mod(l):
        emit_mod(P, condbf, condb, wmod_d[l], bmod[:, l, :], MODV[:, l, :], C.modb, AV[:, l], normg[:, l])

    def mcol(l, i, kind):
        return MODV[:, l, (i * 3 + kind) * 8:(i * 3 + kind + 1) * 8]

    def mix_ap0(fc, c0, w):
        if fc < 4:
            return Ht[:, fc, c0:c0 + w]
        return MIXHI0[:, fc - 4, c0:c0 + w]

    def finish(extra=()):
        ob = Sink("out")
        k.wait_all("sp", list(extra))
        for j in range(4):
            k.dma("sp", [(xo_d[:, :, j * 512:(j + 1) * 512], XT[:, :, j * 512:(j + 1) * 512])], reads=[XB[j]],
                  sembuf=XB[j], sink=ob)
        k.wait_all("sp", [ob])
        k.close()
        return nc

    kcsb = Sink("kcs")
    fgen = emit_filter(P, C, F, kc_s, kcsb)
    next(fgen)
    hcnt = [0]

    def fhook():
        hcnt[0] += 1
        next(fgen, None)

    layer_mod(0)
    emit_adaln(P, X5, H5, AV[:, 0, 0], mcol(0, 0, 0), C)
    emit_ffn(P, X5, H5, U, Ub, w1_d[0, 0], w3_d[0, 0], w2_d[0, 0], mcol(0, 0, 2), C, hook=fhook)
    for _ in fgen:
        pass
    k.barrier()
    ohb = Buf("oh")
    k.dma("sp", [(OH, oh_d), (VM, vm_d)], writes=[ohb])
    k.op("dve", lambda e: e.tensor_copy(out=C.eps[:], in_=C.eps[:]), reads=[ohb, C.cb], writes=[C.cb])
    scrb = emit_expb(P, C, relrep, C.cb, OH, VM, scr, EXPB, P.WBb[0])
    k.barrier(dbufs=[scrb])
    emit_adaln(P, X5, H5, AV[:, 0, 1], mcol(0, 1, 0), C)
    k.barrier()
    MIXb = [Buf(f"mix{j}") for j in range(4)]
    Sb = Buf("S")
    emit_conv0(P, C, H5, mix_ap0, MIXb, win_w[0], convw, convb, flags, S_t, Sb)
    k.barrier()
    QTb = [Buf(f"qt{j}") for j in range(4)]
    KTb = [Buf(f"kt{j}") for j in range(5)]
    VAb = Buf("va")
    k.op("dve", lambda e: e.memset(VA0[:, 0:16, 0, 64:128], 1.0), writes=[VAb])
    k.op("dve", lambda e: e.memset(VA0[:, 0:16, 1, 0:64], 1.0), writes=[VAb])
    emit_qkv0(P, C, H5, win_w[0], QT0, QTb, KT0, KTb, VA0, VAb, gq, gk, flags)
    k.barrier()
    emit_attn0(P, C, QT0, QTb, KT0, KTb, VA0, VAb, EXPB, P.WBb[0], EXS, Ht, MIXb)
    for i in (1, 2):
        P.WBb[i].r = dict(P.WBb[0].r)
        P.WBb[i].w = P.WBb[0].w
    emit_outproj(P, C, X4, mix_ap0, MIXb, wout_d[0], mcol(0, 1, 2))
    k.barrier()
    emit_adaln(P, X4, H4, AV[:, 0, 2], mcol(0, 2, 0), C)
    emit_ffn(P, X4, H4, U, Ub, w1_d[0, 1], w3_d[0, 1], w2_d[0, 1], mcol(0, 2, 2), C)

    if stop_after == "l0":
        return finish([kcsb])

    layer_mod(1)
    emit_adaln(P, X4, H4, AV[:, 1, 0], mcol(1, 0, 0), C)
    emit_ffn(P, X4, H4, U, Ub, w1_d[1, 0], w3_d[1, 0], w2_d[1, 0], mcol(1, 0, 2), C)
    k.barrier()
    ROPE = UQ[:, 0:8192].bitcast(F32).rearrange("p (a t) -> p a t", a=2)
    HB = AR[:, 26624:26624 + 12300].bitcast(F32).rearrange("p (a t) -> p a t", a=3)
    ropeb, HBb = Buf("rope"), Buf("HB")
    k.dma("sp", [(ROPE, rope_d)], writes=[ropeb])
    emit_adaln(P, X4, H4, AV[:, 1, 1], mcol(1, 1, 0), C)
    W1L = win_w[1]
    hbsb = Buf("hbs")
    for pr in range(6):
        wa, wab = P.load_wa(W1L[:, 768 + pr * 256:768 + (pr + 1) * 256])
        for cl in range(2):
            ch = pr * 2 + cl
            ps, psb = P.bank("h1", [0, 1, 2, 3])
            for ci, col in enumerate((0, 2047)):
                for c in range(8):
                    k.op("pe", lambda e, c=c, ps=ps, wa=wa, cl=cl, ci=ci, col=col: e.matmul(
                        ps[:, ci:ci + 1], lhsT=wa[:, c, cl * 128:(cl + 1) * 128], rhs=Ht[:, c, col:col + 1],
                        start=(c == 0), stop=(c == 7)), reads=[wab] + H4.b, writes=[psb])
            k.op("act", lambda e, ps=ps, ch=ch: e.activation(out=HBS[:, ch, :], in_=ps[:, 0:2], func=AF.Copy),
                 reads=[psb], writes=[hbsb])
    hbsrcb, hbgb, hgb = Buf("hbsrc"), Buf("hbg"), Buf("hg")
    k.dma("sp", [(hb_src.ap(), HBS[:].rearrange("p c t -> p (c t)"))], reads=[hbsb], writes=[hbsrcb])
    cc.allgather(hb_src, hb_g, [hbsrcb], hbgb)
    k.dma("sp", [(HG[:].rearrange("p r c t -> p r (c t)"), hb_g.ap().rearrange("(r p) t -> p r t", p=128))],
          reads=[hbgb], writes=[hgb])
    outs = Sink("l1outs")
    for pr in range(3):
        wa, wab = P.load_wa(W1L[:, pr * 256:(pr + 1) * 256])
        for cl in range(2):
            if pr == 2 and cl == 1:
                break
            for j in range(4):
                c0, w = H4.tiles[j]
                ps, psb = P.bank("h1", [0, 1, 2, 3])
                emit_inproj_tile(P, H4, j, wa, wab, cl, ps, psb)
                qn, qnb = C.tmp()
                emit_qknorm(P, C, ps, psb, w, (gq1 if pr < 2 else gk1), qn[:, 0:w], qnb)
                st, stb = C.tmpbf()
                emit_rope(P, C, qn, qnb, ROPE[:, 0, :], ROPE[:, 1, :], ropeb, c0, w, st[:, 0:w], stb)
                dst = q_s.ap()[:, pr * 2 + cl, c0:c0 + w] if pr < 2 else k_src.ap()[:, c0:c0 + w]
                k.dma("sp", [(dst, st[:, 0:w])], reads=[stb], sembuf=stb, sink=outs)
        if pr == 2:
            for tb in range(16):
                j = tb // 4
                ps, psb = P.bank("o", [4, 5])
                for c in range(8):
                    k.op("pe", lambda e, c=c, tb=tb, ps=ps, wa=wa: e.matmul(
                        ps[:, 0:128], lhsT=Ht[:, c, tb * 128:(tb + 1) * 128], rhs=wa[:, c, 128:256],
                        start=(c == 0), stop=(c == 7)), reads=[wab, H4.b[j]], writes=[psb])
                st, stb = C.tmpbf()
                k.op("act", lambda e, ps=ps, st=st: e.activation(out=st[:, 0:128], in_=ps[:, 0:128], func=AF.Copy),
                     reads=[psb], writes=[stb])
                k.dma("sp", [(v_src.ap()[tb * 128:(tb + 1) * 128, :], st[:, 0:128])], reads=[stb], sembuf=stb, sink=outs)
    kgb, vgb = Buf("kg"), Buf("vg")
    if stop_after == "qkv":
        return finish([outs, kcsb])
    cc.allgather(k_src, k_g, [], kgb, sinks=[outs])
    cc.allgather(v_src, v_g, [], vgb, sinks=[outs])
    for side, sel, colx in ((0, selL, 1), (1, selR, 0)):
        k.op("dve", lambda e, side=side, sel=sel, colx=colx: e.tensor_scalar(
            out=HLR[:, side, :], in0=HG[:, 0, :, colx], scalar1=sel[:, 0:1], scalar2=None, op0=ALU.mult),
            reads=[hgb, C.cb], writes=[hgb])
        for r in range(1, 4):
            k.op("dve", lambda e, side=side, sel=sel, colx=colx, r=r: e.scalar_tensor_tensor(
                out=HLR[:, side, :], in0=HG[:, r, :, colx], scalar=sel[:, r:r + 1], in1=HLR[:, side, :],
                op0=ALU.mult, op1=ALU.add), reads=[hgb, C.cb], writes=[hgb])

    zouts = Sink("zouts")
    for hh in range(2):
        was = [P.load_wa(W1L[:, 768 + part * 512 + hh * 256: 768 + part * 512 + (hh + 1) * 256]) for part in range(3)]
        for cl in range(2):
            cch = hh * 2 + cl
            for part in range(3):
                wa, wab = was[part]
                for j in range(4):
                    c0, w = H4.tiles[j]
                    ps, psb = P.bank("h1", [0, 1, 2, 3])
                    emit_inproj_tile(P, H4, j, wa, wab, cl, ps, psb)
                    k.op("act", lambda e, ps=ps, part=part, c0=c0, w=w: e.activation(
                        out=HB[:, part, 1 + c0:1 + c0 + w], in_=ps[:, 0:w], func=AF.Copy), reads=[psb], writes=[HBb])
                ch12 = part * 4 + cch
                k.op("act", lambda e, part=part, ch12=ch12: e.activation(out=HB[:, part, 0:1], in_=HLR[:, 0, ch12:ch12 + 1],
                                                                         func=AF.Copy), reads=[hgb], writes=[HBb])
                k.op("act", lambda e, part=part, ch12=ch12: e.activation(out=HB[:, part, 2049:2050],
                                                                         in_=HLR[:, 1, ch12:ch12 + 1], func=AF.Copy),
                     reads=[hgb], writes=[HBb])
            for j in range(4):
                c0, w = H4.tiles[j]
                cv = []
                for part in range(3):
                    ch12 = part * 4 + cch
                    t, tb_ = C.tmp()
                    k.op("dve", lambda e, t=t, part=part, c0=c0, ch12=ch12: e.tensor_scalar(
                        out=t[:, :], in0=HB[:, part, 1 + c0:1 + c0 + 512], scalar1=dcw[:, ch12, 1:2],
                        scalar2=dcb[:, ch12:ch12 + 1], op0=ALU.mult, op1=ALU.add), reads=[HBb, C.cb], writes=[tb_])
                    k.op("dve", lambda e, t=t, part=part, c0=c0, ch12=ch12: e.scalar_tensor_tensor(
                        out=t[:, :], in0=HB[:, part, c0:c0 + 512], scalar=dcw[:, ch12, 0:1], in1=t[:, :],
                        op0=ALU.mult, op1=ALU.add), reads=[HBb, C.cb, tb_], writes=[tb_])
                    k.op("dve", lambda e, t=t, part=part, c0=c0, ch12=ch12: e.scalar_tensor_tensor(
                        out=t[:, :], in0=HB[:, part, 2 + c0:2 + c0 + 512], scalar=dcw[:, ch12, 2:3], in1=t[:, :],
                        op0=ALU.mult, op1=ALU.add), reads=[HBb, C.cb, tb_], writes=[tb_])
                    cv.append((t, tb_))
                (x0t, x0b), (x1t, x1b), (vt, vb_) = cv
                k.dma("sp", [(x0_s.ap()[:, cch, c0:c0 + 512], x0t[:, :])], reads=[x0b], sembuf=x0b, sink=zouts)
                k.op("dve", lambda e, x1t=x1t, vt=vt: e.tensor_tensor(out=x1t[:, :], in0=x1t[:, :], in1=vt[:, :], op=ALU.mult),
                     reads=[x1b, vb_], writes=[x1b])
                k.dma("sp", [(zf_s.ap()[:, cch, c0:c0 + 512], x1t[:, :])], reads=[x1b], sembuf=x1b, sink=zouts)
                zb16, zb16b = C.tmpbf()
                k.op("act", lambda e, x1t=x1t, zb16=zb16: e.activation(out=zb16[:, :], in_=x1t[:, :], func=AF.Copy),
                     reads=[x1b], writes=[zb16b])
                k.dma("sp", [(z_src.ap()[cch, :, c0:c0 + 512], zb16[:, :])], reads=[zb16b],
                      sembuf=zb16b, sink=zouts)
    zgb = Buf("zg")
    for cch in range(4):
        cc.allgather(z_src, z_g, [], zgb, sinks=[zouts], sl=cch)

    if stop_after == "l1pre":
        return finish([kgb, vgb, zgb, kcsb])

    k.barrier()
    VB = AR[:, 0:12288].rearrange("p (b d) -> p b d", b=64)
    KT = AR[:, 12288:20480]
    QS = [AR[:, 20480 + i * 2048:20480 + (i + 1) * 2048].rearrange("p (h t) -> p h t", h=4) for i in range(2)]
    QSb = [Buf("qs0"), Buf("qs1")]
    MIX = AR[:, 24576:40960].rearrange("p (c t) -> p c t", c=8)
    MIXb = [Buf(f"mixb{j}") for j in range(4)]
    KTb, VBb = Buf("KT"), Buf("VB")
    k.op("dve", lambda e: e.memset(VB[:, :, 64:128], 1.0), writes=[VBb])
    k.dma("sp", [(KT.rearrange("p (r t) -> p r t", r=4), k_g.ap().rearrange("(r p) t -> p r t", p=128))],
          reads=[kgb], writes=[KTb])
    vsrc = v_g.ap().rearrange("(b p) d -> p b d", p=128)
    k.dma("sp", [(VB[:, :, 0:64], vsrc[:, :, 0:64]), (VB[:, :, 128:192], vsrc[:, :, 64:128])], reads=[vgb], writes=[VBb])
    extra_p = [(AR[:, 40960 + i * 512:40960 + (i + 1) * 512], Buf(f"xp{i}")) for i in range(2)]
    spare_t, spare_b = C.tmp.t.pop(), C.tmp.b.pop()
    C.tmp.i = 0
    extra_p.append((spare_t[:, :].bitcast(BF16)[:, 0:512], spare_b))
    emit_attn1g(P, C, q_s, KT, KTb, VB, VBb, QS, QSb, MIX, MIXb, extra_p)

    if stop_after == "attn":
        return finish([zgb, kcsb] + MIXb)

    k.barrier()
    V = Common()
    KC = AR[:, 0:16384].rearrange("p (n c) -> p n c", n=128)
    Zg = AR[:, 0:8192]
    V.ZC = [AR[:, 16384 + i * 512:16384 + (i + 1) * 512].rearrange("p (c k) -> p c k", c=4) for i in range(2)]
    V.ZCb = [Buf("zc0"), Buf("zc1")]
    six = [AR[:, 17408 + i * 512:17408 + (i + 1) * 512].rearrange("p (c k) -> p c k", c=4) for i in range(6)]
    V.Bre, V.Bim, V.Yre, V.Yim, V.Qre, V.Qim = six
    V.Hre = AR[:, 20480:21504].bitcast(F32)
    V.Him = AR[:, 21504:22528].bitcast(F32)
    Zgb, zs2b, KCb = Buf("Zg"), Buf("zs2"), Buf("KC")
    zg_rows = z_g.ap().rearrange("c r t -> (c r) t")
    k.wait_all("pool", [cc.sink(zgb)])
    for r in range(4):
        kb_gather(k, Zg[:, r * 2048:(r + 1) * 2048], zg_rows, IDX[:, r:r + 1], reads=[zgb, C.cb], writes=[Zgb])
    k.dma("sp", [(zs2.ap(), Zg)], reads=[Zgb], writes=[zs2b])
    k.wait_all("sp", [kcsb])
    k.dma("sp", [(KC, kc_s.ap())], reads=[zs2b], writes=[KCb])
    csink = Sink("csrc")
    emit_fftconv(P, C, F, KC, KCb, zs2, zs2b, c_src, csink, V)
    cgb = Buf("cg")
    for i in range(8):
        cc.allgather(c_src, c_g, [], cgb, sinks=[csink], sl=i)

    if stop_after == "fft":
        return finish([cgb] + MIXb)

    cg_rows = c_g.ap().rearrange("i r (a t) -> (i r a) t", t=512)
    k.wait_all("pool", [cc.sink(cgb)])
    for cq in range(4):
        for tq in range(4):
            c0 = tq * 512
            ct, ctb = C.tmp()
            kb_gather(k, ct[:, :], cg_rows, IDX[:, 4 + cq * 4 + tq:5 + cq * 4 + tq], reads=[cgb, C.cb], writes=[ctb])
            zt, ztb = C.tmp()
            k.dma("sp", [(zt[:, :], zf_s.ap()[:, cq, c0:c0 + 512])], writes=[ztb])
            xt_, xtb = C.tmp()
            k.dma("sp", [(xt_[:, :], x0_s.ap()[:, cq, c0:c0 + 512])], writes=[xtb])
            k.op("dve", lambda e, zt=zt, ct=ct, cq=cq: e.scalar_tensor_tensor(
                out=zt[:, :], in0=zt[:, :], scalar=skipv[:, cq:cq + 1], in1=ct[:, :], op0=ALU.mult, op1=ALU.add),
                reads=[ztb, ctb, C.cb], writes=[ztb])
            k.op("dve", lambda e, zt=zt, xt_=xt_, cq=cq, c0=c0: e.tensor_tensor(
                out=MIX[:, 4 + cq, c0:c0 + 512], in0=zt[:, :], in1=xt_[:, :], op=ALU.mult),
                reads=[ztb, xtb], writes=[MIXb[tq]])

    emit_outproj(P, C, X4, lambda fc, c0, w: MIX[:, fc, c0:c0 + w], MIXb, wout_d[1], mcol(1, 1, 2))
    k.barrier()
    Ht2 = AR[:, 0:16384].rearrange("p (c t) -> p c t", c=8)
    U2 = AR[:, 16384:16384 + 12288].rearrange("p (f t) -> p f t", f=6)
    H42 = Act(Ht2, tiles4, "h2")
    Ub2 = [[Buf(f"v{f}_{j}") for j in range(4)] for f in range(6)]
    emit_adaln(P, X4, H42, AV[:, 1, 2], mcol(1, 2, 0), C)
    emit_ffn(P, X4, H42, U2, Ub2, w1_d[1, 1], w3_d[1, 1], w2_d[1, 1], mcol(1, 2, 2), C)
    ob = Sink("out")
    for j in range(4):
        k.dma("sp", [(xo_d[:, :, j * 512:(j + 1) * 512], XT[:, :, j * 512:(j + 1) * 512])], reads=[XB[j]],
              sembuf=XB[j], sink=ob)
    k.wait_all("sp", [ob])
    k.close()
    return nc


def prep_fused(inp, ncore=NCORE):
    base = prep_LA(inp)
    maps = []
    cache = {}
    dcw = np.asarray(inp["d_conv_w"], np.float32)[0]
    dcb = np.asarray(inp["d_conv_b"], np.float32)[0]
    skip = np.asarray(inp["d_skip"], np.float32)[0]
    w4 = np.asarray(inp["d_f_w4"], np.float32)[0]
    p = np.arange(128)
    for core in range(ncore):
        b, qd = core // 4, core % 4
        cq = qd
        if cq not in cache:
            cache[cq] = host_consts_LB(cq)
        zemb, win, dftc, tw = cache[cq]
        m = dict(base[core])
        cf = np.zeros((128, 400), np.float32)
        cf[:, 0:244] = m.pop("cf")[:, 0:244]
        cf[:, 244:280] = fm(dcw).transpose(0, 2, 1).reshape(128, 36)
        cf[:, 280:292] = fm(dcb)
        cf[:, 292:296] = fm(skip)
        if qd > 0:
            cf[:, 296 + qd - 1] = 1.0
        if qd < 3:
            cf[:, 300 + qd + 1] = 1.0
        idx = np.zeros((128, 24), np.uint32)
        for r in range(4):
            idx[:, r] = cq * 512 + r * 128 + p
        for c2 in range(4):
            for tq in range(4):
                idx[:, 4 + c2 * 4 + tq] = ((p // 16) * 64 + c2 * 16 + (p % 16)) * 16 + qd * 4 + tq
        mlp = np.zeros((64, 456), np.float32)
        mlp[0:33, 0:64] = np.asarray(inp["d_f_w1"], np.float32)[0]
        mlp[:, 64:128] = np.asarray(inp["d_f_w2"], np.float32)[0]
        mlp[:, 128:192] = np.asarray(inp["d_f_w3"], np.float32)[0]
        mlp[:, 192] = np.asarray(inp["d_f_b1"], np.float32)[0]
        mlp[:, 193] = np.asarray(inp["d_f_b2"], np.float32)[0]
        mlp[:, 194] = np.asarray(inp["d_f_b3"], np.float32)[0]
        mlp[:, 195] = np.asarray(inp["d_f_freq"], np.float32)[0]
        mlp[:, 200:328] = w4[:, cq * 128:(cq + 1) * 128]
        mlp[:, 328:456] = w4[:, 512 + cq * 128:512 + (cq + 1) * 128]
        m["win_w"] = m.pop("win")
        m.update(cf=cf, idx=idx, mlpw=mlp, zemb=zemb, win=win, dftc=dftc, tw=tw)
        maps.append(m)
    return maps
```

```python
import math
import numpy as np
import concourse.bass as bass
import concourse.mybir as mybir
from concourse.bass_utils import run_bass_kernel_spmd

AF = mybir.ActivationFunctionType
ALU = mybir.AluOpType
F32 = mybir.dt.float32
BF16 = mybir.dt.bfloat16
EPOCH = 12000

D = 1024
S = 8192
TOK = 2048
NCORE = 8
DFF = 2752
DFFP = 2816
NF = 22
HD = 64
EPS = 1e-6
INC = 2304


class Buf:
    __slots__ = ("name", "w", "r", "dsem", "dcnt")

    def __init__(self, name=""):
        self.name = name
        self.w = None
        self.r = {}
        self.dsem = None
        self.dcnt = 0


class Sink:
    def __init__(self, name=""):
        self.name = name
        self.toks = {}


class KB:
    def __init__(self, nc):
        self.nc = nc
        self.engs = {"pe": nc.tensor, "act": nc.scalar, "dve": nc.vector,
                     "pool": nc.gpsimd, "sp": nc.sync}
        self.cnt = {e: 0 for e in self.engs}
        self.sems = {}
        self.waited = {e: {} for e in self.engs}
        self.nsem = 0
        self._stack = []

    def _newsem(self, name):
        cm = self.nc.semaphore(name)
        h = cm.__enter__()
        self._stack.append(cm)
        self.nsem += 1
        return h

    def sb(self, name, shape, dt):
        cm = self.nc.sbuf_tensor(name, shape, dt)
        t = cm.__enter__()
        self._stack.append(cm)
        return t

    def ps(self, name, shape, dt=F32):
        cm = self.nc.psum_tensor(name, shape, dt)
        t = cm.__enter__()
        self._stack.append(cm)
        return t

    def close(self):
        while self._stack:
            self._stack.pop().__exit__(None, None, None)

    def _engsem(self, eng):
        key = (eng, self.cnt[eng] // EPOCH)
        if key not in self.sems:
            self.sems[key] = self._newsem(f"s_{eng}_{key[1]}")
        return key

    def _deps(self, reads, writes):
        deps = {}

        def add(k, v):
            if deps.get(k, 0) < v:
                deps[k] = v
        for b in reads:
            if b.w is not None:
                add(*b.w)
        for b in writes:
            if b.w is not None:
                add(*b.w)
            for kk, v in b.r.items():
                add(kk, v)
        return deps

    def _wait(self, eng, deps):
        w = self.waited[eng]
        e = self.engs[eng]
        for kk, v in deps.items():
            if w.get(kk, 0) >= v:
                continue
            e.wait_ge(self.sems[kk], v)
            w[kk] = v

    def _commit(self, tok, reads, writes):
        kk, v = tok
        for b in reads:
            if b.r.get(kk, 0) < v:
                b.r[kk] = v
        for b in writes:
            b.w = tok
            b.r = {}

    def op(self, eng, fn, reads=(), writes=()):
        deps = self._deps(reads, writes)
        if eng == "pe":
            deps = {kk: v for kk, v in deps.items() if kk[0] != "pe"}
        self._wait(eng, deps)
        key = self._engsem(eng)
        ins = fn(self.engs[eng])
        self.cnt[eng] += 1
        val = self.cnt[eng] - key[1] * EPOCH
        ins.then_inc(self.sems[key], 1)
        tok = (key, val)
        self._commit(tok, reads, writes)
        return tok

    def dma(self, q, pairs, reads=(), writes=(), sembuf=None, sink=None, **kw):
        sbf = sembuf or (writes[0] if writes else reads[0])
        if sbf.dsem is None:
            sbf.dsem = {}
            sbf.dcnt = {}
        sw = (q == "pool")
        if sw not in sbf.dsem:
            sbf.dsem[sw] = ("d", id(sbf), sw)
            sbf.dcnt[sw] = 0
            self.sems[sbf.dsem[sw]] = self._newsem(f"d_{self.nsem}")
        deps = self._deps(reads, writes)
        self._wait(q, deps)
        e = self.engs[q]
        for (o, i) in pairs:
            e.dma_start(out=o, in_=i, **kw).then_inc(self.sems[sbf.dsem[sw]], 16)
            sbf.dcnt[sw] += 16
        tok = (sbf.dsem[sw], sbf.dcnt[sw])
        self._commit(tok, reads, writes)
        if sink is not None and sink.toks.get(tok[0], 0) < tok[1]:
            sink.toks[tok[0]] = tok[1]
        return tok

    def wait_all(self, eng, bufs):
        deps = {}
        for b in bufs:
            d = dict(b.toks) if isinstance(b, Sink) else self._deps([b], [b])
            for kk, v in d.items():
                if deps.get(kk, 0) < v:
                    deps[kk] = v
        self._wait(eng, deps)

    def barrier(self, dbufs=()):
        deps = {}
        for e in ("pe", "act", "dve"):
            if self.cnt[e] == 0:
                continue
            ep = (self.cnt[e] - 1) // EPOCH
            deps[(e, ep)] = self.cnt[e] - ep * EPOCH
        for b in dbufs:
            for kk, v in self._deps([b], [b]).items():
                if deps.get(kk, 0) < v:
                    deps[kk] = v
        for e in ("pe", "act", "dve", "pool", "sp"):
            self._wait(e, dict(deps))


class Prog:
    def __init__(self, nc, k):
        self.nc = nc
        self.k = k
        self.WA = [k.sb(f"WA{i}", [128, 8, 256], BF16) for i in range(4)]
        self.WAb = [Buf(f"WA{i}") for i in range(4)]
        self.WBf = k.sb("WBf", [128, 3, 1024], F32)
        self.WB = [self.WBf[:, i, :].bitcast(BF16).rearrange("p (f n) -> p f n", f=2) for i in range(3)]
        self.WBb = [Buf(f"WB{i}") for i in range(3)]
        self.wa_i = 0
        self.PS = [k.ps(f"ps{i}", [128, 512]) for i in range(8)]
        self.PSb = [Buf(f"ps{i}") for i in range(8)]
        self.rr = {}

    def next_wa(self, parity=None):
        i = self.wa_i
        self.wa_i = (self.wa_i + 1) % 4
        return i

    def bank(self, group, banks):
        i = self.rr.get(group, 0)
        self.rr[group] = (i + 1) % len(banks)
        b = banks[i]
        return self.PS[b], self.PSb[b]

    def load_wa(self, src_ap):
        i = self.next_wa()
        self.k.dma("pool", [(self.WA[i][:], src_ap.rearrange("(c p) n -> p c n", p=128))], writes=[self.WAb[i]])
        return self.WA[i], self.WAb[i]

    def load_wb(self, i, src_ap):
        self.k.dma("pool", [(self.WB[i], src_ap.rearrange("(f p) n -> p f n", p=128))], writes=[self.WBb[i]])
        return self.WB[i], self.WBb[i]


def emit_mod(P, cond_bf, cond_b, wmod_l, bmod_l, modv, modb, a_out, normg_l):
    k = P.k
    ps, psb = P.PS[7], P.PSb[7]
    for ch in range(36):
        wa, wab = P.load_wa(wmod_l[:, ch * 256:(ch + 1) * 256])
        for cl in range(2):
            cc = ch * 2 + cl
            for kc in range(8):
                k.op("pe", lambda e, cc=cc, kc=kc, cl=cl, wa=wa: e.matmul(
                    ps[:, cc:cc + 1], lhsT=wa[:, kc, cl * 128:(cl + 1) * 128], rhs=cond_bf[:, kc:kc + 1],
                    start=(kc == 0), stop=(kc == 7)),
                    reads=[wab, cond_b], writes=[psb])
    k.op("dve", lambda e: e.tensor_tensor(out=modv[:], in0=ps[:, 0:72], in1=bmod_l, op=ALU.add),
         reads=[psb], writes=[modb])
    for i in range(3):
        k.op("dve", lambda e, i=i: e.scalar_tensor_tensor(
            out=a_out[:, i, :], in0=modv[:, (i * 3 + 1) * 8:(i * 3 + 2) * 8], scalar=1.0, in1=normg_l[:, i, :],
            op0=ALU.add, op1=ALU.mult), reads=[modb], writes=[modb])
    for i in (0, 2):
        k.op("dve", lambda e, i=i: e.tensor_scalar(
            out=modv[:, (i * 3 + 2) * 8:(i * 3 + 3) * 8], in0=modv[:, (i * 3 + 2) * 8:(i * 3 + 3) * 8],
            scalar1=0.5, scalar2=None, op0=ALU.mult), reads=[modb], writes=[modb])


class Act:
    def __init__(self, t, tiles, name):
        self.t = t
        self.tiles = tiles
        self.b = [Buf(f"{name}{j}") for j in range(len(tiles))]

    def ap(self, c, j):
        c0, w = self.tiles[j]
        return self.t[:, c, c0:c0 + w]


def emit_adaln(P, X, H, a_ap, shift_ap, C):
    k = P.k
    for j, (c0, w) in enumerate(X.tiles):
        ps, psb = P.PS[6], P.PSb[6]
        for c in range(8):
            sq, sqb = C.tmpbf()
            k.op("act", lambda e, c=c, j=j, w=w, sq=sq: e.activation(out=sq[:, 0:w], in_=X.ap(c, j), func=AF.Square),
                 reads=[X.b[j]], writes=[sqb])
            k.op("pe", lambda e, c=c, w=w, sq=sq: e.matmul(ps[:, 0:w], lhsT=C.ones_bf[:], rhs=sq[:, 0:w],
                                                     start=(c == 0), stop=(c == 7)),
                 reads=[sqb, C.cb], writes=[psb])
        k.op("act", lambda e, w=w: e.activation(out=C.RS[:, 0:w], in_=ps[:, 0:w], func=AF.Sqrt,
                                                bias=C.eps[:, 0:1], scale=1.0 / D),
             reads=[psb, C.cb], writes=[C.RSb])
        k.op("dve", lambda e, w=w: e.reciprocal(out=C.RS[:, 0:w], in_=C.RS[:, 0:w]), reads=[C.RSb], writes=[C.RSb])
        for c in range(8):
            tt, ttb = C.tmp()
            k.op("dve", lambda e, c=c, j=j, w=w, tt=tt: e.scalar_tensor_tensor(
                out=tt[:, 0:w], in0=X.ap(c, j), scalar=a_ap[:, c:c + 1], in1=C.RS[:, 0:w],
                op0=ALU.mult, op1=ALU.mult), reads=[X.b[j], C.RSb, C.modb], writes=[ttb])
            k.op("act", lambda e, c=c, j=j, w=w, tt=tt: e.activation(
                out=H.ap(c, j), in_=tt[:, 0:w], func=AF.Identity, bias=shift_ap[:, c:c + 1], scale=1.0),
                reads=[ttb, C.modb], writes=[H.b[j]])


def emit_ffn(P, X, H, U, Ub, w1, w3, w2, gate_ap, C, hook=None):
    k = P.k
    groups = [(0, 3), (3, 3), (6, 3), (9, 2)]
    ntile = len(X.tiles)
    for (p0, npair) in groups:
        for pl in range(npair):
            p = p0 + pl
            wa1, wa1b = P.load_wa(w1[:, p * 256:(p + 1) * 256])
            wa3, wa3b = P.load_wa(w3[:, p * 256:(p + 1) * 256])
            for fl in range(2):
                fu = pl * 2 + fl
                for j in range(ntile):
                    c0, w = X.tiles[j]
                    ps1, ps1b = P.bank("h1", [0, 1])
                    ps3, ps3b = P.bank("h3", [2, 3])
                    for c in range(8):
                        k.op("pe", lambda e, c=c, j=j, w=w, ps1=ps1, wa1=wa1, fl=fl: e.matmul(
                            ps1[:, 0:w], lhsT=wa1[:, c, fl * 128:(fl + 1) * 128], rhs=H.ap(c, j),
                            start=(c == 0), stop=(c == 7)), reads=[wa1b, H.b[j]], writes=[ps1b])
                    for c in range(8):
                        k.op("pe", lambda e, c=c, j=j, w=w, ps3=ps3, wa3=wa3, fl=fl: e.matmul(
                            ps3[:, 0:w], lhsT=wa3[:, c, fl * 128:(fl + 1) * 128], rhs=H.ap(c, j),
                            start=(c == 0), stop=(c == 7)), reads=[wa3b, H.b[j]], writes=[ps3b])
                    sl, slb = C.tmp()
                    k.op("act", lambda e, w=w, ps1=ps1, sl=sl: e.activation(out=sl[:, 0:w], in_=ps1[:, 0:w], func=AF.Silu),
                         reads=[ps1b], writes=[slb])
                    k.op("dve", lambda e, w=w, c0=c0, ps3=ps3, sl=sl, fu=fu: e.tensor_tensor(
                        out=U[:, fu, c0:c0 + w], in0=ps3[:, 0:w], in1=sl[:, 0:w], op=ALU.mult),
                        reads=[ps3b, slb], writes=[Ub[fu][j]])
                    if hook is not None:
                        hook()
        for pl in range(npair):
            P.load_wb(pl, w2[(p0 + pl) * 256:(p0 + pl + 1) * 256, :])
        nfu = npair * 2
        for c in range(8):
            for j in range(ntile):
                c0, w = X.tiles[j]
                pso, psob = P.bank("o", [4, 5])
                for fu in range(nfu):
                    k.op("pe", lambda e, fu=fu, c=c, w=w, c0=c0, pso=pso: e.matmul(
                        pso[:, 0:w], lhsT=P.WB[fu // 2][:, fu % 2, c * 128:(c + 1) * 128], rhs=U[:, fu, c0:c0 + w],
                        start=(fu == 0), stop=(fu == nfu - 1)), reads=[P.WBb[fu // 2], Ub[fu][j]], writes=[psob])
                k.op("dve", lambda e, c=c, j=j, w=w, pso=pso: e.scalar_tensor_tensor(
                    out=X.ap(c, j), in0=pso[:, 0:w], scalar=gate_ap[:, c:c + 1], in1=X.ap(c, j),
                    op0=ALU.mult, op1=ALU.add), reads=[psob, X.b[j], C.modb], writes=[X.b[j]])


class Common:
    pass


def emit_inproj_tile(P, H, j, wa, wab, cl, ps, psb, cols=None):
    k = P.k
    c0, w = H.tiles[j]
    if cols is not None:
        c0, w = c0 + cols[0], cols[1]
    for c in range(8):
        k.op("pe", lambda e, c=c, c0=c0, w=w: e.matmul(
            ps[:, 0:w], lhsT=wa[:, c, cl * 128:(cl + 1) * 128], rhs=H.t[:, c, c0:c0 + w],
            start=(c == 0), stop=(c == 7)), reads=[wab, H.b[j]], writes=[psb])
    return w


def emit_qknorm(P, C, ps, psb, w, g_ap, out_ap, out_b, extra_reads=()):
    k = P.k
    sq, sqb = C.tmpbf()
    k.op("act", lambda e: e.activation(out=sq[:, 0:w], in_=ps[:, 0:w], func=AF.Square), reads=[psb], writes=[sqb])
    p2, p2b = P.PS[6], P.PSb[6]
    k.op("pe", lambda e: e.matmul(p2[:, 0:w], lhsT=C.bd_bf[:], rhs=sq[:, 0:w], start=True, stop=True),
         reads=[sqb, C.cb], writes=[p2b])
    rs, rsb = C.tmp()
    k.op("act", lambda e: e.activation(out=rs[:, 0:w], in_=p2[:, 0:w], func=AF.Sqrt, bias=C.eps[:, 0:1],
                                       scale=1.0 / HD), reads=[p2b, C.cb], writes=[rsb])
    k.op("dve", lambda e: e.reciprocal(out=rs[:, 0:w], in_=rs[:, 0:w]), reads=[rsb], writes=[rsb])
    k.op("dve", lambda e: e.scalar_tensor_tensor(out=out_ap, in0=ps[:, 0:w], scalar=g_ap, in1=rs[:, 0:w],
                                                 op0=ALU.mult, op1=ALU.mult),
         reads=[psb, rsb, C.cb] + list(extra_reads), writes=[out_b])


def emit_outproj(P, C, X, MIX, MIXb, wout_l, gate_ap):
    k = P.k
    for pr in range(4):
        wa, wab = P.load_wa(wout_l[:, pr * 256:(pr + 1) * 256])
        for cl in range(2):
            c = pr * 2 + cl
            for j, (c0, w) in enumerate(X.tiles):
                pso, psob = P.bank("o", [4, 5])
                for fc in range(8):
                    k.op("pe", lambda e, fc=fc, c0=c0, w=w, pso=pso, wa=wa, cl=cl: e.matmul(
                        pso[:, 0:w], lhsT=wa[:, fc, cl * 128:(cl + 1) * 128], rhs=MIX(fc, c0, w),
                        start=(fc == 0), stop=(fc == 7)), reads=[wab, MIXb[j]], writes=[psob])
                k.op("dve", lambda e, c=c, j=j, w=w, pso=pso: e.scalar_tensor_tensor(
                    out=X.ap(c, j), in0=pso[:, 0:w], scalar=gate_ap[:, c:c + 1], in1=X.ap(c, j),
                    op0=ALU.mult, op1=ALU.add), reads=[psob, X.b[j], C.modb], writes=[X.b[j]])


def emit_expb(P, C, relrep, relb, onehot, vmask, scr, EXPB, EXPBb):
    k = P.k
    nc = P.nc
    E8 = C.E8
    E8b = Buf("E8")
    for h in range(8):
        lt, ltb = C.tmp()
        k.op("dve", lambda e, h=h, lt=lt: e.tensor_scalar(out=lt[0:32, 0:128], in0=C.ones_f[0:32, 0:128],
                                                          scalar1=relrep[0:32, h:h + 1], scalar2=None, op0=ALU.mult),
             reads=[relb, C.cb], writes=[ltb])
        ps, psb = P.bank("h1", [0, 1])
        k.op("pe", lambda e, lt=lt, ps=ps: e.matmul(ps[:, 0:512], lhsT=lt[0:32, 0:128], rhs=onehot[0:32, :],
                                                    start=True, stop=True), reads=[ltb, C.cb], writes=[psb])
        ex, exb = C.tmp()
        k.op("act", lambda e, ps=ps, ex=ex: e.activation(out=ex[:, :], in_=ps[:, 0:512], func=AF.Exp),
             reads=[psb], writes=[exb])
        k.op("dve", lambda e, h=h, ex=ex: e.tensor_tensor(out=E8[:, h, :], in0=ex[:, :], in1=vmask[:, :], op=ALU.mult),
             reads=[exb, C.cb], writes=[E8b])
    scrb = Buf("scr")
    k.dma("sp", [(scr.ap(), E8[:])], reads=[E8b], writes=[scrb])
    pairs = []
    for h in range(8):
        for bi in range(3):
            off = h * 512 + 128 * (1 - bi) + 255
            src = bass.AP(tensor=scr, offset=off, ap=[[8 * 512 - 1, 128], [1, 128]])
            pairs.append((EXPB[:, h, bi, :], src))
    k.dma("sp", pairs, reads=[scrb], writes=[EXPBb])
    return scrb


def emit_conv0(P, C, H, MIX, MIXb, win_l, convw, convb, flags, S_t, Sb):
    k = P.k
    ntile_own = 4
    for hh in range(2):
        wgb, wgbb = P.load_wa(win_l[:, 768 + hh * 256: 768 + (hh + 1) * 256])
        wgc, wgcb = P.load_wa(win_l[:, 1280 + hh * 256: 1280 + (hh + 1) * 256])
        wu, wub = P.load_wa(win_l[:, 1792 + hh * 256: 1792 + (hh + 1) * 256])
        for cl in range(2):
            cc = hh * 2 + cl
            for j in range(ntile_own):
                c0, w = H.tiles[j]
                pg, pgb = P.bank("h1", [0, 1])
                pu, pub = P.bank("h3", [2, 3])
                emit_inproj_tile(P, H, j, wgc, wgcb, cl, pg, pgb)
                emit_inproj_tile(P, H, j, wu, wub, cl, pu, pub)
                tg, tgb = C.tmp()
                k.op("act", lambda e, tg=tg, pg=pg, w=w: e.activation(out=tg[:, 0:w], in_=pg[:, 0:w], func=AF.Copy),
                     reads=[pgb], writes=[tgb])
                k.op("dve", lambda e, tg=tg, pu=pu, w=w, c0=c0: e.tensor_tensor(
                    out=S_t[:, 1 + c0:1 + c0 + w], in0=pu[:, 0:w], in1=tg[:, 0:w], op=ALU.mult),
                    reads=[pub, tgb], writes=[Sb])
            pg, pgb = P.bank("h1", [0, 1])
            pu, pub = P.bank("h3", [2, 3])
            emit_inproj_tile(P, H, 4, wgc, wgcb, cl, pg, pgb, cols=(127, 2))
            emit_inproj_tile(P, H, 4, wu, wub, cl, pu, pub, cols=(127, 2))
            tg, tgb = C.tmp()
            k.op("act", lambda e, tg=tg, pg=pg: e.activation(out=tg[:, 0:2], in_=pg[:, 0:2], func=AF.Copy),
                 reads=[pgb], writes=[tgb])
            k.op("dve", lambda e, tg=tg, pu=pu: e.scalar_tensor_tensor(
                out=S_t[:, 0:1], in0=pu[:, 0:1], scalar=flags[:, 0:1], in1=tg[:, 0:1], op0=ALU.mult, op1=ALU.mult),
                reads=[pub, tgb, C.cb], writes=[Sb])
            k.op("dve", lambda e, tg=tg, pu=pu: e.scalar_tensor_tensor(
                out=S_t[:, 2049:2050], in0=pu[:, 1:2], scalar=flags[:, 1:2], in1=tg[:, 1:2], op0=ALU.mult, op1=ALU.mult),
                reads=[pub, tgb, C.cb], writes=[Sb])
            for j in range(ntile_own):
                c0, w = H.tiles[j]
                pb_, pbb = P.bank("o", [4, 5])
                emit_inproj_tile(P, H, j, wgb, wgbb, cl, pb_, pbb)
                t, tb = C.tmp()
                k.op("dve", lambda e, t=t, c0=c0, w=w, cc=cc: e.tensor_scalar(
                    out=t[:, 0:w], in0=S_t[:, 1 + c0:1 + c0 + w], scalar1=convw[:, cc, 1:2], scalar2=convb[:, cc:cc + 1],
                    op0=ALU.mult, op1=ALU.add), reads=[Sb, C.cb], writes=[tb])
                k.op("dve", lambda e, t=t, c0=c0, w=w, cc=cc: e.scalar_tensor_tensor(
                    out=t[:, 0:w], in0=S_t[:, c0:c0 + w], scalar=convw[:, cc, 0:1], in1=t[:, 0:w],
                    op0=ALU.mult, op1=ALU.add), reads=[Sb, C.cb, tb], writes=[tb])
                k.op("dve", lambda e, t=t, c0=c0, w=w, cc=cc: e.scalar_tensor_tensor(
                    out=t[:, 0:w], in0=S_t[:, 2 + c0:2 + c0 + w], scalar=convw[:, cc, 2:3], in1=t[:, 0:w],
                    op0=ALU.mult, op1=ALU.add), reads=[Sb, C.cb, tb], writes=[tb])
                k.op("dve", lambda e, t=t, c0=c0, w=w, cc=cc, pb_=pb_: e.tensor_tensor(
                    out=MIX(4 + cc, c0, w), in0=pb_[:, 0:w], in1=t[:, 0:w], op=ALU.mult),
                    reads=[pbb, tb], writes=[MIXb[j]])


def emit_qkv0(P, C, H, win_l, QT, QTb, KT, KTb, VA, VAb, gq, gk, flags):
    k = P.k
    for pr in range(2):
        wa, wab = P.load_wa(win_l[:, pr * 256:(pr + 1) * 256])
        for cl in range(2):
            hh = pr * 2 + cl
            for j in range(4):
                c0, w = H.tiles[j]
                ps, psb = P.bank("h1", [0, 1, 2, 3])
                emit_inproj_tile(P, H, j, wa, wab, cl, ps, psb)
                emit_qknorm(P, C, ps, psb, w, gq[:, 0:1], QT[:, hh, c0:c0 + w], QTb[j])
    wa, wab = P.load_wa(win_l[:, 512:768])
    for j in range(5):
        c0, w = H.tiles[j]
        ps, psb = P.bank("h1", [0, 1, 2, 3])
        emit_inproj_tile(P, H, j, wa, wab, 0, ps, psb)
        emit_qknorm(P, C, ps, psb, w, gk[:, 0:1], KT[:, c0:c0 + w], KTb[j])
    for tb in range(18):
        j = tb // 4 if tb < 16 else 4
        ps, psb = P.bank("o", [4, 5])
        for c in range(8):
            k.op("pe", lambda e, c=c, tb=tb, ps=ps: e.matmul(
                ps[:, 0:128], lhsT=H.t[:, c, tb * 128:(tb + 1) * 128], rhs=wa[:, c, 128:256],
                start=(c == 0), stop=(c == 7)), reads=[wab, H.b[j]], writes=[psb])
        if tb < 16:
            k.op("act", lambda e, tb=tb, ps=ps: e.activation(out=VA[:, tb, 0, 0:64], in_=ps[:, 0:64], func=AF.Copy),
                 reads=[psb], writes=[VAb])
            k.op("act", lambda e, tb=tb, ps=ps: e.activation(out=VA[:, tb, 1, 64:128], in_=ps[:, 64:128], func=AF.Copy),
                 reads=[psb], writes=[VAb])
        else:
            fl = flags[:, tb - 16:tb - 15]
            k.op("dve", lambda e, tb=tb, ps=ps, fl=fl: e.tensor_scalar(
                out=VA[:, tb, 0, 0:64], in0=ps[:, 0:64], scalar1=fl, scalar2=None, op0=ALU.mult),
                reads=[psb, C.cb], writes=[VAb])
            k.op("dve", lambda e, tb=tb, ps=ps, fl=fl: e.tensor_scalar(
                out=VA[:, tb, 1, 64:128], in0=ps[:, 64:128], scalar1=fl, scalar2=None, op0=ALU.mult),
                reads=[psb, C.cb], writes=[VAb])
            k.op("dve", lambda e, tb=tb, fl=fl: e.tensor_scalar(
                out=VA[:, tb, 0, 64:128], in0=C.ones_f[:, 0:64], scalar1=fl, scalar2=None, op0=ALU.mult),
                reads=[C.cb], writes=[VAb])
            k.op("dve", lambda e, tb=tb, fl=fl: e.tensor_scalar(
                out=VA[:, tb, 1, 0:64], in0=C.ones_f[:, 0:64], scalar1=fl, scalar2=None, op0=ALU.mult),
                reads=[C.cb], writes=[VAb])


def emit_attn0(P, C, QT, QTb, KT, KTb, VA, VAb, EXPB, EXPBb, expsink, MIXLO, MIXb):
    k = P.k
    for n in range(16):
        j = n // 4
        for g in range(2):
            gs = slice(g * 64, (g + 1) * 64)
            ds_ = slice((1 - g) * 64, (2 - g) * 64)
            po, pob = P.bank("o", [4, 5])
            for bi in range(3):
                kb = n - 1 + bi
                kidx = 16 if kb < 0 else (17 if kb > 15 else kb)
                kj = kidx // 4 if kidx < 16 else 4
                ps, psb = P.bank("h1", [0, 1, 2, 3])
                k.op("pe", lambda e, ps=ps, kidx=kidx, n=n, gs=gs: e.matmul(
                    ps[:, 0:512], lhsT=KT[gs, kidx * 128:(kidx + 1) * 128], rhs=QT[gs, :, n * 128:(n + 1) * 128],
                    start=True, stop=True), reads=[KTb[kj], QTb[j]], writes=[psb])
                ex, exb = C.tmp()
                k.op("act", lambda e, ps=ps, ex=ex: e.activation(out=ex[:, :], in_=ps[:, 0:512], func=AF.Exp,
                                                                 scale=HD ** -0.5),
                     reads=[psb], writes=[exb])
                pt, ptb = C.tmpbf()
                k.op("dve", lambda e, ex=ex, pt=pt, g=g, bi=bi: e.tensor_tensor(
                    out=pt[:, :].rearrange("p (h q) -> p h q", h=4), in0=ex[:, :].rearrange("p (h q) -> p h q", h=4),
                    in1=EXPB[:, g * 4:(g + 1) * 4, bi, :], op=ALU.mult), reads=[exb, EXPBb], writes=[ptb])
                for hh in range(4):
                    k.op("pe", lambda e, po=po, pt=pt, hh=hh, kidx=kidx, g=g, bi=bi: e.matmul(
                        po[:, hh * 128:(hh + 1) * 128], lhsT=VA[:, kidx, g, :], rhs=pt[:, hh * 128:(hh + 1) * 128],
                        start=(bi == 0 and hh == 0), stop=(bi == 2 and hh == 3), skip_group_check=True),
                        reads=[VAb, ptb], writes=[pob])
            rd, rdb = C.tmp()
            for hh in range(4):
                k.op("dve", lambda e, rd=rd, po=po, hh=hh, g=g, ds_=ds_: e.tensor_scalar(
                    out=rd[ds_, hh * 128:(hh + 1) * 128], in0=po[ds_, hh * 128:(hh + 1) * 128],
                    scalar1=expsink[ds_, g * 4 + hh:g * 4 + hh + 1], scalar2=None, op0=ALU.add),
                    reads=[pob, C.cb], writes=[rdb])
            k.op("dve", lambda e, rd=rd, ds_=ds_: e.reciprocal(out=rd[ds_, :], in_=rd[ds_, :]), reads=[rdb], writes=[rdb])
            k.op("dve", lambda e, rd=rd, po=po, gs=gs, ds_=ds_, n=n: e.tensor_tensor(
                out=MIXLO[gs, 0:4, n * 128:(n + 1) * 128], in0=po[gs, :].rearrange("p (h q) -> p h q", h=4),
                in1=rd[ds_, :].rearrange("p (h q) -> p h q", h=4), op=ALU.mult), reads=[pob, rdb], writes=[MIXb[j]])


class RotTmp:
    def __init__(self, k, name, n, dt):
        self.t = [k.sb(f"{name}{i}", [128, 512], dt) for i in range(n)]
        self.b = [Buf(f"{name}{i}") for i in range(n)]
        self.i = 0

    def __call__(self):
        i = self.i
        self.i = (i + 1) % len(self.t)
        return self.t[i], self.b[i]


BUCKET_MODE = "trunc"


def t5_bucket_np(rel):
    n = np.abs(rel)
    v = (np.log(np.maximum(n, 1).astype(np.float32) / np.float32(8)) / np.float32(math.log(16.0)) * np.float32(8))
    large = 8 + (np.rint(v).astype(np.int32) if BUCKET_MODE == "round" else v.astype(np.int32))
    large = np.minimum(large, 15)
    return np.where(rel > 0, 16, 0) + np.where(n < 8, n, large)


def build_LA(do_l1=True, stop_after=None):
    nc = bass.Bass("TRN2", target_bir_lowering=False)
    k = KB(nc)
    dt_in = lambda name, shape: nc.dram_tensor(name, list(shape), F32, kind="ExternalInput").ap()
    xT_d = dt_in("xT", [128, 8, 2048])
    xH_d = dt_in("xH", [128, 8, 256])
    cf_d = dt_in("cf", [128, 1400])
    oh_d = dt_in("oh", [32, 512])
    vm_d = dt_in("vm", [128, 512])
    wmod_d = dt_in("wmod", [2, 1024, 9216])
    w1_d = dt_in("w1", [2, 2, 1024, DFFP])
    w3_d = dt_in("w3", [2, 2, 1024, DFFP])
    w2_d = dt_in("w2", [2, 2, DFFP, 1024])
    win_d = dt_in("win", [2, 1024, INC])
    wout_d = dt_in("wout", [2, 1024, 1024])
    xo_d = nc.dram_tensor("xo", [128, 8, 2048], F32, kind="ExternalOutput").ap()
    scr = nc.dram_tensor("scr", [128, 8 * 512], F32, kind="Internal")

    P = Prog(nc, k)
    C = Common()
    C.tmp = RotTmp(k, "tf", 5, F32)
    C.tmpbf = RotTmp(k, "tb", 3, BF16)
    C.RS = k.sb("RS", [128, 512], F32)
    C.RSb = Buf("RS")
    C.cb = Buf("consts")
    C.modb = Buf("mod")
    CF = k.sb("CF", [128, 1400], F32)
    OH = k.sb("OH", [32, 512], F32)
    VM = k.sb("VM", [128, 512], F32)
    C.ones_bf = k.sb("ones_bf", [128, 128], BF16)
    C.bd_bf = k.sb("bd_bf", [128, 128], BF16)
    C.ones_f = k.sb("ones_f", [128, 128], F32)
    C.eps = k.sb("eps", [128, 1], F32)
    condbf = k.sb("condbf", [128, 8], BF16)
    MODV = k.sb("modv", [128, 2, 72], F32)
    AV = k.sb("av", [128, 2, 3, 8], F32)
    EXS = k.sb("exs", [128, 8], F32)
    XT = k.sb("XT", [128, 8, 2048], F32)
    Ht = k.sb("H", [128, 8, 2304], BF16)
    UQ = k.sb("UQ", [128, 15360], BF16)
    MX = k.sb("MX", [128, 4096], F32)

    o_cT, o_fl, o_ng, o_bm, o_cw, o_cb, o_gq, o_gk, o_sink = 0, 8, 10, 58, 202, 214, 218, 220, 222
    cT = CF[:, o_cT:o_cT + 8]
    flags = CF[:, o_fl:o_fl + 2]
    normg = CF[:, o_ng:o_ng + 48].rearrange("p (l i c) -> p l i c", l=2, i=3)
    bmod = CF[:, o_bm:o_bm + 144].rearrange("p (l m) -> p l m", l=2)
    convw = CF[:, o_cw:o_cw + 12].rearrange("p (c t) -> p c t", c=4)
    convb = CF[:, o_cb:o_cb + 4]
    gq = CF[:, o_gq:o_gq + 2]
    gk = CF[:, o_gk:o_gk + 2]
    sink = CF[:, o_sink:o_sink + 8]
    relrep = CF[:, 232:240]

    XH = MX[:, 0:2048].rearrange("p (c t) -> p c t", c=8)
    MIXHI = MX[:, :].bitcast(BF16).rearrange("p (c t) -> p c t", c=4)
    U = UQ[:, 0:6 * 2304].rearrange("p (f t) -> p f t", f=6)
    QT = UQ[:, 0:8192].rearrange("p (h t) -> p h t", h=4)
    KT = UQ[:, 8192:8192 + 2304]
    VA = UQ[:, 10496:10496 + 4608].rearrange("p (b g d) -> p b g d", b=18, g=2)
    S_t = UQ[:, 10496:10496 + 4100].bitcast(F32)
    C.E8 = UQ[:, 0:8192].bitcast(F32).rearrange("p (h m) -> p h m", h=8)
    EXPB = P.WBf[:, :, :].rearrange("p a (b q) -> p (a b) q", q=128).rearrange("p (h b) q -> p h b q", h=8)

    tiles5 = [(0, 512), (512, 512), (1024, 512), (1536, 512), (2048, 256)]
    tiles4 = tiles5[:4]

    class XAct:
        def __init__(self, tiles):
            self.tiles = tiles
            self.b = XB[:len(tiles)]

        def ap(self, c, j):
            if j < 4:
                return XT[:, c, j * 512:(j + 1) * 512]
            return XH[:, c, :]
    XB = [Buf(f"x{j}") for j in range(5)]
    X5, X4 = XAct(tiles5), XAct(tiles4)
    H5 = Act(Ht, tiles5, "h")
    H4 = Act(Ht, tiles4, "h")
    H4.b = H5.b[:4]
    Ub = [[Buf(f"u{f}_{j}") for j in range(5)] for f in range(6)]

    k.dma("sp", [(CF[:], cf_d)], writes=[C.cb])
    k.dma("sp", [(OH[:], oh_d), (VM[:], vm_d)], writes=[C.cb], sembuf=C.cb)
    for j in range(4):
        k.dma("sp", [(XT[:, :, j * 512:(j + 1) * 512], xT_d[:, :, j * 512:(j + 1) * 512])], writes=[XB[j]])
    k.dma("sp", [(XH, xH_d)], writes=[XB[4]])
    cst = Buf("cst")
    k.op("dve", lambda e: e.memset(C.ones_bf[:], 1.0), writes=[cst])
    k.op("dve", lambda e: e.memset(C.ones_f[:], 1.0), writes=[cst])
    k.op("dve", lambda e: e.memset(C.eps[:], EPS), writes=[cst])
    k.op("dve", lambda e: e.memset(C.bd_bf[:], 0.0), writes=[cst])
    k.op("dve", lambda e: e.memset(C.bd_bf[0:64, 0:64], 1.0), writes=[cst])
    k.op("dve", lambda e: e.memset(C.bd_bf[64:128, 64:128], 1.0), writes=[cst])
    condb = Buf("cond")
    k.op("act", lambda e: e.activation(out=condbf[:], in_=cT, func=AF.Silu), reads=[C.cb], writes=[condb])
    k.op("act", lambda e: e.activation(out=EXS[:], in_=sink, func=AF.Exp), reads=[C.cb], writes=[cst])
    k.op("dve", lambda e: e.tensor_copy(out=C.eps[:], in_=C.eps[:]), reads=[cst, C.cb], writes=[C.cb])

    def layer_mod(l):
        emit_mod(P, condbf, condb, wmod_d[l], bmod[:, l, :], MODV[:, l, :], C.modb, AV[:, l], normg[:, l])

    def mcol(l, i, kind):
        return MODV[:, l, (i * 3 + kind) * 8:(i * 3 + kind + 1) * 8]

    def mix_ap(fc, c0, w):
        if fc < 4:
            return Ht[:, fc, c0:c0 + w]
        return MIXHI[:, fc - 4, c0:c0 + w]

    def finish():
        ob = Sink("out")
        for j in range(4):
            k.dma("sp", [(xo_d[:, :, j * 512:(j + 1) * 512], XT[:, :, j * 512:(j + 1) * 512])], reads=[XB[j]],
                  sembuf=XB[j], sink=ob)
        k.wait_all("sp", [ob])
        k.close()
        return nc

    layer_mod(0)
    emit_adaln(P, X5, H5, AV[:, 0, 0], mcol(0, 0, 0), C)
    emit_ffn(P, X5, H5, U, Ub, w1_d[0, 0], w3_d[0, 0], w2_d[0, 0], mcol(0, 0, 2), C)
    if stop_after == "ffn0":
        return finish()
    k.barrier()
    ohb = C.cb
    scrb = emit_expb(P, C, relrep, C.cb, OH, VM, scr, EXPB, P.WBb[0])
    k.barrier(dbufs=[scrb])
    if stop_after == "expb":
        dbg = nc.dram_tensor("dbg", [128, 3072], F32, kind="ExternalOutput").ap()
        ob2 = Buf("dbg")
        k.dma("sp", [(dbg, P.WBf[:].rearrange("p a b -> p (a b)"))], reads=[P.WBb[0]], writes=[ob2], sembuf=ob2)
        k.wait_all("sp", [ob2])
        return finish()
    emit_adaln(P, X5, H5, AV[:, 0, 1], mcol(0, 1, 0), C)
    k.barrier()
    MIXb = [Buf(f"mix{j}") for j in range(4)]
    Sb = Buf("S")
    emit_conv0(P, C, H5, mix_ap, MIXb, win_d[0], convw, convb, flags, S_t, Sb)
    k.barrier()
    QTb = [Buf(f"qt{j}") for j in range(4)]
    KTb = [Buf(f"kt{j}") for j in range(5)]
    VAb = Buf("va")
    k.op("dve", lambda e: e.memset(VA[:, 0:16, 0, 64:128], 1.0), writes=[VAb])
    k.op("dve", lambda e: e.memset(VA[:, 0:16, 1, 0:64], 1.0), writes=[VAb])
    emit_qkv0(P, C, H5, win_d[0], QT, QTb, KT, KTb, VA, VAb, gq, gk, flags)
    k.barrier()
    emit_attn0(P, C, QT, QTb, KT, KTb, VA, VAb, EXPB, P.WBb[0], EXS, Ht, MIXb)
    for i in (1, 2):
        P.WBb[i].r = dict(P.WBb[0].r)
        P.WBb[i].w = P.WBb[0].w
    if stop_after == "mix":
        dbg = nc.dram_tensor("dbg", [128, 8, 2048], BF16, kind="ExternalOutput").ap()
        ob2 = Buf("dbg")
        k.dma("sp", [(dbg[:, 0:4, :], Ht[:, 0:4, 0:2048]), (dbg[:, 4:8, :], MIXHI)], reads=MIXb, writes=[ob2], sembuf=ob2)
        k.wait_all("sp", [ob2])
        return finish()
    emit_outproj(P, C, X4, mix_ap, MIXb, wout_d[0], mcol(0, 1, 2))
    if stop_after == "mixer":
        return finish()
    k.barrier()
    emit_adaln(P, X4, H4, AV[:, 0, 2], mcol(0, 2, 0), C)
    emit_ffn(P, X4, H4, U, Ub, w1_d[0, 1], w3_d[0, 1], w2_d[0, 1], mcol(0, 2, 2), C)
    if not do_l1:
        return finish()

    rope_d = dt_in("rope", [128, 2, 2048])
    qo_d = nc.dram_tensor("qo", [128, 4, 2048], BF16, kind="ExternalOutput").ap()
    ko_d = nc.dram_tensor("ko", [128, 2048], BF16, kind="ExternalOutput").ap()
    vo_d = nc.dram_tensor("vo", [16, 128, 128], BF16, kind="ExternalOutput").ap()
    hco_d = nc.dram_tensor("hco", [12, 128, 2048], F32, kind="ExternalOutput").ap()
    layer_mod(1)
    emit_adaln(P, X4, H4, AV[:, 1, 0], mcol(1, 0, 0), C)
    emit_ffn(P, X4, H4, U, Ub, w1_d[1, 0], w3_d[1, 0], w2_d[1, 0], mcol(1, 0, 2), C)
    k.barrier()
    ROPE = UQ[:, 0:8192].bitcast(F32).rearrange("p (a t) -> p a t", a=2)
    ropeb = Buf("rope")
    k.dma("sp", [(ROPE, rope_d)], writes=[ropeb])
    emit_adaln(P, X4, H4, AV[:, 1, 1], mcol(1, 1, 0), C)
    outb = Sink("outs")
    modo_d = nc.dram_tensor("modo", [128, 96], F32, kind="ExternalOutput").ap()
    k.dma("sp", [(modo_d[:, 0:72], MODV[:, 1, :]), (modo_d[:, 72:96], AV[:, 1].rearrange("p i c -> p (i c)"))],
          reads=[C.modb], sembuf=C.modb, sink=outb)
    emit_inproj1(P, C, H4, win_d[1], CF[:, 240:241], CF[:, 241:242], ROPE[:, 0, :], ROPE[:, 1, :], ropeb,
                 qo_d, ko_d, vo_d, hco_d, outb)
    k.wait_all("sp", [outb])
    return finish()


def host_consts():
    idx = np.arange(512)
    rel = 255 - idx
    valid = (np.abs(rel) <= 128) & (idx < 511)
    bucket = t5_bucket_np(rel.astype(np.int64))
    oh = np.zeros((32, 512), np.float32)
    oh[bucket[valid], idx[valid]] = 1.0
    vm = np.tile(valid.astype(np.float32)[None, :], (128, 1))
    return oh, vm


def fm(v):
    v = np.asarray(v, np.float32)
    sh = v.shape
    v = v.reshape(sh[:-1] + (sh[-1] // 128, 128))
    return np.moveaxis(v, -1, 0)


def prep_LA(inp):
    x = np.asarray(inp["x"], np.float32)
    qperm = np.concatenate([np.r_[hh * 64:(hh + 1) * 64, (4 + hh) * 64:(5 + hh) * 64] for hh in range(4)])
    win = np.ascontiguousarray(np.asarray(inp["w_in"], np.float32))
    win = np.concatenate([win[:, :, qperm], win[:, :, 512:]], axis=2)
    eo = np.r_[0:64:2, 1:64:2]
    eo_q = np.concatenate([h * 64 + eo for h in range(8)])
    eo_k = np.concatenate([512 + h * 64 + eo for h in range(2)])
    win[1] = np.concatenate([win[1][:, eo_q], win[1][:, eo_k], win[1][:, 640:]], axis=1)
    wout = np.asarray(inp["w_out"], np.float32)
    wout = np.ascontiguousarray(np.concatenate([wout[:, qperm, :], wout[:, 512:, :]], axis=1))
    pad = DFFP - DFF
    w1 = np.pad(np.asarray(inp["ffn_w1"], np.float32), ((0, 0), (0, 0), (0, 0), (0, pad)))
    w3 = np.pad(np.asarray(inp["ffn_w3"], np.float32), ((0, 0), (0, 0), (0, 0), (0, pad)))
    w2 = np.pad(np.asarray(inp["ffn_w2"], np.float32), ((0, 0), (0, 0), (0, pad), (0, 0)))
    wmod = np.ascontiguousarray(np.asarray(inp["w_mod"], np.float32))
    oh, vm = host_consts()
    shared = dict(oh=oh, vm=vm, wmod=wmod, w1=w1, w3=w3, w2=w2, win=np.ascontiguousarray(win), wout=wout)
    maps = []
    for core in range(NCORE):
        b, qd = core // 4, core % 4
        t0 = qd * TOK
        xs = x[b, t0:t0 + TOK]
        xT = np.ascontiguousarray(xs.T.reshape(8, 128, TOK).transpose(1, 0, 2))
        halo = np.zeros((256, D), np.float32)
        fl = np.zeros((2,), np.float32)
        if qd > 0:
            halo[0:128] = x[b, t0 - 128:t0]
            fl[0] = 1.0
        if qd < 3:
            halo[128:256] = x[b, t0 + TOK:t0 + TOK + 128]
            fl[1] = 1.0
        xH = np.ascontiguousarray(halo.T.reshape(8, 128, 256).transpose(1, 0, 2))
        cf = np.zeros((128, 1400), np.float32)
        cf[:, 0:8] = fm(inp["c"][b])
        cf[:, 8:10] = fl[None, :]
        cf[:, 10:58] = fm(inp["norm_g"]).reshape(128, 48)
        cf[:, 58:202] = fm(inp["b_mod"]).reshape(128, 144)
        cw = np.asarray(inp["b_conv_w"], np.float32)[0]
        cf[:, 202:214] = fm(cw).transpose(0, 2, 1).reshape(128, 12)
        cf[:, 214:218] = fm(np.asarray(inp["b_conv_b"], np.float32)[0])
        aq = np.asarray(inp["a_qk_g"], np.float32)[0]
        cf[:, 218] = np.tile(aq[0], 2)
        cf[:, 220] = np.tile(aq[1], 2)
        cf[:, 222:230] = np.asarray(inp["a_sink"], np.float32)[0][None, :]
        cf[0:32, 232:240] = np.asarray(inp["rel_table"], np.float32)
        cg = np.asarray(inp["c_qk_g"], np.float32)[0]
        cf[:, 240] = np.tile(cg[0][eo], 2)
        cf[:, 241] = np.tile(cg[1][eo], 2)
        pos = np.arange(t0, t0 + TOK)
        row = (pos // 64).astype(np.float32)
        col = (pos % 64).astype(np.float32)
        inv = (np.float32(10000.0) ** (-np.arange(0, 32, 2, dtype=np.float32) / np.float32(32))).astype(np.float32)
        ang = np.concatenate([row[:, None] * inv, col[:, None] * inv], axis=-1).astype(np.float32)
        cs = np.cos(ang).astype(np.float32).T
        sn = np.sin(ang).astype(np.float32).T
        rope = np.zeros((128, 2, TOK), np.float32)
        for qd4 in range(4):
            rope[qd4 * 32:(qd4 + 1) * 32, 0] = cs
            rope[qd4 * 32:(qd4 + 1) * 32, 1] = sn if qd4 % 2 == 0 else -sn
        m_rope = rope
        m = dict(shared)
        m.update(xT=xT, xH=xH, cf=cf, rope=m_rope)
        maps.append(m)
    return maps


def emit_rope(P, C, qn, qnb, CS, SNs, ropeb, c0, w, out_ap, out_b):
    k = P.k
    t1, t1b = C.tmp()
    k.op("dve", lambda e: e.tensor_tensor(out=t1[:, 0:w], in0=qn[:, 0:w], in1=CS[:, c0:c0 + w], op=ALU.mult),
         reads=[qnb, ropeb], writes=[t1b])
    t2, t2b = C.tmp()
    for qd in range(4):
        src = qd ^ 1
        k.op("dve", lambda e, qd=qd, src=src: e.tensor_tensor(
            out=t2[qd * 32:(qd + 1) * 32, 0:w], in0=qn[src * 32:(src + 1) * 32, 0:w],
            in1=SNs[src * 32:(src + 1) * 32, c0:c0 + w], op=ALU.mult), reads=[qnb, ropeb], writes=[t2b])
    k.op("dve", lambda e: e.tensor_tensor(out=out_ap, in0=t1[:, 0:w], in1=t2[:, 0:w], op=ALU.add),
         reads=[t1b, t2b], writes=[out_b])


def emit_inproj1(P, C, H, win_l, gq, gk, CS, SNs, ropeb, qo_d, ko_d, vo_d, hco_d, outb):
    k = P.k
    for pr in range(3):
        wa, wab = P.load_wa(win_l[:, pr * 256:(pr + 1) * 256])
        for cl in range(2):
            if pr == 2 and cl == 1:
                break
            for j in range(4):
                c0, w = H.tiles[j]
                ps, psb = P.bank("h1", [0, 1, 2, 3])
                emit_inproj_tile(P, H, j, wa, wab, cl, ps, psb)
                qn, qnb = C.tmp()
                emit_qknorm(P, C, ps, psb, w, (gq if pr < 2 else gk)[:, 0:1], qn[:, 0:w], qnb)
                st, stb = C.tmpbf()
                emit_rope(P, C, qn, qnb, CS, SNs, ropeb, c0, w, st[:, 0:w], stb)
                dst = qo_d[:, pr * 2 + cl, c0:c0 + w] if pr < 2 else ko_d[:, c0:c0 + w]
                k.dma("sp", [(dst, st[:, 0:w])], reads=[stb], sembuf=stb, sink=outb)
        if pr == 2:
            for tb in range(16):
                j = tb // 4
                ps, psb = P.bank("o", [4, 5])
                for c in range(8):
                    k.op("pe", lambda e, c=c, tb=tb, ps=ps: e.matmul(
                        ps[:, 0:128], lhsT=H.t[:, c, tb * 128:(tb + 1) * 128], rhs=wa[:, c, 128:256],
                        start=(c == 0), stop=(c == 7)), reads=[wab, H.b[j]], writes=[psb])
                st, stb = C.tmpbf()
                k.op("act", lambda e, ps=ps, st=st: e.activation(out=st[:, 0:128], in_=ps[:, 0:128], func=AF.Copy),
                     reads=[psb], writes=[stb])
                k.dma("sp", [(vo_d[tb], st[:, 0:128])], reads=[stb], sembuf=stb, sink=outb)
    for pr in range(6):
        wa, wab = P.load_wa(win_l[:, 768 + pr * 256:768 + (pr + 1) * 256])
        for cl in range(2):
            ch = pr * 2 + cl
            for j in range(4):
                c0, w = H.tiles[j]
                ps, psb = P.bank("h1", [0, 1, 2, 3])
                emit_inproj_tile(P, H, j, wa, wab, cl, ps, psb)
                st, stb = C.tmp()
                k.op("act", lambda e, ps=ps, st=st, w=w: e.activation(out=st[:, 0:w], in_=ps[:, 0:w], func=AF.Copy),
                     reads=[psb], writes=[stb])
                k.dma("sp", [(hco_d[ch, :, c0:c0 + w], st[:, 0:w])], reads=[stb], sembuf=stb, sink=outb)


NFFT = 16384


def build_LB():
    nc = bass.Bass("TRN2", target_bir_lowering=False)
    k = KB(nc)
    dt_in = lambda name, shape: nc.dram_tensor(name, list(shape), F32, kind="ExternalInput").ap()
    hc_d = dt_in("hc3", [3, 128, S])
    cf_d = dt_in("cf2", [128, 32])
    mlp_d = dt_in("mlpw", [64, 456])
    zemb_d = dt_in("zemb", [33, NFFT])
    win_d = dt_in("win", [128, 128, 128])
    dft_d = dt_in("dftc", [128, 512])
    tw_d = dt_in("tw", [128, 2, 2, 128])
    y_d = nc.dram_tensor("yT", [128, S], F32, kind="ExternalOutput").ap()
    zs = nc.dram_tensor("zs", [128, S], F32, kind="Internal")
    cs = nc.dram_tensor("cs", [128, S], F32, kind="Internal")

    PS = [k.ps(f"ps{i}", [128, 512]) for i in range(8)]
    PSb = [Buf(f"ps{i}") for i in range(8)]
    rr = {}

    def bank(group, banks):
        i = rr.get(group, 0)
        rr[group] = (i + 1) % len(banks)
        return PS[banks[i]], PSb[banks[i]]

    tmp = RotTmp(k, "tf", 8, F32)
    tmpbf = RotTmp(k, "tb", 4, BF16)
    cb = Buf("consts")
    CF = k.sb("CF", [128, 32], F32)
    MLP = k.sb("MLP", [64, 456], F32)
    DFT = k.sb("DFT", [128, 512], BF16)
    TW = k.sb("TW", [128, 2, 2, 128], F32)
    X0 = k.sb("X0", [128, S], F32)
    Z = k.sb("Z", [128, S], F32)
    IN = k.sb("IN", [128, S + 2], F32)
    KC = k.sb("KC", [128, 128, 128], BF16)
    ZC = k.sb("ZC", [64, 128, 128], BF16)
    ACC = k.sb("ACC", [128, 128], F32)
    SM = k.sb("SM", [128, 16], F32)
    ones_f = k.sb("ones_f", [128, 1], F32)
    X0b, Zb, INb, KCb, ZCb, ACCb, SMb = [Buf(n) for n in "X0 Z IN KC ZC ACC SM".split()]

    k.dma("sp", [(CF[:], cf_d), (MLP[:], mlp_d), (TW[:], tw_d)], writes=[cb])
    dftb = Buf("dft")
    k.dma("pool", [(DFT[:], dft_d)], writes=[dftb])
    Fre, Fim, nFim = DFT[:, 0:128], DFT[:, 128:256], DFT[:, 384:512]
    Fcat, FcatI2, FcatI1 = DFT[:, 0:256], DFT[:, 128:384], DFT[:, 256:512]
    k.op("dve", lambda e: e.memset(ones_f[:], 1.0), writes=[SMb])
    k.op("dve", lambda e: e.memset(SM[:, 0:1], math.pi / 2), writes=[SMb])
    k.op("dve", lambda e: e.memset(SM[:, 1:2], EPS), writes=[SMb])
    k.op("dve", lambda e: e.memset(IN[:, 0:1], 0.0), writes=[INb])
    k.op("dve", lambda e: e.memset(IN[:, S + 1:S + 2], 0.0), writes=[INb])

    CH = 2048
    for part in range(3):
        k.dma("sp", [(IN[:, 1:S + 1], hc_d[part])], writes=[INb])
        for cc in range(S // CH):
            c0 = cc * CH
            wc = CF[:, part * 4:part * 4 + 3]
            bc = CF[:, part * 4 + 3:part * 4 + 4]
            for s0 in range(0, CH, 512):
                a0 = c0 + s0
                if part == 0:
                    o, ob_ = X0[:, a0:a0 + 512], X0b
                elif part == 1:
                    o, ob_ = Z[:, a0:a0 + 512], Zb
                else:
                    tt, ttb = tmp()
                    o, ob_ = tt[:, 0:512], ttb
                k.op("dve", lambda e, o=o, a0=a0, wc=wc, bc=bc: e.tensor_scalar(
                    out=o, in0=IN[:, a0 + 1:a0 + 513], scalar1=wc[:, 1:2], scalar2=bc, op0=ALU.mult, op1=ALU.add),
                    reads=[INb, cb], writes=[ob_])
                k.op("dve", lambda e, o=o, a0=a0, wc=wc: e.scalar_tensor_tensor(
                    out=o, in0=IN[:, a0:a0 + 512], scalar=wc[:, 0:1], in1=o, op0=ALU.mult, op1=ALU.add),
                    reads=[INb, cb, ob_], writes=[ob_])
                k.op("dve", lambda e, o=o, a0=a0, wc=wc: e.scalar_tensor_tensor(
                    out=o, in0=IN[:, a0 + 2:a0 + 514], scalar=wc[:, 2:3], in1=o, op0=ALU.mult, op1=ALU.add),
                    reads=[INb, cb, ob_], writes=[ob_])
                if part == 2:
                    k.op("pool", lambda e, o=o, a0=a0: e.tensor_tensor(
                        out=Z[:, a0:a0 + 512], in0=Z[:, a0:a0 + 512], in1=o, op=ALU.mult), reads=[ob_, Zb], writes=[Zb])
    zsb = Buf("zs")
    k.dma("sp", [(zs.ap(), Z[:])], reads=[Zb], writes=[zsb])
    src = bass.AP(tensor=zs, offset=0, ap=[[128, 64], [S, 128], [1, 128]])
    k.dma("pool", [(ZC[:], src)], reads=[zsb], writes=[ZCb])

    W1, W2, W3 = MLP[0:33, 0:64], MLP[:, 64:128], MLP[:, 128:192]
    W4 = MLP[:, 200:456].rearrange("p (d c) -> p d c", d=2)
    k.op("dve", lambda e: e.tensor_scalar(out=SM[0:64, 2:3], in0=MLP[:, 195:196], scalar1=0.25, scalar2=None, op0=ALU.mult),
         reads=[cb], writes=[SMb])
    for i in range(3):
        k.op("dve", lambda e, i=i: e.tensor_tensor(out=SM[0:64, 3 + i:4 + i], in0=MLP[:, 192 + i:193 + i], in1=SM[0:64, 2:3],
                                                   op=ALU.mult), reads=[cb, SMb], writes=[SMb])
    k.op("dve", lambda e: e.memset(ACC[:], 0.0), writes=[ACCb])

    def sin4(ps, psb, li):
        s1, s1b = tmp()
        k.op("act", lambda e: e.activation(out=s1[0:64, :], in_=ps[0:64, :], func=AF.Sin, bias=SM[0:64, 3 + li:4 + li],
                                           scale=SM[0:64, 2:3]), reads=[psb, SMb], writes=[s1b])
        a1, a1b = tmp()
        k.op("act", lambda e: e.activation(out=a1[0:64, :], in_=ps[0:64, :], func=AF.Abs, bias=SM[0:64, 3 + li:4 + li],
                                           scale=SM[0:64, 2:3]), reads=[psb, SMb], writes=[a1b])
        k.op("act", lambda e: e.activation(out=a1[0:64, :], in_=a1[0:64, :], func=AF.Sin, bias=SM[0:64, 0:1], scale=-1.0),
             reads=[a1b, SMb], writes=[a1b])
        k.op("dve", lambda e: e.tensor_tensor(out=a1[0:64, :], in0=a1[0:64, :], in1=s1[0:64, :], op=ALU.mult),
             reads=[a1b, s1b], writes=[a1b])
        k.op("dve", lambda e: e.tensor_tensor(out=s1[0:64, :], in0=s1[0:64, :], in1=s1[0:64, :], op=ALU.mult),
             reads=[s1b], writes=[s1b])
        k.op("dve", lambda e: e.tensor_scalar(out=s1[0:64, :], in0=s1[0:64, :], scalar1=-2.0, scalar2=1.0,
                                              op0=ALU.mult, op1=ALU.add), reads=[s1b], writes=[s1b])
        k.op("dve", lambda e: e.scalar_tensor_tensor(out=a1[0:64, :], in0=a1[0:64, :], scalar=4.0, in1=s1[0:64, :],
                                                     op0=ALU.mult, op1=ALU.mult), reads=[a1b, s1b], writes=[a1b])
        return a1, a1b

    for c in range(32):
        ze, zeb = tmp()
        k.dma("sp", [(ze[0:33, :], zemb_d[:, c * 512:(c + 1) * 512])], writes=[zeb])
        wt, wtb = tmp()
        k.dma("sp", [(wt[:, :].rearrange("p (n c) -> p n c", n=4), win_d[:, c * 4:(c + 1) * 4, :])], writes=[wtb])
        ps, psb = bank("m", [6, 7])
        k.op("pe", lambda e, ps=ps, ze=ze: e.matmul(ps[0:64, :], lhsT=W1, rhs=ze[0:33, :], start=True, stop=True),
             reads=[zeb, cb], writes=[psb])
        h, hb = sin4(ps, psb, 0)
        for li, W in ((1, W2), (2, W3)):
            ps, psb = bank("m", [6, 7])
            k.op("pe", lambda e, ps=ps, h=h, W=W: e.matmul(ps[0:64, :], lhsT=W, rhs=h[0:64, :], start=True, stop=True),
                 reads=[hb, cb], writes=[psb])
            h, hb = sin4(ps, psb, li)
        ps4, ps4b = bank("m", [6, 7])
        for n2l in range(4):
            for d in range(2):
                k.op("pe", lambda e, ps4=ps4, h=h, n2l=n2l, d=d: e.matmul(
                    ps4[d * 64:(d + 1) * 64, n2l * 128:(n2l + 1) * 128],
                    lhsT=h[0:64, n2l * 128 + d * 64:n2l * 128 + (d + 1) * 64], rhs=W4[:, d, :],
                    start=True, stop=True, skip_group_check=True), reads=[hb, cb], writes=[ps4b])
        kc, kcb = tmp()
        k.op("dve", lambda e, kc=kc, ps4=ps4, wt=wt: e.tensor_tensor(out=kc[:, :], in0=ps4[:, :], in1=wt[:, :], op=ALU.mult),
             reads=[ps4b, wtb], writes=[kcb])
        k.op("act", lambda e, kc=kc, c=c: e.activation(
            out=KC[:, :, c * 4:(c + 1) * 4], in_=kc[:, :].rearrange("p (n c) -> p c n", n=4), func=AF.Copy),
            reads=[kcb], writes=[KCb])
        sq, sqb = tmp()
        k.op("pool", lambda e, kc=kc, sq=sq: e.tensor_tensor(out=sq[:, :], in0=kc[:, :], in1=kc[:, :], op=ALU.mult),
             reads=[kcb], writes=[sqb])
        rd, rdb = tmp()
        k.op("dve", lambda e, sq=sq, rd=rd: e.tensor_reduce(
            out=rd[:, 0:128], in_=sq[:, :].rearrange("p (n c) -> p c n", n=4), axis=mybir.AxisListType.X, op=ALU.add),
            reads=[sqb], writes=[rdb])
        k.op("pool", lambda e, rd=rd: e.tensor_tensor(out=ACC[:], in0=ACC[:], in1=rd[:, 0:128], op=ALU.add),
             reads=[rdb, ACCb], writes=[ACCb])
    ps, psb = bank("m", [6, 7])
    k.op("pe", lambda e, ps=ps: e.matmul(ps[:, 0:1], lhsT=ACC[:], rhs=ones_f[:, 0:1], start=True, stop=True),
         reads=[ACCb, SMb], writes=[psb])
    k.op("act", lambda e, ps=ps: e.activation(out=SM[:, 8:9], in_=ps[:, 0:1], func=AF.Sqrt, bias=SM[:, 1:2], scale=1.0),
         reads=[psb, SMb], writes=[SMb])
    k.op("dve", lambda e: e.reciprocal(out=SM[:, 8:9], in_=SM[:, 8:9]), reads=[SMb], writes=[SMb])

    Bre = k.sb("Bre", [128, 4, 128], BF16)
    Bim = k.sb("Bim", [128, 4, 128], BF16)
    Hre = k.sb("Hre", [128, 512], F32)
    Him = k.sb("Him", [128, 512], F32)
    Yre = k.sb("Yre", [128, 4, 128], BF16)
    Yim = k.sb("Yim", [128, 4, 128], BF16)
    Qre = k.sb("Qre", [128, 4, 128], BF16)
    Qim = k.sb("Qim", [128, 4, 128], BF16)
    Bb, Hb, Yb, Qb = Buf("B"), Buf("H"), Buf("Y"), Buf("Q")
    TWre2, TWim2 = TW[:, 0], TW[:, 1]

    def cmul_from_psum(A, Ab, sign, outre, outim, outb, pr):
        A4 = A[:, :].rearrange("p (c r k) -> p c r k", c=2, r=2)
        Are, Aim = A4[:, :, 0, :], A4[:, :, 1, :]
        t1, t1b = tmp()
        t2, t2b = tmp()
        v = lambda t: t[:, 0:256].rearrange("p (c k) -> p c k", c=2)
        k.op("dve", lambda e: e.tensor_tensor(out=v(t1), in0=Are, in1=TWre2, op=ALU.mult), reads=[Ab, cb], writes=[t1b])
        k.op("dve", lambda e: e.tensor_tensor(out=v(t2), in0=Aim, in1=TWim2, op=ALU.mult), reads=[Ab, cb], writes=[t2b])
        k.op("pool", lambda e: e.tensor_tensor(out=outre[:, pr * 2:pr * 2 + 2, :], in0=v(t1), in1=v(t2),
                                               op=(ALU.subtract if sign > 0 else ALU.add)),
             reads=[t1b, t2b], writes=[outb])
        t3, t3b = tmp()
        t4, t4b = tmp()
        k.op("dve", lambda e: e.tensor_tensor(out=v(t3), in0=Aim, in1=TWre2, op=ALU.mult), reads=[Ab, cb], writes=[t3b])
        k.op("dve", lambda e: e.tensor_tensor(out=v(t4), in0=Are, in1=TWim2, op=ALU.mult), reads=[Ab, cb], writes=[t4b])
        k.op("pool", lambda e: e.tensor_tensor(out=outim[:, pr * 2:pr * 2 + 2, :], in0=v(t3), in1=v(t4),
                                               op=(ALU.add if sign > 0 else ALU.subtract)),
             reads=[t3b, t4b], writes=[outb])

    def fft_fwd(src, srcb, krows, ch0, xre, xreb, xim, ximb):
        for pr in range(2):
            A, Ab = bank("A", [0, 1])
            for cl in range(2):
                ch = ch0 + pr * 2 + cl
                k.op("pe", lambda e, A=A, cl=cl, ch=ch: e.matmul(
                    A[:, cl * 256:(cl + 1) * 256], lhsT=src[0:krows, ch, :], rhs=Fcat[0:krows, :],
                    start=True, stop=True, skip_group_check=True), reads=[srcb, dftb], writes=[Ab])
            cmul_from_psum(A, Ab, +1, Bre, Bim, Bb, pr)
        bre = Bre[:, :, :].rearrange("p c k -> p (c k)")
        bim = Bim[:, :, :].rearrange("p c k -> p (c k)")
        k.op("pe", lambda e: e.matmul(xre[:, :], lhsT=Fre, rhs=bre, start=True, stop=False), reads=[Bb, dftb], writes=[xreb])
        k.op("pe", lambda e: e.matmul(xre[:, :], lhsT=nFim, rhs=bim, start=False, stop=True), reads=[Bb, dftb], writes=[xreb])
        k.op("pe", lambda e: e.matmul(xim[:, :], lhsT=Fim, rhs=bre, start=True, stop=False), reads=[Bb, dftb], writes=[ximb])
        k.op("pe", lambda e: e.matmul(xim[:, :], lhsT=Fre, rhs=bim, start=False, stop=True), reads=[Bb, dftb], writes=[ximb])

    csb = Sink("cs")
    for g in range(32):
        ch0 = g * 4
        fft_fwd(KC, KCb, 128, ch0, PS[2], PSb[2], PS[3], PSb[3])
        k.op("act", lambda e: e.activation(out=Hre[:], in_=PS[2][:, :], func=AF.Copy), reads=[PSb[2]], writes=[Hb])
        k.op("act", lambda e: e.activation(out=Him[:], in_=PS[3][:, :], func=AF.Copy), reads=[PSb[3]], writes=[Hb])
        fft_fwd(ZC, ZCb, 64, ch0, PS[4], PSb[4], PS[5], PSb[5])
        t1, t1b = tmp()
        t2, t2b = tmp()
        k.op("dve", lambda e, t1=t1: e.tensor_tensor(out=t1[:, :], in0=PS[4][:, :], in1=Hre[:], op=ALU.mult),
             reads=[PSb[4], Hb], writes=[t1b])
        k.op("dve", lambda e, t2=t2: e.tensor_tensor(out=t2[:, :], in0=PS[5][:, :], in1=Him[:], op=ALU.mult),
             reads=[PSb[5], Hb], writes=[t2b])
        k.op("pool", lambda e, t1=t1, t2=t2: e.tensor_tensor(out=Yre[:, :, :].rearrange("p c k -> p (c k)"), in0=t1[:, :],
                                                             in1=t2[:, :], op=ALU.subtract), reads=[t1b, t2b], writes=[Yb])
        t3, t3b = tmp()
        t4, t4b = tmp()
        k.op("dve", lambda e, t3=t3: e.tensor_tensor(out=t3[:, :], in0=PS[4][:, :], in1=Him[:], op=ALU.mult),
             reads=[PSb[4], Hb], writes=[t3b])
        k.op("dve", lambda e, t4=t4: e.tensor_tensor(out=t4[:, :], in0=PS[5][:, :], in1=Hre[:], op=ALU.mult),
             reads=[PSb[5], Hb], writes=[t4b])
        k.op("pool", lambda e, t3=t3, t4=t4: e.tensor_tensor(out=Yim[:, :, :].rearrange("p c k -> p (c k)"), in0=t3[:, :],
                                                             in1=t4[:, :], op=ALU.add), reads=[t3b, t4b], writes=[Yb])
        for pr in range(2):
            Pk, Pkb = bank("A", [0, 1])
            for cl in range(2):
                c4 = pr * 2 + cl
                k.op("pe", lambda e, Pk=Pk, cl=cl, c4=c4: e.matmul(
                    Pk[:, cl * 256:(cl + 1) * 256], lhsT=Yre[:, c4, :], rhs=FcatI1, start=True, stop=False,
                    skip_group_check=True), reads=[Yb, dftb], writes=[Pkb])
                k.op("pe", lambda e, Pk=Pk, cl=cl, c4=c4: e.matmul(
                    Pk[:, cl * 256:(cl + 1) * 256], lhsT=Yim[:, c4, :], rhs=FcatI2, start=False, stop=True,
                    skip_group_check=True), reads=[Yb, dftb], writes=[Pkb])
            cmul_from_psum(Pk, Pkb, -1, Qre, Qim, Qb, pr)
        yo, yob = PS[6], PSb[6]
        k.op("pe", lambda e: e.matmul(yo[0:64, :], lhsT=DFT[:, 0:64], rhs=Qre[:, :, :].rearrange("p c k -> p (c k)"),
                                      start=True, stop=False), reads=[Qb, dftb], writes=[yob])
        k.op("pe", lambda e: e.matmul(yo[0:64, :], lhsT=DFT[:, 128:192], rhs=Qim[:, :, :].rearrange("p c k -> p (c k)"),
                                      start=False, stop=True), reads=[Qb, dftb], writes=[yob])
        ys, ysb = tmp()
        k.op("act", lambda e, ys=ys: e.activation(out=ys[0:64, :], in_=yo[0:64, :], func=AF.Copy, scale=1.0 / NFFT),
             reads=[yob], writes=[ysb])
        dst = bass.AP(tensor=cs, offset=ch0 * S, ap=[[128, 64], [S, 4], [1, 128]])
        k.dma("sp", [(dst, ys[0:64, :].rearrange("p (c k) -> p c k", c=4))], reads=[ysb], sembuf=ysb, sink=csb)

    k.wait_all("sp", [csb])
    k.dma("sp", [(IN[:, 0:S], cs.ap())], writes=[INb])
    ob = Sink("out")
    for s0 in range(0, S, 512):
        t, tb = tmp()
        k.op("dve", lambda e, t=t, s0=s0: e.tensor_scalar(out=t[:, :], in0=Z[:, s0:s0 + 512], scalar1=CF[:, 12:13],
                                                          scalar2=None, op0=ALU.mult), reads=[Zb, cb], writes=[tb])
        k.op("dve", lambda e, t=t, s0=s0: e.scalar_tensor_tensor(out=t[:, :], in0=IN[:, s0:s0 + 512], scalar=SM[:, 8:9],
                                                                 in1=t[:, :], op0=ALU.mult, op1=ALU.add),
             reads=[INb, SMb, tb], writes=[tb])
        k.op("pool", lambda e, t=t, s0=s0: e.tensor_tensor(out=t[:, :], in0=t[:, :], in1=X0[:, s0:s0 + 512], op=ALU.mult),
             reads=[tb, X0b], writes=[tb])
        k.dma("sp", [(y_d[:, s0:s0 + 512], t[:, :])], reads=[tb], sembuf=tb, sink=ob)
    k.wait_all("sp", [ob])
    k.close()
    return nc


def host_consts_LB(cq):
    f32 = np.float32
    L = S
    m = np.arange(NFFT)
    lag = np.where(m < L, m, NFFT - m)
    lag = np.where(m == L, 0, lag)
    t_all = np.linspace(0.0, 1.0, L, dtype=f32)
    t = t_all[lag]
    w = (f32(2.0 * math.pi / L) * lag.astype(f32)).astype(f32)
    fr = np.linspace(1e-4, 15, 16, dtype=f32)
    zf = np.concatenate([t[:, None], np.cos(fr[None, :] * w[:, None]), -np.sin(fr[None, :] * w[:, None])], axis=-1).astype(f32)
    zemb = np.ascontiguousarray(zf.reshape(128, 128, 33).transpose(2, 1, 0).reshape(33, NFFT))
    dmin, dmax = math.log(1e-2) / 0.3, math.log(1e-2) / 1.5
    deltas = np.abs(np.linspace(dmin, dmax, 512, dtype=f32))[cq * 128:(cq + 1) * 128]
    win = np.exp(-t[:, None] * deltas[None, :]).astype(f32)
    win[L] = 0.0
    win = np.ascontiguousarray(win.reshape(128, 128, 128))
    n = np.arange(128)
    ang = 2.0 * np.pi * np.outer(n, n) / 128.0
    fre, fim = np.cos(ang), -np.sin(ang)
    dftc = np.concatenate([fre, fim, fre, -fim], axis=1).astype(f32)
    ang2 = 2.0 * np.pi * np.outer(n, n) / NFFT
    tw = np.stack([np.cos(ang2), -np.sin(ang2)], 0).astype(f32)
    tw = np.ascontiguousarray(np.broadcast_to(tw[:, None], (2, 2, 128, 128)).transpose(2, 0, 1, 3))
    return zemb, win, dftc, tw


def prep_LB(inp, hco_all):
    maps = []
    cache = {}
    for core in range(NCORE):
        b, cq = core // 4, core % 4
        if cq not in cache:
            cache[cq] = host_consts_LB(cq)
        zemb, win, dftc, tw = cache[cq]
        hc3 = np.zeros((3, 128, S), np.float32)
        for part in range(3):
            for src in range(4):
                hc3[part, :, src * TOK:(src + 1) * TOK] = hco_all[b * 4 + src][part * 4 + cq]
        cf2 = np.zeros((128, 32), np.float32)
        cw = np.asarray(inp["d_conv_w"], np.float32)[0]
        cbias = np.asarray(inp["d_conv_b"], np.float32)[0]
        for part in range(3):
            sl = slice(part * 512 + cq * 128, part * 512 + (cq + 1) * 128)
            cf2[:, part * 4:part * 4 + 3] = cw[:, sl].T
            cf2[:, part * 4 + 3] = cbias[sl]
        cf2[:, 12] = np.asarray(inp["d_skip"], np.float32)[0][cq * 128:(cq + 1) * 128]
        mlp = np.zeros((64, 456), np.float32)
        mlp[0:33, 0:64] = np.asarray(inp["d_f_w1"], np.float32)[0]
        mlp[:, 64:128] = np.asarray(inp["d_f_w2"], np.float32)[0]
        mlp[:, 128:192] = np.asarray(inp["d_f_w3"], np.float32)[0]
        mlp[:, 192] = np.asarray(inp["d_f_b1"], np.float32)[0]
        mlp[:, 193] = np.asarray(inp["d_f_b2"], np.float32)[0]
        mlp[:, 194] = np.asarray(inp["d_f_b3"], np.float32)[0]
        mlp[:, 195] = np.asarray(inp["d_f_freq"], np.float32)[0]
        w4 = np.asarray(inp["d_f_w4"], np.float32)[0]
        mlp[:, 200:328] = w4[:, cq * 128:(cq + 1) * 128]
        mlp[:, 328:456] = w4[:, 512 + cq * 128:512 + (cq + 1) * 128]
        maps.append(dict(hc3=hc3, cf2=cf2, mlpw=mlp, zemb=zemb, win=win, dftc=dftc, tw=tw))
    return maps


def emit_attn1(P, C, QT, QTb, KT, KTb, VA, VAb, MIX, MIXb):
    k = P.k
    for jq in range(4):
        q0 = jq * 512
        for g in range(2):
            gs = slice(g * 64, (g + 1) * 64)
            ds_ = slice((1 - g) * 64, (2 - g) * 64)
            for hh in range(4):
                po, pob = P.bank("o", [4, 5])
                for kb in range(64):
                    ps, psb = P.bank("h1", [0, 1, 2, 3])
                    k.op("pe", lambda e, ps=ps, kb=kb, gs=gs, hh=hh, q0=q0: e.matmul(
                        ps[:, :], lhsT=KT[gs, kb * 128:(kb + 1) * 128], rhs=QT[gs, hh, q0:q0 + 512],
                        start=True, stop=True), reads=[KTb, QTb], writes=[psb])
                    pt, ptb = C.tmpbf()
                    k.op("act", lambda e, ps=ps, pt=pt: e.activation(out=pt[:, :], in_=ps[:, :], func=AF.Exp,
                                                                     scale=HD ** -0.5), reads=[psb], writes=[ptb])
                    k.op("pe", lambda e, po=po, pt=pt, kb=kb, g=g: e.matmul(
                        po[:, :], lhsT=VA[:, kb, g, :], rhs=pt[:, :], start=(kb == 0), stop=(kb == 63)),
                        reads=[VAb, ptb], writes=[pob])
                rd, rdb = C.tmp()
                k.op("dve", lambda e, rd=rd, po=po, ds_=ds_: e.reciprocal(out=rd[ds_, :], in_=po[ds_, :]),
                     reads=[pob], writes=[rdb])
                k.op("dve", lambda e, rd=rd, po=po, gs=gs, ds_=ds_, hh=hh, q0=q0: e.tensor_tensor(
                    out=MIX[gs, hh, q0:q0 + 512], in0=po[gs, :], in1=rd[ds_, :], op=ALU.mult),
                    reads=[pob, rdb], writes=[MIXb[jq]])


def build_LC():
    nc = bass.Bass("TRN2", target_bir_lowering=False)
    k = KB(nc)
    dt_in = lambda name, shape, dt=F32: nc.dram_tensor(name, list(shape), dt, kind="ExternalInput").ap()
    xT_d = dt_in("xT", [128, 8, 2048])
    mod_d = dt_in("modi", [128, 96])
    q_d = dt_in("q", [128, 4, 2048], BF16)
    k_d = dt_in("kk", [128, S], BF16)
    v_d = dt_in("v", [64, 128, 128], BF16)
    y_d = dt_in("yh", [128, 4, 2048])
    w1_d = dt_in("w1", [1024, DFFP])
    w3_d = dt_in("w3", [1024, DFFP])
    w2_d = dt_in("w2", [DFFP, 1024])
    wout_d = dt_in("wout", [1024, 1024])
    xo_d = nc.dram_tensor("xo", [128, 8, 2048], F32, kind="ExternalOutput").ap()

    P = Prog(nc, k)
    C = Common()
    C.tmp = RotTmp(k, "tf", 5, F32)
    C.tmpbf = RotTmp(k, "tb", 4, BF16)
    C.RS = k.sb("RS", [128, 512], F32)
    C.RSb = Buf("RS")
    C.cb = Buf("consts")
    C.modb = Buf("mod")
    C.ones_bf = k.sb("ones_bf", [128, 128], BF16)
    C.eps = k.sb("eps", [128, 1], F32)
    MOD = k.sb("mod", [128, 96], F32)
    XT = k.sb("XT", [128, 8, 2048], F32)
    AR = k.sb("AR", [128, 32768], BF16)
    MIXt = k.sb("MIX", [128, 8, 2048], BF16)
    VA = AR[:, 0:16384].rearrange("p (b g d) -> p b g d", b=64, g=2)
    KT = AR[:, 16384:24576]
    QT = AR[:, 24576:32768].rearrange("p (h t) -> p h t", h=4)
    Ht = AR[:, 0:16384].rearrange("p (c t) -> p c t", c=8)
    U = AR[:, 16384:16384 + 6 * 2048].rearrange("p (f t) -> p f t", f=6)

    tiles4 = [(0, 512), (512, 512), (1024, 512), (1536, 512)]
    XB = [Buf(f"x{j}") for j in range(4)]

    class XAct:
        tiles = tiles4
        b = XB

        def ap(self, c, j):
            return XT[:, c, j * 512:(j + 1) * 512]
    X4 = XAct()
    H4 = Act(Ht, tiles4, "h")
    Ub = [[Buf(f"u{f}_{j}") for j in range(4)] for f in range(6)]
    MIXb = [Buf(f"mix{j}") for j in range(4)]
    QTb, KTb, VAb = Buf("qt"), Buf("kt"), Buf("va")

    k.dma("sp", [(MOD[:], mod_d)], writes=[C.modb])
    k.dma("sp", [(QT, q_d)], writes=[QTb])
    k.dma("sp", [(KT, k_d)], writes=[KTb])
    k.op("dve", lambda e: e.memset(C.ones_bf[:], 1.0), writes=[C.cb])
    k.op("dve", lambda e: e.memset(C.eps[:], EPS), writes=[C.cb])
    k.op("dve", lambda e: e.memset(VA[:, :, 0, 64:128], 1.0), writes=[VAb])
    k.op("dve", lambda e: e.memset(VA[:, :, 1, 0:64], 1.0), writes=[VAb])
    vsrc = v_d.rearrange("b p d -> p b d")
    k.dma("sp", [(VA[:, :, 0, 0:64], vsrc[:, :, 0:64]), (VA[:, :, 1, 64:128], vsrc[:, :, 64:128])], writes=[VAb])
    for j in range(4):
        k.dma("sp", [(XT[:, :, j * 512:(j + 1) * 512], xT_d[:, :, j * 512:(j + 1) * 512])], writes=[XB[j]])
    for j in range(4):
        k.dma("pool", [(MIXt[:, 4:8, j * 512:(j + 1) * 512], y_d[:, :, j * 512:(j + 1) * 512])], writes=[MIXb[j]])

    def mcol(i, kind):
        return MOD[:, (i * 3 + kind) * 8:(i * 3 + kind + 1) * 8]

    emit_attn1(P, C, QT, QTb, KT, KTb, VA, VAb, MIXt, MIXb)
    emit_outproj(P, C, X4, lambda fc, c0, w: MIXt[:, fc, c0:c0 + w], MIXb, wout_d, mcol(1, 2))
    k.barrier()
    emit_adaln(P, X4, H4, MOD[:, 72 + 16:72 + 24], mcol(2, 0), C)
    emit_ffn(P, X4, H4, U, Ub, w1_d, w3_d, w2_d, mcol(2, 2), C)
    ob = Sink("out")
    for j in range(4):
        k.dma("sp", [(xo_d[:, :, j * 512:(j + 1) * 512], XT[:, :, j * 512:(j + 1) * 512])], reads=[XB[j]],
              sembuf=XB[j], sink=ob)
    k.wait_all("sp", [ob])
    k.close()
    return nc


def prep_LC(inp, la_res, lb_res, shared):
    maps = []
    for core in range(NCORE):
        b, qd = core // 4, core % 4
        kk = np.concatenate([la_res[b * 4 + s]["ko"] for s in range(4)], axis=1)
        v = np.concatenate([la_res[b * 4 + s]["vo"] for s in range(4)], axis=0)
        yh = np.stack([lb_res[b * 4 + cq]["yT"][:, qd * TOK:(qd + 1) * TOK] for cq in range(4)], axis=1)
        maps.append(dict(xT=la_res[core]["xo"], modi=la_res[core]["modo"], q=la_res[core]["qo"],
                         kk=np.ascontiguousarray(kk), v=np.ascontiguousarray(v), yh=np.ascontiguousarray(yh),
                         w1=shared["w1"][1, 1], w3=shared["w3"][1, 1], w2=shared["w2"][1, 1], wout=shared["wout"][1]))
    return maps


_CACHE = {}


FUSED = True


def kernel(**inputs):
    inp = {kk: np.asarray(v) for kk, v in inputs.items()}
    cores = list(range(NCORE))
    if FUSED:
        if "fused" not in _CACHE:
            _CACHE["fused"] = build_fused()
        maps = prep_fused(inp)
        res = run_bass_kernel_spmd(_CACHE["fused"], maps, core_ids=cores).results
        out = np.zeros((2, S, D), np.float32)
        for core in cores:
            b, qd = core // 4, core % 4
            xo = np.asarray(res[core]["xo"], np.float32)
            out[b, qd * TOK:(qd + 1) * TOK] = xo.transpose(2, 1, 0).reshape(TOK, D)
        return out
    if "la" not in _CACHE:
        _CACHE["la"] = build_LA(True)
        _CACHE["lb"] = build_LB()
        _CACHE["lc"] = build_LC()
    la_maps = prep_LA(inp)
    la = run_bass_kernel_spmd(_CACHE["la"], la_maps, core_ids=cores).results
    lb_maps = prep_LB(inp, [la[c]["hco"] for c in cores])
    lb = run_bass_kernel_spmd(_CACHE["lb"], lb_maps, core_ids=cores).results
    lc_maps = prep_LC(inp, la, lb, la_maps[0])
    lc = run_bass_kernel_spmd(_CACHE["lc"], lc_maps, core_ids=cores).results
    out = np.zeros((2, S, D), np.float32)
    for core in cores:
        b, qd = core // 4, core % 4
        xo = np.asarray(lc[core]["xo"], np.float32)
        out[b, qd * TOK:(qd + 1) * TOK] = xo.transpose(2, 1, 0).reshape(TOK, D)
    return out


U32 = mybir.dt.uint32


def kb_gather(k, dst_ap, src_dram_ap, idx_ap, reads=(), writes=()):
    sbf = writes[0]
    if sbf.dsem is None:
        sbf.dsem = {}
        sbf.dcnt = {}
    if True not in sbf.dsem:
        sbf.dsem[True] = ("d", id(sbf), True)
        sbf.dcnt[True] = 0
        k.sems[sbf.dsem[True]] = k._newsem(f"d_{k.nsem}")
    k._wait("pool", k._deps(reads, writes))
    k.nc.gpsimd.indirect_dma_start(out=dst_ap, out_offset=None, in_=src_dram_ap,
                                   in_offset=bass.IndirectOffsetOnAxis(ap=idx_ap, axis=0)
                                   ).then_inc(k.sems[sbf.dsem[True]], 16)
    sbf.dcnt[True] += 16
    tok = (sbf.dsem[True], sbf.dcnt[True])
    k._commit(tok, reads, writes)
    return tok


class CC:
    n = 0

    def __init__(self, k, groups, fake):
        self.k, self.groups, self.fake = k, groups, fake
        self.pool_sems = [] if fake else [k._newsem(f"cc{i}") for i in range(16)]
        self.alltoks = {}

    def allgather(self, src_t, dst_t, reads, dstb, sinks=(), sl=None):
        k = self.k
        sap = src_t.ap() if sl is None else src_t.ap()[sl]
        dap = dst_t.ap() if sl is None else dst_t.ap()[sl]
        rows = sap.shape[0]
        k.wait_all("sp" if self.fake else "pool", list(sinks))
        if self.fake:
            pairs = [(dap[r * rows:(r + 1) * rows, :], sap) for r in range(4)]
            k.dma("sp", pairs, reads=reads, writes=[dstb])
            return
        CC.n += 1
        key = ("cc", CC.n)
        k.sems[key] = self.pool_sems.pop(0)
        k._wait("pool", k._deps(reads, [dstb]))
        k.nc.gpsimd.collective_compute("AllGather", ALU.bypass, replica_groups=self.groups,
                                       ins=[sap.opt()], outs=[dap.opt()]).then_inc(k.sems[key])
        k._commit((key, 1), reads, [dstb])
        self.alltoks.setdefault(id(dstb), Sink("cc")).toks[key] = 1

    def sink(self, dstb):
        return self.alltoks.get(id(dstb), Sink("none"))


def emit_filter(P, C, F, kc_s, kcsb):
    k = P.k
    tmp, MLP, SM = C.tmp, F.MLP, F.SM
    cb = C.cb
    W1, W2, W3 = MLP[0:33, 0:64], MLP[:, 64:128], MLP[:, 128:192]
    W4 = MLP[:, 200:456].rearrange("p (d c) -> p d c", d=2)
    SMb, ACC, ACCb = F.SMb, F.ACC, F.ACCb
    k.op("dve", lambda e: e.memset(SM[:, 0:1], math.pi / 2), writes=[SMb])
    k.op("dve", lambda e: e.memset(SM[:, 1:2], EPS), writes=[SMb])
    k.op("dve", lambda e: e.tensor_scalar(out=SM[0:64, 2:3], in0=MLP[:, 195:196], scalar1=0.25, scalar2=None, op0=ALU.mult),
         reads=[cb], writes=[SMb])
    for i in range(3):
        k.op("dve", lambda e, i=i: e.tensor_tensor(out=SM[0:64, 3 + i:4 + i], in0=MLP[:, 192 + i:193 + i], in1=SM[0:64, 2:3],
                                                   op=ALU.mult), reads=[cb, SMb], writes=[SMb])
    k.op("dve", lambda e: e.memset(ACC[:], 0.0), writes=[ACCb])

    def sin4(ps, psb, li, out, outb):
        s1, s1b = tmp()
        k.op("act", lambda e: e.activation(out=s1[0:64, :], in_=ps[0:64, :], func=AF.Sin, bias=SM[0:64, 3 + li:4 + li],
                                           scale=SM[0:64, 2:3]), reads=[psb, SMb], writes=[s1b])
        a1, a1b = tmp()
        k.op("act", lambda e: e.activation(out=a1[0:64, :], in_=ps[0:64, :], func=AF.Abs, bias=SM[0:64, 3 + li:4 + li],
                                           scale=SM[0:64, 2:3]), reads=[psb, SMb], writes=[a1b])
        k.op("act", lambda e: e.activation(out=a1[0:64, :], in_=a1[0:64, :], func=AF.Sin, bias=SM[0:64, 0:1], scale=-1.0),
             reads=[a1b, SMb], writes=[a1b])
        k.op("dve", lambda e: e.tensor_tensor(out=a1[0:64, :], in0=a1[0:64, :], in1=s1[0:64, :], op=ALU.mult),
             reads=[a1b, s1b], writes=[a1b])
        k.op("dve", lambda e: e.tensor_tensor(out=s1[0:64, :], in0=s1[0:64, :], in1=s1[0:64, :], op=ALU.mult),
             reads=[s1b], writes=[s1b])
        k.op("dve", lambda e: e.tensor_scalar(out=s1[0:64, :], in0=s1[0:64, :], scalar1=-2.0, scalar2=1.0,
                                              op0=ALU.mult, op1=ALU.add), reads=[s1b], writes=[s1b])
        k.op("dve", lambda e: e.scalar_tensor_tensor(out=out[0:64, :], in0=a1[0:64, :], scalar=4.0, in1=s1[0:64, :],
                                                     op0=ALU.mult, op1=ALU.mult), reads=[a1b, s1b], writes=[outb])
        return out, outb

    for c in range(32):
        yield c
        ze, zeb = tmp()
        k.dma("sp", [(ze[0:33, :], F.zemb_d[:, c * 512:(c + 1) * 512])], writes=[zeb])
        ps, psb = P.bank("m", [6, 7])
        k.op("pe", lambda e, ps=ps, ze=ze: e.matmul(ps[0:64, :], lhsT=W1, rhs=ze[0:33, :], start=True, stop=True),
             reads=[zeb, cb], writes=[psb])
        h, hb = sin4(ps, psb, 0, F.HF[0], F.HFb[0])
        yield c
        for li, W in ((1, W2), (2, W3)):
            ps, psb = P.bank("m", [6, 7])
            k.op("pe", lambda e, ps=ps, h=h, W=W: e.matmul(ps[0:64, :], lhsT=W, rhs=h[0:64, :], start=True, stop=True),
                 reads=[hb, cb], writes=[psb])
            h, hb = sin4(ps, psb, li, F.HF[li % 2], F.HFb[li % 2])
            yield c
        wt, wtb = tmp()
        k.dma("sp", [(wt[:, :].rearrange("p (n c) -> p n c", n=4), F.win_d[:, c * 4:(c + 1) * 4, :])], writes=[wtb])
        ps4, ps4b = P.bank("m", [6, 7])
        for n2l in range(4):
            for d in range(2):
                k.op("pe", lambda e, ps4=ps4, h=h, n2l=n2l, d=d: e.matmul(
                    ps4[d * 64:(d + 1) * 64, n2l * 128:(n2l + 1) * 128],
                    lhsT=h[0:64, n2l * 128 + d * 64:n2l * 128 + (d + 1) * 64], rhs=W4[:, d, :],
                    start=True, stop=True, skip_group_check=True), reads=[hb, cb], writes=[ps4b])
        kc, kcb = tmp()
        k.op("dve", lambda e, kc=kc, ps4=ps4, wt=wt: e.tensor_tensor(out=kc[:, :], in0=ps4[:, :], in1=wt[:, :], op=ALU.mult),
             reads=[ps4b, wtb], writes=[kcb])
        st, stb = C.tmpbf()
        k.op("act", lambda e, kc=kc, st=st: e.activation(out=st[:, :], in_=kc[:, :], func=AF.Copy), reads=[kcb], writes=[stb])
        k.dma("sp", [(kc_s.ap()[:, c * 4:(c + 1) * 4, :], st[:, :].rearrange("p (n c) -> p n c", n=4))], reads=[stb],
              sembuf=stb, sink=kcsb)
        sq, sqb = tmp()
        k.op("dve", lambda e, kc=kc, sq=sq: e.tensor_tensor(out=sq[:, :], in0=kc[:, :], in1=kc[:, :], op=ALU.mult),
             reads=[kcb], writes=[sqb])
        rd, rdb = tmp()
        k.op("dve", lambda e, sq=sq, rd=rd: e.tensor_reduce(
            out=rd[:, 0:128], in_=sq[:, :].rearrange("p (n c) -> p c n", n=4), axis=mybir.AxisListType.X, op=ALU.add),
            reads=[sqb], writes=[rdb])
        k.op("dve", lambda e, rd=rd: e.tensor_tensor(out=ACC[:], in0=ACC[:], in1=rd[:, 0:128], op=ALU.add),
             reads=[rdb, ACCb], writes=[ACCb])
    ps, psb = P.bank("m", [6, 7])
    k.op("pe", lambda e, ps=ps: e.matmul(ps[:, 0:128], lhsT=C.ones_f[:, :], rhs=ACC[:], start=True, stop=True),
         reads=[ACCb, cb], writes=[psb])
    k.op("act", lambda e, ps=ps: e.activation(out=F.RSrow[:], in_=ps[:, 0:128], func=AF.Sqrt, bias=SM[:, 1:2], scale=1.0),
         reads=[psb, SMb], writes=[F.RSb])
    k.op("dve", lambda e: e.reciprocal(out=F.RSrow[:], in_=F.RSrow[:]), reads=[F.RSb], writes=[F.RSb])


def emit_fftconv(P, C, F, KC, KCb, zs2, zs2b, c_src, csink, V, after_group=None):
    k = P.k
    tmp = C.tmp
    DFT, TW, dftb, cb = F.DFT, F.TW, F.dftb, C.cb
    Fre, Fim, nFim = DFT[:, 0:128], DFT[:, 128:256], DFT[:, 384:512]
    Fcat, FcatI2, FcatI1 = DFT[:, 0:256], DFT[:, 128:384], DFT[:, 256:512]
    TWre2, TWim2 = TW[:, 0], TW[:, 1]
    Bre, Bim, Yre, Yim, Qre, Qim, Hre, Him = V.Bre, V.Bim, V.Yre, V.Yim, V.Qre, V.Qim, V.Hre, V.Him
    Bb, Hb, Yb, Qb = Buf("B"), Buf("H"), Buf("Y"), Buf("Q")
    PS, PSb = P.PS, P.PSb

    def cmul_from_psum(A, Ab, sign, outre, outim, outb, pr):
        A4 = A[:, :].rearrange("p (c r k) -> p c r k", c=2, r=2)
        Are, Aim = A4[:, :, 0, :], A4[:, :, 1, :]
        v = lambda t: t[:, 0:256].rearrange("p (c k) -> p c k", c=2)
        t1, t1b = tmp()
        t2, t2b = tmp()
        k.op("dve", lambda e: e.tensor_tensor(out=v(t1), in0=Are, in1=TWre2, op=ALU.mult), reads=[Ab, cb], writes=[t1b])
        k.op("dve", lambda e: e.tensor_tensor(out=v(t2), in0=Aim, in1=TWim2, op=ALU.mult), reads=[Ab, cb], writes=[t2b])
        k.op("pool", lambda e: e.tensor_tensor(out=outre[:, pr * 2:pr * 2 + 2, :], in0=v(t1), in1=v(t2),
                                               op=(ALU.subtract if sign > 0 else ALU.add)),
             reads=[t1b, t2b], writes=[outb])
        t3, t3b = tmp()
        t4, t4b = tmp()
        k.op("dve", lambda e: e.tensor_tensor(out=v(t3), in0=Aim, in1=TWre2, op=ALU.mult), reads=[Ab, cb], writes=[t3b])
        k.op("dve", lambda e: e.tensor_tensor(out=v(t4), in0=Are, in1=TWim2, op=ALU.mult), reads=[Ab, cb], writes=[t4b])
        k.op("pool", lambda e: e.tensor_tensor(out=outim[:, pr * 2:pr * 2 + 2, :], in0=v(t3), in1=v(t4),
                                               op=(ALU.add if sign > 0 else ALU.subtract)),
             reads=[t3b, t4b], writes=[outb])

    def fft_fwd(lhs_of, srcb, krows, xre, xreb, xim, ximb):
        for pr in range(2):
            A, Ab = P.bank("A", [0, 1])
            for cl in range(2):
                c4 = pr * 2 + cl
                k.op("pe", lambda e, A=A, cl=cl, c4=c4: e.matmul(
                    A[:, cl * 256:(cl + 1) * 256], lhsT=lhs_of(c4), rhs=Fcat[0:krows, :],
                    start=True, stop=True, skip_group_check=True), reads=[srcb, dftb], writes=[Ab])
            cmul_from_psum(A, Ab, +1, Bre, Bim, Bb, pr)
        bre = Bre.rearrange("p c k -> p (c k)")
        bim = Bim.rearrange("p c k -> p (c k)")
        k.op("pe", lambda e: e.matmul(xre[:, :], lhsT=Fre, rhs=bre, start=True, stop=False), reads=[Bb, dftb], writes=[xreb])
        k.op("pe", lambda e: e.matmul(xre[:, :], lhsT=nFim, rhs=bim, start=False, stop=True), reads=[Bb, dftb], writes=[xreb])
        k.op("pe", lambda e: e.matmul(xim[:, :], lhsT=Fim, rhs=bre, start=True, stop=False), reads=[Bb, dftb], writes=[ximb])
        k.op("pe", lambda e: e.matmul(xim[:, :], lhsT=Fre, rhs=bim, start=False, stop=True), reads=[Bb, dftb], writes=[ximb])

    for g in range(32):
        ch0 = g * 4
        ZC, ZCb = V.ZC[g % 2], V.ZCb[g % 2]
        src = bass.AP(tensor=zs2, offset=ch0 * S, ap=[[128, 64], [S, 4], [1, 128]])
        k.dma("sp", [(ZC[0:64], src)], reads=[zs2b], writes=[ZCb])
        fft_fwd(lambda c4, ch0=ch0: KC[:, :, ch0 + c4], KCb, 128, PS[2], PSb[2], PS[3], PSb[3])
        rsb_ap = F.RSrow[:, ch0:ch0 + 4].unsqueeze(2).broadcast_to([128, 4, 128])
        k.op("dve", lambda e, rsb_ap=rsb_ap: e.tensor_tensor(out=Hre.rearrange("p (c k) -> p c k", c=4),
                                                              in0=PS[2][:, :].rearrange("p (c k) -> p c k", c=4),
                                                              in1=rsb_ap, op=ALU.mult), reads=[PSb[2], F.RSb], writes=[Hb])
        k.op("dve", lambda e, rsb_ap=rsb_ap: e.tensor_tensor(out=Him.rearrange("p (c k) -> p c k", c=4),
                                                              in0=PS[3][:, :].rearrange("p (c k) -> p c k", c=4),
                                                              in1=rsb_ap, op=ALU.mult), reads=[PSb[3], F.RSb], writes=[Hb])
        fft_fwd(lambda c4, ZC=ZC: ZC[0:64, c4, :], ZCb, 64, PS[4], PSb[4], PS[5], PSb[5])
        t1, t1b = tmp()
        t2, t2b = tmp()
        k.op("dve", lambda e, t1=t1: e.tensor_tensor(out=t1[:, :], in0=PS[4][:, :], in1=Hre, op=ALU.mult),
             reads=[PSb[4], Hb], writes=[t1b])
        k.op("dve", lambda e, t2=t2: e.tensor_tensor(out=t2[:, :], in0=PS[5][:, :], in1=Him, op=ALU.mult),
             reads=[PSb[5], Hb], writes=[t2b])
        k.op("pool", lambda e, t1=t1, t2=t2: e.tensor_tensor(out=Yre.rearrange("p c k -> p (c k)"), in0=t1[:, :],
                                                             in1=t2[:, :], op=ALU.subtract), reads=[t1b, t2b], writes=[Yb])
        t3, t3b = tmp()
        t4, t4b = tmp()
        k.op("dve", lambda e, t3=t3: e.tensor_tensor(out=t3[:, :], in0=PS[4][:, :], in1=Him, op=ALU.mult),
             reads=[PSb[4], Hb], writes=[t3b])
        k.op("dve", lambda e, t4=t4: e.tensor_tensor(out=t4[:, :], in0=PS[5][:, :], in1=Hre, op=ALU.mult),
             reads=[PSb[5], Hb], writes=[t4b])
        k.op("pool", lambda e, t3=t3, t4=t4: e.tensor_tensor(out=Yim.rearrange("p c k -> p (c k)"), in0=t3[:, :],
                                                             in1=t4[:, :], op=ALU.add), reads=[t3b, t4b], writes=[Yb])
        for pr in range(2):
            Pk, Pkb = P.bank("A", [0, 1])
            for cl in range(2):
                c4 = pr * 2 + cl
                k.op("pe", lambda e, Pk=Pk, cl=cl, c4=c4: e.matmul(
                    Pk[:, cl * 256:(cl + 1) * 256], lhsT=Yre[:, c4, :], rhs=FcatI1, start=True, stop=False,
                    skip_group_check=True), reads=[Yb, dftb], writes=[Pkb])
                k.op("pe", lambda e, Pk=Pk, cl=cl, c4=c4: e.matmul(
                    Pk[:, cl * 256:(cl + 1) * 256], lhsT=Yim[:, c4, :], rhs=FcatI2, start=False, stop=True,
                    skip_group_check=True), reads=[Yb, dftb], writes=[Pkb])
            cmul_from_psum(Pk, Pkb, -1, Qre, Qim, Qb, pr)
        yo, yob = PS[6], PSb[6]
        k.op("pe", lambda e: e.matmul(yo[0:64, :], lhsT=DFT[:, 0:64], rhs=Qre.rearrange("p c k -> p (c k)"),
                                      start=True, stop=False), reads=[Qb, dftb], writes=[yob])
        k.op("pe", lambda e: e.matmul(yo[0:64, :], lhsT=DFT[:, 128:192], rhs=Qim.rearrange("p c k -> p (c k)"),
                                      start=False, stop=True), reads=[Qb, dftb], writes=[yob])
        ys, ysb = tmp()
        k.op("act", lambda e, ys=ys: e.activation(out=ys[0:64, :], in_=yo[0:64, :], func=AF.Copy, scale=1.0 / NFFT),
             reads=[yob], writes=[ysb])
        dst = bass.AP(tensor=c_src, offset=ch0 * S, ap=[[128, 64], [S, 4], [1, 128]])
        k.dma("sp", [(dst, ys[0:64, :].rearrange("p (c k) -> p c k", c=4))], reads=[ysb], sembuf=ysb,
              sink=(csink[g // 4] if isinstance(csink, list) else csink))
        if after_group is not None:
            after_group(g)


def emit_attn1f(P, C, q_s, KT, KTb, VB, VBb, QS, QSb, MIX, MIXb, SK=3):
    k = P.k
    its = [(jq, g, hh, kb) for jq in range(4) for g in range(2) for hh in range(4) for kb in range(64)]
    st = {}
    cur = {}

    def stage_a(i):
        jq, g, hh, kb = its[i]
        q0 = jq * 512
        QT, QTb = QS[jq % 2], QSb[jq % 2]
        if (g, hh, kb) == (0, 0, 0):
            k.dma("sp", [(QT, q_s.ap()[:, :, q0:q0 + 512])], writes=[QTb])
        gs = slice(g * 64, (g + 1) * 64)
        ps, psb = P.bank("att", [0, 1, 2, 3, 6, 7])
        k.op("pe", lambda e: e.matmul(ps[:, :], lhsT=KT[gs, kb * 128:(kb + 1) * 128], rhs=QT[gs, hh, :],
                                      start=True, stop=True), reads=[KTb, QTb], writes=[psb])
        pt, ptb = C.tmpbf()
        k.op("act", lambda e: e.activation(out=pt[:, :], in_=ps[:, :], func=AF.Exp, scale=HD ** -0.5),
             reads=[psb], writes=[ptb])
        st[i] = (pt, ptb)

    def stage_b(i):
        jq, g, hh, kb = its[i]
        q0 = jq * 512
        gs = slice(g * 64, (g + 1) * 64)
        ds_ = slice((1 - g) * 64, (2 - g) * 64)
        if kb == 0:
            cur["po"] = P.bank("o", [4, 5])
        po, pob = cur["po"]
        pt, ptb = st.pop(i)
        k.op("pe", lambda e: e.matmul(po[:, :], lhsT=VB[:, kb, g * 64:g * 64 + 128], rhs=pt[:, :],
                                      start=(kb == 0), stop=(kb == 63)), reads=[VBb, ptb], writes=[pob])
        if kb == 63:
            rd, rdb = C.tmp()
            k.op("dve", lambda e: e.reciprocal(out=rd[ds_, :], in_=po[ds_, :]), reads=[pob], writes=[rdb])
            k.op("dve", lambda e: e.tensor_tensor(out=MIX[gs, hh, q0:q0 + 512], in0=po[gs, :], in1=rd[ds_, :],
                                                  op=ALU.mult), reads=[pob, rdb], writes=[MIXb[jq]])

    n = len(its)
    for t in range(n + SK):
        if t < n:
            stage_a(t)
        if t - SK >= 0:
            stage_b(t - SK)


def emit_attn1g(P, C, q_s, KT, KTb, VB, VBb, QS, QSb, MIX, MIXb, extra_p):
    k = P.k
    pb = list(zip(C.tmpbf.t, C.tmpbf.b)) + list(extra_p)
    assert len(pb) >= 8
    pi = [0]
    blocks = [(jq, g, kb) for jq in range(4) for g in range(2) for kb in range(64)]
    st = {}

    def stage_a(bi):
        jq, g, kb = blocks[bi]
        q0 = jq * 512
        QT, QTb = QS[jq % 2], QSb[jq % 2]
        if (g, kb) == (0, 0):
            k.dma("sp", [(QT, q_s.ap()[:, :, q0:q0 + 512])], writes=[QTb])
        gs = slice(g * 64, (g + 1) * 64)
        for hh in range(4):
            ps, psb = P.PS[hh], P.PSb[hh]
            k.op("pe", lambda e: e.matmul(ps[:, :], lhsT=KT[gs, kb * 128:(kb + 1) * 128], rhs=QT[gs, hh, :],
                                          start=True, stop=True), reads=[KTb, QTb], writes=[psb])
            pt, ptb = pb[pi[0] % len(pb)]
            pi[0] += 1
            k.op("act", lambda e: e.activation(out=pt[:, :], in_=ps[:, :], func=AF.Exp, scale=HD ** -0.5),
                 reads=[psb], writes=[ptb])
            st[(bi, hh)] = (pt, ptb)

    def stage_b(bi):
        jq, g, kb = blocks[bi]
        q0 = jq * 512
        gs = slice(g * 64, (g + 1) * 64)
        ds_ = slice((1 - g) * 64, (2 - g) * 64)
        for hh in range(4):
            po, pob = P.PS[4 + hh], P.PSb[4 + hh]
            pt, ptb = st.pop((bi, hh))
            k.op("pe", lambda e: e.matmul(po[:, :], lhsT=VB[:, kb, g * 64:g * 64 + 128], rhs=pt[:, :],
                                          start=(kb == 0), stop=(kb == 63)), reads=[VBb, ptb], writes=[pob])
            if kb == 63:
                rd, rdb = C.tmp()
                k.op("dve", lambda e: e.reciprocal(out=rd[ds_, :], in_=po[ds_, :]), reads=[pob], writes=[rdb])
                k.op("dve", lambda e: e.tensor_tensor(out=MIX[gs, hh, q0:q0 + 512], in0=po[gs, :], in1=rd[ds_, :],
                                                      op=ALU.mult), reads=[pob, rdb], writes=[MIXb[jq]])

    n = len(blocks)
    for t in range(n + 1):
        if t < n:
            stage_a(t)
        if t >= 1:
            stage_b(t - 1)


def build_fused(fake_ag=False, ncore=8, stop_after=None):
    nc = bass.Bass("TRN2", target_bir_lowering=False)
    k = KB(nc)
    groups = [[0, 1, 2, 3], [4, 5, 6, 7]] if ncore == 8 else [[0, 1, 2, 3]]
    cc = CC(k, groups, fake_ag)
    dt_in = lambda name, shape, dt=F32: nc.dram_tensor(name, list(shape), dt, kind="ExternalInput").ap()
    dram = lambda name, shape, dt=F32: nc.dram_tensor(name, list(shape), dt, kind="Internal")
    xT_d = dt_in("xT", [128, 8, 2048])
    xH_d = dt_in("xH", [128, 8, 256])
    cf_d = dt_in("cf", [128, 400])
    oh_d = dt_in("oh", [32, 512])
    vm_d = dt_in("vm", [128, 512])
    rope_d = dt_in("rope", [128, 2, 2048])
    idx_d = dt_in("idx", [128, 24], U32)
    mlp_d = dt_in("mlpw", [64, 456])
    zemb_d = dt_in("zemb", [33, NFFT])
    win_d = dt_in("win", [128, 128, 128])
    dft_d = dt_in("dftc", [128, 512])
    tw_d = dt_in("tw", [128, 2, 2, 128])
    wmod_d = dt_in("wmod", [2, 1024, 9216])
    w1_d = dt_in("w1", [2, 2, 1024, DFFP])
    w3_d = dt_in("w3", [2, 2, 1024, DFFP])
    w2_d = dt_in("w2", [2, 2, DFFP, 1024])
    win_w = dt_in("win_w", [2, 1024, INC])
    wout_d = dt_in("wout", [2, 1024, 1024])
    xo_d = nc.dram_tensor("xo", [128, 8, 2048], F32, kind="ExternalOutput").ap()
    scr = dram("scr", [128, 8 * 512])
    q_s = dram("q_s", [128, 4, 2048], BF16)
    k_src, k_g = dram("k_src", [128, 2048], BF16), dram("k_g", [512, 2048], BF16)
    v_src, v_g = dram("v_src", [2048, 128], BF16), dram("v_g", [8192, 128], BF16)
    hb_src, hb_g = dram("hb_src", [128, 24]), dram("hb_g", [512, 24])
    z_src, z_g = dram("z_src", [4, 128, 2048], BF16), dram("z_g", [4, 512, 2048], BF16)
    zf_s = dram("zf_s", [128, 4, 2048])
    x0_s = dram("x0_s", [128, 4, 2048])
    zs2 = dram("zs2", [128, S], BF16)
    kc_s = dram("kc_s", [128, 128, 128], BF16)
    c_src, c_g = dram("c_src", [8, 16, S]), dram("c_g", [8, 64, S])

    P = Prog(nc, k)
    C = Common()
    C.tmp = RotTmp(k, "tf", 6, F32)
    C.tmpbf = RotTmp(k, "tb", 5, BF16)
    C.RS = k.sb("RS", [128, 512], F32)
    C.RSb = Buf("RS")
    C.cb = Buf("consts")
    C.modb = Buf("mod")
    CF = k.sb("CF", [128, 400], F32)
    IDX = k.sb("IDX", [128, 24], U32)
    C.ones_bf = k.sb("ones_bf", [128, 128], BF16)
    C.bd_bf = k.sb("bd_bf", [128, 128], BF16)
    C.ones_f = k.sb("ones_f", [128, 128], F32)
    C.eps = k.sb("eps", [128, 1], F32)
    condbf = k.sb("condbf", [128, 8], BF16)
    MODV = k.sb("modv", [128, 2, 72], F32)
    AV = k.sb("av", [128, 2, 3, 8], F32)
    EXS = k.sb("exs", [128, 8], F32)
    F = Common()
    F.MLP = k.sb("MLP", [64, 456], F32)
    F.DFT = k.sb("DFT", [128, 512], BF16)
    F.TW = k.sb("TW", [128, 2, 2, 128], F32)
    F.ACC = k.sb("ACC", [128, 128], F32)
    F.SM = k.sb("SM", [128, 16], F32)
    F.RSrow = k.sb("RSrow", [128, 128], F32)
    F.SMb, F.ACCb, F.RSb, F.dftb = Buf("SM"), Buf("ACC"), Buf("RSr"), Buf("dft")
    F.zemb_d, F.win_d = zemb_d, win_d
    F.HF = [k.sb(f"HF{i}", [64, 512], F32) for i in range(2)]
    F.HFb = [Buf("HF0"), Buf("HF1")]
    HBS = k.sb("HBS", [128, 12, 2], F32)
    HG = k.sb("HG", [128, 4, 12, 2], F32)
    HLR = k.sb("HLR", [128, 2, 12], F32)
    XT = k.sb("XT", [128, 8, 2048], F32)
    AR = k.sb("AR", [128, 41984], BF16)

    cT = CF[:, 0:8]
    flags = CF[:, 8:10]
    normg = CF[:, 10:58].rearrange("p (l i c) -> p l i c", l=2, i=3)
    bmod = CF[:, 58:202].rearrange("p (l m) -> p l m", l=2)
    convw = CF[:, 202:214].rearrange("p (c t) -> p c t", c=4)
    convb = CF[:, 214:218]
    gq, gk = CF[:, 218:220], CF[:, 220:222]
    sink = CF[:, 222:230]
    relrep = CF[:, 232:240]
    gq1, gk1 = CF[:, 240:241], CF[:, 241:242]
    dcw = CF[:, 244:280].rearrange("p (c t) -> p c t", c=12)
    dcb = CF[:, 280:292]
    skipv = CF[:, 292:296]
    selL, selR = CF[:, 296:300], CF[:, 300:304]

    Ht = AR[:, 0:18432].rearrange("p (c t) -> p c t", c=8)
    UQ = AR[:, 18432:33792]
    MXf = AR[:, 33792:41984].bitcast(F32)
    XH = MXf[:, 0:2048].rearrange("p (c t) -> p c t", c=8)
    MIXHI0 = AR[:, 33792:41984].rearrange("p (c t) -> p c t", c=4)
    U = UQ[:, 0:6 * 2304].rearrange("p (f t) -> p f t", f=6)
    QT0 = UQ[:, 0:8192].rearrange("p (h t) -> p h t", h=4)
    KT0 = UQ[:, 8192:8192 + 2304]
    VA0 = UQ[:, 10496:10496 + 4608].rearrange("p (b g d) -> p b g d", b=18, g=2)
    S_t = UQ[:, 10496:10496 + 4100].bitcast(F32)
    C.E8 = UQ[:, 0:8192].bitcast(F32).rearrange("p (h m) -> p h m", h=8)
    EXPB = P.WBf[:, :, :].rearrange("p a (b q) -> p (a b) q", q=128).rearrange("p (h b) q -> p h b q", h=8)
    OH = AR[0:32, 0:1024].bitcast(F32)
    VM = AR[:, 1024:2048].bitcast(F32)

    tiles5 = [(0, 512), (512, 512), (1024, 512), (1536, 512), (2048, 256)]
    tiles4 = tiles5[:4]
    XB = [Buf(f"x{j}") for j in range(5)]

    class XAct:
        def __init__(self, tiles):
            self.tiles = tiles
            self.b = XB[:len(tiles)]

        def ap(self, c, j):
            if j < 4:
                return XT[:, c, j * 512:(j + 1) * 512]
            return XH[:, c, :]
    X5, X4 = XAct(tiles5), XAct(tiles4)
    H5 = Act(Ht, tiles5, "h")
    H4 = Act(Ht, tiles4, "h")
    H4.b = H5.b[:4]
    Ub = [[Buf(f"u{f}_{j}") for j in range(5)] for f in range(6)]

    k.dma("sp", [(CF[:], cf_d), (F.MLP[:], mlp_d), (F.TW[:], tw_d), (IDX[:], idx_d)], writes=[C.cb])
    k.dma("pool", [(F.DFT[:], dft_d)], writes=[F.dftb])
    for j in range(4):
        k.dma("sp", [(XT[:, :, j * 512:(j + 1) * 512], xT_d[:, :, j * 512:(j + 1) * 512])], writes=[XB[j]])
    k.dma("sp", [(XH, xH_d)], writes=[XB[4]])
    cst = Buf("cst")
    k.op("dve", lambda e: e.memset(C.ones_bf[:], 1.0), writes=[cst])
    k.op("dve", lambda e: e.memset(C.ones_f[:], 1.0), writes=[cst])
    k.op("dve", lambda e: e.memset(C.eps[:], EPS), writes=[cst])
    k.op("dve", lambda e: e.memset(C.bd_bf[:], 0.0), writes=[cst])
    k.op("dve", lambda e: e.memset(C.bd_bf[0:64, 0:64], 1.0), writes=[cst])
    k.op("dve", lambda e: e.memset(C.bd_bf[64:128, 64:128], 1.0), writes=[cst])
    condb = Buf("cond")
    k.op("act", lambda e: e.activation(out=condbf[:], in_=cT, func=AF.Silu), reads=[C.cb], writes=[condb])
    k.op("act", lambda e: e.activation(out=EXS[:], in_=sink, func=AF.Exp), reads=[C.cb], writes=[cst])
    k.op("dve", lambda e: e.tensor_copy(out=C.eps[:], in_=C.eps[:]), reads=[cst, C.cb], writes=[C.cb])

    def layer_mod(l):
        emit_mod(P, condbf, condb, wmod_d[l], bmod[:, l, :], MODV[:, l, :], C.modb, AV[:, l], normg[:, l])

    def mcol(l, i, kind):
        return MODV[:, l, (i * 3 + kind) * 8:(i * 3 + kind + 1) * 8]

    def mix_ap0(fc, c0, w):
        if fc < 4:
            return Ht[:, fc, c0:c0 + w]
        return MIXHI0[:, fc - 4, c0:c0 + w]

    def finish(extra=()):
        ob = Sink("out")
        k.wait_all("sp", list(extra))
        for j in range(4):
            k.dma("sp", [(xo_d[:, :, j * 512:(j + 1) * 512], XT[:, :, j * 512:(j + 1) * 512])], reads=[XB[j]],
                  sembuf=XB[j], sink=ob)
        k.wait_all("sp", [ob])
        k.close()
        return nc

    kcsb = Sink("kcs")
    fgen = emit_filter(P, C, F, kc_s, kcsb)
    next(fgen)
    hcnt = [0]

    def fhook():
        hcnt[0] += 1
        next(fgen, None)

    layer_mod(0)
    emit_adaln(P, X5, H5, AV[:, 0, 0], mcol(0, 0, 0), C)
    emit_ffn(P, X5, H5, U, Ub, w1_d[0, 0], w3_d[0, 0], w2_d[0, 0], mcol(0, 0, 2), C, hook=fhook)
    for _ in fgen:
        pass
    k.barrier()
    ohb = Buf("oh")
    k.dma("sp", [(OH, oh_d), (VM, vm_d)], writes=[ohb])
    k.op("dve", lambda e: e.tensor_copy(out=C.eps[:], in_=C.eps[:]), reads=[ohb, C.cb], writes=[C.cb])
    scrb = emit_expb(P, C, relrep, C.cb, OH, VM, scr, EXPB, P.WBb[0])
    k.barrier(dbufs=[scrb])
    emit_adaln(P, X5, H5, AV[:, 0, 1], mcol(0, 1, 0), C)
    k.barrier()
    MIXb = [Buf(f"mix{j}") for j in range(4)]
    Sb = Buf("S")
    emit_conv0(P, C, H5, mix_ap0, MIXb, win_w[0], convw, convb, flags, S_t, Sb)
    k.barrier()
    QTb = [Buf(f"qt{j}") for j in range(4)]
    KTb = [Buf(f"kt{j}") for j in range(5)]
    VAb = Buf("va")
    k.op("dve", lambda e: e.memset(VA0[:, 0:16, 0, 64:128], 1.0), writes=[VAb])
    k.op("dve", lambda e: e.memset(VA0[:, 0:16, 1, 0:64], 1.0), writes=[VAb])
    emit_qkv0(P, C, H5, win_w[0], QT0, QTb, KT0, KTb, VA0, VAb, gq, gk, flags)
    k.barrier()
    emit_attn0(P, C, QT0, QTb, KT0, KTb, VA0, VAb, EXPB, P.WBb[0], EXS, Ht, MIXb)
    for i in (1, 2):
        P.WBb[i].r = dict(P.WBb[0].r)
        P.WBb[i].w = P.WBb[0].w
    emit_outproj(P, C, X4, mix_ap0, MIXb, wout_d[0], mcol(0, 1, 2))
    k.barrier()
    emit_adaln(P, X4, H4, AV[:, 0, 2], mcol(0, 2, 0), C)
    emit_ffn(P, X4, H4, U, Ub, w1_d[0, 1], w3_d[0, 1], w2_d[0, 1], mcol(0, 2, 2), C)

    if stop_after == "l0":
        return finish([kcsb])

    layer_mod(1)
    emit_adaln(P, X4, H4, AV[:, 1, 0], mcol(1, 0, 0), C)
    emit_ffn(P, X4, H4, U, Ub, w1_d[1, 0], w3_d[1, 0], w2_d[1, 0], mcol(1, 0, 2), C)
    k.barrier()
    ROPE = UQ[:, 0:8192].bitcast(F32).rearrange("p (a t) -> p a t", a=2)
    HB = AR[:, 26624:26624 + 12300].bitcast(F32).rearrange("p (a t) -> p a t", a=3)
    ropeb, HBb = Buf("rope"), Buf("HB")
    k.dma("sp", [(ROPE, rope_d)], writes=[ropeb])
    emit_adaln(P, X4, H4, AV[:, 1, 1], mcol(1, 1, 0), C)
    W1L = win_w[1]
    hbsb = Buf("hbs")
    for pr in range(6):
        wa, wab = P.load_wa(W1L[:, 768 + pr * 256:768 + (pr + 1) * 256])
        for cl in range(2):
            ch = pr * 2 + cl
            ps, psb = P.bank("h1", [0, 1, 2, 3])
            for ci, col in enumerate((0, 2047)):
                for c in range(8):
                    k.op("pe", lambda e, c=c, ps=ps, wa=wa, cl=cl, ci=ci, col=col: e.matmul(
                        ps[:, ci:ci + 1], lhsT=wa[:, c, cl * 128:(cl + 1) * 128], rhs=Ht[:, c, col:col + 1],
                        start=(c == 0), stop=(c == 7)), reads=[wab] + H4.b, writes=[psb])
            k.op("act", lambda e, ps=ps, ch=ch: e.activation(out=HBS[:, ch, :], in_=ps[:, 0:2], func=AF.Copy),
                 reads=[psb], writes=[hbsb])
    hbsrcb, hbgb, hgb = Buf("hbsrc"), Buf("hbg"), Buf("hg")
    k.dma("sp", [(hb_src.ap(), HBS[:].rearrange("p c t -> p (c t)"))], reads=[hbsb], writes=[hbsrcb])
    cc.allgather(hb_src, hb_g, [hbsrcb], hbgb)
    k.dma("sp", [(HG[:].rearrange("p r c t -> p r (c t)"), hb_g.ap().rearrange("(r p) t -> p r t", p=128))],
          reads=[hbgb], writes=[hgb])
    outs = Sink("l1outs")
    for pr in range(3):
        wa, wab = P.load_wa(W1L[:, pr * 256:(pr + 1) * 256])
        for cl in range(2):
            if pr == 2 and cl == 1:
                break
            for j in range(4):
                c0, w = H4.tiles[j]
                ps, psb = P.bank("h1", [0, 1, 2, 3])
                emit_inproj_tile(P, H4, j, wa, wab, cl, ps, psb)
                qn, qnb = C.tmp()
                emit_qknorm(P, C, ps, psb, w, (gq1 if pr < 2 else gk1), qn[:, 0:w], qnb)
                st, stb = C.tmpbf()
                emit_rope(P, C, qn, qnb, ROPE[:, 0, :], ROPE[:, 1, :], ropeb, c0, w, st[:, 0:w], stb)
                dst = q_s.ap()[:, pr * 2 + cl, c0:c0 + w] if pr < 2 else k_src.ap()[:, c0:c0 + w]
                k.dma("sp", [(dst, st[:, 0:w])], reads=[stb], sembuf=stb, sink=outs)
        if pr == 2:
            for tb in range(16):
                j = tb // 4
                ps, psb = P.bank("o", [4, 5])
                for c in range(8):
                    k.op("pe", lambda e, c=c, tb=tb, ps=ps, wa=wa: e.matmul(
                        ps[:, 0:128], lhsT=Ht[:, c, tb * 128:(tb + 1) * 128], rhs=wa[:, c, 128:256],
                        start=(c == 0), stop=(c == 7)), reads=[wab, H4.b[j]], writes=[psb])
                st, stb = C.tmpbf()
                k.op("act", lambda e, ps=ps, st=st: e.activation(out=st[:, 0:128], in_=ps[:, 0:128], func=AF.Copy),
                     reads=[psb], writes=[stb])
                k.dma("sp", [(v_src.ap()[tb * 128:(tb + 1) * 128, :], st[:, 0:128])], reads=[stb], sembuf=stb, sink=outs)
    kgb, vgb = Buf("kg"), Buf("vg")
    if stop_after == "qkv":
        return finish([outs, kcsb])
    cc.allgather(k_src, k_g, [], kgb, sinks=[outs])
    cc.allgather(v_src, v_g, [], vgb, sinks=[outs])
    for side, sel, colx in ((0, selL, 1), (1, selR, 0)):
        k.op("dve", lambda e, side=side, sel=sel, colx=colx: e.tensor_scalar(
            out=HLR[:, side, :], in0=HG[:, 0, :, colx], scalar1=sel[:, 0:1], scalar2=None, op0=ALU.mult),
            reads=[hgb, C.cb], writes=[hgb])
        for r in range(1, 4):
            k.op("dve", lambda e, side=side, sel=sel, colx=colx, r=r: e.scalar_tensor_tensor(
                out=HLR[:, side, :], in0=HG[:, r, :, colx], scalar=sel[:, r:r + 1], in1=HLR[:, side, :],
                op0=ALU.mult, op1=ALU.add), reads=[hgb, C.cb], writes=[hgb])

    zouts = Sink("zouts")
    for hh in range(2):
        was = [P.load_wa(W1L[:, 768 + part * 512 + hh * 256: 768 + part * 512 + (hh + 1) * 256]) for part in range(3)]
        for cl in range(2):
            cch = hh * 2 + cl
            for part in range(3):
                wa, wab = was[part]
                for j in range(4):
                    c0, w = H4.tiles[j]
                    ps, psb = P.bank("h1", [0, 1, 2, 3])
                    emit_inproj_tile(P, H4, j, wa, wab, cl, ps, psb)
                    k.op("act", lambda e, ps=ps, part=part, c0=c0, w=w: e.activation(
                        out=HB[:, part, 1 + c0:1 + c0 + w], in_=ps[:, 0:w], func=AF.Copy), reads=[psb], writes=[HBb])
                ch12 = part * 4 + cch
                k.op("act", lambda e, part=part, ch12=ch12: e.activation(out=HB[:, part, 0:1], in_=HLR[:, 0, ch12:ch12 + 1],
                                                                         func=AF.Copy), reads=[hgb], writes=[HBb])
                k.op("act", lambda e, part=part, ch12=ch12: e.activation(out=HB[:, part, 2049:2050],
                                                                         in_=HLR[:, 1, ch12:ch12 + 1], func=AF.Copy),
                     reads=[hgb], writes=[HBb])
            for j in range(4):
                c0, w = H4.tiles[j]
                cv = []
                for part in range(3):
                    ch12 = part * 4 + cch
                    t, tb_ = C.tmp()
                    k.op("dve", lambda e, t=t, part=part, c0=c0, ch12=ch12: e.tensor_scalar(
                        out=t[:, :], in0=HB[:, part, 1 + c0:1 + c0 + 512], scalar1=dcw[:, ch12, 1:2],
                        scalar2=dcb[:, ch12:ch12 + 1], op0=ALU.mult, op1=ALU.add), reads=[HBb, C.cb], writes=[tb_])
                    k.op("dve", lambda e, t=t, part=part, c0=c0, ch12=ch12: e.scalar_tensor_tensor(
                        out=t[:, :], in0=HB[:, part, c0:c0 + 512], scalar=dcw[:, ch12, 0:1], in1=t[:, :],
                        op0=ALU.mult, op1=ALU.add), reads=[HBb, C.cb, tb_], writes=[tb_])
                    k.op("dve", lambda e, t=t, part=part, c0=c0, ch12=ch12: e.scalar_tensor_tensor(
                        out=t[:, :], in0=HB[:, part, 2 + c0:2 + c0 + 512], scalar=dcw[:, ch12, 2:3], in1=t[:, :],
                        op0=ALU.mult, op1=ALU.add), reads=[HBb, C.cb, tb_], writes=[tb_])
                    cv.append((t, tb_))
                (x0t, x0b), (x1t, x1b), (vt, vb_) = cv
                k.dma("sp", [(x0_s.ap()[:, cch, c0:c0 + 512], x0t[:, :])], reads=[x0b], sembuf=x0b, sink=zouts)
                k.op("dve", lambda e, x1t=x1t, vt=vt: e.tensor_tensor(out=x1t[:, :], in0=x1t[:, :], in1=vt[:, :], op=ALU.mult),
                     reads=[x1b, vb_], writes=[x1b])
                k.dma("sp", [(zf_s.ap()[:, cch, c0:c0 + 512], x1t[:, :])], reads=[x1b], sembuf=x1b, sink=zouts)
                zb16, zb16b = C.tmpbf()
                k.op("act", lambda e, x1t=x1t, zb16=zb16: e.activation(out=zb16[:, :], in_=x1t[:, :], func=AF.Copy),
                     reads=[x1b], writes=[zb16b])
                k.dma("sp", [(z_src.ap()[cch, :, c0:c0 + 512], zb16[:, :])], reads=[zb16b],
                      sembuf=zb16b, sink=zouts)
    zgb = Buf("zg")
    for cch in range(4):
        cc.allgather(z_src, z_g, [], zgb, sinks=[zouts], sl=cch)

    if stop_after == "l1pre":
        return finish([kgb, vgb, zgb, kcsb])

    k.barrier()
    VB = AR[:, 0:12288].rearrange("p (b d) -> p b d", b=64)
    KT = AR[:, 12288:20480]
    QS = [AR[:, 20480 + i * 2048:20480 + (i + 1) * 2048].rearrange("p (h t) -> p h t", h=4) for i in range(2)]
    QSb = [Buf("qs0"), Buf("qs1")]
    MIX = AR[:, 24576:40960].rearrange("p (c t) -> p c t", c=8)
    MIXb = [Buf(f"mixb{j}") for j in range(4)]
    KTb, VBb = Buf("KT"), Buf("VB")
    k.op("dve", lambda e: e.memset(VB[:, :, 64:128], 1.0), writes=[VBb])
    k.dma("sp", [(KT.rearrange("p (r t) -> p r t", r=4), k_g.ap().rearrange("(r p) t -> p r t", p=128))],
          reads=[kgb], writes=[KTb])
    vsrc = v_g.ap().rearrange("(b p) d -> p b d", p=128)
    k.dma("sp", [(VB[:, :, 0:64], vsrc[:, :, 0:64]), (VB[:, :, 128:192], vsrc[:, :, 64:128])], reads=[vgb], writes=[VBb])
    extra_p = [(AR[:, 40960 + i * 512:40960 + (i + 1) * 512], Buf(f"xp{i}")) for i in range(2)]
    spare_t, spare_b = C.tmp.t.pop(), C.tmp.b.pop()
    C.tmp.i = 0
    extra_p.append((spare_t[:, :].bitcast(BF16)[:, 0:512], spare_b))
    emit_attn1g(P, C, q_s, KT, KTb, VB, VBb, QS, QSb, MIX, MIXb, extra_p)

    if stop_after == "attn":
        return finish([zgb, kcsb] + MIXb)

    k.barrier()
    V = Common()
    KC = AR[:, 0:16384].rearrange("p (n c) -> p n c", n=128)
    Zg = AR[:, 0:8192]
    V.ZC = [AR[:, 16384 + i * 512:16384 + (i + 1) * 512].rearrange("p (c k) -> p c k", c=4) for i in range(2)]
    V.ZCb = [Buf("zc0"), Buf("zc1")]
    six = [AR[:, 17408 + i * 512:17408 + (i + 1) * 512].rearrange("p (c k) -> p c k", c=4) for i in range(6)]
    V.Bre, V.Bim, V.Yre, V.Yim, V.Qre, V.Qim = six
    V.Hre = AR[:, 20480:21504].bitcast(F32)
    V.Him = AR[:, 21504:22528].bitcast(F32)
    Zgb, zs2b, KCb = Buf("Zg"), Buf("zs2"), Buf("KC")
    zg_rows = z_g.ap().rearrange("c r t -> (c r) t")
    k.wait_all("pool", [cc.sink(zgb)])
    for r in range(4):
        kb_gather(k, Zg[:, r * 2048:(r + 1) * 2048], zg_rows, IDX[:, r:r + 1], reads=[zgb, C.cb], writes=[Zgb])
    k.dma("sp", [(zs2.ap(), Zg)], reads=[Zgb], writes=[zs2b])
    k.wait_all("sp", [kcsb])
    k.dma("sp", [(KC, kc_s.ap())], reads=[zs2b], writes=[KCb])
    csink = [Sink(f"csrc{i}") for i in range(8)]
    cgb = Buf("cg")

    def after_group(g):
        if g % 4 == 3:
            cc.allgather(c_src, c_g, [], cgb, sinks=[csink[g // 4]], sl=g // 4)
    emit_fftconv(P, C, F, KC, KCb, zs2, zs2b, c_src, csink, V, after_group=after_group)

    if stop_after == "fft":
        return finish([cgb] + MIXb)

    cg_rows = c_g.ap().rearrange("i r (a t) -> (i r a) t", t=512)
    k.wait_all("pool", [cc.sink(cgb)])
    for cq in range(4):
        for tq in range(4):
            c0 = tq * 512
            ct, ctb = C.tmp()
            kb_gather(k, ct[:, :], cg_rows, IDX[:, 4 + cq * 4 + tq:5 + cq * 4 + tq], reads=[cgb, C.cb], writes=[ctb])
            zt, ztb = C.tmp()
            k.dma("sp", [(zt[:, :], zf_s.ap()[:, cq, c0:c0 + 512])], writes=[ztb])
            xt_, xtb = C.tmp()
            k.dma("sp", [(xt_[:, :], x0_s.ap()[:, cq, c0:c0 + 512])], writes=[xtb])
            k.op("dve", lambda e, zt=zt, ct=ct, cq=cq: e.scalar_tensor_tensor(
                out=zt[:, :], in0=zt[:, :], scalar=skipv[:, cq:cq + 1], in1=ct[:, :], op0=ALU.mult, op1=ALU.add),
                reads=[ztb, ctb, C.cb], writes=[ztb])
            k.op("dve", lambda e, zt=zt, xt_=xt_, cq=cq, c0=c0: e.tensor_tensor(
                out=MIX[:, 4 + cq, c0:c0 + 512], in0=zt[:, :], in1=xt_[:, :], op=ALU.mult),
                reads=[ztb, xtb], writes=[MIXb[tq]])

    emit_outproj(P, C, X4, lambda fc, c0, w: MIX[:, fc, c0:c0 + w], MIXb, wout_d[1], mcol(1, 1, 2))
    k.barrier()
    Ht2 = AR[:, 0:16384].rearrange("p (c t) -> p c t", c=8)
    U2 = AR[:, 16384:16384 + 12288].rearrange("p (f t) -> p f t", f=6)
    H42 = Act(Ht2, tiles4, "h2")
    Ub2 = [[Buf(f"v{f}_{j}") for j in range(4)] for f in range(6)]
    emit_adaln(P, X4, H42, AV[:, 1, 2], mcol(1, 2, 0), C)
    emit_ffn(P, X4, H42, U2, Ub2, w1_d[1, 1], w3_d[1, 1], w2_d[1, 1], mcol(1, 2, 2), C)
    ob = Sink("out")
    for j in range(4):
        k.dma("sp", [(xo_d[:, :, j * 512:(j + 1) * 512], XT[:, :, j * 512:(j + 1) * 512])], reads=[XB[j]],
              sembuf=XB[j], sink=ob)
    k.wait_all("sp", [ob])
    k.close()
    return nc


def prep_fused(inp, ncore=NCORE):
    base = prep_LA(inp)
    maps = []
    cache = {}
    dcw = np.asarray(inp["d_conv_w"], np.float32)[0]
    dcb = np.asarray(inp["d_conv_b"], np.float32)[0]
    skip = np.asarray(inp["d_skip"], np.float32)[0]
    w4 = np.asarray(inp["d_f_w4"], np.float32)[0]
    p = np.arange(128)
    for core in range(ncore):
        b, qd = core // 4, core % 4
        cq = qd
        if cq not in cache:
            cache[cq] = host_consts_LB(cq)
        zemb, win, dftc, tw = cache[cq]
        m = dict(base[core])
        cf = np.zeros((128, 400), np.float32)
        cf[:, 0:244] = m.pop("cf")[:, 0:244]
        cf[:, 244:280] = fm(dcw).transpose(0, 2, 1).reshape(128, 36)
        cf[:, 280:292] = fm(dcb)
        cf[:, 292:296] = fm(skip)
        if qd > 0:
            cf[:, 296 + qd - 1] = 1.0
        if qd < 3:
            cf[:, 300 + qd + 1] = 1.0
        idx = np.zeros((128, 24), np.uint32)
        for r in range(4):
            idx[:, r] = cq * 512 + r * 128 + p
        for c2 in range(4):
            for tq in range(4):
                idx[:, 4 + c2 * 4 + tq] = ((p // 16) * 64 + c2 * 16 + (p % 16)) * 16 + qd * 4 + tq
        mlp = np.zeros((64, 456), np.float32)
        mlp[0:33, 0:64] = np.asarray(inp["d_f_w1"], np.float32)[0]
        mlp[:, 64:128] = np.asarray(inp["d_f_w2"], np.float32)[0]
        mlp[:, 128:192] = np.asarray(inp["d_f_w3"], np.float32)[0]
        mlp[:, 192] = np.asarray(inp["d_f_b1"], np.float32)[0]
        mlp[:, 193] = np.asarray(inp["d_f_b2"], np.float32)[0]
        mlp[:, 194] = np.asarray(inp["d_f_b3"], np.float32)[0]
        mlp[:, 195] = np.asarray(inp["d_f_freq"], np.float32)[0]
        mlp[:, 200:328] = w4[:, cq * 128:(cq + 1) * 128]
        mlp[:, 328:456] = w4[:, 512 + cq * 128:512 + (cq + 1) * 128]
        m["win_w"] = m.pop("win")
        m.update(cf=cf, idx=idx, mlpw=mlp, zemb=zemb, win=win, dftc=dftc, tw=tw)
        maps.append(m)
    return maps
```

```python
import math
import numpy as np
import concourse.bass as bass
import concourse.mybir as mybir
from concourse.bass_utils import run_bass_kernel_spmd

AF = mybir.ActivationFunctionType
ALU = mybir.AluOpType
F32 = mybir.dt.float32
BF16 = mybir.dt.bfloat16
EPOCH = 12000

D = 1024
S = 8192
TOK = 2048
NCORE = 8
DFF = 2752
DFFP = 2816
NF = 22
HD = 64
EPS = 1e-6
INC = 2304


class Buf:
    __slots__ = ("name", "w", "r", "dsem", "dcnt")

    def __init__(self, name=""):
        self.name = name
        self.w = None
        self.r = {}
        self.dsem = None
        self.dcnt = 0


class Sink:
    def __init__(self, name=""):
        self.name = name
        self.toks = {}


class KB:
    def __init__(self, nc):
        self.nc = nc
        self.engs = {"pe": nc.tensor, "act": nc.scalar, "dve": nc.vector,
                     "pool": nc.gpsimd, "sp": nc.sync}
        self.cnt = {e: 0 for e in self.engs}
        self.sems = {}
        self.waited = {e: {} for e in self.engs}
        self.nsem = 0
        self._stack = []

    def _newsem(self, name):
        cm = self.nc.semaphore(name)
        h = cm.__enter__()
        self._stack.append(cm)
        self.nsem += 1
        return h

    def sb(self, name, shape, dt):
        cm = self.nc.sbuf_tensor(name, shape, dt)
        t = cm.__enter__()
        self._stack.append(cm)
        return t

    def ps(self, name, shape, dt=F32):
        cm = self.nc.psum_tensor(name, shape, dt)
        t = cm.__enter__()
        self._stack.append(cm)
        return t

    def close(self):
        while self._stack:
            self._stack.pop().__exit__(None, None, None)

    def _engsem(self, eng):
        key = (eng, self.cnt[eng] // EPOCH)
        if key not in self.sems:
            self.sems[key] = self._newsem(f"s_{eng}_{key[1]}")
        return key

    def _deps(self, reads, writes):
        deps = {}

        def add(k, v):
            if deps.get(k, 0) < v:
                deps[k] = v
        for b in reads:
            if b.w is not None:
                add(*b.w)
        for b in writes:
            if b.w is not None:
                add(*b.w)
            for kk, v in b.r.items():
                add(kk, v)
        return deps

    def _wait(self, eng, deps):
        w = self.waited[eng]
        e = self.engs[eng]
        for kk, v in deps.items():
            if w.get(kk, 0) >= v:
                continue
            e.wait_ge(self.sems[kk], v)
            w[kk] = v

    def _commit(self, tok, reads, writes):
        kk, v = tok
        for b in reads:
            if b.r.get(kk, 0) < v:
                b.r[kk] = v
        for b in writes:
            b.w = tok
            b.r = {}

    def op(self, eng, fn, reads=(), writes=()):
        deps = self._deps(reads, writes)
        if eng == "pe":
            deps = {kk: v for kk, v in deps.items() if kk[0] != "pe"}
        self._wait(eng, deps)
        key = self._engsem(eng)
        ins = fn(self.engs[eng])
        self.cnt[eng] += 1
        val = self.cnt[eng] - key[1] * EPOCH
        ins.then_inc(self.sems[key], 1)
        tok = (key, val)
        self._commit(tok, reads, writes)
        return tok

    def dma(self, q, pairs, reads=(), writes=(), sembuf=None, sink=None, **kw):
        sbf = sembuf or (writes[0] if writes else reads[0])
        if sbf.dsem is None:
            sbf.dsem = {}
            sbf.dcnt = {}
        sw = (q == "pool")
        if sw not in sbf.dsem:
            sbf.dsem[sw] = ("d", id(sbf), sw)
            sbf.dcnt[sw] = 0
            self.sems[sbf.dsem[sw]] = self._newsem(f"d_{self.nsem}")
        deps = self._deps(reads, writes)
        self._wait(q, deps)
        e = self.engs[q]
        for (o, i) in pairs:
            e.dma_start(out=o, in_=i, **kw).then_inc(self.sems[sbf.dsem[sw]], 16)
            sbf.dcnt[sw] += 16
        tok = (sbf.dsem[sw], sbf.dcnt[sw])
        self._commit(tok, reads, writes)
        if sink is not None and sink.toks.get(tok[0], 0) < tok[1]:
            sink.toks[tok[0]] = tok[1]
        return tok

    def wait_all(self, eng, bufs):
        deps = {}
        for b in bufs:
            d = dict(b.toks) if isinstance(b, Sink) else self._deps([b], [b])
            for kk, v in d.items():
                if deps.get(kk, 0) < v:
                    deps[kk] = v
        self._wait(eng, deps)

    def barrier(self, dbufs=()):
        deps = {}
        for e in ("pe", "act", "dve"):
            if self.cnt[e] == 0:
                continue
            ep = (self.cnt[e] - 1) // EPOCH
            deps[(e, ep)] = self.cnt[e] - ep * EPOCH
        for b in dbufs:
            for kk, v in self._deps([b], [b]).items():
                if deps.get(kk, 0) < v:
                    deps[kk] = v
        for e in ("pe", "act", "dve", "pool", "sp"):
            self._wait(e, dict(deps))


class Prog:
    def __init__(self, nc, k):
        self.nc = nc
        self.k = k
        self.WA = [k.sb(f"WA{i}", [128, 8, 256], BF16) for i in range(4)]
        self.WAb = [Buf(f"WA{i}") for i in range(4)]
        self.WBf = k.sb("WBf", [128, 3, 1024], F32)
        self.WB = [self.WBf[:, i, :].bitcast(BF16).rearrange("p (f n) -> p f n", f=2) for i in range(3)]
        self.WBb = [Buf(f"WB{i}") for i in range(3)]
        self.wa_i = 0
        self.PS = [k.ps(f"ps{i}", [128, 512]) for i in range(8)]
        self.PSb = [Buf(f"ps{i}") for i in range(8)]
        self.rr = {}

    def next_wa(self, parity=None):
        i = self.wa_i
        self.wa_i = (self.wa_i + 1) % 4
        return i

    def bank(self, group, banks):
        i = self.rr.get(group, 0)
        self.rr[group] = (i + 1) % len(banks)
        b = banks[i]
        return self.PS[b], self.PSb[b]

    def load_wa(self, src_ap):
        i = self.next_wa()
        self.k.dma("pool", [(self.WA[i][:], src_ap.rearrange("(c p) n -> p c n", p=128))], writes=[self.WAb[i]])
        return self.WA[i], self.WAb[i]

    def load_wb(self, i, src_ap):
        self.k.dma("pool", [(self.WB[i], src_ap.rearrange("(f p) n -> p f n", p=128))], writes=[self.WBb[i]])
        return self.WB[i], self.WBb[i]


def emit_mod(P, cond_bf, cond_b, wmod_l, bmod_l, modv, modb, a_out, normg_l):
    k = P.k
    ps, psb = P.PS[7], P.PSb[7]
    for ch in range(36):
        wa, wab = P.load_wa(wmod_l[:, ch * 256:(ch + 1) * 256])
        for cl in range(2):
            cc = ch * 2 + cl
            for kc in range(8):
                k.op("pe", lambda e, cc=cc, kc=kc, cl=cl, wa=wa: e.matmul(
                    ps[:, cc:cc + 1], lhsT=wa[:, kc, cl * 128:(cl + 1) * 128], rhs=cond_bf[:, kc:kc + 1],
                    start=(kc == 0), stop=(kc == 7)),
                    reads=[wab, cond_b], writes=[psb])
    k.op("dve", lambda e: e.tensor_tensor(out=modv[:], in0=ps[:, 0:72], in1=bmod_l, op=ALU.add),
         reads=[psb], writes=[modb])
    for i in range(3):
        k.op("dve", lambda e, i=i: e.scalar_tensor_tensor(
            out=a_out[:, i, :], in0=modv[:, (i * 3 + 1) * 8:(i * 3 + 2) * 8], scalar=1.0, in1=normg_l[:, i, :],
            op0=ALU.add, op1=ALU.mult), reads=[modb], writes=[modb])
    for i in (0, 2):
        k.op("dve", lambda e, i=i: e.tensor_scalar(
            out=modv[:, (i * 3 + 2) * 8:(i * 3 + 3) * 8], in0=modv[:, (i * 3 + 2) * 8:(i * 3 + 3) * 8],
            scalar1=0.5, scalar2=None, op0=ALU.mult), reads=[modb], writes=[modb])


class Act:
    def __init__(self, t, tiles, name):
        self.t = t
        self.tiles = tiles
        self.b = [Buf(f"{name}{j}") for j in range(len(tiles))]

    def ap(self, c, j):
        c0, w = self.tiles[j]
        return self.t[:, c, c0:c0 + w]


def emit_adaln(P, X, H, a_ap, shift_ap, C):
    k = P.k
    for j, (c0, w) in enumerate(X.tiles):
        ps, psb = P.PS[6], P.PSb[6]
        for c in range(8):
            sq, sqb = C.tmpbf()
            k.op("act", lambda e, c=c, j=j, w=w, sq=sq: e.activation(out=sq[:, 0:w], in_=X.ap(c, j), func=AF.Square),
                 reads=[X.b[j]], writes=[sqb])
            k.op("pe", lambda e, c=c, w=w, sq=sq: e.matmul(ps[:, 0:w], lhsT=C.ones_bf[:], rhs=sq[:, 0:w],
                                                     start=(c == 0), stop=(c == 7)),
                 reads=[sqb, C.cb], writes=[psb])
        k.op("act", lambda e, w=w: e.activation(out=C.RS[:, 0:w], in_=ps[:, 0:w], func=AF.Sqrt,
                                                bias=C.eps[:, 0:1], scale=1.0 / D),
             reads=[psb, C.cb], writes=[C.RSb])
        k.op("dve", lambda e, w=w: e.reciprocal(out=C.RS[:, 0:w], in_=C.RS[:, 0:w]), reads=[C.RSb], writes=[C.RSb])
        for c in range(8):
            tt, ttb = C.tmp()
            k.op("dve", lambda e, c=c, j=j, w=w, tt=tt: e.scalar_tensor_tensor(
                out=tt[:, 0:w], in0=X.ap(c, j), scalar=a_ap[:, c:c + 1], in1=C.RS[:, 0:w],
                op0=ALU.mult, op1=ALU.mult), reads=[X.b[j], C.RSb, C.modb], writes=[ttb])
            k.op("act", lambda e, c=c, j=j, w=w, tt=tt: e.activation(
                out=H.ap(c, j), in_=tt[:, 0:w], func=AF.Identity, bias=shift_ap[:, c:c + 1], scale=1.0),
                reads=[ttb, C.modb], writes=[H.b[j]])


def emit_ffn(P, X, H, U, Ub, w1, w3, w2, gate_ap, C, hook=None):
    k = P.k
    groups = [(0, 3), (3, 3), (6, 3), (9, 2)]
    ntile = len(X.tiles)
    for (p0, npair) in groups:
        for pl in range(npair):
            p = p0 + pl
            wa1, wa1b = P.load_wa(w1[:, p * 256:(p + 1) * 256])
            wa3, wa3b = P.load_wa(w3[:, p * 256:(p + 1) * 256])
            for fl in range(2):
                fu = pl * 2 + fl
                for j in range(ntile):
                    c0, w = X.tiles[j]
                    ps1, ps1b = P.bank("h1", [0, 1])
                    ps3, ps3b = P.bank("h3", [2, 3])
                    for c in range(8):
                        k.op("pe", lambda e, c=c, j=j, w=w, ps1=ps1, wa1=wa1, fl=fl: e.matmul(
                            ps1[:, 0:w], lhsT=wa1[:, c, fl * 128:(fl + 1) * 128], rhs=H.ap(c, j),
                            start=(c == 0), stop=(c == 7)), reads=[wa1b, H.b[j]], writes=[ps1b])
                    for c in range(8):
                        k.op("pe", lambda e, c=c, j=j, w=w, ps3=ps3, wa3=wa3, fl=fl: e.matmul(
                            ps3[:, 0:w], lhsT=wa3[:, c, fl * 128:(fl + 1) * 128], rhs=H.ap(c, j),
                            start=(c == 0), stop=(c == 7)), reads=[wa3b, H.b[j]], writes=[ps3b])
                    sl, slb = C.tmp()
                    k.op("act", lambda e, w=w, ps1=ps1, sl=sl: e.activation(out=sl[:, 0:w], in_=ps1[:, 0:w], func=AF.Silu),
                         reads=[ps1b], writes=[slb])
                    k.op("dve", lambda e, w=w, c0=c0, ps3=ps3, sl=sl, fu=fu: e.tensor_tensor(
                        out=U[:, fu, c0:c0 + w], in0=ps3[:, 0:w], in1=sl[:, 0:w], op=ALU.mult),
                        reads=[ps3b, slb], writes=[Ub[fu][j]])
                    if hook is not None:
                        hook()
        for pl in range(npair):
            P.load_wb(pl, w2[(p0 + pl) * 256:(p0 + pl + 1) * 256, :])
        nfu = npair * 2
        for c in range(8):
            for j in range(ntile):
                c0, w = X.tiles[j]
                pso, psob = P.bank("o", [4, 5])
                for fu in range(nfu):
                    k.op("pe", lambda e, fu=fu, c=c, w=w, c0=c0, pso=pso: e.matmul(
                        pso[:, 0:w], lhsT=P.WB[fu // 2][:, fu % 2, c * 128:(c + 1) * 128], rhs=U[:, fu, c0:c0 + w],
                        start=(fu == 0), stop=(fu == nfu - 1)), reads=[P.WBb[fu // 2], Ub[fu][j]], writes=[psob])
                k.op("dve", lambda e, c=c, j=j, w=w, pso=pso: e.scalar_tensor_tensor(
                    out=X.ap(c, j), in0=pso[:, 0:w], scalar=gate_ap[:, c:c + 1], in1=X.ap(c, j),
                    op0=ALU.mult, op1=ALU.add), reads=[psob, X.b[j], C.modb], writes=[X.b[j]])


class Common:
    pass


def emit_inproj_tile(P, H, j, wa, wab, cl, ps, psb, cols=None):
    k = P.k
    c0, w = H.tiles[j]
    if cols is not None:
        c0, w = c0 + cols[0], cols[1]
    for c in range(8):
        k.op("pe", lambda e, c=c, c0=c0, w=w: e.matmul(
            ps[:, 0:w], lhsT=wa[:, c, cl * 128:(cl + 1) * 128], rhs=H.t[:, c, c0:c0 + w],
            start=(c == 0), stop=(c == 7)), reads=[wab, H.b[j]], writes=[psb])
    return w


def emit_qknorm(P, C, ps, psb, w, g_ap, out_ap, out_b, extra_reads=()):
    k = P.k
    sq, sqb = C.tmpbf()
    k.op("act", lambda e: e.activation(out=sq[:, 0:w], in_=ps[:, 0:w], func=AF.Square), reads=[psb], writes=[sqb])
    p2, p2b = P.PS[6], P.PSb[6]
    k.op("pe", lambda e: e.matmul(p2[:, 0:w], lhsT=C.bd_bf[:], rhs=sq[:, 0:w], start=True, stop=True),
         reads=[sqb, C.cb], writes=[p2b])
    rs, rsb = C.tmp()
    k.op("act", lambda e: e.activation(out=rs[:, 0:w], in_=p2[:, 0:w], func=AF.Sqrt, bias=C.eps[:, 0:1],
                                       scale=1.0 / HD), reads=[p2b, C.cb], writes=[rsb])
    k.op("dve", lambda e: e.reciprocal(out=rs[:, 0:w], in_=rs[:, 0:w]), reads=[rsb], writes=[rsb])
    k.op("dve", lambda e: e.scalar_tensor_tensor(out=out_ap, in0=ps[:, 0:w], scalar=g_ap, in1=rs[:, 0:w],
                                                 op0=ALU.mult, op1=ALU.mult),
         reads=[psb, rsb, C.cb] + list(extra_reads), writes=[out_b])


def emit_outproj(P, C, X, MIX, MIXb, wout_l, gate_ap):
    k = P.k
    for pr in range(4):
        wa, wab = P.load_wa(wout_l[:, pr * 256:(pr + 1) * 256])
        for cl in range(2):
            c = pr * 2 + cl
            for j, (c0, w) in enumerate(X.tiles):
                pso, psob = P.bank("o", [4, 5])
                for fc in range(8):
                    k.op("pe", lambda e, fc=fc, c0=c0, w=w, pso=pso, wa=wa, cl=cl: e.matmul(
                        pso[:, 0:w], lhsT=wa[:, fc, cl * 128:(cl + 1) * 128], rhs=MIX(fc, c0, w),
                        start=(fc == 0), stop=(fc == 7)), reads=[wab, MIXb[j]], writes=[psob])
                k.op("dve", lambda e, c=c, j=j, w=w, pso=pso: e.scalar_tensor_tensor(
                    out=X.ap(c, j), in0=pso[:, 0:w], scalar=gate_ap[:, c:c + 1], in1=X.ap(c, j),
                    op0=ALU.mult, op1=ALU.add), reads=[psob, X.b[j], C.modb], writes=[X.b[j]])


def emit_expb(P, C, relrep, relb, onehot, vmask, scr, EXPB, EXPBb):
    k = P.k
    nc = P.nc
    E8 = C.E8
    E8b = Buf("E8")
    for h in range(8):
        lt, ltb = C.tmp()
        k.op("dve", lambda e, h=h, lt=lt: e.tensor_scalar(out=lt[0:32, 0:128], in0=C.ones_f[0:32, 0:128],
                                                          scalar1=relrep[0:32, h:h + 1], scalar2=None, op0=ALU.mult),
             reads=[relb, C.cb], writes=[ltb])
        ps, psb = P.bank("h1", [0, 1])
        k.op("pe", lambda e, lt=lt, ps=ps: e.matmul(ps[:, 0:512], lhsT=lt[0:32, 0:128], rhs=onehot[0:32, :],
                                                    start=True, stop=True), reads=[ltb, C.cb], writes=[psb])
        ex, exb = C.tmp()
        k.op("act", lambda e, ps=ps, ex=ex: e.activation(out=ex[:, :], in_=ps[:, 0:512], func=AF.Exp),
             reads=[psb], writes=[exb])
        k.op("dve", lambda e, h=h, ex=ex: e.tensor_tensor(out=E8[:, h, :], in0=ex[:, :], in1=vmask[:, :], op=ALU.mult),
             reads=[exb, C.cb], writes=[E8b])
    scrb = Buf("scr")
    k.dma("sp", [(scr.ap(), E8[:])], reads=[E8b], writes=[scrb])
    pairs = []
    for h in range(8):
        for bi in range(3):
            off = h * 512 + 128 * (1 - bi) + 255
            src = bass.AP(tensor=scr, offset=off, ap=[[8 * 512 - 1, 128], [1, 128]])
            pairs.append((EXPB[:, h, bi, :], src))
    k.dma("sp", pairs, reads=[scrb], writes=[EXPBb])
    return scrb


def emit_conv0(P, C, H, MIX, MIXb, win_l, convw, convb, flags, S_t, Sb):
    k = P.k
    ntile_own = 4
    for hh in range(2):
        wgb, wgbb = P.load_wa(win_l[:, 768 + hh * 256: 768 + (hh + 1) * 256])
        wgc, wgcb = P.load_wa(win_l[:, 1280 + hh * 256: 1280 + (hh + 1) * 256])
        wu, wub = P.load_wa(win_l[:, 1792 + hh * 256: 1792 + (hh + 1) * 256])
        for cl in range(2):
            cc = hh * 2 + cl
            for j in range(ntile_own):
                c0, w = H.tiles[j]
                pg, pgb = P.bank("h1", [0, 1])
                pu, pub = P.bank("h3", [2, 3])
                emit_inproj_tile(P, H, j, wgc, wgcb, cl, pg, pgb)
                emit_inproj_tile(P, H, j, wu, wub, cl, pu, pub)
                tg, tgb = C.tmp()
                k.op("act", lambda e, tg=tg, pg=pg, w=w: e.activation(out=tg[:, 0:w], in_=pg[:, 0:w], func=AF.Copy),
                     reads=[pgb], writes=[tgb])
                k.op("dve", lambda e, tg=tg, pu=pu, w=w, c0=c0: e.tensor_tensor(
                    out=S_t[:, 1 + c0:1 + c0 + w], in0=pu[:, 0:w], in1=tg[:, 0:w], op=ALU.mult),
                    reads=[pub, tgb], writes=[Sb])
            pg, pgb = P.bank("h1", [0, 1])
            pu, pub = P.bank("h3", [2, 3])
            emit_inproj_tile(P, H, 4, wgc, wgcb, cl, pg, pgb, cols=(127, 2))
            emit_inproj_tile(P, H, 4, wu, wub, cl, pu, pub, cols=(127, 2))
            tg, tgb = C.tmp()
            k.op("act", lambda e, tg=tg, pg=pg: e.activation(out=tg[:, 0:2], in_=pg[:, 0:2], func=AF.Copy),
                 reads=[pgb], writes=[tgb])
            k.op("dve", lambda e, tg=tg, pu=pu: e.scalar_tensor_tensor(
                out=S_t[:, 0:1], in0=pu[:, 0:1], scalar=flags[:, 0:1], in1=tg[:, 0:1], op0=ALU.mult, op1=ALU.mult),
                reads=[pub, tgb, C.cb], writes=[Sb])
            k.op("dve", lambda e, tg=tg, pu=pu: e.scalar_tensor_tensor(
                out=S_t[:, 2049:2050], in0=pu[:, 1:2], scalar=flags[:, 1:2], in1=tg[:, 1:2], op0=ALU.mult, op1=ALU.mult),
                reads=[pub, tgb, C.cb], writes=[Sb])
            for j in range(ntile_own):
                c0, w = H.tiles[j]
                pb_, pbb = P.bank("o", [4, 5])
                emit_inproj_tile(P, H, j, wgb, wgbb, cl, pb_, pbb)
                t, tb = C.tmp()
                k.op("dve", lambda e, t=t, c0=c0, w=w, cc=cc: e.tensor_scalar(
                    out=t[:, 0:w], in0=S_t[:, 1 + c0:1 + c0 + w], scalar1=convw[:, cc, 1:2], scalar2=convb[:, cc:cc + 1],
                    op0=ALU.mult, op1=ALU.add), reads=[Sb, C.cb], writes=[tb])
                k.op("dve", lambda e, t=t, c0=c0, w=w, cc=cc: e.scalar_tensor_tensor(
                    out=t[:, 0:w], in0=S_t[:, c0:c0 + w], scalar=convw[:, cc, 0:1], in1=t[:, 0:w],
                    op0=ALU.mult, op1=ALU.add), reads=[Sb, C.cb, tb], writes=[tb])
                k.op("dve", lambda e, t=t, c0=c0, w=w, cc=cc: e.scalar_tensor_tensor(
                    out=t[:, 0:w], in0=S_t[:, 2 + c0:2 + c0 + w], scalar=convw[:, cc, 2:3], in1=t[:, 0:w],
                    op0=ALU.mult, op1=ALU.add), reads=[Sb, C.cb, tb], writes=[tb])
                k.op("dve", lambda e, t=t, c0=c0, w=w, cc=cc, pb_=pb_: e.tensor_tensor(
                    out=MIX(4 + cc, c0, w), in0=pb_[:, 0:w], in1=t[:, 0:w], op=ALU.mult),
                    reads=[pbb, tb], writes=[MIXb[j]])


def emit_qkv0(P, C, H, win_l, QT, QTb, KT, KTb, VA, VAb, gq, gk, flags):
    k = P.k
    for pr in range(2):
        wa, wab = P.load_wa(win_l[:, pr * 256:(pr + 1) * 256])
        for cl in range(2):
            hh = pr * 2 + cl
            for j in range(4):
                c0, w = H.tiles[j]
                ps, psb = P.bank("h1", [0, 1, 2, 3])
                emit_inproj_tile(P, H, j, wa, wab, cl, ps, psb)
                emit_qknorm(P, C, ps, psb, w, gq[:, 0:1], QT[:, hh, c0:c0 + w], QTb[j])
    wa, wab = P.load_wa(win_l[:, 512:768])
    for j in range(5):
        c0, w = H.tiles[j]
        ps, psb = P.bank("h1", [0, 1, 2, 3])
        emit_inproj_tile(P, H, j, wa, wab, 0, ps, psb)
        emit_qknorm(P, C, ps, psb, w, gk[:, 0:1], KT[:, c0:c0 + w], KTb[j])
    for tb in range(18):
        j = tb // 4 if tb < 16 else 4
        ps, psb = P.bank("o", [4, 5])
        for c in range(8):
            k.op("pe", lambda e, c=c, tb=tb, ps=ps: e.matmul(
                ps[:, 0:128], lhsT=H.t[:, c, tb * 128:(tb + 1) * 128], rhs=wa[:, c, 128:256],
                start=(c == 0), stop=(c == 7)), reads=[wab, H.b[j]], writes=[psb])
        if tb < 16:
            k.op("act", lambda e, tb=tb, ps=ps: e.activation(out=VA[:, tb, 0, 0:64], in_=ps[:, 0:64], func=AF.Copy),
                 reads=[psb], writes=[VAb])
            k.op("act", lambda e, tb=tb, ps=ps: e.activation(out=VA[:, tb, 1, 64:128], in_=ps[:, 64:128], func=AF.Copy),
                 reads=[psb], writes=[VAb])
        else:
            fl = flags[:, tb - 16:tb - 15]
            k.op("dve", lambda e, tb=tb, ps=ps, fl=fl: e.tensor_scalar(
                out=VA[:, tb, 0, 0:64], in0=ps[:, 0:64], scalar1=fl, scalar2=None, op0=ALU.mult),
                reads=[psb, C.cb], writes=[VAb])
            k.op("dve", lambda e, tb=tb, ps=ps, fl=fl: e.tensor_scalar(
                out=VA[:, tb, 1, 64:128], in0=ps[:, 64:128], scalar1=fl, scalar2=None, op0=ALU.mult),
                reads=[psb, C.cb], writes=[VAb])
            k.op("dve", lambda e, tb=tb, fl=fl: e.tensor_scalar(
                out=VA[:, tb, 0, 64:128], in0=C.ones_f[:, 0:64], scalar1=fl, scalar2=None, op0=ALU.mult),
                reads=[C.cb], writes=[VAb])
            k.op("dve", lambda e, tb=tb, fl=fl: e.tensor_scalar(
                out=VA[:, tb, 1, 0:64], in0=C.ones_f[:, 0:64], scalar1=fl, scalar2=None, op0=ALU.mult),
                reads=[C.cb], writes=[VAb])


def emit_attn0(P, C, QT, QTb, KT, KTb, VA, VAb, EXPB, EXPBb, expsink, MIXLO, MIXb, SK=2):
    k = P.k
    its = [(n, g, bi) for n in range(16) for g in range(2) for bi in range(3)]
    st = {}
    cur = {}

    def stage_a(i):
        n, g, bi = its[i]
        j = n // 4
        gs = slice(g * 64, (g + 1) * 64)
        kb = n - 1 + bi
        kidx = 16 if kb < 0 else (17 if kb > 15 else kb)
        kj = kidx // 4 if kidx < 16 else 4
        ps, psb = P.bank("h1", [0, 1, 2, 3])
        k.op("pe", lambda e: e.matmul(ps[:, 0:512], lhsT=KT[gs, kidx * 128:(kidx + 1) * 128],
                                      rhs=QT[gs, :, n * 128:(n + 1) * 128], start=True, stop=True),
             reads=[KTb[kj], QTb[j]], writes=[psb])
        ex, exb = C.tmp()
        k.op("act", lambda e: e.activation(out=ex[:, :], in_=ps[:, 0:512], func=AF.Exp, scale=HD ** -0.5),
             reads=[psb], writes=[exb])
        pt, ptb = C.tmpbf()
        k.op("dve", lambda e: e.tensor_tensor(
            out=pt[:, :].rearrange("p (h q) -> p h q", h=4), in0=ex[:, :].rearrange("p (h q) -> p h q", h=4),
            in1=EXPB[:, g * 4:(g + 1) * 4, bi, :], op=ALU.mult), reads=[exb, EXPBb], writes=[ptb])
        st[i] = (pt, ptb, kidx)

    def stage_b(i):
        n, g, bi = its[i]
        j = n // 4
        gs = slice(g * 64, (g + 1) * 64)
        ds_ = slice((1 - g) * 64, (2 - g) * 64)
        if bi == 0:
            cur["po"] = P.bank("o", [4, 5])
        po, pob = cur["po"]
        pt, ptb, kidx = st.pop(i)
        for hh in range(4):
            k.op("pe", lambda e, hh=hh: e.matmul(
                po[:, hh * 128:(hh + 1) * 128], lhsT=VA[:, kidx, g, :], rhs=pt[:, hh * 128:(hh + 1) * 128],
                start=(bi == 0 and hh == 0), stop=(bi == 2 and hh == 3), skip_group_check=True),
                reads=[VAb, ptb], writes=[pob])
        if bi == 2:
            rd, rdb = C.tmp()
            for hh in range(4):
                k.op("dve", lambda e, hh=hh: e.tensor_scalar(
                    out=rd[ds_, hh * 128:(hh + 1) * 128], in0=po[ds_, hh * 128:(hh + 1) * 128],
                    scalar1=expsink[ds_, g * 4 + hh:g * 4 + hh + 1], scalar2=None, op0=ALU.add),
                    reads=[pob, C.cb], writes=[rdb])
            k.op("dve", lambda e: e.reciprocal(out=rd[ds_, :], in_=rd[ds_, :]), reads=[rdb], writes=[rdb])
            k.op("dve", lambda e: e.tensor_tensor(
                out=MIXLO[gs, 0:4, n * 128:(n + 1) * 128], in0=po[gs, :].rearrange("p (h q) -> p h q", h=4),
                in1=rd[ds_, :].rearrange("p (h q) -> p h q", h=4), op=ALU.mult), reads=[pob, rdb], writes=[MIXb[j]])

    nn = len(its)
    for t in range(nn + SK):
        if t < nn:
            stage_a(t)
        if t - SK >= 0:
            stage_b(t - SK)


class RotTmp:
    def __init__(self, k, name, n, dt):
        self.t = [k.sb(f"{name}{i}", [128, 512], dt) for i in range(n)]
        self.b = [Buf(f"{name}{i}") for i in range(n)]
        self.i = 0

    def __call__(self):
        i = self.i
        self.i = (i + 1) % len(self.t)
        return self.t[i], self.b[i]


BUCKET_MODE = "trunc"


def t5_bucket_np(rel):
    n = np.abs(rel)
    v = (np.log(np.maximum(n, 1).astype(np.float32) / np.float32(8)) / np.float32(math.log(16.0)) * np.float32(8))
    large = 8 + (np.rint(v).astype(np.int32) if BUCKET_MODE == "round" else v.astype(np.int32))
    large = np.minimum(large, 15)
    return np.where(rel > 0, 16, 0) + np.where(n < 8, n, large)


def build_LA(do_l1=True, stop_after=None):
    nc = bass.Bass("TRN2", target_bir_lowering=False)
    k = KB(nc)
    dt_in = lambda name, shape: nc.dram_tensor(name, list(shape), F32, kind="ExternalInput").ap()
    xT_d = dt_in("xT", [128, 8, 2048])
    xH_d = dt_in("xH", [128, 8, 256])
    cf_d = dt_in("cf", [128, 1400])
    oh_d = dt_in("oh", [32, 512])
    vm_d = dt_in("vm", [128, 512])
    wmod_d = dt_in("wmod", [2, 1024, 9216])
    w1_d = dt_in("w1", [2, 2, 1024, DFFP])
    w3_d = dt_in("w3", [2, 2, 1024, DFFP])
    w2_d = dt_in("w2", [2, 2, DFFP, 1024])
    win_d = dt_in("win", [2, 1024, INC])
    wout_d = dt_in("wout", [2, 1024, 1024])
    xo_d = nc.dram_tensor("xo", [128, 8, 2048], F32, kind="ExternalOutput").ap()
    scr = nc.dram_tensor("scr", [128, 8 * 512], F32, kind="Internal")

    P = Prog(nc, k)
    C = Common()
    C.tmp = RotTmp(k, "tf", 5, F32)
    C.tmpbf = RotTmp(k, "tb", 3, BF16)
    C.RS = k.sb("RS", [128, 512], F32)
    C.RSb = Buf("RS")
    C.cb = Buf("consts")
    C.modb = Buf("mod")
    CF = k.sb("CF", [128, 1400], F32)
    OH = k.sb("OH", [32, 512], F32)
    VM = k.sb("VM", [128, 512], F32)
    C.ones_bf = k.sb("ones_bf", [128, 128], BF16)
    C.bd_bf = k.sb("bd_bf", [128, 128], BF16)
    C.ones_f = k.sb("ones_f", [128, 128], F32)
    C.eps = k.sb("eps", [128, 1], F32)
    condbf = k.sb("condbf", [128, 8], BF16)
    MODV = k.sb("modv", [128, 2, 72], F32)
    AV = k.sb("av", [128, 2, 3, 8], F32)
    EXS = k.sb("exs", [128, 8], F32)
    XT = k.sb("XT", [128, 8, 2048], F32)
    Ht = k.sb("H", [128, 8, 2304], BF16)
    UQ = k.sb("UQ", [128, 15360], BF16)
    MX = k.sb("MX", [128, 4096], F32)

    o_cT, o_fl, o_ng, o_bm, o_cw, o_cb, o_gq, o_gk, o_sink = 0, 8, 10, 58, 202, 214, 218, 220, 222
    cT = CF[:, o_cT:o_cT + 8]
    flags = CF[:, o_fl:o_fl + 2]
    normg = CF[:, o_ng:o_ng + 48].rearrange("p (l i c) -> p l i c", l=2, i=3)
    bmod = CF[:, o_bm:o_bm + 144].rearrange("p (l m) -> p l m", l=2)
    convw = CF[:, o_cw:o_cw + 12].rearrange("p (c t) -> p c t", c=4)
    convb = CF[:, o_cb:o_cb + 4]
    gq = CF[:, o_gq:o_gq + 2]
    gk = CF[:, o_gk:o_gk + 2]
    sink = CF[:, o_sink:o_sink + 8]
    relrep = CF[:, 232:240]

    XH = MX[:, 0:2048].rearrange("p (c t) -> p c t", c=8)
    MIXHI = MX[:, :].bitcast(BF16).rearrange("p (c t) -> p c t", c=4)
    U = UQ[:, 0:6 * 2304].rearrange("p (f t) -> p f t", f=6)
    QT = UQ[:, 0:8192].rearrange("p (h t) -> p h t", h=4)
    KT = UQ[:, 8192:8192 + 2304]
    VA = UQ[:, 10496:10496 + 4608].rearrange("p (b g d) -> p b g d", b=18, g=2)
    S_t = UQ[:, 10496:10496 + 4100].bitcast(F32)
    C.E8 = UQ[:, 0:8192].bitcast(F32).rearrange("p (h m) -> p h m", h=8)
    EXPB = P.WBf[:, :, :].rearrange("p a (b q) -> p (a b) q", q=128).rearrange("p (h b) q -> p h b q", h=8)

    tiles5 = [(0, 512), (512, 512), (1024, 512), (1536, 512), (2048, 256)]
    tiles4 = tiles5[:4]

    class XAct:
        def __init__(self, tiles):
            self.tiles = tiles
            self.b = XB[:len(tiles)]

        def ap(self, c, j):
            if j < 4:
                return XT[:, c, j * 512:(j + 1) * 512]
            return XH[:, c, :]
    XB = [Buf(f"x{j}") for j in range(5)]
    X5, X4 = XAct(tiles5), XAct(tiles4)
    H5 = Act(Ht, tiles5, "h")
    H4 = Act(Ht, tiles4, "h")
    H4.b = H5.b[:4]
    Ub = [[Buf(f"u{f}_{j}") for j in range(5)] for f in range(6)]

    k.dma("sp", [(CF[:], cf_d)], writes=[C.cb])
    k.dma("sp", [(OH[:], oh_d), (VM[:], vm_d)], writes=[C.cb], sembuf=C.cb)
    for j in range(4):
        k.dma("sp", [(XT[:, :, j * 512:(j + 1) * 512], xT_d[:, :, j * 512:(j + 1) * 512])], writes=[XB[j]])
    k.dma("sp", [(XH, xH_d)], writes=[XB[4]])
    cst = Buf("cst")
    k.op("dve", lambda e: e.memset(C.ones_bf[:], 1.0), writes=[cst])
    k.op("dve", lambda e: e.memset(C.ones_f[:], 1.0), writes=[cst])
    k.op("dve", lambda e: e.memset(C.eps[:], EPS), writes=[cst])
    k.op("dve", lambda e: e.memset(C.bd_bf[:], 0.0), writes=[cst])
    k.op("dve", lambda e: e.memset(C.bd_bf[0:64, 0:64], 1.0), writes=[cst])
    k.op("dve", lambda e: e.memset(C.bd_bf[64:128, 64:128], 1.0), writes=[cst])
    condb = Buf("cond")
    k.op("act", lambda e: e.activation(out=condbf[:], in_=cT, func=AF.Silu), reads=[C.cb], writes=[condb])
    k.op("act", lambda e: e.activation(out=EXS[:], in_=sink, func=AF.Exp), reads=[C.cb], writes=[cst])
    k.op("dve", lambda e: e.tensor_copy(out=C.eps[:], in_=C.eps[:]), reads=[cst, C.cb], writes=[C.cb])

    def layer_mod(l):
        emit_mod(P, condbf, condb, wmod_d[l], bmod[:, l, :], MODV[:, l, :], C.modb, AV[:, l], normg[:, l])

    def mcol(l, i, kind):
        return MODV[:, l, (i * 3 + kind) * 8:(i * 3 + kind + 1) * 8]

    def mix_ap(fc, c0, w):
        if fc < 4:
            return Ht[:, fc, c0:c0 + w]
        return MIXHI[:, fc - 4, c0:c0 + w]

    def finish():
        ob = Sink("out")
        for j in range(4):
            k.dma("sp", [(xo_d[:, :, j * 512:(j + 1) * 512], XT[:, :, j * 512:(j + 1) * 512])], reads=[XB[j]],
                  sembuf=XB[j], sink=ob)
        k.wait_all("sp", [ob])
        k.close()
        return nc

    layer_mod(0)
    emit_adaln(P, X5, H5, AV[:, 0, 0], mcol(0, 0, 0), C)
    emit_ffn(P, X5, H5, U, Ub, w1_d[0, 0], w3_d[0, 0], w2_d[0, 0], mcol(0, 0, 2), C)
    if stop_after == "ffn0":
        return finish()
    k.barrier()
    ohb = C.cb
    scrb = emit_expb(P, C, relrep, C.cb, OH, VM, scr, EXPB, P.WBb[0])
    k.barrier(dbufs=[scrb])
    if stop_after == "expb":
        dbg = nc.dram_tensor("dbg", [128, 3072], F32, kind="ExternalOutput").ap()
        ob2 = Buf("dbg")
        k.dma("sp", [(dbg, P.WBf[:].rearrange("p a b -> p (a b)"))], reads=[P.WBb[0]], writes=[ob2], sembuf=ob2)
        k.wait_all("sp", [ob2])
        return finish()
    emit_adaln(P, X5, H5, AV[:, 0, 1], mcol(0, 1, 0), C)
    k.barrier()
    MIXb = [Buf(f"mix{j}") for j in range(4)]
    Sb = Buf("S")
    emit_conv0(P, C, H5, mix_ap, MIXb, win_d[0], convw, convb, flags, S_t, Sb)
    k.barrier()
    QTb = [Buf(f"qt{j}") for j in range(4)]
    KTb = [Buf(f"kt{j}") for j in range(5)]
    VAb = Buf("va")
    k.op("dve", lambda e: e.memset(VA[:, 0:16, 0, 64:128], 1.0), writes=[VAb])
    k.op("dve", lambda e: e.memset(VA[:, 0:16, 1, 0:64], 1.0), writes=[VAb])
    emit_qkv0(P, C, H5, win_d[0], QT, QTb, KT, KTb, VA, VAb, gq, gk, flags)
    k.barrier()
    emit_attn0(P, C, QT, QTb, KT, KTb, VA, VAb, EXPB, P.WBb[0], EXS, Ht, MIXb)
    for i in (1, 2):
        P.WBb[i].r = dict(P.WBb[0].r)
        P.WBb[i].w = P.WBb[0].w
    if stop_after == "mix":
        dbg = nc.dram_tensor("dbg", [128, 8, 2048], BF16, kind="ExternalOutput").ap()
        ob2 = Buf("dbg")
        k.dma("sp", [(dbg[:, 0:4, :], Ht[:, 0:4, 0:2048]), (dbg[:, 4:8, :], MIXHI)], reads=MIXb, writes=[ob2], sembuf=ob2)
        k.wait_all("sp", [ob2])
        return finish()
    emit_outproj(P, C, X4, mix_ap, MIXb, wout_d[0], mcol(0, 1, 2))
    if stop_after == "mixer":
        return finish()
    k.barrier()
    emit_adaln(P, X4, H4, AV[:, 0, 2], mcol(0, 2, 0), C)
    emit_ffn(P, X4, H4, U, Ub, w1_d[0, 1], w3_d[0, 1], w2_d[0, 1], mcol(0, 2, 2), C)
    if not do_l1:
        return finish()

    rope_d = dt_in("rope", [128, 2, 2048])
    qo_d = nc.dram_tensor("qo", [128, 4, 2048], BF16, kind="ExternalOutput").ap()
    ko_d = nc.dram_tensor("ko", [128, 2048], BF16, kind="ExternalOutput").ap()
    vo_d = nc.dram_tensor("vo", [16, 128, 128], BF16, kind="ExternalOutput").ap()
    hco_d = nc.dram_tensor("hco", [12, 128, 2048], F32, kind="ExternalOutput").ap()
    layer_mod(1)
    emit_adaln(P, X4, H4, AV[:, 1, 0], mcol(1, 0, 0), C)
    emit_ffn(P, X4, H4, U, Ub, w1_d[1, 0], w3_d[1, 0], w2_d[1, 0], mcol(1, 0, 2), C)
    k.barrier()
    ROPE = UQ[:, 0:8192].bitcast(F32).rearrange("p (a t) -> p a t", a=2)
    ropeb = Buf("rope")
    k.dma("sp", [(ROPE, rope_d)], writes=[ropeb])
    emit_adaln(P, X4, H4, AV[:, 1, 1], mcol(1, 1, 0), C)
    outb = Sink("outs")
    modo_d = nc.dram_tensor("modo", [128, 96], F32, kind="ExternalOutput").ap()
    k.dma("sp", [(modo_d[:, 0:72], MODV[:, 1, :]), (modo_d[:, 72:96], AV[:, 1].rearrange("p i c -> p (i c)"))],
          reads=[C.modb], sembuf=C.modb, sink=outb)
    emit_inproj1(P, C, H4, win_d[1], CF[:, 240:241], CF[:, 241:242], ROPE[:, 0, :], ROPE[:, 1, :], ropeb,
                 qo_d, ko_d, vo_d, hco_d, outb)
    k.wait_all("sp", [outb])
    return finish()


def host_consts():
    idx = np.arange(512)
    rel = 255 - idx
    valid = (np.abs(rel) <= 128) & (idx < 511)
    bucket = t5_bucket_np(rel.astype(np.int64))
    oh = np.zeros((32, 512), np.float32)
    oh[bucket[valid], idx[valid]] = 1.0
    vm = np.tile(valid.astype(np.float32)[None, :], (128, 1))
    return oh, vm


def fm(v):
    v = np.asarray(v, np.float32)
    sh = v.shape
    v = v.reshape(sh[:-1] + (sh[-1] // 128, 128))
    return np.moveaxis(v, -1, 0)


def prep_LA(inp):
    x = np.asarray(inp["x"], np.float32)
    qperm = np.concatenate([np.r_[hh * 64:(hh + 1) * 64, (4 + hh) * 64:(5 + hh) * 64] for hh in range(4)])
    win = np.ascontiguousarray(np.asarray(inp["w_in"], np.float32))
    win = np.concatenate([win[:, :, qperm], win[:, :, 512:]], axis=2)
    eo = np.r_[0:64:2, 1:64:2]
    eo_q = np.concatenate([h * 64 + eo for h in range(8)])
    eo_k = np.concatenate([512 + h * 64 + eo for h in range(2)])
    win[1] = np.concatenate([win[1][:, eo_q], win[1][:, eo_k], win[1][:, 640:]], axis=1)
    wout = np.asarray(inp["w_out"], np.float32)
    wout = np.ascontiguousarray(np.concatenate([wout[:, qperm, :], wout[:, 512:, :]], axis=1))
    pad = DFFP - DFF
    w1 = np.pad(np.asarray(inp["ffn_w1"], np.float32), ((0, 0), (0, 0), (0, 0), (0, pad)))
    w3 = np.pad(np.asarray(inp["ffn_w3"], np.float32), ((0, 0), (0, 0), (0, 0), (0, pad)))
    w2 = np.pad(np.asarray(inp["ffn_w2"], np.float32), ((0, 0), (0, 0), (0, pad), (0, 0)))
    wmod = np.ascontiguousarray(np.asarray(inp["w_mod"], np.float32))
    oh, vm = host_consts()
    shared = dict(oh=oh, vm=vm, wmod=wmod, w1=w1, w3=w3, w2=w2, win=np.ascontiguousarray(win), wout=wout)
    maps = []
    for core in range(NCORE):
        b, qd = core // 4, core % 4
        t0 = qd * TOK
        xs = x[b, t0:t0 + TOK]
        xT = np.ascontiguousarray(xs.T.reshape(8, 128, TOK).transpose(1, 0, 2))
        halo = np.zeros((256, D), np.float32)
        fl = np.zeros((2,), np.float32)
        if qd > 0:
            halo[0:128] = x[b, t0 - 128:t0]
            fl[0] = 1.0
        if qd < 3:
            halo[128:256] = x[b, t0 + TOK:t0 + TOK + 128]
            fl[1] = 1.0
        xH = np.ascontiguousarray(halo.T.reshape(8, 128, 256).transpose(1, 0, 2))
        cf = np.zeros((128, 1400), np.float32)
        cf[:, 0:8] = fm(inp["c"][b])
        cf[:, 8:10] = fl[None, :]
        cf[:, 10:58] = fm(inp["norm_g"]).reshape(128, 48)
        cf[:, 58:202] = fm(inp["b_mod"]).reshape(128, 144)
        cw = np.asarray(inp["b_conv_w"], np.float32)[0]
        cf[:, 202:214] = fm(cw).transpose(0, 2, 1).reshape(128, 12)
        cf[:, 214:218] = fm(np.asarray(inp["b_conv_b"], np.float32)[0])
        aq = np.asarray(inp["a_qk_g"], np.float32)[0]
        cf[:, 218] = np.tile(aq[0], 2)
        cf[:, 220] = np.tile(aq[1], 2)
        cf[:, 222:230] = np.asarray(inp["a_sink"], np.float32)[0][None, :]
        cf[0:32, 232:240] = np.asarray(inp["rel_table"], np.float32)
        cg = np.asarray(inp["c_qk_g"], np.float32)[0]
        cf[:, 240] = np.tile(cg[0][eo], 2)
        cf[:, 241] = np.tile(cg[1][eo], 2)
        pos = np.arange(t0, t0 + TOK)
        row = (pos // 64).astype(np.float32)
        col = (pos % 64).astype(np.float32)
        inv = (np.float32(10000.0) ** (-np.arange(0, 32, 2, dtype=np.float32) / np.float32(32))).astype(np.float32)
        ang = np.concatenate([row[:, None] * inv, col[:, None] * inv], axis=-1).astype(np.float32)
        cs = np.cos(ang).astype(np.float32).T
        sn = np.sin(ang).astype(np.float32).T
        rope = np.zeros((128, 2, TOK), np.float32)
        for qd4 in range(4):
            rope[qd4 * 32:(qd4 + 1) * 32, 0] = cs
            rope[qd4 * 32:(qd4 + 1) * 32, 1] = sn if qd4 % 2 == 0 else -sn
        m_rope = rope
        m = dict(shared)
        m.update(xT=xT, xH=xH, cf=cf, rope=m_rope)
        maps.append(m)
    return maps


def emit_rope(P, C, qn, qnb, CS, SNs, ropeb, c0, w, out_ap, out_b):
    k = P.k
    t1, t1b = C.tmp()
    k.op("dve", lambda e: e.tensor_tensor(out=t1[:, 0:w], in0=qn[:, 0:w], in1=CS[:, c0:c0 + w], op=ALU.mult),
         reads=[qnb, ropeb], writes=[t1b])
    t2, t2b = C.tmp()
    for qd in range(4):
        src = qd ^ 1
        k.op("dve", lambda e, qd=qd, src=src: e.tensor_tensor(
            out=t2[qd * 32:(qd + 1) * 32, 0:w], in0=qn[src * 32:(src + 1) * 32, 0:w],
            in1=SNs[src * 32:(src + 1) * 32, c0:c0 + w], op=ALU.mult), reads=[qnb, ropeb], writes=[t2b])
    k.op("dve", lambda e: e.tensor_tensor(out=out_ap, in0=t1[:, 0:w], in1=t2[:, 0:w], op=ALU.add),
         reads=[t1b, t2b], writes=[out_b])


def emit_inproj1(P, C, H, win_l, gq, gk, CS, SNs, ropeb, qo_d, ko_d, vo_d, hco_d, outb):
    k = P.k
    for pr in range(3):
        wa, wab = P.load_wa(win_l[:, pr * 256:(pr + 1) * 256])
        for cl in range(2):
            if pr == 2 and cl == 1:
                break
            for j in range(4):
                c0, w = H.tiles[j]
                ps, psb = P.bank("h1", [0, 1, 2, 3])
                emit_inproj_tile(P, H, j, wa, wab, cl, ps, psb)
                qn, qnb = C.tmp()
                emit_qknorm(P, C, ps, psb, w, (gq if pr < 2 else gk)[:, 0:1], qn[:, 0:w], qnb)
                st, stb = C.tmpbf()
                emit_rope(P, C, qn, qnb, CS, SNs, ropeb, c0, w, st[:, 0:w], stb)
                dst = qo_d[:, pr * 2 + cl, c0:c0 + w] if pr < 2 else ko_d[:, c0:c0 + w]
                k.dma("sp", [(dst, st[:, 0:w])], reads=[stb], sembuf=stb, sink=outb)
        if pr == 2:
            for tb in range(16):
                j = tb // 4
                ps, psb = P.bank("o", [4, 5])
                for c in range(8):
                    k.op("pe", lambda e, c=c, tb=tb, ps=ps: e.matmul(
                        ps[:, 0:128], lhsT=H.t[:, c, tb * 128:(tb + 1) * 128], rhs=wa[:, c, 128:256],
                        start=(c == 0), stop=(c == 7)), reads=[wab, H.b[j]], writes=[psb])
                st, stb = C.tmpbf()
                k.op("act", lambda e, ps=ps, st=st: e.activation(out=st[:, 0:128], in_=ps[:, 0:128], func=AF.Copy),
                     reads=[psb], writes=[stb])
                k.dma("sp", [(vo_d[tb], st[:, 0:128])], reads=[stb], sembuf=stb, sink=outb)
    for pr in range(6):
        wa, wab = P.load_wa(win_l[:, 768 + pr * 256:768 + (pr + 1) * 256])
        for cl in range(2):
            ch = pr * 2 + cl
            for j in range(4):
                c0, w = H.tiles[j]
                ps, psb = P.bank("h1", [0, 1, 2, 3])
                emit_inproj_tile(P, H, j, wa, wab, cl, ps, psb)
                st, stb = C.tmp()
                k.op("act", lambda e, ps=ps, st=st, w=w: e.activation(out=st[:, 0:w], in_=ps[:, 0:w], func=AF.Copy),
                     reads=[psb], writes=[stb])
                k.dma("sp", [(hco_d[ch, :, c0:c0 + w], st[:, 0:w])], reads=[stb], sembuf=stb, sink=outb)


NFFT = 16384


def build_LB():
    nc = bass.Bass("TRN2", target_bir_lowering=False)
    k = KB(nc)
    dt_in = lambda name, shape: nc.dram_tensor(name, list(shape), F32, kind="ExternalInput").ap()
    hc_d = dt_in("hc3", [3, 128, S])
    cf_d = dt_in("cf2", [128, 32])
    mlp_d = dt_in("mlpw", [64, 456])
    zemb_d = dt_in("zemb", [33, NFFT])
    win_d = dt_in("win", [128, 128, 128])
    dft_d = dt_in("dftc", [128, 512])
    tw_d = dt_in("tw", [128, 2, 2, 128])
    y_d = nc.dram_tensor("yT", [128, S], F32, kind="ExternalOutput").ap()
    zs = nc.dram_tensor("zs", [128, S], F32, kind="Internal")
    cs = nc.dram_tensor("cs", [128, S], F32, kind="Internal")

    PS = [k.ps(f"ps{i}", [128, 512]) for i in range(8)]
    PSb = [Buf(f"ps{i}") for i in range(8)]
    rr = {}

    def bank(group, banks):
        i = rr.get(group, 0)
        rr[group] = (i + 1) % len(banks)
        return PS[banks[i]], PSb[banks[i]]

    tmp = RotTmp(k, "tf", 8, F32)
    tmpbf = RotTmp(k, "tb", 4, BF16)
    cb = Buf("consts")
    CF = k.sb("CF", [128, 32], F32)
    MLP = k.sb("MLP", [64, 456], F32)
    DFT = k.sb("DFT", [128, 512], BF16)
    TW = k.sb("TW", [128, 2, 2, 128], F32)
    X0 = k.sb("X0", [128, S], F32)
    Z = k.sb("Z", [128, S], F32)
    IN = k.sb("IN", [128, S + 2], F32)
    KC = k.sb("KC", [128, 128, 128], BF16)
    ZC = k.sb("ZC", [64, 128, 128], BF16)
    ACC = k.sb("ACC", [128, 128], F32)
    SM = k.sb("SM", [128, 16], F32)
    ones_f = k.sb("ones_f", [128, 1], F32)
    X0b, Zb, INb, KCb, ZCb, ACCb, SMb = [Buf(n) for n in "X0 Z IN KC ZC ACC SM".split()]

    k.dma("sp", [(CF[:], cf_d), (MLP[:], mlp_d), (TW[:], tw_d)], writes=[cb])
    dftb = Buf("dft")
    k.dma("pool", [(DFT[:], dft_d)], writes=[dftb])
    Fre, Fim, nFim = DFT[:, 0:128], DFT[:, 128:256], DFT[:, 384:512]
    Fcat, FcatI2, FcatI1 = DFT[:, 0:256], DFT[:, 128:384], DFT[:, 256:512]
    k.op("dve", lambda e: e.memset(ones_f[:], 1.0), writes=[SMb])
    k.op("dve", lambda e: e.memset(SM[:, 0:1], math.pi / 2), writes=[SMb])
    k.op("dve", lambda e: e.memset(SM[:, 1:2], EPS), writes=[SMb])
    k.op("dve", lambda e: e.memset(IN[:, 0:1], 0.0), writes=[INb])
    k.op("dve", lambda e: e.memset(IN[:, S + 1:S + 2], 0.0), writes=[INb])

    CH = 2048
    for part in range(3):
        k.dma("sp", [(IN[:, 1:S + 1], hc_d[part])], writes=[INb])
        for cc in range(S // CH):
            c0 = cc * CH
            wc = CF[:, part * 4:part * 4 + 3]
            bc = CF[:, part * 4 + 3:part * 4 + 4]
            for s0 in range(0, CH, 512):
                a0 = c0 + s0
                if part == 0:
                    o, ob_ = X0[:, a0:a0 + 512], X0b
                elif part == 1:
                    o, ob_ = Z[:, a0:a0 + 512], Zb
                else:
                    tt, ttb = tmp()
                    o, ob_ = tt[:, 0:512], ttb
                k.op("dve", lambda e, o=o, a0=a0, wc=wc, bc=bc: e.tensor_scalar(
                    out=o, in0=IN[:, a0 + 1:a0 + 513], scalar1=wc[:, 1:2], scalar2=bc, op0=ALU.mult, op1=ALU.add),
                    reads=[INb, cb], writes=[ob_])
                k.op("dve", lambda e, o=o, a0=a0, wc=wc: e.scalar_tensor_tensor(
                    out=o, in0=IN[:, a0:a0 + 512], scalar=wc[:, 0:1], in1=o, op0=ALU.mult, op1=ALU.add),
                    reads=[INb, cb, ob_], writes=[ob_])
                k.op("dve", lambda e, o=o, a0=a0, wc=wc: e.scalar_tensor_tensor(
                    out=o, in0=IN[:, a0 + 2:a0 + 514], scalar=wc[:, 2:3], in1=o, op0=ALU.mult, op1=ALU.add),
                    reads=[INb, cb, ob_], writes=[ob_])
                if part == 2:
                    k.op("pool", lambda e, o=o, a0=a0: e.tensor_tensor(
                        out=Z[:, a0:a0 + 512], in0=Z[:, a0:a0 + 512], in1=o, op=ALU.mult), reads=[ob_, Zb], writes=[Zb])
    zsb = Buf("zs")
    k.dma("sp", [(zs.ap(), Z[:])], reads=[Zb], writes=[zsb])
    src = bass.AP(tensor=zs, offset=0, ap=[[128, 64], [S, 128], [1, 128]])
    k.dma("pool", [(ZC[:], src)], reads=[zsb], writes=[ZCb])

    W1, W2, W3 = MLP[0:33, 0:64], MLP[:, 64:128], MLP[:, 128:192]
    W4 = MLP[:, 200:456].rearrange("p (d c) -> p d c", d=2)
    k.op("dve", lambda e: e.tensor_scalar(out=SM[0:64, 2:3], in0=MLP[:, 195:196], scalar1=0.25, scalar2=None, op0=ALU.mult),
         reads=[cb], writes=[SMb])
    for i in range(3):
        k.op("dve", lambda e, i=i: e.tensor_tensor(out=SM[0:64, 3 + i:4 + i], in0=MLP[:, 192 + i:193 + i], in1=SM[0:64, 2:3],
                                                   op=ALU.mult), reads=[cb, SMb], writes=[SMb])
    k.op("dve", lambda e: e.memset(ACC[:], 0.0), writes=[ACCb])

    def sin4(ps, psb, li):
        s1, s1b = tmp()
        k.op("act", lambda e: e.activation(out=s1[0:64, :], in_=ps[0:64, :], func=AF.Sin, bias=SM[0:64, 3 + li:4 + li],
                                           scale=SM[0:64, 2:3]), reads=[psb, SMb], writes=[s1b])
        a1, a1b = tmp()
        k.op("act", lambda e: e.activation(out=a1[0:64, :], in_=ps[0:64, :], func=AF.Abs, bias=SM[0:64, 3 + li:4 + li],
                                           scale=SM[0:64, 2:3]), reads=[psb, SMb], writes=[a1b])
        k.op("act", lambda e: e.activation(out=a1[0:64, :], in_=a1[0:64, :], func=AF.Sin, bias=SM[0:64, 0:1], scale=-1.0),
             reads=[a1b, SMb], writes=[a1b])
        k.op("dve", lambda e: e.tensor_tensor(out=a1[0:64, :], in0=a1[0:64, :], in1=s1[0:64, :], op=ALU.mult),
             reads=[a1b, s1b], writes=[a1b])
        k.op("dve", lambda e: e.tensor_tensor(out=s1[0:64, :], in0=s1[0:64, :], in1=s1[0:64, :], op=ALU.mult),
             reads=[s1b], writes=[s1b])
        k.op("dve", lambda e: e.tensor_scalar(out=s1[0:64, :], in0=s1[0:64, :], scalar1=-2.0, scalar2=1.0,
                                              op0=ALU.mult, op1=ALU.add), reads=[s1b], writes=[s1b])
        k.op("dve", lambda e: e.scalar_tensor_tensor(out=a1[0:64, :], in0=a1[0:64, :], scalar=4.0, in1=s1[0:64, :],
                                                     op0=ALU.mult, op1=ALU.mult), reads=[a1b, s1b], writes=[a1b])
        return a1, a1b

    for c in range(32):
        ze, zeb = tmp()
        k.dma("sp", [(ze[0:33, :], zemb_d[:, c * 512:(c + 1) * 512])], writes=[zeb])
        wt, wtb = tmp()
        k.dma("sp", [(wt[:, :].rearrange("p (n c) -> p n c", n=4), win_d[:, c * 4:(c + 1) * 4, :])], writes=[wtb])
        ps, psb = bank("m", [6, 7])
        k.op("pe", lambda e, ps=ps, ze=ze: e.matmul(ps[0:64, :], lhsT=W1, rhs=ze[0:33, :], start=True, stop=True),
             reads=[zeb, cb], writes=[psb])
        h, hb = sin4(ps, psb, 0)
        for li, W in ((1, W2), (2, W3)):
            ps, psb = bank("m", [6, 7])
            k.op("pe", lambda e, ps=ps, h=h, W=W: e.matmul(ps[0:64, :], lhsT=W, rhs=h[0:64, :], start=True, stop=True),
                 reads=[hb, cb], writes=[psb])
            h, hb = sin4(ps, psb, li)
        ps4, ps4b = bank("m", [6, 7])
        for n2l in range(4):
            for d in range(2):
                k.op("pe", lambda e, ps4=ps4, h=h, n2l=n2l, d=d: e.matmul(
                    ps4[d * 64:(d + 1) * 64, n2l * 128:(n2l + 1) * 128],
                    lhsT=h[0:64, n2l * 128 + d * 64:n2l * 128 + (d + 1) * 64], rhs=W4[:, d, :],
                    start=True, stop=True, skip_group_check=True), reads=[hb, cb], writes=[ps4b])
        kc, kcb = tmp()
        k.op("dve", lambda e, kc=kc, ps4=ps4, wt=wt: e.tensor_tensor(out=kc[:, :], in0=ps4[:, :], in1=wt[:, :], op=ALU.mult),
             reads=[ps4b, wtb], writes=[kcb])
        k.op("act", lambda e, kc=kc, c=c: e.activation(
            out=KC[:, :, c * 4:(c + 1) * 4], in_=kc[:, :].rearrange("p (n c) -> p c n", n=4), func=AF.Copy),
            reads=[kcb], writes=[KCb])
        sq, sqb = tmp()
        k.op("pool", lambda e, kc=kc, sq=sq: e.tensor_tensor(out=sq[:, :], in0=kc[:, :], in1=kc[:, :], op=ALU.mult),
             reads=[kcb], writes=[sqb])
        rd, rdb = tmp()
        k.op("dve", lambda e, sq=sq, rd=rd: e.tensor_reduce(
            out=rd[:, 0:128], in_=sq[:, :].rearrange("p (n c) -> p c n", n=4), axis=mybir.AxisListType.X, op=ALU.add),
            reads=[sqb], writes=[rdb])
        k.op("pool", lambda e, rd=rd: e.tensor_tensor(out=ACC[:], in0=ACC[:], in1=rd[:, 0:128], op=ALU.add),
             reads=[rdb, ACCb], writes=[ACCb])
    ps, psb = bank("m", [6, 7])
    k.op("pe", lambda e, ps=ps: e.matmul(ps[:, 0:1], lhsT=ACC[:], rhs=ones_f[:, 0:1], start=True, stop=True),
         reads=[ACCb, SMb], writes=[psb])
    k.op("act", lambda e, ps=ps: e.activation(out=SM[:, 8:9], in_=ps[:, 0:1], func=AF.Sqrt, bias=SM[:, 1:2], scale=1.0),
         reads=[psb, SMb], writes=[SMb])
    k.op("dve", lambda e: e.reciprocal(out=SM[:, 8:9], in_=SM[:, 8:9]), reads=[SMb], writes=[SMb])

    Bre = k.sb("Bre", [128, 4, 128], BF16)
    Bim = k.sb("Bim", [128, 4, 128], BF16)
    Hre = k.sb("Hre", [128, 512], F32)
    Him = k.sb("Him", [128, 512], F32)
    Yre = k.sb("Yre", [128, 4, 128], BF16)
    Yim = k.sb("Yim", [128, 4, 128], BF16)
    Qre = k.sb("Qre", [128, 4, 128], BF16)
    Qim = k.sb("Qim", [128, 4, 128], BF16)
    Bb, Hb, Yb, Qb = Buf("B"), Buf("H"), Buf("Y"), Buf("Q")
    TWre2, TWim2 = TW[:, 0], TW[:, 1]

    def cmul_from_psum(A, Ab, sign, outre, outim, outb, pr):
        A4 = A[:, :].rearrange("p (c r k) -> p c r k", c=2, r=2)
        Are, Aim = A4[:, :, 0, :], A4[:, :, 1, :]
        t1, t1b = tmp()
        t2, t2b = tmp()
        v = lambda t: t[:, 0:256].rearrange("p (c k) -> p c k", c=2)
        k.op("dve", lambda e: e.tensor_tensor(out=v(t1), in0=Are, in1=TWre2, op=ALU.mult), reads=[Ab, cb], writes=[t1b])
        k.op("dve", lambda e: e.tensor_tensor(out=v(t2), in0=Aim, in1=TWim2, op=ALU.mult), reads=[Ab, cb], writes=[t2b])
        k.op("pool", lambda e: e.tensor_tensor(out=outre[:, pr * 2:pr * 2 + 2, :], in0=v(t1), in1=v(t2),
                                               op=(ALU.subtract if sign > 0 else ALU.add)),
             reads=[t1b, t2b], writes=[outb])
        t3, t3b = tmp()
        t4, t4b = tmp()
        k.op("dve", lambda e: e.tensor_tensor(out=v(t3), in0=Aim, in1=TWre2, op=ALU.mult), reads=[Ab, cb], writes=[t3b])
        k.op("dve", lambda e: e.tensor_tensor(out=v(t4), in0=Are, in1=TWim2, op=ALU.mult), reads=[Ab, cb], writes=[t4b])
        k.op("pool", lambda e: e.tensor_tensor(out=outim[:, pr * 2:pr * 2 + 2, :], in0=v(t3), in1=v(t4),
                                               op=(ALU.add if sign > 0 else ALU.subtract)),
             reads=[t3b, t4b], writes=[outb])

    def fft_fwd(src, srcb, krows, ch0, xre, xreb, xim, ximb):
        for pr in range(2):
            A, Ab = bank("A", [0, 1])
            for cl in range(2):
                ch = ch0 + pr * 2 + cl
                k.op("pe", lambda e, A=A, cl=cl, ch=ch: e.matmul(
                    A[:, cl * 256:(cl + 1) * 256], lhsT=src[0:krows, ch, :], rhs=Fcat[0:krows, :],
                    start=True, stop=True, skip_group_check=True), reads=[srcb, dftb], writes=[Ab])
            cmul_from_psum(A, Ab, +1, Bre, Bim, Bb, pr)
        bre = Bre[:, :, :].rearrange("p c k -> p (c k)")
        bim = Bim[:, :, :].rearrange("p c k -> p (c k)")
        k.op("pe", lambda e: e.matmul(xre[:, :], lhsT=Fre, rhs=bre, start=True, stop=False), reads=[Bb, dftb], writes=[xreb])
        k.op("pe", lambda e: e.matmul(xre[:, :], lhsT=nFim, rhs=bim, start=False, stop=True), reads=[Bb, dftb], writes=[xreb])
        k.op("pe", lambda e: e.matmul(xim[:, :], lhsT=Fim, rhs=bre, start=True, stop=False), reads=[Bb, dftb], writes=[ximb])
        k.op("pe", lambda e: e.matmul(xim[:, :], lhsT=Fre, rhs=bim, start=False, stop=True), reads=[Bb, dftb], writes=[ximb])

    csb = Sink("cs")
    for g in range(32):
        ch0 = g * 4
        fft_fwd(KC, KCb, 128, ch0, PS[2], PSb[2], PS[3], PSb[3])
        k.op("act", lambda e: e.activation(out=Hre[:], in_=PS[2][:, :], func=AF.Copy), reads=[PSb[2]], writes=[Hb])
        k.op("act", lambda e: e.activation(out=Him[:], in_=PS[3][:, :], func=AF.Copy), reads=[PSb[3]], writes=[Hb])
        fft_fwd(ZC, ZCb, 64, ch0, PS[4], PSb[4], PS[5], PSb[5])
        t1, t1b = tmp()
        t2, t2b = tmp()
        k.op("dve", lambda e, t1=t1: e.tensor_tensor(out=t1[:, :], in0=PS[4][:, :], in1=Hre[:], op=ALU.mult),
             reads=[PSb[4], Hb], writes=[t1b])
        k.op("dve", lambda e, t2=t2: e.tensor_tensor(out=t2[:, :], in0=PS[5][:, :], in1=Him[:], op=ALU.mult),
             reads=[PSb[5], Hb], writes=[t2b])
        k.op("pool", lambda e, t1=t1, t2=t2: e.tensor_tensor(out=Yre[:, :, :].rearrange("p c k -> p (c k)"), in0=t1[:, :],
                                                             in1=t2[:, :], op=ALU.subtract), reads=[t1b, t2b], writes=[Yb])
        t3, t3b = tmp()
        t4, t4b = tmp()
        k.op("dve", lambda e, t3=t3: e.tensor_tensor(out=t3[:, :], in0=PS[4][:, :], in1=Him[:], op=ALU.mult),
             reads=[PSb[4], Hb], writes=[t3b])
        k.op("dve", lambda e, t4=t4: e.tensor_tensor(out=t4[:, :], in0=PS[5][:, :], in1=Hre[:], op=ALU.mult),
             reads=[PSb[5], Hb], writes=[t4b])
        k.op("pool", lambda e, t3=t3, t4=t4: e.tensor_tensor(out=Yim[:, :, :].rearrange("p c k -> p (c k)"), in0=t3[:, :],
                                                             in1=t4[:, :], op=ALU.add), reads=[t3b, t4b], writes=[Yb])
        for pr in range(2):
            Pk, Pkb = bank("A", [0, 1])
            for cl in range(2):
                c4 = pr * 2 + cl
                k.op("pe", lambda e, Pk=Pk, cl=cl, c4=c4: e.matmul(
                    Pk[:, cl * 256:(cl + 1) * 256], lhsT=Yre[:, c4, :], rhs=FcatI1, start=True, stop=False,
                    skip_group_check=True), reads=[Yb, dftb], writes=[Pkb])
                k.op("pe", lambda e, Pk=Pk, cl=cl, c4=c4: e.matmul(
                    Pk[:, cl * 256:(cl + 1) * 256], lhsT=Yim[:, c4, :], rhs=FcatI2, start=False, stop=True,
                    skip_group_check=True), reads=[Yb, dftb], writes=[Pkb])
            cmul_from_psum(Pk, Pkb, -1, Qre, Qim, Qb, pr)
        yo, yob = PS[6], PSb[6]
        k.op("pe", lambda e: e.matmul(yo[0:64, :], lhsT=DFT[:, 0:64], rhs=Qre[:, :, :].rearrange("p c k -> p (c k)"),
                                      start=True, stop=False), reads=[Qb, dftb], writes=[yob])
        k.op("pe", lambda e: e.matmul(yo[0:64, :], lhsT=DFT[:, 128:192], rhs=Qim[:, :, :].rearrange("p c k -> p (c k)"),
                                      start=False, stop=True), reads=[Qb, dftb], writes=[yob])
        ys, ysb = tmp()
        k.op("act", lambda e, ys=ys: e.activation(out=ys[0:64, :], in_=yo[0:64, :], func=AF.Copy, scale=1.0 / NFFT),
             reads=[yob], writes=[ysb])
        dst = bass.AP(tensor=cs, offset=ch0 * S, ap=[[128, 64], [S, 4], [1, 128]])
        k.dma("sp", [(dst, ys[0:64, :].rearrange("p (c k) -> p c k", c=4))], reads=[ysb], sembuf=ysb, sink=csb)

    k.wait_all("sp", [csb])
    k.dma("sp", [(IN[:, 0:S], cs.ap())], writes=[INb])
    ob = Sink("out")
    for s0 in range(0, S, 512):
        t, tb = tmp()
        k.op("dve", lambda e, t=t, s0=s0: e.tensor_scalar(out=t[:, :], in0=Z[:, s0:s0 + 512], scalar1=CF[:, 12:13],
                                                          scalar2=None, op0=ALU.mult), reads=[Zb, cb], writes=[tb])
        k.op("dve", lambda e, t=t, s0=s0: e.scalar_tensor_tensor(out=t[:, :], in0=IN[:, s0:s0 + 512], scalar=SM[:, 8:9],
                                                                 in1=t[:, :], op0=ALU.mult, op1=ALU.add),
             reads=[INb, SMb, tb], writes=[tb])
        k.op("pool", lambda e, t=t, s0=s0: e.tensor_tensor(out=t[:, :], in0=t[:, :], in1=X0[:, s0:s0 + 512], op=ALU.mult),
             reads=[tb, X0b], writes=[tb])
        k.dma("sp", [(y_d[:, s0:s0 + 512], t[:, :])], reads=[tb], sembuf=tb, sink=ob)
    k.wait_all("sp", [ob])
    k.close()
    return nc


def host_consts_LB(cq):
    f32 = np.float32
    L = S
    m = np.arange(NFFT)
    lag = np.where(m < L, m, NFFT - m)
    lag = np.where(m == L, 0, lag)
    t_all = np.linspace(0.0, 1.0, L, dtype=f32)
    t = t_all[lag]
    w = (f32(2.0 * math.pi / L) * lag.astype(f32)).astype(f32)
    fr = np.linspace(1e-4, 15, 16, dtype=f32)
    zf = np.concatenate([t[:, None], np.cos(fr[None, :] * w[:, None]), -np.sin(fr[None, :] * w[:, None])], axis=-1).astype(f32)
    zemb = np.ascontiguousarray(zf.reshape(128, 128, 33).transpose(2, 1, 0).reshape(33, NFFT))
    dmin, dmax = math.log(1e-2) / 0.3, math.log(1e-2) / 1.5
    deltas = np.abs(np.linspace(dmin, dmax, 512, dtype=f32))[cq * 128:(cq + 1) * 128]
    win = np.exp(-t[:, None] * deltas[None, :]).astype(f32)
    win[L] = 0.0
    win = np.ascontiguousarray(win.reshape(128, 128, 128))
    n = np.arange(128)
    ang = 2.0 * np.pi * np.outer(n, n) / 128.0
    fre, fim = np.cos(ang), -np.sin(ang)
    dftc = np.concatenate([fre, fim, fre, -fim], axis=1).astype(f32)
    ang2 = 2.0 * np.pi * np.outer(n, n) / NFFT
    tw = np.stack([np.cos(ang2), -np.sin(ang2)], 0).astype(f32)
    tw = np.ascontiguousarray(np.broadcast_to(tw[:, None], (2, 2, 128, 128)).transpose(2, 0, 1, 3))
    return zemb, win, dftc, tw


def prep_LB(inp, hco_all):
    maps = []
    cache = {}
    for core in range(NCORE):
        b, cq = core // 4, core % 4
        if cq not in cache:
            cache[cq] = host_consts_LB(cq)
        zemb, win, dftc, tw = cache[cq]
        hc3 = np.zeros((3, 128, S), np.float32)
        for part in range(3):
            for src in range(4):
                hc3[part, :, src * TOK:(src + 1) * TOK] = hco_all[b * 4 + src][part * 4 + cq]
        cf2 = np.zeros((128, 32), np.float32)
        cw = np.asarray(inp["d_conv_w"], np.float32)[0]
        cbias = np.asarray(inp["d_conv_b"], np.float32)[0]
        for part in range(3):
            sl = slice(part * 512 + cq * 128, part * 512 + (cq + 1) * 128)
            cf2[:, part * 4:part * 4 + 3] = cw[:, sl].T
            cf2[:, part * 4 + 3] = cbias[sl]
        cf2[:, 12] = np.asarray(inp["d_skip"], np.float32)[0][cq * 128:(cq + 1) * 128]
        mlp = np.zeros((64, 456), np.float32)
        mlp[0:33, 0:64] = np.asarray(inp["d_f_w1"], np.float32)[0]
        mlp[:, 64:128] = np.asarray(inp["d_f_w2"], np.float32)[0]
        mlp[:, 128:192] = np.asarray(inp["d_f_w3"], np.float32)[0]
        mlp[:, 192] = np.asarray(inp["d_f_b1"], np.float32)[0]
        mlp[:, 193] = np.asarray(inp["d_f_b2"], np.float32)[0]
        mlp[:, 194] = np.asarray(inp["d_f_b3"], np.float32)[0]
        mlp[:, 195] = np.asarray(inp["d_f_freq"], np.float32)[0]
        w4 = np.asarray(inp["d_f_w4"], np.float32)[0]
        mlp[:, 200:328] = w4[:, cq * 128:(cq + 1) * 128]
        mlp[:, 328:456] = w4[:, 512 + cq * 128:512 + (cq + 1) * 128]
        maps.append(dict(hc3=hc3, cf2=cf2, mlpw=mlp, zemb=zemb, win=win, dftc=dftc, tw=tw))
    return maps


def emit_attn1(P, C, QT, QTb, KT, KTb, VA, VAb, MIX, MIXb):
    k = P.k
    for jq in range(4):
        q0 = jq * 512
        for g in range(2):
            gs = slice(g * 64, (g + 1) * 64)
            ds_ = slice((1 - g) * 64, (2 - g) * 64)
            for hh in range(4):
                po, pob = P.bank("o", [4, 5])
                for kb in range(64):
                    ps, psb = P.bank("h1", [0, 1, 2, 3])
                    k.op("pe", lambda e, ps=ps, kb=kb, gs=gs, hh=hh, q0=q0: e.matmul(
                        ps[:, :], lhsT=KT[gs, kb * 128:(kb + 1) * 128], rhs=QT[gs, hh, q0:q0 + 512],
                        start=True, stop=True), reads=[KTb, QTb], writes=[psb])
                    pt, ptb = C.tmpbf()
                    k.op("act", lambda e, ps=ps, pt=pt: e.activation(out=pt[:, :], in_=ps[:, :], func=AF.Exp,
                                                                     scale=HD ** -0.5), reads=[psb], writes=[ptb])
                    k.op("pe", lambda e, po=po, pt=pt, kb=kb, g=g: e.matmul(
                        po[:, :], lhsT=VA[:, kb, g, :], rhs=pt[:, :], start=(kb == 0), stop=(kb == 63)),
                        reads=[VAb, ptb], writes=[pob])
                rd, rdb = C.tmp()
                k.op("dve", lambda e, rd=rd, po=po, ds_=ds_: e.reciprocal(out=rd[ds_, :], in_=po[ds_, :]),
                     reads=[pob], writes=[rdb])
                k.op("dve", lambda e, rd=rd, po=po, gs=gs, ds_=ds_, hh=hh, q0=q0: e.tensor_tensor(
                    out=MIX[gs, hh, q0:q0 + 512], in0=po[gs, :], in1=rd[ds_, :], op=ALU.mult),
                    reads=[pob, rdb], writes=[MIXb[jq]])


def build_LC():
    nc = bass.Bass("TRN2", target_bir_lowering=False)
    k = KB(nc)
    dt_in = lambda name, shape, dt=F32: nc.dram_tensor(name, list(shape), dt, kind="ExternalInput").ap()
    xT_d = dt_in("xT", [128, 8, 2048])
    mod_d = dt_in("modi", [128, 96])
    q_d = dt_in("q", [128, 4, 2048], BF16)
    k_d = dt_in("kk", [128, S], BF16)
    v_d = dt_in("v", [64, 128, 128], BF16)
    y_d = dt_in("yh", [128, 4, 2048])
    w1_d = dt_in("w1", [1024, DFFP])
    w3_d = dt_in("w3", [1024, DFFP])
    w2_d = dt_in("w2", [DFFP, 1024])
    wout_d = dt_in("wout", [1024, 1024])
    xo_d = nc.dram_tensor("xo", [128, 8, 2048], F32, kind="ExternalOutput").ap()

    P = Prog(nc, k)
    C = Common()
    C.tmp = RotTmp(k, "tf", 5, F32)
    C.tmpbf = RotTmp(k, "tb", 4, BF16)
    C.RS = k.sb("RS", [128, 512], F32)
    C.RSb = Buf("RS")
    C.cb = Buf("consts")
    C.modb = Buf("mod")
    C.ones_bf = k.sb("ones_bf", [128, 128], BF16)
    C.eps = k.sb("eps", [128, 1], F32)
    MOD = k.sb("mod", [128, 96], F32)
    XT = k.sb("XT", [128, 8, 2048], F32)
    AR = k.sb("AR", [128, 32768], BF16)
    MIXt = k.sb("MIX", [128, 8, 2048], BF16)
    VA = AR[:, 0:16384].rearrange("p (b g d) -> p b g d", b=64, g=2)
    KT = AR[:, 16384:24576]
    QT = AR[:, 24576:32768].rearrange("p (h t) -> p h t", h=4)
    Ht = AR[:, 0:16384].rearrange("p (c t) -> p c t", c=8)
    U = AR[:, 16384:16384 + 6 * 2048].rearrange("p (f t) -> p f t", f=6)

    tiles4 = [(0, 512), (512, 512), (1024, 512), (1536, 512)]
    XB = [Buf(f"x{j}") for j in range(4)]

    class XAct:
        tiles = tiles4
        b = XB

        def ap(self, c, j):
            return XT[:, c, j * 512:(j + 1) * 512]
    X4 = XAct()
    H4 = Act(Ht, tiles4, "h")
    Ub = [[Buf(f"u{f}_{j}") for j in range(4)] for f in range(6)]
    MIXb = [Buf(f"mix{j}") for j in range(4)]
    QTb, KTb, VAb = Buf("qt"), Buf("kt"), Buf("va")

    k.dma("sp", [(MOD[:], mod_d)], writes=[C.modb])
    k.dma("sp", [(QT, q_d)], writes=[QTb])
    k.dma("sp", [(KT, k_d)], writes=[KTb])
    k.op("dve", lambda e: e.memset(C.ones_bf[:], 1.0), writes=[C.cb])
    k.op("dve", lambda e: e.memset(C.eps[:], EPS), writes=[C.cb])
    k.op("dve", lambda e: e.memset(VA[:, :, 0, 64:128], 1.0), writes=[VAb])
    k.op("dve", lambda e: e.memset(VA[:, :, 1, 0:64], 1.0), writes=[VAb])
    vsrc = v_d.rearrange("b p d -> p b d")
    k.dma("sp", [(VA[:, :, 0, 0:64], vsrc[:, :, 0:64]), (VA[:, :, 1, 64:128], vsrc[:, :, 64:128])], writes=[VAb])
    for j in range(4):
        k.dma("sp", [(XT[:, :, j * 512:(j + 1) * 512], xT_d[:, :, j * 512:(j + 1) * 512])], writes=[XB[j]])
    for j in range(4):
        k.dma("pool", [(MIXt[:, 4:8, j * 512:(j + 1) * 512], y_d[:, :, j * 512:(j + 1) * 512])], writes=[MIXb[j]])

    def mcol(i, kind):
        return MOD[:, (i * 3 + kind) * 8:(i * 3 + kind + 1) * 8]

    emit_attn1(P, C, QT, QTb, KT, KTb, VA, VAb, MIXt, MIXb)
    emit_outproj(P, C, X4, lambda fc, c0, w: MIXt[:, fc, c0:c0 + w], MIXb, wout_d, mcol(1, 2))
    k.barrier()
    emit_adaln(P, X4, H4, MOD[:, 72 + 16:72 + 24], mcol(2, 0), C)
    emit_ffn(P, X4, H4, U, Ub, w1_d, w3_d, w2_d, mcol(2, 2), C)
    ob = Sink("out")
    for j in range(4):
        k.dma("sp", [(xo_d[:, :, j * 512:(j + 1) * 512], XT[:, :, j * 512:(j + 1) * 512])], reads=[XB[j]],
              sembuf=XB[j], sink=ob)
    k.wait_all("sp", [ob])
    k.close()
    return nc


def prep_LC(inp, la_res, lb_res, shared):
    maps = []
    for core in range(NCORE):
        b, qd = core // 4, core % 4
        kk = np.concatenate([la_res[b * 4 + s]["ko"] for s in range(4)], axis=1)
        v = np.concatenate([la_res[b * 4 + s]["vo"] for s in range(4)], axis=0)
        yh = np.stack([lb_res[b * 4 + cq]["yT"][:, qd * TOK:(qd + 1) * TOK] for cq in range(4)], axis=1)
        maps.append(dict(xT=la_res[core]["xo"], modi=la_res[core]["modo"], q=la_res[core]["qo"],
                         kk=np.ascontiguousarray(kk), v=np.ascontiguousarray(v), yh=np.ascontiguousarray(yh),
                         w1=shared["w1"][1, 1], w3=shared["w3"][1, 1], w2=shared["w2"][1, 1], wout=shared["wout"][1]))
    return maps


_CACHE = {}


FUSED = True


def kernel(**inputs):
    inp = {kk: np.asarray(v) for kk, v in inputs.items()}
    cores = list(range(NCORE))
    if FUSED:
        if "fused" not in _CACHE:
            _CACHE["fused"] = build_fused()
        maps = prep_fused(inp)
        res = run_bass_kernel_spmd(_CACHE["fused"], maps, core_ids=cores).results
        out = np.zeros((2, S, D), np.float32)
        for core in cores:
            b, qd = core // 4, core % 4
            xo = np.asarray(res[core]["xo"], np.float32)
            out[b, qd * TOK:(qd + 1) * TOK] = xo.transpose(2, 1, 0).reshape(TOK, D)
        return out
    if "la" not in _CACHE:
        _CACHE["la"] = build_LA(True)
        _CACHE["lb"] = build_LB()
        _CACHE["lc"] = build_LC()
    la_maps = prep_LA(inp)
    la = run_bass_kernel_spmd(_CACHE["la"], la_maps, core_ids=cores).results
    lb_maps = prep_LB(inp, [la[c]["hco"] for c in cores])
    lb = run_bass_kernel_spmd(_CACHE["lb"], lb_maps, core_ids=cores).results
    lc_maps = prep_LC(inp, la, lb, la_maps[0])
    lc = run_bass_kernel_spmd(_CACHE["lc"], lc_maps, core_ids=cores).results
    out = np.zeros((2, S, D), np.float32)
    for core in cores:
        b, qd = core // 4, core % 4
        xo = np.asarray(lc[core]["xo"], np.float32)
        out[b, qd * TOK:(qd + 1) * TOK] = xo.transpose(2, 1, 0).reshape(TOK, D)
    return out


U32 = mybir.dt.uint32


def kb_gather(k, dst_ap, src_dram_ap, idx_ap, reads=(), writes=()):
    sbf = writes[0]
    if sbf.dsem is None:
        sbf.dsem = {}
        sbf.dcnt = {}
    if True not in sbf.dsem:
        sbf.dsem[True] = ("d", id(sbf), True)
        sbf.dcnt[True] = 0
        k.sems[sbf.dsem[True]] = k._newsem(f"d_{k.nsem}")
    k._wait("pool", k._deps(reads, writes))
    k.nc.gpsimd.indirect_dma_start(out=dst_ap, out_offset=None, in_=src_dram_ap,
                                   in_offset=bass.IndirectOffsetOnAxis(ap=idx_ap, axis=0)
                                   ).then_inc(k.sems[sbf.dsem[True]], 16)
    sbf.dcnt[True] += 16
    tok = (sbf.dsem[True], sbf.dcnt[True])
    k._commit(tok, reads, writes)
    return tok


class CC:
    n = 0

    def __init__(self, k, groups, fake):
        self.k, self.groups, self.fake = k, groups, fake
        self.pool_sems = [] if fake else [k._newsem(f"cc{i}") for i in range(16)]
        self.alltoks = {}

    def allgather(self, src_t, dst_t, reads, dstb, sinks=(), sl=None):
        k = self.k
        sap = src_t.ap() if sl is None else src_t.ap()[sl]
        dap = dst_t.ap() if sl is None else dst_t.ap()[sl]
        rows = sap.shape[0]
        k.wait_all("sp" if self.fake else "pool", list(sinks))
        if self.fake:
            pairs = [(dap[r * rows:(r + 1) * rows, :], sap) for r in range(4)]
            k.dma("sp", pairs, reads=reads, writes=[dstb])
            return
        CC.n += 1
        key = ("cc", CC.n)
        k.sems[key] = self.pool_sems.pop(0)
        k._wait("pool", k._deps(reads, [dstb]))
        k.nc.gpsimd.collective_compute("AllGather", ALU.bypass, replica_groups=self.groups,
                                       ins=[sap.opt()], outs=[dap.opt()]).then_inc(k.sems[key])
        k._commit((key, 1), reads, [dstb])
        self.alltoks.setdefault(id(dstb), Sink("cc")).toks[key] = 1

    def sink(self, dstb):
        return self.alltoks.get(id(dstb), Sink("none"))


def emit_filter(P, C, F, kc_s, kcsb):
    k = P.k
    tmp, MLP, SM = C.tmp, F.MLP, F.SM
    cb = C.cb
    W1, W2, W3 = MLP[0:33, 0:64], MLP[:, 64:128], MLP[:, 128:192]
    W4 = MLP[:, 200:456].rearrange("p (d c) -> p d c", d=2)
    SMb, ACC, ACCb = F.SMb, F.ACC, F.ACCb
    k.op("dve", lambda e: e.memset(SM[:, 0:1], math.pi / 2), writes=[SMb])
    k.op("dve", lambda e: e.memset(SM[:, 1:2], EPS), writes=[SMb])
    k.op("dve", lambda e: e.tensor_scalar(out=SM[0:64, 2:3], in0=MLP[:, 195:196], scalar1=0.25, scalar2=None, op0=ALU.mult),
         reads=[cb], writes=[SMb])
    for i in range(3):
        k.op("dve", lambda e, i=i: e.tensor_tensor(out=SM[0:64, 3 + i:4 + i], in0=MLP[:, 192 + i:193 + i], in1=SM[0:64, 2:3],
                                                   op=ALU.mult), reads=[cb, SMb], writes=[SMb])
    k.op("dve", lambda e: e.memset(ACC[:], 0.0), writes=[ACCb])

    def sin4(ps, psb, li, out, outb):
        s1, s1b = tmp()
        k.op("act", lambda e: e.activation(out=s1[0:64, :], in_=ps[0:64, :], func=AF.Sin, bias=SM[0:64, 3 + li:4 + li],
                                           scale=SM[0:64, 2:3]), reads=[psb, SMb], writes=[s1b])
        a1, a1b = tmp()
        k.op("act", lambda e: e.activation(out=a1[0:64, :], in_=ps[0:64, :], func=AF.Abs, bias=SM[0:64, 3 + li:4 + li],
                                           scale=SM[0:64, 2:3]), reads=[psb, SMb], writes=[a1b])
        k.op("act", lambda e: e.activation(out=a1[0:64, :], in_=a1[0:64, :], func=AF.Sin, bias=SM[0:64, 0:1], scale=-1.0),
             reads=[a1b, SMb], writes=[a1b])
        k.op("dve", lambda e: e.tensor_tensor(out=a1[0:64, :], in0=a1[0:64, :], in1=s1[0:64, :], op=ALU.mult),
             reads=[a1b, s1b], writes=[a1b])
        k.op("dve", lambda e: e.tensor_tensor(out=s1[0:64, :], in0=s1[0:64, :], in1=s1[0:64, :], op=ALU.mult),
             reads=[s1b], writes=[s1b])
        k.op("dve", lambda e: e.tensor_scalar(out=s1[0:64, :], in0=s1[0:64, :], scalar1=-2.0, scalar2=1.0,
                                              op0=ALU.mult, op1=ALU.add), reads=[s1b], writes=[s1b])
        k.op("dve", lambda e: e.scalar_tensor_tensor(out=out[0:64, :], in0=a1[0:64, :], scalar=4.0, in1=s1[0:64, :],
                                                     op0=ALU.mult, op1=ALU.mult), reads=[a1b, s1b], writes=[outb])
        return out, outb

    for c in range(32):
        yield c
        ze, zeb = tmp()
        k.dma("sp", [(ze[0:33, :], F.zemb_d[:, c * 512:(c + 1) * 512])], writes=[zeb])
        ps, psb = P.bank("m", [6, 7])
        k.op("pe", lambda e, ps=ps, ze=ze: e.matmul(ps[0:64, :], lhsT=W1, rhs=ze[0:33, :], start=True, stop=True),
             reads=[zeb, cb], writes=[psb])
        h, hb = sin4(ps, psb, 0, F.HF[0], F.HFb[0])
        yield c
        for li, W in ((1, W2), (2, W3)):
            ps, psb = P.bank("m", [6, 7])
            k.op("pe", lambda e, ps=ps, h=h, W=W: e.matmul(ps[0:64, :], lhsT=W, rhs=h[0:64, :], start=True, stop=True),
                 reads=[hb, cb], writes=[psb])
            h, hb = sin4(ps, psb, li, F.HF[li % 2], F.HFb[li % 2])
            yield c
        wt, wtb = tmp()
        k.dma("sp", [(wt[:, :].rearrange("p (n c) -> p n c", n=4), F.win_d[:, c * 4:(c + 1) * 4, :])], writes=[wtb])
        ps4, ps4b = P.bank("m", [6, 7])
        for n2l in range(4):
            for d in range(2):
                k.op("pe", lambda e, ps4=ps4, h=h, n2l=n2l, d=d: e.matmul(
                    ps4[d * 64:(d + 1) * 64, n2l * 128:(n2l + 1) * 128],
                    lhsT=h[0:64, n2l * 128 + d * 64:n2l * 128 + (d + 1) * 64], rhs=W4[:, d, :],
                    start=True, stop=True, skip_group_check=True), reads=[hb, cb], writes=[ps4b])
        kc, kcb = tmp()
        k.op("dve", lambda e, kc=kc, ps4=ps4, wt=wt: e.tensor_tensor(out=kc[:, :], in0=ps4[:, :], in1=wt[:, :], op=ALU.mult),
             reads=[ps4b, wtb], writes=[kcb])
        st, stb = C.tmpbf()
        k.op("act", lambda e, kc=kc, st=st: e.activation(out=st[:, :], in_=kc[:, :], func=AF.Copy), reads=[kcb], writes=[stb])
        k.dma("sp", [(kc_s.ap()[:, c * 4:(c + 1) * 4, :], st[:, :].rearrange("p (n c) -> p n c", n=4))], reads=[stb],
              sembuf=stb, sink=kcsb)
        sq, sqb = tmp()
        k.op("dve", lambda e, kc=kc, sq=sq: e.tensor_tensor(out=sq[:, :], in0=kc[:, :], in1=kc[:, :], op=ALU.mult),
             reads=[kcb], writes=[sqb])
        rd, rdb = tmp()
        k.op("dve", lambda e, sq=sq, rd=rd: e.tensor_reduce(
            out=rd[:, 0:128], in_=sq[:, :].rearrange("p (n c) -> p c n", n=4), axis=mybir.AxisListType.X, op=ALU.add),
            reads=[sqb], writes=[rdb])
        k.op("dve", lambda e, rd=rd: e.tensor_tensor(out=ACC[:], in0=ACC[:], in1=rd[:, 0:128], op=ALU.add),
             reads=[rdb, ACCb], writes=[ACCb])
    ps, psb = P.bank("m", [6, 7])
    k.op("pe", lambda e, ps=ps: e.matmul(ps[:, 0:128], lhsT=C.ones_f[:, :], rhs=ACC[:], start=True, stop=True),
         reads=[ACCb, cb], writes=[psb])
    k.op("act", lambda e, ps=ps: e.activation(out=F.RSrow[:], in_=ps[:, 0:128], func=AF.Sqrt, bias=SM[:, 1:2], scale=1.0),
         reads=[psb, SMb], writes=[F.RSb])
    k.op("dve", lambda e: e.reciprocal(out=F.RSrow[:], in_=F.RSrow[:]), reads=[F.RSb], writes=[F.RSb])


def emit_fftconv(P, C, F, KC, KCb, zs2, zs2b, c_src, csink, V, after_group=None):
    k = P.k
    tmp = C.tmp
    DFT, TW, dftb, cb = F.DFT, F.TW, F.dftb, C.cb
    Fre, Fim, nFim = DFT[:, 0:128], DFT[:, 128:256], DFT[:, 384:512]
    Fcat, FcatI2, FcatI1 = DFT[:, 0:256], DFT[:, 128:384], DFT[:, 256:512]
    TWre2, TWim2 = TW[:, 0], TW[:, 1]
    Bre, Bim, Yre, Yim, Qre, Qim, Hre, Him = V.Bre, V.Bim, V.Yre, V.Yim, V.Qre, V.Qim, V.Hre, V.Him
    Bb, Hb, Yb, Qb = Buf("B"), Buf("H"), Buf("Y"), Buf("Q")
    PS, PSb = P.PS, P.PSb

    def cmul_from_psum(A, Ab, sign, outre, outim, outb, pr):
        A4 = A[:, :].rearrange("p (c r k) -> p c r k", c=2, r=2)
        Are, Aim = A4[:, :, 0, :], A4[:, :, 1, :]
        v = lambda t: t[:, 0:256].rearrange("p (c k) -> p c k", c=2)
        t1, t1b = tmp()
        t2, t2b = tmp()
        k.op("dve", lambda e: e.tensor_tensor(out=v(t1), in0=Are, in1=TWre2, op=ALU.mult), reads=[Ab, cb], writes=[t1b])
        k.op("dve", lambda e: e.tensor_tensor(out=v(t2), in0=Aim, in1=TWim2, op=ALU.mult), reads=[Ab, cb], writes=[t2b])
        k.op("pool", lambda e: e.tensor_tensor(out=outre[:, pr * 2:pr * 2 + 2, :], in0=v(t1), in1=v(t2),
                                               op=(ALU.subtract if sign > 0 else ALU.add)),
             reads=[t1b, t2b], writes=[outb])
        t3, t3b = tmp()
        t4, t4b = tmp()
        k.op("dve", lambda e: e.tensor_tensor(out=v(t3), in0=Aim, in1=TWre2, op=ALU.mult), reads=[Ab, cb], writes=[t3b])
        k.op("dve", lambda e: e.tensor_tensor(out=v(t4), in0=Are, in1=TWim2, op=ALU.mult), reads=[Ab, cb], writes=[t4b])
        k.op("pool", lambda e: e.tensor_tensor(out=outim[:, pr * 2:pr * 2 + 2, :], in0=v(t3), in1=v(t4),
                                               op=(ALU.add if sign > 0 else ALU.subtract)),
             reads=[t3b, t4b], writes=[outb])

    def fft_fwd(lhs_of, srcb, krows, xre, xreb, xim, ximb):
        for pr in range(2):
            A, Ab = P.bank("A", [0, 1])
            for cl in range(2):
                c4 = pr * 2 + cl
                k.op("pe", lambda e, A=A, cl=cl, c4=c4: e.matmul(
                    A[:, cl * 256:(cl + 1) * 256], lhsT=lhs_of(c4), rhs=Fcat[0:krows, :],
                    start=True, stop=True, skip_group_check=True), reads=[srcb, dftb], writes=[Ab])
            cmul_from_psum(A, Ab, +1, Bre, Bim, Bb, pr)
        bre = Bre.rearrange("p c k -> p (c k)")
        bim = Bim.rearrange("p c k -> p (c k)")
        k.op("pe", lambda e: e.matmul(xre[:, :], lhsT=Fre, rhs=bre, start=True, stop=False), reads=[Bb, dftb], writes=[xreb])
        k.op("pe", lambda e: e.matmul(xre[:, :], lhsT=nFim, rhs=bim, start=False, stop=True), reads=[Bb, dftb], writes=[xreb])
        k.op("pe", lambda e: e.matmul(xim[:, :], lhsT=Fim, rhs=bre, start=True, stop=False), reads=[Bb, dftb], writes=[ximb])
        k.op("pe", lambda e: e.matmul(xim[:, :], lhsT=Fre, rhs=bim, start=False, stop=True), reads=[Bb, dftb], writes=[ximb])

    for g in range(32):
        ch0 = g * 4
        ZC, ZCb = V.ZC[g % 2], V.ZCb[g % 2]
        src = bass.AP(tensor=zs2, offset=ch0 * S, ap=[[128, 64], [S, 4], [1, 128]])
        k.dma("sp", [(ZC[0:64], src)], reads=[zs2b], writes=[ZCb])
        fft_fwd(lambda c4, ch0=ch0: KC[:, :, ch0 + c4], KCb, 128, PS[2], PSb[2], PS[3], PSb[3])
        rsb_ap = F.RSrow[:, ch0:ch0 + 4].unsqueeze(2).broadcast_to([128, 4, 128])
        k.op("dve", lambda e, rsb_ap=rsb_ap: e.tensor_tensor(out=Hre.rearrange("p (c k) -> p c k", c=4),
                                                              in0=PS[2][:, :].rearrange("p (c k) -> p c k", c=4),
                                                              in1=rsb_ap, op=ALU.mult), reads=[PSb[2], F.RSb], writes=[Hb])
        k.op("dve", lambda e, rsb_ap=rsb_ap: e.tensor_tensor(out=Him.rearrange("p (c k) -> p c k", c=4),
                                                              in0=PS[3][:, :].rearrange("p (c k) -> p c k", c=4),
                                                              in1=rsb_ap, op=ALU.mult), reads=[PSb[3], F.RSb], writes=[Hb])
        fft_fwd(lambda c4, ZC=ZC: ZC[0:64, c4, :], ZCb, 64, PS[4], PSb[4], PS[5], PSb[5])
        t1, t1b = tmp()
        t2, t2b = tmp()
        k.op("dve", lambda e, t1=t1: e.tensor_tensor(out=t1[:, :], in0=PS[4][:, :], in1=Hre, op=ALU.mult),
             reads=[PSb[4], Hb], writes=[t1b])
        k.op("dve", lambda e, t2=t2: e.tensor_tensor(out=t2[:, :], in0=PS[5][:, :], in1=Him, op=ALU.mult),
             reads=[PSb[5], Hb], writes=[t2b])
        k.op("pool", lambda e, t1=t1, t2=t2: e.tensor_tensor(out=Yre.rearrange("p c k -> p (c k)"), in0=t1[:, :],
                                                             in1=t2[:, :], op=ALU.subtract), reads=[t1b, t2b], writes=[Yb])
        t3, t3b = tmp()
        t4, t4b = tmp()
        k.op("dve", lambda e, t3=t3: e.tensor_tensor(out=t3[:, :], in0=PS[4][:, :], in1=Him, op=ALU.mult),
             reads=[PSb[4], Hb], writes=[t3b])
        k.op("dve", lambda e, t4=t4: e.tensor_tensor(out=t4[:, :], in0=PS[5][:, :], in1=Hre, op=ALU.mult),
             reads=[PSb[5], Hb], writes=[t4b])
        k.op("pool", lambda e, t3=t3, t4=t4: e.tensor_tensor(out=Yim.rearrange("p c k -> p (c k)"), in0=t3[:, :],
                                                             in1=t4[:, :], op=ALU.add), reads=[t3b, t4b], writes=[Yb])
        for pr in range(2):
            Pk, Pkb = P.bank("A", [0, 1])
            for cl in range(2):
                c4 = pr * 2 + cl
                k.op("pe", lambda e, Pk=Pk, cl=cl, c4=c4: e.matmul(
                    Pk[:, cl * 256:(cl + 1) * 256], lhsT=Yre[:, c4, :], rhs=FcatI1, start=True, stop=False,
                    skip_group_check=True), reads=[Yb, dftb], writes=[Pkb])
                k.op("pe", lambda e, Pk=Pk, cl=cl, c4=c4: e.matmul(
                    Pk[:, cl * 256:(cl + 1) * 256], lhsT=Yim[:, c4, :], rhs=FcatI2, start=False, stop=True,
                    skip_group_check=True), reads=[Yb, dftb], writes=[Pkb])
            cmul_from_psum(Pk, Pkb, -1, Qre, Qim, Qb, pr)
        yo, yob = PS[6], PSb[6]
        k.op("pe", lambda e: e.matmul(yo[0:64, :], lhsT=DFT[:, 0:64], rhs=Qre.rearrange("p c k -> p (c k)"),
                                      start=True, stop=False), reads=[Qb, dftb], writes=[yob])
        k.op("pe", lambda e: e.matmul(yo[0:64, :], lhsT=DFT[:, 128:192], rhs=Qim.rearrange("p c k -> p (c k)"),
                                      start=False, stop=True), reads=[Qb, dftb], writes=[yob])
        ys, ysb = tmp()
        k.op("act", lambda e, ys=ys: e.activation(out=ys[0:64, :], in_=yo[0:64, :], func=AF.Copy, scale=1.0 / NFFT),
             reads=[yob], writes=[ysb])
        dst = bass.AP(tensor=c_src, offset=ch0 * S, ap=[[128, 64], [S, 4], [1, 128]])
        k.dma("sp", [(dst, ys[0:64, :].rearrange("p (c k) -> p c k", c=4))], reads=[ysb], sembuf=ysb,
              sink=(csink[g // 4] if isinstance(csink, list) else csink))
        if after_group is not None:
            after_group(g)


def emit_attn1f(P, C, q_s, KT, KTb, VB, VBb, QS, QSb, MIX, MIXb, SK=3):
    k = P.k
    its = [(jq, g, hh, kb) for jq in range(4) for g in range(2) for hh in range(4) for kb in range(64)]
    st = {}
    cur = {}

    def stage_a(i):
        jq, g, hh, kb = its[i]
        q0 = jq * 512
        QT, QTb = QS[jq % 2], QSb[jq % 2]
        if (g, hh, kb) == (0, 0, 0):
            k.dma("sp", [(QT, q_s.ap()[:, :, q0:q0 + 512])], writes=[QTb])
        gs = slice(g * 64, (g + 1) * 64)
        ps, psb = P.bank("att", [0, 1, 2, 3, 6, 7])
        k.op("pe", lambda e: e.matmul(ps[:, :], lhsT=KT[gs, kb * 128:(kb + 1) * 128], rhs=QT[gs, hh, :],
                                      start=True, stop=True), reads=[KTb, QTb], writes=[psb])
        pt, ptb = C.tmpbf()
        k.op("act", lambda e: e.activation(out=pt[:, :], in_=ps[:, :], func=AF.Exp, scale=HD ** -0.5),
             reads=[psb], writes=[ptb])
        st[i] = (pt, ptb)

    def stage_b(i):
        jq, g, hh, kb = its[i]
        q0 = jq * 512
        gs = slice(g * 64, (g + 1) * 64)
        ds_ = slice((1 - g) * 64, (2 - g) * 64)
        if kb == 0:
            cur["po"] = P.bank("o", [4, 5])
        po, pob = cur["po"]
        pt, ptb = st.pop(i)
        k.op("pe", lambda e: e.matmul(po[:, :], lhsT=VB[:, kb, g * 64:g * 64 + 128], rhs=pt[:, :],
                                      start=(kb == 0), stop=(kb == 63)), reads=[VBb, ptb], writes=[pob])
        if kb == 63:
            rd, rdb = C.tmp()
            k.op("dve", lambda e: e.reciprocal(out=rd[ds_, :], in_=po[ds_, :]), reads=[pob], writes=[rdb])
            k.op("dve", lambda e: e.tensor_tensor(out=MIX[gs, hh, q0:q0 + 512], in0=po[gs, :], in1=rd[ds_, :],
                                                  op=ALU.mult), reads=[pob, rdb], writes=[MIXb[jq]])

    n = len(its)
    for t in range(n + SK):
        if t < n:
            stage_a(t)
        if t - SK >= 0:
            stage_b(t - SK)


def emit_attn1g(P, C, q_s, KT, KTb, VB, VBb, QS, QSb, MIX, MIXb, extra_p):
    k = P.k
    pb = list(zip(C.tmpbf.t, C.tmpbf.b)) + list(extra_p)
    assert len(pb) >= 8
    pi = [0]
    blocks = [(jq, g, kb) for jq in range(4) for g in range(2) for kb in range(64)]
    st = {}

    def stage_a(bi):
        jq, g, kb = blocks[bi]
        q0 = jq * 512
        QT, QTb = QS[jq % 2], QSb[jq % 2]
        if (g, kb) == (0, 0):
            k.dma("sp", [(QT, q_s.ap()[:, :, q0:q0 + 512])], writes=[QTb])
        gs = slice(g * 64, (g + 1) * 64)
        for hh in range(4):
            ps, psb = P.PS[hh], P.PSb[hh]
            k.op("pe", lambda e: e.matmul(ps[:, :], lhsT=KT[gs, kb * 128:(kb + 1) * 128], rhs=QT[gs, hh, :],
                                          start=True, stop=True), reads=[KTb, QTb], writes=[psb])
            pt, ptb = pb[pi[0] % len(pb)]
            pi[0] += 1
            k.op("act", lambda e: e.activation(out=pt[:, :], in_=ps[:, :], func=AF.Exp, scale=HD ** -0.5),
                 reads=[psb], writes=[ptb])
            st[(bi, hh)] = (pt, ptb)

    def stage_b(bi):
        jq, g, kb = blocks[bi]
        q0 = jq * 512
        gs = slice(g * 64, (g + 1) * 64)
        ds_ = slice((1 - g) * 64, (2 - g) * 64)
        for hh in range(4):
            po, pob = P.PS[4 + hh], P.PSb[4 + hh]
            pt, ptb = st.pop((bi, hh))
            k.op("pe", lambda e: e.matmul(po[:, :], lhsT=VB[:, kb, g * 64:g * 64 + 128], rhs=pt[:, :],
                                          start=(kb == 0), stop=(kb == 63)), reads=[VBb, ptb], writes=[pob])
            if kb == 63:
                rd, rdb = C.tmp()
                k.op("dve", lambda e: e.reciprocal(out=rd[ds_, :], in_=po[ds_, :]), reads=[pob], writes=[rdb])
                k.op("dve", lambda e: e.tensor_tensor(out=MIX[gs, hh, q0:q0 + 512], in0=po[gs, :], in1=rd[ds_, :],
                                                      op=ALU.mult), reads=[pob, rdb], writes=[MIXb[jq]])

    n = len(blocks)
    for t in range(n + 1):
        if t < n:
            stage_a(t)
        if t >= 1:
            stage_b(t - 1)


def build_fused(fake_ag=False, ncore=8, stop_after=None):
    nc = bass.Bass("TRN2", target_bir_lowering=False)
    k = KB(nc)
    groups = [[0, 1, 2, 3], [4, 5, 6, 7]] if ncore == 8 else [[0, 1, 2, 3]]
    cc = CC(k, groups, fake_ag)
    dt_in = lambda name, shape, dt=F32: nc.dram_tensor(name, list(shape), dt, kind="ExternalInput").ap()
    dram = lambda name, shape, dt=F32: nc.dram_tensor(name, list(shape), dt, kind="Internal")
    xT_d = dt_in("xT", [128, 8, 2048])
    xH_d = dt_in("xH", [128, 8, 256])
    cf_d = dt_in("cf", [128, 400])
    oh_d = dt_in("oh", [32, 512])
    vm_d = dt_in("vm", [128, 512])
    rope_d = dt_in("rope", [128, 2, 2048])
    idx_d = dt_in("idx", [128, 24], U32)
    mlp_d = dt_in("mlpw", [64, 456])
    zemb_d = dt_in("zemb", [33, NFFT])
    win_d = dt_in("win", [128, 128, 128])
    dft_d = dt_in("dftc", [128, 512])
    tw_d = dt_in("tw", [128, 2, 2, 128])
    wmod_d = dt_in("wmod", [2, 1024, 9216])
    w1_d = dt_in("w1", [2, 2, 1024, DFFP])
    w3_d = dt_in("w3", [2, 2, 1024, DFFP])
    w2_d = dt_in("w2", [2, 2, DFFP, 1024])
    win_w = dt_in("win_w", [2, 1024, INC])
    wout_d = dt_in("wout", [2, 1024, 1024])
    xo_d = nc.dram_tensor("xo", [128, 8, 2048], F32, kind="ExternalOutput").ap()
    scr = dram("scr", [128, 8 * 512])
    q_s = dram("q_s", [128, 4, 2048], BF16)
    k_src, k_g = dram("k_src", [128, 2048], BF16), dram("k_g", [512, 2048], BF16)
    v_src, v_g = dram("v_src", [2048, 128], BF16), dram("v_g", [8192, 128], BF16)
    hb_src, hb_g = dram("hb_src", [128, 24]), dram("hb_g", [512, 24])
    z_src, z_g = dram("z_src", [4, 128, 2048], BF16), dram("z_g", [4, 512, 2048], BF16)
    zf_s = dram("zf_s", [128, 4, 2048])
    x0_s = dram("x0_s", [128, 4, 2048])
    zs2 = dram("zs2", [128, S], BF16)
    kc_s = dram("kc_s", [128, 128, 128], BF16)
    c_src, c_g = dram("c_src", [8, 16, S]), dram("c_g", [8, 64, S])

    P = Prog(nc, k)
    C = Common()
    C.tmp = RotTmp(k, "tf", 6, F32)
    C.tmpbf = RotTmp(k, "tb", 5, BF16)
    C.RS = k.sb("RS", [128, 512], F32)
    C.RSb = Buf("RS")
    C.cb = Buf("consts")
    C.modb = Buf("mod")
    CF = k.sb("CF", [128, 400], F32)
    IDX = k.sb("IDX", [128, 24], U32)
    C.ones_bf = k.sb("ones_bf", [128, 128], BF16)
    C.bd_bf = k.sb("bd_bf", [128, 128], BF16)
    C.ones_f = k.sb("ones_f", [128, 128], F32)
    C.eps = k.sb("eps", [128, 1], F32)
    condbf = k.sb("condbf", [128, 8], BF16)
    MODV = k.sb("modv", [128, 2, 72], F32)
    AV = k.sb("av", [128, 2, 3, 8], F32)
    EXS = k.sb("exs", [128, 8], F32)
    F = Common()
    F.MLP = k.sb("MLP", [64, 456], F32)
    F.DFT = k.sb("DFT", [128, 512], BF16)
    F.TW = k.sb("TW", [128, 2, 2, 128], F32)
    F.ACC = k.sb("ACC", [128, 128], F32)
    F.SM = k.sb("SM", [128, 16], F32)
    F.RSrow = k.sb("RSrow", [128, 128], F32)
    F.SMb, F.ACCb, F.RSb, F.dftb = Buf("SM"), Buf("ACC"), Buf("RSr"), Buf("dft")
    F.zemb_d, F.win_d = zemb_d, win_d
    F.HF = [k.sb(f"HF{i}", [64, 512], F32) for i in range(2)]
    F.HFb = [Buf("HF0"), Buf("HF1")]
    HBS = k.sb("HBS", [128, 12, 2], F32)
    HG = k.sb("HG", [128, 4, 12, 2], F32)
    HLR = k.sb("HLR", [128, 2, 12], F32)
    XT = k.sb("XT", [128, 8, 2048], F32)
    AR = k.sb("AR", [128, 41984], BF16)

    cT = CF[:, 0:8]
    flags = CF[:, 8:10]
    normg = CF[:, 10:58].rearrange("p (l i c) -> p l i c", l=2, i=3)
    bmod = CF[:, 58:202].rearrange("p (l m) -> p l m", l=2)
    convw = CF[:, 202:214].rearrange("p (c t) -> p c t", c=4)
    convb = CF[:, 214:218]
    gq, gk = CF[:, 218:220], CF[:, 220:222]
    sink = CF[:, 222:230]
    relrep = CF[:, 232:240]
    gq1, gk1 = CF[:, 240:241], CF[:, 241:242]
    dcw = CF[:, 244:280].rearrange("p (c t) -> p c t", c=12)
    dcb = CF[:, 280:292]
    skipv = CF[:, 292:296]
    selL, selR = CF[:, 296:300], CF[:, 300:304]

    Ht = AR[:, 0:18432].rearrange("p (c t) -> p c t", c=8)
    UQ = AR[:, 18432:33792]
    MXf = AR[:, 33792:41984].bitcast(F32)
    XH = MXf[:, 0:2048].rearrange("p (c t) -> p c t", c=8)
    MIXHI0 = AR[:, 33792:41984].rearrange("p (c t) -> p c t", c=4)
    U = UQ[:, 0:6 * 2304].rearrange("p (f t) -> p f t", f=6)
    QT0 = UQ[:, 0:8192].rearrange("p (h t) -> p h t", h=4)
    KT0 = UQ[:, 8192:8192 + 2304]
    VA0 = UQ[:, 10496:10496 + 4608].rearrange("p (b g d) -> p b g d", b=18, g=2)
    S_t = UQ[:, 10496:10496 + 4100].bitcast(F32)
    C.E8 = UQ[:, 0:8192].bitcast(F32).rearrange("p (h m) -> p h m", h=8)
    EXPB = P.WBf[:, :, :].rearrange("p a (b q) -> p (a b) q", q=128).rearrange("p (h b) q -> p h b q", h=8)
    OH = AR[0:32, 0:1024].bitcast(F32)
    VM = AR[:, 1024:2048].bitcast(F32)

    tiles5 = [(0, 512), (512, 512), (1024, 512), (1536, 512), (2048, 256)]
    tiles4 = tiles5[:4]
    XB = [Buf(f"x{j}") for j in range(5)]

    class XAct:
        def __init__(self, tiles):
            self.tiles = tiles
            self.b = XB[:len(tiles)]

        def ap(self, c, j):
            if j < 4:
                return XT[:, c, j * 512:(j + 1) * 512]
            return XH[:, c, :]
    X5, X4 = XAct(tiles5), XAct(tiles4)
    H5 = Act(Ht, tiles5, "h")
    H4 = Act(Ht, tiles4, "h")
    H4.b = H5.b[:4]
    Ub = [[Buf(f"u{f}_{j}") for j in range(5)] for f in range(6)]

    k.dma("sp", [(CF[:], cf_d), (F.MLP[:], mlp_d), (F.TW[:], tw_d), (IDX[:], idx_d)], writes=[C.cb])
    k.dma("pool", [(F.DFT[:], dft_d)], writes=[F.dftb])
    for j in range(4):
        k.dma("sp", [(XT[:, :, j * 512:(j + 1) * 512], xT_d[:, :, j * 512:(j + 1) * 512])], writes=[XB[j]])
    k.dma("sp", [(XH, xH_d)], writes=[XB[4]])
    cst = Buf("cst")
    k.op("dve", lambda e: e.memset(C.ones_bf[:], 1.0), writes=[cst])
    k.op("dve", lambda e: e.memset(C.ones_f[:], 1.0), writes=[cst])
    k.op("dve", lambda e: e.memset(C.eps[:], EPS), writes=[cst])
    k.op("dve", lambda e: e.memset(C.bd_bf[:], 0.0), writes=[cst])
    k.op("dve", lambda e: e.memset(C.bd_bf[0:64, 0:64], 1.0), writes=[cst])
    k.op("dve", lambda e: e.memset(C.bd_bf[64:128, 64:128], 1.0), writes=[cst])
    condb = Buf("cond")
    k.op("act", lambda e: e.activation(out=condbf[:], in_=cT, func=AF.Silu), reads=[C.cb], writes=[condb])
    k.op("act", lambda e: e.activation(out=EXS[:], in_=sink, func=AF.Exp), reads=[C.cb], writes=[cst])
    k.op("dve", lambda e: e.tensor_copy(out=C.eps[:], in_=C.eps[:]), reads=[cst, C.cb], writes=[C.cb])

    def layer_mod(l):
        emit_mod(P, condbf, condb, wmod_d[l], bmod[:, l, :], MODV[:, l, :], C.modb, AV[:, l], normg[:, l])

    def mcol(l, i, kind):
        return MODV[:, l, (i * 3 + kind) * 8:(i * 3 + kind + 1) * 8]

    def mix_ap0(fc, c0, w):
        if fc < 4:
            return Ht[:, fc, c0:c0 + w]
        return MIXHI0[:, fc - 4, c0:c0 + w]

    def finish(extra=()):
        ob = Sink("out")
        k.wait_all("sp", list(extra))
        for j in range(4):
            k.dma("sp", [(xo_d[:, :, j * 512:(j + 1) * 512], XT[:, :, j * 512:(j + 1) * 512])], reads=[XB[j]],
                  sembuf=XB[j], sink=ob)
        k.wait_all("sp", [ob])
        k.close()
        return nc

    kcsb = Sink("kcs")
    fgen = emit_filter(P, C, F, kc_s, kcsb)
    next(fgen)
    hcnt = [0]

    def fhook():
        hcnt[0] += 1
        next(fgen, None)

    layer_mod(0)
    emit_adaln(P, X5, H5, AV[:, 0, 0], mcol(0, 0, 0), C)
    emit_ffn(P, X5, H5, U, Ub, w1_d[0, 0], w3_d[0, 0], w2_d[0, 0], mcol(0, 0, 2), C, hook=fhook)
    for _ in fgen:
        pass
    k.barrier()
    ohb = Buf("oh")
    k.dma("sp", [(OH, oh_d), (VM, vm_d)], writes=[ohb])
    k.op("dve", lambda e: e.tensor_copy(out=C.eps[:], in_=C.eps[:]), reads=[ohb, C.cb], writes=[C.cb])
    scrb = emit_expb(P, C, relrep, C.cb, OH, VM, scr, EXPB, P.WBb[0])
    k.barrier(dbufs=[scrb])
    emit_adaln(P, X5, H5, AV[:, 0, 1], mcol(0, 1, 0), C)
    k.barrier()
    MIXb = [Buf(f"mix{j}") for j in range(4)]
    Sb = Buf("S")
    emit_conv0(P, C, H5, mix_ap0, MIXb, win_w[0], convw, convb, flags, S_t, Sb)
    k.barrier()
    QTb = [Buf(f"qt{j}") for j in range(4)]
    KTb = [Buf(f"kt{j}") for j in range(5)]
    VAb = Buf("va")
    k.op("dve", lambda e: e.memset(VA0[:, 0:16, 0, 64:128], 1.0), writes=[VAb])
    k.op("dve", lambda e: e.memset(VA0[:, 0:16, 1, 0:64], 1.0), writes=[VAb])
    emit_qkv0(P, C, H5, win_w[0], QT0, QTb, KT0, KTb, VA0, VAb, gq, gk, flags)
    k.barrier()
    emit_attn0(P, C, QT0, QTb, KT0, KTb, VA0, VAb, EXPB, P.WBb[0], EXS, Ht, MIXb)
    for i in (1, 2):
        P.WBb[i].r = dict(P.WBb[0].r)
        P.WBb[i].w = P.WBb[0].w
    emit_outproj(P, C, X4, mix_ap0, MIXb, wout_d[0], mcol(0, 1, 2))
    k.barrier()
    emit_adaln(P, X4, H4, AV[:, 0, 2], mcol(0, 2, 0), C)
    emit_ffn(P, X4, H4, U, Ub, w1_d[0, 1], w3_d[0, 1], w2_d[0, 1], mcol(0, 2, 2), C)

    if stop_after == "l0":
        return finish([kcsb])

    layer_mod(1)
    emit_adaln(P, X4, H4, AV[:, 1, 0], mcol(1, 0, 0), C)
    emit_ffn(P, X4, H4, U, Ub, w1_d[1, 0], w3_d[1, 0], w2_d[1, 0], mcol(1, 0, 2), C)
    k.barrier()
    ROPE = UQ[:, 0:8192].bitcast(F32).rearrange("p (a t) -> p a t", a=2)
    HB = AR[:, 26624:26624 + 12300].bitcast(F32).rearrange("p (a t) -> p a t", a=3)
    ropeb, HBb = Buf("rope"), Buf("HB")
    k.dma("sp", [(ROPE, rope_d)], writes=[ropeb])
    emit_adaln(P, X4, H4, AV[:, 1, 1], mcol(1, 1, 0), C)
    W1L = win_w[1]
    hbsb = Buf("hbs")
    for pr in range(6):
        wa, wab = P.load_wa(W1L[:, 768 + pr * 256:768 + (pr + 1) * 256])
        for cl in range(2):
            ch = pr * 2 + cl
            ps, psb = P.bank("h1", [0, 1, 2, 3])
            for ci, col in enumerate((0, 2047)):
                for c in range(8):
                    k.op("pe", lambda e, c=c, ps=ps, wa=wa, cl=cl, ci=ci, col=col: e.matmul(
                        ps[:, ci:ci + 1], lhsT=wa[:, c, cl * 128:(cl + 1) * 128], rhs=Ht[:, c, col:col + 1],
                        start=(c == 0), stop=(c == 7)), reads=[wab] + H4.b, writes=[psb])
            k.op("act", lambda e, ps=ps, ch=ch: e.activation(out=HBS[:, ch, :], in_=ps[:, 0:2], func=AF.Copy),
                 reads=[psb], writes=[hbsb])
    hbsrcb, hbgb, hgb = Buf("hbsrc"), Buf("hbg"), Buf("hg")
    k.dma("sp", [(hb_src.ap(), HBS[:].rearrange("p c t -> p (c t)"))], reads=[hbsb], writes=[hbsrcb])
    cc.allgather(hb_src, hb_g, [hbsrcb], hbgb)
    k.dma("sp", [(HG[:].rearrange("p r c t -> p r (c t)"), hb_g.ap().rearrange("(r p) t -> p r t", p=128))],
          reads=[hbgb], writes=[hgb])
    outs = Sink("l1outs")
    for pr in range(3):
        wa, wab = P.load_wa(W1L[:, pr * 256:(pr + 1) * 256])
        for cl in range(2):
            if pr == 2 and cl == 1:
                break
            for j in range(4):
                c0, w = H4.tiles[j]
                ps, psb = P.bank("h1", [0, 1, 2, 3])
                emit_inproj_tile(P, H4, j, wa, wab, cl, ps, psb)
                qn, qnb = C.tmp()
                emit_qknorm(P, C, ps, psb, w, (gq1 if pr < 2 else gk1), qn[:, 0:w], qnb)
                st, stb = C.tmpbf()
                emit_rope(P, C, qn, qnb, ROPE[:, 0, :], ROPE[:, 1, :], ropeb, c0, w, st[:, 0:w], stb)
                dst = q_s.ap()[:, pr * 2 + cl, c0:c0 + w] if pr < 2 else k_src.ap()[:, c0:c0 + w]
                k.dma("sp", [(dst, st[:, 0:w])], reads=[stb], sembuf=stb, sink=outs)
        if pr == 2:
            for tb in range(16):
                j = tb // 4
                ps, psb = P.bank("o", [4, 5])
                for c in range(8):
                    k.op("pe", lambda e, c=c, tb=tb, ps=ps, wa=wa: e.matmul(
                        ps[:, 0:128], lhsT=Ht[:, c, tb * 128:(tb + 1) * 128], rhs=wa[:, c, 128:256],
                        start=(c == 0), stop=(c == 7)), reads=[wab, H4.b[j]], writes=[psb])
                st, stb = C.tmpbf()
                k.op("act", lambda e, ps=ps, st=st: e.activation(out=st[:, 0:128], in_=ps[:, 0:128], func=AF.Copy),
                     reads=[psb], writes=[stb])
                k.dma("sp", [(v_src.ap()[tb * 128:(tb + 1) * 128, :], st[:, 0:128])], reads=[stb], sembuf=stb, sink=outs)
    kgb, vgb = Buf("kg"), Buf("vg")
    if stop_after == "qkv":
        return finish([outs, kcsb])
    cc.allgather(k_src, k_g, [], kgb, sinks=[outs])
    cc.allgather(v_src, v_g, [], vgb, sinks=[outs])
    for side, sel, colx in ((0, selL, 1), (1, selR, 0)):
        k.op("dve", lambda e, side=side, sel=sel, colx=colx: e.tensor_scalar(
            out=HLR[:, side, :], in0=HG[:, 0, :, colx], scalar1=sel[:, 0:1], scalar2=None, op0=ALU.mult),
            reads=[hgb, C.cb], writes=[hgb])
        for r in range(1, 4):
            k.op("dve", lambda e, side=side, sel=sel, colx=colx, r=r: e.scalar_tensor_tensor(
                out=HLR[:, side, :], in0=HG[:, r, :, colx], scalar=sel[:, r:r + 1], in1=HLR[:, side, :],
                op0=ALU.mult, op1=ALU.add), reads=[hgb, C.cb], writes=[hgb])

    zouts = Sink("zouts")
    for hh in range(2):
        was = [P.load_wa(W1L[:, 768 + part * 512 + hh * 256: 768 + part * 512 + (hh + 1) * 256]) for part in range(3)]
        for cl in range(2):
            cch = hh * 2 + cl
            for part in range(3):
                wa, wab = was[part]
                for j in range(4):
                    c0, w = H4.tiles[j]
                    ps, psb = P.bank("h1", [0, 1, 2, 3])
                    emit_inproj_tile(P, H4, j, wa, wab, cl, ps, psb)
                    k.op("act", lambda e, ps=ps, part=part, c0=c0, w=w: e.activation(
                        out=HB[:, part, 1 + c0:1 + c0 + w], in_=ps[:, 0:w], func=AF.Copy), reads=[psb], writes=[HBb])
                ch12 = part * 4 + cch
                k.op("act", lambda e, part=part, ch12=ch12: e.activation(out=HB[:, part, 0:1], in_=HLR[:, 0, ch12:ch12 + 1],
                                                                         func=AF.Copy), reads=[hgb], writes=[HBb])
                k.op("act", lambda e, part=part, ch12=ch12: e.activation(out=HB[:, part, 2049:2050],
                                                                         in_=HLR[:, 1, ch12:ch12 + 1], func=AF.Copy),
                     reads=[hgb], writes=[HBb])
            for j in range(4):
                c0, w = H4.tiles[j]
                cv = []
                for part in range(3):
                    ch12 = part * 4 + cch
                    t, tb_ = C.tmp()
                    k.op("dve", lambda e, t=t, part=part, c0=c0, ch12=ch12: e.tensor_scalar(
                        out=t[:, :], in0=HB[:, part, 1 + c0:1 + c0 + 512], scalar1=dcw[:, ch12, 1:2],
                        scalar2=dcb[:, ch12:ch12 + 1], op0=ALU.mult, op1=ALU.add), reads=[HBb, C.cb], writes=[tb_])
                    k.op("dve", lambda e, t=t, part=part, c0=c0, ch12=ch12: e.scalar_tensor_tensor(
                        out=t[:, :], in0=HB[:, part, c0:c0 + 512], scalar=dcw[:, ch12, 0:1], in1=t[:, :],
                        op0=ALU.mult, op1=ALU.add), reads=[HBb, C.cb, tb_], writes=[tb_])
                    k.op("dve", lambda e, t=t, part=part, c0=c0, ch12=ch12: e.scalar_tensor_tensor(
                        out=t[:, :], in0=HB[:, part, 2 + c0:2 + c0 + 512], scalar=dcw[:, ch12, 2:3], in1=t[:, :],
                        op0=ALU.mult, op1=ALU.add), reads=[HBb, C.cb, tb_], writes=[tb_])
                    cv.append((t, tb_))
                (x0t, x0b), (x1t, x1b), (vt, vb_) = cv
                k.dma("sp", [(x0_s.ap()[:, cch, c0:c0 + 512], x0t[:, :])], reads=[x0b], sembuf=x0b, sink=zouts)
                k.op("dve", lambda e, x1t=x1t, vt=vt: e.tensor_tensor(out=x1t[:, :], in0=x1t[:, :], in1=vt[:, :], op=ALU.mult),
                     reads=[x1b, vb_], writes=[x1b])
                k.dma("sp", [(zf_s.ap()[:, cch, c0:c0 + 512], x1t[:, :])], reads=[x1b], sembuf=x1b, sink=zouts)
                zb16, zb16b = C.tmpbf()
                k.op("act", lambda e, x1t=x1t, zb16=zb16: e.activation(out=zb16[:, :], in_=x1t[:, :], func=AF.Copy),
                     reads=[x1b], writes=[zb16b])
                k.dma("sp", [(z_src.ap()[cch, :, c0:c0 + 512], zb16[:, :])], reads=[zb16b],
                      sembuf=zb16b, sink=zouts)
    zgb = Buf("zg")
    for cch in range(4):
        cc.allgather(z_src, z_g, [], zgb, sinks=[zouts], sl=cch)

    if stop_after == "l1pre":
        return finish([kgb, vgb, zgb, kcsb])

    k.barrier()
    VB = AR[:, 0:12288].rearrange("p (b d) -> p b d", b=64)
    KT = AR[:, 12288:20480]
    QS = [AR[:, 20480 + i * 2048:20480 + (i + 1) * 2048].rearrange("p (h t) -> p h t", h=4) for i in range(2)]
    QSb = [Buf("qs0"), Buf("qs1")]
    MIX = AR[:, 24576:40960].rearrange("p (c t) -> p c t", c=8)
    MIXb = [Buf(f"mixb{j}") for j in range(4)]
    KTb, VBb = Buf("KT"), Buf("VB")
    k.op("dve", lambda e: e.memset(VB[:, :, 64:128], 1.0), writes=[VBb])
    k.dma("sp", [(KT.rearrange("p (r t) -> p r t", r=4), k_g.ap().rearrange("(r p) t -> p r t", p=128))],
          reads=[kgb], writes=[KTb])
    vsrc = v_g.ap().rearrange("(b p) d -> p b d", p=128)
    k.dma("sp", [(VB[:, :, 0:64], vsrc[:, :, 0:64]), (VB[:, :, 128:192], vsrc[:, :, 64:128])], reads=[vgb], writes=[VBb])
    extra_p = [(AR[:, 40960 + i * 512:40960 + (i + 1) * 512], Buf(f"xp{i}")) for i in range(2)]
    spare_t, spare_b = C.tmp.t.pop(), C.tmp.b.pop()
    C.tmp.i = 0
    extra_p.append((spare_t[:, :].bitcast(BF16)[:, 0:512], spare_b))
    emit_attn1g(P, C, q_s, KT, KTb, VB, VBb, QS, QSb, MIX, MIXb, extra_p)

    if stop_after == "attn":
        return finish([zgb, kcsb] + MIXb)

    k.barrier()
    V = Common()
    KC = AR[:, 0:16384].rearrange("p (n c) -> p n c", n=128)
    Zg = AR[:, 0:8192]
    V.ZC = [AR[:, 16384 + i * 512:16384 + (i + 1) * 512].rearrange("p (c k) -> p c k", c=4) for i in range(2)]
    V.ZCb = [Buf("zc0"), Buf("zc1")]
    six = [AR[:, 17408 + i * 512:17408 + (i + 1) * 512].rearrange("p (c k) -> p c k", c=4) for i in range(6)]
    V.Bre, V.Bim, V.Yre, V.Yim, V.Qre, V.Qim = six
    V.Hre = AR[:, 20480:21504].bitcast(F32)
    V.Him = AR[:, 21504:22528].bitcast(F32)
    Zgb, zs2b, KCb = Buf("Zg"), Buf("zs2"), Buf("KC")
    zg_rows = z_g.ap().rearrange("c r t -> (c r) t")
    k.wait_all("pool", [cc.sink(zgb)])
    for r in range(4):
        kb_gather(k, Zg[:, r * 2048:(r + 1) * 2048], zg_rows, IDX[:, r:r + 1], reads=[zgb, C.cb], writes=[Zgb])
    k.dma("sp", [(zs2.ap(), Zg)], reads=[Zgb], writes=[zs2b])
    k.wait_all("sp", [kcsb])
    k.dma("sp", [(KC, kc_s.ap())], reads=[zs2b], writes=[KCb])
    csink = [Sink(f"csrc{i}") for i in range(8)]
    cgb = Buf("cg")

    def after_group(g):
        if g % 4 == 3:
            cc.allgather(c_src, c_g, [], cgb, sinks=[csink[g // 4]], sl=g // 4)
    emit_fftconv(P, C, F, KC, KCb, zs2, zs2b, c_src, csink, V, after_group=after_group)

    if stop_after == "fft":
        return finish([cgb] + MIXb)

    cg_rows = c_g.ap().rearrange("i r (a t) -> (i r a) t", t=512)
    k.wait_all("pool", [cc.sink(cgb)])
    for cq in range(4):
        for tq in range(4):
            c0 = tq * 512
            ct, ctb = C.tmp()
            kb_gather(k, ct[:, :], cg_rows, IDX[:, 4 + cq * 4 + tq:5 + cq * 4 + tq], reads=[cgb, C.cb], writes=[ctb])
            zt, ztb = C.tmp()
            k.dma("sp", [(zt[:, :], zf_s.ap()[:, cq, c0:c0 + 512])], writes=[ztb])
            xt_, xtb = C.tmp()
            k.dma("sp", [(xt_[:, :], x0_s.ap()[:, cq, c0:c0 + 512])], writes=[xtb])
            k.op("dve", lambda e, zt=zt, ct=ct, cq=cq: e.scalar_tensor_tensor(
                out=zt[:, :], in0=zt[:, :], scalar=skipv[:, cq:cq + 1], in1=ct[:, :], op0=ALU.mult, op1=ALU.add),
                reads=[ztb, ctb, C.cb], writes=[ztb])
            k.op("dve", lambda e, zt=zt, xt_=xt_, cq=cq, c0=c0: e.tensor_tensor(
                out=MIX[:, 4 + cq, c0:c0 + 512], in0=zt[:, :], in1=xt_[:, :], op=ALU.mult),
                reads=[ztb, xtb], writes=[MIXb[tq]])

    emit_outproj(P, C, X4, lambda fc, c0, w: MIX[:, fc, c0:c0 + w], MIXb, wout_d[1], mcol(1, 1, 2))
    k.barrier()
    Ht2 = AR[:, 0:16384].rearrange("p (c t) -> p c t", c=8)
    U2 = AR[:, 16384:16384 + 12288].rearrange("p (f t) -> p f t", f=6)
    H42 = Act(Ht2, tiles4, "h2")
    Ub2 = [[Buf(f"v{f}_{j}") for j in range(4)] for f in range(6)]
    emit_adaln(P, X4, H42, AV[:, 1, 2], mcol(1, 2, 0), C)
    emit_ffn(P, X4, H42, U2, Ub2, w1_d[1, 1], w3_d[1, 1], w2_d[1, 1], mcol(1, 2, 2), C)
    ob = Sink("out")
    for j in range(4):
        k.dma("sp", [(xo_d[:, :, j * 512:(j + 1) * 512], XT[:, :, j * 512:(j + 1) * 512])], reads=[XB[j]],
              sembuf=XB[j], sink=ob)
    k.wait_all("sp", [ob])
    k.close()
    return nc


def prep_fused(inp, ncore=NCORE):
    base = prep_LA(inp)
    maps = []
    cache = {}
    dcw = np.asarray(inp["d_conv_w"], np.float32)[0]
    dcb = np.asarray(inp["d_conv_b"], np.float32)[0]
    skip = np.asarray(inp["d_skip"], np.float32)[0]
    w4 = np.asarray(inp["d_f_w4"], np.float32)[0]
    p = np.arange(128)
    for core in range(ncore):
        b, qd = core // 4, core % 4
        cq = qd
        if cq not in cache:
            cache[cq] = host_consts_LB(cq)
        zemb, win, dftc, tw = cache[cq]
        m = dict(base[core])
        cf = np.zeros((128, 400), np.float32)
        cf[:, 0:244] = m.pop("cf")[:, 0:244]
        cf[:, 244:280] = fm(dcw).transpose(0, 2, 1).reshape(128, 36)
        cf[:, 280:292] = fm(dcb)
        cf[:, 292:296] = fm(skip)
        if qd > 0:
            cf[:, 296 + qd - 1] = 1.0
        if qd < 3:
            cf[:, 300 + qd + 1] = 1.0
        idx = np.zeros((128, 24), np.uint32)
        for r in range(4):
            idx[:, r] = cq * 512 + r * 128 + p
        for c2 in range(4):
            for tq in range(4):
                idx[:, 4 + c2 * 4 + tq] = ((p // 16) * 64 + c2 * 16 + (p % 16)) * 16 + qd * 4 + tq
        mlp = np.zeros((64, 456), np.float32)
        mlp[0:33, 0:64] = np.asarray(inp["d_f_w1"], np.float32)[0]
        mlp[:, 64:128] = np.asarray(inp["d_f_w2"], np.float32)[0]
        mlp[:, 128:192] = np.asarray(inp["d_f_w3"], np.float32)[0]
        mlp[:, 192] = np.asarray(inp["d_f_b1"], np.float32)[0]
        mlp[:, 193] = np.asarray(inp["d_f_b2"], np.float32)[0]
        mlp[:, 194] = np.asarray(inp["d_f_b3"], np.float32)[0]
        mlp[:, 195] = np.asarray(inp["d_f_freq"], np.float32)[0]
        mlp[:, 200:328] = w4[:, cq * 128:(cq + 1) * 128]
        mlp[:, 328:456] = w4[:, 512 + cq * 128:512 + (cq + 1) * 128]
        m["win_w"] = m.pop("win")
        m.update(cf=cf, idx=idx, mlpw=mlp, zemb=zemb, win=win, dftc=dftc, tw=tw)
        maps.append(m)
    return maps
```
